# Optimizing a Trainium2 kernel written in Bass

```python
import math
import jax, jax.numpy as jnp
from jax import lax
import numpy as np

D_MODEL = 1024
BATCH = 32
SEQ = 256
DEPTH = 4
DEC_BATCH = 8
DEC_SEQ = 1024
PAST_LEN = 512

GRID_W = 64
N_MIXERS = 3
LAYER_KINDS = ('A', 'B', 'C', 'A')
A_HEADS = 16
A_KV_HEADS = 4
A_HEAD_DIM = 64
A_GROUP = A_HEADS // A_KV_HEADS
WINDOW = 128
C_HEADS = 8
C_KV_HEADS = 4
C_HEAD_DIM = 128
C_GROUP = C_HEADS // C_KV_HEADS
Q_BLOCK = 128
S5_GROUP = 16
S5_GROUPS = D_MODEL // S5_GROUP
S5_STATE = 64
DT_MIN = 1e-3
DT_MAX = 1e-1
D_FF = 2816
CONV_W = 3
ROPE_THETA = 10000.0
EPS = 1e-6

kernel_name = 'hybrid_diffusion_step'


def rms_norm(x, gain):
    xf = x.astype(jnp.float32)
    y = xf * lax.rsqrt(jnp.mean(xf * xf, axis=-1, keepdims=True) + EPS)
    return (y * gain.astype(jnp.float32)).astype(x.dtype)


def ada_params(cond, w_mod, b_mod):
    m = jax.nn.silu(cond) @ w_mod + b_mod
    return jnp.split(m[..., None, :], 6, axis=-1)


def sublayer_in(x, gain, shift, scale):
    return rms_norm(x, gain) * (1 + scale) + shift


def grid_positions(n_tokens):
    rows = n_tokens // GRID_W
    row = jnp.repeat(jnp.arange(rows), GRID_W)
    col = jnp.tile(jnp.arange(GRID_W), rows)
    return row, col


def axial_rope(x, row, col):
    dh = x.shape[-1]
    half = dh // 2
    quarter = half // 2
    freqs = 1.0 / (ROPE_THETA ** (jnp.arange(quarter, dtype=jnp.float32) / quarter))

    def rot(xa, pos):
        ang = pos.astype(jnp.float32)[:, None] * freqs[None, :]
        cos = jnp.cos(ang)[None, :, None, :].astype(x.dtype)
        sin = jnp.sin(ang)[None, :, None, :].astype(x.dtype)
        x1, x2 = xa[..., :quarter], xa[..., quarter:]
        return jnp.concatenate([x1 * cos - x2 * sin, x1 * sin + x2 * cos], axis=-1)

    return jnp.concatenate([rot(x[..., :half], row), rot(x[..., half:], col)], axis=-1)


def qkv_heads(x, p, n_heads, n_kv, dh):
    b, s, _ = x.shape
    qkv = x @ p['w_qkv']
    q = qkv[..., :n_heads * dh].reshape(b, s, n_heads, dh)
    k = qkv[..., n_heads * dh:(n_heads + n_kv) * dh].reshape(b, s, n_kv, dh)
    v = qkv[..., (n_heads + n_kv) * dh:].reshape(b, s, n_kv, dh)
    return rms_norm(q, p['q_norm']), rms_norm(k, p['k_norm']), v


def attend_block(q, k, v, valid, sink):
    scale = q.shape[-1] ** -0.5
    s = jnp.einsum('bqkgd,bskd->bkgqs', q, k, preferred_element_type=jnp.float32) * scale
    if valid is not None:
        s = jnp.where(valid, s, -jnp.inf)
    if sink is not None:
        sk = jnp.broadcast_to(sink.astype(jnp.float32)[None, :, :, None, None], s.shape[:-1] + (1,))
        probs = jax.nn.softmax(jnp.concatenate([s, sk], axis=-1), axis=-1)[..., :-1]
    else:
        probs = jax.nn.softmax(s, axis=-1)
    return jnp.einsum('bkgqs,bskd->bqkgd', probs.astype(v.dtype), v)


def sweep_query_blocks(q, block_fn):
    b, s = q.shape[:2]
    nb = s // Q_BLOCK
    qb = q.reshape((b, nb, Q_BLOCK) + q.shape[2:]).swapaxes(0, 1)
    out = lax.map(lambda args: block_fn(args[0], args[1]), (qb, jnp.arange(nb)))
    return out.swapaxes(0, 1).reshape(q.shape)


def attn_context(x, p, n_heads, n_kv, dh, sink):
    b, s, _ = x.shape
    q, k, v = qkv_heads(x, p, n_heads, n_kv, dh)
    qg = q.reshape(b, s, n_kv, n_heads // n_kv, dh)
    o = sweep_query_blocks(qg, lambda qb, i: attend_block(qb, k, v, None, sink))
    return o.reshape(b, s, n_heads * dh) @ p['w_o'], (k, v)


def window_attn_latent(x, p, ck, cv, row, col):
    b, s, _ = x.shape
    n_ctx = ck.shape[1]
    q, k, v = qkv_heads(x, p, A_HEADS, A_KV_HEADS, A_HEAD_DIM)
    q = axial_rope(q, row, col)
    k = axial_rope(k, row, col)
    qg = q.reshape(b, s, A_KV_HEADS, A_GROUP, A_HEAD_DIM)
    pad = ((0, 0), (WINDOW, WINDOW), (0, 0), (0, 0))
    kp = jnp.pad(k, pad)
    vp = jnp.pad(v, pad)
    band = Q_BLOCK + 2 * WINDOW
    sink = p['sink'].reshape(A_KV_HEADS, A_GROUP)
    ctx_valid = jnp.ones((Q_BLOCK, n_ctx), dtype=bool)

    def block(qb, i):
        start = i * Q_BLOCK
        kb = lax.dynamic_slice_in_dim(kp, start, band, axis=1)
        vb = lax.dynamic_slice_in_dim(vp, start, band, axis=1)
        qpos = start + jnp.arange(Q_BLOCK)
        kpos = start - WINDOW + jnp.arange(band)
        band_valid = ((jnp.abs(qpos[:, None] - kpos[None, :]) <= WINDOW)
                      & (kpos[None, :] >= 0) & (kpos[None, :] < s))
        valid = jnp.concatenate([ctx_valid, band_valid], axis=1)
        return attend_block(qb, jnp.concatenate([ck, kb], axis=1),
                            jnp.concatenate([cv, vb], axis=1), valid, sink)

    o = sweep_query_blocks(qg, block)
    return o.reshape(b, s, A_HEADS * A_HEAD_DIM) @ p['w_o']


def full_attn_latent(x, p, ck, cv, row, col):
    b, s, _ = x.shape
    q, k, v = qkv_heads(x, p, C_HEADS, C_KV_HEADS, C_HEAD_DIM)
    q = axial_rope(q, row, col)
    k = axial_rope(k, row, col)
    qg = q.reshape(b, s, C_KV_HEADS, C_GROUP, C_HEAD_DIM)
    k_all = jnp.concatenate([ck, k], axis=1)
    v_all = jnp.concatenate([cv, v], axis=1)
    o = sweep_query_blocks(qg, lambda qb, i: attend_block(qb, k_all, v_all, None, None))
    return o.reshape(b, s, C_HEADS * C_HEAD_DIM) @ p['w_o']


def s5_discretize(lam_re, lam_im, log_dt, b_re, b_im):
    lam_re = lam_re.astype(jnp.float32)
    lam_im = lam_im.astype(jnp.float32)
    dt = jnp.exp(log_dt.astype(jnp.float32))[:, None]
    mag = jnp.exp(lam_re * dt)
    a_re = mag * jnp.cos(lam_im * dt)
    a_im = mag * jnp.sin(lam_im * dt)
    den = lam_re * lam_re + lam_im * lam_im
    n_re = a_re - 1.0
    f_re = (n_re * lam_re + a_im * lam_im) / den
    f_im = (a_im * lam_re - n_re * lam_im) / den
    b_re = b_re.astype(jnp.float32)
    b_im = b_im.astype(jnp.float32)
    bb_re = f_re[..., None] * b_re - f_im[..., None] * b_im
    bb_im = f_re[..., None] * b_im + f_im[..., None] * b_re
    return a_re, a_im, bb_re, bb_im


def complex_scan(a_re, a_im, b_re, b_im, reverse):
    def combine(e1, e2):
        a1r, a1i, b1r, b1i = e1
        a2r, a2i, b2r, b2i = e2
        return (a2r * a1r - a2i * a1i, a2r * a1i + a2i * a1r,
                a2r * b1r - a2i * b1i + b2r, a2r * b1i + a2i * b1r + b2i)
    ar = jnp.broadcast_to(a_re, b_re.shape)
    ai = jnp.broadcast_to(a_im, b_re.shape)
    _, _, hr, hi = lax.associative_scan(combine, (ar, ai, b_re, b_im), reverse=reverse, axis=1)
    return hr, hi


def s5_mixer(u, p, s0):
    b, s, d = u.shape
    uf = u.astype(jnp.float32)
    ug = uf.reshape(b, s, S5_GROUPS, S5_GROUP)
    y = uf * p['d_skip'].astype(jnp.float32)
    finals = []
    for direction, reverse in ((0, False), (1, True)):
        a_re, a_im, bb_re, bb_im = s5_discretize(p['lam_re'][direction], p['lam_im'][direction],
                                                 p['log_dt'][direction], p['b_re'][direction],
                                                 p['b_im'][direction])
        br = jnp.einsum('bsgc,gpc->bsgp', ug, bb_re)
        bi = jnp.einsum('bsgc,gpc->bsgp', ug, bb_im)
        if s0 is not None:
            first = -1 if reverse else 0
            s_re = s0[:, direction, 0].astype(jnp.float32)
            s_im = s0[:, direction, 1].astype(jnp.float32)
            br = br.at[:, first].add(a_re * s_re - a_im * s_im)
            bi = bi.at[:, first].add(a_re * s_im + a_im * s_re)
        hr, hi = complex_scan(a_re, a_im, br, bi, reverse)
        c_re = p['c_re'][direction].astype(jnp.float32)
        c_im = p['c_im'][direction].astype(jnp.float32)
        y_dir = jnp.einsum('bsgp,gcp->bsgc', hr, c_re) - jnp.einsum('bsgp,gcp->bsgc', hi, c_im)
        y = y + y_dir.reshape(b, s, d)
        if s0 is None:
            last = 0 if reverse else -1
            finals.append(jnp.stack([hr[:, last], hi[:, last]], axis=1))
    g = jax.nn.gelu(y)
    h = g @ p['w_glu'].astype(jnp.float32)
    out = (h[..., :d] * jax.nn.sigmoid(h[..., d:])).astype(u.dtype)
    state = jnp.stack(finals, axis=1).astype(u.dtype) if s0 is None else None
    return out, state


def conv_ffn(x, w_up, b_up, conv_k, conv_b, w_down):
    s = x.shape[1]
    h = x @ w_up + b_up
    r = CONV_W // 2
    hp = jnp.pad(h, ((0, 0), (r, r), (0, 0)))
    h = conv_b + sum(hp[:, j:j + s] * conv_k[j] for j in range(CONV_W))
    gate, val = jnp.split(h, 2, axis=-1)
    return (jax.nn.silu(gate) * val) @ w_down


def context_mixer(kind, p, h):
    if kind == 'A':
        return attn_context(h, p, A_HEADS, A_KV_HEADS, A_HEAD_DIM,
                            p['sink'].reshape(A_KV_HEADS, A_GROUP))
    if kind == 'B':
        return s5_mixer(h, p, None)
    return attn_context(h, p, C_HEADS, C_KV_HEADS, C_HEAD_DIM, None)


def latent_mixer(kind, p, h, cache, row, col):
    if kind == 'A':
        return window_attn_latent(h, p, cache[0], cache[1], row, col)
    if kind == 'B':
        return s5_mixer(h, p, cache)[0]
    return full_attn_latent(h, p, cache[0], cache[1], row, col)


def setup_inputs(seed: int = 0) -> dict:
    key = jax.random.key(seed)
    keys = iter(jax.random.split(key, 64))

    def nrm(shape, scale):
        return scale * jax.random.normal(next(keys), shape, jnp.float32)

    def gain(shape):
        return 1.0 + nrm(shape, 0.02)

    D = D_MODEL
    a_qkv = (A_HEADS + 2 * A_KV_HEADS) * A_HEAD_DIM
    c_qkv = (C_HEADS + 2 * C_KV_HEADS) * C_HEAD_DIM

    def s5_lam_im():
        base = jnp.broadcast_to(jnp.pi * jnp.arange(S5_STATE, dtype=jnp.float32), (2, S5_GROUPS, S5_STATE))
        return base + nrm((2, S5_GROUPS, S5_STATE), 0.01)

    inputs = {
        'x_prompt': nrm((BATCH, SEQ, D), 1.0),
        'x_sample': nrm((DEC_BATCH, DEC_SEQ, D), 1.0),
        'cache_l0_k': nrm((DEC_BATCH, PAST_LEN, A_KV_HEADS, A_HEAD_DIM), 1.0),
        'cache_l0_v': nrm((DEC_BATCH, PAST_LEN, A_KV_HEADS, A_HEAD_DIM), 1.0),
        'state_l1': nrm((DEC_BATCH, 2, 2, S5_GROUPS, S5_STATE), 0.3),
        'cache_l2_k': nrm((DEC_BATCH, PAST_LEN, C_KV_HEADS, C_HEAD_DIM), 1.0),
        'cache_l2_v': nrm((DEC_BATCH, PAST_LEN, C_KV_HEADS, C_HEAD_DIM), 1.0),
        'cache_l3_k': nrm((DEC_BATCH, PAST_LEN, A_KV_HEADS, A_HEAD_DIM), 1.0),
        'cache_l3_v': nrm((DEC_BATCH, PAST_LEN, A_KV_HEADS, A_HEAD_DIM), 1.0),
        'c': nrm((DEC_BATCH, D), 1.0),
        'c_ctx': nrm((D,), 1.0),
        'norm1': gain((DEPTH, D)),
        'norm2': gain((DEPTH, D)),
        'w_mod': nrm((DEPTH, D, 6 * D), 0.5 * D ** -0.5),
        'b_mod': nrm((DEPTH, 6 * D), 0.02),
        'w_up': nrm((DEPTH, D, 2 * D_FF), D ** -0.5),
        'b_up': nrm((DEPTH, 2 * D_FF), 0.02),
        'conv_k': nrm((DEPTH, CONV_W, 2 * D_FF), CONV_W ** -0.5),
        'conv_b': nrm((DEPTH, 2 * D_FF), 0.02),
        'w_down': nrm((DEPTH, D_FF, D), D_FF ** -0.5),
        'l0_w_qkv': nrm((D, a_qkv), D ** -0.5),
        'l0_q_norm': gain((A_HEAD_DIM,)),
        'l0_k_norm': gain((A_HEAD_DIM,)),
        'l0_sink': nrm((A_HEADS,), 1.0),
        'l0_w_o': nrm((A_HEADS * A_HEAD_DIM, D), (A_HEADS * A_HEAD_DIM) ** -0.5),
        'l1_lam_re': -0.5 + nrm((2, S5_GROUPS, S5_STATE), 0.01),
        'l1_lam_im': s5_lam_im(),
        'l1_log_dt': jax.random.uniform(next(keys), (2, S5_GROUPS), jnp.float32,
                                        math.log(DT_MIN), math.log(DT_MAX)),
        'l1_b_re': nrm((2, S5_GROUPS, S5_STATE, S5_GROUP), (2 * S5_GROUP) ** -0.5),
        'l1_b_im': nrm((2, S5_GROUPS, S5_STATE, S5_GROUP), (2 * S5_GROUP) ** -0.5),
        'l1_c_re': nrm((2, S5_GROUPS, S5_GROUP, S5_STATE), (2 * S5_STATE) ** -0.5),
        'l1_c_im': nrm((2, S5_GROUPS, S5_GROUP, S5_STATE), (2 * S5_STATE) ** -0.5),
        'l1_d_skip': nrm((D,), 1.0),
        'l1_w_glu': nrm((D, 2 * D), D ** -0.5),
        'l2_w_qkv': nrm((D, c_qkv), D ** -0.5),
        'l2_q_norm': gain((C_HEAD_DIM,)),
        'l2_k_norm': gain((C_HEAD_DIM,)),
        'l2_w_o': nrm((C_HEADS * C_HEAD_DIM, D), (C_HEADS * C_HEAD_DIM) ** -0.5),
        'l3_w_qkv': nrm((D, a_qkv), D ** -0.5),
        'l3_q_norm': gain((A_HEAD_DIM,)),
        'l3_k_norm': gain((A_HEAD_DIM,)),
        'l3_sink': nrm((A_HEADS,), 1.0),
        'l3_w_o': nrm((A_HEADS * A_HEAD_DIM, D), (A_HEADS * A_HEAD_DIM) ** -0.5),
    }
    return inputs


def reference(x_prompt, x_sample, cache_l0_k, cache_l0_v, state_l1, cache_l2_k, cache_l2_v,
              cache_l3_k, cache_l3_v, c, c_ctx, norm1, norm2, w_mod, b_mod, w_up, b_up, conv_k,
              conv_b, w_down, l0_w_qkv, l0_q_norm, l0_k_norm, l0_sink, l0_w_o, l1_lam_re, l1_lam_im,
              l1_log_dt, l1_b_re, l1_b_im, l1_c_re, l1_c_im, l1_d_skip, l1_w_glu, l2_w_qkv,
              l2_q_norm, l2_k_norm, l2_w_o, l3_w_qkv, l3_q_norm, l3_k_norm, l3_sink, l3_w_o):
    mixer_params = (
        {'w_qkv': l0_w_qkv, 'q_norm': l0_q_norm, 'k_norm': l0_k_norm, 'sink': l0_sink, 'w_o': l0_w_o},
        {'lam_re': l1_lam_re, 'lam_im': l1_lam_im, 'log_dt': l1_log_dt, 'b_re': l1_b_re,
         'b_im': l1_b_im, 'c_re': l1_c_re, 'c_im': l1_c_im, 'd_skip': l1_d_skip, 'w_glu': l1_w_glu},
        {'w_qkv': l2_w_qkv, 'q_norm': l2_q_norm, 'k_norm': l2_k_norm, 'w_o': l2_w_o},
        {'w_qkv': l3_w_qkv, 'q_norm': l3_q_norm, 'k_norm': l3_k_norm, 'sink': l3_sink, 'w_o': l3_w_o},
    )
    caches = ((cache_l0_k, cache_l0_v), state_l1, (cache_l2_k, cache_l2_v), (cache_l3_k, cache_l3_v))

    row, col = grid_positions(x_sample.shape[1])

    h_ctx = x_prompt
    h_lat = x_sample
    new_state = []
    for l in range(DEPTH):
        kind = LAYER_KINDS[l % N_MIXERS] if l < len(LAYER_KINDS) else 'ABC'[l % N_MIXERS]
        p = mixer_params[l]
        ffn_p = (w_up[l], b_up[l], conv_k[l], conv_b[l], w_down[l])

        sh1, sc1, g1, sh2, sc2, g2 = ada_params(c_ctx, w_mod[l], b_mod[l])
        mix, st = context_mixer(kind, p, sublayer_in(h_ctx, norm1[l], sh1, sc1))
        h_ctx = h_ctx + g1 * mix
        h_ctx = h_ctx + g2 * conv_ffn(sublayer_in(h_ctx, norm2[l], sh2, sc2), *ffn_p)
        new_state.append(st)

        sh1, sc1, g1, sh2, sc2, g2 = ada_params(c, w_mod[l], b_mod[l])
        mix = latent_mixer(kind, p, sublayer_in(h_lat, norm1[l], sh1, sc1), caches[l], row, col)
        h_lat = h_lat + g1 * mix
        h_lat = h_lat + g2 * conv_ffn(sublayer_in(h_lat, norm2[l], sh2, sc2), *ffn_p)

    new_l0_k, new_l0_v = new_state[0]
    new_l1_state = new_state[1]
    new_l2_k, new_l2_v = new_state[2]
    new_l3_k, new_l3_v = new_state[3]
    return (h_ctx, h_lat, new_l0_k, new_l0_v, new_l1_state, new_l2_k, new_l2_v, new_l3_k, new_l3_v)
```

```python
import math
import numpy as np
from contextlib import ExitStack
import concourse.bass as bass
import concourse.mybir as mybir
from concourse.bass_utils import run_bass_kernel_spmd

F32 = mybir.dt.float32
BF16 = mybir.dt.bfloat16
I32 = mybir.dt.int32
U8 = mybir.dt.uint8
AF = mybir.ActivationFunctionType
ALU = mybir.AluOpType
AX = mybir.AxisListType

D = 1024
DFF = 2816
NT = 1024
ENGS = ["pe", "act", "dve", "pool", "sp"]
EPS = 1e-6
NLAYERS = 4
SKIP_S5 = False
DEBUG_H = False
STAGE = 7
ATT_STOP = 9


class Sched:
    def __init__(self, nc, stack):
        self.nc = nc
        self.stack = stack
        self.prog = {e: [] for e in ENGS}
        self.cnt = {e: 0 for e in ENGS}
        self.seen = {e: {} for e in ENGS}
        self.lastw = {}
        self.readers = {}
        self.sem_objs = []
        self.esid = {}
        for e in ENGS:
            self.esid[e] = self._new_sem("s_" + e)
        self.dsem = {}
        self.dfree = []

    def _new_sem(self, name):
        self.sem_objs.append(self.stack.enter_context(self.nc.semaphore(name)))
        return len(self.sem_objs) - 1

    def _dma_sem(self, key):
        if key not in self.dsem:
            if self.dfree:
                self.dsem[key] = self.dfree.pop()
            else:
                self.dsem[key] = [self._new_sem("d%d" % len(self.sem_objs)), 0]
        return self.dsem[key]

    def _deps(self, eng, reads, writes, acc=False):
        need = {}

        def add(rec, same_ok=False):
            sid, val, e = rec
            if same_ok and e == eng:
                return
            if need.get(sid, 0) < val:
                need[sid] = val

        for k in reads:
            if k in self.lastw:
                add(self.lastw[k])
        for k in writes:
            if k in self.lastw:
                add(self.lastw[k], same_ok=(acc and eng == "pe"))
            for r in self.readers.get(k, ()):
                add(r, same_ok=True)
        waits = []
        seen = self.seen[eng]
        for sid, val in need.items():
            if seen.get(sid, 0) >= val:
                continue
            seen[sid] = val
            waits.append((sid, val))
        return waits

    def op(self, eng, fn, reads=(), writes=(), acc=False):
        waits = self._deps(eng, reads, writes, acc)
        self.cnt[eng] += 1
        rec = (self.esid[eng], self.cnt[eng], eng)
        self.prog[eng].append((waits, fn, self.esid[eng], 1))
        for k in reads:
            self.readers.setdefault(k, []).append(rec)
        for k in writes:
            self.lastw[k] = rec
            self.readers[k] = []

    def dma(self, q, fn, reads=(), writes=(), key=None):
        if key is None:
            key = writes[0] if writes else reads[0]
        waits = self._deps(q, reads, writes)
        ds = self._dma_sem(key)
        ds[1] += 16
        rec = (ds[0], ds[1], "dma")
        self.prog[q].append((waits, fn, ds[0], 16))
        for k in reads:
            self.readers.setdefault(k, []).append(rec)
        for k in writes:
            self.lastw[k] = rec
            self.readers[k] = []

    def barrier(self):
        allw = [(self.esid[e], self.cnt[e], e) for e in ENGS if self.cnt[e] > 0]
        allw += [(sid, c, None) for k, (sid, c) in self.dsem.items() if c > 0]
        for e in ENGS:
            waits = []
            for sid, val, owner in allw:
                if owner == e:
                    continue
                if self.seen[e].get(sid, 0) >= val:
                    continue
                self.seen[e][sid] = val
                waits.append((sid, val))
            if waits:
                self.prog[e].append((waits, None, None, 0))
        self.dfree.extend(self.dsem.values())
        self.dsem = {}
        self.lastw = {}
        self.readers = {}

    def emit(self):
        nc, sems, prog = self.nc, self.sem_objs, self.prog

        def run(ename):
            def body(eng):
                for waits, fn, incsid, incv in prog[ename]:
                    for sid, val in waits:
                        eng.wait_ge(sems[sid], val)
                    if fn is not None:
                        fn(eng).then_inc(sems[incsid], incv)
            return body

        with nc.Block() as block:
            block.tensor(run("pe"))
            block.scalar(run("act"))
            block.vector(run("dve"))
            block.gpsimd(run("pool"))
            block.sync(run("sp"))


class Arena:
    def __init__(self, tensor, size):
        self.t = tensor
        self.size = size
        self.off = 0

    def alloc(self, shape, dt):
        nb = {F32: 4, BF16: 2, I32: 4}[dt]
        n = int(np.prod(shape[1:]))
        off = (self.off + 63) // 64 * 64
        assert off + n * nb <= self.size, ("SBUF arena overflow", off, n * nb, self.size)
        self.off = off + n * nb
        ap = self.t[:, off:off + n * nb].bitcast(dt)
        if len(shape) == 3:
            ap = ap.rearrange("p (a b) -> p a b", a=shape[1])
        elif len(shape) == 4:
            ap = ap.rearrange("p (a b c) -> p a b c", a=shape[1], b=shape[2])
        if shape[0] != 128:
            ap = ap[0:shape[0]]
        return ap

    def mark(self):
        return self.off

    def release(self, m):
        self.off = m


LAYERS = [
    dict(kind="A", dh=64, H=16, ckv=256, wqkv="l0_w_qkv", qn="l0_q_norm", kn="l0_k_norm", sink="l0_sink", wo="l0_w_o",
         ck="ck0", cv="cv0", nk="nk0", nv="nv0"),
    dict(kind="B"),
    dict(kind="C", dh=128, H=8, ckv=512, wqkv="l2_w_qkv", qn="l2_q_norm", kn="l2_k_norm", sink=None, wo="l2_w_o",
         ck="ck2", cv="cv2", nk="nk2", nv="nv2"),
    dict(kind="A", dh=64, H=16, ckv=256, wqkv="l3_w_qkv", qn="l3_q_norm", kn="l3_k_norm", sink="l3_sink", wo="l3_w_o",
         ck="ck3", cv="cv3", nk="nk3", nv="nv3"),
]

IN_SPECS = [
    ("xin", [2, NT, D]), ("cond", [2, D]),
    ("ck0", [512, 256]), ("cv0", [512, 256]), ("st1", [2, 2, 64, 64]),
    ("ck2", [512, 512]), ("cv2", [512, 512]), ("ck3", [512, 256]), ("cv3", [512, 256]),
    ("norm1", [4, D]), ("norm2", [4, D]), ("w_mod", [4, D, 6 * D]), ("b_mod", [4, 6 * D]),
    ("w_up", [4, D, 2 * DFF]), ("b_up", [4, 2 * DFF]), ("conv_k", [4, 3, 2 * DFF]), ("conv_b", [4, 2 * DFF]),
    ("w_down", [4, DFF, D]),
    ("l0_w_qkv", [D, 1536]), ("l0_q_norm", [64]), ("l0_k_norm", [64]), ("l0_sink", [16]), ("l0_w_o", [D, D]),
    ("l1_lam_re", [2, 64, 64]), ("l1_lam_im", [2, 64, 64]), ("l1_log_dt", [2, 64]),
    ("l1_b_re", [2, 64, 64, 16]), ("l1_b_im", [2, 64, 64, 16]), ("l1_c_re", [2, 64, 16, 64]), ("l1_c_im", [2, 64, 16, 64]),
    ("l1_d_skip", [D]), ("l1_w_glu", [D, 2 * D]),
    ("l2_w_qkv", [D, 2048]), ("l2_q_norm", [128]), ("l2_k_norm", [128]), ("l2_w_o", [D, D]),
    ("l3_w_qkv", [D, 1536]), ("l3_q_norm", [64]), ("l3_k_norm", [64]), ("l3_sink", [16]), ("l3_w_o", [D, D]),
    ("cst", [9, 128, 128]), ("rope", [4, 128, NT]),
]
OUT_SPECS = [
    ("y", [2, NT, D]), ("nk0", [NT, 256]), ("nv0", [NT, 256]), ("nst", [4, 2, 2, 64, 64]),
    ("nk2", [NT, 512]), ("nv2", [NT, 512]), ("nk3", [NT, 256]), ("nv3", [NT, 256]),
]
CST_IDENT, CST_BD64, CST_MLO, CST_MHI, CST_PA, CST_PC, CST_S5F, CST_S5B, CST_R16 = range(9)


def build_program():
    nc = bass.Bass("TRN2", target_bir_lowering=False)
    T = {}
    for name, shape in IN_SPECS:
        T[name] = nc.dram_tensor(name, shape, F32, kind="ExternalInput").ap()
    for name, shape in OUT_SPECS:
        T[name] = nc.dram_tensor(name, shape, F32, kind="ExternalOutput").ap()
    for name, shape, dt_ in [("TX", [128, 16384], BF16), ("TZ", [128, 16384], BF16), ("TM", [64, 128, 128], BF16),
                             ("TE", [2, 64, 128, 128], BF16), ("TS", [2, 64, 128, 128], BF16), ("TP8", [2, 128, 64], F32),
                             ("D1", [2, 1024, 1024], BF16), ("D2", [2, 1024, 1024], BF16)]:
        T[name] = nc.dram_tensor(name, shape, dt_, kind="Internal").ap()
    if DEBUG_H:
        T["dbg"] = nc.dram_tensor("dbg", [2, NT, D], F32, kind="ExternalOutput").ap()

    with ExitStack() as st:
        S = Sched(nc, st)
        ARENA_BYTES = 190 * 1024
        arena_t = st.enter_context(nc.sbuf_tensor("arena", [128, ARENA_BYTES], U8))
        AR = Arena(arena_t, ARENA_BYTES)
        ps = st.enter_context(nc.psum_tensor("ps", [128, 8, 512], F32))
        PSK = [("ps", b) for b in range(8)]
        uid = [0]

        def fresh(prefix):
            uid[0] += 1
            return (prefix, uid[0])

        ident_f = AR.alloc([128, 128], F32)
        ident_b = AR.alloc([128, 128], BF16)
        ones_b = AR.alloc([128, 128], BF16)
        ones_f = AR.alloc([128, 128], F32)
        bd64_b = AR.alloc([128, 128], BF16)
        mlo_b = AR.alloc([128, 4, 128], BF16)
        mhi_b = AR.alloc([128, 4, 128], BF16)
        pa_b = AR.alloc([128, 128], BF16)
        pc_b = AR.alloc([128, 128], BF16)
        cstage = AR.alloc([128, 128], F32)
        S.dma("sp", lambda e: e.dma_start(out=ident_f, in_=T["cst"][CST_IDENT]), writes=["ident_f"])
        S.op("dve", lambda e: e.tensor_copy(out=ident_b, in_=ident_f), reads=["ident_f"], writes=["ident_b"])
        S.op("dve", lambda e: e.memset(ones_b, 1.0), writes=["ones_b"])
        S.op("dve", lambda e: e.memset(ones_f, 1.0), writes=["ones_f"])
        for ci, dst, rep in [(CST_BD64, bd64_b, 0), (CST_MLO, mlo_b, 4), (CST_MHI, mhi_b, 4), (CST_PA, pa_b, 0), (CST_PC, pc_b, 0)]:
            S.dma("sp", lambda e, ci=ci: e.dma_start(out=cstage, in_=T["cst"][ci]), writes=["cstage"])
            if rep:
                for r in range(rep):
                    S.op("dve", lambda e, dst=dst, r=r: e.tensor_copy(out=dst[:, r, :], in_=cstage), reads=["cstage"], writes=["cst_misc"])
            else:
                S.op("dve", lambda e, dst=dst: e.tensor_copy(out=dst, in_=cstage), reads=["cstage"], writes=["cst_misc"])
        S.barrier()
        base_mark = AR.mark()

        def mm(out, lhsT, rhs, start, stop, reads, wkey):
            S.op("pe", lambda e: e.matmul(out, lhsT, rhs, start=start, stop=stop), reads=reads, writes=[wkey], acc=True)

        tstage = AR.alloc([128, 128], F32)
        tst2 = AR.alloc([128, 128], F32)
        S.op("dve", lambda e: e.memset(tst2, 0.0), writes=["tst2z"])
        S.barrier()
        base_mark2 = [None]

        def small_load_T(dst, src_ap, pattern, key, **kw):
            n = dst.shape[1]
            rows = src_ap.rearrange("(c q) -> c q", q=128)
            S.dma("sp", lambda e: e.dma_start(out=tstage[0:n, :], in_=rows), writes=["tstage"])
            S.op("pe", lambda e: e.transpose(ps[:, 7, 0:n], tstage[0:n, :], ident_f[0:n, 0:n]), reads=["tstage", "ident_f"], writes=[PSK[7]])
            S.op("dve", lambda e: e.tensor_copy(out=dst, in_=ps[:, 7, 0:n]), reads=[PSK[7]], writes=[key])

        def small_load_col(dst, src_ap, dh, key):
            rep = 128 // dh
            for r in range(rep):
                S.dma("sp", lambda e, r=r: e.dma_start(out=tst2[0:1, r * dh:(r + 1) * dh], in_=src_ap.rearrange("(o q) -> o q", o=1)),
                      writes=[("tst2", r)], key=("tst2", r))
            S.op("pe", lambda e: e.transpose(ps[:, 7, 0:2], tst2[0:2, :], ident_f[0:2, 0:2]),
                 reads=[("tst2", r) for r in range(rep)] + ["tst2z", "ident_f"], writes=[PSK[7], ("tst2", 0), ("tst2", 1)])
            S.op("dve", lambda e: e.tensor_copy(out=dst, in_=ps[:, 7, 0:1]), reads=[PSK[7]], writes=[key])

        base_mark = AR.mark()

        def s5_tables():
            m = AR.mark()
            TWO_PI = 2.0 * math.pi
            lr = AR.alloc([128, 64], F32)
            li = AR.alloc([128, 64], F32)
            dtc = AR.alloc([128, 1], F32)
            xx = AR.alloc([128, 64], F32)
            th = AR.alloc([128, 64], F32)
            PWr = AR.alloc([128, 16, 64], F32)
            PWi = AR.alloc([128, 16, 64], F32)
            Br = AR.alloc([128, 64, 16], F32)
            Bi = AR.alloc([128, 64, 16], F32)
            Bbr = AR.alloc([128, 64, 16], F32)
            Bbi = AR.alloc([128, 64, 16], F32)
            Cr = AR.alloc([128, 16, 64], F32)
            Ci = AR.alloc([128, 16, 64], F32)
            w = [AR.alloc([128, 1024], F32) for _ in range(4)]
            s64 = [AR.alloc([128, 64], F32) for _ in range(8)]
            ki = AR.alloc([128, 64], I32)
            PMr = AR.alloc([128, 64], F32)
            PMi = AR.alloc([128, 64], F32)
            ST = [AR.alloc([128, 16384], BF16) for _ in range(2)]

            def dv(fn, reads, writes):
                S.op("dve", fn, reads=reads, writes=writes)

            S.dma("sp", lambda e: e.dma_start(out=lr, in_=T["l1_lam_re"].rearrange("d g p -> (d g) p")), writes=["lr"])
            S.dma("sp", lambda e: e.dma_start(out=li, in_=T["l1_lam_im"].rearrange("d g p -> (d g) p")), writes=["li"])
            S.dma("sp", lambda e: e.dma_start(out=dtc, in_=T["l1_log_dt"].rearrange("d (g o) -> (d g) o", o=1)), writes=["dtc"])
            S.dma("sp", lambda e: e.dma_start(out=Br, in_=T["l1_b_re"].rearrange("d g p c -> (d g) p c")), writes=["Br"])
            S.dma("sp", lambda e: e.dma_start(out=Bi, in_=T["l1_b_im"].rearrange("d g p c -> (d g) p c")), writes=["Bi"])
            S.dma("sp", lambda e: e.dma_start(out=Cr, in_=T["l1_c_re"].rearrange("d g c p -> (d g) c p")), writes=["Cr"])
            S.dma("sp", lambda e: e.dma_start(out=Ci, in_=T["l1_c_im"].rearrange("d g c p -> (d g) c p")), writes=["Ci"])
            S.op("act", lambda e: e.activation(out=dtc, in_=dtc, func=AF.Exp), reads=["dtc"], writes=["dtc"])
            dv(lambda e: e.tensor_scalar(out=xx, in0=lr, scalar1=dtc[:, 0:1], scalar2=None, op0=ALU.mult), ["lr", "dtc"], ["xx"])
            dv(lambda e: e.tensor_scalar(out=th, in0=li, scalar1=dtc[:, 0:1], scalar2=None, op0=ALU.mult), ["li", "dtc"], ["th"])

            def sinred(out, okey, shift):
                v, kf, r, mk = s64[0], s64[1], s64[2], s64[3]
                dv(lambda e: e.tensor_scalar(out=v, in0=th, scalar1=float(shift), scalar2=None, op0=ALU.add), ["th"], ["sr_v"])
                dv(lambda e: e.tensor_scalar(out=kf, in0=v, scalar1=1.0 / TWO_PI, scalar2=None, op0=ALU.mult), ["sr_v"], ["sr_kf"])
                dv(lambda e: e.tensor_copy(out=ki, in_=kf), ["sr_kf"], ["sr_ki"])
                dv(lambda e: e.tensor_copy(out=kf, in_=ki), ["sr_ki"], ["sr_kf"])
                dv(lambda e: e.scalar_tensor_tensor(out=r, in0=kf, scalar=-TWO_PI, in1=v, op0=ALU.mult, op1=ALU.add), ["sr_kf", "sr_v"], ["sr_r"])
                dv(lambda e: e.tensor_scalar(out=mk, in0=r, scalar1=math.pi, scalar2=None, op0=ALU.is_gt), ["sr_r"], ["sr_m"])
                dv(lambda e: e.scalar_tensor_tensor(out=r, in0=mk, scalar=-TWO_PI, in1=r, op0=ALU.mult, op1=ALU.add), ["sr_m", "sr_r"], ["sr_r"])
                dv(lambda e: e.tensor_scalar(out=mk, in0=r, scalar1=-math.pi, scalar2=None, op0=ALU.is_lt), ["sr_r"], ["sr_m"])
                dv(lambda e: e.scalar_tensor_tensor(out=r, in0=mk, scalar=TWO_PI, in1=r, op0=ALU.mult, op1=ALU.add), ["sr_m", "sr_r"], ["sr_r"])
                dv(lambda e: e.tensor_scalar(out=r, in0=r, scalar1=3.1415925, scalar2=-3.1415925, op0=ALU.min, op1=ALU.max), ["sr_r"], ["sr_r"])
                S.op("act", lambda e: e.activation(out=out, in_=r, func=AF.Sin), reads=["sr_r"], writes=[okey])

            sn, cs, ex, exm = s64[4], s64[5], s64[6], s64[7]
            sinred(sn, "sn", 0.0)
            sinred(cs, "cs", math.pi / 2.0)
            S.op("act", lambda e: e.activation(out=ex, in_=xx, func=AF.Exp), reads=["xx"], writes=["ex"])
            S.op("act", lambda e: e.activation(out=exm, in_=xx, func=AF.Exp, scale=-1.0), reads=["xx"], writes=["exm"])
            PK = lambda j: ("PW", j)
            dv(lambda e: e.memset(PWr[:, 7, :], 1.0), [], [("PWr", 0)])
            dv(lambda e: e.memset(PWi[:, 7, :], 0.0), [], [("PWi", 0)])
            dv(lambda e: e.tensor_tensor(out=PWr[:, 8, :], in0=ex, in1=cs, op=ALU.mult), ["ex", "cs"], [("PWr", 1)])
            dv(lambda e: e.tensor_tensor(out=PWi[:, 8, :], in0=ex, in1=sn, op=ALU.mult), ["ex", "sn"], [("PWi", 1)])
            dv(lambda e: e.tensor_tensor(out=PWr[:, 6, :], in0=exm, in1=cs, op=ALU.mult), ["exm", "cs"], [("PWr", -1)])
            dv(lambda e: e.scalar_tensor_tensor(out=PWi[:, 6, :], in0=exm, scalar=-1.0, in1=sn, op0=ALU.mult, op1=ALU.mult), ["exm", "sn"], [("PWi", -1)])

            def cmul64(j_out, j_a, j_b):
                orr, oi = PWr[:, j_out + 7, :], PWi[:, j_out + 7, :]
                ar, ai = PWr[:, j_a + 7, :], PWi[:, j_a + 7, :]
                br, bi = PWr[:, j_b + 7, :], PWi[:, j_b + 7, :]
                ka = [("PWr", j_a), ("PWi", j_a), ("PWr", j_b), ("PWi", j_b)]
                t1, t2 = s64[0], s64[1]
                dv(lambda e: e.tensor_tensor(out=t1, in0=ar, in1=br, op=ALU.mult), ka, ["c64a"])
                dv(lambda e: e.tensor_tensor(out=t2, in0=ai, in1=bi, op=ALU.mult), ka, ["c64b"])
                dv(lambda e: e.tensor_tensor(out=orr, in0=t1, in1=t2, op=ALU.subtract), ["c64a", "c64b"], [("PWr", j_out)])
                dv(lambda e: e.tensor_tensor(out=t1, in0=ar, in1=bi, op=ALU.mult), ka, ["c64a"])
                dv(lambda e: e.tensor_tensor(out=t2, in0=ai, in1=br, op=ALU.mult), ka, ["c64b"])
                dv(lambda e: e.tensor_tensor(out=oi, in0=t1, in1=t2, op=ALU.add), ["c64a", "c64b"], [("PWi", j_out)])

            for j in range(2, 9):
                cmul64(j, j - 1, 1)
            for j in range(2, 8):
                cmul64(-j, -(j - 1), -1)
            S.dma("sp", lambda e: e.dma_start(out=T["TP8"][0], in_=PWr[:, 15, :]), reads=[("PWr", 8)], key="tp8a")
            S.dma("sp", lambda e: e.dma_start(out=T["TP8"][1], in_=PWi[:, 15, :]), reads=[("PWi", 8)], key="tp8b")
            nr, den, fr, fi = s64[0], s64[1], s64[2], s64[3]
            t5, t6 = s64[4], s64[5]
            a1 = [("PWr", 1), ("PWi", 1)]
            dv(lambda e: e.tensor_scalar(out=nr, in0=PWr[:, 8, :], scalar1=-1.0, scalar2=None, op0=ALU.add), a1, ["nr"])
            dv(lambda e: e.tensor_tensor(out=den, in0=lr, in1=lr, op=ALU.mult), ["lr"], ["den"])
            dv(lambda e: e.tensor_tensor(out=t5, in0=li, in1=li, op=ALU.mult), ["li"], ["t5"])
            dv(lambda e: e.tensor_tensor(out=den, in0=den, in1=t5, op=ALU.add), ["den", "t5"], ["den"])
            dv(lambda e: e.reciprocal(out=den, in_=den), ["den"], ["den"])
            dv(lambda e: e.tensor_tensor(out=fr, in0=nr, in1=lr, op=ALU.mult), ["nr", "lr"], ["fr"])
            dv(lambda e: e.tensor_tensor(out=t5, in0=PWi[:, 8, :], in1=li, op=ALU.mult), a1 + ["li"], ["t5"])
            dv(lambda e: e.tensor_tensor(out=fr, in0=fr, in1=t5, op=ALU.add), ["fr", "t5"], ["fr"])
            dv(lambda e: e.tensor_tensor(out=fr, in0=fr, in1=den, op=ALU.mult), ["fr", "den"], ["fr"])
            dv(lambda e: e.tensor_tensor(out=fi, in0=PWi[:, 8, :], in1=lr, op=ALU.mult), a1 + ["lr"], ["fi"])
            dv(lambda e: e.tensor_tensor(out=t6, in0=nr, in1=li, op=ALU.mult), ["nr", "li"], ["t6"])
            dv(lambda e: e.tensor_tensor(out=fi, in0=fi, in1=t6, op=ALU.subtract), ["fi", "t6"], ["fi"])
            dv(lambda e: e.tensor_tensor(out=fi, in0=fi, in1=den, op=ALU.mult), ["fi", "den"], ["fi"])

            def bc_b(a):
                return a.unsqueeze(2).broadcast_to([128, 64, 16])

            def bc_c(a):
                return a.unsqueeze(1).broadcast_to([128, 16, 64])

            w3b = [x.rearrange("q (p c) -> q p c", c=16) for x in w]
            w3c = [x.rearrange("q (c p) -> q c p", p=64) for x in w]

            def cmul_big(pr, pi, pk, xr, xi, xk, bc, wv, out_r, out_i, okey, neg_im):
                t1, t2, t3, t4 = wv
                dv(lambda e: e.tensor_tensor(out=t1, in0=xr, in1=bc(pr), op=ALU.mult), pk + xk, ["cb1"])
                dv(lambda e: e.tensor_tensor(out=t2, in0=xi, in1=bc(pi), op=ALU.mult), pk + xk, ["cb2"])
                dv(lambda e: e.tensor_tensor(out=out_r, in0=t1, in1=t2, op=ALU.subtract), ["cb1", "cb2"], [okey])
                dv(lambda e: e.tensor_tensor(out=t3, in0=xi, in1=bc(pr), op=ALU.mult), pk + xk, ["cb3"])
                dv(lambda e: e.tensor_tensor(out=t4, in0=xr, in1=bc(pi), op=ALU.mult), pk + xk, ["cb4"])
                if neg_im:
                    dv(lambda e: e.scalar_tensor_tensor(out=out_i, in0=t3, scalar=-1.0, in1=t4, op0=ALU.mult, op1=ALU.subtract), ["cb3", "cb4"], [okey])
                else:
                    dv(lambda e: e.tensor_tensor(out=out_i, in0=t3, in1=t4, op=ALU.add), ["cb3", "cb4"], [okey])

            cmul_big(fr, fi, ["fr", "fi"], Br, Bi, ["Br", "Bi"], bc_b, w3b, Bbr, Bbi, "Bb", False)

            def mixed_power(jf, jb):
                dv(lambda e: e.tensor_copy(out=PMr[0:64], in_=PWr[0:64, jf + 7, :]), [("PWr", jf)], ["PM"])
                dv(lambda e: e.tensor_copy(out=PMi[0:64], in_=PWi[0:64, jf + 7, :]), [("PWi", jf)], ["PM"])
                dv(lambda e: e.tensor_copy(out=PMr[64:128], in_=PWr[64:128, jb + 7, :]), [("PWr", jb)], ["PM"])
                dv(lambda e: e.tensor_copy(out=PMi[64:128], in_=PWi[64:128, jb + 7, :]), [("PWi", jb)], ["PM"])

            for ti, (side, pf, pb_, dst, layout) in enumerate([
                    ("B", lambda t: -t, lambda t: t, "TX", "ript"),
                    ("C", lambda t: t, lambda t: -t, "TZ", "ript"),
                    ("C", lambda t: t + 1, lambda t: 8 - t, "TE", "ript"),
                    ("B", lambda t: 7 - t, lambda t: t, "TS", "ritp")]):
                st_ = ST[ti % 2]
                stk = ("ST", ti % 2)
                if layout == "ript":
                    st5 = st_.rearrange("q (r p t c) -> q r p t c", r=2, p=64, t=8)
                else:
                    st5 = st_.rearrange("q (r t c p) -> q r t c p", r=2, t=8, c=16)
                for t in range(8):
                    mixed_power(pf(t), pb_(t))
                    if side == "B":
                        if layout == "ript":
                            o_r, o_i = st5[:, 0, :, t, :], st5[:, 1, :, t, :]
                        else:
                            o_r = st5[:, 0, t, :, :].rearrange("q c p -> q p c")
                            o_i = st5[:, 1, t, :, :].rearrange("q c p -> q p c")
                        cmul_big(PMr, PMi, ["PM"], Bbr, Bbi, ["Bb"], bc_b, w3b, o_r, o_i, stk, False)
                    else:
                        o_r = st5[:, 0, :, t, :].rearrange("q p c -> q c p")
                        o_i = st5[:, 1, :, t, :].rearrange("q p c -> q c p")
                        cmul_big(PMr, PMi, ["PM"], Cr, Ci, ["Cr", "Ci"], bc_c, w3c, o_r, o_i, stk, True)
                if dst in ("TX", "TZ"):
                    S.dma("sp", lambda e, st_=st_, dst=dst: e.dma_start(out=T[dst], in_=st_), reads=[stk], writes=[dst], key=stk)
                elif dst == "TE":
                    st4 = st_.rearrange("q (r p n) -> q r p n", r=2, p=64)
                    for d in range(2):
                        for r_ in range(2):
                            S.dma("sp", lambda e, d=d, r_=r_, st4=st4: e.dma_start(
                                out=T["TE"][r_, :, d * 64:(d + 1) * 64, :], in_=st4[d * 64:(d + 1) * 64, r_]), reads=[stk], writes=["TE"], key=stk)
                else:
                    st4 = st_.rearrange("q (r n p) -> q r n p", r=2, p=64)
                    for d in range(2):
                        for r_ in range(2):
                            for g8 in range(8):
                                S.dma("sp", lambda e, d=d, r_=r_, g8=g8, st4=st4: e.dma_start(
                                    out=T["TS"][r_, g8 * 8:(g8 + 1) * 8, :, d * 64:(d + 1) * 64],
                                    in_=st4[d * 64 + g8 * 8:d * 64 + (g8 + 1) * 8, r_]), reads=[stk], writes=["TS"], key=stk)
            S.barrier()
            AR.release(m)
            m = AR.mark()
            mkf = AR.alloc([128, 128], F32)
            mkb = AR.alloc([128, 128], F32)
            r16 = AR.alloc([128, 128], F32)
            dgc = AR.alloc([128, 16], F32)
            dmat = AR.alloc([128, 64], F32)
            dcol = AR.alloc([128, 64], F32)
            S.dma("sp", lambda e: e.dma_start(out=mkf, in_=T["cst"][CST_S5F]), writes=["mkf"])
            S.dma("sp", lambda e: e.dma_start(out=mkb, in_=T["cst"][CST_S5B]), writes=["mkb"])
            S.dma("sp", lambda e: e.dma_start(out=r16, in_=T["cst"][CST_R16]), writes=["r16"])
            S.dma("sp", lambda e: e.dma_start(out=dgc[0:64, :], in_=T["l1_d_skip"].rearrange("(g c) -> g c", c=16)), writes=["dgc"])
            S.op("pe", lambda e: e.transpose(ps[0:16, 7, 0:64], dgc[0:64, :], ident_f[0:64, 0:64]), reads=["dgc", "ident_f"], writes=[PSK[7]])
            S.op("dve", lambda e: e.tensor_copy(out=dmat[0:16, :], in_=ps[0:16, 7, 0:64]), reads=[PSK[7]], writes=["dmat"])
            S.op("pe", lambda e: e.matmul(ps[:, 7, 64:128], r16[0:16, :], dmat[0:16, :], start=True, stop=True), reads=["r16", "dmat"], writes=[PSK[7]])
            S.op("dve", lambda e: e.tensor_copy(out=dcol, in_=ps[:, 7, 64:128]), reads=[PSK[7]], writes=["dcol"])
            XL = [[AR.alloc([128, 16, 128], BF16) for _ in range(2)] for _ in range(2)]
            ZL = [[AR.alloc([128, 16, 128], BF16) for _ in range(2)] for _ in range(2)]
            Mst = [AR.alloc([128, 16, 128], BF16) for _ in range(2)]
            tA = [AR.alloc([128, 128], F32) for _ in range(2)]
            tB = [AR.alloc([128, 128], F32) for _ in range(2)]
            TX4 = T["TX"].rearrange("(d g) (k n) -> d g k n", d=2, k=128)
            TZ4 = T["TZ"].rearrange("(d g) (k n) -> d g k n", d=2, k=128)
            for gb in range(4):
                bf = gb % 2
                for d in range(2):
                    S.dma("sp", lambda e, d=d, gb=gb, bf=bf: e.dma_start(out=XL[bf][d], in_=TX4[d, gb * 16:(gb + 1) * 16].rearrange("g k n -> k g n")),
                          reads=["TX"], writes=[("XL", bf, d)])
                    S.dma("sp", lambda e, d=d, gb=gb, bf=bf: e.dma_start(out=ZL[bf][d], in_=TZ4[d, gb * 16:(gb + 1) * 16].rearrange("g k n -> k g n")),
                          reads=["TZ"], writes=[("ZL", bf, d)])
                for gi in range(16):
                    g = gb * 16 + gi
                    b = gi % 4
                    for d in range(2):
                        S.op("pe", lambda e, b=b, d=d, bf=bf, gi=gi: e.matmul(ps[:, b, d * 128:(d + 1) * 128], XL[bf][d][:, gi, :], ZL[bf][d][:, gi, :],
                                                                            start=True, stop=True),
                             reads=[("XL", bf, d), ("ZL", bf, d)], writes=[PSK[b]], acc=True)
                    a_, b_ = tA[gi % 2], tB[gi % 2]
                    S.op("dve", lambda e, a_=a_, b=b: e.tensor_tensor(out=a_, in0=ps[:, b, 0:128], in1=mkf, op=ALU.mult), reads=[PSK[b], "mkf"], writes=[("tA", gi % 2)])
                    S.op("dve", lambda e, b_=b_, b=b: e.tensor_tensor(out=b_, in0=ps[:, b, 128:256], in1=mkb, op=ALU.mult), reads=[PSK[b], "mkb"], writes=[("tB", gi % 2)])
                    S.op("dve", lambda e, a_=a_, b_=b_: e.tensor_tensor(out=a_, in0=a_, in1=b_, op=ALU.add), reads=[("tA", gi % 2), ("tB", gi % 2)], writes=[("tA", gi % 2)])
                    S.op("dve", lambda e, a_=a_, g=g, gi=gi, bf=bf: e.scalar_tensor_tensor(out=Mst[bf][:, gi, :], in0=ident_f, scalar=dcol[:, g:g + 1], in1=a_,
                                                                                         op0=ALU.mult, op1=ALU.add),
                         reads=[("tA", gi % 2), "dcol", "ident_f"], writes=[("Mst", bf)])
                S.dma("sp", lambda e, gb=gb, bf=bf: e.dma_start(out=T["TM"][gb * 16:(gb + 1) * 16].rearrange("g k n -> k g n"), in_=Mst[bf]),
                      reads=[("Mst", bf)], writes=["TM"], key=("Mst", bf))
            S.barrier()
            AR.release(m)

        def run_path(p):
            nseq, slen = (4, 256) if p == 0 else (1, 1024)
            AR.release(base_mark)
            hT = AR.alloc([128, 8, NT], F32)
            xnT = AR.alloc([128, 8, NT], BF16)
            modT = AR.alloc([128, 48], F32)
            bmodT = AR.alloc([128, 48], F32)
            n1T = AR.alloc([128, 8], F32)
            n2T = AR.alloc([128, 8], F32)
            A1 = AR.alloc([128, 8], F32)
            A2 = AR.alloc([128, 8], F32)
            condT = AR.alloc([128, 8], F32)
            sT = AR.alloc([128, 8], BF16)
            rstd = AR.alloc([128, NT], F32)
            lnv = AR.alloc([128, 512], F32)
            path_mark = AR.mark()
            HK = [("hT", kc) for kc in range(8)]
            XK = [("xnT", kc) for kc in range(8)]

            m0 = AR.mark()
            xst = [AR.alloc([128, D], F32) for _ in range(2)]
            for tt in range(8):
                xs = xst[tt % 2]
                xk = ("xst", tt % 2)
                S.dma("sp", lambda e, xs=xs, tt=tt: e.dma_start(out=xs, in_=T["xin"][p, tt * 128:(tt + 1) * 128, :]), writes=[xk])
                for hf in range(2):
                    b = (tt * 2 + hf) % 8
                    for q in range(4):
                        kc = hf * 4 + q
                        S.op("pe", lambda e, b=b, q=q, kc=kc, xs=xs: e.transpose(ps[:, b, q * 128:(q + 1) * 128], xs[:, kc * 128:(kc + 1) * 128], ident_f),
                             reads=[xk, "ident_f"], writes=[PSK[b]], acc=True)
                    S.op("act" if hf else "dve",
                         (lambda e, b=b, hf=hf, tt=tt: e.activation(out=hT[:, hf * 4:hf * 4 + 4, tt * 128:(tt + 1) * 128],
                                                                     in_=ps[:, b, :].rearrange("p (a b) -> p a b", a=4), func=AF.Copy)) if hf else
                         (lambda e, b=b, hf=hf, tt=tt: e.tensor_copy(out=hT[:, hf * 4:hf * 4 + 4, tt * 128:(tt + 1) * 128],
                                                                      in_=ps[:, b, :].rearrange("p (a b) -> p a b", a=4))),
                         reads=[PSK[b]], writes=HK[hf * 4:hf * 4 + 4])
            S.barrier()
            AR.release(m0)
            small_load_T(condT, T["cond"][p], "(kc q) -> q kc", "condT", q=128)
            S.op("act", lambda e: e.activation(out=sT, in_=condT, func=AF.Silu), reads=["condT"], writes=["sT"])

            def compute_mod(l):
                m = AR.mark()
                small_load_T(bmodT, T["b_mod"][l], "(c q) -> q c", "bmodT", q=128)
                small_load_T(n1T, T["norm1"][l], "(kc q) -> q kc", "n1T", q=128)
                small_load_T(n2T, T["norm2"][l], "(kc q) -> q kc", "n2T", q=128)
                wm = [AR.alloc([128, 8, 512], BF16) for _ in range(2)]
                for blk in range(12):
                    w = wm[blk % 2]
                    wk = ("wm", blk % 2)
                    S.dma("pool", lambda e, w=w, blk=blk: e.dma_start(
                        out=w, in_=T["w_mod"][l][:, blk * 512:(blk + 1) * 512].rearrange("(kc q) n -> q kc n", q=128)), writes=[wk])
                    for cc in range(4):
                        col = blk * 4 + cc
                        for kc in range(8):
                            mm(ps[:, 0, col:col + 1], w[:, kc, cc * 128:(cc + 1) * 128], sT[:, kc:kc + 1], kc == 0, kc == 7,
                               [wk, "sT"], PSK[0])
                S.op("dve", lambda e: e.tensor_tensor(out=modT, in0=ps[:, 0, 0:48], in1=bmodT, op=ALU.add),
                     reads=[PSK[0], "bmodT"], writes=["modT"])
                S.op("dve", lambda e: e.scalar_tensor_tensor(out=A1, in0=modT[:, 8:16], scalar=1.0, in1=n1T, op0=ALU.add, op1=ALU.mult),
                     reads=["modT", "n1T"], writes=["A1"])
                S.op("dve", lambda e: e.scalar_tensor_tensor(out=A2, in0=modT[:, 32:40], scalar=1.0, in1=n2T, op0=ALU.add, op1=ALU.mult),
                     reads=["modT", "n2T"], writes=["A2"])
                AR.release(m)

            def norm_mod(Acol, Akey, Boff, perm8=False):
                m = AR.mark()
                sq = AR.alloc([128, 8, NT], BF16)
                tmp = [AR.alloc([128, NT], F32) for _ in range(2)]
                S.op("act", lambda e: e.activation(out=sq, in_=hT, func=AF.Square), reads=HK, writes=["sq"])
                for hf in range(2):
                    b = hf
                    for kc in range(8):
                        mm(ps[:, b, :], ones_b, sq[:, kc, hf * 512:(hf + 1) * 512], kc == 0, kc == 7, ["sq", "ones_b"], PSK[b])
                    S.op("act", lambda e, b=b: e.activation(out=lnv, in_=ps[:, b, :], func=AF.Ln, bias=EPS, scale=1.0 / D),
                         reads=[PSK[b]], writes=["lnv"])
                    S.op("act", lambda e, hf=hf: e.activation(out=rstd[:, hf * 512:(hf + 1) * 512], in_=lnv, func=AF.Exp, scale=-0.5),
                         reads=["lnv"], writes=[("rstd", hf)])
                for kc in range(8):
                    t = tmp[kc % 2]
                    tk = ("nm_tmp", kc % 2)
                    S.op("dve", lambda e, t=t, kc=kc: e.scalar_tensor_tensor(out=t, in0=hT[:, kc, :], scalar=Acol[:, kc:kc + 1], in1=rstd,
                                                                              op0=ALU.mult, op1=ALU.mult),
                         reads=[HK[kc], Akey, ("rstd", 0), ("rstd", 1)], writes=[tk])
                    if perm8:
                        o = xnT[:, kc, :].rearrange("p (t k) -> p k t", t=8)
                    else:
                        o = xnT[:, kc, :]
                    S.op("act", lambda e, t=t, kc=kc, o=o: e.activation(out=o, in_=t, func=AF.Identity, bias=modT[:, Boff + kc:Boff + kc + 1]),
                         reads=[tk, "modT"], writes=[XK[kc]])
                S.barrier()
                AR.release(m)

            def resid(oc, hf, b, goff):
                S.op("dve", lambda e: e.scalar_tensor_tensor(out=hT[:, oc, hf * 512:(hf + 1) * 512], in0=ps[:, b, :],
                                                              scalar=modT[:, goff + oc:goff + oc + 1],
                                                              in1=hT[:, oc, hf * 512:(hf + 1) * 512], op0=ALU.mult, op1=ALU.add),
                     reads=[PSK[b], "modT", HK[oc]], writes=[HK[oc]])

            def ffn(l):
                m = AR.mark()
                norm_mod(A2, "A2", 24)
                gT = AR.alloc([128, 22, NT], BF16)
                wd = AR.alloc([128, 22, D], BF16)
                bupT = AR.alloc([128, 44], F32)
                cbT = AR.alloc([128, 44], F32)
                ckT = AR.alloc([128, 3, 44], F32)
                wu = [AR.alloc([128, 2, 8, 256], BF16) for _ in range(2)]
                hb = [AR.alloc([128, nseq, slen + 2], F32) for _ in range(2)]
                tc = [AR.alloc([128, nseq, slen], F32) for _ in range(2)]
                sg = AR.alloc([128, nseq, slen], F32)
                small_load_T(bupT, T["b_up"][l], "(c q) -> q c", "bupT", q=128)
                small_load_T(cbT, T["conv_b"][l], "(c q) -> q c", "cbT", q=128)
                for t3 in range(3):
                    small_load_T(ckT[:, t3, :], T["conv_k"][l, t3], "(c q) -> q c", ("ckT", t3), q=128)
                CK = [("ckT", t3) for t3 in range(3)]
                for i in range(2):
                    S.op("pool", lambda e, i=i: e.memset(hb[i], 0.0), writes=[("hb", i)])
                for j4 in range(0, 22, 6):
                    je = min(22, j4 + 6)
                    S.dma("pool", lambda e, j4=j4, je=je: e.dma_start(
                        out=wd[:, j4:je, :], in_=T["w_down"][l][j4 * 128:je * 128, :].rearrange("(j q) n -> q j n", q=128)),
                        writes=[("wd", j4)])
                WDK = [("wd", j4) for j4 in range(0, 22, 6)]
                for jb in range(11):
                    w = wu[jb % 2]
                    wk = ("wu", jb % 2)
                    for gv in range(2):
                        c0 = gv * DFF + jb * 256
                        S.dma("pool", lambda e, w=w, gv=gv, c0=c0: e.dma_start(
                            out=w[:, gv], in_=T["w_up"][l][:, c0:c0 + 256].rearrange("(kc q) n -> q kc n", q=128)),
                            writes=[(wk, gv)], key=(wk, gv))
                    for sub in range(2):
                        j = jb * 2 + sub
                        bs = (j % 2) * 4
                        for gv in range(2):
                            for hf in range(2):
                                b = bs + gv * 2 + hf
                                for kc in range(8):
                                    mm(ps[:, b, :], w[:, gv, kc, sub * 128:(sub + 1) * 128], xnT[:, kc, hf * 512:(hf + 1) * 512],
                                       kc == 0, kc == 7, [(wk, gv), XK[kc]], PSK[b])
                        for gv in range(2):
                            b = bs + gv * 2
                            col = gv * 22 + j
                            h_ = hb[gv]
                            t_ = tc[gv]
                            pin = ps[:, b:b + 2, :].rearrange("p b (s t) -> p (b s) t", t=slen) if nseq == 4 else \
                                ps[:, b:b + 2, :].rearrange("p (o b) t -> p o (b t)", o=1)
                            S.op("act", lambda e, h_=h_, pin=pin, col=col: e.activation(out=h_[:, :, 1:slen + 1], in_=pin, func=AF.Identity,
                                                                                        bias=bupT[:, col:col + 1]),
                                 reads=[PSK[b], PSK[b + 1], "bupT"], writes=[("hb", gv)])
                            S.op("act", lambda e, h_=h_, t_=t_, col=col: e.activation(out=t_, in_=h_[:, :, 1:slen + 1], func=AF.Identity,
                                                                                      bias=cbT[:, col:col + 1], scale=ckT[:, 1, col:col + 1]),
                                 reads=[("hb", gv), "cbT", CK[1]], writes=[("tc", gv)])
                            S.op("dve", lambda e, h_=h_, t_=t_, col=col: e.scalar_tensor_tensor(out=t_, in0=h_[:, :, 0:slen], scalar=ckT[:, 0, col:col + 1],
                                                                                                in1=t_, op0=ALU.mult, op1=ALU.add),
                                 reads=[("hb", gv), CK[0], ("tc", gv)], writes=[("tc", gv)])
                            S.op("dve", lambda e, h_=h_, t_=t_, col=col: e.scalar_tensor_tensor(out=t_, in0=h_[:, :, 2:slen + 2], scalar=ckT[:, 2, col:col + 1],
                                                                                                in1=t_, op0=ALU.mult, op1=ALU.add),
                                 reads=[("hb", gv), CK[2], ("tc", gv)], writes=[("tc", gv)])
                        S.op("act", lambda e: e.activation(out=sg, in_=tc[0], func=AF.Silu), reads=[("tc", 0)], writes=["sg"])
                        go = gT[:, j, :].rearrange("p (s t) -> p s t", s=nseq)
                        S.op("dve", lambda e, go=go: e.tensor_tensor(out=go, in0=sg, in1=tc[1], op=ALU.mult),
                             reads=["sg", ("tc", 1)], writes=[("gT", j)])
                bi = 0
                for oc in range(8):
                    for hf in range(2):
                        b = bi % 8
                        bi += 1
                        for j in range(22):
                            mm(ps[:, b, :], wd[:, j, oc * 128:(oc + 1) * 128], gT[:, j, hf * 512:(hf + 1) * 512], j == 0, j == 21,
                               [WDK[j // 6], ("gT", j)], PSK[b])
                        resid(oc, hf, b, 40)
                AR.release(m)

            def attention(l):
                L = LAYERS[l]
                dh, H, ckv = L["dh"], L["H"], L["ckv"]
                G = H // 4
                nkc = ckv // 128
                ctot = D + 2 * ckv
                scale = dh ** -0.5
                rope = (p == 1)
                m = AR.mark()
                norm_mod(A1, "A1", 0)
                if ATT_STOP <= -1:
                    AR.release(m)
                    return
                wbig = AR.alloc([128, 8, ctot], BF16)
                qT = AR.alloc([128, 8, NT], BF16)
                kT = AR.alloc([128, nkc, NT], BF16)
                vx = AR.alloc([128, 8, 4, dh + 2], BF16)
                qn_c = AR.alloc([128, 1], F32)
                kn_c = AR.alloc([128, 1], F32)
                sqb = [AR.alloc([128, 512], BF16) for _ in range(2)]
                rs = [AR.alloc([128, 512], F32) for _ in range(2)]
                qf = [AR.alloc([128, 512], F32) for _ in range(2)]
                S.dma("pool", lambda e: e.dma_start(out=wbig[:, :, 0:768], in_=T[L["wqkv"]][:, 0:768].rearrange("(kc q) n -> q kc n", q=128)),
                      writes=[("wbig", 0)])
                S.dma("pool", lambda e: e.dma_start(out=wbig[:, :, 768:ctot], in_=T[L["wqkv"]][:, 768:ctot].rearrange("(kc q) n -> q kc n", q=128)),
                      writes=[("wbig", 1)])
                WB = [("wbig", 0), ("wbig", 1)]
                small_load_col(qn_c, T[L["qn"]], dh, "qn_c")
                small_load_col(kn_c, T[L["kn"]], dh, "kn_c")
                QNK = ["qn_c"]
                KNK = ["kn_c"]
                if rope:
                    cosT = AR.alloc([128, NT], F32)
                    sinT = AR.alloc([128, NT], F32)
                    ri = 0 if dh == 64 else 2
                    S.dma("sp", lambda e: e.dma_start(out=cosT, in_=T["rope"][ri]), writes=["cosT"])
                    S.dma("sp", lambda e: e.dma_start(out=sinT, in_=T["rope"][ri + 1]), writes=["sinT"])
                    pm = pa_b if dh == 64 else pc_b
                    r1 = [AR.alloc([128, 512], F32) for _ in range(2)]
                    qb16 = [AR.alloc([128, 512], BF16) for _ in range(2)]
                esink = None
                if L["sink"]:
                    esink = AR.alloc([128, 16], F32)
                    S.dma("sp", lambda e: e.dma_start(out=tst2[0:1, 0:16], in_=T[L["sink"]].rearrange("(o q) -> o q", o=1)),
                          writes=[("tst2", 0)], key=("tst2", 0))
                    S.op("pe", lambda e: e.matmul(ps[:, 7, 0:16], ones_f[0:2, :], tst2[0:2, 0:16], start=True, stop=True),
                         reads=[("tst2", 0), "tst2z", "ones_f"], writes=[PSK[7], ("tst2", 0)])
                    S.op("act", lambda e: e.activation(out=esink, in_=ps[:, 7, 0:16], func=AF.Exp), reads=[PSK[7]], writes=["esink"])
                if p == 0:
                    kst = [AR.alloc([128, ckv], F32) for _ in range(2)]
                    vst = [AR.alloc([128, ckv], F32) for _ in range(2)]
                    kf32 = AR.alloc([128, nkc, NT], F32)
                S.op("pool", lambda e: e.memset(vx, 1.0), writes=["vx_init"])

                if ATT_STOP <= 0:
                    AR.release(m)
                    return
                for tt in range(8):
                    b = 4 + tt % 2
                    for kc in range(8):
                        mm(ps[:, b, 0:ckv], xnT[:, kc, tt * 128:(tt + 1) * 128], wbig[:, kc, D + ckv:D + 2 * ckv], kc == 0, kc == 7,
                           [XK[kc], WB[1]], PSK[b])
                    if p == 0:
                        v_ = vst[tt % 2]
                        vk = ("vst", tt % 2)
                        S.op("dve", lambda e, b=b, v_=v_: e.tensor_copy(out=v_, in_=ps[:, b, 0:ckv]), reads=[PSK[b]], writes=[vk])
                        S.op("act", lambda e, v_=v_, tt=tt: e.activation(out=vx[:, tt, :, 0:dh], in_=v_.rearrange("p (g d) -> p g d", g=4),
                                                                        func=AF.Copy),
                             reads=[vk, "vx_init"], writes=[("vx", tt)])
                        S.dma("sp", lambda e, v_=v_, tt=tt: e.dma_start(out=T[L["nv"]][tt * 128:(tt + 1) * 128, :], in_=v_), reads=[vk])
                    else:
                        S.op("act", lambda e, b=b, tt=tt: e.activation(out=vx[:, tt, :, 0:dh], in_=ps[:, b, 0:ckv].rearrange("p (g d) -> p g d", g=4),
                                                                       func=AF.Copy),
                             reads=[PSK[b], "vx_init"], writes=[("vx", tt)])
                if ATT_STOP <= 1:
                    AR.release(m)
                    return
                def lhs_cols(kind, c):
                    if kind == "q":
                        if dh == 128:
                            return [(c * 128, 128, 0)]
                        mq, i = c // 4, c % 4
                        return [((8 * mq + i) * 64, 64, 0), ((8 * mq + 4 + i) * 64, 64, 64)]
                    return [(D + c * 128, 128, 0)]

                ci = 0
                for kind, nch in (("k", nkc), ("q", 8)):
                    for c in range(nch):
                        for hf in range(2):
                            b = ci % 2
                            b2 = 2 + ci % 2
                            sl = slice(hf * 512, (hf + 1) * 512)
                            cols = lhs_cols(kind, c)
                            for (c0, w_, po) in cols:
                                for kc in range(8):
                                    mm(ps[po:po + w_, b, :], wbig[:, kc, c0:c0 + w_], xnT[:, kc, sl], kc == 0, kc == 7,
                                       [XK[kc], WB[0], WB[1]], PSK[b])
                            sq_ = sqb[ci % 2]
                            sk = ("sqb", ci % 2)
                            S.op("act", lambda e, sq_=sq_, b=b: e.activation(out=sq_, in_=ps[:, b, :], func=AF.Square), reads=[PSK[b]], writes=[sk])
                            mm(ps[:, b2, :], bd64_b if dh == 64 else ones_b, sq_, True, True, [sk, "cst_misc", "ones_b"], PSK[b2])
                            rs_ = rs[ci % 2]
                            rk = ("rs", ci % 2)
                            S.op("act", lambda e, b2=b2: e.activation(out=lnv, in_=ps[:, b2, :], func=AF.Ln, bias=EPS, scale=1.0 / dh),
                                 reads=[PSK[b2]], writes=["lnv"])
                            S.op("act", lambda e, rs_=rs_: e.activation(out=rs_, in_=lnv, func=AF.Exp, scale=-0.5), reads=["lnv"], writes=[rk])
                            gcol, gk = (qn_c, QNK) if kind == "q" else (kn_c, KNK)
                            dstT = qT if kind == "q" else kT
                            dk = (kind + "T", c, hf)
                            if not rope:
                                if kind == "k":
                                    S.op("dve", lambda e, c=c, sl=sl, b=b, rs_=rs_: e.scalar_tensor_tensor(
                                        out=kf32[:, c, sl], in0=ps[:, b, :], scalar=kn_c[:, 0:1], in1=rs_, op0=ALU.mult, op1=ALU.mult),
                                        reads=[PSK[b], rk] + gk, writes=[("kf32", c, hf)])
                                    S.op("pool", lambda e, c=c, sl=sl: e.tensor_copy(out=kT[:, c, sl], in_=kf32[:, c, sl]),
                                         reads=[("kf32", c, hf)], writes=[dk])
                                else:
                                    S.op("dve", lambda e, c=c, sl=sl, b=b, rs_=rs_, gcol=gcol, dstT=dstT: e.scalar_tensor_tensor(
                                        out=dstT[:, c, sl], in0=ps[:, b, :], scalar=gcol[:, 0:1], in1=rs_, op0=ALU.mult, op1=ALU.mult),
                                        reads=[PSK[b], rk] + gk, writes=[dk])
                            else:
                                q_ = qf[ci % 2]
                                qk_ = ("qf", ci % 2)
                                qb_ = qb16[ci % 2]
                                qbk = ("qb16", ci % 2)
                                r_ = r1[ci % 2]
                                r1k = ("r1", ci % 2)
                                S.op("dve", lambda e, q_=q_, b=b, rs_=rs_, gcol=gcol: e.scalar_tensor_tensor(
                                    out=q_, in0=ps[:, b, :], scalar=gcol[:, 0:1], in1=rs_, op0=ALU.mult, op1=ALU.mult),
                                    reads=[PSK[b], rk] + gk, writes=[qk_])
                                S.op("pool", lambda e, q_=q_, qb_=qb_: e.tensor_copy(out=qb_, in_=q_), reads=[qk_], writes=[qbk])
                                mm(ps[:, b, :], pm, qb_, True, True, [qbk, "cst_misc"], PSK[b])
                                S.op("dve", lambda e, r_=r_, b=b, sl=sl: e.tensor_tensor(out=r_, in0=ps[:, b, :], in1=sinT[:, sl], op=ALU.mult),
                                     reads=[PSK[b], "sinT"], writes=[r1k])
                                S.op("pool", lambda e, q_=q_, sl=sl: e.tensor_tensor(out=q_, in0=q_, in1=cosT[:, sl], op=ALU.mult),
                                     reads=[qk_, "cosT"], writes=[qk_])
                                S.op("dve", lambda e, q_=q_, r_=r_, c=c, sl=sl, dstT=dstT: e.tensor_tensor(out=dstT[:, c, sl], in0=q_, in1=r_, op=ALU.add),
                                     reads=[qk_, r1k], writes=[dk])
                            ci += 1
                if ATT_STOP <= 2:
                    AR.release(m)
                    return
                if p == 0:
                    for tt in range(8):
                        b = 6 + tt % 2
                        for c in range(nkc):
                            S.op("pe", lambda e, b=b, c=c, tt=tt: e.transpose(ps[:, b, c * 128:(c + 1) * 128], kf32[:, c, tt * 128:(tt + 1) * 128], ident_f),
                                 reads=[("kf32", c, tt // 4), "ident_f"], writes=[PSK[b]], acc=True)
                        k_ = kst[tt % 2]
                        kk = ("kst", tt % 2)
                        S.op("dve", lambda e, b=b, k_=k_: e.tensor_copy(out=k_, in_=ps[:, b, 0:ckv]), reads=[PSK[b]], writes=[kk])
                        S.dma("sp", lambda e, k_=k_, tt=tt: e.dma_start(out=T[L["nk"]][tt * 128:(tt + 1) * 128, :], in_=k_), reads=[kk])

                if p == 1:
                    kcT = AR.alloc([128, nkc, 512], BF16)
                    vcx = AR.alloc([128, 4, 4, dh + 2], BF16)
                    cst_ = [AR.alloc([128, ckv], F32) for _ in range(2)]
                    cb_ = [AR.alloc([128, ckv], BF16) for _ in range(2)]
                    S.op("pool", lambda e: e.memset(vcx, 1.0), writes=["vcx_init"])
                    for t4 in range(4):
                        c_ = cst_[t4 % 2]
                        ck_ = ("cst_", t4 % 2)
                        S.dma("sp", lambda e, c_=c_, t4=t4: e.dma_start(out=c_, in_=T[L["cv"]][t4 * 128:(t4 + 1) * 128, :]), writes=[ck_])
                        S.op("dve", lambda e, c_=c_, t4=t4: e.tensor_copy(out=vcx[:, t4, :, 0:dh], in_=c_.rearrange("p (g d) -> p g d", g=4)),
                             reads=[ck_, "vcx_init"], writes=[("vcx", t4)])
                    for t4 in range(4):
                        c_ = cst_[t4 % 2]
                        ck_ = ("cst_", t4 % 2)
                        bb = cb_[t4 % 2]
                        bk = ("cb_", t4 % 2)
                        S.dma("sp", lambda e, c_=c_, t4=t4: e.dma_start(out=c_, in_=T[L["ck"]][t4 * 128:(t4 + 1) * 128, :]), writes=[ck_])
                        S.op("dve", lambda e, c_=c_, bb=bb: e.tensor_copy(out=bb, in_=c_), reads=[ck_], writes=[bk])
                        b = 6 + t4 % 2
                        pb = ps[:, b, 0:64 * nkc].bitcast(BF16).rearrange("p (c t) -> p c t", c=nkc)
                        for c in range(nkc):
                            S.op("pe", lambda e, pb=pb, c=c, bb=bb: e.transpose(pb[:, c, :], bb[:, c * 128:(c + 1) * 128], ident_b),
                                 reads=[bk, "ident_b"], writes=[PSK[b]], acc=True)
                        S.op("dve", lambda e, pb=pb, t4=t4: e.tensor_copy(out=kcT[:, :, t4 * 128:(t4 + 1) * 128], in_=pb),
                             reads=[PSK[b]], writes=[("kcT", t4)])

                if ATT_STOP <= 3:
                    AR.release(m)
                    return
                S.dma("pool", lambda e: e.dma_start(out=wbig[:, :, 0:D], in_=T[L["wo"]].rearrange("(kc q) n -> q kc n", q=128)),
                      reads=[], writes=[("wbig", 0), ("wbig", 1)], key=("wbig", 0))

                nkt_max = 7 if L["kind"] == "A" else 12
                PT = [AR.alloc([128, nkt_max, G * 128], BF16) for _ in range(2)]
                otok = [AR.alloc([128, D], BF16) for _ in range(2)]
                OT = xnT
                den = AR.alloc([128, 4], F32)
                rden = AR.alloc([128, 4], F32)
                it = 0
                sb_i = 0
                for qt in range(8):
                    s_ = qt // (slen // 128)
                    ql = qt % (slen // 128)
                    kts = []
                    if p == 1:
                        kts += [("c", t4, None) for t4 in range(4)]
                        if L["kind"] == "A":
                            if qt > 0:
                                kts.append(("n", qt - 1, mlo_b))
                            kts.append(("n", qt, None))
                            if qt < 7:
                                kts.append(("n", qt + 1, mhi_b))
                        else:
                            kts += [("n", t8, None) for t8 in range(8)]
                    else:
                        kts += [("n", s_ * 2 + t2, None) for t2 in range(2)]
                    ot = otok[qt % 2]
                    otk = ("otok", qt % 2)
                    for g in range(4):
                        pt = PT[it % 2]
                        ptk = ("PT", it % 2)
                        ob = 4 + it % 2
                        it += 1
                        if dh == 64:
                            mq, e2 = g // 2, g % 2
                            prt = slice(64 * e2, 64 * e2 + 64)
                            rhs = qT[prt, 4 * mq:4 * mq + 4, qt * 128:(qt + 1) * 128]
                            qreads = [("qT", 4 * mq + i, qt // 4) for i in range(4)]
                            kch = mq
                        else:
                            prt = slice(0, 128)
                            rhs = qT[:, 2 * g:2 * g + 2, qt * 128:(qt + 1) * 128]
                            qreads = [("qT", 2 * g + i, qt // 4) for i in range(2)]
                            kch = g
                        for ki, (src, ti, msk) in enumerate(kts):
                            b = sb_i % 4
                            sb_i += 1
                            if src == "c":
                                lhsT = kcT[prt, kch, ti * 128:(ti + 1) * 128]
                                kr = [("kcT", ti)]
                            else:
                                lhsT = kT[prt, kch, ti * 128:(ti + 1) * 128]
                                kr = [("kT", kch, ti // 4)]
                            mm(ps[:, b, 0:G * 128], lhsT, rhs, True, True, kr + qreads, PSK[b])
                            S.op("act", lambda e, pt=pt, ki=ki, b=b: e.activation(out=pt[:, ki, :], in_=ps[:, b, 0:G * 128], func=AF.Exp, scale=scale),
                                 reads=[PSK[b]], writes=[(ptk, ki)])
                            if msk is not None:
                                S.op("pool", lambda e, pt=pt, ki=ki, msk=msk: e.tensor_tensor(
                                    out=pt[:, ki, :], in0=pt[:, ki, :], in1=msk[:, 0:G, :].rearrange("p g q -> p (g q)"), op=ALU.mult),
                                    reads=[(ptk, ki), "cst_misc"], writes=[(ptk, ki)])
                        opv = ps[:, ob, 0:G * (dh + 1)].rearrange("p (i d) -> p i d", i=G)
                        for i in range(G):
                            for ki, (src, ti, msk) in enumerate(kts):
                                if src == "c":
                                    vr = vcx[:, ti, g, 0:dh + 1]
                                    vk = [("vcx", ti)]
                                else:
                                    vr = vx[:, ti, g, 0:dh + 1]
                                    vk = [("vx", ti)]
                                mm(opv[:, i, :], pt[:, ki, i * 128:(i + 1) * 128], vr, ki == 0, ki == len(kts) - 1,
                                   [(ptk, ki)] + vk, PSK[ob])
                        if esink is not None:
                            S.op("dve", lambda e, opv=opv, g=g: e.tensor_tensor(out=den[:, 0:G], in0=opv[:, :, dh], in1=esink[:, g * G:(g + 1) * G], op=ALU.add),
                                 reads=[PSK[ob], "esink"], writes=["den"])
                            S.op("dve", lambda e: e.reciprocal(out=rden[:, 0:G], in_=den[:, 0:G]), reads=["den"], writes=["rden"])
                        else:
                            S.op("dve", lambda e, opv=opv: e.reciprocal(out=rden[:, 0:G], in_=opv[:, :, dh]), reads=[PSK[ob]], writes=["rden"])
                        for i in range(G):
                            h = g * G + i
                            S.op("dve", lambda e, opv=opv, i=i, h=h, ot=ot: e.tensor_scalar(out=ot[:, h * dh:(h + 1) * dh], in0=opv[:, i, 0:dh],
                                                                                             scalar1=rden[:, i:i + 1], scalar2=None, op0=ALU.mult),
                                 reads=[PSK[ob], "rden"], writes=[(otk, h)])
                    b = 6 + qt % 2
                    pb = ps[:, b, :].bitcast(BF16).rearrange("p (c t) -> p c t", c=8)
                    for c in range(8):
                        S.op("pe", lambda e, pb=pb, c=c, ot=ot: e.transpose(pb[:, c, :], ot[:, c * 128:(c + 1) * 128], ident_b),
                             reads=[(otk, h) for h in range(H)] + ["ident_b"], writes=[PSK[b]], acc=True)
                    S.op("act", lambda e, pb=pb, qt=qt: e.activation(out=OT[:, :, qt * 128:(qt + 1) * 128], in_=pb, func=AF.Copy),
                         reads=[PSK[b]], writes=[("OT", qt)] + XK)
                if ATT_STOP <= 4:
                    AR.release(m)
                    return
                bi = 0
                for oc in range(8):
                    for hf in range(2):
                        b = bi % 4
                        bi += 1
                        for kc in range(8):
                            mm(ps[:, b, :], wbig[:, kc, oc * 128:(oc + 1) * 128], OT[:, kc, hf * 512:(hf + 1) * 512], kc == 0, kc == 7,
                               [("wbig", 0), ("wbig", 1)] + [("OT", hf * 4 + q) for q in range(4)], PSK[b])
                        resid(oc, hf, b, 16)
                AR.release(m)

            def s5_layer(l):
                CH = 128 // nseq
                m = AR.mark()
                norm_mod(A1, "A1", 0, perm8=True)
                U = AR.alloc([128, 64, 128], BF16)
                TB = AR.alloc([128, 2, 64, 128], BF16)
                SS = AR.alloc([128, 2, 64, 128], F32)
                ALr = AR.alloc([128, 64], F32)
                ALi = AR.alloc([128, 64], F32)
                H0 = AR.alloc([128, 2, 64], F32)
                L8 = AR.alloc([128, 2, 128], F32)
                FS = AR.alloc([128, 2, 64], F32)
                FSo = AR.alloc([128, 2, 128], F32)
                tt_ = [[AR.alloc([128, 64], F32) for _ in range(2)] for _ in range(2)]
                S.dma("sp", lambda e: e.dma_start(out=T["D1"][p].rearrange("(fc q) n -> q fc n", q=128), in_=xnT), reads=XK, writes=["D1"])
                d1v = T["D1"][p].rearrange("(g c) (t k) -> t c g k", c=16, t=8)
                for t in range(8):
                    S.dma("sp", lambda e, t=t: e.dma_start(out=U[16 * t:16 * (t + 1)], in_=d1v[t]), reads=["D1"], writes=[("U", t)])
                UK = [("U", t) for t in range(8)]
                for r_ in range(2):
                    for g4 in range(4):
                        S.dma("sp", lambda e, r_=r_, g4=g4: e.dma_start(out=TB[:, r_, g4 * 16:(g4 + 1) * 16, :],
                                                                         in_=T["TS"][r_, g4 * 16:(g4 + 1) * 16].rearrange("g k n -> k g n")),
                              writes=[("TB", r_, g4)])
                S.dma("sp", lambda e: e.dma_start(out=L8[0:64], in_=T["TP8"].rearrange("r (d g) p -> g r d p", d=2)), writes=["L8"])
                for r_, dst in ((0, ALr), (1, ALi)):
                    S.op("pe", lambda e, r_=r_: e.transpose(ps[:, 7, 0:64], L8[0:64, r_, :], ident_f[0:64, 0:64]), reads=["L8", "ident_f"], writes=[PSK[7]])
                    S.op("dve", lambda e, dst=dst: e.tensor_copy(out=dst, in_=ps[:, 7, 0:64]), reads=[PSK[7]], writes=["AL"])
                if p == 1:
                    for r_ in range(2):
                        S.dma("sp", lambda e, r_=r_: e.dma_start(out=L8[0:64, r_, :].rearrange("g (d p) -> g d p", d=2),
                                                                  in_=T["st1"][:, r_].rearrange("d g p -> g d p")), reads=["L8"], writes=["L8"], key=("L8", r_))
                    for r_ in range(2):
                        S.op("pe", lambda e, r_=r_: e.transpose(ps[:, 7, 0:64], L8[0:64, r_, :], ident_f[0:64, 0:64]), reads=["L8", "ident_f"], writes=[PSK[7]])
                        S.op("dve", lambda e, r_=r_: e.tensor_copy(out=H0[:, r_, :], in_=ps[:, 7, 0:64]), reads=[PSK[7]], writes=["H0"])
                else:
                    S.op("dve", lambda e: e.memset(H0, 0.0), writes=["H0"])
                for gq in range(16):
                    for r_ in range(2):
                        b = (gq * 2 + r_) % 4
                        for gi in range(4):
                            g = gq * 4 + gi
                            S.op("pe", lambda e, b=b, gi=gi, g=g, r_=r_: e.matmul(ps[:, b, gi * 128:(gi + 1) * 128], TB[:, r_, g, :], U[:, g, :], start=True, stop=True),
                                 reads=UK + [("TB", r_, g // 16)], writes=[PSK[b]], acc=True)
                        eng = "act" if r_ == 0 else "dve"
                        if r_ == 0:
                            S.op("act", lambda e, b=b, gq=gq: e.activation(out=SS[:, 0, gq * 4:(gq + 1) * 4, :].rearrange("q g k -> q (g k)"), in_=ps[:, b, :], func=AF.Copy),
                                 reads=[PSK[b]], writes=[("SS", 0)])
                        else:
                            S.op("dve", lambda e, b=b, gq=gq: e.tensor_copy(out=SS[:, 1, gq * 4:(gq + 1) * 4, :].rearrange("q g k -> q (g k)"), in_=ps[:, b, :]),
                                 reads=[PSK[b]], writes=[("SS", 1)])
                S.barrier()
                for s_ in range(nseq):
                    for d, eng in ((0, "dve"), (1, "pool")):
                        hs = slice(64 * d, 64 * d + 64)
                        order = range(CH) if d == 0 else range(CH - 1, -1, -1)
                        t1, t2 = tt_[d]
                        k1, k2, kS = ("rt1", d), ("rt2", d), ("SSd", d)
                        first = True
                        for kk in order:
                            k = s_ * CH + kk
                            if first:
                                pr_, pi_ = H0[hs, 0, :], H0[hs, 1, :]
                            else:
                                kp = k - 1 if d == 0 else k + 1
                                pr_, pi_ = SS[hs, 0, :, kp], SS[hs, 1, :, kp]
                            first = False
                            cr_, ci_ = SS[hs, 0, :, k], SS[hs, 1, :, k]
                            ar_, ai_ = ALr[hs], ALi[hs]
                            S.op(eng, lambda e, hs=hs, t1=t1, pr_=pr_, ar_=ar_: e.tensor_tensor(out=t1[hs], in0=ar_, in1=pr_, op=ALU.mult), reads=[kS, "AL", "H0"], writes=[k1])
                            S.op(eng, lambda e, hs=hs, t2=t2, pi_=pi_, ai_=ai_: e.tensor_tensor(out=t2[hs], in0=ai_, in1=pi_, op=ALU.mult), reads=[kS, "AL", "H0"], writes=[k2])
                            S.op(eng, lambda e, hs=hs, t1=t1, t2=t2: e.tensor_tensor(out=t1[hs], in0=t1[hs], in1=t2[hs], op=ALU.subtract), reads=[k1, k2], writes=[k1])
                            S.op(eng, lambda e, hs=hs, t1=t1, pi_=pi_, ar_=ar_, t2=t2: e.tensor_tensor(out=t2[hs], in0=ar_, in1=pi_, op=ALU.mult), reads=[kS, "AL", "H0", k2], writes=[k2])
                            S.op(eng, lambda e, hs=hs, cr_=cr_, t1=t1: e.tensor_tensor(out=cr_, in0=cr_, in1=t1[hs], op=ALU.add), reads=[kS, k1], writes=[(kS, "r")])
                            S.op(eng, lambda e, hs=hs, t1=t1, pr_=pr_, ai_=ai_: e.tensor_tensor(out=t1[hs], in0=ai_, in1=pr_, op=ALU.mult), reads=[kS, "AL", "H0", k1, (kS, "r")], writes=[k1])
                            S.op(eng, lambda e, hs=hs, t1=t1, t2=t2: e.tensor_tensor(out=t2[hs], in0=t1[hs], in1=t2[hs], op=ALU.add), reads=[k1, k2], writes=[k2])
                            S.op(eng, lambda e, hs=hs, ci_=ci_, t2=t2: e.tensor_tensor(out=ci_, in0=ci_, in1=t2[hs], op=ALU.add), reads=[kS, (kS, "r"), k2], writes=[kS])
                S.barrier()
                if p == 0:
                    for s_ in range(nseq):
                        for r_ in range(2):
                            S.op("dve", lambda e, r_=r_, s_=s_: e.tensor_copy(out=FS[0:64, r_, :], in_=SS[0:64, r_, :, s_ * CH + CH - 1]), reads=[], writes=["FS"])
                            S.op("dve", lambda e, r_=r_, s_=s_: e.tensor_copy(out=FS[64:128, r_, :], in_=SS[64:128, r_, :, s_ * CH]), reads=[], writes=["FS"])
                            S.op("pe", lambda e, r_=r_: e.transpose(ps[0:64, 7, r_ * 128:(r_ + 1) * 128], FS[:, r_, :], ident_f), reads=["FS", "ident_f"], writes=[PSK[7]], acc=True)
                        S.op("dve", lambda e: e.tensor_copy(out=FSo[0:64].rearrange("g r n -> g (r n)"), in_=ps[0:64, 7, 0:256]), reads=[PSK[7]], writes=["FSo"])
                        for r_ in range(2):
                            S.dma("sp", lambda e, s_=s_, r_=r_: e.dma_start(out=T["nst"][s_][:, r_].rearrange("d g p -> g d p"),
                                                                           in_=FSo[0:64, r_, :].rearrange("g (d p) -> g d p", d=2)), reads=["FSo"], key=("nst", r_))
                for r_ in range(2):
                    for s_ in range(nseq):
                        k0 = s_ * CH
                        S.op("act", lambda e, r_=r_, k0=k0: e.activation(out=TB[0:64, r_, :, k0 + 1:k0 + CH], in_=SS[0:64, r_, :, k0:k0 + CH - 1], func=AF.Copy),
                             reads=[], writes=["Hin"])
                        S.op("act", lambda e, r_=r_, k0=k0: e.activation(out=TB[0:64, r_, :, k0], in_=H0[0:64, r_, :], func=AF.Copy), reads=["H0"], writes=["Hin"])
                        S.op("dve", lambda e, r_=r_, k0=k0: e.tensor_copy(out=TB[64:128, r_, :, k0:k0 + CH - 1], in_=SS[64:128, r_, :, k0 + 1:k0 + CH]),
                             reads=[], writes=["Hin"])
                        S.op("dve", lambda e, r_=r_, k0=k0: e.tensor_copy(out=TB[64:128, r_, :, k0 + CH - 1], in_=H0[64:128, r_, :]), reads=["H0"], writes=["Hin"])
                S.barrier()
                SSb = SS.rearrange("q r g k -> q (r g k)").bitcast(BF16)
                EL = SSb[:, 0:16384].rearrange("q (r g n) -> q r g n", r=2, g=64)
                ML = SSb[:, 16384:24576].rearrange("q (g n) -> q g n", g=64)
                YC = SSb[:, 24576:32768].rearrange("q (g n) -> q g n", g=64)
                for r_ in range(2):
                    for g4 in range(4):
                        S.dma("sp", lambda e, r_=r_, g4=g4: e.dma_start(out=EL[:, r_, g4 * 16:(g4 + 1) * 16, :],
                                                                         in_=T["TE"][r_, g4 * 16:(g4 + 1) * 16].rearrange("g k n -> k g n")),
                              writes=[("EL", r_, g4)])
                for g4 in range(4):
                    S.dma("sp", lambda e, g4=g4: e.dma_start(out=ML[:, g4 * 16:(g4 + 1) * 16, :], in_=T["TM"][g4 * 16:(g4 + 1) * 16].rearrange("g k n -> k g n")),
                          writes=[("ML", g4)])
                gu = [AR.alloc([128, 512], F32) for _ in range(2)]
                gs = [AR.alloc([128, 512], F32) for _ in range(2)]
                for gq in range(16):
                    b = gq % 4
                    for gi in range(4):
                        g = gq * 4 + gi
                        o_ = ps[:, b, gi * 128:(gi + 1) * 128]
                        S.op("pe", lambda e, o_=o_, g=g: e.matmul(o_, ML[:, g, :], U[:, g, :], start=True, stop=False), reads=[("ML", g // 16)] + UK, writes=[PSK[b]], acc=True)
                        S.op("pe", lambda e, o_=o_, g=g: e.matmul(o_, EL[:, 0, g, :], TB[:, 0, g, :], start=False, stop=False), reads=[("EL", 0, g // 16), "Hin"], writes=[PSK[b]], acc=True)
                        S.op("pe", lambda e, o_=o_, g=g: e.matmul(o_, EL[:, 1, g, :], TB[:, 1, g, :], start=False, stop=True), reads=[("EL", 1, g // 16), "Hin"], writes=[PSK[b]], acc=True)
                    u_, s2 = gu[gq % 2], gs[gq % 2]
                    uk, sk = ("gu", gq % 2), ("gs", gq % 2)
                    S.op("act", lambda e, u_=u_, b=b: e.activation(out=u_, in_=ps[:, b, :], func=AF.Square), reads=[PSK[b]], writes=[uk])
                    S.op("dve", lambda e, u_=u_: e.tensor_scalar(out=u_, in0=u_, scalar1=0.044715, scalar2=1.0, op0=ALU.mult, op1=ALU.add), reads=[uk], writes=[uk])
                    S.op("dve", lambda e, u_=u_, b=b: e.tensor_tensor(out=u_, in0=u_, in1=ps[:, b, :], op=ALU.mult), reads=[uk, PSK[b]], writes=[uk])
                    S.op("act", lambda e, u_=u_, s2=s2: e.activation(out=s2, in_=u_, func=AF.Sigmoid, scale=1.5957691216057308), reads=[uk], writes=[sk])
                    S.op("dve", lambda e, s2=s2, b=b, gq=gq: e.tensor_tensor(out=YC[:, gq * 4:(gq + 1) * 4, :].rearrange("q g k -> q (g k)"), in0=s2, in1=ps[:, b, :], op=ALU.mult),
                         reads=[sk, PSK[b]], writes=["YC"])
                d2v = T["D2"][p].rearrange("(g c) (t k) -> t c g k", c=16, t=8)
                for t in range(8):
                    S.dma("sp", lambda e, t=t: e.dma_start(out=d2v[t], in_=YC[16 * t:16 * (t + 1)]), reads=["YC"], writes=["D2"], key=("D2w", t))
                S.dma("sp", lambda e: e.dma_start(out=xnT, in_=T["D2"][p].rearrange("(fc q) n -> q fc n", q=128)), reads=["D2"], writes=XK, key="D2r")
                S.barrier()
                AR.release(m)
                m = AR.mark()
                wg = AR.alloc([128, 8, 2 * D], BF16)
                sg2 = [AR.alloc([128, 512], F32) for _ in range(2)]
                mx = [AR.alloc([128, 512], F32) for _ in range(2)]
                for h2 in range(2):
                    S.dma("pool", lambda e, h2=h2: e.dma_start(out=wg[:, :, h2 * D:(h2 + 1) * D],
                                                               in_=T["l1_w_glu"][:, h2 * D:(h2 + 1) * D].rearrange("(kc q) n -> q kc n", q=128)),
                          writes=[("wg", h2)])
                it = 0
                for oc in range(8):
                    for hf in range(2):
                        bA, bB = (it % 2) * 2, (it % 2) * 2 + 1
                        s_, x_ = sg2[it % 2], mx[it % 2]
                        sk, xk_ = ("sg2", it % 2), ("mx", it % 2)
                        it += 1
                        for kc in range(8):
                            mm(ps[:, bA, :], wg[:, kc, oc * 128:(oc + 1) * 128], xnT[:, kc, hf * 512:(hf + 1) * 512], kc == 0, kc == 7, [("wg", 0), XK[kc]], PSK[bA])
                        for kc in range(8):
                            mm(ps[:, bB, :], wg[:, kc, D + oc * 128:D + (oc + 1) * 128], xnT[:, kc, hf * 512:(hf + 1) * 512], kc == 0, kc == 7, [("wg", 1), XK[kc]], PSK[bB])
                        S.op("act", lambda e, s_=s_, bB=bB: e.activation(out=s_, in_=ps[:, bB, :], func=AF.Sigmoid), reads=[PSK[bB]], writes=[sk])
                        S.op("dve", lambda e, s_=s_, x_=x_, bA=bA: e.tensor_tensor(out=x_, in0=ps[:, bA, :], in1=s_, op=ALU.mult), reads=[PSK[bA], sk], writes=[xk_])
                        hv = hT[:, oc, :].rearrange("q (k t) -> q t k", t=8)[:, 4 * hf:4 * hf + 4, :]
                        S.op("dve", lambda e, x_=x_, hv=hv, oc=oc: e.scalar_tensor_tensor(out=hv, in0=x_.rearrange("q (t k) -> q t k", t=4), scalar=modT[:, 16 + oc:17 + oc],
                                                                                        in1=hv, op0=ALU.mult, op1=ALU.add),
                             reads=[xk_, "modT", HK[oc]], writes=[HK[oc]])
                AR.release(m)

            for l in range(NLAYERS):
                if STAGE & 1:
                    compute_mod(l)
                S.barrier()
                if STAGE & 2:
                    if LAYERS[l]["kind"] == "B":
                        if not SKIP_S5:
                            s5_layer(l)
                    else:
                        attention(l)
                S.barrier()
                if STAGE & 4:
                    ffn(l)
                S.barrier()

            m0 = AR.mark()
            yst = [AR.alloc([128, D], F32) for _ in range(2)]
            for tt in range(8):
                y_ = yst[tt % 2]
                yk = ("yst", tt % 2)
                for hf in range(2):
                    b = (tt * 2 + hf) % 8
                    for q in range(4):
                        kc = hf * 4 + q
                        S.op("pe", lambda e, b=b, q=q, kc=kc, tt=tt: e.transpose(ps[:, b, q * 128:(q + 1) * 128], hT[:, kc, tt * 128:(tt + 1) * 128], ident_f),
                             reads=[HK[kc], "ident_f"], writes=[PSK[b]], acc=True)
                    if hf:
                        S.op("act", lambda e, b=b, y_=y_: e.activation(out=y_[:, 512:1024], in_=ps[:, b, :], func=AF.Copy), reads=[PSK[b]], writes=[(yk, 1)])
                    else:
                        S.op("dve", lambda e, b=b, y_=y_: e.tensor_copy(out=y_[:, 0:512], in_=ps[:, b, :]), reads=[PSK[b]], writes=[(yk, 0)])
                S.dma("sp", lambda e, y_=y_, tt=tt: e.dma_start(out=T["y"][p, tt * 128:(tt + 1) * 128, :], in_=y_), reads=[(yk, 0), (yk, 1)], key=yk)
            S.barrier()
            AR.release(m0)

        if not SKIP_S5 and NLAYERS > 1 and (STAGE & 2):
            s5_tables()
        run_path(0)
        run_path(1)
        S.barrier()
        print('n_sems', len(S.sem_objs), 'insts', {e: len(S.prog[e]) for e in ENGS})
        S.emit()
    return nc


def _consts():
    cst = np.zeros((9, 128, 128), np.float32)
    for tt_ in range(8):
        for cc_ in range(16):
            cst[CST_R16, cc_, tt_ * 16 + cc_] = 1.0
    cst[CST_IDENT] = np.eye(128)
    bd = np.zeros((128, 128), np.float32)
    bd[:64, :64] = 1
    bd[64:, 64:] = 1
    cst[CST_BD64] = bd
    kk = np.arange(128)[:, None]
    qq = np.arange(128)[None, :]
    cst[CST_MLO] = (qq <= kk)
    cst[CST_MHI] = (kk <= qq)

    def perm(dh):
        qt = dh // 4
        half = dh // 2
        Pm = np.zeros((128, 128), np.float32)
        for m in range(128):
            d = m % dh
            if (d % half) < qt:
                Pm[m, m + qt] = -1.0
            else:
                Pm[m, m - qt] = 1.0
        return Pm.T.copy()

    cst[CST_PA] = perm(64)
    cst[CST_PC] = perm(128)
    t1 = np.arange(8)
    tau_p = np.repeat(t1, 16)[:, None]
    tau = np.repeat(t1, 16)[None, :]
    cst[CST_S5F] = (tau >= tau_p)
    cst[CST_S5B] = (tau <= tau_p)
    rope = np.zeros((4, 128, NT), np.float32)
    t = np.arange(NT)
    row = (t // 64).astype(np.float32)
    col = (t % 64).astype(np.float32)
    for idx, dh in ((0, 64), (2, 128)):
        half, qt = dh // 2, dh // 4
        freqs = (1.0 / (np.float32(10000.0) ** (np.arange(qt, dtype=np.float32) / np.float32(qt)))).astype(np.float32)
        for pp in range(128):
            d = pp % dh
            pos = row if d < half else col
            ang = (pos * freqs[d % qt]).astype(np.float32)
            rope[idx, pp] = np.cos(ang)
            rope[idx + 1, pp] = np.sin(ang)
    return cst, rope


_PROG = {}


def kernel(**inp):
    f32 = lambda a: np.ascontiguousarray(np.asarray(a, dtype=np.float32))
    inp = {k: f32(v) for k, v in inp.items()}
    if "nc" not in _PROG:
        _PROG["nc"] = build_program()
    nc = _PROG["nc"]
    cst, rope = _consts()
    shared = {}
    for name, shape in IN_SPECS:
        if name in inp:
            shared[name] = inp[name].reshape(shape)
    shared["cst"] = cst
    shared["rope"] = rope
    in_maps = []
    for c in range(8):
        m = dict(shared)
        m["xin"] = np.stack([inp["x_prompt"][4 * c:4 * c + 4].reshape(NT, D), inp["x_sample"][c]])
        m["cond"] = np.stack([inp["c_ctx"], inp["c"][c]])
        m["ck0"] = inp["cache_l0_k"][c].reshape(512, 256)
        m["cv0"] = inp["cache_l0_v"][c].reshape(512, 256)
        m["st1"] = inp["state_l1"][c]
        m["ck2"] = inp["cache_l2_k"][c].reshape(512, 512)
        m["cv2"] = inp["cache_l2_v"][c].reshape(512, 512)
        m["ck3"] = inp["cache_l3_k"][c].reshape(512, 256)
        m["cv3"] = inp["cache_l3_v"][c].reshape(512, 256)
        in_maps.append({k: np.ascontiguousarray(v) for k, v in m.items()})
    res = run_bass_kernel_spmd(nc, in_maps, core_ids=list(range(8)))
    R = res.results
    y_prompt = np.concatenate([R[c]["y"][0].reshape(4, 256, D) for c in range(8)], 0)
    y_sample = np.stack([R[c]["y"][1] for c in range(8)], 0)
    cat = lambda k, shp: np.concatenate([R[c][k].reshape(shp) for c in range(8)], 0)
    out = (y_prompt, y_sample,
           cat("nk0", (4, 256, 4, 64)), cat("nv0", (4, 256, 4, 64)),
           cat("nst", (4, 2, 2, 64, 64)),
           cat("nk2", (4, 256, 4, 128)), cat("nv2", (4, 256, 4, 128)),
           cat("nk3", (4, 256, 4, 64)), cat("nv3", (4, 256, 4, 64)))
    return tuple(np.ascontiguousarray(o, dtype=np.float32) for o in out)
```

```python
import math
import numpy as np
from contextlib import ExitStack
import concourse.bass as bass
import concourse.mybir as mybir
from concourse.bass_utils import run_bass_kernel_spmd

F32 = mybir.dt.float32
BF16 = mybir.dt.bfloat16
I32 = mybir.dt.int32
U8 = mybir.dt.uint8
AF = mybir.ActivationFunctionType
ALU = mybir.AluOpType
AX = mybir.AxisListType

D = 1024
DFF = 2816
NT = 1024
ENGS = ["pe", "act", "dve", "pool", "sp"]
EPS = 1e-6
NLAYERS = 4
SKIP_S5 = False
DEBUG_H = False
STAGE = 7
ATT_STOP = 9


class Sched:
    def __init__(self, nc, stack):
        self.nc = nc
        self.stack = stack
        self.prog = {e: [] for e in ENGS}
        self.cnt = {e: 0 for e in ENGS}
        self.seen = {e: {} for e in ENGS}
        self.lastw = {}
        self.readers = {}
        self.sem_objs = []
        self.esid = {}
        for e in ENGS:
            self.esid[e] = self._new_sem("s_" + e)
        self.dsem = {}
        self.dfree = []

    def _new_sem(self, name):
        self.sem_objs.append(self.stack.enter_context(self.nc.semaphore(name)))
        return len(self.sem_objs) - 1

    def _dma_sem(self, key):
        if key not in self.dsem:
            if self.dfree:
                self.dsem[key] = self.dfree.pop()
            else:
                self.dsem[key] = [self._new_sem("d%d" % len(self.sem_objs)), 0]
        return self.dsem[key]

    def _deps(self, eng, reads, writes, acc=False):
        need = {}

        def add(rec, same_ok=False):
            sid, val, e = rec
            if same_ok and e == eng:
                return
            if need.get(sid, 0) < val:
                need[sid] = val

        for k in reads:
            if k in self.lastw:
                add(self.lastw[k])
        for k in writes:
            if k in self.lastw:
                add(self.lastw[k], same_ok=(acc and eng == "pe"))
            for r in self.readers.get(k, ()):
                add(r, same_ok=True)
        waits = []
        seen = self.seen[eng]
        for sid, val in need.items():
            if seen.get(sid, 0) >= val:
                continue
            seen[sid] = val
            waits.append((sid, val))
        return waits

    def op(self, eng, fn, reads=(), writes=(), acc=False):
        waits = self._deps(eng, reads, writes, acc)
        self.cnt[eng] += 1
        rec = (self.esid[eng], self.cnt[eng], eng)
        self.prog[eng].append((waits, fn, self.esid[eng], 1))
        for k in reads:
            self.readers.setdefault(k, []).append(rec)
        for k in writes:
            self.lastw[k] = rec
            self.readers[k] = []

    def dma(self, q, fn, reads=(), writes=(), key=None):
        if key is None:
            key = writes[0] if writes else reads[0]
        waits = self._deps(q, reads, writes)
        ds = self._dma_sem(key)
        ds[1] += 16
        rec = (ds[0], ds[1], "dma")
        self.prog[q].append((waits, fn, ds[0], 16))
        for k in reads:
            self.readers.setdefault(k, []).append(rec)
        for k in writes:
            self.lastw[k] = rec
            self.readers[k] = []

    def barrier(self):
        allw = [(self.esid[e], self.cnt[e], e) for e in ENGS if self.cnt[e] > 0]
        allw += [(sid, c, None) for k, (sid, c) in self.dsem.items() if c > 0]
        for e in ENGS:
            waits = []
            for sid, val, owner in allw:
                if owner == e:
                    continue
                if self.seen[e].get(sid, 0) >= val:
                    continue
                self.seen[e][sid] = val
                waits.append((sid, val))
            if waits:
                self.prog[e].append((waits, None, None, 0))
        self.dfree.extend(self.dsem.values())
        self.dsem = {}
        self.lastw = {}
        self.readers = {}

    def emit(self):
        nc, sems, prog = self.nc, self.sem_objs, self.prog

        def run(ename):
            def body(eng):
                for waits, fn, incsid, incv in prog[ename]:
                    for sid, val in waits:
                        eng.wait_ge(sems[sid], val)
                    if fn is not None:
                        fn(eng).then_inc(sems[incsid], incv)
            return body

        with nc.Block() as block:
            block.tensor(run("pe"))
            block.scalar(run("act"))
            block.vector(run("dve"))
            block.gpsimd(run("pool"))
            block.sync(run("sp"))


class Arena:
    def __init__(self, tensor, size):
        self.t = tensor
        self.size = size
        self.off = 0

    def alloc(self, shape, dt):
        nb = {F32: 4, BF16: 2, I32: 4}[dt]
        n = int(np.prod(shape[1:]))
        off = (self.off + 63) // 64 * 64
        assert off + n * nb <= self.size, ("SBUF arena overflow", off, n * nb, self.size)
        self.off = off + n * nb
        ap = self.t[:, off:off + n * nb].bitcast(dt)
        if len(shape) == 3:
            ap = ap.rearrange("p (a b) -> p a b", a=shape[1])
        elif len(shape) == 4:
            ap = ap.rearrange("p (a b c) -> p a b c", a=shape[1], b=shape[2])
        if shape[0] != 128:
            ap = ap[0:shape[0]]
        return ap

    def mark(self):
        return self.off

    def release(self, m):
        self.off = m


LAYERS = [
    dict(kind="A", dh=64, H=16, ckv=256, wqkv="l0_w_qkv", qn="l0_q_norm", kn="l0_k_norm", sink="l0_sink", wo="l0_w_o",
         ck="ck0", cv="cv0", nk="nk0", nv="nv0"),
    dict(kind="B"),
    dict(kind="C", dh=128, H=8, ckv=512, wqkv="l2_w_qkv", qn="l2_q_norm", kn="l2_k_norm", sink=None, wo="l2_w_o",
         ck="ck2", cv="cv2", nk="nk2", nv="nv2"),
    dict(kind="A", dh=64, H=16, ckv=256, wqkv="l3_w_qkv", qn="l3_q_norm", kn="l3_k_norm", sink="l3_sink", wo="l3_w_o",
         ck="ck3", cv="cv3", nk="nk3", nv="nv3"),
]

IN_SPECS = [
    ("xin", [2, NT, D]), ("cond", [2, D]),
    ("ck0", [512, 256]), ("cv0", [512, 256]), ("st1", [2, 2, 64, 64]),
    ("ck2", [512, 512]), ("cv2", [512, 512]), ("ck3", [512, 256]), ("cv3", [512, 256]),
    ("norm1", [4, D]), ("norm2", [4, D]), ("w_mod", [4, D, 6 * D]), ("b_mod", [4, 6 * D]),
    ("w_up", [4, D, 2 * DFF]), ("b_up", [4, 2 * DFF]), ("conv_k", [4, 3, 2 * DFF]), ("conv_b", [4, 2 * DFF]),
    ("w_down", [4, DFF, D]),
    ("l0_w_qkv", [D, 1536]), ("l0_q_norm", [64]), ("l0_k_norm", [64]), ("l0_sink", [16]), ("l0_w_o", [D, D]),
    ("l1_lam_re", [2, 64, 64]), ("l1_lam_im", [2, 64, 64]), ("l1_log_dt", [2, 64]),
    ("l1_b_re", [2, 64, 64, 16]), ("l1_b_im", [2, 64, 64, 16]), ("l1_c_re", [2, 64, 16, 64]), ("l1_c_im", [2, 64, 16, 64]),
    ("l1_d_skip", [D]), ("l1_w_glu", [D, 2 * D]),
    ("l2_w_qkv", [D, 2048]), ("l2_q_norm", [128]), ("l2_k_norm", [128]), ("l2_w_o", [D, D]),
    ("l3_w_qkv", [D, 1536]), ("l3_q_norm", [64]), ("l3_k_norm", [64]), ("l3_sink", [16]), ("l3_w_o", [D, D]),
    ("cst", [9, 128, 128]), ("rope", [4, 128, NT]),
]
OUT_SPECS = [
    ("y", [2, NT, D]), ("nk0", [NT, 256]), ("nv0", [NT, 256]), ("nst", [4, 2, 2, 64, 64]),
    ("nk2", [NT, 512]), ("nv2", [NT, 512]), ("nk3", [NT, 256]), ("nv3", [NT, 256]),
]
CST_IDENT, CST_BD64, CST_MLO, CST_MHI, CST_PA, CST_PC, CST_S5F, CST_S5B, CST_R16 = range(9)


def build_program():
    nc = bass.Bass("TRN2", target_bir_lowering=False)
    T = {}
    for name, shape in IN_SPECS:
        T[name] = nc.dram_tensor(name, shape, F32, kind="ExternalInput").ap()
    for name, shape in OUT_SPECS:
        T[name] = nc.dram_tensor(name, shape, F32, kind="ExternalOutput").ap()
    for name, shape, dt_ in [("TX", [128, 16384], BF16), ("TZ", [128, 16384], BF16), ("TM", [64, 128, 128], BF16),
                             ("TE", [2, 64, 128, 128], BF16), ("TS", [2, 64, 128, 128], BF16), ("TP8", [2, 128, 64], F32),
                             ("D1", [2, 1024, 1024], BF16), ("D2", [2, 1024, 1024], BF16)]:
        T[name] = nc.dram_tensor(name, shape, dt_, kind="Internal").ap()
    if DEBUG_H:
        T["dbg"] = nc.dram_tensor("dbg", [2, NT, D], F32, kind="ExternalOutput").ap()

    with ExitStack() as st:
        S = Sched(nc, st)
        ARENA_BYTES = 190 * 1024
        arena_t = st.enter_context(nc.sbuf_tensor("arena", [128, ARENA_BYTES], U8))
        AR = Arena(arena_t, ARENA_BYTES)
        ps = st.enter_context(nc.psum_tensor("ps", [128, 8, 512], F32))
        PSK = [("ps", b) for b in range(8)]
        uid = [0]

        def fresh(prefix):
            uid[0] += 1
            return (prefix, uid[0])

        ident_f = AR.alloc([128, 128], F32)
        ident_b = AR.alloc([128, 128], BF16)
        ones_b = AR.alloc([128, 128], BF16)
        ones_f = AR.alloc([128, 128], F32)
        bd64_b = AR.alloc([128, 128], BF16)
        mlo_b = AR.alloc([128, 4, 128], BF16)
        mhi_b = AR.alloc([128, 4, 128], BF16)
        pa_b = AR.alloc([128, 128], BF16)
        pc_b = AR.alloc([128, 128], BF16)
        cstage = AR.alloc([128, 128], F32)
        S.dma("sp", lambda e: e.dma_start(out=ident_f, in_=T["cst"][CST_IDENT]), writes=["ident_f"])
        S.op("dve", lambda e: e.tensor_copy(out=ident_b, in_=ident_f), reads=["ident_f"], writes=["ident_b"])
        S.op("dve", lambda e: e.memset(ones_b, 1.0), writes=["ones_b"])
        S.op("dve", lambda e: e.memset(ones_f, 1.0), writes=["ones_f"])
        for ci, dst, rep in [(CST_BD64, bd64_b, 0), (CST_MLO, mlo_b, 4), (CST_MHI, mhi_b, 4), (CST_PA, pa_b, 0), (CST_PC, pc_b, 0)]:
            S.dma("sp", lambda e, ci=ci: e.dma_start(out=cstage, in_=T["cst"][ci]), writes=["cstage"])
            if rep:
                for r in range(rep):
                    S.op("dve", lambda e, dst=dst, r=r: e.tensor_copy(out=dst[:, r, :], in_=cstage), reads=["cstage"], writes=["cst_misc"])
            else:
                S.op("dve", lambda e, dst=dst: e.tensor_copy(out=dst, in_=cstage), reads=["cstage"], writes=["cst_misc"])
        S.barrier()
        base_mark = AR.mark()

        def mm(out, lhsT, rhs, start, stop, reads, wkey):
            S.op("pe", lambda e: e.matmul(out, lhsT, rhs, start=start, stop=stop), reads=reads, writes=[wkey], acc=True)

        tstage = AR.alloc([128, 128], F32)
        modAll = AR.alloc([128, 4, 2, 48], F32)
        tst2 = AR.alloc([128, 128], F32)
        S.op("dve", lambda e: e.memset(tst2, 0.0), writes=["tst2z"])
        S.barrier()
        base_mark2 = [None]

        def small_load_T(dst, src_ap, pattern, key, **kw):
            n = dst.shape[1]
            rows = src_ap.rearrange("(c q) -> c q", q=128)
            S.dma("sp", lambda e: e.dma_start(out=tstage[0:n, :], in_=rows), writes=["tstage"])
            S.op("pe", lambda e: e.transpose(ps[:, 7, 0:n], tstage[0:n, :], ident_f[0:n, 0:n]), reads=["tstage", "ident_f"], writes=[PSK[7]])
            S.op("dve", lambda e: e.tensor_copy(out=dst, in_=ps[:, 7, 0:n]), reads=[PSK[7]], writes=[key])

        def small_load_col(dst, src_ap, dh, key):
            rep = 128 // dh
            for r in range(rep):
                S.dma("sp", lambda e, r=r: e.dma_start(out=tst2[0:1, r * dh:(r + 1) * dh], in_=src_ap.rearrange("(o q) -> o q", o=1)),
                      writes=[("tst2", r)], key=("tst2", r))
            S.op("pe", lambda e: e.transpose(ps[:, 7, 0:2], tst2[0:2, :], ident_f[0:2, 0:2]),
                 reads=[("tst2", r) for r in range(rep)] + ["tst2z", "ident_f"], writes=[PSK[7], ("tst2", 0), ("tst2", 1)])
            S.op("dve", lambda e: e.tensor_copy(out=dst, in_=ps[:, 7, 0:1]), reads=[PSK[7]], writes=[key])

        base_mark = AR.mark()

        def s5_tables():
            m = AR.mark()
            TWO_PI = 2.0 * math.pi
            lr = AR.alloc([128, 64], F32)
            li = AR.alloc([128, 64], F32)
            dtc = AR.alloc([128, 1], F32)
            xx = AR.alloc([128, 64], F32)
            th = AR.alloc([128, 64], F32)
            PWr = AR.alloc([128, 16, 64], F32)
            PWi = AR.alloc([128, 16, 64], F32)
            Br = AR.alloc([128, 64, 16], F32)
            Bi = AR.alloc([128, 64, 16], F32)
            Bbr = AR.alloc([128, 64, 16], F32)
            Bbi = AR.alloc([128, 64, 16], F32)
            Cr = AR.alloc([128, 16, 64], F32)
            Ci = AR.alloc([128, 16, 64], F32)
            w = [AR.alloc([128, 1024], F32) for _ in range(4)]
            s64 = [AR.alloc([128, 64], F32) for _ in range(8)]
            ki = AR.alloc([128, 64], I32)
            PMr = AR.alloc([128, 64], F32)
            PMi = AR.alloc([128, 64], F32)
            ST = [AR.alloc([128, 16384], BF16) for _ in range(2)]

            def dv(fn, reads, writes):
                S.op("dve", fn, reads=reads, writes=writes)

            S.dma("sp", lambda e: e.dma_start(out=lr, in_=T["l1_lam_re"].rearrange("d g p -> (d g) p")), writes=["lr"])
            S.dma("sp", lambda e: e.dma_start(out=li, in_=T["l1_lam_im"].rearrange("d g p -> (d g) p")), writes=["li"])
            S.dma("sp", lambda e: e.dma_start(out=dtc, in_=T["l1_log_dt"].rearrange("d (g o) -> (d g) o", o=1)), writes=["dtc"])
            S.dma("sp", lambda e: e.dma_start(out=Br, in_=T["l1_b_re"].rearrange("d g p c -> (d g) p c")), writes=["Br"])
            S.dma("sp", lambda e: e.dma_start(out=Bi, in_=T["l1_b_im"].rearrange("d g p c -> (d g) p c")), writes=["Bi"])
            S.dma("sp", lambda e: e.dma_start(out=Cr, in_=T["l1_c_re"].rearrange("d g c p -> (d g) c p")), writes=["Cr"])
            S.dma("sp", lambda e: e.dma_start(out=Ci, in_=T["l1_c_im"].rearrange("d g c p -> (d g) c p")), writes=["Ci"])
            S.op("act", lambda e: e.activation(out=dtc, in_=dtc, func=AF.Exp), reads=["dtc"], writes=["dtc"])
            dv(lambda e: e.tensor_scalar(out=xx, in0=lr, scalar1=dtc[:, 0:1], scalar2=None, op0=ALU.mult), ["lr", "dtc"], ["xx"])
            dv(lambda e: e.tensor_scalar(out=th, in0=li, scalar1=dtc[:, 0:1], scalar2=None, op0=ALU.mult), ["li", "dtc"], ["th"])

            def sinred(out, okey, shift):
                v, kf, r, mk = s64[0], s64[1], s64[2], s64[3]
                dv(lambda e: e.tensor_scalar(out=v, in0=th, scalar1=float(shift), scalar2=None, op0=ALU.add), ["th"], ["sr_v"])
                dv(lambda e: e.tensor_scalar(out=kf, in0=v, scalar1=1.0 / TWO_PI, scalar2=None, op0=ALU.mult), ["sr_v"], ["sr_kf"])
                dv(lambda e: e.tensor_copy(out=ki, in_=kf), ["sr_kf"], ["sr_ki"])
                dv(lambda e: e.tensor_copy(out=kf, in_=ki), ["sr_ki"], ["sr_kf"])
                dv(lambda e: e.scalar_tensor_tensor(out=r, in0=kf, scalar=-TWO_PI, in1=v, op0=ALU.mult, op1=ALU.add), ["sr_kf", "sr_v"], ["sr_r"])
                dv(lambda e: e.tensor_scalar(out=mk, in0=r, scalar1=math.pi, scalar2=None, op0=ALU.is_gt), ["sr_r"], ["sr_m"])
                dv(lambda e: e.scalar_tensor_tensor(out=r, in0=mk, scalar=-TWO_PI, in1=r, op0=ALU.mult, op1=ALU.add), ["sr_m", "sr_r"], ["sr_r"])
                dv(lambda e: e.tensor_scalar(out=mk, in0=r, scalar1=-math.pi, scalar2=None, op0=ALU.is_lt), ["sr_r"], ["sr_m"])
                dv(lambda e: e.scalar_tensor_tensor(out=r, in0=mk, scalar=TWO_PI, in1=r, op0=ALU.mult, op1=ALU.add), ["sr_m", "sr_r"], ["sr_r"])
                dv(lambda e: e.tensor_scalar(out=r, in0=r, scalar1=3.1415925, scalar2=-3.1415925, op0=ALU.min, op1=ALU.max), ["sr_r"], ["sr_r"])
                S.op("act", lambda e: e.activation(out=out, in_=r, func=AF.Sin), reads=["sr_r"], writes=[okey])

            sn, cs, ex, exm = s64[4], s64[5], s64[6], s64[7]
            sinred(sn, "sn", 0.0)
            sinred(cs, "cs", math.pi / 2.0)
            S.op("act", lambda e: e.activation(out=ex, in_=xx, func=AF.Exp), reads=["xx"], writes=["ex"])
            S.op("act", lambda e: e.activation(out=exm, in_=xx, func=AF.Exp, scale=-1.0), reads=["xx"], writes=["exm"])
            PK = lambda j: ("PW", j)
            dv(lambda e: e.memset(PWr[:, 7, :], 1.0), [], [("PWr", 0)])
            dv(lambda e: e.memset(PWi[:, 7, :], 0.0), [], [("PWi", 0)])
            dv(lambda e: e.tensor_tensor(out=PWr[:, 8, :], in0=ex, in1=cs, op=ALU.mult), ["ex", "cs"], [("PWr", 1)])
            dv(lambda e: e.tensor_tensor(out=PWi[:, 8, :], in0=ex, in1=sn, op=ALU.mult), ["ex", "sn"], [("PWi", 1)])
            dv(lambda e: e.tensor_tensor(out=PWr[:, 6, :], in0=exm, in1=cs, op=ALU.mult), ["exm", "cs"], [("PWr", -1)])
            dv(lambda e: e.scalar_tensor_tensor(out=PWi[:, 6, :], in0=exm, scalar=-1.0, in1=sn, op0=ALU.mult, op1=ALU.mult), ["exm", "sn"], [("PWi", -1)])

            def cmul64(j_out, j_a, j_b):
                orr, oi = PWr[:, j_out + 7, :], PWi[:, j_out + 7, :]
                ar, ai = PWr[:, j_a + 7, :], PWi[:, j_a + 7, :]
                br, bi = PWr[:, j_b + 7, :], PWi[:, j_b + 7, :]
                ka = [("PWr", j_a), ("PWi", j_a), ("PWr", j_b), ("PWi", j_b)]
                t1, t2 = s64[0], s64[1]
                dv(lambda e: e.tensor_tensor(out=t1, in0=ar, in1=br, op=ALU.mult), ka, ["c64a"])
                dv(lambda e: e.tensor_tensor(out=t2, in0=ai, in1=bi, op=ALU.mult), ka, ["c64b"])
                dv(lambda e: e.tensor_tensor(out=orr, in0=t1, in1=t2, op=ALU.subtract), ["c64a", "c64b"], [("PWr", j_out)])
                dv(lambda e: e.tensor_tensor(out=t1, in0=ar, in1=bi, op=ALU.mult), ka, ["c64a"])
                dv(lambda e: e.tensor_tensor(out=t2, in0=ai, in1=br, op=ALU.mult), ka, ["c64b"])
                dv(lambda e: e.tensor_tensor(out=oi, in0=t1, in1=t2, op=ALU.add), ["c64a", "c64b"], [("PWi", j_out)])

            for j in range(2, 9):
                cmul64(j, j - 1, 1)
            for j in range(2, 8):
                cmul64(-j, -(j - 1), -1)
            S.dma("sp", lambda e: e.dma_start(out=T["TP8"][0], in_=PWr[:, 15, :]), reads=[("PWr", 8)], key="tp8a")
            S.dma("sp", lambda e: e.dma_start(out=T["TP8"][1], in_=PWi[:, 15, :]), reads=[("PWi", 8)], key="tp8b")
            nr, den, fr, fi = s64[0], s64[1], s64[2], s64[3]
            t5, t6 = s64[4], s64[5]
            a1 = [("PWr", 1), ("PWi", 1)]
            dv(lambda e: e.tensor_scalar(out=nr, in0=PWr[:, 8, :], scalar1=-1.0, scalar2=None, op0=ALU.add), a1, ["nr"])
            dv(lambda e: e.tensor_tensor(out=den, in0=lr, in1=lr, op=ALU.mult), ["lr"], ["den"])
            dv(lambda e: e.tensor_tensor(out=t5, in0=li, in1=li, op=ALU.mult), ["li"], ["t5"])
            dv(lambda e: e.tensor_tensor(out=den, in0=den, in1=t5, op=ALU.add), ["den", "t5"], ["den"])
            dv(lambda e: e.reciprocal(out=den, in_=den), ["den"], ["den"])
            dv(lambda e: e.tensor_tensor(out=fr, in0=nr, in1=lr, op=ALU.mult), ["nr", "lr"], ["fr"])
            dv(lambda e: e.tensor_tensor(out=t5, in0=PWi[:, 8, :], in1=li, op=ALU.mult), a1 + ["li"], ["t5"])
            dv(lambda e: e.tensor_tensor(out=fr, in0=fr, in1=t5, op=ALU.add), ["fr", "t5"], ["fr"])
            dv(lambda e: e.tensor_tensor(out=fr, in0=fr, in1=den, op=ALU.mult), ["fr", "den"], ["fr"])
            dv(lambda e: e.tensor_tensor(out=fi, in0=PWi[:, 8, :], in1=lr, op=ALU.mult), a1 + ["lr"], ["fi"])
            dv(lambda e: e.tensor_tensor(out=t6, in0=nr, in1=li, op=ALU.mult), ["nr", "li"], ["t6"])
            dv(lambda e: e.tensor_tensor(out=fi, in0=fi, in1=t6, op=ALU.subtract), ["fi", "t6"], ["fi"])
            dv(lambda e: e.tensor_tensor(out=fi, in0=fi, in1=den, op=ALU.mult), ["fi", "den"], ["fi"])

            def bc_b(a):
                return a.unsqueeze(2).broadcast_to([128, 64, 16])

            def bc_c(a):
                return a.unsqueeze(1).broadcast_to([128, 16, 64])

            w3b = [x.rearrange("q (p c) -> q p c", c=16) for x in w]
            w3c = [x.rearrange("q (c p) -> q c p", p=64) for x in w]

            def cmul_big(pr, pi, pk, xr, xi, xk, bc, wv, out_r, out_i, okey, neg_im):
                t1, t2, t3, t4 = wv
                dv(lambda e: e.tensor_tensor(out=t1, in0=xr, in1=bc(pr), op=ALU.mult), pk + xk, ["cb1"])
                dv(lambda e: e.tensor_tensor(out=t2, in0=xi, in1=bc(pi), op=ALU.mult), pk + xk, ["cb2"])
                dv(lambda e: e.tensor_tensor(out=out_r, in0=t1, in1=t2, op=ALU.subtract), ["cb1", "cb2"], [okey])
                dv(lambda e: e.tensor_tensor(out=t3, in0=xi, in1=bc(pr), op=ALU.mult), pk + xk, ["cb3"])
                dv(lambda e: e.tensor_tensor(out=t4, in0=xr, in1=bc(pi), op=ALU.mult), pk + xk, ["cb4"])
                if neg_im:
                    dv(lambda e: e.scalar_tensor_tensor(out=out_i, in0=t3, scalar=-1.0, in1=t4, op0=ALU.mult, op1=ALU.subtract), ["cb3", "cb4"], [okey])
                else:
                    dv(lambda e: e.tensor_tensor(out=out_i, in0=t3, in1=t4, op=ALU.add), ["cb3", "cb4"], [okey])

            cmul_big(fr, fi, ["fr", "fi"], Br, Bi, ["Br", "Bi"], bc_b, w3b, Bbr, Bbi, "Bb", False)

            def mixed_power(jf, jb):
                dv(lambda e: e.tensor_copy(out=PMr[0:64], in_=PWr[0:64, jf + 7, :]), [("PWr", jf)], ["PM"])
                dv(lambda e: e.tensor_copy(out=PMi[0:64], in_=PWi[0:64, jf + 7, :]), [("PWi", jf)], ["PM"])
                dv(lambda e: e.tensor_copy(out=PMr[64:128], in_=PWr[64:128, jb + 7, :]), [("PWr", jb)], ["PM"])
                dv(lambda e: e.tensor_copy(out=PMi[64:128], in_=PWi[64:128, jb + 7, :]), [("PWi", jb)], ["PM"])

            for ti, (side, pf, pb_, dst, layout) in enumerate([
                    ("B", lambda t: -t, lambda t: t, "TX", "ript"),
                    ("C", lambda t: t, lambda t: -t, "TZ", "ript"),
                    ("C", lambda t: t + 1, lambda t: 8 - t, "TE", "ript"),
                    ("B", lambda t: 7 - t, lambda t: t, "TS", "ritp")]):
                st_ = ST[ti % 2]
                stk = ("ST", ti % 2)
                if layout == "ript":
                    st5 = st_.rearrange("q (r p t c) -> q r p t c", r=2, p=64, t=8)
                else:
                    st5 = st_.rearrange("q (r t c p) -> q r t c p", r=2, t=8, c=16)
                for t in range(8):
                    mixed_power(pf(t), pb_(t))
                    if side == "B":
                        if layout == "ript":
                            o_r, o_i = st5[:, 0, :, t, :], st5[:, 1, :, t, :]
                        else:
                            o_r = st5[:, 0, t, :, :].rearrange("q c p -> q p c")
                            o_i = st5[:, 1, t, :, :].rearrange("q c p -> q p c")
                        cmul_big(PMr, PMi, ["PM"], Bbr, Bbi, ["Bb"], bc_b, w3b, o_r, o_i, stk, False)
                    else:
                        o_r = st5[:, 0, :, t, :].rearrange("q p c -> q c p")
                        o_i = st5[:, 1, :, t, :].rearrange("q p c -> q c p")
                        cmul_big(PMr, PMi, ["PM"], Cr, Ci, ["Cr", "Ci"], bc_c, w3c, o_r, o_i, stk, True)
                if dst in ("TX", "TZ"):
                    S.dma("sp", lambda e, st_=st_, dst=dst: e.dma_start(out=T[dst], in_=st_), reads=[stk], writes=[dst], key=stk)
                elif dst == "TE":
                    st4 = st_.rearrange("q (r p n) -> q r p n", r=2, p=64)
                    for d in range(2):
                        for r_ in range(2):
                            S.dma("sp", lambda e, d=d, r_=r_, st4=st4: e.dma_start(
                                out=T["TE"][r_, :, d * 64:(d + 1) * 64, :], in_=st4[d * 64:(d + 1) * 64, r_]), reads=[stk], writes=["TE"], key=stk)
                else:
                    st4 = st_.rearrange("q (r n p) -> q r n p", r=2, p=64)
                    for d in range(2):
                        for r_ in range(2):
                            for g8 in range(8):
                                S.dma("sp", lambda e, d=d, r_=r_, g8=g8, st4=st4: e.dma_start(
                                    out=T["TS"][r_, g8 * 8:(g8 + 1) * 8, :, d * 64:(d + 1) * 64],
                                    in_=st4[d * 64 + g8 * 8:d * 64 + (g8 + 1) * 8, r_]), reads=[stk], writes=["TS"], key=stk)
            S.barrier()
            AR.release(m)
            m = AR.mark()
            mkf = AR.alloc([128, 128], F32)
            mkb = AR.alloc([128, 128], F32)
            r16 = AR.alloc([128, 128], F32)
            dgc = AR.alloc([128, 16], F32)
            dmat = AR.alloc([128, 64], F32)
            dcol = AR.alloc([128, 64], F32)
            S.dma("sp", lambda e: e.dma_start(out=mkf, in_=T["cst"][CST_S5F]), writes=["mkf"])
            S.dma("sp", lambda e: e.dma_start(out=mkb, in_=T["cst"][CST_S5B]), writes=["mkb"])
            S.dma("sp", lambda e: e.dma_start(out=r16, in_=T["cst"][CST_R16]), writes=["r16"])
            S.dma("sp", lambda e: e.dma_start(out=dgc[0:64, :], in_=T["l1_d_skip"].rearrange("(g c) -> g c", c=16)), writes=["dgc"])
            S.op("pe", lambda e: e.transpose(ps[0:16, 7, 0:64], dgc[0:64, :], ident_f[0:64, 0:64]), reads=["dgc", "ident_f"], writes=[PSK[7]])
            S.op("dve", lambda e: e.tensor_copy(out=dmat[0:16, :], in_=ps[0:16, 7, 0:64]), reads=[PSK[7]], writes=["dmat"])
            S.op("pe", lambda e: e.matmul(ps[:, 7, 64:128], r16[0:16, :], dmat[0:16, :], start=True, stop=True), reads=["r16", "dmat"], writes=[PSK[7]])
            S.op("dve", lambda e: e.tensor_copy(out=dcol, in_=ps[:, 7, 64:128]), reads=[PSK[7]], writes=["dcol"])
            XL = [[AR.alloc([128, 16, 128], BF16) for _ in range(2)] for _ in range(2)]
            ZL = [[AR.alloc([128, 16, 128], BF16) for _ in range(2)] for _ in range(2)]
            Mst = [AR.alloc([128, 16, 128], BF16) for _ in range(2)]
            tA = [AR.alloc([128, 128], F32) for _ in range(2)]
            tB = [AR.alloc([128, 128], F32) for _ in range(2)]
            TX4 = T["TX"].rearrange("(d g) (k n) -> d g k n", d=2, k=128)
            TZ4 = T["TZ"].rearrange("(d g) (k n) -> d g k n", d=2, k=128)
            for gb in range(4):
                bf = gb % 2
                for d in range(2):
                    S.dma("sp", lambda e, d=d, gb=gb, bf=bf: e.dma_start(out=XL[bf][d], in_=TX4[d, gb * 16:(gb + 1) * 16].rearrange("g k n -> k g n")),
                          reads=["TX"], writes=[("XL", bf, d)])
                    S.dma("sp", lambda e, d=d, gb=gb, bf=bf: e.dma_start(out=ZL[bf][d], in_=TZ4[d, gb * 16:(gb + 1) * 16].rearrange("g k n -> k g n")),
                          reads=["TZ"], writes=[("ZL", bf, d)])
                for gi in range(16):
                    g = gb * 16 + gi
                    b = gi % 4
                    for d in range(2):
                        S.op("pe", lambda e, b=b, d=d, bf=bf, gi=gi: e.matmul(ps[:, b, d * 128:(d + 1) * 128], XL[bf][d][:, gi, :], ZL[bf][d][:, gi, :],
                                                                            start=True, stop=True),
                             reads=[("XL", bf, d), ("ZL", bf, d)], writes=[PSK[b]], acc=True)
                    a_, b_ = tA[gi % 2], tB[gi % 2]
                    S.op("dve", lambda e, a_=a_, b=b: e.tensor_tensor(out=a_, in0=ps[:, b, 0:128], in1=mkf, op=ALU.mult), reads=[PSK[b], "mkf"], writes=[("tA", gi % 2)])
                    S.op("dve", lambda e, b_=b_, b=b: e.tensor_tensor(out=b_, in0=ps[:, b, 128:256], in1=mkb, op=ALU.mult), reads=[PSK[b], "mkb"], writes=[("tB", gi % 2)])
                    S.op("dve", lambda e, a_=a_, b_=b_: e.tensor_tensor(out=a_, in0=a_, in1=b_, op=ALU.add), reads=[("tA", gi % 2), ("tB", gi % 2)], writes=[("tA", gi % 2)])
                    S.op("dve", lambda e, a_=a_, g=g, gi=gi, bf=bf: e.scalar_tensor_tensor(out=Mst[bf][:, gi, :], in0=ident_f, scalar=dcol[:, g:g + 1], in1=a_,
                                                                                         op0=ALU.mult, op1=ALU.add),
                         reads=[("tA", gi % 2), "dcol", "ident_f"], writes=[("Mst", bf)])
                S.dma("sp", lambda e, gb=gb, bf=bf: e.dma_start(out=T["TM"][gb * 16:(gb + 1) * 16].rearrange("g k n -> k g n"), in_=Mst[bf]),
                      reads=[("Mst", bf)], writes=["TM"], key=("Mst", bf))
            S.barrier()
            AR.release(m)

        def run_path(p):
            nseq, slen = (4, 256) if p == 0 else (1, 1024)
            AR.release(base_mark)
            hT = AR.alloc([128, 8, NT], F32)
            xnT = AR.alloc([128, 8, NT], BF16)
            modT = AR.alloc([128, 48], F32)
            bmodT = AR.alloc([128, 48], F32)
            n1T = AR.alloc([128, 8], F32)
            n2T = AR.alloc([128, 8], F32)
            A1 = AR.alloc([128, 8], F32)
            A2 = AR.alloc([128, 8], F32)
            condT = AR.alloc([128, 8], F32)
            sT = AR.alloc([128, 8], BF16)
            rstd = AR.alloc([128, NT], F32)
            lnv = AR.alloc([128, 512], F32)
            path_mark = AR.mark()
            HK = [("hT", kc) for kc in range(8)]
            XK = [("xnT", kc) for kc in range(8)]

            m0 = AR.mark()
            xst = [AR.alloc([128, D], F32) for _ in range(2)]
            for tt in range(8):
                xs = xst[tt % 2]
                xk = ("xst", tt % 2)
                S.dma("sp", lambda e, xs=xs, tt=tt: e.dma_start(out=xs, in_=T["xin"][p, tt * 128:(tt + 1) * 128, :]), writes=[xk])
                for hf in range(2):
                    b = (tt * 2 + hf) % 8
                    for q in range(4):
                        kc = hf * 4 + q
                        S.op("pe", lambda e, b=b, q=q, kc=kc, xs=xs: e.transpose(ps[:, b, q * 128:(q + 1) * 128], xs[:, kc * 128:(kc + 1) * 128], ident_f),
                             reads=[xk, "ident_f"], writes=[PSK[b]], acc=True)
                    S.op("act" if hf else "dve",
                         (lambda e, b=b, hf=hf, tt=tt: e.activation(out=hT[:, hf * 4:hf * 4 + 4, tt * 128:(tt + 1) * 128],
                                                                     in_=ps[:, b, :].rearrange("p (a b) -> p a b", a=4), func=AF.Copy)) if hf else
                         (lambda e, b=b, hf=hf, tt=tt: e.tensor_copy(out=hT[:, hf * 4:hf * 4 + 4, tt * 128:(tt + 1) * 128],
                                                                      in_=ps[:, b, :].rearrange("p (a b) -> p a b", a=4))),
                         reads=[PSK[b]], writes=HK[hf * 4:hf * 4 + 4])
            S.barrier()
            AR.release(m0)
            if p == 0:
                cond2 = AR.alloc([128, 2, 8], F32)
                sT2 = AR.alloc([128, 8, 2], BF16)
                path_mark = AR.mark()
                for c_ in range(2):
                    small_load_T(cond2[:, c_, :], T["cond"][c_], "(kc q) -> q kc", ("cond2", c_), q=128)
                S.op("act", lambda e: e.activation(out=sT2.rearrange("q k c -> q c k"), in_=cond2, func=AF.Silu),
                     reads=[("cond2", 0), ("cond2", 1)], writes=["sT"])

            def compute_mod(l):
                m = AR.mark()
                small_load_T(n1T, T["norm1"][l], "(kc q) -> q kc", "n1T", q=128)
                small_load_T(n2T, T["norm2"][l], "(kc q) -> q kc", "n2T", q=128)
                if p == 0:
                    small_load_T(bmodT, T["b_mod"][l], "(c q) -> q c", "bmodT", q=128)
                    wm = [AR.alloc([128, 8, 512], BF16) for _ in range(2)]
                    for blk in range(12):
                        w = wm[blk % 2]
                        wk = ("wm", blk % 2)
                        S.dma("pool", lambda e, w=w, blk=blk: e.dma_start(
                            out=w, in_=T["w_mod"][l][:, blk * 512:(blk + 1) * 512].rearrange("(kc q) n -> q kc n", q=128)), writes=[wk])
                        for cc in range(4):
                            col = blk * 4 + cc
                            for kc in range(8):
                                mm(ps[:, 0, 2 * col:2 * col + 2], w[:, kc, cc * 128:(cc + 1) * 128], sT2[:, kc, :], kc == 0, kc == 7,
                                   [wk, "sT"], PSK[0])
                    pv = ps[:, 0, 0:96].rearrange("q (n c) -> q c n", c=2)
                    for c_ in range(2):
                        S.op("dve", lambda e, c_=c_: e.tensor_tensor(out=modAll[:, l, c_, :], in0=pv[:, c_, :], in1=bmodT, op=ALU.add),
                             reads=[PSK[0], "bmodT"], writes=[("modAll", l)])
                S.op("dve", lambda e: e.tensor_copy(out=modT, in_=modAll[:, l, p, :]), reads=[("modAll", l)], writes=["modT"])
                S.op("dve", lambda e: e.scalar_tensor_tensor(out=A1, in0=modT[:, 8:16], scalar=1.0, in1=n1T, op0=ALU.add, op1=ALU.mult),
                     reads=["modT", "n1T"], writes=["A1"])
                S.op("dve", lambda e: e.scalar_tensor_tensor(out=A2, in0=modT[:, 32:40], scalar=1.0, in1=n2T, op0=ALU.add, op1=ALU.mult),
                     reads=["modT", "n2T"], writes=["A2"])
                AR.release(m)

            def norm_mod(Acol, Akey, Boff, perm8=False):
                m = AR.mark()
                sq = AR.alloc([128, 8, NT], BF16)
                tmp = [AR.alloc([128, NT], F32) for _ in range(2)]
                S.op("act", lambda e: e.activation(out=sq, in_=hT, func=AF.Square), reads=HK, writes=["sq"])
                for hf in range(2):
                    b = hf
                    for kc in range(8):
                        mm(ps[:, b, :], ones_b, sq[:, kc, hf * 512:(hf + 1) * 512], kc == 0, kc == 7, ["sq", "ones_b"], PSK[b])
                    S.op("act", lambda e, b=b: e.activation(out=lnv, in_=ps[:, b, :], func=AF.Ln, bias=EPS, scale=1.0 / D),
                         reads=[PSK[b]], writes=["lnv"])
                    S.op("act", lambda e, hf=hf: e.activation(out=rstd[:, hf * 512:(hf + 1) * 512], in_=lnv, func=AF.Exp, scale=-0.5),
                         reads=["lnv"], writes=[("rstd", hf)])
                for kc in range(8):
                    t = tmp[kc % 2]
                    tk = ("nm_tmp", kc % 2)
                    S.op("dve", lambda e, t=t, kc=kc: e.scalar_tensor_tensor(out=t, in0=hT[:, kc, :], scalar=Acol[:, kc:kc + 1], in1=rstd,
                                                                              op0=ALU.mult, op1=ALU.mult),
                         reads=[HK[kc], Akey, ("rstd", 0), ("rstd", 1)], writes=[tk])
                    if perm8:
                        o = xnT[:, kc, :].rearrange("p (t k) -> p k t", t=8)
                    else:
                        o = xnT[:, kc, :]
                    S.op("act", lambda e, t=t, kc=kc, o=o: e.activation(out=o, in_=t, func=AF.Identity, bias=modT[:, Boff + kc:Boff + kc + 1]),
                         reads=[tk, "modT"], writes=[XK[kc]])
                S.barrier()
                AR.release(m)

            def resid(oc, hf, b, goff):
                S.op("dve", lambda e: e.scalar_tensor_tensor(out=hT[:, oc, hf * 512:(hf + 1) * 512], in0=ps[:, b, :],
                                                              scalar=modT[:, goff + oc:goff + oc + 1],
                                                              in1=hT[:, oc, hf * 512:(hf + 1) * 512], op0=ALU.mult, op1=ALU.add),
                     reads=[PSK[b], "modT", HK[oc]], writes=[HK[oc]])

            def ffn(l):
                m = AR.mark()
                norm_mod(A2, "A2", 24)
                gT = AR.alloc([128, 22, NT], BF16)
                wd = AR.alloc([128, 22, D], BF16)
                bupT = AR.alloc([128, 44], F32)
                cbT = AR.alloc([128, 44], F32)
                ckT = AR.alloc([128, 3, 44], F32)
                wu = [AR.alloc([128, 2, 8, 256], BF16) for _ in range(2)]
                hb = [AR.alloc([128, nseq, slen + 2], F32) for _ in range(2)]
                tc = [AR.alloc([128, nseq, slen], F32) for _ in range(2)]
                sg = AR.alloc([128, nseq, slen], F32)
                small_load_T(bupT, T["b_up"][l], "(c q) -> q c", "bupT", q=128)
                small_load_T(cbT, T["conv_b"][l], "(c q) -> q c", "cbT", q=128)
                for t3 in range(3):
                    small_load_T(ckT[:, t3, :], T["conv_k"][l, t3], "(c q) -> q c", ("ckT", t3), q=128)
                CK = [("ckT", t3) for t3 in range(3)]
                for i in range(2):
                    S.op("pool", lambda e, i=i: e.memset(hb[i], 0.0), writes=[("hb", i)])
                for j4 in range(0, 22, 6):
                    je = min(22, j4 + 6)
                    S.dma("pool", lambda e, j4=j4, je=je: e.dma_start(
                        out=wd[:, j4:je, :], in_=T["w_down"][l][j4 * 128:je * 128, :].rearrange("(j q) n -> q j n", q=128)),
                        writes=[("wd", j4)])
                WDK = [("wd", j4) for j4 in range(0, 22, 6)]
                for jb in range(11):
                    w = wu[jb % 2]
                    wk = ("wu", jb % 2)
                    for gv in range(2):
                        c0 = gv * DFF + jb * 256
                        S.dma("pool", lambda e, w=w, gv=gv, c0=c0: e.dma_start(
                            out=w[:, gv], in_=T["w_up"][l][:, c0:c0 + 256].rearrange("(kc q) n -> q kc n", q=128)),
                            writes=[(wk, gv)], key=(wk, gv))
                    for sub in range(2):
                        j = jb * 2 + sub
                        bs = (j % 2) * 4
                        for gv in range(2):
                            for hf in range(2):
                                b = bs + gv * 2 + hf
                                for kc in range(8):
                                    mm(ps[:, b, :], w[:, gv, kc, sub * 128:(sub + 1) * 128], xnT[:, kc, hf * 512:(hf + 1) * 512],
                                       kc == 0, kc == 7, [(wk, gv), XK[kc]], PSK[b])
                        for gv in range(2):
                            b = bs + gv * 2
                            col = gv * 22 + j
                            h_ = hb[gv]
                            t_ = tc[gv]
                            pin = ps[:, b:b + 2, :].rearrange("p b (s t) -> p (b s) t", t=slen) if nseq == 4 else \
                                ps[:, b:b + 2, :].rearrange("p (o b) t -> p o (b t)", o=1)
                            S.op("act", lambda e, h_=h_, pin=pin, col=col: e.activation(out=h_[:, :, 1:slen + 1], in_=pin, func=AF.Identity,
                                                                                        bias=bupT[:, col:col + 1]),
                                 reads=[PSK[b], PSK[b + 1], "bupT"], writes=[("hb", gv)])
                            S.op("act", lambda e, h_=h_, t_=t_, col=col: e.activation(out=t_, in_=h_[:, :, 1:slen + 1], func=AF.Identity,
                                                                                      bias=cbT[:, col:col + 1], scale=ckT[:, 1, col:col + 1]),
                                 reads=[("hb", gv), "cbT", CK[1]], writes=[("tc", gv)])
                            S.op("dve", lambda e, h_=h_, t_=t_, col=col: e.scalar_tensor_tensor(out=t_, in0=h_[:, :, 0:slen], scalar=ckT[:, 0, col:col + 1],
                                                                                                in1=t_, op0=ALU.mult, op1=ALU.add),
                                 reads=[("hb", gv), CK[0], ("tc", gv)], writes=[("tc", gv)])
                            S.op("dve", lambda e, h_=h_, t_=t_, col=col: e.scalar_tensor_tensor(out=t_, in0=h_[:, :, 2:slen + 2], scalar=ckT[:, 2, col:col + 1],
                                                                                                in1=t_, op0=ALU.mult, op1=ALU.add),
                                 reads=[("hb", gv), CK[2], ("tc", gv)], writes=[("tc", gv)])
                        S.op("act", lambda e: e.activation(out=sg, in_=tc[0], func=AF.Silu), reads=[("tc", 0)], writes=["sg"])
                        go = gT[:, j, :].rearrange("p (s t) -> p s t", s=nseq)
                        S.op("dve", lambda e, go=go: e.tensor_tensor(out=go, in0=sg, in1=tc[1], op=ALU.mult),
                             reads=["sg", ("tc", 1)], writes=[("gT", j)])
                bi = 0
                for oc in range(8):
                    for hf in range(2):
                        b = bi % 8
                        bi += 1
                        for j in range(22):
                            mm(ps[:, b, :], wd[:, j, oc * 128:(oc + 1) * 128], gT[:, j, hf * 512:(hf + 1) * 512], j == 0, j == 21,
                               [WDK[j // 6], ("gT", j)], PSK[b])
                        resid(oc, hf, b, 40)
                AR.release(m)

            def attention(l):
                L = LAYERS[l]
                dh, H, ckv = L["dh"], L["H"], L["ckv"]
                G = H // 4
                nkc = ckv // 128
                ctot = D + 2 * ckv
                scale = dh ** -0.5
                rope = (p == 1)
                m = AR.mark()
                norm_mod(A1, "A1", 0)
                if ATT_STOP <= -1:
                    AR.release(m)
                    return
                wbig = AR.alloc([128, 8, ctot], BF16)
                qT = AR.alloc([128, 8, NT], BF16)
                kT = AR.alloc([128, nkc, NT], BF16)
                vx = AR.alloc([128, 8, 4, dh + 2], BF16)
                qn_c = AR.alloc([128, 1], F32)
                kn_c = AR.alloc([128, 1], F32)
                sqb = [AR.alloc([128, 512], BF16) for _ in range(2)]
                rs = [AR.alloc([128, 512], F32) for _ in range(2)]
                qf = [AR.alloc([128, 512], F32) for _ in range(2)]
                S.dma("pool", lambda e: e.dma_start(out=wbig[:, :, 0:768], in_=T[L["wqkv"]][:, 0:768].rearrange("(kc q) n -> q kc n", q=128)),
                      writes=[("wbig", 0)])
                S.dma("pool", lambda e: e.dma_start(out=wbig[:, :, 768:ctot], in_=T[L["wqkv"]][:, 768:ctot].rearrange("(kc q) n -> q kc n", q=128)),
                      writes=[("wbig", 1)])
                WB = [("wbig", 0), ("wbig", 1)]
                small_load_col(qn_c, T[L["qn"]], dh, "qn_c")
                small_load_col(kn_c, T[L["kn"]], dh, "kn_c")
                QNK = ["qn_c"]
                KNK = ["kn_c"]
                if rope:
                    cosT = AR.alloc([128, NT], F32)
                    sinT = AR.alloc([128, NT], F32)
                    ri = 0 if dh == 64 else 2
                    S.dma("sp", lambda e: e.dma_start(out=cosT, in_=T["rope"][ri]), writes=["cosT"])
                    S.dma("sp", lambda e: e.dma_start(out=sinT, in_=T["rope"][ri + 1]), writes=["sinT"])
                    pm = pa_b if dh == 64 else pc_b
                    r1 = [AR.alloc([128, 512], F32) for _ in range(2)]
                    qb16 = [AR.alloc([128, 512], BF16) for _ in range(2)]
                esink = None
                if L["sink"]:
                    esink = AR.alloc([128, 16], F32)
                    S.dma("sp", lambda e: e.dma_start(out=tst2[0:1, 0:16], in_=T[L["sink"]].rearrange("(o q) -> o q", o=1)),
                          writes=[("tst2", 0)], key=("tst2", 0))
                    S.op("pe", lambda e: e.matmul(ps[:, 7, 0:16], ones_f[0:2, :], tst2[0:2, 0:16], start=True, stop=True),
                         reads=[("tst2", 0), "tst2z", "ones_f"], writes=[PSK[7], ("tst2", 0)])
                    S.op("act", lambda e: e.activation(out=esink, in_=ps[:, 7, 0:16], func=AF.Exp), reads=[PSK[7]], writes=["esink"])
                if p == 0:
                    kst = [AR.alloc([128, ckv], F32) for _ in range(2)]
                    vst = [AR.alloc([128, ckv], F32) for _ in range(2)]
                    kf32 = AR.alloc([128, nkc, NT], F32)
                S.op("pool", lambda e: e.memset(vx, 1.0), writes=["vx_init"])

                if ATT_STOP <= 0:
                    AR.release(m)
                    return
                for tt in range(8):
                    b = 4 + tt % 2
                    for kc in range(8):
                        mm(ps[:, b, 0:ckv], xnT[:, kc, tt * 128:(tt + 1) * 128], wbig[:, kc, D + ckv:D + 2 * ckv], kc == 0, kc == 7,
                           [XK[kc], WB[1]], PSK[b])
                    if p == 0:
                        v_ = vst[tt % 2]
                        vk = ("vst", tt % 2)
                        S.op("dve", lambda e, b=b, v_=v_: e.tensor_copy(out=v_, in_=ps[:, b, 0:ckv]), reads=[PSK[b]], writes=[vk])
                        S.op("act", lambda e, v_=v_, tt=tt: e.activation(out=vx[:, tt, :, 0:dh], in_=v_.rearrange("p (g d) -> p g d", g=4),
                                                                        func=AF.Copy),
                             reads=[vk, "vx_init"], writes=[("vx", tt)])
                        S.dma("sp", lambda e, v_=v_, tt=tt: e.dma_start(out=T[L["nv"]][tt * 128:(tt + 1) * 128, :], in_=v_), reads=[vk])
                    else:
                        S.op("act", lambda e, b=b, tt=tt: e.activation(out=vx[:, tt, :, 0:dh], in_=ps[:, b, 0:ckv].rearrange("p (g d) -> p g d", g=4),
                                                                       func=AF.Copy),
                             reads=[PSK[b], "vx_init"], writes=[("vx", tt)])
                if ATT_STOP <= 1:
                    AR.release(m)
                    return
                def lhs_cols(kind, c):
                    if kind == "q":
                        if dh == 128:
                            return [(c * 128, 128, 0)]
                        mq, i = c // 4, c % 4
                        return [((8 * mq + i) * 64, 64, 0), ((8 * mq + 4 + i) * 64, 64, 64)]
                    return [(D + c * 128, 128, 0)]

                ci = 0
                for kind, nch in (("k", nkc), ("q", 8)):
                    for c in range(nch):
                        for hf in range(2):
                            b = ci % 2
                            b2 = 2 + ci % 2
                            sl = slice(hf * 512, (hf + 1) * 512)
                            cols = lhs_cols(kind, c)
                            for (c0, w_, po) in cols:
                                for kc in range(8):
                                    mm(ps[po:po + w_, b, :], wbig[:, kc, c0:c0 + w_], xnT[:, kc, sl], kc == 0, kc == 7,
                                       [XK[kc], WB[0], WB[1]], PSK[b])
                            sq_ = sqb[ci % 2]
                            sk = ("sqb", ci % 2)
                            S.op("act", lambda e, sq_=sq_, b=b: e.activation(out=sq_, in_=ps[:, b, :], func=AF.Square), reads=[PSK[b]], writes=[sk])
                            mm(ps[:, b2, :], bd64_b if dh == 64 else ones_b, sq_, True, True, [sk, "cst_misc", "ones_b"], PSK[b2])
                            rs_ = rs[ci % 2]
                            rk = ("rs", ci % 2)
                            S.op("act", lambda e, b2=b2: e.activation(out=lnv, in_=ps[:, b2, :], func=AF.Ln, bias=EPS, scale=1.0 / dh),
                                 reads=[PSK[b2]], writes=["lnv"])
                            S.op("act", lambda e, rs_=rs_: e.activation(out=rs_, in_=lnv, func=AF.Exp, scale=-0.5), reads=["lnv"], writes=[rk])
                            gcol, gk = (qn_c, QNK) if kind == "q" else (kn_c, KNK)
                            dstT = qT if kind == "q" else kT
                            dk = (kind + "T", c, hf)
                            if not rope:
                                if kind == "k":
                                    S.op("dve", lambda e, c=c, sl=sl, b=b, rs_=rs_: e.scalar_tensor_tensor(
                                        out=kf32[:, c, sl], in0=ps[:, b, :], scalar=kn_c[:, 0:1], in1=rs_, op0=ALU.mult, op1=ALU.mult),
                                        reads=[PSK[b], rk] + gk, writes=[("kf32", c, hf)])
                                    S.op("pool", lambda e, c=c, sl=sl: e.tensor_copy(out=kT[:, c, sl], in_=kf32[:, c, sl]),
                                         reads=[("kf32", c, hf)], writes=[dk])
                                else:
                                    S.op("dve", lambda e, c=c, sl=sl, b=b, rs_=rs_, gcol=gcol, dstT=dstT: e.scalar_tensor_tensor(
                                        out=dstT[:, c, sl], in0=ps[:, b, :], scalar=gcol[:, 0:1], in1=rs_, op0=ALU.mult, op1=ALU.mult),
                                        reads=[PSK[b], rk] + gk, writes=[dk])
                            else:
                                q_ = qf[ci % 2]
                                qk_ = ("qf", ci % 2)
                                qb_ = qb16[ci % 2]
                                qbk = ("qb16", ci % 2)
                                r_ = r1[ci % 2]
                                r1k = ("r1", ci % 2)
                                S.op("dve", lambda e, q_=q_, b=b, rs_=rs_, gcol=gcol: e.scalar_tensor_tensor(
                                    out=q_, in0=ps[:, b, :], scalar=gcol[:, 0:1], in1=rs_, op0=ALU.mult, op1=ALU.mult),
                                    reads=[PSK[b], rk] + gk, writes=[qk_])
                                S.op("pool", lambda e, q_=q_, qb_=qb_: e.tensor_copy(out=qb_, in_=q_), reads=[qk_], writes=[qbk])
                                mm(ps[:, b, :], pm, qb_, True, True, [qbk, "cst_misc"], PSK[b])
                                S.op("dve", lambda e, r_=r_, b=b, sl=sl: e.tensor_tensor(out=r_, in0=ps[:, b, :], in1=sinT[:, sl], op=ALU.mult),
                                     reads=[PSK[b], "sinT"], writes=[r1k])
                                S.op("pool", lambda e, q_=q_, sl=sl: e.tensor_tensor(out=q_, in0=q_, in1=cosT[:, sl], op=ALU.mult),
                                     reads=[qk_, "cosT"], writes=[qk_])
                                S.op("dve", lambda e, q_=q_, r_=r_, c=c, sl=sl, dstT=dstT: e.tensor_tensor(out=dstT[:, c, sl], in0=q_, in1=r_, op=ALU.add),
                                     reads=[qk_, r1k], writes=[dk])
                            ci += 1
                if ATT_STOP <= 2:
                    AR.release(m)
                    return
                if p == 0:
                    for tt in range(8):
                        b = 6 + tt % 2
                        for c in range(nkc):
                            S.op("pe", lambda e, b=b, c=c, tt=tt: e.transpose(ps[:, b, c * 128:(c + 1) * 128], kf32[:, c, tt * 128:(tt + 1) * 128], ident_f),
                                 reads=[("kf32", c, tt // 4), "ident_f"], writes=[PSK[b]], acc=True)
                        k_ = kst[tt % 2]
                        kk = ("kst", tt % 2)
                        S.op("dve", lambda e, b=b, k_=k_: e.tensor_copy(out=k_, in_=ps[:, b, 0:ckv]), reads=[PSK[b]], writes=[kk])
                        S.dma("sp", lambda e, k_=k_, tt=tt: e.dma_start(out=T[L["nk"]][tt * 128:(tt + 1) * 128, :], in_=k_), reads=[kk])

                if p == 1:
                    kcT = AR.alloc([128, nkc, 512], BF16)
                    vcx = AR.alloc([128, 4, 4, dh + 2], BF16)
                    cst_ = [AR.alloc([128, ckv], F32) for _ in range(2)]
                    cb_ = [AR.alloc([128, ckv], BF16) for _ in range(2)]
                    S.op("pool", lambda e: e.memset(vcx, 1.0), writes=["vcx_init"])
                    for t4 in range(4):
                        c_ = cst_[t4 % 2]
                        ck_ = ("cst_", t4 % 2)
                        S.dma("sp", lambda e, c_=c_, t4=t4: e.dma_start(out=c_, in_=T[L["cv"]][t4 * 128:(t4 + 1) * 128, :]), writes=[ck_])
                        S.op("dve", lambda e, c_=c_, t4=t4: e.tensor_copy(out=vcx[:, t4, :, 0:dh], in_=c_.rearrange("p (g d) -> p g d", g=4)),
                             reads=[ck_, "vcx_init"], writes=[("vcx", t4)])
                    for t4 in range(4):
                        c_ = cst_[t4 % 2]
                        ck_ = ("cst_", t4 % 2)
                        bb = cb_[t4 % 2]
                        bk = ("cb_", t4 % 2)
                        S.dma("sp", lambda e, c_=c_, t4=t4: e.dma_start(out=c_, in_=T[L["ck"]][t4 * 128:(t4 + 1) * 128, :]), writes=[ck_])
                        S.op("dve", lambda e, c_=c_, bb=bb: e.tensor_copy(out=bb, in_=c_), reads=[ck_], writes=[bk])
                        b = 6 + t4 % 2
                        pb = ps[:, b, 0:64 * nkc].bitcast(BF16).rearrange("p (c t) -> p c t", c=nkc)
                        for c in range(nkc):
                            S.op("pe", lambda e, pb=pb, c=c, bb=bb: e.transpose(pb[:, c, :], bb[:, c * 128:(c + 1) * 128], ident_b),
                                 reads=[bk, "ident_b"], writes=[PSK[b]], acc=True)
                        S.op("dve", lambda e, pb=pb, t4=t4: e.tensor_copy(out=kcT[:, :, t4 * 128:(t4 + 1) * 128], in_=pb),
                             reads=[PSK[b]], writes=[("kcT", t4)])

                if ATT_STOP <= 3:
                    AR.release(m)
                    return
                S.dma("pool", lambda e: e.dma_start(out=wbig[:, :, 0:D], in_=T[L["wo"]].rearrange("(kc q) n -> q kc n", q=128)),
                      reads=[], writes=[("wbig", 0), ("wbig", 1)], key=("wbig", 0))

                nkt_max = 7 if L["kind"] == "A" else 12
                PT = [AR.alloc([128, nkt_max, G * 128], BF16) for _ in range(2)]
                otok = [AR.alloc([128, D], BF16) for _ in range(2)]
                OT = xnT
                den = AR.alloc([128, 4], F32)
                rden = AR.alloc([128, 4], F32)
                it = 0
                sb_i = 0
                for qt in range(8):
                    s_ = qt // (slen // 128)
                    ql = qt % (slen // 128)
                    kts = []
                    if p == 1:
                        kts += [("c", t4, None) for t4 in range(4)]
                        if L["kind"] == "A":
                            if qt > 0:
                                kts.append(("n", qt - 1, mlo_b))
                            kts.append(("n", qt, None))
                            if qt < 7:
                                kts.append(("n", qt + 1, mhi_b))
                        else:
                            kts += [("n", t8, None) for t8 in range(8)]
                    else:
                        kts += [("n", s_ * 2 + t2, None) for t2 in range(2)]
                    ot = otok[qt % 2]
                    otk = ("otok", qt % 2)
                    for g in range(4):
                        pt = PT[it % 2]
                        ptk = ("PT", it % 2)
                        ob = 4 + it % 2
                        it += 1
                        if dh == 64:
                            mq, e2 = g // 2, g % 2
                            prt = slice(64 * e2, 64 * e2 + 64)
                            rhs = qT[prt, 4 * mq:4 * mq + 4, qt * 128:(qt + 1) * 128]
                            qreads = [("qT", 4 * mq + i, qt // 4) for i in range(4)]
                            kch = mq
                        else:
                            prt = slice(0, 128)
                            rhs = qT[:, 2 * g:2 * g + 2, qt * 128:(qt + 1) * 128]
                            qreads = [("qT", 2 * g + i, qt // 4) for i in range(2)]
                            kch = g
                        for ki, (src, ti, msk) in enumerate(kts):
                            b = sb_i % 4
                            sb_i += 1
                            if src == "c":
                                lhsT = kcT[prt, kch, ti * 128:(ti + 1) * 128]
                                kr = [("kcT", ti)]
                            else:
                                lhsT = kT[prt, kch, ti * 128:(ti + 1) * 128]
                                kr = [("kT", kch, ti // 4)]
                            mm(ps[:, b, 0:G * 128], lhsT, rhs, True, True, kr + qreads, PSK[b])
                            S.op("act", lambda e, pt=pt, ki=ki, b=b: e.activation(out=pt[:, ki, :], in_=ps[:, b, 0:G * 128], func=AF.Exp, scale=scale),
                                 reads=[PSK[b]], writes=[(ptk, ki)])
                            if msk is not None:
                                S.op("pool", lambda e, pt=pt, ki=ki, msk=msk: e.tensor_tensor(
                                    out=pt[:, ki, :], in0=pt[:, ki, :], in1=msk[:, 0:G, :].rearrange("p g q -> p (g q)"), op=ALU.mult),
                                    reads=[(ptk, ki), "cst_misc"], writes=[(ptk, ki)])
                        opv = ps[:, ob, 0:G * (dh + 1)].rearrange("p (i d) -> p i d", i=G)
                        for i in range(G):
                            for ki, (src, ti, msk) in enumerate(kts):
                                if src == "c":
                                    vr = vcx[:, ti, g, 0:dh + 1]
                                    vk = [("vcx", ti)]
                                else:
                                    vr = vx[:, ti, g, 0:dh + 1]
                                    vk = [("vx", ti)]
                                mm(opv[:, i, :], pt[:, ki, i * 128:(i + 1) * 128], vr, ki == 0, ki == len(kts) - 1,
                                   [(ptk, ki)] + vk, PSK[ob])
                        if esink is not None:
                            S.op("dve", lambda e, opv=opv, g=g: e.tensor_tensor(out=den[:, 0:G], in0=opv[:, :, dh], in1=esink[:, g * G:(g + 1) * G], op=ALU.add),
                                 reads=[PSK[ob], "esink"], writes=["den"])
                            S.op("dve", lambda e: e.reciprocal(out=rden[:, 0:G], in_=den[:, 0:G]), reads=["den"], writes=["rden"])
                        else:
                            S.op("dve", lambda e, opv=opv: e.reciprocal(out=rden[:, 0:G], in_=opv[:, :, dh]), reads=[PSK[ob]], writes=["rden"])
                        for i in range(G):
                            h = g * G + i
                            S.op("dve", lambda e, opv=opv, i=i, h=h, ot=ot: e.tensor_scalar(out=ot[:, h * dh:(h + 1) * dh], in0=opv[:, i, 0:dh],
                                                                                             scalar1=rden[:, i:i + 1], scalar2=None, op0=ALU.mult),
                                 reads=[PSK[ob], "rden"], writes=[(otk, h)])
                    b = 6 + qt % 2
                    pb = ps[:, b, :].bitcast(BF16).rearrange("p (c t) -> p c t", c=8)
                    for c in range(8):
                        S.op("pe", lambda e, pb=pb, c=c, ot=ot: e.transpose(pb[:, c, :], ot[:, c * 128:(c + 1) * 128], ident_b),
                             reads=[(otk, h) for h in range(H)] + ["ident_b"], writes=[PSK[b]], acc=True)
                    S.op("act", lambda e, pb=pb, qt=qt: e.activation(out=OT[:, :, qt * 128:(qt + 1) * 128], in_=pb, func=AF.Copy),
                         reads=[PSK[b]], writes=[("OT", qt)] + XK)
                if ATT_STOP <= 4:
                    AR.release(m)
                    return
                bi = 0
                for oc in range(8):
                    for hf in range(2):
                        b = bi % 4
                        bi += 1
                        for kc in range(8):
                            mm(ps[:, b, :], wbig[:, kc, oc * 128:(oc + 1) * 128], OT[:, kc, hf * 512:(hf + 1) * 512], kc == 0, kc == 7,
                               [("wbig", 0), ("wbig", 1)] + [("OT", hf * 4 + q) for q in range(4)], PSK[b])
                        resid(oc, hf, b, 16)
                AR.release(m)

            def s5_layer(l):
                CH = 128 // nseq
                m = AR.mark()
                norm_mod(A1, "A1", 0, perm8=True)
                U = AR.alloc([128, 64, 128], BF16)
                TB = AR.alloc([128, 2, 64, 128], BF16)
                SS = AR.alloc([128, 2, 64, 128], F32)
                ALL = AR.alloc([128, 2, 64], F32)
                ALS = AR.alloc([128, 2, 64], F32)
                xf = xnT.rearrange("q a n -> q (a n)").bitcast(F32)
                H0v = xf[:, 0:128 * nseq].rearrange("q (r g s) -> q r g s", r=2, g=64)
                L8 = AR.alloc([128, 2, 128], F32)
                FS = AR.alloc([128, 2, 64], F32)
                FSo = AR.alloc([128, 2, 128], F32)
                tt_ = [[xf[:, 128 * nseq * (1 + 2 * a_ + b_):128 * nseq * (2 + 2 * a_ + b_)].rearrange("q (r g s) -> q r g s", r=2, g=64)
                        for b_ in range(2)] for a_ in range(2)]
                S.dma("sp", lambda e: e.dma_start(out=T["D1"][p].rearrange("(fc q) n -> q fc n", q=128), in_=xnT), reads=XK, writes=["D1"])
                d1v = T["D1"][p].rearrange("(g c) (t k) -> t c g k", c=16, t=8)
                for t in range(8):
                    S.dma("sp", lambda e, t=t: e.dma_start(out=U[16 * t:16 * (t + 1)], in_=d1v[t]), reads=["D1"], writes=[("U", t)])
                UK = [("U", t) for t in range(8)]
                for r_ in range(2):
                    for g4 in range(4):
                        S.dma("sp", lambda e, r_=r_, g4=g4: e.dma_start(out=TB[:, r_, g4 * 16:(g4 + 1) * 16, :],
                                                                         in_=T["TS"][r_, g4 * 16:(g4 + 1) * 16].rearrange("g k n -> k g n")),
                              writes=[("TB", r_, g4)])
                S.dma("sp", lambda e: e.dma_start(out=L8[0:64], in_=T["TP8"].rearrange("r (d g) p -> g r d p", d=2)), writes=["L8"])
                for r_ in range(2):
                    S.op("pe", lambda e, r_=r_: e.transpose(ps[:, 7, 0:64], L8[0:64, r_, :], ident_f[0:64, 0:64]), reads=["L8", "ident_f"], writes=[PSK[7]])
                    if r_ == 0:
                        S.op("dve", lambda e: e.tensor_copy(out=ALL[:, 0, :], in_=ps[:, 7, 0:64]), reads=[PSK[7]], writes=["AL"])
                        S.op("dve", lambda e: e.tensor_copy(out=ALL[:, 1, :], in_=ps[:, 7, 0:64]), reads=[PSK[7]], writes=["AL"])
                    else:
                        S.op("dve", lambda e: e.tensor_scalar(out=ALS[:, 0, :], in0=ps[:, 7, 0:64], scalar1=-1.0, scalar2=None, op0=ALU.mult), reads=[PSK[7]], writes=["AL"])
                        S.op("dve", lambda e: e.tensor_copy(out=ALS[:, 1, :], in_=ps[:, 7, 0:64]), reads=[PSK[7]], writes=["AL"])
                if p == 1:
                    for r_ in range(2):
                        S.dma("sp", lambda e, r_=r_: e.dma_start(out=L8[0:64, r_, :].rearrange("g (d p) -> g d p", d=2),
                                                                  in_=T["st1"][:, r_].rearrange("d g p -> g d p")), reads=["L8"], writes=["L8"], key=("L8", r_))
                    for r_ in range(2):
                        S.op("pe", lambda e, r_=r_: e.transpose(ps[:, 7, 0:64], L8[0:64, r_, :], ident_f[0:64, 0:64]), reads=["L8", "ident_f"], writes=[PSK[7]])
                        S.op("dve", lambda e, r_=r_: e.tensor_copy(out=H0v[:, r_, :, 0], in_=ps[:, 7, 0:64]), reads=[PSK[7]], writes=["H0"] + XK)
                else:
                    S.op("dve", lambda e: e.memset(H0v, 0.0), writes=["H0"] + XK)
                for gq in range(16):
                    for r_ in range(2):
                        b = (gq * 2 + r_) % 4
                        for gi in range(4):
                            g = gq * 4 + gi
                            S.op("pe", lambda e, b=b, gi=gi, g=g, r_=r_: e.matmul(ps[:, b, gi * 128:(gi + 1) * 128], TB[:, r_, g, :], U[:, g, :], start=True, stop=True),
                                 reads=UK + [("TB", r_, g // 16)], writes=[PSK[b]], acc=True)
                        eng = "act" if r_ == 0 else "dve"
                        if r_ == 0:
                            S.op("act", lambda e, b=b, gq=gq: e.activation(out=SS[:, 0, gq * 4:(gq + 1) * 4, :].rearrange("q g k -> q (g k)"), in_=ps[:, b, :], func=AF.Copy),
                                 reads=[PSK[b]], writes=[("SS", 0)])
                        else:
                            S.op("dve", lambda e, b=b, gq=gq: e.tensor_copy(out=SS[:, 1, gq * 4:(gq + 1) * 4, :].rearrange("q g k -> q (g k)"), in_=ps[:, b, :]),
                                 reads=[PSK[b]], writes=[("SS", 1)])
                S.barrier()
                NS, CHV = nseq, CH
                SSv = SS.rearrange("q r g (s k) -> q r g s k", k=CHV)
                for d, eng in ((0, "dve"), (1, "pool")):
                    hs = slice(64 * d, 64 * d + 64)
                    order = list(range(CHV)) if d == 0 else list(range(CHV - 1, -1, -1))
                    t1, t2 = tt_[d]
                    k1, k2, kS = ("rt1", d), ("rt2", d), ("SSd", d)
                    all_ = ALL[hs].unsqueeze(3).broadcast_to([64, 2, 64, NS])
                    als_ = ALS[hs].unsqueeze(3).broadcast_to([64, 2, 64, NS])
                    for ii, kk in enumerate(order):
                        if ii == 0:
                            prev, prev_sw = H0v[hs], H0v[hs, ::-1]
                        else:
                            kp = order[ii - 1]
                            prev, prev_sw = SSv[hs, :, :, :, kp], SSv[hs, ::-1, :, :, kp]
                        cur = SSv[hs, :, :, :, kk]
                        S.op(eng, lambda e, hs=hs, t1=t1, prev=prev, all_=all_: e.tensor_tensor(out=t1[hs], in0=prev, in1=all_, op=ALU.mult),
                             reads=[kS, "AL", "H0"], writes=[k1])
                        S.op(eng, lambda e, hs=hs, t2=t2, prev_sw=prev_sw, als_=als_: e.tensor_tensor(out=t2[hs], in0=prev_sw, in1=als_, op=ALU.mult),
                             reads=[kS, "AL", "H0"], writes=[k2])
                        S.op(eng, lambda e, hs=hs, t1=t1, t2=t2: e.tensor_tensor(out=t1[hs], in0=t1[hs], in1=t2[hs], op=ALU.add), reads=[k1, k2], writes=[k1])
                        S.op(eng, lambda e, hs=hs, t1=t1, cur=cur: e.tensor_tensor(out=cur, in0=cur, in1=t1[hs], op=ALU.add), reads=[kS, k1], writes=[kS])
                S.barrier()
                if p == 0:
                    for s_ in range(nseq):
                        for r_ in range(2):
                            S.op("dve", lambda e, r_=r_, s_=s_: e.tensor_copy(out=FS[0:64, r_, :], in_=SS[0:64, r_, :, s_ * CH + CH - 1]), reads=[], writes=["FS"])
                            S.op("dve", lambda e, r_=r_, s_=s_: e.tensor_copy(out=FS[64:128, r_, :], in_=SS[64:128, r_, :, s_ * CH]), reads=[], writes=["FS"])
                            S.op("pe", lambda e, r_=r_: e.transpose(ps[0:64, 7, r_ * 128:(r_ + 1) * 128], FS[:, r_, :], ident_f), reads=["FS", "ident_f"], writes=[PSK[7]], acc=True)
                        S.op("dve", lambda e: e.tensor_copy(out=FSo[0:64].rearrange("g r n -> g (r n)"), in_=ps[0:64, 7, 0:256]), reads=[PSK[7]], writes=["FSo"])
                        for r_ in range(2):
                            S.dma("sp", lambda e, s_=s_, r_=r_: e.dma_start(out=T["nst"][s_][:, r_].rearrange("d g p -> g d p"),
                                                                           in_=FSo[0:64, r_, :].rearrange("g (d p) -> g d p", d=2)), reads=["FSo"], key=("nst", r_))
                for r_ in range(2):
                    for s_ in range(nseq):
                        k0 = s_ * CH
                        S.op("act", lambda e, r_=r_, k0=k0: e.activation(out=TB[0:64, r_, :, k0 + 1:k0 + CH], in_=SS[0:64, r_, :, k0:k0 + CH - 1], func=AF.Copy),
                             reads=[], writes=["Hin"])
                        S.op("act", lambda e, r_=r_, k0=k0, s_=s_: e.activation(out=TB[0:64, r_, :, k0], in_=H0v[0:64, r_, :, s_], func=AF.Copy), reads=["H0"], writes=["Hin"])
                        S.op("dve", lambda e, r_=r_, k0=k0: e.tensor_copy(out=TB[64:128, r_, :, k0:k0 + CH - 1], in_=SS[64:128, r_, :, k0 + 1:k0 + CH]),
                             reads=[], writes=["Hin"])
                        S.op("dve", lambda e, r_=r_, k0=k0, s_=s_: e.tensor_copy(out=TB[64:128, r_, :, k0 + CH - 1], in_=H0v[64:128, r_, :, s_]), reads=["H0"], writes=["Hin"])
                S.barrier()
                SSb = SS.rearrange("q r g k -> q (r g k)").bitcast(BF16)
                EL = SSb[:, 0:16384].rearrange("q (r g n) -> q r g n", r=2, g=64)
                ML = SSb[:, 16384:24576].rearrange("q (g n) -> q g n", g=64)
                YC = SSb[:, 24576:32768].rearrange("q (g n) -> q g n", g=64)
                for r_ in range(2):
                    for g4 in range(4):
                        S.dma("sp", lambda e, r_=r_, g4=g4: e.dma_start(out=EL[:, r_, g4 * 16:(g4 + 1) * 16, :],
                                                                         in_=T["TE"][r_, g4 * 16:(g4 + 1) * 16].rearrange("g k n -> k g n")),
                              writes=[("EL", r_, g4)])
                for g4 in range(4):
                    S.dma("sp", lambda e, g4=g4: e.dma_start(out=ML[:, g4 * 16:(g4 + 1) * 16, :], in_=T["TM"][g4 * 16:(g4 + 1) * 16].rearrange("g k n -> k g n")),
                          writes=[("ML", g4)])
                gu = [AR.alloc([128, 512], F32) for _ in range(2)]
                gs = [AR.alloc([128, 512], F32) for _ in range(2)]
                for gq in range(16):
                    b = gq % 4
                    for gi in range(4):
                        g = gq * 4 + gi
                        o_ = ps[:, b, gi * 128:(gi + 1) * 128]
                        S.op("pe", lambda e, o_=o_, g=g: e.matmul(o_, ML[:, g, :], U[:, g, :], start=True, stop=False), reads=[("ML", g // 16)] + UK, writes=[PSK[b]], acc=True)
                        S.op("pe", lambda e, o_=o_, g=g: e.matmul(o_, EL[:, 0, g, :], TB[:, 0, g, :], start=False, stop=False), reads=[("EL", 0, g // 16), "Hin"], writes=[PSK[b]], acc=True)
                        S.op("pe", lambda e, o_=o_, g=g: e.matmul(o_, EL[:, 1, g, :], TB[:, 1, g, :], start=False, stop=True), reads=[("EL", 1, g // 16), "Hin"], writes=[PSK[b]], acc=True)
                    u_, s2 = gu[gq % 2], gs[gq % 2]
                    uk, sk = ("gu", gq % 2), ("gs", gq % 2)
                    S.op("act", lambda e, u_=u_, b=b: e.activation(out=u_, in_=ps[:, b, :], func=AF.Square), reads=[PSK[b]], writes=[uk])
                    S.op("dve", lambda e, u_=u_: e.tensor_scalar(out=u_, in0=u_, scalar1=0.044715, scalar2=1.0, op0=ALU.mult, op1=ALU.add), reads=[uk], writes=[uk])
                    S.op("dve", lambda e, u_=u_, b=b: e.tensor_tensor(out=u_, in0=u_, in1=ps[:, b, :], op=ALU.mult), reads=[uk, PSK[b]], writes=[uk])
                    S.op("act", lambda e, u_=u_, s2=s2: e.activation(out=s2, in_=u_, func=AF.Sigmoid, scale=1.5957691216057308), reads=[uk], writes=[sk])
                    S.op("dve", lambda e, s2=s2, b=b, gq=gq: e.tensor_tensor(out=YC[:, gq * 4:(gq + 1) * 4, :].rearrange("q g k -> q (g k)"), in0=s2, in1=ps[:, b, :], op=ALU.mult),
                         reads=[sk, PSK[b]], writes=["YC"])
                d2v = T["D2"][p].rearrange("(g c) (t k) -> t c g k", c=16, t=8)
                for t in range(8):
                    S.dma("sp", lambda e, t=t: e.dma_start(out=d2v[t], in_=YC[16 * t:16 * (t + 1)]), reads=["YC"], writes=["D2"], key=("D2w", t))
                S.dma("sp", lambda e: e.dma_start(out=xnT, in_=T["D2"][p].rearrange("(fc q) n -> q fc n", q=128)), reads=["D2"], writes=XK, key="D2r")
                S.barrier()
                AR.release(m)
                m = AR.mark()
                wg = AR.alloc([128, 8, 2 * D], BF16)
                sg2 = [AR.alloc([128, 512], F32) for _ in range(2)]
                mx = [AR.alloc([128, 512], F32) for _ in range(2)]
                for h2 in range(2):
                    S.dma("pool", lambda e, h2=h2: e.dma_start(out=wg[:, :, h2 * D:(h2 + 1) * D],
                                                               in_=T["l1_w_glu"][:, h2 * D:(h2 + 1) * D].rearrange("(kc q) n -> q kc n", q=128)),
                          writes=[("wg", h2)])
                it = 0
                for oc in range(8):
                    for hf in range(2):
                        bA, bB = (it % 2) * 2, (it % 2) * 2 + 1
                        s_, x_ = sg2[it % 2], mx[it % 2]
                        sk, xk_ = ("sg2", it % 2), ("mx", it % 2)
                        it += 1
                        for kc in range(8):
                            mm(ps[:, bA, :], wg[:, kc, oc * 128:(oc + 1) * 128], xnT[:, kc, hf * 512:(hf + 1) * 512], kc == 0, kc == 7, [("wg", 0), XK[kc]], PSK[bA])
                        for kc in range(8):
                            mm(ps[:, bB, :], wg[:, kc, D + oc * 128:D + (oc + 1) * 128], xnT[:, kc, hf * 512:(hf + 1) * 512], kc == 0, kc == 7, [("wg", 1), XK[kc]], PSK[bB])
                        S.op("act", lambda e, s_=s_, bB=bB: e.activation(out=s_, in_=ps[:, bB, :], func=AF.Sigmoid), reads=[PSK[bB]], writes=[sk])
                        S.op("dve", lambda e, s_=s_, x_=x_, bA=bA: e.tensor_tensor(out=x_, in0=ps[:, bA, :], in1=s_, op=ALU.mult), reads=[PSK[bA], sk], writes=[xk_])
                        hv = hT[:, oc, :].rearrange("q (k t) -> q t k", t=8)[:, 4 * hf:4 * hf + 4, :]
                        S.op("dve", lambda e, x_=x_, hv=hv, oc=oc: e.scalar_tensor_tensor(out=hv, in0=x_.rearrange("q (t k) -> q t k", t=4), scalar=modT[:, 16 + oc:17 + oc],
                                                                                        in1=hv, op0=ALU.mult, op1=ALU.add),
                             reads=[xk_, "modT", HK[oc]], writes=[HK[oc]])
                AR.release(m)

            for l in range(NLAYERS):
                if STAGE & 1:
                    compute_mod(l)
                S.barrier()
                if STAGE & 2:
                    if LAYERS[l]["kind"] == "B":
                        if not SKIP_S5:
                            s5_layer(l)
                    else:
                        attention(l)
                S.barrier()
                if STAGE & 4:
                    ffn(l)
                S.barrier()

            m0 = AR.mark()
            yst = [AR.alloc([128, D], F32) for _ in range(2)]
            for tt in range(8):
                y_ = yst[tt % 2]
                yk = ("yst", tt % 2)
                for hf in range(2):
                    b = (tt * 2 + hf) % 8
                    for q in range(4):
                        kc = hf * 4 + q
                        S.op("pe", lambda e, b=b, q=q, kc=kc, tt=tt: e.transpose(ps[:, b, q * 128:(q + 1) * 128], hT[:, kc, tt * 128:(tt + 1) * 128], ident_f),
                             reads=[HK[kc], "ident_f"], writes=[PSK[b]], acc=True)
                    if hf:
                        S.op("act", lambda e, b=b, y_=y_: e.activation(out=y_[:, 512:1024], in_=ps[:, b, :], func=AF.Copy), reads=[PSK[b]], writes=[(yk, 1)])
                    else:
                        S.op("dve", lambda e, b=b, y_=y_: e.tensor_copy(out=y_[:, 0:512], in_=ps[:, b, :]), reads=[PSK[b]], writes=[(yk, 0)])
                S.dma("sp", lambda e, y_=y_, tt=tt: e.dma_start(out=T["y"][p, tt * 128:(tt + 1) * 128, :], in_=y_), reads=[(yk, 0), (yk, 1)], key=yk)
            S.barrier()
            AR.release(m0)

        if not SKIP_S5 and NLAYERS > 1 and (STAGE & 2):
            s5_tables()
        run_path(0)
        run_path(1)
        S.barrier()
        print('n_sems', len(S.sem_objs), 'insts', {e: len(S.prog[e]) for e in ENGS})
        S.emit()
    return nc


def _consts():
    cst = np.zeros((9, 128, 128), np.float32)
    for tt_ in range(8):
        for cc_ in range(16):
            cst[CST_R16, cc_, tt_ * 16 + cc_] = 1.0
    cst[CST_IDENT] = np.eye(128)
    bd = np.zeros((128, 128), np.float32)
    bd[:64, :64] = 1
    bd[64:, 64:] = 1
    cst[CST_BD64] = bd
    kk = np.arange(128)[:, None]
    qq = np.arange(128)[None, :]
    cst[CST_MLO] = (qq <= kk)
    cst[CST_MHI] = (kk <= qq)

    def perm(dh):
        qt = dh // 4
        half = dh // 2
        Pm = np.zeros((128, 128), np.float32)
        for m in range(128):
            d = m % dh
            if (d % half) < qt:
                Pm[m, m + qt] = -1.0
            else:
                Pm[m, m - qt] = 1.0
        return Pm.T.copy()

    cst[CST_PA] = perm(64)
    cst[CST_PC] = perm(128)
    t1 = np.arange(8)
    tau_p = np.repeat(t1, 16)[:, None]
    tau = np.repeat(t1, 16)[None, :]
    cst[CST_S5F] = (tau >= tau_p)
    cst[CST_S5B] = (tau <= tau_p)
    rope = np.zeros((4, 128, NT), np.float32)
    t = np.arange(NT)
    row = (t // 64).astype(np.float32)
    col = (t % 64).astype(np.float32)
    for idx, dh in ((0, 64), (2, 128)):
        half, qt = dh // 2, dh // 4
        freqs = (1.0 / (np.float32(10000.0) ** (np.arange(qt, dtype=np.float32) / np.float32(qt)))).astype(np.float32)
        for pp in range(128):
            d = pp % dh
            pos = row if d < half else col
            ang = (pos * freqs[d % qt]).astype(np.float32)
            rope[idx, pp] = np.cos(ang)
            rope[idx + 1, pp] = np.sin(ang)
    return cst, rope


_PROG = {}


def kernel(**inp):
    f32 = lambda a: np.ascontiguousarray(np.asarray(a, dtype=np.float32))
    inp = {k: f32(v) for k, v in inp.items()}
    if "nc" not in _PROG:
        _PROG["nc"] = build_program()
    nc = _PROG["nc"]
    cst, rope = _consts()
    shared = {}
    for name, shape in IN_SPECS:
        if name in inp:
            shared[name] = inp[name].reshape(shape)
    shared["cst"] = cst
    shared["rope"] = rope
    in_maps = []
    for c in range(8):
        m = dict(shared)
        m["xin"] = np.stack([inp["x_prompt"][4 * c:4 * c + 4].reshape(NT, D), inp["x_sample"][c]])
        m["cond"] = np.stack([inp["c_ctx"], inp["c"][c]])
        m["ck0"] = inp["cache_l0_k"][c].reshape(512, 256)
        m["cv0"] = inp["cache_l0_v"][c].reshape(512, 256)
        m["st1"] = inp["state_l1"][c]
        m["ck2"] = inp["cache_l2_k"][c].reshape(512, 512)
        m["cv2"] = inp["cache_l2_v"][c].reshape(512, 512)
        m["ck3"] = inp["cache_l3_k"][c].reshape(512, 256)
        m["cv3"] = inp["cache_l3_v"][c].reshape(512, 256)
        in_maps.append({k: np.ascontiguousarray(v) for k, v in m.items()})
    res = run_bass_kernel_spmd(nc, in_maps, core_ids=list(range(8)))
    R = res.results
    y_prompt = np.concatenate([R[c]["y"][0].reshape(4, 256, D) for c in range(8)], 0)
    y_sample = np.stack([R[c]["y"][1] for c in range(8)], 0)
    cat = lambda k, shp: np.concatenate([R[c][k].reshape(shp) for c in range(8)], 0)
    out = (y_prompt, y_sample,
           cat("nk0", (4, 256, 4, 64)), cat("nv0", (4, 256, 4, 64)),
           cat("nst", (4, 2, 2, 64, 64)),
           cat("nk2", (4, 256, 4, 128)), cat("nv2", (4, 256, 4, 128)),
           cat("nk3", (4, 256, 4, 64)), cat("nv3", (4, 256, 4, 64)))
    return tuple(np.ascontiguousarray(o, dtype=np.float32) for o in out)
```

```python
import math
import numpy as np
from contextlib import ExitStack
import concourse.bass as bass
import concourse.mybir as mybir
from concourse.bass_utils import run_bass_kernel_spmd

F32 = mybir.dt.float32
BF16 = mybir.dt.bfloat16
I32 = mybir.dt.int32
U8 = mybir.dt.uint8
AF = mybir.ActivationFunctionType
ALU = mybir.AluOpType
AX = mybir.AxisListType

D = 1024
DFF = 2816
NT = 1024
ENGS = ["pe", "act", "dve", "pool", "sp"]
EPS = 1e-6
NLAYERS = 4
SKIP_S5 = False
DEBUG_H = False
STAGE = 7
ATT_STOP = 9


class Sched:
    def __init__(self, nc, stack):
        self.nc = nc
        self.stack = stack
        self.prog = {e: [] for e in ENGS}
        self.cnt = {e: 0 for e in ENGS}
        self.seen = {e: {} for e in ENGS}
        self.lastw = {}
        self.readers = {}
        self.sem_objs = []
        self.esid = {}
        for e in ENGS:
            self.esid[e] = self._new_sem("s_" + e)
        self.dsem = {}
        self.dfree = []

    def _new_sem(self, name):
        self.sem_objs.append(self.stack.enter_context(self.nc.semaphore(name)))
        return len(self.sem_objs) - 1

    def _dma_sem(self, key):
        if key not in self.dsem:
            if self.dfree:
                self.dsem[key] = self.dfree.pop()
            else:
                self.dsem[key] = [self._new_sem("d%d" % len(self.sem_objs)), 0]
        return self.dsem[key]

    def _deps(self, eng, reads, writes, acc=False):
        need = {}

        def add(rec, same_ok=False):
            sid, val, e = rec
            if same_ok and e == eng:
                return
            if need.get(sid, 0) < val:
                need[sid] = val

        for k in reads:
            if k in self.lastw:
                add(self.lastw[k])
        for k in writes:
            if k in self.lastw:
                add(self.lastw[k], same_ok=(acc and eng == "pe"))
            for r in self.readers.get(k, ()):
                add(r, same_ok=True)
        waits = []
        seen = self.seen[eng]
        for sid, val in need.items():
            if seen.get(sid, 0) >= val:
                continue
            seen[sid] = val
            waits.append((sid, val))
        return waits

    def op(self, eng, fn, reads=(), writes=(), acc=False):
        waits = self._deps(eng, reads, writes, acc)
        self.cnt[eng] += 1
        rec = (self.esid[eng], self.cnt[eng], eng)
        self.prog[eng].append((waits, fn, self.esid[eng], 1))
        for k in reads:
            self.readers.setdefault(k, []).append(rec)
        for k in writes:
            self.lastw[k] = rec
            self.readers[k] = []

    def dma(self, q, fn, reads=(), writes=(), key=None):
        if key is None:
            key = writes[0] if writes else reads[0]
        waits = self._deps(q, reads, writes)
        ds = self._dma_sem(key)
        ds[1] += 16
        rec = (ds[0], ds[1], "dma")
        self.prog[q].append((waits, fn, ds[0], 16))
        for k in reads:
            self.readers.setdefault(k, []).append(rec)
        for k in writes:
            self.lastw[k] = rec
            self.readers[k] = []

    def barrier(self):
        allw = [(self.esid[e], self.cnt[e], e) for e in ENGS if self.cnt[e] > 0]
        allw += [(sid, c, None) for k, (sid, c) in self.dsem.items() if c > 0]
        for e in ENGS:
            waits = []
            for sid, val, owner in allw:
                if owner == e:
                    continue
                if self.seen[e].get(sid, 0) >= val:
                    continue
                self.seen[e][sid] = val
                waits.append((sid, val))
            if waits:
                self.prog[e].append((waits, None, None, 0))
        self.dfree.extend(self.dsem.values())
        self.dsem = {}
        self.lastw = {}
        self.readers = {}

    def emit(self):
        nc, sems, prog = self.nc, self.sem_objs, self.prog

        def run(ename):
            def body(eng):
                for waits, fn, incsid, incv in prog[ename]:
                    for sid, val in waits:
                        eng.wait_ge(sems[sid], val)
                    if fn is not None:
                        fn(eng).then_inc(sems[incsid], incv)
            return body

        with nc.Block() as block:
            block.tensor(run("pe"))
            block.scalar(run("act"))
            block.vector(run("dve"))
            block.gpsimd(run("pool"))
            block.sync(run("sp"))


class Arena:
    def __init__(self, tensor, size):
        self.t = tensor
        self.size = size
        self.off = 0

    def alloc(self, shape, dt):
        nb = {F32: 4, BF16: 2, I32: 4}[dt]
        n = int(np.prod(shape[1:]))
        off = (self.off + 63) // 64 * 64
        assert off + n * nb <= self.size, ("SBUF arena overflow", off, n * nb, self.size)
        self.off = off + n * nb
        ap = self.t[:, off:off + n * nb].bitcast(dt)
        if len(shape) == 3:
            ap = ap.rearrange("p (a b) -> p a b", a=shape[1])
        elif len(shape) == 4:
            ap = ap.rearrange("p (a b c) -> p a b c", a=shape[1], b=shape[2])
        if shape[0] != 128:
            ap = ap[0:shape[0]]
        return ap

    def mark(self):
        return self.off

    def release(self, m):
        self.off = m


LAYERS = [
    dict(kind="A", dh=64, H=16, ckv=256, wqkv="l0_w_qkv", qn="l0_q_norm", kn="l0_k_norm", sink="l0_sink", wo="l0_w_o",
         ck="ck0", cv="cv0", nk="nk0", nv="nv0"),
    dict(kind="B"),
    dict(kind="C", dh=128, H=8, ckv=512, wqkv="l2_w_qkv", qn="l2_q_norm", kn="l2_k_norm", sink=None, wo="l2_w_o",
         ck="ck2", cv="cv2", nk="nk2", nv="nv2"),
    dict(kind="A", dh=64, H=16, ckv=256, wqkv="l3_w_qkv", qn="l3_q_norm", kn="l3_k_norm", sink="l3_sink", wo="l3_w_o",
         ck="ck3", cv="cv3", nk="nk3", nv="nv3"),
]

IN_SPECS = [
    ("xin", [2, NT, D]), ("cond", [2, D]),
    ("ck0", [512, 256]), ("cv0", [512, 256]), ("st1", [2, 2, 64, 64]),
    ("ck2", [512, 512]), ("cv2", [512, 512]), ("ck3", [512, 256]), ("cv3", [512, 256]),
    ("norm1", [4, D]), ("norm2", [4, D]), ("w_mod", [4, D, 6 * D]), ("b_mod", [4, 6 * D]),
    ("w_up", [4, D, 2 * DFF]), ("b_up", [4, 2 * DFF]), ("conv_k", [4, 3, 2 * DFF]), ("conv_b", [4, 2 * DFF]),
    ("w_down", [4, DFF, D]),
    ("l0_w_qkv", [D, 1536]), ("l0_q_norm", [64]), ("l0_k_norm", [64]), ("l0_sink", [16]), ("l0_w_o", [D, D]),
    ("l1_lam_re", [2, 64, 64]), ("l1_lam_im", [2, 64, 64]), ("l1_log_dt", [2, 64]),
    ("l1_b_re", [2, 64, 64, 16]), ("l1_b_im", [2, 64, 64, 16]), ("l1_c_re", [2, 64, 16, 64]), ("l1_c_im", [2, 64, 16, 64]),
    ("l1_d_skip", [D]), ("l1_w_glu", [D, 2 * D]),
    ("l2_w_qkv", [D, 2048]), ("l2_q_norm", [128]), ("l2_k_norm", [128]), ("l2_w_o", [D, D]),
    ("l3_w_qkv", [D, 1536]), ("l3_q_norm", [64]), ("l3_k_norm", [64]), ("l3_sink", [16]), ("l3_w_o", [D, D]),
    ("cst", [9, 128, 128]), ("rope", [4, 128, NT]),
]
OUT_SPECS = [
    ("y", [2, NT, D]), ("nk0", [NT, 256]), ("nv0", [NT, 256]), ("nst", [4, 2, 2, 64, 64]),
    ("nk2", [NT, 512]), ("nv2", [NT, 512]), ("nk3", [NT, 256]), ("nv3", [NT, 256]),
]
CST_IDENT, CST_BD64, CST_MLO, CST_MHI, CST_PA, CST_PC, CST_S5F, CST_S5B, CST_R16 = range(9)


def build_program():
    nc = bass.Bass("TRN2", target_bir_lowering=False)
    T = {}
    for name, shape in IN_SPECS:
        T[name] = nc.dram_tensor(name, shape, F32, kind="ExternalInput").ap()
    for name, shape in OUT_SPECS:
        T[name] = nc.dram_tensor(name, shape, F32, kind="ExternalOutput").ap()
    for name, shape, dt_ in [("TX", [128, 16384], BF16), ("TZ", [128, 16384], BF16), ("TM", [64, 128, 128], BF16),
                             ("TE", [2, 64, 128, 128], BF16), ("TS", [2, 64, 128, 128], BF16), ("TP8", [2, 128, 64], F32),
                             ("D1", [2, 1024, 1024], BF16), ("D2", [2, 1024, 1024], BF16)]:
        T[name] = nc.dram_tensor(name, shape, dt_, kind="Internal").ap()
    if DEBUG_H:
        T["dbg"] = nc.dram_tensor("dbg", [2, NT, D], F32, kind="ExternalOutput").ap()

    with ExitStack() as st:
        S = Sched(nc, st)
        ARENA_BYTES = 190 * 1024
        arena_t = st.enter_context(nc.sbuf_tensor("arena", [128, ARENA_BYTES], U8))
        AR = Arena(arena_t, ARENA_BYTES)
        ps = st.enter_context(nc.psum_tensor("ps", [128, 8, 512], F32))
        PSK = [("ps", b) for b in range(8)]
        uid = [0]

        def fresh(prefix):
            uid[0] += 1
            return (prefix, uid[0])

        ident_f = AR.alloc([128, 128], F32)
        ident_b = AR.alloc([128, 128], BF16)
        ones_b = AR.alloc([128, 128], BF16)
        ones_f = AR.alloc([128, 128], F32)
        bd64_b = AR.alloc([128, 128], BF16)
        mlo_b = AR.alloc([128, 4, 128], BF16)
        mhi_b = AR.alloc([128, 4, 128], BF16)
        pa_b = AR.alloc([128, 128], BF16)
        pc_b = AR.alloc([128, 128], BF16)
        cstage = AR.alloc([128, 128], F32)
        S.dma("sp", lambda e: e.dma_start(out=ident_f, in_=T["cst"][CST_IDENT]), writes=["ident_f"])
        S.op("dve", lambda e: e.tensor_copy(out=ident_b, in_=ident_f), reads=["ident_f"], writes=["ident_b"])
        S.op("dve", lambda e: e.memset(ones_b, 1.0), writes=["ones_b"])
        S.op("dve", lambda e: e.memset(ones_f, 1.0), writes=["ones_f"])
        for ci, dst, rep in [(CST_BD64, bd64_b, 0), (CST_MLO, mlo_b, 4), (CST_MHI, mhi_b, 4), (CST_PA, pa_b, 0), (CST_PC, pc_b, 0)]:
            S.dma("sp", lambda e, ci=ci: e.dma_start(out=cstage, in_=T["cst"][ci]), writes=["cstage"])
            if rep:
                for r in range(rep):
                    S.op("dve", lambda e, dst=dst, r=r: e.tensor_copy(out=dst[:, r, :], in_=cstage), reads=["cstage"], writes=["cst_misc"])
            else:
                S.op("dve", lambda e, dst=dst: e.tensor_copy(out=dst, in_=cstage), reads=["cstage"], writes=["cst_misc"])
        S.barrier()
        base_mark = AR.mark()

        def mm(out, lhsT, rhs, start, stop, reads, wkey):
            S.op("pe", lambda e: e.matmul(out, lhsT, rhs, start=start, stop=stop), reads=reads, writes=[wkey], acc=True)

        tstage = AR.alloc([128, 128], F32)
        modAll = AR.alloc([128, 4, 2, 48], F32)
        tst2 = AR.alloc([128, 128], F32)
        S.op("dve", lambda e: e.memset(tst2, 0.0), writes=["tst2z"])
        S.barrier()
        base_mark2 = [None]

        def small_load_T(dst, src_ap, pattern, key, **kw):
            n = dst.shape[1]
            rows = src_ap.rearrange("(c q) -> c q", q=128)
            S.dma("sp", lambda e: e.dma_start(out=tstage[0:n, :], in_=rows), writes=["tstage"])
            S.op("pe", lambda e: e.transpose(ps[:, 7, 0:n], tstage[0:n, :], ident_f[0:n, 0:n]), reads=["tstage", "ident_f"], writes=[PSK[7]])
            S.op("dve", lambda e: e.tensor_copy(out=dst, in_=ps[:, 7, 0:n]), reads=[PSK[7]], writes=[key])

        def small_load_col(dst, src_ap, dh, key):
            rep = 128 // dh
            for r in range(rep):
                S.dma("sp", lambda e, r=r: e.dma_start(out=tst2[0:1, r * dh:(r + 1) * dh], in_=src_ap.rearrange("(o q) -> o q", o=1)),
                      writes=[("tst2", r)], key=("tst2", r))
            S.op("pe", lambda e: e.transpose(ps[:, 7, 0:2], tst2[0:2, :], ident_f[0:2, 0:2]),
                 reads=[("tst2", r) for r in range(rep)] + ["tst2z", "ident_f"], writes=[PSK[7], ("tst2", 0), ("tst2", 1)])
            S.op("dve", lambda e: e.tensor_copy(out=dst, in_=ps[:, 7, 0:1]), reads=[PSK[7]], writes=[key])

        base_mark = AR.mark()

        def s5_tables():
            m = AR.mark()
            TWO_PI = 2.0 * math.pi
            lr = AR.alloc([128, 64], F32)
            li = AR.alloc([128, 64], F32)
            dtc = AR.alloc([128, 1], F32)
            xx = AR.alloc([128, 64], F32)
            th = AR.alloc([128, 64], F32)
            PWr = AR.alloc([128, 16, 64], F32)
            PWi = AR.alloc([128, 16, 64], F32)
            Br = AR.alloc([128, 64, 16], F32)
            Bi = AR.alloc([128, 64, 16], F32)
            Bbr = AR.alloc([128, 64, 16], F32)
            Bbi = AR.alloc([128, 64, 16], F32)
            Cr = AR.alloc([128, 16, 64], F32)
            Ci = AR.alloc([128, 16, 64], F32)
            w = [AR.alloc([128, 1024], F32) for _ in range(4)]
            s64 = [AR.alloc([128, 64], F32) for _ in range(8)]
            ki = AR.alloc([128, 64], I32)
            PMr = AR.alloc([128, 64], F32)
            PMi = AR.alloc([128, 64], F32)
            ST = [AR.alloc([128, 16384], BF16) for _ in range(2)]

            def dv(fn, reads, writes):
                S.op("dve", fn, reads=reads, writes=writes)

            S.dma("sp", lambda e: e.dma_start(out=lr, in_=T["l1_lam_re"].rearrange("d g p -> (d g) p")), writes=["lr"])
            S.dma("sp", lambda e: e.dma_start(out=li, in_=T["l1_lam_im"].rearrange("d g p -> (d g) p")), writes=["li"])
            S.dma("sp", lambda e: e.dma_start(out=dtc, in_=T["l1_log_dt"].rearrange("d (g o) -> (d g) o", o=1)), writes=["dtc"])
            S.dma("sp", lambda e: e.dma_start(out=Br, in_=T["l1_b_re"].rearrange("d g p c -> (d g) p c")), writes=["Br"])
            S.dma("sp", lambda e: e.dma_start(out=Bi, in_=T["l1_b_im"].rearrange("d g p c -> (d g) p c")), writes=["Bi"])
            S.dma("sp", lambda e: e.dma_start(out=Cr, in_=T["l1_c_re"].rearrange("d g c p -> (d g) c p")), writes=["Cr"])
            S.dma("sp", lambda e: e.dma_start(out=Ci, in_=T["l1_c_im"].rearrange("d g c p -> (d g) c p")), writes=["Ci"])
            S.op("act", lambda e: e.activation(out=dtc, in_=dtc, func=AF.Exp), reads=["dtc"], writes=["dtc"])
            dv(lambda e: e.tensor_scalar(out=xx, in0=lr, scalar1=dtc[:, 0:1], scalar2=None, op0=ALU.mult), ["lr", "dtc"], ["xx"])
            dv(lambda e: e.tensor_scalar(out=th, in0=li, scalar1=dtc[:, 0:1], scalar2=None, op0=ALU.mult), ["li", "dtc"], ["th"])

            def sinred(out, okey, shift):
                v, kf, r, mk = s64[0], s64[1], s64[2], s64[3]
                dv(lambda e: e.tensor_scalar(out=v, in0=th, scalar1=float(shift), scalar2=None, op0=ALU.add), ["th"], ["sr_v"])
                dv(lambda e: e.tensor_scalar(out=kf, in0=v, scalar1=1.0 / TWO_PI, scalar2=None, op0=ALU.mult), ["sr_v"], ["sr_kf"])
                dv(lambda e: e.tensor_copy(out=ki, in_=kf), ["sr_kf"], ["sr_ki"])
                dv(lambda e: e.tensor_copy(out=kf, in_=ki), ["sr_ki"], ["sr_kf"])
                dv(lambda e: e.scalar_tensor_tensor(out=r, in0=kf, scalar=-TWO_PI, in1=v, op0=ALU.mult, op1=ALU.add), ["sr_kf", "sr_v"], ["sr_r"])
                dv(lambda e: e.tensor_scalar(out=mk, in0=r, scalar1=math.pi, scalar2=None, op0=ALU.is_gt), ["sr_r"], ["sr_m"])
                dv(lambda e: e.scalar_tensor_tensor(out=r, in0=mk, scalar=-TWO_PI, in1=r, op0=ALU.mult, op1=ALU.add), ["sr_m", "sr_r"], ["sr_r"])
                dv(lambda e: e.tensor_scalar(out=mk, in0=r, scalar1=-math.pi, scalar2=None, op0=ALU.is_lt), ["sr_r"], ["sr_m"])
                dv(lambda e: e.scalar_tensor_tensor(out=r, in0=mk, scalar=TWO_PI, in1=r, op0=ALU.mult, op1=ALU.add), ["sr_m", "sr_r"], ["sr_r"])
                dv(lambda e: e.tensor_scalar(out=r, in0=r, scalar1=3.1415925, scalar2=-3.1415925, op0=ALU.min, op1=ALU.max), ["sr_r"], ["sr_r"])
                S.op("act", lambda e: e.activation(out=out, in_=r, func=AF.Sin), reads=["sr_r"], writes=[okey])

            sn, cs, ex, exm = s64[4], s64[5], s64[6], s64[7]
            sinred(sn, "sn", 0.0)
            sinred(cs, "cs", math.pi / 2.0)
            S.op("act", lambda e: e.activation(out=ex, in_=xx, func=AF.Exp), reads=["xx"], writes=["ex"])
            S.op("act", lambda e: e.activation(out=exm, in_=xx, func=AF.Exp, scale=-1.0), reads=["xx"], writes=["exm"])
            PK = lambda j: ("PW", j)
            dv(lambda e: e.memset(PWr[:, 7, :], 1.0), [], [("PWr", 0)])
            dv(lambda e: e.memset(PWi[:, 7, :], 0.0), [], [("PWi", 0)])
            dv(lambda e: e.tensor_tensor(out=PWr[:, 8, :], in0=ex, in1=cs, op=ALU.mult), ["ex", "cs"], [("PWr", 1)])
            dv(lambda e: e.tensor_tensor(out=PWi[:, 8, :], in0=ex, in1=sn, op=ALU.mult), ["ex", "sn"], [("PWi", 1)])
            dv(lambda e: e.tensor_tensor(out=PWr[:, 6, :], in0=exm, in1=cs, op=ALU.mult), ["exm", "cs"], [("PWr", -1)])
            dv(lambda e: e.scalar_tensor_tensor(out=PWi[:, 6, :], in0=exm, scalar=-1.0, in1=sn, op0=ALU.mult, op1=ALU.mult), ["exm", "sn"], [("PWi", -1)])

            def cmul64(j_out, j_a, j_b):
                orr, oi = PWr[:, j_out + 7, :], PWi[:, j_out + 7, :]
                ar, ai = PWr[:, j_a + 7, :], PWi[:, j_a + 7, :]
                br, bi = PWr[:, j_b + 7, :], PWi[:, j_b + 7, :]
                ka = [("PWr", j_a), ("PWi", j_a), ("PWr", j_b), ("PWi", j_b)]
                t1, t2 = s64[0], s64[1]
                dv(lambda e: e.tensor_tensor(out=t1, in0=ar, in1=br, op=ALU.mult), ka, ["c64a"])
                dv(lambda e: e.tensor_tensor(out=t2, in0=ai, in1=bi, op=ALU.mult), ka, ["c64b"])
                dv(lambda e: e.tensor_tensor(out=orr, in0=t1, in1=t2, op=ALU.subtract), ["c64a", "c64b"], [("PWr", j_out)])
                dv(lambda e: e.tensor_tensor(out=t1, in0=ar, in1=bi, op=ALU.mult), ka, ["c64a"])
                dv(lambda e: e.tensor_tensor(out=t2, in0=ai, in1=br, op=ALU.mult), ka, ["c64b"])
                dv(lambda e: e.tensor_tensor(out=oi, in0=t1, in1=t2, op=ALU.add), ["c64a", "c64b"], [("PWi", j_out)])

            for j in range(2, 9):
                cmul64(j, j - 1, 1)
            for j in range(2, 8):
                cmul64(-j, -(j - 1), -1)
            S.dma("sp", lambda e: e.dma_start(out=T["TP8"][0], in_=PWr[:, 15, :]), reads=[("PWr", 8)], key="tp8a")
            S.dma("sp", lambda e: e.dma_start(out=T["TP8"][1], in_=PWi[:, 15, :]), reads=[("PWi", 8)], key="tp8b")
            nr, den, fr, fi = s64[0], s64[1], s64[2], s64[3]
            t5, t6 = s64[4], s64[5]
            a1 = [("PWr", 1), ("PWi", 1)]
            dv(lambda e: e.tensor_scalar(out=nr, in0=PWr[:, 8, :], scalar1=-1.0, scalar2=None, op0=ALU.add), a1, ["nr"])
            dv(lambda e: e.tensor_tensor(out=den, in0=lr, in1=lr, op=ALU.mult), ["lr"], ["den"])
            dv(lambda e: e.tensor_tensor(out=t5, in0=li, in1=li, op=ALU.mult), ["li"], ["t5"])
            dv(lambda e: e.tensor_tensor(out=den, in0=den, in1=t5, op=ALU.add), ["den", "t5"], ["den"])
            dv(lambda e: e.reciprocal(out=den, in_=den), ["den"], ["den"])
            dv(lambda e: e.tensor_tensor(out=fr, in0=nr, in1=lr, op=ALU.mult), ["nr", "lr"], ["fr"])
            dv(lambda e: e.tensor_tensor(out=t5, in0=PWi[:, 8, :], in1=li, op=ALU.mult), a1 + ["li"], ["t5"])
            dv(lambda e: e.tensor_tensor(out=fr, in0=fr, in1=t5, op=ALU.add), ["fr", "t5"], ["fr"])
            dv(lambda e: e.tensor_tensor(out=fr, in0=fr, in1=den, op=ALU.mult), ["fr", "den"], ["fr"])
            dv(lambda e: e.tensor_tensor(out=fi, in0=PWi[:, 8, :], in1=lr, op=ALU.mult), a1 + ["lr"], ["fi"])
            dv(lambda e: e.tensor_tensor(out=t6, in0=nr, in1=li, op=ALU.mult), ["nr", "li"], ["t6"])
            dv(lambda e: e.tensor_tensor(out=fi, in0=fi, in1=t6, op=ALU.subtract), ["fi", "t6"], ["fi"])
            dv(lambda e: e.tensor_tensor(out=fi, in0=fi, in1=den, op=ALU.mult), ["fi", "den"], ["fi"])

            def bc_b(a):
                return a.unsqueeze(2).broadcast_to([128, 64, 16])

            def bc_c(a):
                return a.unsqueeze(1).broadcast_to([128, 16, 64])

            w3b = [x.rearrange("q (p c) -> q p c", c=16) for x in w]
            w3c = [x.rearrange("q (c p) -> q c p", p=64) for x in w]

            def cmul_big(eng, pr, pi, pk, xr, xi, xk, bc, wv, out_r, out_i, okey, xr_n=None, xi_n=None):
                t1, t2, t3, t4 = wv
                tg = "cb" + eng
                op = lambda fn, r, w_: S.op(eng, fn, reads=r, writes=w_)
                op(lambda e: e.tensor_tensor(out=t1, in0=xr, in1=bc(pr), op=ALU.mult), pk + xk, [tg + "1"])
                op(lambda e: e.tensor_tensor(out=t2, in0=xi, in1=bc(pi), op=ALU.mult), pk + xk, [tg + "2"])
                op(lambda e: e.tensor_tensor(out=out_r, in0=t1, in1=t2, op=ALU.subtract), [tg + "1", tg + "2"], [okey])
                a_i = xi if xi_n is None else xi_n
                a_r = xr if xr_n is None else xr_n
                op(lambda e: e.tensor_tensor(out=t3, in0=a_i, in1=bc(pr), op=ALU.mult), pk + xk, [tg + "3"])
                op(lambda e: e.tensor_tensor(out=t4, in0=a_r, in1=bc(pi), op=ALU.mult), pk + xk, [tg + "4"])
                op(lambda e: e.tensor_tensor(out=out_i, in0=t3, in1=t4, op=ALU.add), [tg + "3", tg + "4"], [okey])

            cmul_big("dve", fr, fi, ["fr", "fi"], Br, Bi, ["Br", "Bi"], bc_b, w3b, Bbr, Bbi, "Bb")
            nCr = Br.rearrange("q p c -> q (p c)").rearrange("q (c p) -> q c p", p=64)
            nCi = Bi.rearrange("q p c -> q (p c)").rearrange("q (c p) -> q c p", p=64)
            dv(lambda e: e.tensor_scalar(out=nCr, in0=Cr, scalar1=-1.0, scalar2=None, op0=ALU.mult), ["Cr", "Bb"], ["nC"])
            dv(lambda e: e.tensor_scalar(out=nCi, in0=Ci, scalar1=-1.0, scalar2=None, op0=ALU.mult), ["Ci", "Bb"], ["nC"])
            PM = {"dve": (PMr, PMi), "pool": (AR.alloc([128, 64], F32), AR.alloc([128, 64], F32))}
            wp = [AR.alloc([128, 1024], F32) for _ in range(4)]
            WV = {"dve": (w3b, w3c),
                  "pool": ([x.rearrange("q (p c) -> q p c", c=16) for x in wp], [x.rearrange("q (c p) -> q c p", p=64) for x in wp])}

            def mixed_power(eng, jf, jb):
                pmr, pmi = PM[eng]
                k = "PM" + eng
                S.op(eng, lambda e: e.tensor_copy(out=pmr[0:64], in_=PWr[0:64, jf + 7, :]), reads=[("PWr", jf)], writes=[k])
                S.op(eng, lambda e: e.tensor_copy(out=pmi[0:64], in_=PWi[0:64, jf + 7, :]), reads=[("PWi", jf)], writes=[k])
                S.op(eng, lambda e: e.tensor_copy(out=pmr[64:128], in_=PWr[64:128, jb + 7, :]), reads=[("PWr", jb)], writes=[k])
                S.op(eng, lambda e: e.tensor_copy(out=pmi[64:128], in_=PWi[64:128, jb + 7, :]), reads=[("PWi", jb)], writes=[k])

            specs = [("B", lambda t: -t, lambda t: t, "TX", "ript", "dve"),
                     ("C", lambda t: t, lambda t: -t, "TZ", "ript", "dve"),
                     ("B", lambda t: 7 - t, lambda t: t, "TS", "ritp", "dve"),
                     ("C", lambda t: t + 1, lambda t: 8 - t, "TE", "ript", "dve")]
            for ti, (side, pf, pb_, dst, layout, eng) in enumerate(specs):
                st_ = ST[ti % 2]
                stk = ("ST", ti % 2)
                pmr, pmi = PM[eng]
                pmk = ["PM" + eng]
                if layout == "ript":
                    st5 = st_.rearrange("q (r p t c) -> q r p t c", r=2, p=64, t=8)
                else:
                    st5 = st_.rearrange("q (r t c p) -> q r t c p", r=2, t=8, c=16)
                for t in range(8):
                    mixed_power(eng, pf(t), pb_(t))
                    if side == "B":
                        if layout == "ript":
                            o_r, o_i = st5[:, 0, :, t, :], st5[:, 1, :, t, :]
                        else:
                            o_r = st5[:, 0, t, :, :].rearrange("q c p -> q p c")
                            o_i = st5[:, 1, t, :, :].rearrange("q c p -> q p c")
                        cmul_big(eng, pmr, pmi, pmk, Bbr, Bbi, ["Bb"], bc_b, WV[eng][0], o_r, o_i, stk)
                    else:
                        o_r = st5[:, 0, :, t, :].rearrange("q p c -> q c p")
                        o_i = st5[:, 1, :, t, :].rearrange("q p c -> q c p")
                        cmul_big(eng, pmr, pmi, pmk, Cr, Ci, ["Cr", "Ci", "nC"], bc_c, WV[eng][1], o_r, o_i, stk, xr_n=nCr, xi_n=nCi)
                if dst in ("TX", "TZ"):
                    S.dma("sp", lambda e, st_=st_, dst=dst: e.dma_start(out=T[dst], in_=st_), reads=[stk], writes=[dst], key=stk)
                elif dst == "TE":
                    st4 = st_.rearrange("q (r p n) -> q r p n", r=2, p=64)
                    for d in range(2):
                        for r_ in range(2):
                            S.dma("sp", lambda e, d=d, r_=r_, st4=st4: e.dma_start(
                                out=T["TE"][r_, :, d * 64:(d + 1) * 64, :], in_=st4[d * 64:(d + 1) * 64, r_]), reads=[stk], writes=["TE"], key=stk)
                else:
                    st4 = st_.rearrange("q (r n p) -> q r n p", r=2, p=64)
                    for d in range(2):
                        for r_ in range(2):
                            for g8 in range(8):
                                S.dma("sp", lambda e, d=d, r_=r_, g8=g8, st4=st4: e.dma_start(
                                    out=T["TS"][r_, g8 * 8:(g8 + 1) * 8, :, d * 64:(d + 1) * 64],
                                    in_=st4[d * 64 + g8 * 8:d * 64 + (g8 + 1) * 8, r_]), reads=[stk], writes=["TS"], key=stk)
            S.barrier()
            AR.release(m)
            m = AR.mark()
            mkf = AR.alloc([128, 128], F32)
            mkb = AR.alloc([128, 128], F32)
            r16 = AR.alloc([128, 128], F32)
            dgc = AR.alloc([128, 16], F32)
            dmat = AR.alloc([128, 64], F32)
            dcol = AR.alloc([128, 64], F32)
            S.dma("sp", lambda e: e.dma_start(out=mkf, in_=T["cst"][CST_S5F]), writes=["mkf"])
            S.dma("sp", lambda e: e.dma_start(out=mkb, in_=T["cst"][CST_S5B]), writes=["mkb"])
            S.dma("sp", lambda e: e.dma_start(out=r16, in_=T["cst"][CST_R16]), writes=["r16"])
            S.dma("sp", lambda e: e.dma_start(out=dgc[0:64, :], in_=T["l1_d_skip"].rearrange("(g c) -> g c", c=16)), writes=["dgc"])
            S.op("pe", lambda e: e.transpose(ps[0:16, 7, 0:64], dgc[0:64, :], ident_f[0:64, 0:64]), reads=["dgc", "ident_f"], writes=[PSK[7]])
            S.op("dve", lambda e: e.tensor_copy(out=dmat[0:16, :], in_=ps[0:16, 7, 0:64]), reads=[PSK[7]], writes=["dmat"])
            S.op("pe", lambda e: e.matmul(ps[:, 7, 64:128], r16[0:16, :], dmat[0:16, :], start=True, stop=True), reads=["r16", "dmat"], writes=[PSK[7]])
            S.op("dve", lambda e: e.tensor_copy(out=dcol, in_=ps[:, 7, 64:128]), reads=[PSK[7]], writes=["dcol"])
            XL = [[AR.alloc([128, 16, 128], BF16) for _ in range(2)] for _ in range(2)]
            ZL = [[AR.alloc([128, 16, 128], BF16) for _ in range(2)] for _ in range(2)]
            Mst = [AR.alloc([128, 16, 128], BF16) for _ in range(2)]
            tA = [AR.alloc([128, 128], F32) for _ in range(2)]
            tB = [AR.alloc([128, 128], F32) for _ in range(2)]
            TX4 = T["TX"].rearrange("(d g) (k n) -> d g k n", d=2, k=128)
            TZ4 = T["TZ"].rearrange("(d g) (k n) -> d g k n", d=2, k=128)
            for gb in range(4):
                bf = gb % 2
                for d in range(2):
                    S.dma("sp", lambda e, d=d, gb=gb, bf=bf: e.dma_start(out=XL[bf][d], in_=TX4[d, gb * 16:(gb + 1) * 16].rearrange("g k n -> k g n")),
                          reads=["TX"], writes=[("XL", bf, d)])
                    S.dma("sp", lambda e, d=d, gb=gb, bf=bf: e.dma_start(out=ZL[bf][d], in_=TZ4[d, gb * 16:(gb + 1) * 16].rearrange("g k n -> k g n")),
                          reads=["TZ"], writes=[("ZL", bf, d)])
                for gi in range(16):
                    g = gb * 16 + gi
                    b = gi % 4
                    for d in range(2):
                        S.op("pe", lambda e, b=b, d=d, bf=bf, gi=gi: e.matmul(ps[:, b, d * 128:(d + 1) * 128], XL[bf][d][:, gi, :], ZL[bf][d][:, gi, :],
                                                                            start=True, stop=True),
                             reads=[("XL", bf, d), ("ZL", bf, d)], writes=[PSK[b]], acc=True)
                    a_, b_ = tA[gi % 2], tB[gi % 2]
                    S.op("dve", lambda e, a_=a_, b=b: e.tensor_tensor(out=a_, in0=ps[:, b, 0:128], in1=mkf, op=ALU.mult), reads=[PSK[b], "mkf"], writes=[("tA", gi % 2)])
                    S.op("dve", lambda e, b_=b_, b=b: e.tensor_tensor(out=b_, in0=ps[:, b, 128:256], in1=mkb, op=ALU.mult), reads=[PSK[b], "mkb"], writes=[("tB", gi % 2)])
                    S.op("dve", lambda e, a_=a_, b_=b_: e.tensor_tensor(out=a_, in0=a_, in1=b_, op=ALU.add), reads=[("tA", gi % 2), ("tB", gi % 2)], writes=[("tA", gi % 2)])
                    S.op("dve", lambda e, a_=a_, g=g, gi=gi, bf=bf: e.scalar_tensor_tensor(out=Mst[bf][:, gi, :], in0=ident_f, scalar=dcol[:, g:g + 1], in1=a_,
                                                                                         op0=ALU.mult, op1=ALU.add),
                         reads=[("tA", gi % 2), "dcol", "ident_f"], writes=[("Mst", bf)])
                S.dma("sp", lambda e, gb=gb, bf=bf: e.dma_start(out=T["TM"][gb * 16:(gb + 1) * 16].rearrange("g k n -> k g n"), in_=Mst[bf]),
                      reads=[("Mst", bf)], writes=["TM"], key=("Mst", bf))
            S.barrier()
            AR.release(m)

        def run_path(p):
            nseq, slen = (4, 256) if p == 0 else (1, 1024)
            AR.release(base_mark)
            hT = AR.alloc([128, 8, NT], F32)
            xnT = AR.alloc([128, 8, NT], BF16)
            modT = AR.alloc([128, 48], F32)
            bmodT = AR.alloc([128, 48], F32)
            n1T = AR.alloc([128, 8], F32)
            n2T = AR.alloc([128, 8], F32)
            A1 = AR.alloc([128, 8], F32)
            A2 = AR.alloc([128, 8], F32)
            condT = AR.alloc([128, 8], F32)
            sT = AR.alloc([128, 8], BF16)
            rstd = AR.alloc([128, NT], F32)
            lnv = AR.alloc([128, 512], F32)
            path_mark = AR.mark()
            HK = [("hT", kc) for kc in range(8)]
            XK = [("xnT", kc) for kc in range(8)]

            m0 = AR.mark()
            xst = [AR.alloc([128, D], F32) for _ in range(2)]
            for tt in range(8):
                xs = xst[tt % 2]
                xk = ("xst", tt % 2)
                S.dma("sp", lambda e, xs=xs, tt=tt: e.dma_start(out=xs, in_=T["xin"][p, tt * 128:(tt + 1) * 128, :]), writes=[xk])
                for hf in range(2):
                    b = (tt * 2 + hf) % 8
                    for q in range(4):
                        kc = hf * 4 + q
                        S.op("pe", lambda e, b=b, q=q, kc=kc, xs=xs: e.transpose(ps[:, b, q * 128:(q + 1) * 128], xs[:, kc * 128:(kc + 1) * 128], ident_f),
                             reads=[xk, "ident_f"], writes=[PSK[b]], acc=True)
                    S.op("act" if hf else "dve",
                         (lambda e, b=b, hf=hf, tt=tt: e.activation(out=hT[:, hf * 4:hf * 4 + 4, tt * 128:(tt + 1) * 128],
                                                                     in_=ps[:, b, :].rearrange("p (a b) -> p a b", a=4), func=AF.Copy)) if hf else
                         (lambda e, b=b, hf=hf, tt=tt: e.tensor_copy(out=hT[:, hf * 4:hf * 4 + 4, tt * 128:(tt + 1) * 128],
                                                                      in_=ps[:, b, :].rearrange("p (a b) -> p a b", a=4))),
                         reads=[PSK[b]], writes=HK[hf * 4:hf * 4 + 4])
            S.barrier()
            AR.release(m0)
            if p == 0:
                cond2 = AR.alloc([128, 2, 8], F32)
                sT2 = AR.alloc([128, 8, 2], BF16)
                path_mark = AR.mark()
                for c_ in range(2):
                    small_load_T(cond2[:, c_, :], T["cond"][c_], "(kc q) -> q kc", ("cond2", c_), q=128)
                S.op("act", lambda e: e.activation(out=sT2.rearrange("q k c -> q c k"), in_=cond2, func=AF.Silu),
                     reads=[("cond2", 0), ("cond2", 1)], writes=["sT"])

            def compute_mod(l):
                m = AR.mark()
                small_load_T(n1T, T["norm1"][l], "(kc q) -> q kc", "n1T", q=128)
                small_load_T(n2T, T["norm2"][l], "(kc q) -> q kc", "n2T", q=128)
                if p == 0:
                    small_load_T(bmodT, T["b_mod"][l], "(c q) -> q c", "bmodT", q=128)
                    wm = [AR.alloc([128, 8, 512], BF16) for _ in range(2)]
                    for blk in range(12):
                        w = wm[blk % 2]
                        wk = ("wm", blk % 2)
                        S.dma("pool", lambda e, w=w, blk=blk: e.dma_start(
                            out=w, in_=T["w_mod"][l][:, blk * 512:(blk + 1) * 512].rearrange("(kc q) n -> q kc n", q=128)), writes=[wk])
                        for cc in range(4):
                            col = blk * 4 + cc
                            for kc in range(8):
                                mm(ps[:, 0, 2 * col:2 * col + 2], w[:, kc, cc * 128:(cc + 1) * 128], sT2[:, kc, :], kc == 0, kc == 7,
                                   [wk, "sT"], PSK[0])
                    pv = ps[:, 0, 0:96].rearrange("q (n c) -> q c n", c=2)
                    for c_ in range(2):
                        S.op("dve", lambda e, c_=c_: e.tensor_tensor(out=modAll[:, l, c_, :], in0=pv[:, c_, :], in1=bmodT, op=ALU.add),
                             reads=[PSK[0], "bmodT"], writes=[("modAll", l)])
                S.op("dve", lambda e: e.tensor_copy(out=modT, in_=modAll[:, l, p, :]), reads=[("modAll", l)], writes=["modT"])
                S.op("dve", lambda e: e.scalar_tensor_tensor(out=A1, in0=modT[:, 8:16], scalar=1.0, in1=n1T, op0=ALU.add, op1=ALU.mult),
                     reads=["modT", "n1T"], writes=["A1"])
                S.op("dve", lambda e: e.scalar_tensor_tensor(out=A2, in0=modT[:, 32:40], scalar=1.0, in1=n2T, op0=ALU.add, op1=ALU.mult),
                     reads=["modT", "n2T"], writes=["A2"])
                AR.release(m)

            def norm_mod(Acol, Akey, Boff, perm8=False):
                m = AR.mark()
                sq = AR.alloc([128, 8, NT], BF16)
                tmp = [AR.alloc([128, NT], F32) for _ in range(2)]
                S.op("act", lambda e: e.activation(out=sq, in_=hT, func=AF.Square), reads=HK, writes=["sq"])
                for hf in range(2):
                    b = hf
                    for kc in range(8):
                        mm(ps[:, b, :], ones_b, sq[:, kc, hf * 512:(hf + 1) * 512], kc == 0, kc == 7, ["sq", "ones_b"], PSK[b])
                    S.op("act", lambda e, b=b: e.activation(out=lnv, in_=ps[:, b, :], func=AF.Ln, bias=EPS, scale=1.0 / D),
                         reads=[PSK[b]], writes=["lnv"])
                    S.op("act", lambda e, hf=hf: e.activation(out=rstd[:, hf * 512:(hf + 1) * 512], in_=lnv, func=AF.Exp, scale=-0.5),
                         reads=["lnv"], writes=[("rstd", hf)])
                for kc in range(8):
                    t = tmp[kc % 2]
                    tk = ("nm_tmp", kc % 2)
                    S.op("dve", lambda e, t=t, kc=kc: e.scalar_tensor_tensor(out=t, in0=hT[:, kc, :], scalar=Acol[:, kc:kc + 1], in1=rstd,
                                                                              op0=ALU.mult, op1=ALU.mult),
                         reads=[HK[kc], Akey, ("rstd", 0), ("rstd", 1)], writes=[tk])
                    if perm8:
                        o = xnT[:, kc, :].rearrange("p (t k) -> p k t", t=8)
                    else:
                        o = xnT[:, kc, :]
                    S.op("act", lambda e, t=t, kc=kc, o=o: e.activation(out=o, in_=t, func=AF.Identity, bias=modT[:, Boff + kc:Boff + kc + 1]),
                         reads=[tk, "modT"], writes=[XK[kc]])
                S.barrier()
                AR.release(m)

            def resid(oc, hf, b, goff):
                S.op("dve", lambda e: e.scalar_tensor_tensor(out=hT[:, oc, hf * 512:(hf + 1) * 512], in0=ps[:, b, :],
                                                              scalar=modT[:, goff + oc:goff + oc + 1],
                                                              in1=hT[:, oc, hf * 512:(hf + 1) * 512], op0=ALU.mult, op1=ALU.add),
                     reads=[PSK[b], "modT", HK[oc]], writes=[HK[oc]])

            def ffn(l):
                m = AR.mark()
                norm_mod(A2, "A2", 24)
                gT = AR.alloc([128, 22, NT], BF16)
                wd = AR.alloc([128, 22, D], BF16)
                bupT = AR.alloc([128, 44], F32)
                cbT = AR.alloc([128, 44], F32)
                ckT = AR.alloc([128, 3, 44], F32)
                wu = [AR.alloc([128, 2, 8, 256], BF16) for _ in range(2)]
                tc = [AR.alloc([128, nseq, slen], F32) for _ in range(2)]
                sg = AR.alloc([128, nseq, slen], F32)
                beff = AR.alloc([128, 44], F32)
                e0 = AR.alloc([128, 44], F32)
                e2 = AR.alloc([128, 44], F32)
                small_load_T(bupT, T["b_up"][l], "(c q) -> q c", "bupT", q=128)
                small_load_T(cbT, T["conv_b"][l], "(c q) -> q c", "cbT", q=128)
                for t3 in range(3):
                    small_load_T(ckT[:, t3, :], T["conv_k"][l, t3], "(c q) -> q c", ("ckT", t3), q=128)
                CK = [("ckT", t3) for t3 in range(3)]
                S.op("dve", lambda e: e.tensor_tensor(out=beff, in0=ckT[:, 0, :], in1=ckT[:, 1, :], op=ALU.add), reads=CK, writes=["beff"])
                S.op("dve", lambda e: e.tensor_tensor(out=beff, in0=beff, in1=ckT[:, 2, :], op=ALU.add), reads=CK + ["beff"], writes=["beff"])
                S.op("dve", lambda e: e.tensor_tensor(out=beff, in0=beff, in1=bupT, op=ALU.mult), reads=["beff", "bupT"], writes=["beff"])
                S.op("dve", lambda e: e.tensor_tensor(out=beff, in0=beff, in1=cbT, op=ALU.add), reads=["beff", "cbT"], writes=["beff"])
                S.op("dve", lambda e: e.scalar_tensor_tensor(out=e0, in0=bupT, scalar=-1.0, in1=ckT[:, 0, :], op0=ALU.mult, op1=ALU.mult), reads=["bupT"] + CK, writes=["e0"])
                S.op("dve", lambda e: e.scalar_tensor_tensor(out=e2, in0=bupT, scalar=-1.0, in1=ckT[:, 2, :], op0=ALU.mult, op1=ALU.mult), reads=["bupT"] + CK, writes=["e2"])
                for j4 in range(0, 22, 6):
                    je = min(22, j4 + 6)
                    S.dma("pool", lambda e, j4=j4, je=je: e.dma_start(
                        out=wd[:, j4:je, :], in_=T["w_down"][l][j4 * 128:je * 128, :].rearrange("(j q) n -> q j n", q=128)),
                        writes=[("wd", j4)])
                WDK = [("wd", j4) for j4 in range(0, 22, 6)]
                for jb in range(11):
                    w = wu[jb % 2]
                    wk = ("wu", jb % 2)
                    for gv in range(2):
                        c0 = gv * DFF + jb * 256
                        S.dma("pool", lambda e, w=w, gv=gv, c0=c0: e.dma_start(
                            out=w[:, gv], in_=T["w_up"][l][:, c0:c0 + 256].rearrange("(kc q) n -> q kc n", q=128)),
                            writes=[(wk, gv)], key=(wk, gv))
                    for sub in range(2):
                        j = jb * 2 + sub
                        bs = (j % 2) * 4
                        for gv in range(2):
                            for hf in range(2):
                                b = bs + gv * 2 + hf
                                for kc in range(8):
                                    mm(ps[:, b, :], w[:, gv, kc, sub * 128:(sub + 1) * 128], xnT[:, kc, hf * 512:(hf + 1) * 512],
                                       kc == 0, kc == 7, [(wk, gv), XK[kc]], PSK[b])
                        for gv in range(2):
                            b = bs + gv * 2
                            col = gv * 22 + j
                            t_ = tc[gv]
                            tk = ("tc", gv)
                            pin = ps[:, b:b + 2, :].rearrange("p b (s t) -> p (b s) t", t=slen) if nseq == 4 else \
                                ps[:, b:b + 2, :].rearrange("p (o b) t -> p o (b t)", o=1)
                            PB = [PSK[b], PSK[b + 1]]
                            S.op("act", lambda e, t_=t_, pin=pin, col=col: e.activation(out=t_, in_=pin, func=AF.Identity,
                                                                                      bias=beff[:, col:col + 1], scale=ckT[:, 1, col:col + 1]),
                                 reads=PB + ["beff", CK[1]], writes=[tk])
                            S.op("dve", lambda e, t_=t_, pin=pin, col=col: e.scalar_tensor_tensor(out=t_[:, :, 1:slen], in0=pin[:, :, 0:slen - 1],
                                                                                                scalar=ckT[:, 0, col:col + 1], in1=t_[:, :, 1:slen],
                                                                                                op0=ALU.mult, op1=ALU.add),
                                 reads=PB + [CK[0], tk], writes=[tk])
                            S.op("dve", lambda e, t_=t_, pin=pin, col=col: e.scalar_tensor_tensor(out=t_[:, :, 0:slen - 1], in0=pin[:, :, 1:slen],
                                                                                                scalar=ckT[:, 2, col:col + 1], in1=t_[:, :, 0:slen - 1],
                                                                                                op0=ALU.mult, op1=ALU.add),
                                 reads=PB + [CK[2], tk], writes=[tk])
                            S.op("dve", lambda e, t_=t_, col=col: e.tensor_scalar(out=t_[:, :, 0], in0=t_[:, :, 0], scalar1=e0[:, col:col + 1], scalar2=None, op0=ALU.add),
                                 reads=[tk, "e0"], writes=[tk])
                            S.op("dve", lambda e, t_=t_, col=col: e.tensor_scalar(out=t_[:, :, slen - 1], in0=t_[:, :, slen - 1], scalar1=e2[:, col:col + 1], scalar2=None,
                                                                                  op0=ALU.add),
                                 reads=[tk, "e2"], writes=[tk])
                        S.op("act", lambda e: e.activation(out=sg, in_=tc[0], func=AF.Silu), reads=[("tc", 0)], writes=["sg"])
                        go = gT[:, j, :].rearrange("p (s t) -> p s t", s=nseq)
                        S.op("dve", lambda e, go=go: e.tensor_tensor(out=go, in0=sg, in1=tc[1], op=ALU.mult),
                             reads=["sg", ("tc", 1)], writes=[("gT", j)])
                bi = 0
                for oc in range(8):
                    for hf in range(2):
                        b = bi % 8
                        bi += 1
                        for j in range(22):
                            mm(ps[:, b, :], wd[:, j, oc * 128:(oc + 1) * 128], gT[:, j, hf * 512:(hf + 1) * 512], j == 0, j == 21,
                               [WDK[j // 6], ("gT", j)], PSK[b])
                        resid(oc, hf, b, 40)
                AR.release(m)

            def attention(l):
                L = LAYERS[l]
                dh, H, ckv = L["dh"], L["H"], L["ckv"]
                G = H // 4
                nkc = ckv // 128
                ctot = D + 2 * ckv
                scale = dh ** -0.5
                rope = (p == 1)
                m = AR.mark()
                norm_mod(A1, "A1", 0)
                if ATT_STOP <= -1:
                    AR.release(m)
                    return
                wbig = AR.alloc([128, 8, ctot], BF16)
                qT = AR.alloc([128, 8, NT], BF16)
                kT = AR.alloc([128, nkc, NT], BF16)
                vx = AR.alloc([128, 8, 4, dh + 2], BF16)
                qn_c = AR.alloc([128, 1], F32)
                kn_c = AR.alloc([128, 1], F32)
                sqb = [AR.alloc([128, 512], BF16) for _ in range(2)]
                rs = [AR.alloc([128, 512], F32) for _ in range(2)]
                qf = [AR.alloc([128, 512], F32) for _ in range(2)]
                S.dma("pool", lambda e: e.dma_start(out=wbig[:, :, 0:768], in_=T[L["wqkv"]][:, 0:768].rearrange("(kc q) n -> q kc n", q=128)),
                      writes=[("wbig", 0)])
                S.dma("pool", lambda e: e.dma_start(out=wbig[:, :, 768:ctot], in_=T[L["wqkv"]][:, 768:ctot].rearrange("(kc q) n -> q kc n", q=128)),
                      writes=[("wbig", 1)])
                WB = [("wbig", 0), ("wbig", 1)]
                small_load_col(qn_c, T[L["qn"]], dh, "qn_c")
                small_load_col(kn_c, T[L["kn"]], dh, "kn_c")
                QNK = ["qn_c"]
                KNK = ["kn_c"]
                if rope:
                    cosT = AR.alloc([128, NT], F32)
                    sinT = AR.alloc([128, NT], F32)
                    ri = 0 if dh == 64 else 2
                    S.dma("sp", lambda e: e.dma_start(out=cosT, in_=T["rope"][ri]), writes=["cosT"])
                    S.dma("sp", lambda e: e.dma_start(out=sinT, in_=T["rope"][ri + 1]), writes=["sinT"])
                    pm = pa_b if dh == 64 else pc_b
                    r1 = [AR.alloc([128, 512], F32) for _ in range(2)]
                    qb16 = [AR.alloc([128, 512], BF16) for _ in range(2)]
                esink = None
                if L["sink"]:
                    esink = AR.alloc([128, 16], F32)
                    S.dma("sp", lambda e: e.dma_start(out=tst2[0:1, 0:16], in_=T[L["sink"]].rearrange("(o q) -> o q", o=1)),
                          writes=[("tst2", 0)], key=("tst2", 0))
                    S.op("pe", lambda e: e.matmul(ps[:, 7, 0:16], ones_f[0:2, :], tst2[0:2, 0:16], start=True, stop=True),
                         reads=[("tst2", 0), "tst2z", "ones_f"], writes=[PSK[7], ("tst2", 0)])
                    S.op("act", lambda e: e.activation(out=esink, in_=ps[:, 7, 0:16], func=AF.Exp), reads=[PSK[7]], writes=["esink"])
                if p == 0:
                    kst = [AR.alloc([128, ckv], F32) for _ in range(2)]
                    vst = [AR.alloc([128, ckv], F32) for _ in range(2)]
                    kf32 = AR.alloc([128, nkc, NT], F32)
                S.op("pool", lambda e: e.memset(vx, 1.0), writes=["vx_init"])

                if ATT_STOP <= 0:
                    AR.release(m)
                    return
                for tt in range(8):
                    b = 4 + tt % 2
                    for kc in range(8):
                        mm(ps[:, b, 0:ckv], xnT[:, kc, tt * 128:(tt + 1) * 128], wbig[:, kc, D + ckv:D + 2 * ckv], kc == 0, kc == 7,
                           [XK[kc], WB[1]], PSK[b])
                    if p == 0:
                        v_ = vst[tt % 2]
                        vk = ("vst", tt % 2)
                        S.op("dve", lambda e, b=b, v_=v_: e.tensor_copy(out=v_, in_=ps[:, b, 0:ckv]), reads=[PSK[b]], writes=[vk])
                        S.op("act", lambda e, v_=v_, tt=tt: e.activation(out=vx[:, tt, :, 0:dh], in_=v_.rearrange("p (g d) -> p g d", g=4),
                                                                        func=AF.Copy),
                             reads=[vk, "vx_init"], writes=[("vx", tt)])
                        S.dma("sp", lambda e, v_=v_, tt=tt: e.dma_start(out=T[L["nv"]][tt * 128:(tt + 1) * 128, :], in_=v_), reads=[vk])
                    else:
                        S.op("act", lambda e, b=b, tt=tt: e.activation(out=vx[:, tt, :, 0:dh], in_=ps[:, b, 0:ckv].rearrange("p (g d) -> p g d", g=4),
                                                                       func=AF.Copy),
                             reads=[PSK[b], "vx_init"], writes=[("vx", tt)])
                if ATT_STOP <= 1:
                    AR.release(m)
                    return
                def lhs_cols(kind, c):
                    if kind == "q":
                        if dh == 128:
                            return [(c * 128, 128, 0)]
                        mq, i = c // 4, c % 4
                        return [((8 * mq + i) * 64, 64, 0), ((8 * mq + 4 + i) * 64, 64, 64)]
                    return [(D + c * 128, 128, 0)]

                ci = 0
                for kind, nch in (("k", nkc), ("q", 8)):
                    for c in range(nch):
                        for hf in range(2):
                            b = ci % 2
                            b2 = 2 + ci % 2
                            sl = slice(hf * 512, (hf + 1) * 512)
                            cols = lhs_cols(kind, c)
                            for (c0, w_, po) in cols:
                                for kc in range(8):
                                    mm(ps[po:po + w_, b, :], wbig[:, kc, c0:c0 + w_], xnT[:, kc, sl], kc == 0, kc == 7,
                                       [XK[kc], WB[0], WB[1]], PSK[b])
                            sq_ = sqb[ci % 2]
                            sk = ("sqb", ci % 2)
                            S.op("act", lambda e, sq_=sq_, b=b: e.activation(out=sq_, in_=ps[:, b, :], func=AF.Square), reads=[PSK[b]], writes=[sk])
                            mm(ps[:, b2, :], bd64_b if dh == 64 else ones_b, sq_, True, True, [sk, "cst_misc", "ones_b"], PSK[b2])
                            rs_ = rs[ci % 2]
                            rk = ("rs", ci % 2)
                            S.op("act", lambda e, b2=b2: e.activation(out=lnv, in_=ps[:, b2, :], func=AF.Ln, bias=EPS, scale=1.0 / dh),
                                 reads=[PSK[b2]], writes=["lnv"])
                            S.op("act", lambda e, rs_=rs_: e.activation(out=rs_, in_=lnv, func=AF.Exp, scale=-0.5), reads=["lnv"], writes=[rk])
                            gcol, gk = (qn_c, QNK) if kind == "q" else (kn_c, KNK)
                            dstT = qT if kind == "q" else kT
                            dk = (kind + "T", c, hf)
                            if not rope:
                                if kind == "k":
                                    S.op("dve", lambda e, c=c, sl=sl, b=b, rs_=rs_: e.scalar_tensor_tensor(
                                        out=kf32[:, c, sl], in0=ps[:, b, :], scalar=kn_c[:, 0:1], in1=rs_, op0=ALU.mult, op1=ALU.mult),
                                        reads=[PSK[b], rk] + gk, writes=[("kf32", c, hf)])
                                    S.op("pool", lambda e, c=c, sl=sl: e.tensor_copy(out=kT[:, c, sl], in_=kf32[:, c, sl]),
                                         reads=[("kf32", c, hf)], writes=[dk])
                                else:
                                    S.op("dve", lambda e, c=c, sl=sl, b=b, rs_=rs_, gcol=gcol, dstT=dstT: e.scalar_tensor_tensor(
                                        out=dstT[:, c, sl], in0=ps[:, b, :], scalar=gcol[:, 0:1], in1=rs_, op0=ALU.mult, op1=ALU.mult),
                                        reads=[PSK[b], rk] + gk, writes=[dk])
                            else:
                                q_ = qf[ci % 2]
                                qk_ = ("qf", ci % 2)
                                qb_ = qb16[ci % 2]
                                qbk = ("qb16", ci % 2)
                                r_ = r1[ci % 2]
                                r1k = ("r1", ci % 2)
                                S.op("dve", lambda e, q_=q_, b=b, rs_=rs_, gcol=gcol: e.scalar_tensor_tensor(
                                    out=q_, in0=ps[:, b, :], scalar=gcol[:, 0:1], in1=rs_, op0=ALU.mult, op1=ALU.mult),
                                    reads=[PSK[b], rk] + gk, writes=[qk_])
                                S.op("pool", lambda e, q_=q_, qb_=qb_: e.tensor_copy(out=qb_, in_=q_), reads=[qk_], writes=[qbk])
                                mm(ps[:, b, :], pm, qb_, True, True, [qbk, "cst_misc"], PSK[b])
                                S.op("dve", lambda e, r_=r_, b=b, sl=sl: e.tensor_tensor(out=r_, in0=ps[:, b, :], in1=sinT[:, sl], op=ALU.mult),
                                     reads=[PSK[b], "sinT"], writes=[r1k])
                                S.op("pool", lambda e, q_=q_, sl=sl: e.tensor_tensor(out=q_, in0=q_, in1=cosT[:, sl], op=ALU.mult),
                                     reads=[qk_, "cosT"], writes=[qk_])
                                S.op("dve", lambda e, q_=q_, r_=r_, c=c, sl=sl, dstT=dstT: e.tensor_tensor(out=dstT[:, c, sl], in0=q_, in1=r_, op=ALU.add),
                                     reads=[qk_, r1k], writes=[dk])
                            ci += 1
                if ATT_STOP <= 2:
                    AR.release(m)
                    return
                if p == 0:
                    for tt in range(8):
                        b = 6 + tt % 2
                        for c in range(nkc):
                            S.op("pe", lambda e, b=b, c=c, tt=tt: e.transpose(ps[:, b, c * 128:(c + 1) * 128], kf32[:, c, tt * 128:(tt + 1) * 128], ident_f),
                                 reads=[("kf32", c, tt // 4), "ident_f"], writes=[PSK[b]], acc=True)
                        k_ = kst[tt % 2]
                        kk = ("kst", tt % 2)
                        S.op("dve", lambda e, b=b, k_=k_: e.tensor_copy(out=k_, in_=ps[:, b, 0:ckv]), reads=[PSK[b]], writes=[kk])
                        S.dma("sp", lambda e, k_=k_, tt=tt: e.dma_start(out=T[L["nk"]][tt * 128:(tt + 1) * 128, :], in_=k_), reads=[kk])

                if p == 1:
                    kcT = AR.alloc([128, nkc, 512], BF16)
                    vcx = AR.alloc([128, 4, 4, dh + 2], BF16)
                    cst_ = [AR.alloc([128, ckv], F32) for _ in range(2)]
                    cb_ = [AR.alloc([128, ckv], BF16) for _ in range(2)]
                    S.op("pool", lambda e: e.memset(vcx, 1.0), writes=["vcx_init"])
                    for t4 in range(4):
                        c_ = cst_[t4 % 2]
                        ck_ = ("cst_", t4 % 2)
                        S.dma("sp", lambda e, c_=c_, t4=t4: e.dma_start(out=c_, in_=T[L["cv"]][t4 * 128:(t4 + 1) * 128, :]), writes=[ck_])
                        S.op("dve", lambda e, c_=c_, t4=t4: e.tensor_copy(out=vcx[:, t4, :, 0:dh], in_=c_.rearrange("p (g d) -> p g d", g=4)),
                             reads=[ck_, "vcx_init"], writes=[("vcx", t4)])
                    for t4 in range(4):
                        c_ = cst_[t4 % 2]
                        ck_ = ("cst_", t4 % 2)
                        bb = cb_[t4 % 2]
                        bk = ("cb_", t4 % 2)
                        S.dma("sp", lambda e, c_=c_, t4=t4: e.dma_start(out=c_, in_=T[L["ck"]][t4 * 128:(t4 + 1) * 128, :]), writes=[ck_])
                        S.op("dve", lambda e, c_=c_, bb=bb: e.tensor_copy(out=bb, in_=c_), reads=[ck_], writes=[bk])
                        b = 6 + t4 % 2
                        pb = ps[:, b, 0:64 * nkc].bitcast(BF16).rearrange("p (c t) -> p c t", c=nkc)
                        for c in range(nkc):
                            S.op("pe", lambda e, pb=pb, c=c, bb=bb: e.transpose(pb[:, c, :], bb[:, c * 128:(c + 1) * 128], ident_b),
                                 reads=[bk, "ident_b"], writes=[PSK[b]], acc=True)
                        S.op("dve", lambda e, pb=pb, t4=t4: e.tensor_copy(out=kcT[:, :, t4 * 128:(t4 + 1) * 128], in_=pb),
                             reads=[PSK[b]], writes=[("kcT", t4)])

                if ATT_STOP <= 3:
                    AR.release(m)
                    return
                S.dma("pool", lambda e: e.dma_start(out=wbig[:, :, 0:D], in_=T[L["wo"]].rearrange("(kc q) n -> q kc n", q=128)),
                      reads=[], writes=[("wbig", 0), ("wbig", 1)], key=("wbig", 0))

                nkt_max = 7 if L["kind"] == "A" else 12
                PT = [AR.alloc([128, nkt_max, G * 128], BF16) for _ in range(2)]
                otok = [AR.alloc([128, D], BF16) for _ in range(2)]
                OT = xnT
                den = AR.alloc([128, 4], F32)
                rden = AR.alloc([128, 4], F32)
                sbc = [0]

                def key_tiles(qt):
                    s_ = qt // (slen // 128)
                    kts = []
                    if p == 1:
                        kts += [("c", t4, None) for t4 in range(4)]
                        if L["kind"] == "A":
                            if qt > 0:
                                kts.append(("n", qt - 1, mlo_b))
                            kts.append(("n", qt, None))
                            if qt < 7:
                                kts.append(("n", qt + 1, mhi_b))
                        else:
                            kts += [("n", t8, None) for t8 in range(8)]
                    else:
                        kts += [("n", s_ * 2 + t2, None) for t2 in range(2)]
                    return kts

                def emit_scores(qt, g, it):
                    kts = key_tiles(qt)
                    pt = PT[it % 2]
                    ptk = ("PT", it % 2)
                    if dh == 64:
                        mq, e2_ = g // 2, g % 2
                        prt = slice(64 * e2_, 64 * e2_ + 64)
                        rhs = qT[prt, 4 * mq:4 * mq + 4, qt * 128:(qt + 1) * 128]
                        qreads = [("qT", 4 * mq + i, qt // 4) for i in range(4)]
                        kch = mq
                    else:
                        prt = slice(0, 128)
                        rhs = qT[:, 2 * g:2 * g + 2, qt * 128:(qt + 1) * 128]
                        qreads = [("qT", 2 * g + i, qt // 4) for i in range(2)]
                        kch = g
                    for ki, (src, ti, msk) in enumerate(kts):
                        b = sbc[0] % 4
                        sbc[0] += 1
                        if src == "c":
                            lhsT = kcT[prt, kch, ti * 128:(ti + 1) * 128]
                            kr = [("kcT", ti)]
                        else:
                            lhsT = kT[prt, kch, ti * 128:(ti + 1) * 128]
                            kr = [("kT", kch, ti // 4)]
                        mm(ps[:, b, 0:G * 128], lhsT, rhs, True, True, kr + qreads, PSK[b])
                        S.op("act", lambda e, pt=pt, ki=ki, b=b: e.activation(out=pt[:, ki, :], in_=ps[:, b, 0:G * 128], func=AF.Exp, scale=scale),
                             reads=[PSK[b]], writes=[(ptk, ki)])
                        if msk is not None:
                            S.op("pool", lambda e, pt=pt, ki=ki, msk=msk: e.tensor_tensor(
                                out=pt[:, ki, :], in0=pt[:, ki, :], in1=msk[:, 0:G, :].rearrange("p g q -> p (g q)"), op=ALU.mult),
                                reads=[(ptk, ki), "cst_misc"], writes=[(ptk, ki)])

                def emit_pv(qt, g, it):
                    kts = key_tiles(qt)
                    pt = PT[it % 2]
                    ptk = ("PT", it % 2)
                    ob = 4 + it % 2
                    ot = otok[qt % 2]
                    otk = ("otok", qt % 2)
                    opv = ps[:, ob, 0:G * (dh + 1)].rearrange("p (i d) -> p i d", i=G)
                    for i in range(G):
                        for ki, (src, ti, msk) in enumerate(kts):
                            if src == "c":
                                vr = vcx[:, ti, g, 0:dh + 1]
                                vk = [("vcx", ti)]
                            else:
                                vr = vx[:, ti, g, 0:dh + 1]
                                vk = [("vx", ti)]
                            mm(opv[:, i, :], pt[:, ki, i * 128:(i + 1) * 128], vr, ki == 0, ki == len(kts) - 1,
                               [(ptk, ki)] + vk, PSK[ob])
                    if esink is not None:
                        S.op("dve", lambda e, opv=opv, g=g: e.tensor_tensor(out=den[:, 0:G], in0=opv[:, :, dh], in1=esink[:, g * G:(g + 1) * G], op=ALU.add),
                             reads=[PSK[ob], "esink"], writes=["den"])
                        S.op("dve", lambda e: e.reciprocal(out=rden[:, 0:G], in_=den[:, 0:G]), reads=["den"], writes=["rden"])
                    else:
                        S.op("dve", lambda e, opv=opv: e.reciprocal(out=rden[:, 0:G], in_=opv[:, :, dh]), reads=[PSK[ob]], writes=["rden"])
                    for i in range(G):
                        h = g * G + i
                        S.op("dve", lambda e, opv=opv, i=i, h=h, ot=ot: e.tensor_scalar(out=ot[:, h * dh:(h + 1) * dh], in0=opv[:, i, 0:dh],
                                                                                         scalar1=rden[:, i:i + 1], scalar2=None, op0=ALU.mult),
                             reads=[PSK[ob], "rden"], writes=[(otk, h)])
                    if g == 3:
                        b = 6 + qt % 2
                        pb = ps[:, b, :].bitcast(BF16).rearrange("p (c t) -> p c t", c=8)
                        for c in range(8):
                            S.op("pe", lambda e, pb=pb, c=c, ot=ot: e.transpose(pb[:, c, :], ot[:, c * 128:(c + 1) * 128], ident_b),
                                 reads=[(otk, h) for h in range(H)] + ["ident_b"], writes=[PSK[b]], acc=True)
                        S.op("act", lambda e, pb=pb, qt=qt: e.activation(out=OT[:, :, qt * 128:(qt + 1) * 128], in_=pb, func=AF.Copy),
                             reads=[PSK[b]], writes=[("OT", qt)] + XK)

                items = [(qt, g) for qt in range(8) for g in range(4)]
                emit_scores(items[0][0], items[0][1], 0)
                for it, (qt, g) in enumerate(items):
                    if it + 1 < len(items):
                        emit_scores(items[it + 1][0], items[it + 1][1], it + 1)
                    emit_pv(qt, g, it)
                bi = 0
                for oc in range(8):
                    for hf in range(2):
                        b = bi % 4
                        bi += 1
                        for kc in range(8):
                            mm(ps[:, b, :], wbig[:, kc, oc * 128:(oc + 1) * 128], OT[:, kc, hf * 512:(hf + 1) * 512], kc == 0, kc == 7,
                               [("wbig", 0), ("wbig", 1)] + [("OT", hf * 4 + q) for q in range(4)], PSK[b])
                        resid(oc, hf, b, 16)
                AR.release(m)

            def s5_layer(l):
                CH = 128 // nseq
                m = AR.mark()
                norm_mod(A1, "A1", 0, perm8=True)
                U = AR.alloc([128, 64, 128], BF16)
                TB = AR.alloc([128, 2, 64, 128], BF16)
                SS = AR.alloc([128, 2, 64, 128], F32)
                ALL = AR.alloc([128, 2, 64], F32)
                ALS = AR.alloc([128, 2, 64], F32)
                xf = xnT.rearrange("q a n -> q (a n)").bitcast(F32)
                H0v = xf[:, 0:128 * nseq].rearrange("q (r g s) -> q r g s", r=2, g=64)
                L8 = AR.alloc([128, 2, 128], F32)
                FS = AR.alloc([128, 2, 64], F32)
                FSo = AR.alloc([128, 2, 128], F32)
                tt_ = [[xf[:, 128 * nseq * (1 + 2 * a_ + b_):128 * nseq * (2 + 2 * a_ + b_)].rearrange("q (r g s) -> q r g s", r=2, g=64)
                        for b_ in range(2)] for a_ in range(2)]
                S.dma("sp", lambda e: e.dma_start(out=T["D1"][p].rearrange("(fc q) n -> q fc n", q=128), in_=xnT), reads=XK, writes=["D1"])
                d1v = T["D1"][p].rearrange("(g c) (t k) -> t c g k", c=16, t=8)
                for t in range(8):
                    S.dma("sp", lambda e, t=t: e.dma_start(out=U[16 * t:16 * (t + 1)], in_=d1v[t]), reads=["D1"], writes=[("U", t)])
                UK = [("U", t) for t in range(8)]
                for r_ in range(2):
                    for g4 in range(4):
                        S.dma("sp", lambda e, r_=r_, g4=g4: e.dma_start(out=TB[:, r_, g4 * 16:(g4 + 1) * 16, :],
                                                                         in_=T["TS"][r_, g4 * 16:(g4 + 1) * 16].rearrange("g k n -> k g n")),
                              writes=[("TB", r_, g4)])
                S.dma("sp", lambda e: e.dma_start(out=L8[0:64], in_=T["TP8"].rearrange("r (d g) p -> g r d p", d=2)), writes=["L8"])
                for r_ in range(2):
                    S.op("pe", lambda e, r_=r_: e.transpose(ps[:, 7, 0:64], L8[0:64, r_, :], ident_f[0:64, 0:64]), reads=["L8", "ident_f"], writes=[PSK[7]])
                    if r_ == 0:
                        S.op("dve", lambda e: e.tensor_copy(out=ALL[:, 0, :], in_=ps[:, 7, 0:64]), reads=[PSK[7]], writes=["AL"])
                        S.op("dve", lambda e: e.tensor_copy(out=ALL[:, 1, :], in_=ps[:, 7, 0:64]), reads=[PSK[7]], writes=["AL"])
                    else:
                        S.op("dve", lambda e: e.tensor_scalar(out=ALS[:, 0, :], in0=ps[:, 7, 0:64], scalar1=-1.0, scalar2=None, op0=ALU.mult), reads=[PSK[7]], writes=["AL"])
                        S.op("dve", lambda e: e.tensor_copy(out=ALS[:, 1, :], in_=ps[:, 7, 0:64]), reads=[PSK[7]], writes=["AL"])
                if p == 1:
                    for r_ in range(2):
                        S.dma("sp", lambda e, r_=r_: e.dma_start(out=L8[0:64, r_, :].rearrange("g (d p) -> g d p", d=2),
                                                                  in_=T["st1"][:, r_].rearrange("d g p -> g d p")), reads=["L8"], writes=["L8"], key=("L8", r_))
                    for r_ in range(2):
                        S.op("pe", lambda e, r_=r_: e.transpose(ps[:, 7, 0:64], L8[0:64, r_, :], ident_f[0:64, 0:64]), reads=["L8", "ident_f"], writes=[PSK[7]])
                        S.op("dve", lambda e, r_=r_: e.tensor_copy(out=H0v[:, r_, :, 0], in_=ps[:, 7, 0:64]), reads=[PSK[7]], writes=["H0"] + XK)
                else:
                    S.op("dve", lambda e: e.memset(H0v, 0.0), writes=["H0"] + XK)
                for gq in range(16):
                    for r_ in range(2):
                        b = (gq * 2 + r_) % 4
                        for gi in range(4):
                            g = gq * 4 + gi
                            S.op("pe", lambda e, b=b, gi=gi, g=g, r_=r_: e.matmul(ps[:, b, gi * 128:(gi + 1) * 128], TB[:, r_, g, :], U[:, g, :], start=True, stop=True),
                                 reads=UK + [("TB", r_, g // 16)], writes=[PSK[b]], acc=True)
                        eng = "act" if r_ == 0 else "dve"
                        if r_ == 0:
                            S.op("act", lambda e, b=b, gq=gq: e.activation(out=SS[:, 0, gq * 4:(gq + 1) * 4, :].rearrange("q g k -> q (g k)"), in_=ps[:, b, :], func=AF.Copy),
                                 reads=[PSK[b]], writes=[("SS", 0)])
                        else:
                            S.op("dve", lambda e, b=b, gq=gq: e.tensor_copy(out=SS[:, 1, gq * 4:(gq + 1) * 4, :].rearrange("q g k -> q (g k)"), in_=ps[:, b, :]),
                                 reads=[PSK[b]], writes=[("SS", 1)])
                S.barrier()
                NS, CHV = nseq, CH
                SSv = SS.rearrange("q r g (s k) -> q r g s k", k=CHV)
                for d, eng in ((0, "dve"), (1, "pool")):
                    hs = slice(64 * d, 64 * d + 64)
                    order = list(range(CHV)) if d == 0 else list(range(CHV - 1, -1, -1))
                    t1, t2 = tt_[d]
                    k1, k2, kS = ("rt1", d), ("rt2", d), ("SSd", d)
                    all_ = ALL[hs].unsqueeze(3).broadcast_to([64, 2, 64, NS])
                    als_ = ALS[hs].unsqueeze(3).broadcast_to([64, 2, 64, NS])
                    for ii, kk in enumerate(order):
                        if ii == 0:
                            prev, prev_sw = H0v[hs], H0v[hs, ::-1]
                        else:
                            kp = order[ii - 1]
                            prev, prev_sw = SSv[hs, :, :, :, kp], SSv[hs, ::-1, :, :, kp]
                        cur = SSv[hs, :, :, :, kk]
                        S.op(eng, lambda e, hs=hs, t1=t1, prev=prev, all_=all_: e.tensor_tensor(out=t1[hs], in0=prev, in1=all_, op=ALU.mult),
                             reads=[kS, "AL", "H0"], writes=[k1])
                        S.op(eng, lambda e, hs=hs, t2=t2, prev_sw=prev_sw, als_=als_: e.tensor_tensor(out=t2[hs], in0=prev_sw, in1=als_, op=ALU.mult),
                             reads=[kS, "AL", "H0"], writes=[k2])
                        S.op(eng, lambda e, hs=hs, t1=t1, t2=t2: e.tensor_tensor(out=t1[hs], in0=t1[hs], in1=t2[hs], op=ALU.add), reads=[k1, k2], writes=[k1])
                        S.op(eng, lambda e, hs=hs, t1=t1, cur=cur: e.tensor_tensor(out=cur, in0=cur, in1=t1[hs], op=ALU.add), reads=[kS, k1], writes=[kS])
                S.barrier()
                if p == 0:
                    for s_ in range(nseq):
                        for r_ in range(2):
                            S.op("dve", lambda e, r_=r_, s_=s_: e.tensor_copy(out=FS[0:64, r_, :], in_=SS[0:64, r_, :, s_ * CH + CH - 1]), reads=[], writes=["FS"])
                            S.op("dve", lambda e, r_=r_, s_=s_: e.tensor_copy(out=FS[64:128, r_, :], in_=SS[64:128, r_, :, s_ * CH]), reads=[], writes=["FS"])
                            S.op("pe", lambda e, r_=r_: e.transpose(ps[0:64, 7, r_ * 128:(r_ + 1) * 128], FS[:, r_, :], ident_f), reads=["FS", "ident_f"], writes=[PSK[7]], acc=True)
                        S.op("dve", lambda e: e.tensor_copy(out=FSo[0:64].rearrange("g r n -> g (r n)"), in_=ps[0:64, 7, 0:256]), reads=[PSK[7]], writes=["FSo"])
                        for r_ in range(2):
                            S.dma("sp", lambda e, s_=s_, r_=r_: e.dma_start(out=T["nst"][s_][:, r_].rearrange("d g p -> g d p"),
                                                                           in_=FSo[0:64, r_, :].rearrange("g (d p) -> g d p", d=2)), reads=["FSo"], key=("nst", r_))
                for r_ in range(2):
                    for s_ in range(nseq):
                        k0 = s_ * CH
                        S.op("act", lambda e, r_=r_, k0=k0: e.activation(out=TB[0:64, r_, :, k0 + 1:k0 + CH], in_=SS[0:64, r_, :, k0:k0 + CH - 1], func=AF.Copy),
                             reads=[], writes=["Hin"])
                        S.op("act", lambda e, r_=r_, k0=k0, s_=s_: e.activation(out=TB[0:64, r_, :, k0], in_=H0v[0:64, r_, :, s_], func=AF.Copy), reads=["H0"], writes=["Hin"])
                        S.op("dve", lambda e, r_=r_, k0=k0: e.tensor_copy(out=TB[64:128, r_, :, k0:k0 + CH - 1], in_=SS[64:128, r_, :, k0 + 1:k0 + CH]),
                             reads=[], writes=["Hin"])
                        S.op("dve", lambda e, r_=r_, k0=k0, s_=s_: e.tensor_copy(out=TB[64:128, r_, :, k0 + CH - 1], in_=H0v[64:128, r_, :, s_]), reads=["H0"], writes=["Hin"])
                S.barrier()
                SSb = SS.rearrange("q r g k -> q (r g k)").bitcast(BF16)
                EL = SSb[:, 0:16384].rearrange("q (r g n) -> q r g n", r=2, g=64)
                ML = SSb[:, 16384:24576].rearrange("q (g n) -> q g n", g=64)
                YC = SSb[:, 24576:32768].rearrange("q (g n) -> q g n", g=64)
                for r_ in range(2):
                    for g4 in range(4):
                        S.dma("sp", lambda e, r_=r_, g4=g4: e.dma_start(out=EL[:, r_, g4 * 16:(g4 + 1) * 16, :],
                                                                         in_=T["TE"][r_, g4 * 16:(g4 + 1) * 16].rearrange("g k n -> k g n")),
                              writes=[("EL", r_, g4)])
                for g4 in range(4):
                    S.dma("sp", lambda e, g4=g4: e.dma_start(out=ML[:, g4 * 16:(g4 + 1) * 16, :], in_=T["TM"][g4 * 16:(g4 + 1) * 16].rearrange("g k n -> k g n")),
                          writes=[("ML", g4)])
                gu = [AR.alloc([128, 512], F32) for _ in range(2)]
                gs = [AR.alloc([128, 512], F32) for _ in range(2)]
                for gq in range(16):
                    b = gq % 4
                    for gi in range(4):
                        g = gq * 4 + gi
                        o_ = ps[:, b, gi * 128:(gi + 1) * 128]
                        S.op("pe", lambda e, o_=o_, g=g: e.matmul(o_, ML[:, g, :], U[:, g, :], start=True, stop=False), reads=[("ML", g // 16)] + UK, writes=[PSK[b]], acc=True)
                        S.op("pe", lambda e, o_=o_, g=g: e.matmul(o_, EL[:, 0, g, :], TB[:, 0, g, :], start=False, stop=False), reads=[("EL", 0, g // 16), "Hin"], writes=[PSK[b]], acc=True)
                        S.op("pe", lambda e, o_=o_, g=g: e.matmul(o_, EL[:, 1, g, :], TB[:, 1, g, :], start=False, stop=True), reads=[("EL", 1, g // 16), "Hin"], writes=[PSK[b]], acc=True)
                    u_, s2 = gu[gq % 2], gs[gq % 2]
                    uk, sk = ("gu", gq % 2), ("gs", gq % 2)
                    S.op("act", lambda e, u_=u_, b=b: e.activation(out=u_, in_=ps[:, b, :], func=AF.Square), reads=[PSK[b]], writes=[uk])
                    S.op("dve", lambda e, u_=u_: e.tensor_scalar(out=u_, in0=u_, scalar1=0.044715, scalar2=1.0, op0=ALU.mult, op1=ALU.add), reads=[uk], writes=[uk])
                    S.op("dve", lambda e, u_=u_, b=b: e.tensor_tensor(out=u_, in0=u_, in1=ps[:, b, :], op=ALU.mult), reads=[uk, PSK[b]], writes=[uk])
                    S.op("act", lambda e, u_=u_, s2=s2: e.activation(out=s2, in_=u_, func=AF.Sigmoid, scale=1.5957691216057308), reads=[uk], writes=[sk])
                    S.op("dve", lambda e, s2=s2, b=b, gq=gq: e.tensor_tensor(out=YC[:, gq * 4:(gq + 1) * 4, :].rearrange("q g k -> q (g k)"), in0=s2, in1=ps[:, b, :], op=ALU.mult),
                         reads=[sk, PSK[b]], writes=["YC"])
                d2v = T["D2"][p].rearrange("(g c) (t k) -> t c g k", c=16, t=8)
                for t in range(8):
                    S.dma("sp", lambda e, t=t: e.dma_start(out=d2v[t], in_=YC[16 * t:16 * (t + 1)]), reads=["YC"], writes=["D2"], key=("D2w", t))
                S.dma("sp", lambda e: e.dma_start(out=xnT, in_=T["D2"][p].rearrange("(fc q) n -> q fc n", q=128)), reads=["D2"], writes=XK, key="D2r")
                S.barrier()
                AR.release(m)
                m = AR.mark()
                wg = AR.alloc([128, 8, 2 * D], BF16)
                sg2 = [AR.alloc([128, 512], F32) for _ in range(2)]
                mx = [AR.alloc([128, 512], F32) for _ in range(2)]
                for h2 in range(2):
                    S.dma("pool", lambda e, h2=h2: e.dma_start(out=wg[:, :, h2 * D:(h2 + 1) * D],
                                                               in_=T["l1_w_glu"][:, h2 * D:(h2 + 1) * D].rearrange("(kc q) n -> q kc n", q=128)),
                          writes=[("wg", h2)])
                it = 0
                for oc in range(8):
                    for hf in range(2):
                        bA, bB = (it % 2) * 2, (it % 2) * 2 + 1
                        s_, x_ = sg2[it % 2], mx[it % 2]
                        sk, xk_ = ("sg2", it % 2), ("mx", it % 2)
                        it += 1
                        for kc in range(8):
                            mm(ps[:, bA, :], wg[:, kc, oc * 128:(oc + 1) * 128], xnT[:, kc, hf * 512:(hf + 1) * 512], kc == 0, kc == 7, [("wg", 0), XK[kc]], PSK[bA])
                        for kc in range(8):
                            mm(ps[:, bB, :], wg[:, kc, D + oc * 128:D + (oc + 1) * 128], xnT[:, kc, hf * 512:(hf + 1) * 512], kc == 0, kc == 7, [("wg", 1), XK[kc]], PSK[bB])
                        S.op("act", lambda e, s_=s_, bB=bB: e.activation(out=s_, in_=ps[:, bB, :], func=AF.Sigmoid), reads=[PSK[bB]], writes=[sk])
                        S.op("dve", lambda e, s_=s_, x_=x_, bA=bA: e.tensor_tensor(out=x_, in0=ps[:, bA, :], in1=s_, op=ALU.mult), reads=[PSK[bA], sk], writes=[xk_])
                        hv = hT[:, oc, :].rearrange("q (k t) -> q t k", t=8)[:, 4 * hf:4 * hf + 4, :]
                        S.op("dve", lambda e, x_=x_, hv=hv, oc=oc: e.scalar_tensor_tensor(out=hv, in0=x_.rearrange("q (t k) -> q t k", t=4), scalar=modT[:, 16 + oc:17 + oc],
                                                                                        in1=hv, op0=ALU.mult, op1=ALU.add),
                             reads=[xk_, "modT", HK[oc]], writes=[HK[oc]])
                AR.release(m)

            for l in range(NLAYERS):
                if STAGE & 1:
                    compute_mod(l)
                S.barrier()
                if STAGE & 2:
                    if LAYERS[l]["kind"] == "B":
                        if not SKIP_S5:
                            s5_layer(l)
                    else:
                        attention(l)
                S.barrier()
                if STAGE & 4:
                    ffn(l)
                S.barrier()

            m0 = AR.mark()
            yst = [AR.alloc([128, D], F32) for _ in range(2)]
            for tt in range(8):
                y_ = yst[tt % 2]
                yk = ("yst", tt % 2)
                for hf in range(2):
                    b = (tt * 2 + hf) % 8
                    for q in range(4):
                        kc = hf * 4 + q
                        S.op("pe", lambda e, b=b, q=q, kc=kc, tt=tt: e.transpose(ps[:, b, q * 128:(q + 1) * 128], hT[:, kc, tt * 128:(tt + 1) * 128], ident_f),
                             reads=[HK[kc], "ident_f"], writes=[PSK[b]], acc=True)
                    if hf:
                        S.op("act", lambda e, b=b, y_=y_: e.activation(out=y_[:, 512:1024], in_=ps[:, b, :], func=AF.Copy), reads=[PSK[b]], writes=[(yk, 1)])
                    else:
                        S.op("dve", lambda e, b=b, y_=y_: e.tensor_copy(out=y_[:, 0:512], in_=ps[:, b, :]), reads=[PSK[b]], writes=[(yk, 0)])
                S.dma("sp", lambda e, y_=y_, tt=tt: e.dma_start(out=T["y"][p, tt * 128:(tt + 1) * 128, :], in_=y_), reads=[(yk, 0), (yk, 1)], key=yk)
            S.barrier()
            AR.release(m0)

        if not SKIP_S5 and NLAYERS > 1 and (STAGE & 2):
            s5_tables()
        run_path(0)
        run_path(1)
        S.barrier()
        print('n_sems', len(S.sem_objs), 'insts', {e: len(S.prog[e]) for e in ENGS})
        S.emit()
    return nc


def _consts():
    cst = np.zeros((9, 128, 128), np.float32)
    for tt_ in range(8):
        for cc_ in range(16):
            cst[CST_R16, cc_, tt_ * 16 + cc_] = 1.0
    cst[CST_IDENT] = np.eye(128)
    bd = np.zeros((128, 128), np.float32)
    bd[:64, :64] = 1
    bd[64:, 64:] = 1
    cst[CST_BD64] = bd
    kk = np.arange(128)[:, None]
    qq = np.arange(128)[None, :]
    cst[CST_MLO] = (qq <= kk)
    cst[CST_MHI] = (kk <= qq)

    def perm(dh):
        qt = dh // 4
        half = dh // 2
        Pm = np.zeros((128, 128), np.float32)
        for m in range(128):
            d = m % dh
            if (d % half) < qt:
                Pm[m, m + qt] = -1.0
            else:
                Pm[m, m - qt] = 1.0
        return Pm.T.copy()

    cst[CST_PA] = perm(64)
    cst[CST_PC] = perm(128)
    t1 = np.arange(8)
    tau_p = np.repeat(t1, 16)[:, None]
    tau = np.repeat(t1, 16)[None, :]
    cst[CST_S5F] = (tau >= tau_p)
    cst[CST_S5B] = (tau <= tau_p)
    rope = np.zeros((4, 128, NT), np.float32)
    t = np.arange(NT)
    row = (t // 64).astype(np.float32)
    col = (t % 64).astype(np.float32)
    for idx, dh in ((0, 64), (2, 128)):
        half, qt = dh // 2, dh // 4
        freqs = (1.0 / (np.float32(10000.0) ** (np.arange(qt, dtype=np.float32) / np.float32(qt)))).astype(np.float32)
        for pp in range(128):
            d = pp % dh
            pos = row if d < half else col
            ang = (pos * freqs[d % qt]).astype(np.float32)
            rope[idx, pp] = np.cos(ang)
            rope[idx + 1, pp] = np.sin(ang)
    return cst, rope


_PROG = {}


def kernel(**inp):
    f32 = lambda a: np.ascontiguousarray(np.asarray(a, dtype=np.float32))
    inp = {k: f32(v) for k, v in inp.items()}
    if "nc" not in _PROG:
        _PROG["nc"] = build_program()
    nc = _PROG["nc"]
    cst, rope = _consts()
    shared = {}
    for name, shape in IN_SPECS:
        if name in inp:
            shared[name] = inp[name].reshape(shape)
    shared["cst"] = cst
    shared["rope"] = rope
    in_maps = []
    for c in range(8):
        m = dict(shared)
        m["xin"] = np.stack([inp["x_prompt"][4 * c:4 * c + 4].reshape(NT, D), inp["x_sample"][c]])
        m["cond"] = np.stack([inp["c_ctx"], inp["c"][c]])
        m["ck0"] = inp["cache_l0_k"][c].reshape(512, 256)
        m["cv0"] = inp["cache_l0_v"][c].reshape(512, 256)
        m["st1"] = inp["state_l1"][c]
        m["ck2"] = inp["cache_l2_k"][c].reshape(512, 512)
        m["cv2"] = inp["cache_l2_v"][c].reshape(512, 512)
        m["ck3"] = inp["cache_l3_k"][c].reshape(512, 256)
        m["cv3"] = inp["cache_l3_v"][c].reshape(512, 256)
        in_maps.append({k: np.ascontiguousarray(v) for k, v in m.items()})
    res = run_bass_kernel_spmd(nc, in_maps, core_ids=list(range(8)))
    R = res.results
    y_prompt = np.concatenate([R[c]["y"][0].reshape(4, 256, D) for c in range(8)], 0)
    y_sample = np.stack([R[c]["y"][1] for c in range(8)], 0)
    cat = lambda k, shp: np.concatenate([R[c][k].reshape(shp) for c in range(8)], 0)
    out = (y_prompt, y_sample,
           cat("nk0", (4, 256, 4, 64)), cat("nv0", (4, 256, 4, 64)),
           cat("nst", (4, 2, 2, 64, 64)),
           cat("nk2", (4, 256, 4, 128)), cat("nv2", (4, 256, 4, 128)),
           cat("nk3", (4, 256, 4, 64)), cat("nv3", (4, 256, 4, 64)))
    return tuple(np.ascontiguousarray(o, dtype=np.float32) for o in out)
```

```python
import math
import numpy as np
from contextlib import ExitStack
import concourse.bass as bass
import concourse.mybir as mybir
from concourse.bass_utils import run_bass_kernel_spmd

F32 = mybir.dt.float32
BF16 = mybir.dt.bfloat16
I32 = mybir.dt.int32
U8 = mybir.dt.uint8
AF = mybir.ActivationFunctionType
ALU = mybir.AluOpType
AX = mybir.AxisListType

D = 1024
DFF = 2816
NT = 1024
ENGS = ["pe", "act", "dve", "pool", "sp"]
EPS = 1e-6
NLAYERS = 4
SKIP_S5 = False
DEBUG_H = False
STAGE = 7
ATT_STOP = 9


class Sched:
    def __init__(self, nc, stack):
        self.nc = nc
        self.stack = stack
        self.prog = {e: [] for e in ENGS}
        self.cnt = {e: 0 for e in ENGS}
        self.seen = {e: {} for e in ENGS}
        self.lastw = {}
        self.readers = {}
        self.sem_objs = []
        self.esid = {}
        for e in ENGS:
            self.esid[e] = self._new_sem("s_" + e)
        self.dsem = {}
        self.dfree = []

    def _new_sem(self, name):
        self.sem_objs.append(self.stack.enter_context(self.nc.semaphore(name)))
        return len(self.sem_objs) - 1

    def _dma_sem(self, key):
        if key not in self.dsem:
            if self.dfree:
                self.dsem[key] = self.dfree.pop()
            else:
                self.dsem[key] = [self._new_sem("d%d" % len(self.sem_objs)), 0]
        return self.dsem[key]

    def _deps(self, eng, reads, writes, acc=False):
        need = {}

        def add(rec, same_ok=False):
            sid, val, e = rec
            if same_ok and e == eng:
                return
            if need.get(sid, 0) < val:
                need[sid] = val

        for k in reads:
            if k in self.lastw:
                add(self.lastw[k])
        for k in writes:
            if k in self.lastw:
                add(self.lastw[k], same_ok=(acc and eng == "pe"))
            for r in self.readers.get(k, ()):
                add(r, same_ok=True)
        waits = []
        seen = self.seen[eng]
        for sid, val in need.items():
            if seen.get(sid, 0) >= val:
                continue
            seen[sid] = val
            waits.append((sid, val))
        return waits

    def op(self, eng, fn, reads=(), writes=(), acc=False):
        waits = self._deps(eng, reads, writes, acc)
        self.cnt[eng] += 1
        rec = (self.esid[eng], self.cnt[eng], eng)
        self.prog[eng].append((waits, fn, self.esid[eng], 1))
        for k in reads:
            self.readers.setdefault(k, []).append(rec)
        for k in writes:
            self.lastw[k] = rec
            self.readers[k] = []

    def dma(self, q, fn, reads=(), writes=(), key=None):
        if key is None:
            key = writes[0] if writes else reads[0]
        waits = self._deps(q, reads, writes)
        ds = self._dma_sem(key)
        ds[1] += 16
        rec = (ds[0], ds[1], "dma")
        self.prog[q].append((waits, fn, ds[0], 16))
        for k in reads:
            self.readers.setdefault(k, []).append(rec)
        for k in writes:
            self.lastw[k] = rec
            self.readers[k] = []

    def barrier(self):
        allw = [(self.esid[e], self.cnt[e], e) for e in ENGS if self.cnt[e] > 0]
        allw += [(sid, c, None) for k, (sid, c) in self.dsem.items() if c > 0]
        for e in ENGS:
            waits = []
            for sid, val, owner in allw:
                if owner == e:
                    continue
                if self.seen[e].get(sid, 0) >= val:
                    continue
                self.seen[e][sid] = val
                waits.append((sid, val))
            if waits:
                self.prog[e].append((waits, None, None, 0))
        self.dfree.extend(self.dsem.values())
        self.dsem = {}
        self.lastw = {}
        self.readers = {}

    def emit(self):
        nc, sems, prog = self.nc, self.sem_objs, self.prog

        def run(ename):
            def body(eng):
                for waits, fn, incsid, incv in prog[ename]:
                    for sid, val in waits:
                        eng.wait_ge(sems[sid], val)
                    if fn is not None:
                        fn(eng).then_inc(sems[incsid], incv)
            return body

        with nc.Block() as block:
            block.tensor(run("pe"))
            block.scalar(run("act"))
            block.vector(run("dve"))
            block.gpsimd(run("pool"))
            block.sync(run("sp"))


class Arena:
    def __init__(self, tensor, size):
        self.t = tensor
        self.size = size
        self.off = 0

    def alloc(self, shape, dt):
        nb = {F32: 4, BF16: 2, I32: 4}[dt]
        n = int(np.prod(shape[1:]))
        off = (self.off + 63) // 64 * 64
        assert off + n * nb <= self.size, ("SBUF arena overflow", off, n * nb, self.size)
        self.off = off + n * nb
        ap = self.t[:, off:off + n * nb].bitcast(dt)
        if len(shape) == 3:
            ap = ap.rearrange("p (a b) -> p a b", a=shape[1])
        elif len(shape) == 4:
            ap = ap.rearrange("p (a b c) -> p a b c", a=shape[1], b=shape[2])
        if shape[0] != 128:
            ap = ap[0:shape[0]]
        return ap

    def mark(self):
        return self.off

    def release(self, m):
        self.off = m


LAYERS = [
    dict(kind="A", dh=64, H=16, ckv=256, wqkv="l0_w_qkv", qn="l0_q_norm", kn="l0_k_norm", sink="l0_sink", wo="l0_w_o",
         ck="ck0", cv="cv0", nk="nk0", nv="nv0"),
    dict(kind="B"),
    dict(kind="C", dh=128, H=8, ckv=512, wqkv="l2_w_qkv", qn="l2_q_norm", kn="l2_k_norm", sink=None, wo="l2_w_o",
         ck="ck2", cv="cv2", nk="nk2", nv="nv2"),
    dict(kind="A", dh=64, H=16, ckv=256, wqkv="l3_w_qkv", qn="l3_q_norm", kn="l3_k_norm", sink="l3_sink", wo="l3_w_o",
         ck="ck3", cv="cv3", nk="nk3", nv="nv3"),
]

IN_SPECS = [
    ("xin", [2, NT, D]), ("cond", [2, D]),
    ("ck0", [512, 256]), ("cv0", [512, 256]), ("st1", [2, 2, 64, 64]),
    ("ck2", [512, 512]), ("cv2", [512, 512]), ("ck3", [512, 256]), ("cv3", [512, 256]),
    ("norm1", [4, D]), ("norm2", [4, D]), ("w_mod", [4, D, 6 * D]), ("b_mod", [4, 6 * D]),
    ("w_up", [4, D, 2 * DFF]), ("b_up", [4, 2 * DFF]), ("conv_k", [4, 3, 2 * DFF]), ("conv_b", [4, 2 * DFF]),
    ("w_down", [4, DFF, D]),
    ("l0_w_qkv", [D, 1536]), ("l0_q_norm", [64]), ("l0_k_norm", [64]), ("l0_sink", [16]), ("l0_w_o", [D, D]),
    ("l1_lam_re", [2, 64, 64]), ("l1_lam_im", [2, 64, 64]), ("l1_log_dt", [2, 64]),
    ("l1_b_re", [2, 64, 64, 16]), ("l1_b_im", [2, 64, 64, 16]), ("l1_c_re", [2, 64, 16, 64]), ("l1_c_im", [2, 64, 16, 64]),
    ("l1_d_skip", [D]), ("l1_w_glu", [D, 2 * D]),
    ("l2_w_qkv", [D, 2048]), ("l2_q_norm", [128]), ("l2_k_norm", [128]), ("l2_w_o", [D, D]),
    ("l3_w_qkv", [D, 1536]), ("l3_q_norm", [64]), ("l3_k_norm", [64]), ("l3_sink", [16]), ("l3_w_o", [D, D]),
    ("cst", [9, 128, 128]), ("rope", [4, 128, NT]),
]
OUT_SPECS = [
    ("y", [2, NT, D]), ("nk0", [NT, 256]), ("nv0", [NT, 256]), ("nst", [4, 2, 2, 64, 64]),
    ("nk2", [NT, 512]), ("nv2", [NT, 512]), ("nk3", [NT, 256]), ("nv3", [NT, 256]),
]
CST_IDENT, CST_BD64, CST_MLO, CST_MHI, CST_PA, CST_PC, CST_S5F, CST_S5B, CST_R16 = range(9)


def build_program():
    nc = bass.Bass("TRN2", target_bir_lowering=False)
    T = {}
    for name, shape in IN_SPECS:
        T[name] = nc.dram_tensor(name, shape, F32, kind="ExternalInput").ap()
    for name, shape in OUT_SPECS:
        T[name] = nc.dram_tensor(name, shape, F32, kind="ExternalOutput").ap()
    for name, shape, dt_ in [("TX", [128, 16384], BF16), ("TZ", [128, 16384], BF16), ("TM", [64, 128, 128], BF16),
                             ("TE", [2, 64, 128, 128], BF16), ("TS", [2, 64, 128, 128], BF16), ("TP8", [2, 128, 64], F32),
                             ("D1", [2, 1024, 1024], BF16), ("D2", [2, 1024, 1024], BF16)]:
        T[name] = nc.dram_tensor(name, shape, dt_, kind="Internal").ap()
    if DEBUG_H:
        T["dbg"] = nc.dram_tensor("dbg", [2, NT, D], F32, kind="ExternalOutput").ap()

    with ExitStack() as st:
        S = Sched(nc, st)
        ARENA_BYTES = 190 * 1024
        arena_t = st.enter_context(nc.sbuf_tensor("arena", [128, ARENA_BYTES], U8))
        AR = Arena(arena_t, ARENA_BYTES)
        ps = st.enter_context(nc.psum_tensor("ps", [128, 8, 512], F32))
        PSK = [("ps", b) for b in range(8)]
        uid = [0]

        def fresh(prefix):
            uid[0] += 1
            return (prefix, uid[0])

        ident_f = AR.alloc([128, 128], F32)
        ident_b = AR.alloc([128, 128], BF16)
        ones_b = AR.alloc([128, 128], BF16)
        ones_f = AR.alloc([128, 128], F32)
        bd64_b = AR.alloc([128, 128], BF16)
        mlo_b = AR.alloc([128, 4, 128], BF16)
        mhi_b = AR.alloc([128, 4, 128], BF16)
        pa_b = AR.alloc([128, 128], BF16)
        pc_b = AR.alloc([128, 128], BF16)
        cstage = AR.alloc([128, 128], F32)
        S.dma("sp", lambda e: e.dma_start(out=ident_f, in_=T["cst"][CST_IDENT]), writes=["ident_f"])
        S.op("dve", lambda e: e.tensor_copy(out=ident_b, in_=ident_f), reads=["ident_f"], writes=["ident_b"])
        S.op("dve", lambda e: e.memset(ones_b, 1.0), writes=["ones_b"])
        S.op("dve", lambda e: e.memset(ones_f, 1.0), writes=["ones_f"])
        for ci, dst, rep in [(CST_BD64, bd64_b, 0), (CST_MLO, mlo_b, 4), (CST_MHI, mhi_b, 4), (CST_PA, pa_b, 0), (CST_PC, pc_b, 0)]:
            S.dma("sp", lambda e, ci=ci: e.dma_start(out=cstage, in_=T["cst"][ci]), writes=["cstage"])
            if rep:
                for r in range(rep):
                    S.op("dve", lambda e, dst=dst, r=r: e.tensor_copy(out=dst[:, r, :], in_=cstage), reads=["cstage"], writes=["cst_misc"])
            else:
                S.op("dve", lambda e, dst=dst: e.tensor_copy(out=dst, in_=cstage), reads=["cstage"], writes=["cst_misc"])
        S.barrier()
        base_mark = AR.mark()

        def mm(out, lhsT, rhs, start, stop, reads, wkey):
            S.op("pe", lambda e: e.matmul(out, lhsT, rhs, start=start, stop=stop), reads=reads, writes=[wkey], acc=True)

        tstage = AR.alloc([128, 128], F32)
        modAll = AR.alloc([128, 4, 2, 48], F32)
        tst2 = AR.alloc([128, 128], F32)
        S.op("dve", lambda e: e.memset(tst2, 0.0), writes=["tst2z"])
        S.barrier()
        base_mark2 = [None]

        def small_load_T(dst, src_ap, pattern, key, **kw):
            n = dst.shape[1]
            rows = src_ap.rearrange("(c q) -> c q", q=128)
            S.dma("sp", lambda e: e.dma_start(out=tstage[0:n, :], in_=rows), writes=["tstage"])
            S.op("pe", lambda e: e.transpose(ps[:, 7, 0:n], tstage[0:n, :], ident_f[0:n, 0:n]), reads=["tstage", "ident_f"], writes=[PSK[7]])
            S.op("dve", lambda e: e.tensor_copy(out=dst, in_=ps[:, 7, 0:n]), reads=[PSK[7]], writes=[key])

        def small_load_col(dst, src_ap, dh, key):
            rep = 128 // dh
            for r in range(rep):
                S.dma("sp", lambda e, r=r: e.dma_start(out=tst2[0:1, r * dh:(r + 1) * dh], in_=src_ap.rearrange("(o q) -> o q", o=1)),
                      writes=[("tst2", r)], key=("tst2", r))
            S.op("pe", lambda e: e.transpose(ps[:, 7, 0:2], tst2[0:2, :], ident_f[0:2, 0:2]),
                 reads=[("tst2", r) for r in range(rep)] + ["tst2z", "ident_f"], writes=[PSK[7], ("tst2", 0), ("tst2", 1)])
            S.op("dve", lambda e: e.tensor_copy(out=dst, in_=ps[:, 7, 0:1]), reads=[PSK[7]], writes=[key])

        base_mark = AR.mark()

        def s5_tables():
            m = AR.mark()
            TWO_PI = 2.0 * math.pi
            lr = AR.alloc([128, 64], F32)
            li = AR.alloc([128, 64], F32)
            dtc = AR.alloc([128, 1], F32)
            xx = AR.alloc([128, 64], F32)
            th = AR.alloc([128, 64], F32)
            PWr = AR.alloc([128, 16, 64], F32)
            PWi = AR.alloc([128, 16, 64], F32)
            Br = AR.alloc([128, 64, 16], F32)
            Bi = AR.alloc([128, 64, 16], F32)
            Bbr = AR.alloc([128, 64, 16], F32)
            Bbi = AR.alloc([128, 64, 16], F32)
            Cr = AR.alloc([128, 16, 64], F32)
            Ci = AR.alloc([128, 16, 64], F32)
            w = [AR.alloc([128, 1024], F32) for _ in range(4)]
            s64 = [AR.alloc([128, 64], F32) for _ in range(8)]
            ki = AR.alloc([128, 64], I32)
            PMr = AR.alloc([128, 64], F32)
            PMi = AR.alloc([128, 64], F32)
            ST = [AR.alloc([128, 16384], BF16) for _ in range(2)]

            def dv(fn, reads, writes):
                S.op("dve", fn, reads=reads, writes=writes)

            S.dma("sp", lambda e: e.dma_start(out=lr, in_=T["l1_lam_re"].rearrange("d g p -> (d g) p")), writes=["lr"])
            S.dma("sp", lambda e: e.dma_start(out=li, in_=T["l1_lam_im"].rearrange("d g p -> (d g) p")), writes=["li"])
            S.dma("sp", lambda e: e.dma_start(out=dtc, in_=T["l1_log_dt"].rearrange("d (g o) -> (d g) o", o=1)), writes=["dtc"])
            S.dma("sp", lambda e: e.dma_start(out=Br, in_=T["l1_b_re"].rearrange("d g p c -> (d g) p c")), writes=["Br"])
            S.dma("sp", lambda e: e.dma_start(out=Bi, in_=T["l1_b_im"].rearrange("d g p c -> (d g) p c")), writes=["Bi"])
            S.dma("sp", lambda e: e.dma_start(out=Cr, in_=T["l1_c_re"].rearrange("d g c p -> (d g) c p")), writes=["Cr"])
            S.dma("sp", lambda e: e.dma_start(out=Ci, in_=T["l1_c_im"].rearrange("d g c p -> (d g) c p")), writes=["Ci"])
            S.op("act", lambda e: e.activation(out=dtc, in_=dtc, func=AF.Exp), reads=["dtc"], writes=["dtc"])
            dv(lambda e: e.tensor_scalar(out=xx, in0=lr, scalar1=dtc[:, 0:1], scalar2=None, op0=ALU.mult), ["lr", "dtc"], ["xx"])
            dv(lambda e: e.tensor_scalar(out=th, in0=li, scalar1=dtc[:, 0:1], scalar2=None, op0=ALU.mult), ["li", "dtc"], ["th"])

            def sinred(out, okey, shift):
                v, kf, r, mk = s64[0], s64[1], s64[2], s64[3]
                dv(lambda e: e.tensor_scalar(out=v, in0=th, scalar1=float(shift), scalar2=None, op0=ALU.add), ["th"], ["sr_v"])
                dv(lambda e: e.tensor_scalar(out=kf, in0=v, scalar1=1.0 / TWO_PI, scalar2=None, op0=ALU.mult), ["sr_v"], ["sr_kf"])
                dv(lambda e: e.tensor_copy(out=ki, in_=kf), ["sr_kf"], ["sr_ki"])
                dv(lambda e: e.tensor_copy(out=kf, in_=ki), ["sr_ki"], ["sr_kf"])
                dv(lambda e: e.scalar_tensor_tensor(out=r, in0=kf, scalar=-TWO_PI, in1=v, op0=ALU.mult, op1=ALU.add), ["sr_kf", "sr_v"], ["sr_r"])
                dv(lambda e: e.tensor_scalar(out=mk, in0=r, scalar1=math.pi, scalar2=None, op0=ALU.is_gt), ["sr_r"], ["sr_m"])
                dv(lambda e: e.scalar_tensor_tensor(out=r, in0=mk, scalar=-TWO_PI, in1=r, op0=ALU.mult, op1=ALU.add), ["sr_m", "sr_r"], ["sr_r"])
                dv(lambda e: e.tensor_scalar(out=mk, in0=r, scalar1=-math.pi, scalar2=None, op0=ALU.is_lt), ["sr_r"], ["sr_m"])
                dv(lambda e: e.scalar_tensor_tensor(out=r, in0=mk, scalar=TWO_PI, in1=r, op0=ALU.mult, op1=ALU.add), ["sr_m", "sr_r"], ["sr_r"])
                dv(lambda e: e.tensor_scalar(out=r, in0=r, scalar1=3.1415925, scalar2=-3.1415925, op0=ALU.min, op1=ALU.max), ["sr_r"], ["sr_r"])
                S.op("act", lambda e: e.activation(out=out, in_=r, func=AF.Sin), reads=["sr_r"], writes=[okey])

            sn, cs, ex, exm = s64[4], s64[5], s64[6], s64[7]
            sinred(sn, "sn", 0.0)
            sinred(cs, "cs", math.pi / 2.0)
            S.op("act", lambda e: e.activation(out=ex, in_=xx, func=AF.Exp), reads=["xx"], writes=["ex"])
            S.op("act", lambda e: e.activation(out=exm, in_=xx, func=AF.Exp, scale=-1.0), reads=["xx"], writes=["exm"])
            PK = lambda j: ("PW", j)
            dv(lambda e: e.memset(PWr[:, 7, :], 1.0), [], [("PWr", 0)])
            dv(lambda e: e.memset(PWi[:, 7, :], 0.0), [], [("PWi", 0)])
            dv(lambda e: e.tensor_tensor(out=PWr[:, 8, :], in0=ex, in1=cs, op=ALU.mult), ["ex", "cs"], [("PWr", 1)])
            dv(lambda e: e.tensor_tensor(out=PWi[:, 8, :], in0=ex, in1=sn, op=ALU.mult), ["ex", "sn"], [("PWi", 1)])
            dv(lambda e: e.tensor_tensor(out=PWr[:, 6, :], in0=exm, in1=cs, op=ALU.mult), ["exm", "cs"], [("PWr", -1)])
            dv(lambda e: e.scalar_tensor_tensor(out=PWi[:, 6, :], in0=exm, scalar=-1.0, in1=sn, op0=ALU.mult, op1=ALU.mult), ["exm", "sn"], [("PWi", -1)])

            def cmul64(j_out, j_a, j_b):
                orr, oi = PWr[:, j_out + 7, :], PWi[:, j_out + 7, :]
                ar, ai = PWr[:, j_a + 7, :], PWi[:, j_a + 7, :]
                br, bi = PWr[:, j_b + 7, :], PWi[:, j_b + 7, :]
                ka = [("PWr", j_a), ("PWi", j_a), ("PWr", j_b), ("PWi", j_b)]
                t1, t2 = s64[0], s64[1]
                dv(lambda e: e.tensor_tensor(out=t1, in0=ar, in1=br, op=ALU.mult), ka, ["c64a"])
                dv(lambda e: e.tensor_tensor(out=t2, in0=ai, in1=bi, op=ALU.mult), ka, ["c64b"])
                dv(lambda e: e.tensor_tensor(out=orr, in0=t1, in1=t2, op=ALU.subtract), ["c64a", "c64b"], [("PWr", j_out)])
                dv(lambda e: e.tensor_tensor(out=t1, in0=ar, in1=bi, op=ALU.mult), ka, ["c64a"])
                dv(lambda e: e.tensor_tensor(out=t2, in0=ai, in1=br, op=ALU.mult), ka, ["c64b"])
                dv(lambda e: e.tensor_tensor(out=oi, in0=t1, in1=t2, op=ALU.add), ["c64a", "c64b"], [("PWi", j_out)])

            for j in range(2, 9):
                cmul64(j, j - 1, 1)
            for j in range(2, 8):
                cmul64(-j, -(j - 1), -1)
            S.dma("sp", lambda e: e.dma_start(out=T["TP8"][0], in_=PWr[:, 15, :]), reads=[("PWr", 8)], key="tp8a")
            S.dma("sp", lambda e: e.dma_start(out=T["TP8"][1], in_=PWi[:, 15, :]), reads=[("PWi", 8)], key="tp8b")
            nr, den, fr, fi = s64[0], s64[1], s64[2], s64[3]
            t5, t6 = s64[4], s64[5]
            a1 = [("PWr", 1), ("PWi", 1)]
            dv(lambda e: e.tensor_scalar(out=nr, in0=PWr[:, 8, :], scalar1=-1.0, scalar2=None, op0=ALU.add), a1, ["nr"])
            dv(lambda e: e.tensor_tensor(out=den, in0=lr, in1=lr, op=ALU.mult), ["lr"], ["den"])
            dv(lambda e: e.tensor_tensor(out=t5, in0=li, in1=li, op=ALU.mult), ["li"], ["t5"])
            dv(lambda e: e.tensor_tensor(out=den, in0=den, in1=t5, op=ALU.add), ["den", "t5"], ["den"])
            dv(lambda e: e.reciprocal(out=den, in_=den), ["den"], ["den"])
            dv(lambda e: e.tensor_tensor(out=fr, in0=nr, in1=lr, op=ALU.mult), ["nr", "lr"], ["fr"])
            dv(lambda e: e.tensor_tensor(out=t5, in0=PWi[:, 8, :], in1=li, op=ALU.mult), a1 + ["li"], ["t5"])
            dv(lambda e: e.tensor_tensor(out=fr, in0=fr, in1=t5, op=ALU.add), ["fr", "t5"], ["fr"])
            dv(lambda e: e.tensor_tensor(out=fr, in0=fr, in1=den, op=ALU.mult), ["fr", "den"], ["fr"])
            dv(lambda e: e.tensor_tensor(out=fi, in0=PWi[:, 8, :], in1=lr, op=ALU.mult), a1 + ["lr"], ["fi"])
            dv(lambda e: e.tensor_tensor(out=t6, in0=nr, in1=li, op=ALU.mult), ["nr", "li"], ["t6"])
            dv(lambda e: e.tensor_tensor(out=fi, in0=fi, in1=t6, op=ALU.subtract), ["fi", "t6"], ["fi"])
            dv(lambda e: e.tensor_tensor(out=fi, in0=fi, in1=den, op=ALU.mult), ["fi", "den"], ["fi"])

            def bc_b(a):
                return a.unsqueeze(2).broadcast_to([128, 64, 16])

            def bc_c(a):
                return a.unsqueeze(1).broadcast_to([128, 16, 64])

            w3b = [x.rearrange("q (p c) -> q p c", c=16) for x in w]
            w3c = [x.rearrange("q (c p) -> q c p", p=64) for x in w]

            def cmul_big(eng, pr, pi, pk, xr, xi, xk, bc, wv, out_r, out_i, okey, xr_n=None, xi_n=None):
                t1, t2, t3, t4 = wv
                tg = "cb" + eng
                op = lambda fn, r, w_: S.op(eng, fn, reads=r, writes=w_)
                op(lambda e: e.tensor_tensor(out=t1, in0=xr, in1=bc(pr), op=ALU.mult), pk + xk, [tg + "1"])
                op(lambda e: e.tensor_tensor(out=t2, in0=xi, in1=bc(pi), op=ALU.mult), pk + xk, [tg + "2"])
                op(lambda e: e.tensor_tensor(out=out_r, in0=t1, in1=t2, op=ALU.subtract), [tg + "1", tg + "2"], [okey])
                a_i = xi if xi_n is None else xi_n
                a_r = xr if xr_n is None else xr_n
                op(lambda e: e.tensor_tensor(out=t3, in0=a_i, in1=bc(pr), op=ALU.mult), pk + xk, [tg + "3"])
                op(lambda e: e.tensor_tensor(out=t4, in0=a_r, in1=bc(pi), op=ALU.mult), pk + xk, [tg + "4"])
                op(lambda e: e.tensor_tensor(out=out_i, in0=t3, in1=t4, op=ALU.add), [tg + "3", tg + "4"], [okey])

            cmul_big("dve", fr, fi, ["fr", "fi"], Br, Bi, ["Br", "Bi"], bc_b, w3b, Bbr, Bbi, "Bb")
            nCr = Br.rearrange("q p c -> q (p c)").rearrange("q (c p) -> q c p", p=64)
            nCi = Bi.rearrange("q p c -> q (p c)").rearrange("q (c p) -> q c p", p=64)
            dv(lambda e: e.tensor_scalar(out=nCr, in0=Cr, scalar1=-1.0, scalar2=None, op0=ALU.mult), ["Cr", "Bb"], ["nC"])
            dv(lambda e: e.tensor_scalar(out=nCi, in0=Ci, scalar1=-1.0, scalar2=None, op0=ALU.mult), ["Ci", "Bb"], ["nC"])
            PM = {"dve": (PMr, PMi), "pool": (AR.alloc([128, 64], F32), AR.alloc([128, 64], F32))}
            wp = [AR.alloc([128, 1024], F32) for _ in range(4)]
            WV = {"dve": (w3b, w3c),
                  "pool": ([x.rearrange("q (p c) -> q p c", c=16) for x in wp], [x.rearrange("q (c p) -> q c p", p=64) for x in wp])}

            def mixed_power(eng, jf, jb):
                pmr, pmi = PM[eng]
                k = "PM" + eng
                S.op(eng, lambda e: e.tensor_copy(out=pmr[0:64], in_=PWr[0:64, jf + 7, :]), reads=[("PWr", jf)], writes=[k])
                S.op(eng, lambda e: e.tensor_copy(out=pmi[0:64], in_=PWi[0:64, jf + 7, :]), reads=[("PWi", jf)], writes=[k])
                S.op(eng, lambda e: e.tensor_copy(out=pmr[64:128], in_=PWr[64:128, jb + 7, :]), reads=[("PWr", jb)], writes=[k])
                S.op(eng, lambda e: e.tensor_copy(out=pmi[64:128], in_=PWi[64:128, jb + 7, :]), reads=[("PWi", jb)], writes=[k])

            specs = [("B", lambda t: -t, lambda t: t, "TX", "ript", "dve"),
                     ("C", lambda t: t, lambda t: -t, "TZ", "ript", "dve"),
                     ("B", lambda t: 7 - t, lambda t: t, "TS", "ritp", "dve"),
                     ("C", lambda t: t + 1, lambda t: 8 - t, "TE", "ript", "dve")]
            for ti, (side, pf, pb_, dst, layout, eng) in enumerate(specs):
                st_ = ST[ti % 2]
                stk = ("ST", ti % 2)
                pmr, pmi = PM[eng]
                pmk = ["PM" + eng]
                if layout == "ript":
                    st5 = st_.rearrange("q (r p t c) -> q r p t c", r=2, p=64, t=8)
                else:
                    st5 = st_.rearrange("q (r t c p) -> q r t c p", r=2, t=8, c=16)
                for t in range(8):
                    mixed_power(eng, pf(t), pb_(t))
                    if side == "B":
                        if layout == "ript":
                            o_r, o_i = st5[:, 0, :, t, :], st5[:, 1, :, t, :]
                        else:
                            o_r = st5[:, 0, t, :, :].rearrange("q c p -> q p c")
                            o_i = st5[:, 1, t, :, :].rearrange("q c p -> q p c")
                        cmul_big(eng, pmr, pmi, pmk, Bbr, Bbi, ["Bb"], bc_b, WV[eng][0], o_r, o_i, stk)
                    else:
                        o_r = st5[:, 0, :, t, :].rearrange("q p c -> q c p")
                        o_i = st5[:, 1, :, t, :].rearrange("q p c -> q c p")
                        cmul_big(eng, pmr, pmi, pmk, Cr, Ci, ["Cr", "Ci", "nC"], bc_c, WV[eng][1], o_r, o_i, stk, xr_n=nCr, xi_n=nCi)
                if dst in ("TX", "TZ"):
                    S.dma("sp", lambda e, st_=st_, dst=dst: e.dma_start(out=T[dst], in_=st_), reads=[stk], writes=[dst], key=stk)
                elif dst == "TE":
                    st4 = st_.rearrange("q (r p n) -> q r p n", r=2, p=64)
                    for d in range(2):
                        for r_ in range(2):
                            S.dma("sp", lambda e, d=d, r_=r_, st4=st4: e.dma_start(
                                out=T["TE"][r_, :, d * 64:(d + 1) * 64, :], in_=st4[d * 64:(d + 1) * 64, r_]), reads=[stk], writes=["TE"], key=stk)
                else:
                    st4 = st_.rearrange("q (r n p) -> q r n p", r=2, p=64)
                    for d in range(2):
                        for r_ in range(2):
                            for g8 in range(8):
                                S.dma("sp", lambda e, d=d, r_=r_, g8=g8, st4=st4: e.dma_start(
                                    out=T["TS"][r_, g8 * 8:(g8 + 1) * 8, :, d * 64:(d + 1) * 64],
                                    in_=st4[d * 64 + g8 * 8:d * 64 + (g8 + 1) * 8, r_]), reads=[stk], writes=["TS"], key=stk)
            S.barrier()
            AR.release(m)
            m = AR.mark()
            mkf = AR.alloc([128, 128], F32)
            mkb = AR.alloc([128, 128], F32)
            r16 = AR.alloc([128, 128], F32)
            dgc = AR.alloc([128, 16], F32)
            dmat = AR.alloc([128, 64], F32)
            dcol = AR.alloc([128, 64], F32)
            S.dma("sp", lambda e: e.dma_start(out=mkf, in_=T["cst"][CST_S5F]), writes=["mkf"])
            S.dma("sp", lambda e: e.dma_start(out=mkb, in_=T["cst"][CST_S5B]), writes=["mkb"])
            S.dma("sp", lambda e: e.dma_start(out=r16, in_=T["cst"][CST_R16]), writes=["r16"])
            S.dma("sp", lambda e: e.dma_start(out=dgc[0:64, :], in_=T["l1_d_skip"].rearrange("(g c) -> g c", c=16)), writes=["dgc"])
            S.op("pe", lambda e: e.transpose(ps[0:16, 7, 0:64], dgc[0:64, :], ident_f[0:64, 0:64]), reads=["dgc", "ident_f"], writes=[PSK[7]])
            S.op("dve", lambda e: e.tensor_copy(out=dmat[0:16, :], in_=ps[0:16, 7, 0:64]), reads=[PSK[7]], writes=["dmat"])
            S.op("pe", lambda e: e.matmul(ps[:, 7, 64:128], r16[0:16, :], dmat[0:16, :], start=True, stop=True), reads=["r16", "dmat"], writes=[PSK[7]])
            S.op("dve", lambda e: e.tensor_copy(out=dcol, in_=ps[:, 7, 64:128]), reads=[PSK[7]], writes=["dcol"])
            XL = [[AR.alloc([128, 16, 128], BF16) for _ in range(2)] for _ in range(2)]
            ZL = [[AR.alloc([128, 16, 128], BF16) for _ in range(2)] for _ in range(2)]
            Mst = [AR.alloc([128, 16, 128], BF16) for _ in range(2)]
            tA = [AR.alloc([128, 128], F32) for _ in range(2)]
            tB = [AR.alloc([128, 128], F32) for _ in range(2)]
            TX4 = T["TX"].rearrange("(d g) (k n) -> d g k n", d=2, k=128)
            TZ4 = T["TZ"].rearrange("(d g) (k n) -> d g k n", d=2, k=128)
            for gb in range(4):
                bf = gb % 2
                for d in range(2):
                    S.dma("sp", lambda e, d=d, gb=gb, bf=bf: e.dma_start(out=XL[bf][d], in_=TX4[d, gb * 16:(gb + 1) * 16].rearrange("g k n -> k g n")),
                          reads=["TX"], writes=[("XL", bf, d)])
                    S.dma("sp", lambda e, d=d, gb=gb, bf=bf: e.dma_start(out=ZL[bf][d], in_=TZ4[d, gb * 16:(gb + 1) * 16].rearrange("g k n -> k g n")),
                          reads=["TZ"], writes=[("ZL", bf, d)])
                for gi in range(16):
                    g = gb * 16 + gi
                    b = gi % 4
                    for d in range(2):
                        S.op("pe", lambda e, b=b, d=d, bf=bf, gi=gi: e.matmul(ps[:, b, d * 128:(d + 1) * 128], XL[bf][d][:, gi, :], ZL[bf][d][:, gi, :],
                                                                            start=True, stop=True),
                             reads=[("XL", bf, d), ("ZL", bf, d)], writes=[PSK[b]], acc=True)
                    a_, b_ = tA[gi % 2], tB[gi % 2]
                    S.op("dve", lambda e, a_=a_, b=b: e.tensor_tensor(out=a_, in0=ps[:, b, 0:128], in1=mkf, op=ALU.mult), reads=[PSK[b], "mkf"], writes=[("tA", gi % 2)])
                    S.op("dve", lambda e, b_=b_, b=b: e.tensor_tensor(out=b_, in0=ps[:, b, 128:256], in1=mkb, op=ALU.mult), reads=[PSK[b], "mkb"], writes=[("tB", gi % 2)])
                    S.op("dve", lambda e, a_=a_, b_=b_: e.tensor_tensor(out=a_, in0=a_, in1=b_, op=ALU.add), reads=[("tA", gi % 2), ("tB", gi % 2)], writes=[("tA", gi % 2)])
                    S.op("dve", lambda e, a_=a_, g=g, gi=gi, bf=bf: e.scalar_tensor_tensor(out=Mst[bf][:, gi, :], in0=ident_f, scalar=dcol[:, g:g + 1], in1=a_,
                                                                                         op0=ALU.mult, op1=ALU.add),
                         reads=[("tA", gi % 2), "dcol", "ident_f"], writes=[("Mst", bf)])
                S.dma("sp", lambda e, gb=gb, bf=bf: e.dma_start(out=T["TM"][gb * 16:(gb + 1) * 16].rearrange("g k n -> k g n"), in_=Mst[bf]),
                      reads=[("Mst", bf)], writes=["TM"], key=("Mst", bf))
            S.barrier()
            AR.release(m)

        def run_path(p):
            nseq, slen = (4, 256) if p == 0 else (1, 1024)
            AR.release(base_mark)
            hT = AR.alloc([128, 8, NT], F32)
            xnT = AR.alloc([128, 8, NT], BF16)
            modT = AR.alloc([128, 48], F32)
            bmodT = AR.alloc([128, 48], F32)
            n1T = AR.alloc([128, 8], F32)
            n2T = AR.alloc([128, 8], F32)
            A1 = AR.alloc([128, 8], F32)
            A2 = AR.alloc([128, 8], F32)
            condT = AR.alloc([128, 8], F32)
            sT = AR.alloc([128, 8], BF16)
            rstd = AR.alloc([128, NT], F32)
            lnv = AR.alloc([128, 512], F32)
            path_mark = AR.mark()
            HK = [("hT", kc) for kc in range(8)]
            XK = [("xnT", kc) for kc in range(8)]

            m0 = AR.mark()
            xst = [AR.alloc([128, D], F32) for _ in range(2)]
            for tt in range(8):
                xs = xst[tt % 2]
                xk = ("xst", tt % 2)
                S.dma("sp", lambda e, xs=xs, tt=tt: e.dma_start(out=xs, in_=T["xin"][p, tt * 128:(tt + 1) * 128, :]), writes=[xk])
                for hf in range(2):
                    b = (tt * 2 + hf) % 8
                    for q in range(4):
                        kc = hf * 4 + q
                        S.op("pe", lambda e, b=b, q=q, kc=kc, xs=xs: e.transpose(ps[:, b, q * 128:(q + 1) * 128], xs[:, kc * 128:(kc + 1) * 128], ident_f),
                             reads=[xk, "ident_f"], writes=[PSK[b]], acc=True)
                    S.op("act" if hf else "dve",
                         (lambda e, b=b, hf=hf, tt=tt: e.activation(out=hT[:, hf * 4:hf * 4 + 4, tt * 128:(tt + 1) * 128],
                                                                     in_=ps[:, b, :].rearrange("p (a b) -> p a b", a=4), func=AF.Copy)) if hf else
                         (lambda e, b=b, hf=hf, tt=tt: e.tensor_copy(out=hT[:, hf * 4:hf * 4 + 4, tt * 128:(tt + 1) * 128],
                                                                      in_=ps[:, b, :].rearrange("p (a b) -> p a b", a=4))),
                         reads=[PSK[b]], writes=HK[hf * 4:hf * 4 + 4])
            S.barrier()
            AR.release(m0)
            if p == 0:
                cond2 = AR.alloc([128, 2, 8], F32)
                sT2 = AR.alloc([128, 8, 2], BF16)
                path_mark = AR.mark()
                for c_ in range(2):
                    small_load_T(cond2[:, c_, :], T["cond"][c_], "(kc q) -> q kc", ("cond2", c_), q=128)
                S.op("act", lambda e: e.activation(out=sT2.rearrange("q k c -> q c k"), in_=cond2, func=AF.Silu),
                     reads=[("cond2", 0), ("cond2", 1)], writes=["sT"])

            def compute_mod(l):
                m = AR.mark()
                small_load_T(n1T, T["norm1"][l], "(kc q) -> q kc", "n1T", q=128)
                small_load_T(n2T, T["norm2"][l], "(kc q) -> q kc", "n2T", q=128)
                if p == 0:
                    small_load_T(bmodT, T["b_mod"][l], "(c q) -> q c", "bmodT", q=128)
                    wm = [AR.alloc([128, 8, 512], BF16) for _ in range(2)]
                    for blk in range(12):
                        w = wm[blk % 2]
                        wk = ("wm", blk % 2)
                        S.dma("pool", lambda e, w=w, blk=blk: e.dma_start(
                            out=w, in_=T["w_mod"][l][:, blk * 512:(blk + 1) * 512].rearrange("(kc q) n -> q kc n", q=128)), writes=[wk])
                        for cc in range(4):
                            col = blk * 4 + cc
                            for kc in range(8):
                                mm(ps[:, 0, 2 * col:2 * col + 2], w[:, kc, cc * 128:(cc + 1) * 128], sT2[:, kc, :], kc == 0, kc == 7,
                                   [wk, "sT"], PSK[0])
                    pv = ps[:, 0, 0:96].rearrange("q (n c) -> q c n", c=2)
                    for c_ in range(2):
                        S.op("dve", lambda e, c_=c_: e.tensor_tensor(out=modAll[:, l, c_, :], in0=pv[:, c_, :], in1=bmodT, op=ALU.add),
                             reads=[PSK[0], "bmodT"], writes=[("modAll", l)])
                S.op("dve", lambda e: e.tensor_copy(out=modT, in_=modAll[:, l, p, :]), reads=[("modAll", l)], writes=["modT"])
                S.op("dve", lambda e: e.scalar_tensor_tensor(out=A1, in0=modT[:, 8:16], scalar=1.0, in1=n1T, op0=ALU.add, op1=ALU.mult),
                     reads=["modT", "n1T"], writes=["A1"])
                S.op("dve", lambda e: e.scalar_tensor_tensor(out=A2, in0=modT[:, 32:40], scalar=1.0, in1=n2T, op0=ALU.add, op1=ALU.mult),
                     reads=["modT", "n2T"], writes=["A2"])
                AR.release(m)

            def norm_mod(Acol, Akey, Boff, perm8=False):
                m = AR.mark()
                sq = AR.alloc([128, 8, NT], BF16)
                tmp = [AR.alloc([128, NT], F32) for _ in range(2)]
                S.op("act", lambda e: e.activation(out=sq, in_=hT, func=AF.Square), reads=HK, writes=["sq"])
                for hf in range(2):
                    b = hf
                    for kc in range(8):
                        mm(ps[:, b, :], ones_b, sq[:, kc, hf * 512:(hf + 1) * 512], kc == 0, kc == 7, ["sq", "ones_b"], PSK[b])
                    S.op("act", lambda e, b=b: e.activation(out=lnv, in_=ps[:, b, :], func=AF.Ln, bias=EPS, scale=1.0 / D),
                         reads=[PSK[b]], writes=["lnv"])
                    S.op("act", lambda e, hf=hf: e.activation(out=rstd[:, hf * 512:(hf + 1) * 512], in_=lnv, func=AF.Exp, scale=-0.5),
                         reads=["lnv"], writes=[("rstd", hf)])
                for kc in range(8):
                    t = tmp[kc % 2]
                    tk = ("nm_tmp", kc % 2)
                    S.op("dve", lambda e, t=t, kc=kc: e.scalar_tensor_tensor(out=t, in0=hT[:, kc, :], scalar=Acol[:, kc:kc + 1], in1=rstd,
                                                                              op0=ALU.mult, op1=ALU.mult),
                         reads=[HK[kc], Akey, ("rstd", 0), ("rstd", 1)], writes=[tk])
                    if perm8:
                        o = xnT[:, kc, :].rearrange("p (t k) -> p k t", t=8)
                    else:
                        o = xnT[:, kc, :]
                    S.op("act", lambda e, t=t, kc=kc, o=o: e.activation(out=o, in_=t, func=AF.Identity, bias=modT[:, Boff + kc:Boff + kc + 1]),
                         reads=[tk, "modT"], writes=[XK[kc]])
                S.barrier()
                AR.release(m)

            def resid(oc, hf, b, goff):
                S.op("dve", lambda e: e.scalar_tensor_tensor(out=hT[:, oc, hf * 512:(hf + 1) * 512], in0=ps[:, b, :],
                                                              scalar=modT[:, goff + oc:goff + oc + 1],
                                                              in1=hT[:, oc, hf * 512:(hf + 1) * 512], op0=ALU.mult, op1=ALU.add),
                     reads=[PSK[b], "modT", HK[oc]], writes=[HK[oc]])

            def ffn(l):
                m = AR.mark()
                norm_mod(A2, "A2", 24)
                gT = AR.alloc([128, 22, NT], BF16)
                wd = AR.alloc([128, 22, D], BF16)
                bupT = AR.alloc([128, 44], F32)
                cbT = AR.alloc([128, 44], F32)
                ckT = AR.alloc([128, 3, 44], F32)
                wu = [AR.alloc([128, 2, 8, 256], BF16) for _ in range(2)]
                tc = [AR.alloc([128, nseq, slen], F32) for _ in range(2)]
                sg = AR.alloc([128, nseq, slen], F32)
                beff = AR.alloc([128, 44], F32)
                e0 = AR.alloc([128, 44], F32)
                e2 = AR.alloc([128, 44], F32)
                small_load_T(bupT, T["b_up"][l], "(c q) -> q c", "bupT", q=128)
                small_load_T(cbT, T["conv_b"][l], "(c q) -> q c", "cbT", q=128)
                for t3 in range(3):
                    small_load_T(ckT[:, t3, :], T["conv_k"][l, t3], "(c q) -> q c", ("ckT", t3), q=128)
                CK = [("ckT", t3) for t3 in range(3)]
                S.op("dve", lambda e: e.tensor_tensor(out=beff, in0=ckT[:, 0, :], in1=ckT[:, 1, :], op=ALU.add), reads=CK, writes=["beff"])
                S.op("dve", lambda e: e.tensor_tensor(out=beff, in0=beff, in1=ckT[:, 2, :], op=ALU.add), reads=CK + ["beff"], writes=["beff"])
                S.op("dve", lambda e: e.tensor_tensor(out=beff, in0=beff, in1=bupT, op=ALU.mult), reads=["beff", "bupT"], writes=["beff"])
                S.op("dve", lambda e: e.tensor_tensor(out=beff, in0=beff, in1=cbT, op=ALU.add), reads=["beff", "cbT"], writes=["beff"])
                S.op("dve", lambda e: e.scalar_tensor_tensor(out=e0, in0=bupT, scalar=-1.0, in1=ckT[:, 0, :], op0=ALU.mult, op1=ALU.mult), reads=["bupT"] + CK, writes=["e0"])
                S.op("dve", lambda e: e.scalar_tensor_tensor(out=e2, in0=bupT, scalar=-1.0, in1=ckT[:, 2, :], op0=ALU.mult, op1=ALU.mult), reads=["bupT"] + CK, writes=["e2"])
                for j4 in range(0, 22, 6):
                    je = min(22, j4 + 6)
                    S.dma("pool", lambda e, j4=j4, je=je: e.dma_start(
                        out=wd[:, j4:je, :], in_=T["w_down"][l][j4 * 128:je * 128, :].rearrange("(j q) n -> q j n", q=128)),
                        writes=[("wd", j4)])
                WDK = [("wd", j4) for j4 in range(0, 22, 6)]
                for jb in range(11):
                    w = wu[jb % 2]
                    wk = ("wu", jb % 2)
                    for gv in range(2):
                        c0 = gv * DFF + jb * 256
                        S.dma("pool", lambda e, w=w, gv=gv, c0=c0: e.dma_start(
                            out=w[:, gv], in_=T["w_up"][l][:, c0:c0 + 256].rearrange("(kc q) n -> q kc n", q=128)),
                            writes=[(wk, gv)], key=(wk, gv))
                    for sub in range(2):
                        j = jb * 2 + sub
                        bs = (j % 2) * 4
                        for gv in range(2):
                            for hf in range(2):
                                b = bs + gv * 2 + hf
                                for kc in range(8):
                                    mm(ps[:, b, :], w[:, gv, kc, sub * 128:(sub + 1) * 128], xnT[:, kc, hf * 512:(hf + 1) * 512],
                                       kc == 0, kc == 7, [(wk, gv), XK[kc]], PSK[b])
                        for gv in range(2):
                            b = bs + gv * 2
                            col = gv * 22 + j
                            t_ = tc[gv]
                            tk = ("tc", gv)
                            pin = ps[:, b:b + 2, :].rearrange("p b (s t) -> p (b s) t", t=slen) if nseq == 4 else \
                                ps[:, b:b + 2, :].rearrange("p (o b) t -> p o (b t)", o=1)
                            PB = [PSK[b], PSK[b + 1]]
                            S.op("act", lambda e, t_=t_, pin=pin, col=col: e.activation(out=t_, in_=pin, func=AF.Identity,
                                                                                      bias=beff[:, col:col + 1], scale=ckT[:, 1, col:col + 1]),
                                 reads=PB + ["beff", CK[1]], writes=[tk])
                            S.op("dve", lambda e, t_=t_, pin=pin, col=col: e.scalar_tensor_tensor(out=t_[:, :, 1:slen], in0=pin[:, :, 0:slen - 1],
                                                                                                scalar=ckT[:, 0, col:col + 1], in1=t_[:, :, 1:slen],
                                                                                                op0=ALU.mult, op1=ALU.add),
                                 reads=PB + [CK[0], tk], writes=[tk])
                            S.op("dve", lambda e, t_=t_, pin=pin, col=col: e.scalar_tensor_tensor(out=t_[:, :, 0:slen - 1], in0=pin[:, :, 1:slen],
                                                                                                scalar=ckT[:, 2, col:col + 1], in1=t_[:, :, 0:slen - 1],
                                                                                                op0=ALU.mult, op1=ALU.add),
                                 reads=PB + [CK[2], tk], writes=[tk])
                            S.op("dve", lambda e, t_=t_, col=col: e.tensor_scalar(out=t_[:, :, 0], in0=t_[:, :, 0], scalar1=e0[:, col:col + 1], scalar2=None, op0=ALU.add),
                                 reads=[tk, "e0"], writes=[tk])
                            S.op("dve", lambda e, t_=t_, col=col: e.tensor_scalar(out=t_[:, :, slen - 1], in0=t_[:, :, slen - 1], scalar1=e2[:, col:col + 1], scalar2=None,
                                                                                  op0=ALU.add),
                                 reads=[tk, "e2"], writes=[tk])
                        S.op("act", lambda e: e.activation(out=sg, in_=tc[0], func=AF.Silu), reads=[("tc", 0)], writes=["sg"])
                        go = gT[:, j, :].rearrange("p (s t) -> p s t", s=nseq)
                        S.op("dve", lambda e, go=go: e.tensor_tensor(out=go, in0=sg, in1=tc[1], op=ALU.mult),
                             reads=["sg", ("tc", 1)], writes=[("gT", j)])
                bi = 0
                for oc in range(8):
                    for hf in range(2):
                        b = bi % 8
                        bi += 1
                        for j in range(22):
                            mm(ps[:, b, :], wd[:, j, oc * 128:(oc + 1) * 128], gT[:, j, hf * 512:(hf + 1) * 512], j == 0, j == 21,
                               [WDK[j // 6], ("gT", j)], PSK[b])
                        resid(oc, hf, b, 40)
                AR.release(m)

            def attention(l):
                L = LAYERS[l]
                dh, H, ckv = L["dh"], L["H"], L["ckv"]
                G = H // 4
                nkc = ckv // 128
                ctot = D + 2 * ckv
                scale = dh ** -0.5
                rope = (p == 1)
                m = AR.mark()
                norm_mod(A1, "A1", 0)
                if ATT_STOP <= -1:
                    AR.release(m)
                    return
                wbig = AR.alloc([128, 8, ctot], BF16)
                qT = AR.alloc([128, 8, NT], BF16)
                kT = AR.alloc([128, nkc, NT], BF16)
                vx = AR.alloc([128, 8, 4, dh + 2], BF16)
                qn_c = AR.alloc([128, 1], F32)
                kn_c = AR.alloc([128, 1], F32)
                wo_b = AR.alloc([128, 8, D], BF16)
                esink = AR.alloc([128, 16], F32) if L["sink"] else None
                if p == 1:
                    kcT = AR.alloc([128, nkc, 512], BF16)
                    vcx = AR.alloc([128, 4, 4, dh + 2], BF16)
                tmp_mark = AR.mark()
                sqb = [AR.alloc([128, 512], BF16) for _ in range(2)]
                rs = [AR.alloc([128, 512], F32) for _ in range(2)]
                qf = [AR.alloc([128, 512], F32) for _ in range(2)]
                S.dma("pool", lambda e: e.dma_start(out=wbig[:, :, 0:768], in_=T[L["wqkv"]][:, 0:768].rearrange("(kc q) n -> q kc n", q=128)),
                      writes=[("wbig", 0)])
                S.dma("pool", lambda e: e.dma_start(out=wbig[:, :, 768:ctot], in_=T[L["wqkv"]][:, 768:ctot].rearrange("(kc q) n -> q kc n", q=128)),
                      writes=[("wbig", 1)])
                WB = [("wbig", 0), ("wbig", 1)]
                S.dma("pool", lambda e: e.dma_start(out=wo_b, in_=T[L["wo"]].rearrange("(kc q) n -> q kc n", q=128)), writes=["wo_b"])
                small_load_col(qn_c, T[L["qn"]], dh, "qn_c")
                small_load_col(kn_c, T[L["kn"]], dh, "kn_c")
                QNK = ["qn_c"]
                KNK = ["kn_c"]
                if rope:
                    cosT = AR.alloc([128, NT], F32)
                    sinT = AR.alloc([128, NT], F32)
                    ri = 0 if dh == 64 else 2
                    S.dma("sp", lambda e: e.dma_start(out=cosT, in_=T["rope"][ri]), writes=["cosT"])
                    S.dma("sp", lambda e: e.dma_start(out=sinT, in_=T["rope"][ri + 1]), writes=["sinT"])
                    pm = pa_b if dh == 64 else pc_b
                    r1 = [AR.alloc([128, 512], F32) for _ in range(2)]
                    qb16 = [AR.alloc([128, 512], BF16) for _ in range(2)]
                if L["sink"]:
                    S.dma("sp", lambda e: e.dma_start(out=tst2[0:1, 0:16], in_=T[L["sink"]].rearrange("(o q) -> o q", o=1)),
                          writes=[("tst2", 0)], key=("tst2", 0))
                    S.op("pe", lambda e: e.matmul(ps[:, 7, 0:16], ones_f[0:2, :], tst2[0:2, 0:16], start=True, stop=True),
                         reads=[("tst2", 0), "tst2z", "ones_f"], writes=[PSK[7], ("tst2", 0)])
                    S.op("act", lambda e: e.activation(out=esink, in_=ps[:, 7, 0:16], func=AF.Exp), reads=[PSK[7]], writes=["esink"])
                if p == 0:
                    kst = [AR.alloc([128, ckv], F32) for _ in range(1)]
                    vst = [AR.alloc([128, ckv], F32) for _ in range(1)]
                    kf32 = AR.alloc([128, nkc, NT], F32)
                S.op("pool", lambda e: e.memset(vx, 1.0), writes=["vx_init"])

                if ATT_STOP <= 0:
                    AR.release(m)
                    return
                for tt in range(8):
                    b = 4 + tt % 2
                    for kc in range(8):
                        mm(ps[:, b, 0:ckv], xnT[:, kc, tt * 128:(tt + 1) * 128], wbig[:, kc, D + ckv:D + 2 * ckv], kc == 0, kc == 7,
                           [XK[kc], WB[1]], PSK[b])
                    if p == 0:
                        v_ = vst[0]
                        vk = ("vst", 0)
                        S.op("dve", lambda e, b=b, v_=v_: e.tensor_copy(out=v_, in_=ps[:, b, 0:ckv]), reads=[PSK[b]], writes=[vk])
                        S.op("act", lambda e, v_=v_, tt=tt: e.activation(out=vx[:, tt, :, 0:dh], in_=v_.rearrange("p (g d) -> p g d", g=4),
                                                                        func=AF.Copy),
                             reads=[vk, "vx_init"], writes=[("vx", tt)])
                        S.dma("sp", lambda e, v_=v_, tt=tt: e.dma_start(out=T[L["nv"]][tt * 128:(tt + 1) * 128, :], in_=v_), reads=[vk])
                    else:
                        S.op("act", lambda e, b=b, tt=tt: e.activation(out=vx[:, tt, :, 0:dh], in_=ps[:, b, 0:ckv].rearrange("p (g d) -> p g d", g=4),
                                                                       func=AF.Copy),
                             reads=[PSK[b], "vx_init"], writes=[("vx", tt)])
                if ATT_STOP <= 1:
                    AR.release(m)
                    return
                def lhs_cols(kind, c):
                    if kind == "q":
                        if dh == 128:
                            return [(c * 128, 128, 0)]
                        mq, i = c // 4, c % 4
                        return [((8 * mq + i) * 64, 64, 0), ((8 * mq + 4 + i) * 64, 64, 64)]
                    return [(D + c * 128, 128, 0)]

                blocks = [(kind, c, hf) for kind, nch in (("k", nkc), ("q", 8)) for c in range(nch) for hf in range(2)]
                nb_ = len(blocks)
                sq3 = sqb + [AR.alloc([128, 512], BF16)]

                def stage_a(i):
                    kind, c, hf = blocks[i]
                    b = i % 3
                    sl = slice(hf * 512, (hf + 1) * 512)
                    for (c0, w_, po) in lhs_cols(kind, c):
                        for kc in range(8):
                            mm(ps[po:po + w_, b, :], wbig[:, kc, c0:c0 + w_], xnT[:, kc, sl], kc == 0, kc == 7,
                               [XK[kc], WB[0], WB[1]], PSK[b])
                    sq_ = sq3[i % 3]
                    S.op("act", lambda e, sq_=sq_, b=b: e.activation(out=sq_, in_=ps[:, b, :], func=AF.Square), reads=[PSK[b]], writes=[("sqb", i % 3)])

                def stage_b(i):
                    kind, c, hf = blocks[i]
                    b = i % 3
                    b2 = 3 + i % 2
                    sl = slice(hf * 512, (hf + 1) * 512)
                    sq_ = sq3[i % 3]
                    mm(ps[:, b2, :], bd64_b if dh == 64 else ones_b, sq_, True, True, [("sqb", i % 3), "cst_misc", "ones_b"], PSK[b2])
                    rs_ = rs[i % 2]
                    rk = ("rs", i % 2)
                    S.op("act", lambda e, b2=b2: e.activation(out=lnv, in_=ps[:, b2, :], func=AF.Ln, bias=EPS, scale=1.0 / dh),
                         reads=[PSK[b2]], writes=["lnv"])
                    S.op("act", lambda e, rs_=rs_: e.activation(out=rs_, in_=lnv, func=AF.Exp, scale=-0.5), reads=["lnv"], writes=[rk])
                    gcol, gk = (qn_c, QNK) if kind == "q" else (kn_c, KNK)
                    dstT = qT if kind == "q" else kT
                    dk = (kind + "T", c, hf)
                    if not rope:
                        if kind == "k":
                            S.op("dve", lambda e, c=c, sl=sl, b=b, rs_=rs_: e.scalar_tensor_tensor(
                                out=kf32[:, c, sl], in0=ps[:, b, :], scalar=kn_c[:, 0:1], in1=rs_, op0=ALU.mult, op1=ALU.mult),
                                reads=[PSK[b], rk] + gk, writes=[("kf32", c, hf)])
                            S.op("pool", lambda e, c=c, sl=sl: e.tensor_copy(out=kT[:, c, sl], in_=kf32[:, c, sl]),
                                 reads=[("kf32", c, hf)], writes=[dk])
                        else:
                            S.op("dve", lambda e, c=c, sl=sl, b=b, rs_=rs_, gcol=gcol, dstT=dstT: e.scalar_tensor_tensor(
                                out=dstT[:, c, sl], in0=ps[:, b, :], scalar=gcol[:, 0:1], in1=rs_, op0=ALU.mult, op1=ALU.mult),
                                reads=[PSK[b], rk] + gk, writes=[dk])
                    else:
                        q_, qk_ = qf[i % 2], ("qf", i % 2)
                        qb_, qbk = qb16[i % 2], ("qb16", i % 2)
                        S.op("dve", lambda e, q_=q_, b=b, rs_=rs_, gcol=gcol: e.scalar_tensor_tensor(
                            out=q_, in0=ps[:, b, :], scalar=gcol[:, 0:1], in1=rs_, op0=ALU.mult, op1=ALU.mult),
                            reads=[PSK[b], rk] + gk, writes=[qk_])
                        S.op("pool", lambda e, q_=q_, qb_=qb_: e.tensor_copy(out=qb_, in_=q_), reads=[qk_], writes=[qbk])

                def stage_c(i):
                    kind, c, hf = blocks[i]
                    b3 = 5 + i % 3
                    sl = slice(hf * 512, (hf + 1) * 512)
                    dstT = qT if kind == "q" else kT
                    dk = (kind + "T", c, hf)
                    q_, qk_ = qf[i % 2], ("qf", i % 2)
                    qb_, qbk = qb16[i % 2], ("qb16", i % 2)
                    r_, r1k = r1[i % 2], ("r1", i % 2)
                    mm(ps[:, b3, :], pm, qb_, True, True, [qbk, "cst_misc"], PSK[b3])
                    S.op("dve", lambda e, r_=r_, b3=b3, sl=sl: e.tensor_tensor(out=r_, in0=ps[:, b3, :], in1=sinT[:, sl], op=ALU.mult),
                         reads=[PSK[b3], "sinT"], writes=[r1k])
                    S.op("pool", lambda e, q_=q_, sl=sl: e.tensor_tensor(out=q_, in0=q_, in1=cosT[:, sl], op=ALU.mult),
                         reads=[qk_, "cosT"], writes=[qk_])
                    S.op("dve", lambda e, q_=q_, r_=r_, c=c, sl=sl, dstT=dstT: e.tensor_tensor(out=dstT[:, c, sl], in0=q_, in1=r_, op=ALU.add),
                         reads=[qk_, r1k], writes=[dk])

                for step in range(nb_ + 2):
                    if step < nb_:
                        stage_a(step)
                    if 0 <= step - 1 < nb_:
                        stage_b(step - 1)
                    if rope and 0 <= step - 2 < nb_:
                        stage_c(step - 2)
                if ATT_STOP <= 2:
                    AR.release(m)
                    return
                if p == 0:
                    for tt in range(8):
                        b = 6 + tt % 2
                        for c in range(nkc):
                            S.op("pe", lambda e, b=b, c=c, tt=tt: e.transpose(ps[:, b, c * 128:(c + 1) * 128], kf32[:, c, tt * 128:(tt + 1) * 128], ident_f),
                                 reads=[("kf32", c, tt // 4), "ident_f"], writes=[PSK[b]], acc=True)
                        k_ = kst[0]
                        kk = ("kst", 0)
                        S.op("dve", lambda e, b=b, k_=k_: e.tensor_copy(out=k_, in_=ps[:, b, 0:ckv]), reads=[PSK[b]], writes=[kk])
                        S.dma("sp", lambda e, k_=k_, tt=tt: e.dma_start(out=T[L["nk"]][tt * 128:(tt + 1) * 128, :], in_=k_), reads=[kk])

                if p == 1:
                    cst_ = [AR.alloc([128, ckv], F32) for _ in range(1)]
                    cb_ = [AR.alloc([128, ckv], BF16) for _ in range(1)]
                    S.op("pool", lambda e: e.memset(vcx, 1.0), writes=["vcx_init"])
                    for t4 in range(4):
                        c_ = cst_[0]
                        ck_ = ("cst_", 0)
                        S.dma("sp", lambda e, c_=c_, t4=t4: e.dma_start(out=c_, in_=T[L["cv"]][t4 * 128:(t4 + 1) * 128, :]), writes=[ck_])
                        S.op("dve", lambda e, c_=c_, t4=t4: e.tensor_copy(out=vcx[:, t4, :, 0:dh], in_=c_.rearrange("p (g d) -> p g d", g=4)),
                             reads=[ck_, "vcx_init"], writes=[("vcx", t4)])
                    for t4 in range(4):
                        c_ = cst_[0]
                        ck_ = ("cst_", 0)
                        bb = cb_[0]
                        bk = ("cb_", 0)
                        S.dma("sp", lambda e, c_=c_, t4=t4: e.dma_start(out=c_, in_=T[L["ck"]][t4 * 128:(t4 + 1) * 128, :]), writes=[ck_])
                        S.op("dve", lambda e, c_=c_, bb=bb: e.tensor_copy(out=bb, in_=c_), reads=[ck_], writes=[bk])
                        b = 6 + t4 % 2
                        pb = ps[:, b, 0:64 * nkc].bitcast(BF16).rearrange("p (c t) -> p c t", c=nkc)
                        for c in range(nkc):
                            S.op("pe", lambda e, pb=pb, c=c, bb=bb: e.transpose(pb[:, c, :], bb[:, c * 128:(c + 1) * 128], ident_b),
                                 reads=[bk, "ident_b"], writes=[PSK[b]], acc=True)
                        S.op("dve", lambda e, pb=pb, t4=t4: e.tensor_copy(out=kcT[:, :, t4 * 128:(t4 + 1) * 128], in_=pb),
                             reads=[PSK[b]], writes=[("kcT", t4)])

                if ATT_STOP <= 3:
                    AR.release(m)
                    return
                nkt_max = 7 if L["kind"] == "A" else 12
                S.barrier()
                AR.release(tmp_mark)
                PT = [AR.alloc([128, nkt_max, G * 128], BF16) for _ in range(2)]
                otok = [AR.alloc([128, D], BF16) for _ in range(2)]
                OT = xnT
                den = AR.alloc([128, 4], F32)
                rden = AR.alloc([128, 4], F32)
                sbc = [0]

                def key_tiles(qt):
                    s_ = qt // (slen // 128)
                    kts = []
                    if p == 1:
                        kts += [("c", t4, None) for t4 in range(4)]
                        if L["kind"] == "A":
                            if qt > 0:
                                kts.append(("n", qt - 1, mlo_b))
                            kts.append(("n", qt, None))
                            if qt < 7:
                                kts.append(("n", qt + 1, mhi_b))
                        else:
                            kts += [("n", t8, None) for t8 in range(8)]
                    else:
                        kts += [("n", s_ * 2 + t2, None) for t2 in range(2)]
                    return kts

                def emit_scores(qt, g, it):
                    kts = key_tiles(qt)
                    pt = PT[it % 2]
                    ptk = ("PT", it % 2)
                    if dh == 64:
                        mq, e2_ = g // 2, g % 2
                        prt = slice(64 * e2_, 64 * e2_ + 64)
                        rhs = qT[prt, 4 * mq:4 * mq + 4, qt * 128:(qt + 1) * 128]
                        qreads = [("qT", 4 * mq + i, qt // 4) for i in range(4)]
                        kch = mq
                    else:
                        prt = slice(0, 128)
                        rhs = qT[:, 2 * g:2 * g + 2, qt * 128:(qt + 1) * 128]
                        qreads = [("qT", 2 * g + i, qt // 4) for i in range(2)]
                        kch = g
                    for ki, (src, ti, msk) in enumerate(kts):
                        b = sbc[0] % 4
                        sbc[0] += 1
                        if src == "c":
                            lhsT = kcT[prt, kch, ti * 128:(ti + 1) * 128]
                            kr = [("kcT", ti)]
                        else:
                            lhsT = kT[prt, kch, ti * 128:(ti + 1) * 128]
                            kr = [("kT", kch, ti // 4)]
                        mm(ps[:, b, 0:G * 128], lhsT, rhs, True, True, kr + qreads, PSK[b])
                        S.op("act", lambda e, pt=pt, ki=ki, b=b: e.activation(out=pt[:, ki, :], in_=ps[:, b, 0:G * 128], func=AF.Exp, scale=scale),
                             reads=[PSK[b]], writes=[(ptk, ki)])
                        if msk is not None:
                            S.op("pool", lambda e, pt=pt, ki=ki, msk=msk: e.tensor_tensor(
                                out=pt[:, ki, :], in0=pt[:, ki, :], in1=msk[:, 0:G, :].rearrange("p g q -> p (g q)"), op=ALU.mult),
                                reads=[(ptk, ki), "cst_misc"], writes=[(ptk, ki)])

                def emit_pv(qt, g, it):
                    kts = key_tiles(qt)
                    pt = PT[it % 2]
                    ptk = ("PT", it % 2)
                    ob = 4 + it % 2
                    ot = otok[qt % 2]
                    otk = ("otok", qt % 2)
                    opv = ps[:, ob, 0:G * (dh + 1)].rearrange("p (i d) -> p i d", i=G)
                    for i in range(G):
                        for ki, (src, ti, msk) in enumerate(kts):
                            if src == "c":
                                vr = vcx[:, ti, g, 0:dh + 1]
                                vk = [("vcx", ti)]
                            else:
                                vr = vx[:, ti, g, 0:dh + 1]
                                vk = [("vx", ti)]
                            mm(opv[:, i, :], pt[:, ki, i * 128:(i + 1) * 128], vr, ki == 0, ki == len(kts) - 1,
                               [(ptk, ki)] + vk, PSK[ob])
                    if esink is not None:
                        S.op("dve", lambda e, opv=opv, g=g: e.tensor_tensor(out=den[:, 0:G], in0=opv[:, :, dh], in1=esink[:, g * G:(g + 1) * G], op=ALU.add),
                             reads=[PSK[ob], "esink"], writes=["den"])
                        S.op("dve", lambda e: e.reciprocal(out=rden[:, 0:G], in_=den[:, 0:G]), reads=["den"], writes=["rden"])
                    else:
                        S.op("dve", lambda e, opv=opv: e.reciprocal(out=rden[:, 0:G], in_=opv[:, :, dh]), reads=[PSK[ob]], writes=["rden"])
                    for i in range(G):
                        h = g * G + i
                        S.op("dve", lambda e, opv=opv, i=i, h=h, ot=ot: e.tensor_scalar(out=ot[:, h * dh:(h + 1) * dh], in0=opv[:, i, 0:dh],
                                                                                         scalar1=rden[:, i:i + 1], scalar2=None, op0=ALU.mult),
                             reads=[PSK[ob], "rden"], writes=[(otk, h)])
                    if g == 3:
                        b = 6 + qt % 2
                        pb = ps[:, b, :].bitcast(BF16).rearrange("p (c t) -> p c t", c=8)
                        for c in range(8):
                            S.op("pe", lambda e, pb=pb, c=c, ot=ot: e.transpose(pb[:, c, :], ot[:, c * 128:(c + 1) * 128], ident_b),
                                 reads=[(otk, h) for h in range(H)] + ["ident_b"], writes=[PSK[b]], acc=True)
                        S.op("act", lambda e, pb=pb, qt=qt: e.activation(out=OT[:, :, qt * 128:(qt + 1) * 128], in_=pb, func=AF.Copy),
                             reads=[PSK[b]], writes=[("OT", qt)] + XK)

                items = [(qt, g) for qt in range(8) for g in range(4)]
                emit_scores(items[0][0], items[0][1], 0)
                for it, (qt, g) in enumerate(items):
                    if it + 1 < len(items):
                        emit_scores(items[it + 1][0], items[it + 1][1], it + 1)
                    emit_pv(qt, g, it)
                bi = 0
                for oc in range(8):
                    for hf in range(2):
                        b = bi % 4
                        bi += 1
                        for kc in range(8):
                            mm(ps[:, b, :], wo_b[:, kc, oc * 128:(oc + 1) * 128], OT[:, kc, hf * 512:(hf + 1) * 512], kc == 0, kc == 7,
                               ["wo_b"] + [("OT", hf * 4 + q) for q in range(4)], PSK[b])
                        resid(oc, hf, b, 16)
                AR.release(m)

            def s5_layer(l):
                CH = 128 // nseq
                m = AR.mark()
                norm_mod(A1, "A1", 0, perm8=True)
                U = AR.alloc([128, 64, 128], BF16)
                TB = AR.alloc([128, 2, 64, 128], BF16)
                SS = AR.alloc([128, 2, 64, 128], F32)
                ALL = AR.alloc([128, 2, 64], F32)
                ALS = AR.alloc([128, 2, 64], F32)
                xf = xnT.rearrange("q a n -> q (a n)").bitcast(F32)
                H0v = xf[:, 0:128 * nseq].rearrange("q (r g s) -> q r g s", r=2, g=64)
                L8 = AR.alloc([128, 2, 128], F32)
                FS = AR.alloc([128, 2, 64], F32)
                FSo = AR.alloc([128, 2, 128], F32)
                tt_ = [[xf[:, 128 * nseq * (1 + 2 * a_ + b_):128 * nseq * (2 + 2 * a_ + b_)].rearrange("q (r g s) -> q r g s", r=2, g=64)
                        for b_ in range(2)] for a_ in range(2)]
                S.dma("sp", lambda e: e.dma_start(out=T["D1"][p].rearrange("(fc q) n -> q fc n", q=128), in_=xnT), reads=XK, writes=["D1"])
                d1v = T["D1"][p].rearrange("(g c) (t k) -> t c g k", c=16, t=8)
                for t in range(8):
                    S.dma("sp", lambda e, t=t: e.dma_start(out=U[16 * t:16 * (t + 1)], in_=d1v[t]), reads=["D1"], writes=[("U", t)])
                UK = [("U", t) for t in range(8)]
                for r_ in range(2):
                    for g4 in range(4):
                        S.dma("sp", lambda e, r_=r_, g4=g4: e.dma_start(out=TB[:, r_, g4 * 16:(g4 + 1) * 16, :],
                                                                         in_=T["TS"][r_, g4 * 16:(g4 + 1) * 16].rearrange("g k n -> k g n")),
                              writes=[("TB", r_, g4)])
                S.dma("sp", lambda e: e.dma_start(out=L8[0:64], in_=T["TP8"].rearrange("r (d g) p -> g r d p", d=2)), writes=["L8"])
                for r_ in range(2):
                    S.op("pe", lambda e, r_=r_: e.transpose(ps[:, 7, 0:64], L8[0:64, r_, :], ident_f[0:64, 0:64]), reads=["L8", "ident_f"], writes=[PSK[7]])
                    if r_ == 0:
                        S.op("dve", lambda e: e.tensor_copy(out=ALL[:, 0, :], in_=ps[:, 7, 0:64]), reads=[PSK[7]], writes=["AL"])
                        S.op("dve", lambda e: e.tensor_copy(out=ALL[:, 1, :], in_=ps[:, 7, 0:64]), reads=[PSK[7]], writes=["AL"])
                    else:
                        S.op("dve", lambda e: e.tensor_scalar(out=ALS[:, 0, :], in0=ps[:, 7, 0:64], scalar1=-1.0, scalar2=None, op0=ALU.mult), reads=[PSK[7]], writes=["AL"])
                        S.op("dve", lambda e: e.tensor_copy(out=ALS[:, 1, :], in_=ps[:, 7, 0:64]), reads=[PSK[7]], writes=["AL"])
                if p == 1:
                    for r_ in range(2):
                        S.dma("sp", lambda e, r_=r_: e.dma_start(out=L8[0:64, r_, :].rearrange("g (d p) -> g d p", d=2),
                                                                  in_=T["st1"][:, r_].rearrange("d g p -> g d p")), reads=["L8"], writes=["L8"], key=("L8", r_))
                    for r_ in range(2):
                        S.op("pe", lambda e, r_=r_: e.transpose(ps[:, 7, 0:64], L8[0:64, r_, :], ident_f[0:64, 0:64]), reads=["L8", "ident_f"], writes=[PSK[7]])
                        S.op("dve", lambda e, r_=r_: e.tensor_copy(out=H0v[:, r_, :, 0], in_=ps[:, 7, 0:64]), reads=[PSK[7]], writes=["H0"] + XK)
                else:
                    S.op("dve", lambda e: e.memset(H0v, 0.0), writes=["H0"] + XK)
                for gq in range(16):
                    for r_ in range(2):
                        b = (gq * 2 + r_) % 4
                        for gi in range(4):
                            g = gq * 4 + gi
                            S.op("pe", lambda e, b=b, gi=gi, g=g, r_=r_: e.matmul(ps[:, b, gi * 128:(gi + 1) * 128], TB[:, r_, g, :], U[:, g, :], start=True, stop=True),
                                 reads=UK + [("TB", r_, g // 16)], writes=[PSK[b]], acc=True)
                        eng = "act" if r_ == 0 else "dve"
                        if r_ == 0:
                            S.op("act", lambda e, b=b, gq=gq: e.activation(out=SS[:, 0, gq * 4:(gq + 1) * 4, :].rearrange("q g k -> q (g k)"), in_=ps[:, b, :], func=AF.Copy),
                                 reads=[PSK[b]], writes=[("SS", 0)])
                        else:
                            S.op("dve", lambda e, b=b, gq=gq: e.tensor_copy(out=SS[:, 1, gq * 4:(gq + 1) * 4, :].rearrange("q g k -> q (g k)"), in_=ps[:, b, :]),
                                 reads=[PSK[b]], writes=[("SS", 1)])
                S.barrier()
                NS, CHV = nseq, CH
                SSv = SS.rearrange("q r g (s k) -> q r g s k", k=CHV)
                for d, eng in ((0, "dve"), (1, "pool")):
                    hs = slice(64 * d, 64 * d + 64)
                    order = list(range(CHV)) if d == 0 else list(range(CHV - 1, -1, -1))
                    t1, t2 = tt_[d]
                    k1, k2, kS = ("rt1", d), ("rt2", d), ("SSd", d)
                    all_ = ALL[hs].unsqueeze(3).broadcast_to([64, 2, 64, NS])
                    als_ = ALS[hs].unsqueeze(3).broadcast_to([64, 2, 64, NS])
                    for ii, kk in enumerate(order):
                        if ii == 0:
                            prev, prev_sw = H0v[hs], H0v[hs, ::-1]
                        else:
                            kp = order[ii - 1]
                            prev, prev_sw = SSv[hs, :, :, :, kp], SSv[hs, ::-1, :, :, kp]
                        cur = SSv[hs, :, :, :, kk]
                        S.op(eng, lambda e, hs=hs, t1=t1, prev=prev, all_=all_: e.tensor_tensor(out=t1[hs], in0=prev, in1=all_, op=ALU.mult),
                             reads=[kS, "AL", "H0"], writes=[k1])
                        S.op(eng, lambda e, hs=hs, t2=t2, prev_sw=prev_sw, als_=als_: e.tensor_tensor(out=t2[hs], in0=prev_sw, in1=als_, op=ALU.mult),
                             reads=[kS, "AL", "H0"], writes=[k2])
                        S.op(eng, lambda e, hs=hs, t1=t1, t2=t2: e.tensor_tensor(out=t1[hs], in0=t1[hs], in1=t2[hs], op=ALU.add), reads=[k1, k2], writes=[k1])
                        S.op(eng, lambda e, hs=hs, t1=t1, cur=cur: e.tensor_tensor(out=cur, in0=cur, in1=t1[hs], op=ALU.add), reads=[kS, k1], writes=[kS])
                S.barrier()
                if p == 0:
                    for s_ in range(nseq):
                        for r_ in range(2):
                            S.op("dve", lambda e, r_=r_, s_=s_: e.tensor_copy(out=FS[0:64, r_, :], in_=SS[0:64, r_, :, s_ * CH + CH - 1]), reads=[], writes=["FS"])
                            S.op("dve", lambda e, r_=r_, s_=s_: e.tensor_copy(out=FS[64:128, r_, :], in_=SS[64:128, r_, :, s_ * CH]), reads=[], writes=["FS"])
                            S.op("pe", lambda e, r_=r_: e.transpose(ps[0:64, 7, r_ * 128:(r_ + 1) * 128], FS[:, r_, :], ident_f), reads=["FS", "ident_f"], writes=[PSK[7]], acc=True)
                        S.op("dve", lambda e: e.tensor_copy(out=FSo[0:64].rearrange("g r n -> g (r n)"), in_=ps[0:64, 7, 0:256]), reads=[PSK[7]], writes=["FSo"])
                        for r_ in range(2):
                            S.dma("sp", lambda e, s_=s_, r_=r_: e.dma_start(out=T["nst"][s_][:, r_].rearrange("d g p -> g d p"),
                                                                           in_=FSo[0:64, r_, :].rearrange("g (d p) -> g d p", d=2)), reads=["FSo"], key=("nst", r_))
                for r_ in range(2):
                    for s_ in range(nseq):
                        k0 = s_ * CH
                        S.op("act", lambda e, r_=r_, k0=k0: e.activation(out=TB[0:64, r_, :, k0 + 1:k0 + CH], in_=SS[0:64, r_, :, k0:k0 + CH - 1], func=AF.Copy),
                             reads=[], writes=["Hin"])
                        S.op("act", lambda e, r_=r_, k0=k0, s_=s_: e.activation(out=TB[0:64, r_, :, k0], in_=H0v[0:64, r_, :, s_], func=AF.Copy), reads=["H0"], writes=["Hin"])
                        S.op("dve", lambda e, r_=r_, k0=k0: e.tensor_copy(out=TB[64:128, r_, :, k0:k0 + CH - 1], in_=SS[64:128, r_, :, k0 + 1:k0 + CH]),
                             reads=[], writes=["Hin"])
                        S.op("dve", lambda e, r_=r_, k0=k0, s_=s_: e.tensor_copy(out=TB[64:128, r_, :, k0 + CH - 1], in_=H0v[64:128, r_, :, s_]), reads=["H0"], writes=["Hin"])
                S.barrier()
                SSb = SS.rearrange("q r g k -> q (r g k)").bitcast(BF16)
                EL = SSb[:, 0:16384].rearrange("q (r g n) -> q r g n", r=2, g=64)
                ML = SSb[:, 16384:24576].rearrange("q (g n) -> q g n", g=64)
                YC = SSb[:, 24576:32768].rearrange("q (g n) -> q g n", g=64)
                for r_ in range(2):
                    for g4 in range(4):
                        S.dma("sp", lambda e, r_=r_, g4=g4: e.dma_start(out=EL[:, r_, g4 * 16:(g4 + 1) * 16, :],
                                                                         in_=T["TE"][r_, g4 * 16:(g4 + 1) * 16].rearrange("g k n -> k g n")),
                              writes=[("EL", r_, g4)])
                for g4 in range(4):
                    S.dma("sp", lambda e, g4=g4: e.dma_start(out=ML[:, g4 * 16:(g4 + 1) * 16, :], in_=T["TM"][g4 * 16:(g4 + 1) * 16].rearrange("g k n -> k g n")),
                          writes=[("ML", g4)])
                gu = [AR.alloc([128, 512], F32) for _ in range(2)]
                gs = [AR.alloc([128, 512], F32) for _ in range(2)]
                for gq in range(16):
                    b = gq % 4
                    for gi in range(4):
                        g = gq * 4 + gi
                        o_ = ps[:, b, gi * 128:(gi + 1) * 128]
                        S.op("pe", lambda e, o_=o_, g=g: e.matmul(o_, ML[:, g, :], U[:, g, :], start=True, stop=False), reads=[("ML", g // 16)] + UK, writes=[PSK[b]], acc=True)
                        S.op("pe", lambda e, o_=o_, g=g: e.matmul(o_, EL[:, 0, g, :], TB[:, 0, g, :], start=False, stop=False), reads=[("EL", 0, g // 16), "Hin"], writes=[PSK[b]], acc=True)
                        S.op("pe", lambda e, o_=o_, g=g: e.matmul(o_, EL[:, 1, g, :], TB[:, 1, g, :], start=False, stop=True), reads=[("EL", 1, g // 16), "Hin"], writes=[PSK[b]], acc=True)
                    u_, s2 = gu[gq % 2], gs[gq % 2]
                    uk, sk = ("gu", gq % 2), ("gs", gq % 2)
                    S.op("act", lambda e, u_=u_, b=b: e.activation(out=u_, in_=ps[:, b, :], func=AF.Square), reads=[PSK[b]], writes=[uk])
                    S.op("dve", lambda e, u_=u_: e.tensor_scalar(out=u_, in0=u_, scalar1=0.044715, scalar2=1.0, op0=ALU.mult, op1=ALU.add), reads=[uk], writes=[uk])
                    S.op("dve", lambda e, u_=u_, b=b: e.tensor_tensor(out=u_, in0=u_, in1=ps[:, b, :], op=ALU.mult), reads=[uk, PSK[b]], writes=[uk])
                    S.op("act", lambda e, u_=u_, s2=s2: e.activation(out=s2, in_=u_, func=AF.Sigmoid, scale=1.5957691216057308), reads=[uk], writes=[sk])
                    S.op("dve", lambda e, s2=s2, b=b, gq=gq: e.tensor_tensor(out=YC[:, gq * 4:(gq + 1) * 4, :].rearrange("q g k -> q (g k)"), in0=s2, in1=ps[:, b, :], op=ALU.mult),
                         reads=[sk, PSK[b]], writes=["YC"])
                d2v = T["D2"][p].rearrange("(g c) (t k) -> t c g k", c=16, t=8)
                for t in range(8):
                    S.dma("sp", lambda e, t=t: e.dma_start(out=d2v[t], in_=YC[16 * t:16 * (t + 1)]), reads=["YC"], writes=["D2"], key=("D2w", t))
                S.dma("sp", lambda e: e.dma_start(out=xnT, in_=T["D2"][p].rearrange("(fc q) n -> q fc n", q=128)), reads=["D2"], writes=XK, key="D2r")
                S.barrier()
                AR.release(m)
                m = AR.mark()
                wg = AR.alloc([128, 8, 2 * D], BF16)
                sg2 = [AR.alloc([128, 512], F32) for _ in range(2)]
                mx = [AR.alloc([128, 512], F32) for _ in range(2)]
                for h2 in range(2):
                    S.dma("pool", lambda e, h2=h2: e.dma_start(out=wg[:, :, h2 * D:(h2 + 1) * D],
                                                               in_=T["l1_w_glu"][:, h2 * D:(h2 + 1) * D].rearrange("(kc q) n -> q kc n", q=128)),
                          writes=[("wg", h2)])
                it = 0
                for oc in range(8):
                    for hf in range(2):
                        bA, bB = (it % 2) * 2, (it % 2) * 2 + 1
                        s_, x_ = sg2[it % 2], mx[it % 2]
                        sk, xk_ = ("sg2", it % 2), ("mx", it % 2)
                        it += 1
                        for kc in range(8):
                            mm(ps[:, bA, :], wg[:, kc, oc * 128:(oc + 1) * 128], xnT[:, kc, hf * 512:(hf + 1) * 512], kc == 0, kc == 7, [("wg", 0), XK[kc]], PSK[bA])
                        for kc in range(8):
                            mm(ps[:, bB, :], wg[:, kc, D + oc * 128:D + (oc + 1) * 128], xnT[:, kc, hf * 512:(hf + 1) * 512], kc == 0, kc == 7, [("wg", 1), XK[kc]], PSK[bB])
                        S.op("act", lambda e, s_=s_, bB=bB: e.activation(out=s_, in_=ps[:, bB, :], func=AF.Sigmoid), reads=[PSK[bB]], writes=[sk])
                        S.op("dve", lambda e, s_=s_, x_=x_, bA=bA: e.tensor_tensor(out=x_, in0=ps[:, bA, :], in1=s_, op=ALU.mult), reads=[PSK[bA], sk], writes=[xk_])
                        hv = hT[:, oc, :].rearrange("q (k t) -> q t k", t=8)[:, 4 * hf:4 * hf + 4, :]
                        S.op("dve", lambda e, x_=x_, hv=hv, oc=oc: e.scalar_tensor_tensor(out=hv, in0=x_.rearrange("q (t k) -> q t k", t=4), scalar=modT[:, 16 + oc:17 + oc],
                                                                                        in1=hv, op0=ALU.mult, op1=ALU.add),
                             reads=[xk_, "modT", HK[oc]], writes=[HK[oc]])
                AR.release(m)

            for l in range(NLAYERS):
                if STAGE & 1:
                    compute_mod(l)
                S.barrier()
                if STAGE & 2:
                    if LAYERS[l]["kind"] == "B":
                        if not SKIP_S5:
                            s5_layer(l)
                    else:
                        attention(l)
                S.barrier()
                if STAGE & 4:
                    ffn(l)
                S.barrier()

            m0 = AR.mark()
            yst = [AR.alloc([128, D], F32) for _ in range(2)]
            for tt in range(8):
                y_ = yst[tt % 2]
                yk = ("yst", tt % 2)
                for hf in range(2):
                    b = (tt * 2 + hf) % 8
                    for q in range(4):
                        kc = hf * 4 + q
                        S.op("pe", lambda e, b=b, q=q, kc=kc, tt=tt: e.transpose(ps[:, b, q * 128:(q + 1) * 128], hT[:, kc, tt * 128:(tt + 1) * 128], ident_f),
                             reads=[HK[kc], "ident_f"], writes=[PSK[b]], acc=True)
                    if hf:
                        S.op("act", lambda e, b=b, y_=y_: e.activation(out=y_[:, 512:1024], in_=ps[:, b, :], func=AF.Copy), reads=[PSK[b]], writes=[(yk, 1)])
                    else:
                        S.op("dve", lambda e, b=b, y_=y_: e.tensor_copy(out=y_[:, 0:512], in_=ps[:, b, :]), reads=[PSK[b]], writes=[(yk, 0)])
                S.dma("sp", lambda e, y_=y_, tt=tt: e.dma_start(out=T["y"][p, tt * 128:(tt + 1) * 128, :], in_=y_), reads=[(yk, 0), (yk, 1)], key=yk)
            S.barrier()
            AR.release(m0)

        if not SKIP_S5 and NLAYERS > 1 and (STAGE & 2):
            s5_tables()
        run_path(0)
        run_path(1)
        S.barrier()
        print('n_sems', len(S.sem_objs), 'insts', {e: len(S.prog[e]) for e in ENGS})
        S.emit()
    return nc


def _consts():
    cst = np.zeros((9, 128, 128), np.float32)
    for tt_ in range(8):
        for cc_ in range(16):
            cst[CST_R16, cc_, tt_ * 16 + cc_] = 1.0
    cst[CST_IDENT] = np.eye(128)
    bd = np.zeros((128, 128), np.float32)
    bd[:64, :64] = 1
    bd[64:, 64:] = 1
    cst[CST_BD64] = bd
    kk = np.arange(128)[:, None]
    qq = np.arange(128)[None, :]
    cst[CST_MLO] = (qq <= kk)
    cst[CST_MHI] = (kk <= qq)

    def perm(dh):
        qt = dh // 4
        half = dh // 2
        Pm = np.zeros((128, 128), np.float32)
        for m in range(128):
            d = m % dh
            if (d % half) < qt:
                Pm[m, m + qt] = -1.0
            else:
                Pm[m, m - qt] = 1.0
        return Pm.T.copy()

    cst[CST_PA] = perm(64)
    cst[CST_PC] = perm(128)
    t1 = np.arange(8)
    tau_p = np.repeat(t1, 16)[:, None]
    tau = np.repeat(t1, 16)[None, :]
    cst[CST_S5F] = (tau >= tau_p)
    cst[CST_S5B] = (tau <= tau_p)
    rope = np.zeros((4, 128, NT), np.float32)
    t = np.arange(NT)
    row = (t // 64).astype(np.float32)
    col = (t % 64).astype(np.float32)
    for idx, dh in ((0, 64), (2, 128)):
        half, qt = dh // 2, dh // 4
        freqs = (1.0 / (np.float32(10000.0) ** (np.arange(qt, dtype=np.float32) / np.float32(qt)))).astype(np.float32)
        for pp in range(128):
            d = pp % dh
            pos = row if d < half else col
            ang = (pos * freqs[d % qt]).astype(np.float32)
            rope[idx, pp] = np.cos(ang)
            rope[idx + 1, pp] = np.sin(ang)
    return cst, rope


_PROG = {}


def kernel(**inp):
    f32 = lambda a: np.ascontiguousarray(np.asarray(a, dtype=np.float32))
    inp = {k: f32(v) for k, v in inp.items()}
    if "nc" not in _PROG:
        _PROG["nc"] = build_program()
    nc = _PROG["nc"]
    cst, rope = _consts()
    shared = {}
    for name, shape in IN_SPECS:
        if name in inp:
            shared[name] = inp[name].reshape(shape)
    shared["cst"] = cst
    shared["rope"] = rope
    in_maps = []
    for c in range(8):
        m = dict(shared)
        m["xin"] = np.stack([inp["x_prompt"][4 * c:4 * c + 4].reshape(NT, D), inp["x_sample"][c]])
        m["cond"] = np.stack([inp["c_ctx"], inp["c"][c]])
        m["ck0"] = inp["cache_l0_k"][c].reshape(512, 256)
        m["cv0"] = inp["cache_l0_v"][c].reshape(512, 256)
        m["st1"] = inp["state_l1"][c]
        m["ck2"] = inp["cache_l2_k"][c].reshape(512, 512)
        m["cv2"] = inp["cache_l2_v"][c].reshape(512, 512)
        m["ck3"] = inp["cache_l3_k"][c].reshape(512, 256)
        m["cv3"] = inp["cache_l3_v"][c].reshape(512, 256)
        in_maps.append({k: np.ascontiguousarray(v) for k, v in m.items()})
    res = run_bass_kernel_spmd(nc, in_maps, core_ids=list(range(8)))
    R = res.results
    y_prompt = np.concatenate([R[c]["y"][0].reshape(4, 256, D) for c in range(8)], 0)
    y_sample = np.stack([R[c]["y"][1] for c in range(8)], 0)
    cat = lambda k, shp: np.concatenate([R[c][k].reshape(shp) for c in range(8)], 0)
    out = (y_prompt, y_sample,
           cat("nk0", (4, 256, 4, 64)), cat("nv0", (4, 256, 4, 64)),
           cat("nst", (4, 2, 2, 64, 64)),
           cat("nk2", (4, 256, 4, 128)), cat("nv2", (4, 256, 4, 128)),
           cat("nk3", (4, 256, 4, 64)), cat("nv3", (4, 256, 4, 64)))
    return tuple(np.ascontiguousarray(o, dtype=np.float32) for o in out)
```

```python
import math
import numpy as np
from contextlib import ExitStack
import concourse.bass as bass
import concourse.mybir as mybir
from concourse.bass_utils import run_bass_kernel_spmd

F32 = mybir.dt.float32
BF16 = mybir.dt.bfloat16
I32 = mybir.dt.int32
U8 = mybir.dt.uint8
AF = mybir.ActivationFunctionType
ALU = mybir.AluOpType
AX = mybir.AxisListType

D = 1024
DFF = 2816
NT = 1024
ENGS = ["pe", "act", "dve", "pool", "sp"]
EPS = 1e-6
NLAYERS = 4
SKIP_S5 = False
DEBUG_H = False
STAGE = 7
ATT_STOP = 9


class Sched:
    def __init__(self, nc, stack):
        self.nc = nc
        self.stack = stack
        self.prog = {e: [] for e in ENGS}
        self.cnt = {e: 0 for e in ENGS}
        self.seen = {e: {} for e in ENGS}
        self.lastw = {}
        self.readers = {}
        self.sem_objs = []
        self.esid = {}
        for e in ENGS:
            self.esid[e] = self._new_sem("s_" + e)
        self.dsem = {}
        self.dfree = []

    def _new_sem(self, name):
        self.sem_objs.append(self.stack.enter_context(self.nc.semaphore(name)))
        return len(self.sem_objs) - 1

    def _dma_sem(self, key):
        if key not in self.dsem:
            if self.dfree:
                self.dsem[key] = self.dfree.pop()
            else:
                self.dsem[key] = [self._new_sem("d%d" % len(self.sem_objs)), 0]
        return self.dsem[key]

    def _deps(self, eng, reads, writes, acc=False):
        need = {}

        def add(rec, same_ok=False):
            sid, val, e = rec
            if same_ok and e == eng:
                return
            if need.get(sid, 0) < val:
                need[sid] = val

        for k in reads:
            if k in self.lastw:
                add(self.lastw[k])
        for k in writes:
            if k in self.lastw:
                add(self.lastw[k], same_ok=(acc and eng == "pe"))
            for r in self.readers.get(k, ()):
                add(r, same_ok=True)
        waits = []
        seen = self.seen[eng]
        for sid, val in need.items():
            if seen.get(sid, 0) >= val:
                continue
            seen[sid] = val
            waits.append((sid, val))
        return waits

    def op(self, eng, fn, reads=(), writes=(), acc=False):
        waits = self._deps(eng, reads, writes, acc)
        self.cnt[eng] += 1
        rec = (self.esid[eng], self.cnt[eng], eng)
        self.prog[eng].append((waits, fn, self.esid[eng], 1))
        for k in reads:
            self.readers.setdefault(k, []).append(rec)
        for k in writes:
            self.lastw[k] = rec
            self.readers[k] = []

    def dma(self, q, fn, reads=(), writes=(), key=None):
        if key is None:
            key = writes[0] if writes else reads[0]
        waits = self._deps(q, reads, writes)
        ds = self._dma_sem(key)
        ds[1] += 16
        rec = (ds[0], ds[1], "dma")
        self.prog[q].append((waits, fn, ds[0], 16))
        for k in reads:
            self.readers.setdefault(k, []).append(rec)
        for k in writes:
            self.lastw[k] = rec
            self.readers[k] = []

    def barrier(self):
        allw = [(self.esid[e], self.cnt[e], e) for e in ENGS if self.cnt[e] > 0]
        allw += [(sid, c, None) for k, (sid, c) in self.dsem.items() if c > 0]
        for e in ENGS:
            waits = []
            for sid, val, owner in allw:
                if owner == e:
                    continue
                if self.seen[e].get(sid, 0) >= val:
                    continue
                self.seen[e][sid] = val
                waits.append((sid, val))
            if waits:
                self.prog[e].append((waits, None, None, 0))
        self.dfree.extend(self.dsem.values())
        self.dsem = {}
        self.lastw = {}
        self.readers = {}

    def emit(self):
        nc, sems, prog = self.nc, self.sem_objs, self.prog

        def run(ename):
            def body(eng):
                for waits, fn, incsid, incv in prog[ename]:
                    for sid, val in waits:
                        eng.wait_ge(sems[sid], val)
                    if fn is not None:
                        fn(eng).then_inc(sems[incsid], incv)
            return body

        with nc.Block() as block:
            block.tensor(run("pe"))
            block.scalar(run("act"))
            block.vector(run("dve"))
            block.gpsimd(run("pool"))
            block.sync(run("sp"))


class Arena:
    def __init__(self, tensor, size):
        self.t = tensor
        self.size = size
        self.off = 0

    def alloc(self, shape, dt):
        nb = {F32: 4, BF16: 2, I32: 4}[dt]
        n = int(np.prod(shape[1:]))
        off = (self.off + 63) // 64 * 64
        assert off + n * nb <= self.size, ("SBUF arena overflow", off, n * nb, self.size)
        self.off = off + n * nb
        ap = self.t[:, off:off + n * nb].bitcast(dt)
        if len(shape) == 3:
            ap = ap.rearrange("p (a b) -> p a b", a=shape[1])
        elif len(shape) == 4:
            ap = ap.rearrange("p (a b c) -> p a b c", a=shape[1], b=shape[2])
        if shape[0] != 128:
            ap = ap[0:shape[0]]
        return ap

    def mark(self):
        return self.off

    def release(self, m):
        self.off = m


LAYERS = [
    dict(kind="A", dh=64, H=16, ckv=256, wqkv="l0_w_qkv", qn="l0_q_norm", kn="l0_k_norm", sink="l0_sink", wo="l0_w_o",
         ck="ck0", cv="cv0", nk="nk0", nv="nv0"),
    dict(kind="B"),
    dict(kind="C", dh=128, H=8, ckv=512, wqkv="l2_w_qkv", qn="l2_q_norm", kn="l2_k_norm", sink=None, wo="l2_w_o",
         ck="ck2", cv="cv2", nk="nk2", nv="nv2"),
    dict(kind="A", dh=64, H=16, ckv=256, wqkv="l3_w_qkv", qn="l3_q_norm", kn="l3_k_norm", sink="l3_sink", wo="l3_w_o",
         ck="ck3", cv="cv3", nk="nk3", nv="nv3"),
]

IN_SPECS = [
    ("xin", [2, NT, D]), ("cond", [2, D]),
    ("ck0", [512, 256]), ("cv0", [512, 256]), ("st1", [2, 2, 64, 64]),
    ("ck2", [512, 512]), ("cv2", [512, 512]), ("ck3", [512, 256]), ("cv3", [512, 256]),
    ("norm1", [4, D]), ("norm2", [4, D]), ("w_mod", [4, D, 6 * D]), ("b_mod", [4, 6 * D]),
    ("w_up", [4, D, 2 * DFF]), ("b_up", [4, 2 * DFF]), ("conv_k", [4, 3, 2 * DFF]), ("conv_b", [4, 2 * DFF]),
    ("w_down", [4, DFF, D]),
    ("l0_w_qkv", [D, 1536]), ("l0_q_norm", [64]), ("l0_k_norm", [64]), ("l0_sink", [16]), ("l0_w_o", [D, D]),
    ("l1_lam_re", [2, 64, 64]), ("l1_lam_im", [2, 64, 64]), ("l1_log_dt", [2, 64]),
    ("l1_b_re", [2, 64, 64, 16]), ("l1_b_im", [2, 64, 64, 16]), ("l1_c_re", [2, 64, 16, 64]), ("l1_c_im", [2, 64, 16, 64]),
    ("l1_d_skip", [D]), ("l1_w_glu", [D, 2 * D]),
    ("l2_w_qkv", [D, 2048]), ("l2_q_norm", [128]), ("l2_k_norm", [128]), ("l2_w_o", [D, D]),
    ("l3_w_qkv", [D, 1536]), ("l3_q_norm", [64]), ("l3_k_norm", [64]), ("l3_sink", [16]), ("l3_w_o", [D, D]),
    ("cst", [9, 128, 128]), ("rope", [4, 128, NT]),
]
OUT_SPECS = [
    ("y", [2, NT, D]), ("nk0", [NT, 256]), ("nv0", [NT, 256]), ("nst", [4, 2, 2, 64, 64]),
    ("nk2", [NT, 512]), ("nv2", [NT, 512]), ("nk3", [NT, 256]), ("nv3", [NT, 256]),
]
CST_IDENT, CST_BD64, CST_MLO, CST_MHI, CST_PA, CST_PC, CST_S5F, CST_S5B, CST_R16 = range(9)


def build_program():
    nc = bass.Bass("TRN2", target_bir_lowering=False)
    T = {}
    for name, shape in IN_SPECS:
        T[name] = nc.dram_tensor(name, shape, F32, kind="ExternalInput").ap()
    for name, shape in OUT_SPECS:
        T[name] = nc.dram_tensor(name, shape, F32, kind="ExternalOutput").ap()
    for name, shape, dt_ in [("TX", [128, 16384], BF16), ("TZ", [128, 16384], BF16), ("TM", [64, 128, 128], BF16),
                             ("TE", [2, 64, 128, 128], BF16), ("TS", [2, 64, 128, 128], BF16), ("TP8", [2, 128, 64], F32),
                             ("D1", [2, 1024, 1024], BF16), ("D2", [2, 1024, 1024], BF16)]:
        T[name] = nc.dram_tensor(name, shape, dt_, kind="Internal").ap()
    if DEBUG_H:
        T["dbg"] = nc.dram_tensor("dbg", [2, NT, D], F32, kind="ExternalOutput").ap()

    with ExitStack() as st:
        S = Sched(nc, st)
        ARENA_BYTES = 190 * 1024
        arena_t = st.enter_context(nc.sbuf_tensor("arena", [128, ARENA_BYTES], U8))
        AR = Arena(arena_t, ARENA_BYTES)
        ps = st.enter_context(nc.psum_tensor("ps", [128, 8, 512], F32))
        PSK = [("ps", b) for b in range(8)]
        uid = [0]

        def fresh(prefix):
            uid[0] += 1
            return (prefix, uid[0])

        ident_f = AR.alloc([128, 128], F32)
        ident_b = AR.alloc([128, 128], BF16)
        ones_b = AR.alloc([128, 128], BF16)
        ones_f = AR.alloc([128, 128], F32)
        bd64_b = AR.alloc([128, 128], BF16)
        mlo_b = AR.alloc([128, 4, 128], BF16)
        mhi_b = AR.alloc([128, 4, 128], BF16)
        pa_b = AR.alloc([128, 128], BF16)
        pc_b = AR.alloc([128, 128], BF16)
        cstage = AR.alloc([128, 128], F32)
        S.dma("sp", lambda e: e.dma_start(out=ident_f, in_=T["cst"][CST_IDENT]), writes=["ident_f"])
        S.op("dve", lambda e: e.tensor_copy(out=ident_b, in_=ident_f), reads=["ident_f"], writes=["ident_b"])
        S.op("dve", lambda e: e.memset(ones_b, 1.0), writes=["ones_b"])
        S.op("dve", lambda e: e.memset(ones_f, 1.0), writes=["ones_f"])
        for ci, dst, rep in [(CST_BD64, bd64_b, 0), (CST_MLO, mlo_b, 4), (CST_MHI, mhi_b, 4), (CST_PA, pa_b, 0), (CST_PC, pc_b, 0)]:
            S.dma("sp", lambda e, ci=ci: e.dma_start(out=cstage, in_=T["cst"][ci]), writes=["cstage"])
            if rep:
                for r in range(rep):
                    S.op("dve", lambda e, dst=dst, r=r: e.tensor_copy(out=dst[:, r, :], in_=cstage), reads=["cstage"], writes=["cst_misc"])
            else:
                S.op("dve", lambda e, dst=dst: e.tensor_copy(out=dst, in_=cstage), reads=["cstage"], writes=["cst_misc"])
        S.barrier()
        base_mark = AR.mark()

        def mm(out, lhsT, rhs, start, stop, reads, wkey):
            S.op("pe", lambda e: e.matmul(out, lhsT, rhs, start=start, stop=stop), reads=reads, writes=[wkey], acc=True)

        tstage = AR.alloc([128, 128], F32)
        modAll = AR.alloc([128, 4, 2, 48], F32)
        bmodAll = AR.alloc([128, 4, 48], F32)
        cond2 = AR.alloc([128, 2, 8], F32)
        sT2 = AR.alloc([128, 8, 2], BF16)
        tst2 = AR.alloc([128, 128], F32)
        S.op("dve", lambda e: e.memset(tst2, 0.0), writes=["tst2z"])
        S.barrier()
        base_mark2 = [None]

        def small_load_T(dst, src_ap, pattern, key, **kw):
            n = dst.shape[1]
            rows = src_ap.rearrange("(c q) -> c q", q=128)
            S.dma("sp", lambda e: e.dma_start(out=tstage[0:n, :], in_=rows), writes=["tstage"])
            S.op("pe", lambda e: e.transpose(ps[:, 7, 0:n], tstage[0:n, :], ident_f[0:n, 0:n]), reads=["tstage", "ident_f"], writes=[PSK[7]])
            S.op("dve", lambda e: e.tensor_copy(out=dst, in_=ps[:, 7, 0:n]), reads=[PSK[7]], writes=[key])

        def small_load_col(dst, src_ap, dh, key):
            rep = 128 // dh
            for r in range(rep):
                S.dma("sp", lambda e, r=r: e.dma_start(out=tst2[0:1, r * dh:(r + 1) * dh], in_=src_ap.rearrange("(o q) -> o q", o=1)),
                      writes=[("tst2", r)], key=("tst2", r))
            S.op("pe", lambda e: e.transpose(ps[:, 7, 0:2], tst2[0:2, :], ident_f[0:2, 0:2]),
                 reads=[("tst2", r) for r in range(rep)] + ["tst2z", "ident_f"], writes=[PSK[7], ("tst2", 0), ("tst2", 1)])
            S.op("dve", lambda e: e.tensor_copy(out=dst, in_=ps[:, 7, 0:1]), reads=[PSK[7]], writes=[key])

        base_mark = AR.mark()

        def s5_tables():
            m = AR.mark()
            TWO_PI = 2.0 * math.pi
            lr = AR.alloc([128, 64], F32)
            li = AR.alloc([128, 64], F32)
            dtc = AR.alloc([128, 1], F32)
            xx = AR.alloc([128, 64], F32)
            th = AR.alloc([128, 64], F32)
            PWr = AR.alloc([128, 16, 64], F32)
            PWi = AR.alloc([128, 16, 64], F32)
            Br = AR.alloc([128, 64, 16], F32)
            Bi = AR.alloc([128, 64, 16], F32)
            Bbr = AR.alloc([128, 64, 16], F32)
            Bbi = AR.alloc([128, 64, 16], F32)
            Cr = AR.alloc([128, 16, 64], F32)
            Ci = AR.alloc([128, 16, 64], F32)
            w = [AR.alloc([128, 1024], F32) for _ in range(4)]
            s64 = [AR.alloc([128, 64], F32) for _ in range(8)]
            ki = AR.alloc([128, 64], I32)
            PMr = AR.alloc([128, 64], F32)
            PMi = AR.alloc([128, 64], F32)
            ST = [AR.alloc([128, 16384], BF16) for _ in range(2)]

            def dv(fn, reads, writes):
                S.op("dve", fn, reads=reads, writes=writes)

            S.dma("sp", lambda e: e.dma_start(out=lr, in_=T["l1_lam_re"].rearrange("d g p -> (d g) p")), writes=["lr"])
            S.dma("sp", lambda e: e.dma_start(out=li, in_=T["l1_lam_im"].rearrange("d g p -> (d g) p")), writes=["li"])
            S.dma("sp", lambda e: e.dma_start(out=dtc, in_=T["l1_log_dt"].rearrange("d (g o) -> (d g) o", o=1)), writes=["dtc"])
            S.dma("sp", lambda e: e.dma_start(out=Br, in_=T["l1_b_re"].rearrange("d g p c -> (d g) p c")), writes=["Br"])
            S.dma("sp", lambda e: e.dma_start(out=Bi, in_=T["l1_b_im"].rearrange("d g p c -> (d g) p c")), writes=["Bi"])
            S.dma("sp", lambda e: e.dma_start(out=Cr, in_=T["l1_c_re"].rearrange("d g c p -> (d g) c p")), writes=["Cr"])
            S.dma("sp", lambda e: e.dma_start(out=Ci, in_=T["l1_c_im"].rearrange("d g c p -> (d g) c p")), writes=["Ci"])
            S.op("act", lambda e: e.activation(out=dtc, in_=dtc, func=AF.Exp), reads=["dtc"], writes=["dtc"])
            dv(lambda e: e.tensor_scalar(out=xx, in0=lr, scalar1=dtc[:, 0:1], scalar2=None, op0=ALU.mult), ["lr", "dtc"], ["xx"])
            dv(lambda e: e.tensor_scalar(out=th, in0=li, scalar1=dtc[:, 0:1], scalar2=None, op0=ALU.mult), ["li", "dtc"], ["th"])

            def sinred(out, okey, shift):
                v, kf, r, mk = s64[0], s64[1], s64[2], s64[3]
                dv(lambda e: e.tensor_scalar(out=v, in0=th, scalar1=float(shift), scalar2=None, op0=ALU.add), ["th"], ["sr_v"])
                dv(lambda e: e.tensor_scalar(out=kf, in0=v, scalar1=1.0 / TWO_PI, scalar2=None, op0=ALU.mult), ["sr_v"], ["sr_kf"])
                dv(lambda e: e.tensor_copy(out=ki, in_=kf), ["sr_kf"], ["sr_ki"])
                dv(lambda e: e.tensor_copy(out=kf, in_=ki), ["sr_ki"], ["sr_kf"])
                dv(lambda e: e.scalar_tensor_tensor(out=r, in0=kf, scalar=-TWO_PI, in1=v, op0=ALU.mult, op1=ALU.add), ["sr_kf", "sr_v"], ["sr_r"])
                dv(lambda e: e.tensor_scalar(out=mk, in0=r, scalar1=math.pi, scalar2=None, op0=ALU.is_gt), ["sr_r"], ["sr_m"])
                dv(lambda e: e.scalar_tensor_tensor(out=r, in0=mk, scalar=-TWO_PI, in1=r, op0=ALU.mult, op1=ALU.add), ["sr_m", "sr_r"], ["sr_r"])
                dv(lambda e: e.tensor_scalar(out=mk, in0=r, scalar1=-math.pi, scalar2=None, op0=ALU.is_lt), ["sr_r"], ["sr_m"])
                dv(lambda e: e.scalar_tensor_tensor(out=r, in0=mk, scalar=TWO_PI, in1=r, op0=ALU.mult, op1=ALU.add), ["sr_m", "sr_r"], ["sr_r"])
                dv(lambda e: e.tensor_scalar(out=r, in0=r, scalar1=3.1415925, scalar2=-3.1415925, op0=ALU.min, op1=ALU.max), ["sr_r"], ["sr_r"])
                S.op("act", lambda e: e.activation(out=out, in_=r, func=AF.Sin), reads=["sr_r"], writes=[okey])

            sn, cs, ex, exm = s64[4], s64[5], s64[6], s64[7]
            sinred(sn, "sn", 0.0)
            sinred(cs, "cs", math.pi / 2.0)
            S.op("act", lambda e: e.activation(out=ex, in_=xx, func=AF.Exp), reads=["xx"], writes=["ex"])
            S.op("act", lambda e: e.activation(out=exm, in_=xx, func=AF.Exp, scale=-1.0), reads=["xx"], writes=["exm"])
            PK = lambda j: ("PW", j)
            dv(lambda e: e.memset(PWr[:, 7, :], 1.0), [], [("PWr", 0)])
            dv(lambda e: e.memset(PWi[:, 7, :], 0.0), [], [("PWi", 0)])
            dv(lambda e: e.tensor_tensor(out=PWr[:, 8, :], in0=ex, in1=cs, op=ALU.mult), ["ex", "cs"], [("PWr", 1)])
            dv(lambda e: e.tensor_tensor(out=PWi[:, 8, :], in0=ex, in1=sn, op=ALU.mult), ["ex", "sn"], [("PWi", 1)])
            dv(lambda e: e.tensor_tensor(out=PWr[:, 6, :], in0=exm, in1=cs, op=ALU.mult), ["exm", "cs"], [("PWr", -1)])
            dv(lambda e: e.scalar_tensor_tensor(out=PWi[:, 6, :], in0=exm, scalar=-1.0, in1=sn, op0=ALU.mult, op1=ALU.mult), ["exm", "sn"], [("PWi", -1)])

            def cmul64(j_out, j_a, j_b):
                orr, oi = PWr[:, j_out + 7, :], PWi[:, j_out + 7, :]
                ar, ai = PWr[:, j_a + 7, :], PWi[:, j_a + 7, :]
                br, bi = PWr[:, j_b + 7, :], PWi[:, j_b + 7, :]
                ka = [("PWr", j_a), ("PWi", j_a), ("PWr", j_b), ("PWi", j_b)]
                t1, t2 = s64[0], s64[1]
                dv(lambda e: e.tensor_tensor(out=t1, in0=ar, in1=br, op=ALU.mult), ka, ["c64a"])
                dv(lambda e: e.tensor_tensor(out=t2, in0=ai, in1=bi, op=ALU.mult), ka, ["c64b"])
                dv(lambda e: e.tensor_tensor(out=orr, in0=t1, in1=t2, op=ALU.subtract), ["c64a", "c64b"], [("PWr", j_out)])
                dv(lambda e: e.tensor_tensor(out=t1, in0=ar, in1=bi, op=ALU.mult), ka, ["c64a"])
                dv(lambda e: e.tensor_tensor(out=t2, in0=ai, in1=br, op=ALU.mult), ka, ["c64b"])
                dv(lambda e: e.tensor_tensor(out=oi, in0=t1, in1=t2, op=ALU.add), ["c64a", "c64b"], [("PWi", j_out)])

            for j in range(2, 9):
                cmul64(j, j - 1, 1)
            for j in range(2, 8):
                cmul64(-j, -(j - 1), -1)
            S.dma("sp", lambda e: e.dma_start(out=T["TP8"][0], in_=PWr[:, 15, :]), reads=[("PWr", 8)], key="tp8a")
            S.dma("sp", lambda e: e.dma_start(out=T["TP8"][1], in_=PWi[:, 15, :]), reads=[("PWi", 8)], key="tp8b")
            nr, den, fr, fi = s64[0], s64[1], s64[2], s64[3]
            t5, t6 = s64[4], s64[5]
            a1 = [("PWr", 1), ("PWi", 1)]
            dv(lambda e: e.tensor_scalar(out=nr, in0=PWr[:, 8, :], scalar1=-1.0, scalar2=None, op0=ALU.add), a1, ["nr"])
            dv(lambda e: e.tensor_tensor(out=den, in0=lr, in1=lr, op=ALU.mult), ["lr"], ["den"])
            dv(lambda e: e.tensor_tensor(out=t5, in0=li, in1=li, op=ALU.mult), ["li"], ["t5"])
            dv(lambda e: e.tensor_tensor(out=den, in0=den, in1=t5, op=ALU.add), ["den", "t5"], ["den"])
            dv(lambda e: e.reciprocal(out=den, in_=den), ["den"], ["den"])
            dv(lambda e: e.tensor_tensor(out=fr, in0=nr, in1=lr, op=ALU.mult), ["nr", "lr"], ["fr"])
            dv(lambda e: e.tensor_tensor(out=t5, in0=PWi[:, 8, :], in1=li, op=ALU.mult), a1 + ["li"], ["t5"])
            dv(lambda e: e.tensor_tensor(out=fr, in0=fr, in1=t5, op=ALU.add), ["fr", "t5"], ["fr"])
            dv(lambda e: e.tensor_tensor(out=fr, in0=fr, in1=den, op=ALU.mult), ["fr", "den"], ["fr"])
            dv(lambda e: e.tensor_tensor(out=fi, in0=PWi[:, 8, :], in1=lr, op=ALU.mult), a1 + ["lr"], ["fi"])
            dv(lambda e: e.tensor_tensor(out=t6, in0=nr, in1=li, op=ALU.mult), ["nr", "li"], ["t6"])
            dv(lambda e: e.tensor_tensor(out=fi, in0=fi, in1=t6, op=ALU.subtract), ["fi", "t6"], ["fi"])
            dv(lambda e: e.tensor_tensor(out=fi, in0=fi, in1=den, op=ALU.mult), ["fi", "den"], ["fi"])

            def bc_b(a):
                return a.unsqueeze(2).broadcast_to([128, 64, 16])

            def bc_c(a):
                return a.unsqueeze(1).broadcast_to([128, 16, 64])

            w3b = [x.rearrange("q (p c) -> q p c", c=16) for x in w]
            w3c = [x.rearrange("q (c p) -> q c p", p=64) for x in w]

            def cmul_big(eng, pr, pi, pk, xr, xi, xk, bc, wv, out_r, out_i, okey, xr_n=None, xi_n=None):
                t1, t2, t3, t4 = wv
                tg = "cb" + eng
                op = lambda fn, r, w_: S.op(eng, fn, reads=r, writes=w_)
                op(lambda e: e.tensor_tensor(out=t1, in0=xr, in1=bc(pr), op=ALU.mult), pk + xk, [tg + "1"])
                op(lambda e: e.tensor_tensor(out=t2, in0=xi, in1=bc(pi), op=ALU.mult), pk + xk, [tg + "2"])
                op(lambda e: e.tensor_tensor(out=out_r, in0=t1, in1=t2, op=ALU.subtract), [tg + "1", tg + "2"], [okey])
                a_i = xi if xi_n is None else xi_n
                a_r = xr if xr_n is None else xr_n
                op(lambda e: e.tensor_tensor(out=t3, in0=a_i, in1=bc(pr), op=ALU.mult), pk + xk, [tg + "3"])
                op(lambda e: e.tensor_tensor(out=t4, in0=a_r, in1=bc(pi), op=ALU.mult), pk + xk, [tg + "4"])
                op(lambda e: e.tensor_tensor(out=out_i, in0=t3, in1=t4, op=ALU.add), [tg + "3", tg + "4"], [okey])

            cmul_big("dve", fr, fi, ["fr", "fi"], Br, Bi, ["Br", "Bi"], bc_b, w3b, Bbr, Bbi, "Bb")
            nCr = Br.rearrange("q p c -> q (p c)").rearrange("q (c p) -> q c p", p=64)
            nCi = Bi.rearrange("q p c -> q (p c)").rearrange("q (c p) -> q c p", p=64)
            dv(lambda e: e.tensor_scalar(out=nCr, in0=Cr, scalar1=-1.0, scalar2=None, op0=ALU.mult), ["Cr", "Bb"], ["nC"])
            dv(lambda e: e.tensor_scalar(out=nCi, in0=Ci, scalar1=-1.0, scalar2=None, op0=ALU.mult), ["Ci", "Bb"], ["nC"])
            PM = {"dve": (PMr, PMi), "pool": (AR.alloc([128, 64], F32), AR.alloc([128, 64], F32))}
            wp = [AR.alloc([128, 1024], F32) for _ in range(4)]
            WV = {"dve": (w3b, w3c),
                  "pool": ([x.rearrange("q (p c) -> q p c", c=16) for x in wp], [x.rearrange("q (c p) -> q c p", p=64) for x in wp])}

            def mixed_power(eng, jf, jb):
                pmr, pmi = PM[eng]
                k = "PM" + eng
                S.op(eng, lambda e: e.tensor_copy(out=pmr[0:64], in_=PWr[0:64, jf + 7, :]), reads=[("PWr", jf)], writes=[k])
                S.op(eng, lambda e: e.tensor_copy(out=pmi[0:64], in_=PWi[0:64, jf + 7, :]), reads=[("PWi", jf)], writes=[k])
                S.op(eng, lambda e: e.tensor_copy(out=pmr[64:128], in_=PWr[64:128, jb + 7, :]), reads=[("PWr", jb)], writes=[k])
                S.op(eng, lambda e: e.tensor_copy(out=pmi[64:128], in_=PWi[64:128, jb + 7, :]), reads=[("PWi", jb)], writes=[k])

            specs = [("B", lambda t: -t, lambda t: t, "TX", "ript", "dve"),
                     ("C", lambda t: t, lambda t: -t, "TZ", "ript", "dve"),
                     ("B", lambda t: 7 - t, lambda t: t, "TS", "ritp", "dve"),
                     ("C", lambda t: t + 1, lambda t: 8 - t, "TE", "ript", "dve")]
            for ti, (side, pf, pb_, dst, layout, eng) in enumerate(specs):
                st_ = ST[ti % 2]
                stk = ("ST", ti % 2)
                pmr, pmi = PM[eng]
                pmk = ["PM" + eng]
                if layout == "ript":
                    st5 = st_.rearrange("q (r p t c) -> q r p t c", r=2, p=64, t=8)
                else:
                    st5 = st_.rearrange("q (r t c p) -> q r t c p", r=2, t=8, c=16)
                for t in range(8):
                    mixed_power(eng, pf(t), pb_(t))
                    if side == "B":
                        if layout == "ript":
                            o_r, o_i = st5[:, 0, :, t, :], st5[:, 1, :, t, :]
                        else:
                            o_r = st5[:, 0, t, :, :].rearrange("q c p -> q p c")
                            o_i = st5[:, 1, t, :, :].rearrange("q c p -> q p c")
                        cmul_big(eng, pmr, pmi, pmk, Bbr, Bbi, ["Bb"], bc_b, WV[eng][0], o_r, o_i, stk)
                    else:
                        o_r = st5[:, 0, :, t, :].rearrange("q p c -> q c p")
                        o_i = st5[:, 1, :, t, :].rearrange("q p c -> q c p")
                        cmul_big(eng, pmr, pmi, pmk, Cr, Ci, ["Cr", "Ci", "nC"], bc_c, WV[eng][1], o_r, o_i, stk, xr_n=nCr, xi_n=nCi)
                if dst in ("TX", "TZ"):
                    S.dma("sp", lambda e, st_=st_, dst=dst: e.dma_start(out=T[dst], in_=st_), reads=[stk], writes=[dst], key=stk)
                elif dst == "TE":
                    st4 = st_.rearrange("q (r p n) -> q r p n", r=2, p=64)
                    for d in range(2):
                        for r_ in range(2):
                            S.dma("sp", lambda e, d=d, r_=r_, st4=st4: e.dma_start(
                                out=T["TE"][r_, :, d * 64:(d + 1) * 64, :], in_=st4[d * 64:(d + 1) * 64, r_]), reads=[stk], writes=["TE"], key=stk)
                else:
                    st4 = st_.rearrange("q (r n p) -> q r n p", r=2, p=64)
                    for d in range(2):
                        for r_ in range(2):
                            for g8 in range(8):
                                S.dma("sp", lambda e, d=d, r_=r_, g8=g8, st4=st4: e.dma_start(
                                    out=T["TS"][r_, g8 * 8:(g8 + 1) * 8, :, d * 64:(d + 1) * 64],
                                    in_=st4[d * 64 + g8 * 8:d * 64 + (g8 + 1) * 8, r_]), reads=[stk], writes=["TS"], key=stk)
            S.barrier()
            AR.release(m)
            m = AR.mark()
            mkf = AR.alloc([128, 128], F32)
            mkb = AR.alloc([128, 128], F32)
            r16 = AR.alloc([128, 128], F32)
            dgc = AR.alloc([128, 16], F32)
            dmat = AR.alloc([128, 64], F32)
            dcol = AR.alloc([128, 64], F32)
            S.dma("sp", lambda e: e.dma_start(out=mkf, in_=T["cst"][CST_S5F]), writes=["mkf"])
            S.dma("sp", lambda e: e.dma_start(out=mkb, in_=T["cst"][CST_S5B]), writes=["mkb"])
            S.dma("sp", lambda e: e.dma_start(out=r16, in_=T["cst"][CST_R16]), writes=["r16"])
            S.dma("sp", lambda e: e.dma_start(out=dgc[0:64, :], in_=T["l1_d_skip"].rearrange("(g c) -> g c", c=16)), writes=["dgc"])
            S.op("pe", lambda e: e.transpose(ps[0:16, 7, 0:64], dgc[0:64, :], ident_f[0:64, 0:64]), reads=["dgc", "ident_f"], writes=[PSK[7]])
            S.op("dve", lambda e: e.tensor_copy(out=dmat[0:16, :], in_=ps[0:16, 7, 0:64]), reads=[PSK[7]], writes=["dmat"])
            S.op("pe", lambda e: e.matmul(ps[:, 7, 64:128], r16[0:16, :], dmat[0:16, :], start=True, stop=True), reads=["r16", "dmat"], writes=[PSK[7]])
            S.op("dve", lambda e: e.tensor_copy(out=dcol, in_=ps[:, 7, 64:128]), reads=[PSK[7]], writes=["dcol"])
            XL = [[AR.alloc([128, 16, 128], BF16) for _ in range(2)] for _ in range(2)]
            ZL = [[AR.alloc([128, 16, 128], BF16) for _ in range(2)] for _ in range(2)]
            Mst = [AR.alloc([128, 16, 128], BF16) for _ in range(2)]
            tA = [AR.alloc([128, 128], F32) for _ in range(2)]
            tB = [AR.alloc([128, 128], F32) for _ in range(2)]
            TX4 = T["TX"].rearrange("(d g) (k n) -> d g k n", d=2, k=128)
            TZ4 = T["TZ"].rearrange("(d g) (k n) -> d g k n", d=2, k=128)
            for gb in range(4):
                bf = gb % 2
                for d in range(2):
                    S.dma("sp", lambda e, d=d, gb=gb, bf=bf: e.dma_start(out=XL[bf][d], in_=TX4[d, gb * 16:(gb + 1) * 16].rearrange("g k n -> k g n")),
                          reads=["TX"], writes=[("XL", bf, d)])
                    S.dma("sp", lambda e, d=d, gb=gb, bf=bf: e.dma_start(out=ZL[bf][d], in_=TZ4[d, gb * 16:(gb + 1) * 16].rearrange("g k n -> k g n")),
                          reads=["TZ"], writes=[("ZL", bf, d)])
                for gi in range(16):
                    g = gb * 16 + gi
                    b = gi % 4
                    for d in range(2):
                        S.op("pe", lambda e, b=b, d=d, bf=bf, gi=gi: e.matmul(ps[:, b, d * 128:(d + 1) * 128], XL[bf][d][:, gi, :], ZL[bf][d][:, gi, :],
                                                                            start=True, stop=True),
                             reads=[("XL", bf, d), ("ZL", bf, d)], writes=[PSK[b]], acc=True)
                    a_, b_ = tA[gi % 2], tB[gi % 2]
                    S.op("dve", lambda e, a_=a_, b=b: e.tensor_tensor(out=a_, in0=ps[:, b, 0:128], in1=mkf, op=ALU.mult), reads=[PSK[b], "mkf"], writes=[("tA", gi % 2)])
                    S.op("dve", lambda e, b_=b_, b=b: e.tensor_tensor(out=b_, in0=ps[:, b, 128:256], in1=mkb, op=ALU.mult), reads=[PSK[b], "mkb"], writes=[("tB", gi % 2)])
                    S.op("dve", lambda e, a_=a_, b_=b_: e.tensor_tensor(out=a_, in0=a_, in1=b_, op=ALU.add), reads=[("tA", gi % 2), ("tB", gi % 2)], writes=[("tA", gi % 2)])
                    S.op("dve", lambda e, a_=a_, g=g, gi=gi, bf=bf: e.scalar_tensor_tensor(out=Mst[bf][:, gi, :], in0=ident_f, scalar=dcol[:, g:g + 1], in1=a_,
                                                                                         op0=ALU.mult, op1=ALU.add),
                         reads=[("tA", gi % 2), "dcol", "ident_f"], writes=[("Mst", bf)])
                S.dma("sp", lambda e, gb=gb, bf=bf: e.dma_start(out=T["TM"][gb * 16:(gb + 1) * 16].rearrange("g k n -> k g n"), in_=Mst[bf]),
                      reads=[("Mst", bf)], writes=["TM"], key=("Mst", bf))
            S.barrier()
            AR.release(m)

        def prologue_mod_start():
            for c_ in range(2):
                small_load_T(cond2[:, c_, :], T["cond"][c_], "(kc q) -> q kc", ("cond2", c_), q=128)
            S.op("act", lambda e: e.activation(out=sT2.rearrange("q k c -> q c k"), in_=cond2, func=AF.Silu),
                 reads=[("cond2", 0), ("cond2", 1)], writes=["sT"])
            wm = [AR.alloc([128, 8, 512], BF16) for _ in range(2)]
            bi_ = 0
            for l in range(NLAYERS):
                for blk in range(12):
                    w = wm[bi_ % 2]
                    wk = ("wm", bi_ % 2)
                    bi_ += 1
                    S.dma("pool", lambda e, w=w, blk=blk, l=l: e.dma_start(
                        out=w, in_=T["w_mod"][l][:, blk * 512:(blk + 1) * 512].rearrange("(kc q) n -> q kc n", q=128)), writes=[wk])
                    for cc in range(4):
                        col = l * 96 + 2 * (blk * 4 + cc)
                        for kc in range(8):
                            mm(ps[:, 6, col:col + 2], w[:, kc, cc * 128:(cc + 1) * 128], sT2[:, kc, :], kc == 0, kc == 7, [wk, "sT"], PSK[6])

        def prologue_mod_finish():
            for l in range(NLAYERS):
                small_load_T(bmodAll[:, l, :], T["b_mod"][l], "(c q) -> q c", ("bmodAll", l), q=128)
                pv = ps[:, 6, l * 96:(l + 1) * 96].rearrange("q (n c) -> q c n", c=2)
                for c_ in range(2):
                    S.op("dve", lambda e, c_=c_, l=l, pv=pv: e.tensor_tensor(out=modAll[:, l, c_, :], in0=pv[:, c_, :], in1=bmodAll[:, l, :], op=ALU.add),
                         reads=[PSK[6], ("bmodAll", l)], writes=[("modAll", l)])
            S.barrier()

        def run_path(p):
            nseq, slen = (4, 256) if p == 0 else (1, 1024)
            AR.release(base_mark)
            hT = AR.alloc([128, 8, NT], F32)
            xnT = AR.alloc([128, 8, NT], BF16)
            modT = AR.alloc([128, 48], F32)
            bmodT = AR.alloc([128, 48], F32)
            n1T = AR.alloc([128, 8], F32)
            n2T = AR.alloc([128, 8], F32)
            A1 = AR.alloc([128, 8], F32)
            A2 = AR.alloc([128, 8], F32)
            condT = AR.alloc([128, 8], F32)
            sT = AR.alloc([128, 8], BF16)
            rstd = AR.alloc([128, NT], F32)
            lnv = AR.alloc([128, 512], F32)
            path_mark = AR.mark()
            HK = [("hT", kc) for kc in range(8)]
            XK = [("xnT", kc) for kc in range(8)]

            m0 = AR.mark()
            xst = [AR.alloc([128, D], F32) for _ in range(2)]
            for tt in range(8):
                xs = xst[tt % 2]
                xk = ("xst", tt % 2)
                S.dma("sp", lambda e, xs=xs, tt=tt: e.dma_start(out=xs, in_=T["xin"][p, tt * 128:(tt + 1) * 128, :]), writes=[xk])
                for hf in range(2):
                    b = (tt * 2 + hf) % 8
                    for q in range(4):
                        kc = hf * 4 + q
                        S.op("pe", lambda e, b=b, q=q, kc=kc, xs=xs: e.transpose(ps[:, b, q * 128:(q + 1) * 128], xs[:, kc * 128:(kc + 1) * 128], ident_f),
                             reads=[xk, "ident_f"], writes=[PSK[b]], acc=True)
                    S.op("act" if hf else "dve",
                         (lambda e, b=b, hf=hf, tt=tt: e.activation(out=hT[:, hf * 4:hf * 4 + 4, tt * 128:(tt + 1) * 128],
                                                                     in_=ps[:, b, :].rearrange("p (a b) -> p a b", a=4), func=AF.Copy)) if hf else
                         (lambda e, b=b, hf=hf, tt=tt: e.tensor_copy(out=hT[:, hf * 4:hf * 4 + 4, tt * 128:(tt + 1) * 128],
                                                                      in_=ps[:, b, :].rearrange("p (a b) -> p a b", a=4))),
                         reads=[PSK[b]], writes=HK[hf * 4:hf * 4 + 4])
            S.barrier()
            AR.release(m0)
            def compute_mod(l):
                m = AR.mark()
                small_load_T(n1T, T["norm1"][l], "(kc q) -> q kc", "n1T", q=128)
                small_load_T(n2T, T["norm2"][l], "(kc q) -> q kc", "n2T", q=128)
                S.op("dve", lambda e: e.tensor_copy(out=modT, in_=modAll[:, l, p, :]), reads=[("modAll", l)], writes=["modT"])
                S.op("dve", lambda e: e.scalar_tensor_tensor(out=A1, in0=modT[:, 8:16], scalar=1.0, in1=n1T, op0=ALU.add, op1=ALU.mult),
                     reads=["modT", "n1T"], writes=["A1"])
                S.op("dve", lambda e: e.scalar_tensor_tensor(out=A2, in0=modT[:, 32:40], scalar=1.0, in1=n2T, op0=ALU.add, op1=ALU.mult),
                     reads=["modT", "n2T"], writes=["A2"])
                AR.release(m)

            def norm_mod(Acol, Akey, Boff, perm8=False):
                m = AR.mark()
                sq = AR.alloc([128, 8, NT], BF16)
                tmp = [AR.alloc([128, NT], F32) for _ in range(2)]
                S.op("act", lambda e: e.activation(out=sq, in_=hT, func=AF.Square), reads=HK, writes=["sq"])
                for hf in range(2):
                    b = hf
                    for kc in range(8):
                        mm(ps[:, b, :], ones_b, sq[:, kc, hf * 512:(hf + 1) * 512], kc == 0, kc == 7, ["sq", "ones_b"], PSK[b])
                    S.op("act", lambda e, b=b: e.activation(out=lnv, in_=ps[:, b, :], func=AF.Ln, bias=EPS, scale=1.0 / D),
                         reads=[PSK[b]], writes=["lnv"])
                    S.op("act", lambda e, hf=hf: e.activation(out=rstd[:, hf * 512:(hf + 1) * 512], in_=lnv, func=AF.Exp, scale=-0.5),
                         reads=["lnv"], writes=[("rstd", hf)])
                for kc in range(8):
                    t = tmp[kc % 2]
                    tk = ("nm_tmp", kc % 2)
                    S.op("dve", lambda e, t=t, kc=kc: e.scalar_tensor_tensor(out=t, in0=hT[:, kc, :], scalar=Acol[:, kc:kc + 1], in1=rstd,
                                                                              op0=ALU.mult, op1=ALU.mult),
                         reads=[HK[kc], Akey, ("rstd", 0), ("rstd", 1)], writes=[tk])
                    if perm8:
                        o = xnT[:, kc, :].rearrange("p (t k) -> p k t", t=8)
                    else:
                        o = xnT[:, kc, :]
                    S.op("act", lambda e, t=t, kc=kc, o=o: e.activation(out=o, in_=t, func=AF.Identity, bias=modT[:, Boff + kc:Boff + kc + 1]),
                         reads=[tk, "modT"], writes=[XK[kc]])
                S.barrier()
                AR.release(m)

            def resid(oc, hf, b, goff):
                S.op("dve", lambda e: e.scalar_tensor_tensor(out=hT[:, oc, hf * 512:(hf + 1) * 512], in0=ps[:, b, :],
                                                              scalar=modT[:, goff + oc:goff + oc + 1],
                                                              in1=hT[:, oc, hf * 512:(hf + 1) * 512], op0=ALU.mult, op1=ALU.add),
                     reads=[PSK[b], "modT", HK[oc]], writes=[HK[oc]])

            def ffn(l):
                m = AR.mark()
                norm_mod(A2, "A2", 24)
                gT = AR.alloc([128, 22, NT], BF16)
                wd = AR.alloc([128, 22, D], BF16)
                bupT = AR.alloc([128, 44], F32)
                cbT = AR.alloc([128, 44], F32)
                ckT = AR.alloc([128, 3, 44], F32)
                wu = [AR.alloc([128, 2, 8, 256], BF16) for _ in range(2)]
                tc = [AR.alloc([128, nseq, slen], F32) for _ in range(2)]
                sg = AR.alloc([128, nseq, slen], F32)
                beff = AR.alloc([128, 44], F32)
                e0 = AR.alloc([128, 44], F32)
                e2 = AR.alloc([128, 44], F32)
                small_load_T(bupT, T["b_up"][l], "(c q) -> q c", "bupT", q=128)
                small_load_T(cbT, T["conv_b"][l], "(c q) -> q c", "cbT", q=128)
                for t3 in range(3):
                    small_load_T(ckT[:, t3, :], T["conv_k"][l, t3], "(c q) -> q c", ("ckT", t3), q=128)
                CK = [("ckT", t3) for t3 in range(3)]
                S.op("dve", lambda e: e.tensor_tensor(out=beff, in0=ckT[:, 0, :], in1=ckT[:, 1, :], op=ALU.add), reads=CK, writes=["beff"])
                S.op("dve", lambda e: e.tensor_tensor(out=beff, in0=beff, in1=ckT[:, 2, :], op=ALU.add), reads=CK + ["beff"], writes=["beff"])
                S.op("dve", lambda e: e.tensor_tensor(out=beff, in0=beff, in1=bupT, op=ALU.mult), reads=["beff", "bupT"], writes=["beff"])
                S.op("dve", lambda e: e.tensor_tensor(out=beff, in0=beff, in1=cbT, op=ALU.add), reads=["beff", "cbT"], writes=["beff"])
                S.op("dve", lambda e: e.scalar_tensor_tensor(out=e0, in0=bupT, scalar=-1.0, in1=ckT[:, 0, :], op0=ALU.mult, op1=ALU.mult), reads=["bupT"] + CK, writes=["e0"])
                S.op("dve", lambda e: e.scalar_tensor_tensor(out=e2, in0=bupT, scalar=-1.0, in1=ckT[:, 2, :], op0=ALU.mult, op1=ALU.mult), reads=["bupT"] + CK, writes=["e2"])
                for j4 in range(0, 22, 6):
                    je = min(22, j4 + 6)
                    S.dma("pool", lambda e, j4=j4, je=je: e.dma_start(
                        out=wd[:, j4:je, :], in_=T["w_down"][l][j4 * 128:je * 128, :].rearrange("(j q) n -> q j n", q=128)),
                        writes=[("wd", j4)])
                WDK = [("wd", j4) for j4 in range(0, 22, 6)]
                for jb in range(11):
                    w = wu[jb % 2]
                    wk = ("wu", jb % 2)
                    for gv in range(2):
                        c0 = gv * DFF + jb * 256
                        S.dma("pool", lambda e, w=w, gv=gv, c0=c0: e.dma_start(
                            out=w[:, gv], in_=T["w_up"][l][:, c0:c0 + 256].rearrange("(kc q) n -> q kc n", q=128)),
                            writes=[(wk, gv)], key=(wk, gv))
                    for sub in range(2):
                        j = jb * 2 + sub
                        bs = (j % 2) * 4
                        for gv in range(2):
                            for hf in range(2):
                                b = bs + gv * 2 + hf
                                for kc in range(8):
                                    mm(ps[:, b, :], w[:, gv, kc, sub * 128:(sub + 1) * 128], xnT[:, kc, hf * 512:(hf + 1) * 512],
                                       kc == 0, kc == 7, [(wk, gv), XK[kc]], PSK[b])
                        for gv in range(2):
                            b = bs + gv * 2
                            col = gv * 22 + j
                            t_ = tc[gv]
                            tk = ("tc", gv)
                            pin = ps[:, b:b + 2, :].rearrange("p b (s t) -> p (b s) t", t=slen) if nseq == 4 else \
                                ps[:, b:b + 2, :].rearrange("p (o b) t -> p o (b t)", o=1)
                            PB = [PSK[b], PSK[b + 1]]
                            S.op("act", lambda e, t_=t_, pin=pin, col=col: e.activation(out=t_, in_=pin, func=AF.Identity,
                                                                                      bias=beff[:, col:col + 1], scale=ckT[:, 1, col:col + 1]),
                                 reads=PB + ["beff", CK[1]], writes=[tk])
                            S.op("dve", lambda e, t_=t_, pin=pin, col=col: e.scalar_tensor_tensor(out=t_[:, :, 1:slen], in0=pin[:, :, 0:slen - 1],
                                                                                                scalar=ckT[:, 0, col:col + 1], in1=t_[:, :, 1:slen],
                                                                                                op0=ALU.mult, op1=ALU.add),
                                 reads=PB + [CK[0], tk], writes=[tk])
                            S.op("dve", lambda e, t_=t_, pin=pin, col=col: e.scalar_tensor_tensor(out=t_[:, :, 0:slen - 1], in0=pin[:, :, 1:slen],
                                                                                                scalar=ckT[:, 2, col:col + 1], in1=t_[:, :, 0:slen - 1],
                                                                                                op0=ALU.mult, op1=ALU.add),
                                 reads=PB + [CK[2], tk], writes=[tk])
                            S.op("dve", lambda e, t_=t_, col=col: e.tensor_scalar(out=t_[:, :, 0], in0=t_[:, :, 0], scalar1=e0[:, col:col + 1], scalar2=None, op0=ALU.add),
                                 reads=[tk, "e0"], writes=[tk])
                            S.op("dve", lambda e, t_=t_, col=col: e.tensor_scalar(out=t_[:, :, slen - 1], in0=t_[:, :, slen - 1], scalar1=e2[:, col:col + 1], scalar2=None,
                                                                                  op0=ALU.add),
                                 reads=[tk, "e2"], writes=[tk])
                        S.op("act", lambda e: e.activation(out=sg, in_=tc[0], func=AF.Silu), reads=[("tc", 0)], writes=["sg"])
                        go = gT[:, j, :].rearrange("p (s t) -> p s t", s=nseq)
                        S.op("dve", lambda e, go=go: e.tensor_tensor(out=go, in0=sg, in1=tc[1], op=ALU.mult),
                             reads=["sg", ("tc", 1)], writes=[("gT", j)])
                bi = 0
                for oc in range(8):
                    for hf in range(2):
                        b = bi % 8
                        bi += 1
                        for j in range(22):
                            mm(ps[:, b, :], wd[:, j, oc * 128:(oc + 1) * 128], gT[:, j, hf * 512:(hf + 1) * 512], j == 0, j == 21,
                               [WDK[j // 6], ("gT", j)], PSK[b])
                        resid(oc, hf, b, 40)
                AR.release(m)

            def attention(l):
                L = LAYERS[l]
                dh, H, ckv = L["dh"], L["H"], L["ckv"]
                G = H // 4
                nkc = ckv // 128
                ctot = D + 2 * ckv
                scale = dh ** -0.5
                rope = (p == 1)
                m = AR.mark()
                norm_mod(A1, "A1", 0)
                if ATT_STOP <= -1:
                    AR.release(m)
                    return
                wbig = AR.alloc([128, 8, ctot], BF16)
                qT = AR.alloc([128, 8, NT], BF16)
                kT = AR.alloc([128, nkc, NT], BF16)
                vx = AR.alloc([128, 8, 4, dh + 2], BF16)
                qn_c = AR.alloc([128, 1], F32)
                kn_c = AR.alloc([128, 1], F32)
                wo_b = AR.alloc([128, 8, D], BF16)
                esink = AR.alloc([128, 16], F32) if L["sink"] else None
                if p == 1:
                    kcT = AR.alloc([128, nkc, 512], BF16)
                    vcx = AR.alloc([128, 4, 4, dh + 2], BF16)
                tmp_mark = AR.mark()
                sqb = [AR.alloc([128, 512], BF16) for _ in range(2)]
                rs = [AR.alloc([128, 512], F32) for _ in range(2)]
                qf = [AR.alloc([128, 512], F32) for _ in range(2)]
                S.dma("pool", lambda e: e.dma_start(out=wbig[:, :, 0:768], in_=T[L["wqkv"]][:, 0:768].rearrange("(kc q) n -> q kc n", q=128)),
                      writes=[("wbig", 0)])
                S.dma("pool", lambda e: e.dma_start(out=wbig[:, :, 768:ctot], in_=T[L["wqkv"]][:, 768:ctot].rearrange("(kc q) n -> q kc n", q=128)),
                      writes=[("wbig", 1)])
                WB = [("wbig", 0), ("wbig", 1)]
                S.dma("pool", lambda e: e.dma_start(out=wo_b, in_=T[L["wo"]].rearrange("(kc q) n -> q kc n", q=128)), writes=["wo_b"])
                small_load_col(qn_c, T[L["qn"]], dh, "qn_c")
                small_load_col(kn_c, T[L["kn"]], dh, "kn_c")
                QNK = ["qn_c"]
                KNK = ["kn_c"]
                if rope:
                    cosT = AR.alloc([128, NT], F32)
                    sinT = AR.alloc([128, NT], F32)
                    ri = 0 if dh == 64 else 2
                    S.dma("sp", lambda e: e.dma_start(out=cosT, in_=T["rope"][ri]), writes=["cosT"])
                    S.dma("sp", lambda e: e.dma_start(out=sinT, in_=T["rope"][ri + 1]), writes=["sinT"])
                    pm = pa_b if dh == 64 else pc_b
                    r1 = [AR.alloc([128, 512], F32) for _ in range(2)]
                    qb16 = [AR.alloc([128, 512], BF16) for _ in range(2)]
                if L["sink"]:
                    S.dma("sp", lambda e: e.dma_start(out=tst2[0:1, 0:16], in_=T[L["sink"]].rearrange("(o q) -> o q", o=1)),
                          writes=[("tst2", 0)], key=("tst2", 0))
                    S.op("pe", lambda e: e.matmul(ps[:, 7, 0:16], ones_f[0:2, :], tst2[0:2, 0:16], start=True, stop=True),
                         reads=[("tst2", 0), "tst2z", "ones_f"], writes=[PSK[7], ("tst2", 0)])
                    S.op("act", lambda e: e.activation(out=esink, in_=ps[:, 7, 0:16], func=AF.Exp), reads=[PSK[7]], writes=["esink"])
                if p == 0:
                    kst = [AR.alloc([128, ckv], F32) for _ in range(1)]
                    vst = [AR.alloc([128, ckv], F32) for _ in range(1)]
                    kf32 = AR.alloc([128, nkc, NT], F32)
                S.op("pool", lambda e: e.memset(vx, 1.0), writes=["vx_init"])

                if ATT_STOP <= 0:
                    AR.release(m)
                    return
                for tt in range(8):
                    b = 4 + tt % 2
                    for kc in range(8):
                        mm(ps[:, b, 0:ckv], xnT[:, kc, tt * 128:(tt + 1) * 128], wbig[:, kc, D + ckv:D + 2 * ckv], kc == 0, kc == 7,
                           [XK[kc], WB[1]], PSK[b])
                    if p == 0:
                        v_ = vst[0]
                        vk = ("vst", 0)
                        S.op("dve", lambda e, b=b, v_=v_: e.tensor_copy(out=v_, in_=ps[:, b, 0:ckv]), reads=[PSK[b]], writes=[vk])
                        S.op("act", lambda e, v_=v_, tt=tt: e.activation(out=vx[:, tt, :, 0:dh], in_=v_.rearrange("p (g d) -> p g d", g=4),
                                                                        func=AF.Copy),
                             reads=[vk, "vx_init"], writes=[("vx", tt)])
                        S.dma("sp", lambda e, v_=v_, tt=tt: e.dma_start(out=T[L["nv"]][tt * 128:(tt + 1) * 128, :], in_=v_), reads=[vk])
                    else:
                        S.op("act", lambda e, b=b, tt=tt: e.activation(out=vx[:, tt, :, 0:dh], in_=ps[:, b, 0:ckv].rearrange("p (g d) -> p g d", g=4),
                                                                       func=AF.Copy),
                             reads=[PSK[b], "vx_init"], writes=[("vx", tt)])
                if ATT_STOP <= 1:
                    AR.release(m)
                    return
                def lhs_cols(kind, c):
                    if kind == "q":
                        if dh == 128:
                            return [(c * 128, 128, 0)]
                        mq, i = c // 4, c % 4
                        return [((8 * mq + i) * 64, 64, 0), ((8 * mq + 4 + i) * 64, 64, 64)]
                    return [(D + c * 128, 128, 0)]

                blocks = [(kind, c, hf) for kind, nch in (("k", nkc), ("q", 8)) for c in range(nch) for hf in range(2)]
                nb_ = len(blocks)
                sq3 = sqb + [AR.alloc([128, 512], BF16)]

                def stage_a(i):
                    kind, c, hf = blocks[i]
                    b = i % 3
                    sl = slice(hf * 512, (hf + 1) * 512)
                    for (c0, w_, po) in lhs_cols(kind, c):
                        for kc in range(8):
                            mm(ps[po:po + w_, b, :], wbig[:, kc, c0:c0 + w_], xnT[:, kc, sl], kc == 0, kc == 7,
                               [XK[kc], WB[0], WB[1]], PSK[b])
                    sq_ = sq3[i % 3]
                    S.op("act", lambda e, sq_=sq_, b=b: e.activation(out=sq_, in_=ps[:, b, :], func=AF.Square), reads=[PSK[b]], writes=[("sqb", i % 3)])

                def stage_b(i):
                    kind, c, hf = blocks[i]
                    b = i % 3
                    b2 = 3 + i % 2
                    sl = slice(hf * 512, (hf + 1) * 512)
                    sq_ = sq3[i % 3]
                    mm(ps[:, b2, :], bd64_b if dh == 64 else ones_b, sq_, True, True, [("sqb", i % 3), "cst_misc", "ones_b"], PSK[b2])
                    rs_ = rs[i % 2]
                    rk = ("rs", i % 2)
                    S.op("act", lambda e, b2=b2: e.activation(out=lnv, in_=ps[:, b2, :], func=AF.Ln, bias=EPS, scale=1.0 / dh),
                         reads=[PSK[b2]], writes=["lnv"])
                    S.op("act", lambda e, rs_=rs_: e.activation(out=rs_, in_=lnv, func=AF.Exp, scale=-0.5), reads=["lnv"], writes=[rk])
                    gcol, gk = (qn_c, QNK) if kind == "q" else (kn_c, KNK)
                    dstT = qT if kind == "q" else kT
                    dk = (kind + "T", c, hf)
                    if not rope:
                        if kind == "k":
                            S.op("dve", lambda e, c=c, sl=sl, b=b, rs_=rs_: e.scalar_tensor_tensor(
                                out=kf32[:, c, sl], in0=ps[:, b, :], scalar=kn_c[:, 0:1], in1=rs_, op0=ALU.mult, op1=ALU.mult),
                                reads=[PSK[b], rk] + gk, writes=[("kf32", c, hf)])
                            S.op("pool", lambda e, c=c, sl=sl: e.tensor_copy(out=kT[:, c, sl], in_=kf32[:, c, sl]),
                                 reads=[("kf32", c, hf)], writes=[dk])
                        else:
                            S.op("dve", lambda e, c=c, sl=sl, b=b, rs_=rs_, gcol=gcol, dstT=dstT: e.scalar_tensor_tensor(
                                out=dstT[:, c, sl], in0=ps[:, b, :], scalar=gcol[:, 0:1], in1=rs_, op0=ALU.mult, op1=ALU.mult),
                                reads=[PSK[b], rk] + gk, writes=[dk])
                    else:
                        q_, qk_ = qf[i % 2], ("qf", i % 2)
                        qb_, qbk = qb16[i % 2], ("qb16", i % 2)
                        S.op("dve", lambda e, q_=q_, b=b, rs_=rs_, gcol=gcol: e.scalar_tensor_tensor(
                            out=q_, in0=ps[:, b, :], scalar=gcol[:, 0:1], in1=rs_, op0=ALU.mult, op1=ALU.mult),
                            reads=[PSK[b], rk] + gk, writes=[qk_])
                        S.op("pool", lambda e, q_=q_, qb_=qb_: e.tensor_copy(out=qb_, in_=q_), reads=[qk_], writes=[qbk])

                def stage_c(i):
                    kind, c, hf = blocks[i]
                    b3 = 5 + i % 3
                    sl = slice(hf * 512, (hf + 1) * 512)
                    dstT = qT if kind == "q" else kT
                    dk = (kind + "T", c, hf)
                    q_, qk_ = qf[i % 2], ("qf", i % 2)
                    qb_, qbk = qb16[i % 2], ("qb16", i % 2)
                    r_, r1k = r1[i % 2], ("r1", i % 2)
                    mm(ps[:, b3, :], pm, qb_, True, True, [qbk, "cst_misc"], PSK[b3])
                    S.op("dve", lambda e, r_=r_, b3=b3, sl=sl: e.tensor_tensor(out=r_, in0=ps[:, b3, :], in1=sinT[:, sl], op=ALU.mult),
                         reads=[PSK[b3], "sinT"], writes=[r1k])
                    S.op("pool", lambda e, q_=q_, sl=sl: e.tensor_tensor(out=q_, in0=q_, in1=cosT[:, sl], op=ALU.mult),
                         reads=[qk_, "cosT"], writes=[qk_])
                    S.op("dve", lambda e, q_=q_, r_=r_, c=c, sl=sl, dstT=dstT: e.tensor_tensor(out=dstT[:, c, sl], in0=q_, in1=r_, op=ALU.add),
                         reads=[qk_, r1k], writes=[dk])

                for step in range(nb_ + 2):
                    if step < nb_:
                        stage_a(step)
                    if 0 <= step - 1 < nb_:
                        stage_b(step - 1)
                    if rope and 0 <= step - 2 < nb_:
                        stage_c(step - 2)
                if ATT_STOP <= 2:
                    AR.release(m)
                    return
                if p == 0:
                    for tt in range(8):
                        b = 6 + tt % 2
                        for c in range(nkc):
                            S.op("pe", lambda e, b=b, c=c, tt=tt: e.transpose(ps[:, b, c * 128:(c + 1) * 128], kf32[:, c, tt * 128:(tt + 1) * 128], ident_f),
                                 reads=[("kf32", c, tt // 4), "ident_f"], writes=[PSK[b]], acc=True)
                        k_ = kst[0]
                        kk = ("kst", 0)
                        S.op("dve", lambda e, b=b, k_=k_: e.tensor_copy(out=k_, in_=ps[:, b, 0:ckv]), reads=[PSK[b]], writes=[kk])
                        S.dma("sp", lambda e, k_=k_, tt=tt: e.dma_start(out=T[L["nk"]][tt * 128:(tt + 1) * 128, :], in_=k_), reads=[kk])

                if p == 1:
                    cst_ = [AR.alloc([128, ckv], F32) for _ in range(1)]
                    cb_ = [AR.alloc([128, ckv], BF16) for _ in range(1)]
                    S.op("pool", lambda e: e.memset(vcx, 1.0), writes=["vcx_init"])
                    for t4 in range(4):
                        c_ = cst_[0]
                        ck_ = ("cst_", 0)
                        S.dma("sp", lambda e, c_=c_, t4=t4: e.dma_start(out=c_, in_=T[L["cv"]][t4 * 128:(t4 + 1) * 128, :]), writes=[ck_])
                        S.op("dve", lambda e, c_=c_, t4=t4: e.tensor_copy(out=vcx[:, t4, :, 0:dh], in_=c_.rearrange("p (g d) -> p g d", g=4)),
                             reads=[ck_, "vcx_init"], writes=[("vcx", t4)])
                    for t4 in range(4):
                        c_ = cst_[0]
                        ck_ = ("cst_", 0)
                        bb = cb_[0]
                        bk = ("cb_", 0)
                        S.dma("sp", lambda e, c_=c_, t4=t4: e.dma_start(out=c_, in_=T[L["ck"]][t4 * 128:(t4 + 1) * 128, :]), writes=[ck_])
                        S.op("dve", lambda e, c_=c_, bb=bb: e.tensor_copy(out=bb, in_=c_), reads=[ck_], writes=[bk])
                        b = 6 + t4 % 2
                        pb = ps[:, b, 0:64 * nkc].bitcast(BF16).rearrange("p (c t) -> p c t", c=nkc)
                        for c in range(nkc):
                            S.op("pe", lambda e, pb=pb, c=c, bb=bb: e.transpose(pb[:, c, :], bb[:, c * 128:(c + 1) * 128], ident_b),
                                 reads=[bk, "ident_b"], writes=[PSK[b]], acc=True)
                        S.op("dve", lambda e, pb=pb, t4=t4: e.tensor_copy(out=kcT[:, :, t4 * 128:(t4 + 1) * 128], in_=pb),
                             reads=[PSK[b]], writes=[("kcT", t4)])

                if ATT_STOP <= 3:
                    AR.release(m)
                    return
                nkt_max = 7 if L["kind"] == "A" else 12
                S.barrier()
                AR.release(tmp_mark)
                PT = [AR.alloc([128, nkt_max, G * 128], BF16) for _ in range(2)]
                otok = [AR.alloc([128, D], BF16) for _ in range(2)]
                OT = xnT
                den = AR.alloc([128, 4], F32)
                rden = AR.alloc([128, 4], F32)
                sbc = [0]

                def key_tiles(qt):
                    s_ = qt // (slen // 128)
                    kts = []
                    if p == 1:
                        kts += [("c", t4, None) for t4 in range(4)]
                        if L["kind"] == "A":
                            if qt > 0:
                                kts.append(("n", qt - 1, mlo_b))
                            kts.append(("n", qt, None))
                            if qt < 7:
                                kts.append(("n", qt + 1, mhi_b))
                        else:
                            kts += [("n", t8, None) for t8 in range(8)]
                    else:
                        kts += [("n", s_ * 2 + t2, None) for t2 in range(2)]
                    return kts

                def emit_scores(qt, g, it):
                    kts = key_tiles(qt)
                    pt = PT[it % 2]
                    ptk = ("PT", it % 2)
                    if dh == 64:
                        mq, e2_ = g // 2, g % 2
                        prt = slice(64 * e2_, 64 * e2_ + 64)
                        rhs = qT[prt, 4 * mq:4 * mq + 4, qt * 128:(qt + 1) * 128]
                        qreads = [("qT", 4 * mq + i, qt // 4) for i in range(4)]
                        kch = mq
                    else:
                        prt = slice(0, 128)
                        rhs = qT[:, 2 * g:2 * g + 2, qt * 128:(qt + 1) * 128]
                        qreads = [("qT", 2 * g + i, qt // 4) for i in range(2)]
                        kch = g
                    for ki, (src, ti, msk) in enumerate(kts):
                        b = sbc[0] % 4
                        sbc[0] += 1
                        if src == "c":
                            lhsT = kcT[prt, kch, ti * 128:(ti + 1) * 128]
                            kr = [("kcT", ti)]
                        else:
                            lhsT = kT[prt, kch, ti * 128:(ti + 1) * 128]
                            kr = [("kT", kch, ti // 4)]
                        mm(ps[:, b, 0:G * 128], lhsT, rhs, True, True, kr + qreads, PSK[b])
                        S.op("act", lambda e, pt=pt, ki=ki, b=b: e.activation(out=pt[:, ki, :], in_=ps[:, b, 0:G * 128], func=AF.Exp, scale=scale),
                             reads=[PSK[b]], writes=[(ptk, ki)])
                        if msk is not None:
                            S.op("pool", lambda e, pt=pt, ki=ki, msk=msk: e.tensor_tensor(
                                out=pt[:, ki, :], in0=pt[:, ki, :], in1=msk[:, 0:G, :].rearrange("p g q -> p (g q)"), op=ALU.mult),
                                reads=[(ptk, ki), "cst_misc"], writes=[(ptk, ki)])

                def emit_pv(qt, g, it):
                    kts = key_tiles(qt)
                    pt = PT[it % 2]
                    ptk = ("PT", it % 2)
                    ob = 4 + it % 2
                    ot = otok[qt % 2]
                    otk = ("otok", qt % 2)
                    opv = ps[:, ob, 0:G * (dh + 1)].rearrange("p (i d) -> p i d", i=G)
                    for i in range(G):
                        for ki, (src, ti, msk) in enumerate(kts):
                            if src == "c":
                                vr = vcx[:, ti, g, 0:dh + 1]
                                vk = [("vcx", ti)]
                            else:
                                vr = vx[:, ti, g, 0:dh + 1]
                                vk = [("vx", ti)]
                            mm(opv[:, i, :], pt[:, ki, i * 128:(i + 1) * 128], vr, ki == 0, ki == len(kts) - 1,
                               [(ptk, ki)] + vk, PSK[ob])
                    if esink is not None:
                        S.op("dve", lambda e, opv=opv, g=g: e.tensor_tensor(out=den[:, 0:G], in0=opv[:, :, dh], in1=esink[:, g * G:(g + 1) * G], op=ALU.add),
                             reads=[PSK[ob], "esink"], writes=["den"])
                        S.op("dve", lambda e: e.reciprocal(out=rden[:, 0:G], in_=den[:, 0:G]), reads=["den"], writes=["rden"])
                    else:
                        S.op("dve", lambda e, opv=opv: e.reciprocal(out=rden[:, 0:G], in_=opv[:, :, dh]), reads=[PSK[ob]], writes=["rden"])
                    for i in range(G):
                        h = g * G + i
                        S.op("dve", lambda e, opv=opv, i=i, h=h, ot=ot: e.tensor_scalar(out=ot[:, h * dh:(h + 1) * dh], in0=opv[:, i, 0:dh],
                                                                                         scalar1=rden[:, i:i + 1], scalar2=None, op0=ALU.mult),
                             reads=[PSK[ob], "rden"], writes=[(otk, h)])
                    if g == 3:
                        b = 6 + qt % 2
                        pb = ps[:, b, :].bitcast(BF16).rearrange("p (c t) -> p c t", c=8)
                        for c in range(8):
                            S.op("pe", lambda e, pb=pb, c=c, ot=ot: e.transpose(pb[:, c, :], ot[:, c * 128:(c + 1) * 128], ident_b),
                                 reads=[(otk, h) for h in range(H)] + ["ident_b"], writes=[PSK[b]], acc=True)
                        S.op("act", lambda e, pb=pb, qt=qt: e.activation(out=OT[:, :, qt * 128:(qt + 1) * 128], in_=pb, func=AF.Copy),
                             reads=[PSK[b]], writes=[("OT", qt)] + XK)

                items = [(qt, g) for qt in range(8) for g in range(4)]
                emit_scores(items[0][0], items[0][1], 0)
                for it, (qt, g) in enumerate(items):
                    if it + 1 < len(items):
                        emit_scores(items[it + 1][0], items[it + 1][1], it + 1)
                    emit_pv(qt, g, it)
                bi = 0
                for oc in range(8):
                    for hf in range(2):
                        b = bi % 4
                        bi += 1
                        for kc in range(8):
                            mm(ps[:, b, :], wo_b[:, kc, oc * 128:(oc + 1) * 128], OT[:, kc, hf * 512:(hf + 1) * 512], kc == 0, kc == 7,
                               ["wo_b"] + [("OT", hf * 4 + q) for q in range(4)], PSK[b])
                        resid(oc, hf, b, 16)
                AR.release(m)

            def s5_layer(l):
                CH = 128 // nseq
                m = AR.mark()
                norm_mod(A1, "A1", 0, perm8=True)
                U = AR.alloc([128, 64, 128], BF16)
                TB = AR.alloc([128, 2, 64, 128], BF16)
                SS = AR.alloc([128, 2, 64, 128], F32)
                ALL = AR.alloc([128, 2, 64], F32)
                ALS = AR.alloc([128, 2, 64], F32)
                xf = xnT.rearrange("q a n -> q (a n)").bitcast(F32)
                H0v = xf[:, 0:128 * nseq].rearrange("q (r g s) -> q r g s", r=2, g=64)
                L8 = AR.alloc([128, 2, 128], F32)
                FS = AR.alloc([128, 2, 64], F32)
                FSo = AR.alloc([128, 2, 128], F32)
                tt_ = [[xf[:, 128 * nseq * (1 + 2 * a_ + b_):128 * nseq * (2 + 2 * a_ + b_)].rearrange("q (r g s) -> q r g s", r=2, g=64)
                        for b_ in range(2)] for a_ in range(2)]
                S.dma("sp", lambda e: e.dma_start(out=T["D1"][p].rearrange("(fc q) n -> q fc n", q=128), in_=xnT), reads=XK, writes=["D1"])
                d1v = T["D1"][p].rearrange("(g c) (t k) -> t c g k", c=16, t=8)
                for t in range(8):
                    S.dma("sp", lambda e, t=t: e.dma_start(out=U[16 * t:16 * (t + 1)], in_=d1v[t]), reads=["D1"], writes=[("U", t)])
                UK = [("U", t) for t in range(8)]
                for r_ in range(2):
                    for g4 in range(4):
                        S.dma("sp", lambda e, r_=r_, g4=g4: e.dma_start(out=TB[:, r_, g4 * 16:(g4 + 1) * 16, :],
                                                                         in_=T["TS"][r_, g4 * 16:(g4 + 1) * 16].rearrange("g k n -> k g n")),
                              writes=[("TB", r_, g4)])
                S.dma("sp", lambda e: e.dma_start(out=L8[0:64], in_=T["TP8"].rearrange("r (d g) p -> g r d p", d=2)), writes=["L8"])
                for r_ in range(2):
                    S.op("pe", lambda e, r_=r_: e.transpose(ps[:, 7, 0:64], L8[0:64, r_, :], ident_f[0:64, 0:64]), reads=["L8", "ident_f"], writes=[PSK[7]])
                    if r_ == 0:
                        S.op("dve", lambda e: e.tensor_copy(out=ALL[:, 0, :], in_=ps[:, 7, 0:64]), reads=[PSK[7]], writes=["AL"])
                        S.op("dve", lambda e: e.tensor_copy(out=ALL[:, 1, :], in_=ps[:, 7, 0:64]), reads=[PSK[7]], writes=["AL"])
                    else:
                        S.op("dve", lambda e: e.tensor_scalar(out=ALS[:, 0, :], in0=ps[:, 7, 0:64], scalar1=-1.0, scalar2=None, op0=ALU.mult), reads=[PSK[7]], writes=["AL"])
                        S.op("dve", lambda e: e.tensor_copy(out=ALS[:, 1, :], in_=ps[:, 7, 0:64]), reads=[PSK[7]], writes=["AL"])
                if p == 1:
                    for r_ in range(2):
                        S.dma("sp", lambda e, r_=r_: e.dma_start(out=L8[0:64, r_, :].rearrange("g (d p) -> g d p", d=2),
                                                                  in_=T["st1"][:, r_].rearrange("d g p -> g d p")), reads=["L8"], writes=["L8"], key=("L8", r_))
                    for r_ in range(2):
                        S.op("pe", lambda e, r_=r_: e.transpose(ps[:, 7, 0:64], L8[0:64, r_, :], ident_f[0:64, 0:64]), reads=["L8", "ident_f"], writes=[PSK[7]])
                        S.op("dve", lambda e, r_=r_: e.tensor_copy(out=H0v[:, r_, :, 0], in_=ps[:, 7, 0:64]), reads=[PSK[7]], writes=["H0"] + XK)
                else:
                    S.op("dve", lambda e: e.memset(H0v, 0.0), writes=["H0"] + XK)
                for gq in range(16):
                    for r_ in range(2):
                        b = (gq * 2 + r_) % 4
                        for gi in range(4):
                            g = gq * 4 + gi
                            S.op("pe", lambda e, b=b, gi=gi, g=g, r_=r_: e.matmul(ps[:, b, gi * 128:(gi + 1) * 128], TB[:, r_, g, :], U[:, g, :], start=True, stop=True),
                                 reads=UK + [("TB", r_, g // 16)], writes=[PSK[b]], acc=True)
                        eng = "act" if r_ == 0 else "dve"
                        if r_ == 0:
                            S.op("act", lambda e, b=b, gq=gq: e.activation(out=SS[:, 0, gq * 4:(gq + 1) * 4, :].rearrange("q g k -> q (g k)"), in_=ps[:, b, :], func=AF.Copy),
                                 reads=[PSK[b]], writes=[("SS", 0)])
                        else:
                            S.op("dve", lambda e, b=b, gq=gq: e.tensor_copy(out=SS[:, 1, gq * 4:(gq + 1) * 4, :].rearrange("q g k -> q (g k)"), in_=ps[:, b, :]),
                                 reads=[PSK[b]], writes=[("SS", 1)])
                S.barrier()
                NS, CHV = nseq, CH
                SSv = SS.rearrange("q r g (s k) -> q r g s k", k=CHV)
                for d, eng in ((0, "dve"), (1, "pool")):
                    hs = slice(64 * d, 64 * d + 64)
                    order = list(range(CHV)) if d == 0 else list(range(CHV - 1, -1, -1))
                    t1, t2 = tt_[d]
                    k1, k2, kS = ("rt1", d), ("rt2", d), ("SSd", d)
                    all_ = ALL[hs].unsqueeze(3).broadcast_to([64, 2, 64, NS])
                    als_ = ALS[hs].unsqueeze(3).broadcast_to([64, 2, 64, NS])
                    for ii, kk in enumerate(order):
                        if ii == 0:
                            prev, prev_sw = H0v[hs], H0v[hs, ::-1]
                        else:
                            kp = order[ii - 1]
                            prev, prev_sw = SSv[hs, :, :, :, kp], SSv[hs, ::-1, :, :, kp]
                        cur = SSv[hs, :, :, :, kk]
                        S.op(eng, lambda e, hs=hs, t1=t1, prev=prev, all_=all_: e.tensor_tensor(out=t1[hs], in0=prev, in1=all_, op=ALU.mult),
                             reads=[kS, "AL", "H0"], writes=[k1])
                        S.op(eng, lambda e, hs=hs, t2=t2, prev_sw=prev_sw, als_=als_: e.tensor_tensor(out=t2[hs], in0=prev_sw, in1=als_, op=ALU.mult),
                             reads=[kS, "AL", "H0"], writes=[k2])
                        S.op(eng, lambda e, hs=hs, t1=t1, t2=t2: e.tensor_tensor(out=t1[hs], in0=t1[hs], in1=t2[hs], op=ALU.add), reads=[k1, k2], writes=[k1])
                        S.op(eng, lambda e, hs=hs, t1=t1, cur=cur: e.tensor_tensor(out=cur, in0=cur, in1=t1[hs], op=ALU.add), reads=[kS, k1], writes=[kS])
                S.barrier()
                if p == 0:
                    for s_ in range(nseq):
                        for r_ in range(2):
                            S.op("dve", lambda e, r_=r_, s_=s_: e.tensor_copy(out=FS[0:64, r_, :], in_=SS[0:64, r_, :, s_ * CH + CH - 1]), reads=[], writes=["FS"])
                            S.op("dve", lambda e, r_=r_, s_=s_: e.tensor_copy(out=FS[64:128, r_, :], in_=SS[64:128, r_, :, s_ * CH]), reads=[], writes=["FS"])
                            S.op("pe", lambda e, r_=r_: e.transpose(ps[0:64, 7, r_ * 128:(r_ + 1) * 128], FS[:, r_, :], ident_f), reads=["FS", "ident_f"], writes=[PSK[7]], acc=True)
                        S.op("dve", lambda e: e.tensor_copy(out=FSo[0:64].rearrange("g r n -> g (r n)"), in_=ps[0:64, 7, 0:256]), reads=[PSK[7]], writes=["FSo"])
                        for r_ in range(2):
                            S.dma("sp", lambda e, s_=s_, r_=r_: e.dma_start(out=T["nst"][s_][:, r_].rearrange("d g p -> g d p"),
                                                                           in_=FSo[0:64, r_, :].rearrange("g (d p) -> g d p", d=2)), reads=["FSo"], key=("nst", r_))
                for r_ in range(2):
                    for s_ in range(nseq):
                        k0 = s_ * CH
                        S.op("act", lambda e, r_=r_, k0=k0: e.activation(out=TB[0:64, r_, :, k0 + 1:k0 + CH], in_=SS[0:64, r_, :, k0:k0 + CH - 1], func=AF.Copy),
                             reads=[], writes=["Hin"])
                        S.op("act", lambda e, r_=r_, k0=k0, s_=s_: e.activation(out=TB[0:64, r_, :, k0], in_=H0v[0:64, r_, :, s_], func=AF.Copy), reads=["H0"], writes=["Hin"])
                        S.op("dve", lambda e, r_=r_, k0=k0: e.tensor_copy(out=TB[64:128, r_, :, k0:k0 + CH - 1], in_=SS[64:128, r_, :, k0 + 1:k0 + CH]),
                             reads=[], writes=["Hin"])
                        S.op("dve", lambda e, r_=r_, k0=k0, s_=s_: e.tensor_copy(out=TB[64:128, r_, :, k0 + CH - 1], in_=H0v[64:128, r_, :, s_]), reads=["H0"], writes=["Hin"])
                S.barrier()
                SSb = SS.rearrange("q r g k -> q (r g k)").bitcast(BF16)
                EL = SSb[:, 0:16384].rearrange("q (r g n) -> q r g n", r=2, g=64)
                ML = SSb[:, 16384:24576].rearrange("q (g n) -> q g n", g=64)
                YC = SSb[:, 24576:32768].rearrange("q (g n) -> q g n", g=64)
                for r_ in range(2):
                    for g4 in range(4):
                        S.dma("sp", lambda e, r_=r_, g4=g4: e.dma_start(out=EL[:, r_, g4 * 16:(g4 + 1) * 16, :],
                                                                         in_=T["TE"][r_, g4 * 16:(g4 + 1) * 16].rearrange("g k n -> k g n")),
                              writes=[("EL", r_, g4)])
                for g4 in range(4):
                    S.dma("sp", lambda e, g4=g4: e.dma_start(out=ML[:, g4 * 16:(g4 + 1) * 16, :], in_=T["TM"][g4 * 16:(g4 + 1) * 16].rearrange("g k n -> k g n")),
                          writes=[("ML", g4)])
                gu = [AR.alloc([128, 512], F32) for _ in range(2)]
                gs = [AR.alloc([128, 512], F32) for _ in range(2)]
                for gq in range(16):
                    b = gq % 4
                    for gi in range(4):
                        g = gq * 4 + gi
                        o_ = ps[:, b, gi * 128:(gi + 1) * 128]
                        S.op("pe", lambda e, o_=o_, g=g: e.matmul(o_, ML[:, g, :], U[:, g, :], start=True, stop=False), reads=[("ML", g // 16)] + UK, writes=[PSK[b]], acc=True)
                        S.op("pe", lambda e, o_=o_, g=g: e.matmul(o_, EL[:, 0, g, :], TB[:, 0, g, :], start=False, stop=False), reads=[("EL", 0, g // 16), "Hin"], writes=[PSK[b]], acc=True)
                        S.op("pe", lambda e, o_=o_, g=g: e.matmul(o_, EL[:, 1, g, :], TB[:, 1, g, :], start=False, stop=True), reads=[("EL", 1, g // 16), "Hin"], writes=[PSK[b]], acc=True)
                    u_, s2 = gu[gq % 2], gs[gq % 2]
                    uk, sk = ("gu", gq % 2), ("gs", gq % 2)
                    S.op("act", lambda e, u_=u_, b=b: e.activation(out=u_, in_=ps[:, b, :], func=AF.Square), reads=[PSK[b]], writes=[uk])
                    S.op("dve", lambda e, u_=u_: e.tensor_scalar(out=u_, in0=u_, scalar1=0.044715, scalar2=1.0, op0=ALU.mult, op1=ALU.add), reads=[uk], writes=[uk])
                    S.op("dve", lambda e, u_=u_, b=b: e.tensor_tensor(out=u_, in0=u_, in1=ps[:, b, :], op=ALU.mult), reads=[uk, PSK[b]], writes=[uk])
                    S.op("act", lambda e, u_=u_, s2=s2: e.activation(out=s2, in_=u_, func=AF.Sigmoid, scale=1.5957691216057308), reads=[uk], writes=[sk])
                    S.op("dve", lambda e, s2=s2, b=b, gq=gq: e.tensor_tensor(out=YC[:, gq * 4:(gq + 1) * 4, :].rearrange("q g k -> q (g k)"), in0=s2, in1=ps[:, b, :], op=ALU.mult),
                         reads=[sk, PSK[b]], writes=["YC"])
                d2v = T["D2"][p].rearrange("(g c) (t k) -> t c g k", c=16, t=8)
                for t in range(8):
                    S.dma("sp", lambda e, t=t: e.dma_start(out=d2v[t], in_=YC[16 * t:16 * (t + 1)]), reads=["YC"], writes=["D2"], key=("D2w", t))
                S.dma("sp", lambda e: e.dma_start(out=xnT, in_=T["D2"][p].rearrange("(fc q) n -> q fc n", q=128)), reads=["D2"], writes=XK, key="D2r")
                S.barrier()
                AR.release(m)
                m = AR.mark()
                wg = AR.alloc([128, 8, 2 * D], BF16)
                sg2 = [AR.alloc([128, 512], F32) for _ in range(2)]
                mx = [AR.alloc([128, 512], F32) for _ in range(2)]
                for h2 in range(2):
                    S.dma("pool", lambda e, h2=h2: e.dma_start(out=wg[:, :, h2 * D:(h2 + 1) * D],
                                                               in_=T["l1_w_glu"][:, h2 * D:(h2 + 1) * D].rearrange("(kc q) n -> q kc n", q=128)),
                          writes=[("wg", h2)])
                it = 0
                for oc in range(8):
                    for hf in range(2):
                        bA, bB = (it % 2) * 2, (it % 2) * 2 + 1
                        s_, x_ = sg2[it % 2], mx[it % 2]
                        sk, xk_ = ("sg2", it % 2), ("mx", it % 2)
                        it += 1
                        for kc in range(8):
                            mm(ps[:, bA, :], wg[:, kc, oc * 128:(oc + 1) * 128], xnT[:, kc, hf * 512:(hf + 1) * 512], kc == 0, kc == 7, [("wg", 0), XK[kc]], PSK[bA])
                        for kc in range(8):
                            mm(ps[:, bB, :], wg[:, kc, D + oc * 128:D + (oc + 1) * 128], xnT[:, kc, hf * 512:(hf + 1) * 512], kc == 0, kc == 7, [("wg", 1), XK[kc]], PSK[bB])
                        S.op("act", lambda e, s_=s_, bB=bB: e.activation(out=s_, in_=ps[:, bB, :], func=AF.Sigmoid), reads=[PSK[bB]], writes=[sk])
                        S.op("dve", lambda e, s_=s_, x_=x_, bA=bA: e.tensor_tensor(out=x_, in0=ps[:, bA, :], in1=s_, op=ALU.mult), reads=[PSK[bA], sk], writes=[xk_])
                        hv = hT[:, oc, :].rearrange("q (k t) -> q t k", t=8)[:, 4 * hf:4 * hf + 4, :]
                        S.op("dve", lambda e, x_=x_, hv=hv, oc=oc: e.scalar_tensor_tensor(out=hv, in0=x_.rearrange("q (t k) -> q t k", t=4), scalar=modT[:, 16 + oc:17 + oc],
                                                                                        in1=hv, op0=ALU.mult, op1=ALU.add),
                             reads=[xk_, "modT", HK[oc]], writes=[HK[oc]])
                AR.release(m)

            for l in range(NLAYERS):
                if STAGE & 1:
                    compute_mod(l)
                S.barrier()
                if STAGE & 2:
                    if LAYERS[l]["kind"] == "B":
                        if not SKIP_S5:
                            s5_layer(l)
                    else:
                        attention(l)
                S.barrier()
                if STAGE & 4:
                    ffn(l)
                S.barrier()

            m0 = AR.mark()
            yst = [AR.alloc([128, D], F32) for _ in range(2)]
            for tt in range(8):
                y_ = yst[tt % 2]
                yk = ("yst", tt % 2)
                for hf in range(2):
                    b = (tt * 2 + hf) % 8
                    for q in range(4):
                        kc = hf * 4 + q
                        S.op("pe", lambda e, b=b, q=q, kc=kc, tt=tt: e.transpose(ps[:, b, q * 128:(q + 1) * 128], hT[:, kc, tt * 128:(tt + 1) * 128], ident_f),
                             reads=[HK[kc], "ident_f"], writes=[PSK[b]], acc=True)
                    if hf:
                        S.op("act", lambda e, b=b, y_=y_: e.activation(out=y_[:, 512:1024], in_=ps[:, b, :], func=AF.Copy), reads=[PSK[b]], writes=[(yk, 1)])
                    else:
                        S.op("dve", lambda e, b=b, y_=y_: e.tensor_copy(out=y_[:, 0:512], in_=ps[:, b, :]), reads=[PSK[b]], writes=[(yk, 0)])
                S.dma("sp", lambda e, y_=y_, tt=tt: e.dma_start(out=T["y"][p, tt * 128:(tt + 1) * 128, :], in_=y_), reads=[(yk, 0), (yk, 1)], key=yk)
            S.barrier()
            AR.release(m0)

        pm_mark = AR.mark()
        prologue_mod_start()
        if not SKIP_S5 and NLAYERS > 1 and (STAGE & 2):
            s5_tables()
        prologue_mod_finish()
        AR.release(pm_mark)
        run_path(0)
        run_path(1)
        S.barrier()
        print('n_sems', len(S.sem_objs), 'insts', {e: len(S.prog[e]) for e in ENGS})
        S.emit()
    return nc


def _consts():
    cst = np.zeros((9, 128, 128), np.float32)
    for tt_ in range(8):
        for cc_ in range(16):
            cst[CST_R16, cc_, tt_ * 16 + cc_] = 1.0
    cst[CST_IDENT] = np.eye(128)
    bd = np.zeros((128, 128), np.float32)
    bd[:64, :64] = 1
    bd[64:, 64:] = 1
    cst[CST_BD64] = bd
    kk = np.arange(128)[:, None]
    qq = np.arange(128)[None, :]
    cst[CST_MLO] = (qq <= kk)
    cst[CST_MHI] = (kk <= qq)

    def perm(dh):
        qt = dh // 4
        half = dh // 2
        Pm = np.zeros((128, 128), np.float32)
        for m in range(128):
            d = m % dh
            if (d % half) < qt:
                Pm[m, m + qt] = -1.0
            else:
                Pm[m, m - qt] = 1.0
        return Pm.T.copy()

    cst[CST_PA] = perm(64)
    cst[CST_PC] = perm(128)
    t1 = np.arange(8)
    tau_p = np.repeat(t1, 16)[:, None]
    tau = np.repeat(t1, 16)[None, :]
    cst[CST_S5F] = (tau >= tau_p)
    cst[CST_S5B] = (tau <= tau_p)
    rope = np.zeros((4, 128, NT), np.float32)
    t = np.arange(NT)
    row = (t // 64).astype(np.float32)
    col = (t % 64).astype(np.float32)
    for idx, dh in ((0, 64), (2, 128)):
        half, qt = dh // 2, dh // 4
        freqs = (1.0 / (np.float32(10000.0) ** (np.arange(qt, dtype=np.float32) / np.float32(qt)))).astype(np.float32)
        for pp in range(128):
            d = pp % dh
            pos = row if d < half else col
            ang = (pos * freqs[d % qt]).astype(np.float32)
            rope[idx, pp] = np.cos(ang)
            rope[idx + 1, pp] = np.sin(ang)
    return cst, rope


_PROG = {}


def kernel(**inp):
    f32 = lambda a: np.ascontiguousarray(np.asarray(a, dtype=np.float32))
    inp = {k: f32(v) for k, v in inp.items()}
    if "nc" not in _PROG:
        _PROG["nc"] = build_program()
    nc = _PROG["nc"]
    cst, rope = _consts()
    shared = {}
    for name, shape in IN_SPECS:
        if name in inp:
            shared[name] = inp[name].reshape(shape)
    shared["cst"] = cst
    shared["rope"] = rope
    in_maps = []
    for c in range(8):
        m = dict(shared)
        m["xin"] = np.stack([inp["x_prompt"][4 * c:4 * c + 4].reshape(NT, D), inp["x_sample"][c]])
        m["cond"] = np.stack([inp["c_ctx"], inp["c"][c]])
        m["ck0"] = inp["cache_l0_k"][c].reshape(512, 256)
        m["cv0"] = inp["cache_l0_v"][c].reshape(512, 256)
        m["st1"] = inp["state_l1"][c]
        m["ck2"] = inp["cache_l2_k"][c].reshape(512, 512)
        m["cv2"] = inp["cache_l2_v"][c].reshape(512, 512)
        m["ck3"] = inp["cache_l3_k"][c].reshape(512, 256)
        m["cv3"] = inp["cache_l3_v"][c].reshape(512, 256)
        in_maps.append({k: np.ascontiguousarray(v) for k, v in m.items()})
    res = run_bass_kernel_spmd(nc, in_maps, core_ids=list(range(8)))
    R = res.results
    y_prompt = np.concatenate([R[c]["y"][0].reshape(4, 256, D) for c in range(8)], 0)
    y_sample = np.stack([R[c]["y"][1] for c in range(8)], 0)
    cat = lambda k, shp: np.concatenate([R[c][k].reshape(shp) for c in range(8)], 0)
    out = (y_prompt, y_sample,
           cat("nk0", (4, 256, 4, 64)), cat("nv0", (4, 256, 4, 64)),
           cat("nst", (4, 2, 2, 64, 64)),
           cat("nk2", (4, 256, 4, 128)), cat("nv2", (4, 256, 4, 128)),
           cat("nk3", (4, 256, 4, 64)), cat("nv3", (4, 256, 4, 64)))
    return tuple(np.ascontiguousarray(o, dtype=np.float32) for o in out)
```

```python
import math
import numpy as np
from contextlib import ExitStack
import concourse.bass as bass
import concourse.mybir as mybir
from concourse.bass_utils import run_bass_kernel_spmd

F32 = mybir.dt.float32
BF16 = mybir.dt.bfloat16
I32 = mybir.dt.int32
U8 = mybir.dt.uint8
AF = mybir.ActivationFunctionType
ALU = mybir.AluOpType
AX = mybir.AxisListType

D = 1024
DFF = 2816
NT = 1024
ENGS = ["pe", "act", "dve", "pool", "sp"]
EPS = 1e-6
NLAYERS = 4
SKIP_S5 = False
DEBUG_H = False
STAGE = 7
ATT_STOP = 9


class Sched:
    def __init__(self, nc, stack):
        self.nc = nc
        self.stack = stack
        self.prog = {e: [] for e in ENGS}
        self.cnt = {e: 0 for e in ENGS}
        self.seen = {e: {} for e in ENGS}
        self.lastw = {}
        self.readers = {}
        self.sem_objs = []
        self.esid = {}
        for e in ENGS:
            self.esid[e] = self._new_sem("s_" + e)
        self.dsem = {}
        self.dfree = []

    def _new_sem(self, name):
        self.sem_objs.append(self.stack.enter_context(self.nc.semaphore(name)))
        return len(self.sem_objs) - 1

    def _dma_sem(self, key):
        if key not in self.dsem:
            if self.dfree:
                self.dsem[key] = self.dfree.pop()
            else:
                self.dsem[key] = [self._new_sem("d%d" % len(self.sem_objs)), 0]
        return self.dsem[key]

    def _deps(self, eng, reads, writes, acc=False):
        need = {}

        def add(rec, same_ok=False):
            sid, val, e = rec
            if same_ok and e == eng:
                return
            if need.get(sid, 0) < val:
                need[sid] = val

        for k in reads:
            if k in self.lastw:
                add(self.lastw[k])
        for k in writes:
            if k in self.lastw:
                add(self.lastw[k], same_ok=(acc and eng == "pe"))
            for r in self.readers.get(k, ()):
                add(r, same_ok=True)
        waits = []
        seen = self.seen[eng]
        for sid, val in need.items():
            if seen.get(sid, 0) >= val:
                continue
            seen[sid] = val
            waits.append((sid, val))
        return waits

    def op(self, eng, fn, reads=(), writes=(), acc=False):
        waits = self._deps(eng, reads, writes, acc)
        self.cnt[eng] += 1
        rec = (self.esid[eng], self.cnt[eng], eng)
        self.prog[eng].append((waits, fn, self.esid[eng], 1))
        for k in reads:
            self.readers.setdefault(k, []).append(rec)
        for k in writes:
            self.lastw[k] = rec
            self.readers[k] = []

    def dma(self, q, fn, reads=(), writes=(), key=None):
        if key is None:
            key = writes[0] if writes else reads[0]
        waits = self._deps(q, reads, writes)
        ds = self._dma_sem(key)
        ds[1] += 16
        rec = (ds[0], ds[1], "dma")
        self.prog[q].append((waits, fn, ds[0], 16))
        for k in reads:
            self.readers.setdefault(k, []).append(rec)
        for k in writes:
            self.lastw[k] = rec
            self.readers[k] = []

    def barrier(self):
        allw = [(self.esid[e], self.cnt[e], e) for e in ENGS if self.cnt[e] > 0]
        allw += [(sid, c, None) for k, (sid, c) in self.dsem.items() if c > 0]
        for e in ENGS:
            waits = []
            for sid, val, owner in allw:
                if owner == e:
                    continue
                if self.seen[e].get(sid, 0) >= val:
                    continue
                self.seen[e][sid] = val
                waits.append((sid, val))
            if waits:
                self.prog[e].append((waits, None, None, 0))
        self.dfree.extend(self.dsem.values())
        self.dsem = {}
        self.lastw = {}
        self.readers = {}

    def emit(self):
        nc, sems, prog = self.nc, self.sem_objs, self.prog

        def run(ename):
            def body(eng):
                for waits, fn, incsid, incv in prog[ename]:
                    for sid, val in waits:
                        eng.wait_ge(sems[sid], val)
                    if fn is not None:
                        fn(eng).then_inc(sems[incsid], incv)
            return body

        with nc.Block() as block:
            block.tensor(run("pe"))
            block.scalar(run("act"))
            block.vector(run("dve"))
            block.gpsimd(run("pool"))
            block.sync(run("sp"))


class Arena:
    def __init__(self, tensor, size):
        self.t = tensor
        self.size = size
        self.off = 0

    def alloc(self, shape, dt):
        nb = {F32: 4, BF16: 2, I32: 4}[dt]
        n = int(np.prod(shape[1:]))
        off = (self.off + 63) // 64 * 64
        assert off + n * nb <= self.size, ("SBUF arena overflow", off, n * nb, self.size)
        self.off = off + n * nb
        ap = self.t[:, off:off + n * nb].bitcast(dt)
        if len(shape) == 3:
            ap = ap.rearrange("p (a b) -> p a b", a=shape[1])
        elif len(shape) == 4:
            ap = ap.rearrange("p (a b c) -> p a b c", a=shape[1], b=shape[2])
        if shape[0] != 128:
            ap = ap[0:shape[0]]
        return ap

    def mark(self):
        return self.off

    def release(self, m):
        self.off = m


LAYERS = [
    dict(kind="A", dh=64, H=16, ckv=256, wqkv="l0_w_qkv", qn="l0_q_norm", kn="l0_k_norm", sink="l0_sink", wo="l0_w_o",
         ck="ck0", cv="cv0", nk="nk0", nv="nv0"),
    dict(kind="B"),
    dict(kind="C", dh=128, H=8, ckv=512, wqkv="l2_w_qkv", qn="l2_q_norm", kn="l2_k_norm", sink=None, wo="l2_w_o",
         ck="ck2", cv="cv2", nk="nk2", nv="nv2"),
    dict(kind="A", dh=64, H=16, ckv=256, wqkv="l3_w_qkv", qn="l3_q_norm", kn="l3_k_norm", sink="l3_sink", wo="l3_w_o",
         ck="ck3", cv="cv3", nk="nk3", nv="nv3"),
]

IN_SPECS = [
    ("xin", [2, NT, D]), ("cond", [2, D]),
    ("ck0", [512, 256]), ("cv0", [512, 256]), ("st1", [2, 2, 64, 64]),
    ("ck2", [512, 512]), ("cv2", [512, 512]), ("ck3", [512, 256]), ("cv3", [512, 256]),
    ("norm1", [4, D]), ("norm2", [4, D]), ("w_mod", [4, D, 6 * D]), ("b_mod", [4, 6 * D]),
    ("w_up", [4, D, 2 * DFF]), ("b_up", [4, 2 * DFF]), ("conv_k", [4, 3, 2 * DFF]), ("conv_b", [4, 2 * DFF]),
    ("w_down", [4, DFF, D]),
    ("l0_w_qkv", [D, 1536]), ("l0_q_norm", [64]), ("l0_k_norm", [64]), ("l0_sink", [16]), ("l0_w_o", [D, D]),
    ("l1_lam_re", [2, 64, 64]), ("l1_lam_im", [2, 64, 64]), ("l1_log_dt", [2, 64]),
    ("l1_b_re", [2, 64, 64, 16]), ("l1_b_im", [2, 64, 64, 16]), ("l1_c_re", [2, 64, 16, 64]), ("l1_c_im", [2, 64, 16, 64]),
    ("l1_d_skip", [D]), ("l1_w_glu", [D, 2 * D]),
    ("l2_w_qkv", [D, 2048]), ("l2_q_norm", [128]), ("l2_k_norm", [128]), ("l2_w_o", [D, D]),
    ("l3_w_qkv", [D, 1536]), ("l3_q_norm", [64]), ("l3_k_norm", [64]), ("l3_sink", [16]), ("l3_w_o", [D, D]),
    ("cst", [9, 128, 128]), ("rope", [4, 128, NT]),
]
OUT_SPECS = [
    ("y", [2, NT, D]), ("nk0", [NT, 256]), ("nv0", [NT, 256]), ("nst", [4, 2, 2, 64, 64]),
    ("nk2", [NT, 512]), ("nv2", [NT, 512]), ("nk3", [NT, 256]), ("nv3", [NT, 256]),
]
CST_IDENT, CST_BD64, CST_MLO, CST_MHI, CST_PA, CST_PC, CST_S5F, CST_S5B, CST_R16 = range(9)


def build_program():
    nc = bass.Bass("TRN2", target_bir_lowering=False)
    T = {}
    for name, shape in IN_SPECS:
        T[name] = nc.dram_tensor(name, shape, F32, kind="ExternalInput").ap()
    for name, shape in OUT_SPECS:
        T[name] = nc.dram_tensor(name, shape, F32, kind="ExternalOutput").ap()
    for name, shape, dt_ in [("TX", [128, 16384], BF16), ("TZ", [128, 16384], BF16), ("TM", [128, 64, 128], BF16),
                             ("TE", [2, 128, 64, 128], BF16), ("TS", [2, 128, 64, 128], BF16), ("TP8", [2, 128, 64], F32),
                             ("D1", [2, 1024, 1024], BF16), ("D2", [2, 1024, 1024], BF16)]:
        T[name] = nc.dram_tensor(name, shape, dt_, kind="Internal").ap()
    if DEBUG_H:
        T["dbg"] = nc.dram_tensor("dbg", [2, NT, D], F32, kind="ExternalOutput").ap()

    with ExitStack() as st:
        S = Sched(nc, st)
        ARENA_BYTES = 190 * 1024
        arena_t = st.enter_context(nc.sbuf_tensor("arena", [128, ARENA_BYTES], U8))
        AR = Arena(arena_t, ARENA_BYTES)
        ps = st.enter_context(nc.psum_tensor("ps", [128, 8, 512], F32))
        PSK = [("ps", b) for b in range(8)]
        uid = [0]

        def fresh(prefix):
            uid[0] += 1
            return (prefix, uid[0])

        ident_f = AR.alloc([128, 128], F32)
        ident_b = AR.alloc([128, 128], BF16)
        ones_b = AR.alloc([128, 128], BF16)
        ones_f = AR.alloc([128, 128], F32)
        bd64_b = AR.alloc([128, 128], BF16)
        mlo_b = AR.alloc([128, 4, 128], BF16)
        mhi_b = AR.alloc([128, 4, 128], BF16)
        pa_b = AR.alloc([128, 128], BF16)
        pc_b = AR.alloc([128, 128], BF16)
        cstage = AR.alloc([128, 128], F32)
        S.dma("sp", lambda e: e.dma_start(out=ident_f, in_=T["cst"][CST_IDENT]), writes=["ident_f"])
        S.op("dve", lambda e: e.tensor_copy(out=ident_b, in_=ident_f), reads=["ident_f"], writes=["ident_b"])
        S.op("dve", lambda e: e.memset(ones_b, 1.0), writes=["ones_b"])
        S.op("dve", lambda e: e.memset(ones_f, 1.0), writes=["ones_f"])
        for ci, dst, rep in [(CST_BD64, bd64_b, 0), (CST_MLO, mlo_b, 4), (CST_MHI, mhi_b, 4), (CST_PA, pa_b, 0), (CST_PC, pc_b, 0)]:
            S.dma("sp", lambda e, ci=ci: e.dma_start(out=cstage, in_=T["cst"][ci]), writes=["cstage"])
            if rep:
                for r in range(rep):
                    S.op("dve", lambda e, dst=dst, r=r: e.tensor_copy(out=dst[:, r, :], in_=cstage), reads=["cstage"], writes=["cst_misc"])
            else:
                S.op("dve", lambda e, dst=dst: e.tensor_copy(out=dst, in_=cstage), reads=["cstage"], writes=["cst_misc"])
        S.barrier()
        base_mark = AR.mark()

        def mm(out, lhsT, rhs, start, stop, reads, wkey):
            S.op("pe", lambda e: e.matmul(out, lhsT, rhs, start=start, stop=stop), reads=reads, writes=[wkey], acc=True)

        tstage = AR.alloc([128, 128], F32)
        modAll = AR.alloc([128, 4, 2, 48], F32)
        bmodAll = AR.alloc([128, 4, 48], F32)
        cond2 = AR.alloc([128, 2, 8], F32)
        sT2 = AR.alloc([128, 8, 2], BF16)
        tst2 = AR.alloc([128, 128], F32)
        S.op("dve", lambda e: e.memset(tst2, 0.0), writes=["tst2z"])
        S.barrier()
        base_mark2 = [None]

        def small_load_T(dst, src_ap, pattern, key, **kw):
            n = dst.shape[1]
            rows = src_ap.rearrange("(c q) -> c q", q=128)
            S.dma("sp", lambda e: e.dma_start(out=tstage[0:n, :], in_=rows), writes=["tstage"])
            S.op("pe", lambda e: e.transpose(ps[:, 7, 0:n], tstage[0:n, :], ident_f[0:n, 0:n]), reads=["tstage", "ident_f"], writes=[PSK[7]])
            S.op("dve", lambda e: e.tensor_copy(out=dst, in_=ps[:, 7, 0:n]), reads=[PSK[7]], writes=[key])

        def small_load_col(dst, src_ap, dh, key):
            rep = 128 // dh
            for r in range(rep):
                S.dma("sp", lambda e, r=r: e.dma_start(out=tst2[0:1, r * dh:(r + 1) * dh], in_=src_ap.rearrange("(o q) -> o q", o=1)),
                      writes=[("tst2", r)], key=("tst2", r))
            S.op("pe", lambda e: e.transpose(ps[:, 7, 0:2], tst2[0:2, :], ident_f[0:2, 0:2]),
                 reads=[("tst2", r) for r in range(rep)] + ["tst2z", "ident_f"], writes=[PSK[7], ("tst2", 0), ("tst2", 1)])
            S.op("dve", lambda e: e.tensor_copy(out=dst, in_=ps[:, 7, 0:1]), reads=[PSK[7]], writes=[key])

        base_mark = AR.mark()

        def s5_tables():
            m = AR.mark()
            TWO_PI = 2.0 * math.pi
            lr = AR.alloc([128, 64], F32)
            li = AR.alloc([128, 64], F32)
            dtc = AR.alloc([128, 1], F32)
            xx = AR.alloc([128, 64], F32)
            th = AR.alloc([128, 64], F32)
            PWr = AR.alloc([128, 16, 64], F32)
            PWi = AR.alloc([128, 16, 64], F32)
            Br = AR.alloc([128, 64, 16], F32)
            Bi = AR.alloc([128, 64, 16], F32)
            Bbr = AR.alloc([128, 64, 16], F32)
            Bbi = AR.alloc([128, 64, 16], F32)
            Cr = AR.alloc([128, 16, 64], F32)
            Ci = AR.alloc([128, 16, 64], F32)
            w = [AR.alloc([128, 1024], F32) for _ in range(4)]
            s64 = [AR.alloc([128, 64], F32) for _ in range(8)]
            ki = AR.alloc([128, 64], I32)
            PMr = AR.alloc([128, 64], F32)
            PMi = AR.alloc([128, 64], F32)
            ST = [AR.alloc([128, 16384], BF16) for _ in range(2)]

            def dv(fn, reads, writes):
                S.op("dve", fn, reads=reads, writes=writes)

            S.dma("sp", lambda e: e.dma_start(out=lr, in_=T["l1_lam_re"].rearrange("d g p -> (d g) p")), writes=["lr"])
            S.dma("sp", lambda e: e.dma_start(out=li, in_=T["l1_lam_im"].rearrange("d g p -> (d g) p")), writes=["li"])
            S.dma("sp", lambda e: e.dma_start(out=dtc, in_=T["l1_log_dt"].rearrange("d (g o) -> (d g) o", o=1)), writes=["dtc"])
            S.dma("sp", lambda e: e.dma_start(out=Br, in_=T["l1_b_re"].rearrange("d g p c -> (d g) p c")), writes=["Br"])
            S.dma("sp", lambda e: e.dma_start(out=Bi, in_=T["l1_b_im"].rearrange("d g p c -> (d g) p c")), writes=["Bi"])
            S.dma("sp", lambda e: e.dma_start(out=Cr, in_=T["l1_c_re"].rearrange("d g c p -> (d g) c p")), writes=["Cr"])
            S.dma("sp", lambda e: e.dma_start(out=Ci, in_=T["l1_c_im"].rearrange("d g c p -> (d g) c p")), writes=["Ci"])
            S.op("act", lambda e: e.activation(out=dtc, in_=dtc, func=AF.Exp), reads=["dtc"], writes=["dtc"])
            dv(lambda e: e.tensor_scalar(out=xx, in0=lr, scalar1=dtc[:, 0:1], scalar2=None, op0=ALU.mult), ["lr", "dtc"], ["xx"])
            dv(lambda e: e.tensor_scalar(out=th, in0=li, scalar1=dtc[:, 0:1], scalar2=None, op0=ALU.mult), ["li", "dtc"], ["th"])

            def sinred(out, okey, shift):
                v, kf, r, mk = s64[0], s64[1], s64[2], s64[3]
                dv(lambda e: e.tensor_scalar(out=v, in0=th, scalar1=float(shift), scalar2=None, op0=ALU.add), ["th"], ["sr_v"])
                dv(lambda e: e.tensor_scalar(out=kf, in0=v, scalar1=1.0 / TWO_PI, scalar2=None, op0=ALU.mult), ["sr_v"], ["sr_kf"])
                dv(lambda e: e.tensor_copy(out=ki, in_=kf), ["sr_kf"], ["sr_ki"])
                dv(lambda e: e.tensor_copy(out=kf, in_=ki), ["sr_ki"], ["sr_kf"])
                dv(lambda e: e.scalar_tensor_tensor(out=r, in0=kf, scalar=-TWO_PI, in1=v, op0=ALU.mult, op1=ALU.add), ["sr_kf", "sr_v"], ["sr_r"])
                dv(lambda e: e.tensor_scalar(out=mk, in0=r, scalar1=math.pi, scalar2=None, op0=ALU.is_gt), ["sr_r"], ["sr_m"])
                dv(lambda e: e.scalar_tensor_tensor(out=r, in0=mk, scalar=-TWO_PI, in1=r, op0=ALU.mult, op1=ALU.add), ["sr_m", "sr_r"], ["sr_r"])
                dv(lambda e: e.tensor_scalar(out=mk, in0=r, scalar1=-math.pi, scalar2=None, op0=ALU.is_lt), ["sr_r"], ["sr_m"])
                dv(lambda e: e.scalar_tensor_tensor(out=r, in0=mk, scalar=TWO_PI, in1=r, op0=ALU.mult, op1=ALU.add), ["sr_m", "sr_r"], ["sr_r"])
                dv(lambda e: e.tensor_scalar(out=r, in0=r, scalar1=3.1415925, scalar2=-3.1415925, op0=ALU.min, op1=ALU.max), ["sr_r"], ["sr_r"])
                S.op("act", lambda e: e.activation(out=out, in_=r, func=AF.Sin), reads=["sr_r"], writes=[okey])

            sn, cs, ex, exm = s64[4], s64[5], s64[6], s64[7]
            sinred(sn, "sn", 0.0)
            sinred(cs, "cs", math.pi / 2.0)
            S.op("act", lambda e: e.activation(out=ex, in_=xx, func=AF.Exp), reads=["xx"], writes=["ex"])
            S.op("act", lambda e: e.activation(out=exm, in_=xx, func=AF.Exp, scale=-1.0), reads=["xx"], writes=["exm"])
            PK = lambda j: ("PW", j)
            dv(lambda e: e.memset(PWr[:, 7, :], 1.0), [], [("PWr", 0)])
            dv(lambda e: e.memset(PWi[:, 7, :], 0.0), [], [("PWi", 0)])
            dv(lambda e: e.tensor_tensor(out=PWr[:, 8, :], in0=ex, in1=cs, op=ALU.mult), ["ex", "cs"], [("PWr", 1)])
            dv(lambda e: e.tensor_tensor(out=PWi[:, 8, :], in0=ex, in1=sn, op=ALU.mult), ["ex", "sn"], [("PWi", 1)])
            dv(lambda e: e.tensor_tensor(out=PWr[:, 6, :], in0=exm, in1=cs, op=ALU.mult), ["exm", "cs"], [("PWr", -1)])
            dv(lambda e: e.scalar_tensor_tensor(out=PWi[:, 6, :], in0=exm, scalar=-1.0, in1=sn, op0=ALU.mult, op1=ALU.mult), ["exm", "sn"], [("PWi", -1)])

            def cmul64(j_out, j_a, j_b):
                orr, oi = PWr[:, j_out + 7, :], PWi[:, j_out + 7, :]
                ar, ai = PWr[:, j_a + 7, :], PWi[:, j_a + 7, :]
                br, bi = PWr[:, j_b + 7, :], PWi[:, j_b + 7, :]
                ka = [("PWr", j_a), ("PWi", j_a), ("PWr", j_b), ("PWi", j_b)]
                t1, t2 = s64[0], s64[1]
                dv(lambda e: e.tensor_tensor(out=t1, in0=ar, in1=br, op=ALU.mult), ka, ["c64a"])
                dv(lambda e: e.tensor_tensor(out=t2, in0=ai, in1=bi, op=ALU.mult), ka, ["c64b"])
                dv(lambda e: e.tensor_tensor(out=orr, in0=t1, in1=t2, op=ALU.subtract), ["c64a", "c64b"], [("PWr", j_out)])
                dv(lambda e: e.tensor_tensor(out=t1, in0=ar, in1=bi, op=ALU.mult), ka, ["c64a"])
                dv(lambda e: e.tensor_tensor(out=t2, in0=ai, in1=br, op=ALU.mult), ka, ["c64b"])
                dv(lambda e: e.tensor_tensor(out=oi, in0=t1, in1=t2, op=ALU.add), ["c64a", "c64b"], [("PWi", j_out)])

            for j in range(2, 9):
                cmul64(j, j - 1, 1)
            for j in range(2, 8):
                cmul64(-j, -(j - 1), -1)
            S.dma("sp", lambda e: e.dma_start(out=T["TP8"][0], in_=PWr[:, 15, :]), reads=[("PWr", 8)], key="tp8a")
            S.dma("sp", lambda e: e.dma_start(out=T["TP8"][1], in_=PWi[:, 15, :]), reads=[("PWi", 8)], key="tp8b")
            nr, den, fr, fi = s64[0], s64[1], s64[2], s64[3]
            t5, t6 = s64[4], s64[5]
            a1 = [("PWr", 1), ("PWi", 1)]
            dv(lambda e: e.tensor_scalar(out=nr, in0=PWr[:, 8, :], scalar1=-1.0, scalar2=None, op0=ALU.add), a1, ["nr"])
            dv(lambda e: e.tensor_tensor(out=den, in0=lr, in1=lr, op=ALU.mult), ["lr"], ["den"])
            dv(lambda e: e.tensor_tensor(out=t5, in0=li, in1=li, op=ALU.mult), ["li"], ["t5"])
            dv(lambda e: e.tensor_tensor(out=den, in0=den, in1=t5, op=ALU.add), ["den", "t5"], ["den"])
            dv(lambda e: e.reciprocal(out=den, in_=den), ["den"], ["den"])
            dv(lambda e: e.tensor_tensor(out=fr, in0=nr, in1=lr, op=ALU.mult), ["nr", "lr"], ["fr"])
            dv(lambda e: e.tensor_tensor(out=t5, in0=PWi[:, 8, :], in1=li, op=ALU.mult), a1 + ["li"], ["t5"])
            dv(lambda e: e.tensor_tensor(out=fr, in0=fr, in1=t5, op=ALU.add), ["fr", "t5"], ["fr"])
            dv(lambda e: e.tensor_tensor(out=fr, in0=fr, in1=den, op=ALU.mult), ["fr", "den"], ["fr"])
            dv(lambda e: e.tensor_tensor(out=fi, in0=PWi[:, 8, :], in1=lr, op=ALU.mult), a1 + ["lr"], ["fi"])
            dv(lambda e: e.tensor_tensor(out=t6, in0=nr, in1=li, op=ALU.mult), ["nr", "li"], ["t6"])
            dv(lambda e: e.tensor_tensor(out=fi, in0=fi, in1=t6, op=ALU.subtract), ["fi", "t6"], ["fi"])
            dv(lambda e: e.tensor_tensor(out=fi, in0=fi, in1=den, op=ALU.mult), ["fi", "den"], ["fi"])

            def bc_b(a):
                return a.unsqueeze(2).broadcast_to([128, 64, 16])

            def bc_c(a):
                return a.unsqueeze(1).broadcast_to([128, 16, 64])

            w3b = [x.rearrange("q (p c) -> q p c", c=16) for x in w]
            w3c = [x.rearrange("q (c p) -> q c p", p=64) for x in w]

            def cmul_big(eng, pr, pi, pk, xr, xi, xk, bc, wv, out_r, out_i, okey, xr_n=None, xi_n=None):
                t1, t2, t3, t4 = wv
                tg = "cb" + eng
                op = lambda fn, r, w_: S.op(eng, fn, reads=r, writes=w_)
                op(lambda e: e.tensor_tensor(out=t1, in0=xr, in1=bc(pr), op=ALU.mult), pk + xk, [tg + "1"])
                op(lambda e: e.tensor_tensor(out=t2, in0=xi, in1=bc(pi), op=ALU.mult), pk + xk, [tg + "2"])
                op(lambda e: e.tensor_tensor(out=out_r, in0=t1, in1=t2, op=ALU.subtract), [tg + "1", tg + "2"], [okey])
                a_i = xi if xi_n is None else xi_n
                a_r = xr if xr_n is None else xr_n
                op(lambda e: e.tensor_tensor(out=t3, in0=a_i, in1=bc(pr), op=ALU.mult), pk + xk, [tg + "3"])
                op(lambda e: e.tensor_tensor(out=t4, in0=a_r, in1=bc(pi), op=ALU.mult), pk + xk, [tg + "4"])
                op(lambda e: e.tensor_tensor(out=out_i, in0=t3, in1=t4, op=ALU.add), [tg + "3", tg + "4"], [okey])

            cmul_big("dve", fr, fi, ["fr", "fi"], Br, Bi, ["Br", "Bi"], bc_b, w3b, Bbr, Bbi, "Bb")
            nCr = Br.rearrange("q p c -> q (p c)").rearrange("q (c p) -> q c p", p=64)
            nCi = Bi.rearrange("q p c -> q (p c)").rearrange("q (c p) -> q c p", p=64)
            dv(lambda e: e.tensor_scalar(out=nCr, in0=Cr, scalar1=-1.0, scalar2=None, op0=ALU.mult), ["Cr", "Bb"], ["nC"])
            dv(lambda e: e.tensor_scalar(out=nCi, in0=Ci, scalar1=-1.0, scalar2=None, op0=ALU.mult), ["Ci", "Bb"], ["nC"])
            PM = {"dve": (PMr, PMi), "pool": (AR.alloc([128, 64], F32), AR.alloc([128, 64], F32))}
            wp = [AR.alloc([128, 1024], F32) for _ in range(4)]
            WV = {"dve": (w3b, w3c),
                  "pool": ([x.rearrange("q (p c) -> q p c", c=16) for x in wp], [x.rearrange("q (c p) -> q c p", p=64) for x in wp])}

            def mixed_power(eng, jf, jb):
                pmr, pmi = PM[eng]
                k = "PM" + eng
                S.op(eng, lambda e: e.tensor_copy(out=pmr[0:64], in_=PWr[0:64, jf + 7, :]), reads=[("PWr", jf)], writes=[k])
                S.op(eng, lambda e: e.tensor_copy(out=pmi[0:64], in_=PWi[0:64, jf + 7, :]), reads=[("PWi", jf)], writes=[k])
                S.op(eng, lambda e: e.tensor_copy(out=pmr[64:128], in_=PWr[64:128, jb + 7, :]), reads=[("PWr", jb)], writes=[k])
                S.op(eng, lambda e: e.tensor_copy(out=pmi[64:128], in_=PWi[64:128, jb + 7, :]), reads=[("PWi", jb)], writes=[k])

            specs = [("B", lambda t: -t, lambda t: t, "TX", "ript", "dve"),
                     ("C", lambda t: t, lambda t: -t, "TZ", "ript", "dve"),
                     ("B", lambda t: 7 - t, lambda t: t, "TS", "ritp", "dve"),
                     ("C", lambda t: t + 1, lambda t: 8 - t, "TE", "ript", "dve")]
            for ti, (side, pf, pb_, dst, layout, eng) in enumerate(specs):
                st_ = ST[ti % 2]
                stk = ("ST", ti % 2)
                pmr, pmi = PM[eng]
                pmk = ["PM" + eng]
                if layout == "ript":
                    st5 = st_.rearrange("q (r p t c) -> q r p t c", r=2, p=64, t=8)
                else:
                    st5 = st_.rearrange("q (r t c p) -> q r t c p", r=2, t=8, c=16)
                for t in range(8):
                    mixed_power(eng, pf(t), pb_(t))
                    if side == "B":
                        if layout == "ript":
                            o_r, o_i = st5[:, 0, :, t, :], st5[:, 1, :, t, :]
                        else:
                            o_r = st5[:, 0, t, :, :].rearrange("q c p -> q p c")
                            o_i = st5[:, 1, t, :, :].rearrange("q c p -> q p c")
                        cmul_big(eng, pmr, pmi, pmk, Bbr, Bbi, ["Bb"], bc_b, WV[eng][0], o_r, o_i, stk)
                    else:
                        o_r = st5[:, 0, :, t, :].rearrange("q p c -> q c p")
                        o_i = st5[:, 1, :, t, :].rearrange("q p c -> q c p")
                        cmul_big(eng, pmr, pmi, pmk, Cr, Ci, ["Cr", "Ci", "nC"], bc_c, WV[eng][1], o_r, o_i, stk, xr_n=nCr, xi_n=nCi)
                if dst in ("TX", "TZ"):
                    S.dma("sp", lambda e, st_=st_, dst=dst: e.dma_start(out=T[dst], in_=st_), reads=[stk], writes=[dst], key=stk)
                elif dst == "TE":
                    st4 = st_.rearrange("q (r p n) -> q r p n", r=2, p=64)
                    for d in range(2):
                        for r_ in range(2):
                            for p4 in range(4):
                                S.dma("sp", lambda e, d=d, r_=r_, p4=p4, st4=st4: e.dma_start(
                                    out=T["TE"][r_, d * 64 + p4 * 16:d * 64 + (p4 + 1) * 16].rearrange("k g n -> g k n"),
                                    in_=st4[d * 64:(d + 1) * 64, r_, p4 * 16:(p4 + 1) * 16, :]), reads=[stk], writes=["TE"], key=stk)
                else:
                    st4 = st_.rearrange("q (r n p) -> q r n p", r=2, p=64)
                    for d in range(2):
                        for r_ in range(2):
                            for k8 in range(8):
                                S.dma("sp", lambda e, d=d, r_=r_, k8=k8, st4=st4: e.dma_start(
                                    out=T["TS"][r_, k8 * 16:(k8 + 1) * 16, :, d * 64:(d + 1) * 64].rearrange("k g p -> g k p"),
                                    in_=st4[d * 64:(d + 1) * 64, r_, k8 * 16:(k8 + 1) * 16, :]), reads=[stk], writes=["TS"], key=stk)
            S.barrier()
            AR.release(m)
            m = AR.mark()
            mkf = AR.alloc([128, 128], F32)
            mkb = AR.alloc([128, 128], F32)
            r16 = AR.alloc([128, 128], F32)
            dgc = AR.alloc([128, 16], F32)
            dmat = AR.alloc([128, 64], F32)
            dcol = AR.alloc([128, 64], F32)
            S.dma("sp", lambda e: e.dma_start(out=mkf, in_=T["cst"][CST_S5F]), writes=["mkf"])
            S.dma("sp", lambda e: e.dma_start(out=mkb, in_=T["cst"][CST_S5B]), writes=["mkb"])
            S.dma("sp", lambda e: e.dma_start(out=r16, in_=T["cst"][CST_R16]), writes=["r16"])
            S.dma("sp", lambda e: e.dma_start(out=dgc[0:64, :], in_=T["l1_d_skip"].rearrange("(g c) -> g c", c=16)), writes=["dgc"])
            S.op("pe", lambda e: e.transpose(ps[0:16, 7, 0:64], dgc[0:64, :], ident_f[0:64, 0:64]), reads=["dgc", "ident_f"], writes=[PSK[7]])
            S.op("dve", lambda e: e.tensor_copy(out=dmat[0:16, :], in_=ps[0:16, 7, 0:64]), reads=[PSK[7]], writes=["dmat"])
            S.op("pe", lambda e: e.matmul(ps[:, 7, 64:128], r16[0:16, :], dmat[0:16, :], start=True, stop=True), reads=["r16", "dmat"], writes=[PSK[7]])
            S.op("dve", lambda e: e.tensor_copy(out=dcol, in_=ps[:, 7, 64:128]), reads=[PSK[7]], writes=["dcol"])
            XL = [[AR.alloc([128, 16, 128], BF16) for _ in range(2)] for _ in range(2)]
            ZL = [[AR.alloc([128, 16, 128], BF16) for _ in range(2)] for _ in range(2)]
            Mst = [AR.alloc([128, 16, 128], BF16) for _ in range(2)]
            tA = [AR.alloc([128, 128], F32) for _ in range(2)]
            tB = [AR.alloc([128, 128], F32) for _ in range(2)]
            TX4 = T["TX"].rearrange("(d g) (k n) -> d g k n", d=2, k=128)
            TZ4 = T["TZ"].rearrange("(d g) (k n) -> d g k n", d=2, k=128)
            for gb in range(4):
                bf = gb % 2
                for d in range(2):
                    S.dma("sp", lambda e, d=d, gb=gb, bf=bf: e.dma_start(out=XL[bf][d], in_=TX4[d, gb * 16:(gb + 1) * 16].rearrange("g k n -> k g n")),
                          reads=["TX"], writes=[("XL", bf, d)])
                    S.dma("sp", lambda e, d=d, gb=gb, bf=bf: e.dma_start(out=ZL[bf][d], in_=TZ4[d, gb * 16:(gb + 1) * 16].rearrange("g k n -> k g n")),
                          reads=["TZ"], writes=[("ZL", bf, d)])
                for gi in range(16):
                    g = gb * 16 + gi
                    b = gi % 4
                    for d in range(2):
                        S.op("pe", lambda e, b=b, d=d, bf=bf, gi=gi: e.matmul(ps[:, b, d * 128:(d + 1) * 128], XL[bf][d][:, gi, :], ZL[bf][d][:, gi, :],
                                                                            start=True, stop=True),
                             reads=[("XL", bf, d), ("ZL", bf, d)], writes=[PSK[b]], acc=True)
                    a_, b_ = tA[gi % 2], tB[gi % 2]
                    S.op("dve", lambda e, a_=a_, b=b: e.tensor_tensor(out=a_, in0=ps[:, b, 0:128], in1=mkf, op=ALU.mult), reads=[PSK[b], "mkf"], writes=[("tA", gi % 2)])
                    S.op("dve", lambda e, b_=b_, b=b: e.tensor_tensor(out=b_, in0=ps[:, b, 128:256], in1=mkb, op=ALU.mult), reads=[PSK[b], "mkb"], writes=[("tB", gi % 2)])
                    S.op("dve", lambda e, a_=a_, b_=b_: e.tensor_tensor(out=a_, in0=a_, in1=b_, op=ALU.add), reads=[("tA", gi % 2), ("tB", gi % 2)], writes=[("tA", gi % 2)])
                    S.op("dve", lambda e, a_=a_, g=g, gi=gi, bf=bf: e.scalar_tensor_tensor(out=Mst[bf][:, gi, :], in0=ident_f, scalar=dcol[:, g:g + 1], in1=a_,
                                                                                         op0=ALU.mult, op1=ALU.add),
                         reads=[("tA", gi % 2), "dcol", "ident_f"], writes=[("Mst", bf)])
                S.dma("sp", lambda e, gb=gb, bf=bf: e.dma_start(out=T["TM"][:, gb * 16:(gb + 1) * 16, :], in_=Mst[bf]),
                      reads=[("Mst", bf)], writes=["TM"], key=("Mst", bf))
            S.barrier()
            AR.release(m)

        def prologue_mod_start():
            for c_ in range(2):
                small_load_T(cond2[:, c_, :], T["cond"][c_], "(kc q) -> q kc", ("cond2", c_), q=128)
            S.op("act", lambda e: e.activation(out=sT2.rearrange("q k c -> q c k"), in_=cond2, func=AF.Silu),
                 reads=[("cond2", 0), ("cond2", 1)], writes=["sT"])
            wm = [AR.alloc([128, 8, 512], BF16) for _ in range(2)]
            bi_ = 0
            for l in range(NLAYERS):
                for blk in range(12):
                    w = wm[bi_ % 2]
                    wk = ("wm", bi_ % 2)
                    bi_ += 1
                    S.dma("pool", lambda e, w=w, blk=blk, l=l: e.dma_start(
                        out=w, in_=T["w_mod"][l][:, blk * 512:(blk + 1) * 512].rearrange("(kc q) n -> q kc n", q=128)), writes=[wk])
                    for cc in range(4):
                        col = l * 96 + 2 * (blk * 4 + cc)
                        for kc in range(8):
                            mm(ps[:, 6, col:col + 2], w[:, kc, cc * 128:(cc + 1) * 128], sT2[:, kc, :], kc == 0, kc == 7, [wk, "sT"], PSK[6])

        def prologue_mod_finish():
            for l in range(NLAYERS):
                small_load_T(bmodAll[:, l, :], T["b_mod"][l], "(c q) -> q c", ("bmodAll", l), q=128)
                pv = ps[:, 6, l * 96:(l + 1) * 96].rearrange("q (n c) -> q c n", c=2)
                for c_ in range(2):
                    S.op("dve", lambda e, c_=c_, l=l, pv=pv: e.tensor_tensor(out=modAll[:, l, c_, :], in0=pv[:, c_, :], in1=bmodAll[:, l, :], op=ALU.add),
                         reads=[PSK[6], ("bmodAll", l)], writes=[("modAll", l)])
            S.barrier()

        def run_path(p):
            nseq, slen = (4, 256) if p == 0 else (1, 1024)
            AR.release(base_mark)
            hT = AR.alloc([128, 8, NT], F32)
            xnT = AR.alloc([128, 8, NT], BF16)
            modT = AR.alloc([128, 48], F32)
            bmodT = AR.alloc([128, 48], F32)
            n1T = AR.alloc([128, 8], F32)
            n2T = AR.alloc([128, 8], F32)
            A1 = AR.alloc([128, 8], F32)
            A2 = AR.alloc([128, 8], F32)
            condT = AR.alloc([128, 8], F32)
            sT = AR.alloc([128, 8], BF16)
            rstd = AR.alloc([128, NT], F32)
            lnv = AR.alloc([128, 512], F32)
            path_mark = AR.mark()
            HK = [("hT", kc) for kc in range(8)]
            XK = [("xnT", kc) for kc in range(8)]

            m0 = AR.mark()
            xst = [AR.alloc([128, D], F32) for _ in range(2)]
            for tt in range(8):
                xs = xst[tt % 2]
                xk = ("xst", tt % 2)
                S.dma("sp", lambda e, xs=xs, tt=tt: e.dma_start(out=xs, in_=T["xin"][p, tt * 128:(tt + 1) * 128, :]), writes=[xk])
                for hf in range(2):
                    b = (tt * 2 + hf) % 8
                    for q in range(4):
                        kc = hf * 4 + q
                        S.op("pe", lambda e, b=b, q=q, kc=kc, xs=xs: e.transpose(ps[:, b, q * 128:(q + 1) * 128], xs[:, kc * 128:(kc + 1) * 128], ident_f),
                             reads=[xk, "ident_f"], writes=[PSK[b]], acc=True)
                    S.op("act" if hf else "dve",
                         (lambda e, b=b, hf=hf, tt=tt: e.activation(out=hT[:, hf * 4:hf * 4 + 4, tt * 128:(tt + 1) * 128],
                                                                     in_=ps[:, b, :].rearrange("p (a b) -> p a b", a=4), func=AF.Copy)) if hf else
                         (lambda e, b=b, hf=hf, tt=tt: e.tensor_copy(out=hT[:, hf * 4:hf * 4 + 4, tt * 128:(tt + 1) * 128],
                                                                      in_=ps[:, b, :].rearrange("p (a b) -> p a b", a=4))),
                         reads=[PSK[b]], writes=HK[hf * 4:hf * 4 + 4])
            S.barrier()
            AR.release(m0)
            def compute_mod(l):
                m = AR.mark()
                small_load_T(n1T, T["norm1"][l], "(kc q) -> q kc", "n1T", q=128)
                small_load_T(n2T, T["norm2"][l], "(kc q) -> q kc", "n2T", q=128)
                S.op("dve", lambda e: e.tensor_copy(out=modT, in_=modAll[:, l, p, :]), reads=[("modAll", l)], writes=["modT"])
                S.op("dve", lambda e: e.scalar_tensor_tensor(out=A1, in0=modT[:, 8:16], scalar=1.0, in1=n1T, op0=ALU.add, op1=ALU.mult),
                     reads=["modT", "n1T"], writes=["A1"])
                S.op("dve", lambda e: e.scalar_tensor_tensor(out=A2, in0=modT[:, 32:40], scalar=1.0, in1=n2T, op0=ALU.add, op1=ALU.mult),
                     reads=["modT", "n2T"], writes=["A2"])
                AR.release(m)

            def norm_mod(Acol, Akey, Boff, perm8=False):
                m = AR.mark()
                sq = AR.alloc([128, 8, NT], BF16)
                tmp = [AR.alloc([128, NT], F32) for _ in range(2)]
                S.op("act", lambda e: e.activation(out=sq, in_=hT, func=AF.Square), reads=HK, writes=["sq"])
                for hf in range(2):
                    b = hf
                    for kc in range(8):
                        mm(ps[:, b, :], ones_b, sq[:, kc, hf * 512:(hf + 1) * 512], kc == 0, kc == 7, ["sq", "ones_b"], PSK[b])
                    S.op("act", lambda e, b=b: e.activation(out=lnv, in_=ps[:, b, :], func=AF.Ln, bias=EPS, scale=1.0 / D),
                         reads=[PSK[b]], writes=["lnv"])
                    S.op("act", lambda e, hf=hf: e.activation(out=rstd[:, hf * 512:(hf + 1) * 512], in_=lnv, func=AF.Exp, scale=-0.5),
                         reads=["lnv"], writes=[("rstd", hf)])
                for kc in range(8):
                    t = tmp[kc % 2]
                    tk = ("nm_tmp", kc % 2)
                    S.op("dve", lambda e, t=t, kc=kc: e.scalar_tensor_tensor(out=t, in0=hT[:, kc, :], scalar=Acol[:, kc:kc + 1], in1=rstd,
                                                                              op0=ALU.mult, op1=ALU.mult),
                         reads=[HK[kc], Akey, ("rstd", 0), ("rstd", 1)], writes=[tk])
                    if perm8:
                        o = xnT[:, kc, :].rearrange("p (t k) -> p k t", t=8)
                    else:
                        o = xnT[:, kc, :]
                    S.op("act", lambda e, t=t, kc=kc, o=o: e.activation(out=o, in_=t, func=AF.Identity, bias=modT[:, Boff + kc:Boff + kc + 1]),
                         reads=[tk, "modT"], writes=[XK[kc]])
                S.barrier()
                AR.release(m)

            def resid(oc, hf, b, goff):
                S.op("dve", lambda e: e.scalar_tensor_tensor(out=hT[:, oc, hf * 512:(hf + 1) * 512], in0=ps[:, b, :],
                                                              scalar=modT[:, goff + oc:goff + oc + 1],
                                                              in1=hT[:, oc, hf * 512:(hf + 1) * 512], op0=ALU.mult, op1=ALU.add),
                     reads=[PSK[b], "modT", HK[oc]], writes=[HK[oc]])

            def ffn(l):
                m = AR.mark()
                norm_mod(A2, "A2", 24)
                gT = AR.alloc([128, 22, NT], BF16)
                wd = AR.alloc([128, 22, D], BF16)
                bupT = AR.alloc([128, 44], F32)
                cbT = AR.alloc([128, 44], F32)
                ckT = AR.alloc([128, 3, 44], F32)
                wu = [AR.alloc([128, 2, 8, 256], BF16) for _ in range(2)]
                tc = [AR.alloc([128, nseq, slen], F32) for _ in range(2)]
                sg = AR.alloc([128, nseq, slen], F32)
                beff = AR.alloc([128, 44], F32)
                e0 = AR.alloc([128, 44], F32)
                e2 = AR.alloc([128, 44], F32)
                small_load_T(bupT, T["b_up"][l], "(c q) -> q c", "bupT", q=128)
                small_load_T(cbT, T["conv_b"][l], "(c q) -> q c", "cbT", q=128)
                for t3 in range(3):
                    small_load_T(ckT[:, t3, :], T["conv_k"][l, t3], "(c q) -> q c", ("ckT", t3), q=128)
                CK = [("ckT", t3) for t3 in range(3)]
                S.op("dve", lambda e: e.tensor_tensor(out=beff, in0=ckT[:, 0, :], in1=ckT[:, 1, :], op=ALU.add), reads=CK, writes=["beff"])
                S.op("dve", lambda e: e.tensor_tensor(out=beff, in0=beff, in1=ckT[:, 2, :], op=ALU.add), reads=CK + ["beff"], writes=["beff"])
                S.op("dve", lambda e: e.tensor_tensor(out=beff, in0=beff, in1=bupT, op=ALU.mult), reads=["beff", "bupT"], writes=["beff"])
                S.op("dve", lambda e: e.tensor_tensor(out=beff, in0=beff, in1=cbT, op=ALU.add), reads=["beff", "cbT"], writes=["beff"])
                S.op("dve", lambda e: e.scalar_tensor_tensor(out=e0, in0=bupT, scalar=-1.0, in1=ckT[:, 0, :], op0=ALU.mult, op1=ALU.mult), reads=["bupT"] + CK, writes=["e0"])
                S.op("dve", lambda e: e.scalar_tensor_tensor(out=e2, in0=bupT, scalar=-1.0, in1=ckT[:, 2, :], op0=ALU.mult, op1=ALU.mult), reads=["bupT"] + CK, writes=["e2"])
                for j4 in range(0, 22, 6):
                    je = min(22, j4 + 6)
                    S.dma("pool", lambda e, j4=j4, je=je: e.dma_start(
                        out=wd[:, j4:je, :], in_=T["w_down"][l][j4 * 128:je * 128, :].rearrange("(j q) n -> q j n", q=128)),
                        writes=[("wd", j4)])
                WDK = [("wd", j4) for j4 in range(0, 22, 6)]
                for jb in range(11):
                    w = wu[jb % 2]
                    wk = ("wu", jb % 2)
                    for gv in range(2):
                        c0 = gv * DFF + jb * 256
                        S.dma("pool", lambda e, w=w, gv=gv, c0=c0: e.dma_start(
                            out=w[:, gv], in_=T["w_up"][l][:, c0:c0 + 256].rearrange("(kc q) n -> q kc n", q=128)),
                            writes=[(wk, gv)], key=(wk, gv))
                    for sub in range(2):
                        j = jb * 2 + sub
                        bs = (j % 2) * 4
                        for gv in range(2):
                            for hf in range(2):
                                b = bs + gv * 2 + hf
                                for kc in range(8):
                                    mm(ps[:, b, :], w[:, gv, kc, sub * 128:(sub + 1) * 128], xnT[:, kc, hf * 512:(hf + 1) * 512],
                                       kc == 0, kc == 7, [(wk, gv), XK[kc]], PSK[b])
                        for gv in range(2):
                            b = bs + gv * 2
                            col = gv * 22 + j
                            t_ = tc[gv]
                            tk = ("tc", gv)
                            pin = ps[:, b:b + 2, :].rearrange("p b (s t) -> p (b s) t", t=slen) if nseq == 4 else \
                                ps[:, b:b + 2, :].rearrange("p (o b) t -> p o (b t)", o=1)
                            PB = [PSK[b], PSK[b + 1]]
                            S.op("act", lambda e, t_=t_, pin=pin, col=col: e.activation(out=t_, in_=pin, func=AF.Identity,
                                                                                      bias=beff[:, col:col + 1], scale=ckT[:, 1, col:col + 1]),
                                 reads=PB + ["beff", CK[1]], writes=[tk])
                            S.op("dve", lambda e, t_=t_, pin=pin, col=col: e.scalar_tensor_tensor(out=t_[:, :, 1:slen], in0=pin[:, :, 0:slen - 1],
                                                                                                scalar=ckT[:, 0, col:col + 1], in1=t_[:, :, 1:slen],
                                                                                                op0=ALU.mult, op1=ALU.add),
                                 reads=PB + [CK[0], tk], writes=[tk])
                            S.op("dve", lambda e, t_=t_, pin=pin, col=col: e.scalar_tensor_tensor(out=t_[:, :, 0:slen - 1], in0=pin[:, :, 1:slen],
                                                                                                scalar=ckT[:, 2, col:col + 1], in1=t_[:, :, 0:slen - 1],
                                                                                                op0=ALU.mult, op1=ALU.add),
                                 reads=PB + [CK[2], tk], writes=[tk])
                            S.op("dve", lambda e, t_=t_, col=col: e.tensor_scalar(out=t_[:, :, 0], in0=t_[:, :, 0], scalar1=e0[:, col:col + 1], scalar2=None, op0=ALU.add),
                                 reads=[tk, "e0"], writes=[tk])
                            S.op("dve", lambda e, t_=t_, col=col: e.tensor_scalar(out=t_[:, :, slen - 1], in0=t_[:, :, slen - 1], scalar1=e2[:, col:col + 1], scalar2=None,
                                                                                  op0=ALU.add),
                                 reads=[tk, "e2"], writes=[tk])
                        S.op("act", lambda e: e.activation(out=sg, in_=tc[0], func=AF.Silu), reads=[("tc", 0)], writes=["sg"])
                        go = gT[:, j, :].rearrange("p (s t) -> p s t", s=nseq)
                        S.op("dve", lambda e, go=go: e.tensor_tensor(out=go, in0=sg, in1=tc[1], op=ALU.mult),
                             reads=["sg", ("tc", 1)], writes=[("gT", j)])
                bi = 0
                for oc in range(8):
                    for hf in range(2):
                        b = bi % 8
                        bi += 1
                        for j in range(22):
                            mm(ps[:, b, :], wd[:, j, oc * 128:(oc + 1) * 128], gT[:, j, hf * 512:(hf + 1) * 512], j == 0, j == 21,
                               [WDK[j // 6], ("gT", j)], PSK[b])
                        resid(oc, hf, b, 40)
                AR.release(m)

            def attention(l):
                L = LAYERS[l]
                dh, H, ckv = L["dh"], L["H"], L["ckv"]
                G = H // 4
                nkc = ckv // 128
                ctot = D + 2 * ckv
                scale = dh ** -0.5
                rope = (p == 1)
                m = AR.mark()
                norm_mod(A1, "A1", 0)
                if ATT_STOP <= -1:
                    AR.release(m)
                    return
                wbig = AR.alloc([128, 8, ctot], BF16)
                qT = AR.alloc([128, 8, NT], BF16)
                kT = AR.alloc([128, nkc, NT], BF16)
                vx = AR.alloc([128, 8, 4, dh + 2], BF16)
                qn_c = AR.alloc([128, 1], F32)
                kn_c = AR.alloc([128, 1], F32)
                wo_b = AR.alloc([128, 8, D], BF16)
                esink = AR.alloc([128, 16], F32) if L["sink"] else None
                if p == 1:
                    kcT = AR.alloc([128, nkc, 512], BF16)
                    vcx = AR.alloc([128, 4, 4, dh + 2], BF16)
                tmp_mark = AR.mark()
                sqb = [AR.alloc([128, 512], BF16) for _ in range(2)]
                rs = [AR.alloc([128, 512], F32) for _ in range(2)]
                qf = [AR.alloc([128, 512], F32) for _ in range(2)]
                S.dma("pool", lambda e: e.dma_start(out=wbig[:, :, 0:768], in_=T[L["wqkv"]][:, 0:768].rearrange("(kc q) n -> q kc n", q=128)),
                      writes=[("wbig", 0)])
                S.dma("pool", lambda e: e.dma_start(out=wbig[:, :, 768:ctot], in_=T[L["wqkv"]][:, 768:ctot].rearrange("(kc q) n -> q kc n", q=128)),
                      writes=[("wbig", 1)])
                WB = [("wbig", 0), ("wbig", 1)]
                S.dma("pool", lambda e: e.dma_start(out=wo_b, in_=T[L["wo"]].rearrange("(kc q) n -> q kc n", q=128)), writes=["wo_b"])
                small_load_col(qn_c, T[L["qn"]], dh, "qn_c")
                small_load_col(kn_c, T[L["kn"]], dh, "kn_c")
                QNK = ["qn_c"]
                KNK = ["kn_c"]
                if rope:
                    cosT = AR.alloc([128, NT], F32)
                    sinT = AR.alloc([128, NT], F32)
                    ri = 0 if dh == 64 else 2
                    S.dma("sp", lambda e: e.dma_start(out=cosT, in_=T["rope"][ri]), writes=["cosT"])
                    S.dma("sp", lambda e: e.dma_start(out=sinT, in_=T["rope"][ri + 1]), writes=["sinT"])
                    pm = pa_b if dh == 64 else pc_b
                    r1 = [AR.alloc([128, 512], F32) for _ in range(2)]
                    qb16 = [AR.alloc([128, 512], BF16) for _ in range(2)]
                if L["sink"]:
                    S.dma("sp", lambda e: e.dma_start(out=tst2[0:1, 0:16], in_=T[L["sink"]].rearrange("(o q) -> o q", o=1)),
                          writes=[("tst2", 0)], key=("tst2", 0))
                    S.op("pe", lambda e: e.matmul(ps[:, 7, 0:16], ones_f[0:2, :], tst2[0:2, 0:16], start=True, stop=True),
                         reads=[("tst2", 0), "tst2z", "ones_f"], writes=[PSK[7], ("tst2", 0)])
                    S.op("act", lambda e: e.activation(out=esink, in_=ps[:, 7, 0:16], func=AF.Exp), reads=[PSK[7]], writes=["esink"])
                if p == 0:
                    kst = [AR.alloc([128, ckv], F32) for _ in range(1)]
                    vst = [AR.alloc([128, ckv], F32) for _ in range(1)]
                    kf32 = AR.alloc([128, nkc, NT], F32)
                S.op("pool", lambda e: e.memset(vx, 1.0), writes=["vx_init"])

                if ATT_STOP <= 0:
                    AR.release(m)
                    return
                for tt in range(8):
                    b = 4 + tt % 2
                    for kc in range(8):
                        mm(ps[:, b, 0:ckv], xnT[:, kc, tt * 128:(tt + 1) * 128], wbig[:, kc, D + ckv:D + 2 * ckv], kc == 0, kc == 7,
                           [XK[kc], WB[1]], PSK[b])
                    if p == 0:
                        v_ = vst[0]
                        vk = ("vst", 0)
                        S.op("dve", lambda e, b=b, v_=v_: e.tensor_copy(out=v_, in_=ps[:, b, 0:ckv]), reads=[PSK[b]], writes=[vk])
                        S.op("act", lambda e, v_=v_, tt=tt: e.activation(out=vx[:, tt, :, 0:dh], in_=v_.rearrange("p (g d) -> p g d", g=4),
                                                                        func=AF.Copy),
                             reads=[vk, "vx_init"], writes=[("vx", tt)])
                        S.dma("sp", lambda e, v_=v_, tt=tt: e.dma_start(out=T[L["nv"]][tt * 128:(tt + 1) * 128, :], in_=v_), reads=[vk])
                    else:
                        S.op("act", lambda e, b=b, tt=tt: e.activation(out=vx[:, tt, :, 0:dh], in_=ps[:, b, 0:ckv].rearrange("p (g d) -> p g d", g=4),
                                                                       func=AF.Copy),
                             reads=[PSK[b], "vx_init"], writes=[("vx", tt)])
                if ATT_STOP <= 1:
                    AR.release(m)
                    return
                def lhs_cols(kind, c):
                    if kind == "q":
                        if dh == 128:
                            return [(c * 128, 128, 0)]
                        mq, i = c // 4, c % 4
                        return [((8 * mq + i) * 64, 64, 0), ((8 * mq + 4 + i) * 64, 64, 64)]
                    return [(D + c * 128, 128, 0)]

                blocks = [(kind, c, hf) for kind, nch in (("k", nkc), ("q", 8)) for c in range(nch) for hf in range(2)]
                nb_ = len(blocks)
                sq3 = sqb + [AR.alloc([128, 512], BF16)]

                def stage_a(i):
                    kind, c, hf = blocks[i]
                    b = i % 3
                    sl = slice(hf * 512, (hf + 1) * 512)
                    for (c0, w_, po) in lhs_cols(kind, c):
                        for kc in range(8):
                            mm(ps[po:po + w_, b, :], wbig[:, kc, c0:c0 + w_], xnT[:, kc, sl], kc == 0, kc == 7,
                               [XK[kc], WB[0], WB[1]], PSK[b])
                    sq_ = sq3[i % 3]
                    S.op("act", lambda e, sq_=sq_, b=b: e.activation(out=sq_, in_=ps[:, b, :], func=AF.Square), reads=[PSK[b]], writes=[("sqb", i % 3)])

                def stage_b(i):
                    kind, c, hf = blocks[i]
                    b = i % 3
                    b2 = 3 + i % 2
                    sl = slice(hf * 512, (hf + 1) * 512)
                    sq_ = sq3[i % 3]
                    mm(ps[:, b2, :], bd64_b if dh == 64 else ones_b, sq_, True, True, [("sqb", i % 3), "cst_misc", "ones_b"], PSK[b2])
                    rs_ = rs[i % 2]
                    rk = ("rs", i % 2)
                    S.op("act", lambda e, b2=b2: e.activation(out=lnv, in_=ps[:, b2, :], func=AF.Ln, bias=EPS, scale=1.0 / dh),
                         reads=[PSK[b2]], writes=["lnv"])
                    S.op("act", lambda e, rs_=rs_: e.activation(out=rs_, in_=lnv, func=AF.Exp, scale=-0.5), reads=["lnv"], writes=[rk])
                    gcol, gk = (qn_c, QNK) if kind == "q" else (kn_c, KNK)
                    dstT = qT if kind == "q" else kT
                    dk = (kind + "T", c, hf)
                    if not rope:
                        if kind == "k":
                            S.op("dve", lambda e, c=c, sl=sl, b=b, rs_=rs_: e.scalar_tensor_tensor(
                                out=kf32[:, c, sl], in0=ps[:, b, :], scalar=kn_c[:, 0:1], in1=rs_, op0=ALU.mult, op1=ALU.mult),
                                reads=[PSK[b], rk] + gk, writes=[("kf32", c, hf)])
                            S.op("pool", lambda e, c=c, sl=sl: e.tensor_copy(out=kT[:, c, sl], in_=kf32[:, c, sl]),
                                 reads=[("kf32", c, hf)], writes=[dk])
                        else:
                            S.op("dve", lambda e, c=c, sl=sl, b=b, rs_=rs_, gcol=gcol, dstT=dstT: e.scalar_tensor_tensor(
                                out=dstT[:, c, sl], in0=ps[:, b, :], scalar=gcol[:, 0:1], in1=rs_, op0=ALU.mult, op1=ALU.mult),
                                reads=[PSK[b], rk] + gk, writes=[dk])
                    else:
                        q_, qk_ = qf[i % 2], ("qf", i % 2)
                        qb_, qbk = qb16[i % 2], ("qb16", i % 2)
                        S.op("dve", lambda e, q_=q_, b=b, rs_=rs_, gcol=gcol: e.scalar_tensor_tensor(
                            out=q_, in0=ps[:, b, :], scalar=gcol[:, 0:1], in1=rs_, op0=ALU.mult, op1=ALU.mult),
                            reads=[PSK[b], rk] + gk, writes=[qk_])
                        S.op("pool", lambda e, q_=q_, qb_=qb_: e.tensor_copy(out=qb_, in_=q_), reads=[qk_], writes=[qbk])

                def stage_c(i):
                    kind, c, hf = blocks[i]
                    b3 = 5 + i % 3
                    sl = slice(hf * 512, (hf + 1) * 512)
                    dstT = qT if kind == "q" else kT
                    dk = (kind + "T", c, hf)
                    q_, qk_ = qf[i % 2], ("qf", i % 2)
                    qb_, qbk = qb16[i % 2], ("qb16", i % 2)
                    r_, r1k = r1[i % 2], ("r1", i % 2)
                    mm(ps[:, b3, :], pm, qb_, True, True, [qbk, "cst_misc"], PSK[b3])
                    S.op("dve", lambda e, r_=r_, b3=b3, sl=sl: e.tensor_tensor(out=r_, in0=ps[:, b3, :], in1=sinT[:, sl], op=ALU.mult),
                         reads=[PSK[b3], "sinT"], writes=[r1k])
                    S.op("pool", lambda e, q_=q_, sl=sl: e.tensor_tensor(out=q_, in0=q_, in1=cosT[:, sl], op=ALU.mult),
                         reads=[qk_, "cosT"], writes=[qk_])
                    S.op("dve", lambda e, q_=q_, r_=r_, c=c, sl=sl, dstT=dstT: e.tensor_tensor(out=dstT[:, c, sl], in0=q_, in1=r_, op=ALU.add),
                         reads=[qk_, r1k], writes=[dk])

                for step in range(nb_ + 2):
                    if step < nb_:
                        stage_a(step)
                    if 0 <= step - 1 < nb_:
                        stage_b(step - 1)
                    if rope and 0 <= step - 2 < nb_:
                        stage_c(step - 2)
                if ATT_STOP <= 2:
                    AR.release(m)
                    return
                if p == 0:
                    for tt in range(8):
                        b = 6 + tt % 2
                        for c in range(nkc):
                            S.op("pe", lambda e, b=b, c=c, tt=tt: e.transpose(ps[:, b, c * 128:(c + 1) * 128], kf32[:, c, tt * 128:(tt + 1) * 128], ident_f),
                                 reads=[("kf32", c, tt // 4), "ident_f"], writes=[PSK[b]], acc=True)
                        k_ = kst[0]
                        kk = ("kst", 0)
                        S.op("dve", lambda e, b=b, k_=k_: e.tensor_copy(out=k_, in_=ps[:, b, 0:ckv]), reads=[PSK[b]], writes=[kk])
                        S.dma("sp", lambda e, k_=k_, tt=tt: e.dma_start(out=T[L["nk"]][tt * 128:(tt + 1) * 128, :], in_=k_), reads=[kk])

                if p == 1:
                    cst_ = [AR.alloc([128, ckv], F32) for _ in range(1)]
                    cb_ = [AR.alloc([128, ckv], BF16) for _ in range(1)]
                    S.op("pool", lambda e: e.memset(vcx, 1.0), writes=["vcx_init"])
                    for t4 in range(4):
                        c_ = cst_[0]
                        ck_ = ("cst_", 0)
                        S.dma("sp", lambda e, c_=c_, t4=t4: e.dma_start(out=c_, in_=T[L["cv"]][t4 * 128:(t4 + 1) * 128, :]), writes=[ck_])
                        S.op("dve", lambda e, c_=c_, t4=t4: e.tensor_copy(out=vcx[:, t4, :, 0:dh], in_=c_.rearrange("p (g d) -> p g d", g=4)),
                             reads=[ck_, "vcx_init"], writes=[("vcx", t4)])
                    for t4 in range(4):
                        c_ = cst_[0]
                        ck_ = ("cst_", 0)
                        bb = cb_[0]
                        bk = ("cb_", 0)
                        S.dma("sp", lambda e, c_=c_, t4=t4: e.dma_start(out=c_, in_=T[L["ck"]][t4 * 128:(t4 + 1) * 128, :]), writes=[ck_])
                        S.op("dve", lambda e, c_=c_, bb=bb: e.tensor_copy(out=bb, in_=c_), reads=[ck_], writes=[bk])
                        b = 6 + t4 % 2
                        pb = ps[:, b, 0:64 * nkc].bitcast(BF16).rearrange("p (c t) -> p c t", c=nkc)
                        for c in range(nkc):
                            S.op("pe", lambda e, pb=pb, c=c, bb=bb: e.transpose(pb[:, c, :], bb[:, c * 128:(c + 1) * 128], ident_b),
                                 reads=[bk, "ident_b"], writes=[PSK[b]], acc=True)
                        S.op("dve", lambda e, pb=pb, t4=t4: e.tensor_copy(out=kcT[:, :, t4 * 128:(t4 + 1) * 128], in_=pb),
                             reads=[PSK[b]], writes=[("kcT", t4)])

                if ATT_STOP <= 3:
                    AR.release(m)
                    return
                nkt_max = 7 if L["kind"] == "A" else 12
                S.barrier()
                AR.release(tmp_mark)
                PT = [AR.alloc([128, nkt_max, G * 128], BF16) for _ in range(2)]
                otok = [AR.alloc([128, D], BF16) for _ in range(2)]
                OT = xnT
                den = AR.alloc([128, 4], F32)
                rden = AR.alloc([128, 4], F32)
                sbc = [0]

                def key_tiles(qt):
                    s_ = qt // (slen // 128)
                    kts = []
                    if p == 1:
                        kts += [("c", t4, None) for t4 in range(4)]
                        if L["kind"] == "A":
                            if qt > 0:
                                kts.append(("n", qt - 1, mlo_b))
                            kts.append(("n", qt, None))
                            if qt < 7:
                                kts.append(("n", qt + 1, mhi_b))
                        else:
                            kts += [("n", t8, None) for t8 in range(8)]
                    else:
                        kts += [("n", s_ * 2 + t2, None) for t2 in range(2)]
                    return kts

                def emit_scores(qt, g, it):
                    kts = key_tiles(qt)
                    pt = PT[it % 2]
                    ptk = ("PT", it % 2)
                    if dh == 64:
                        mq, e2_ = g // 2, g % 2
                        prt = slice(64 * e2_, 64 * e2_ + 64)
                        rhs = qT[prt, 4 * mq:4 * mq + 4, qt * 128:(qt + 1) * 128]
                        qreads = [("qT", 4 * mq + i, qt // 4) for i in range(4)]
                        kch = mq
                    else:
                        prt = slice(0, 128)
                        rhs = qT[:, 2 * g:2 * g + 2, qt * 128:(qt + 1) * 128]
                        qreads = [("qT", 2 * g + i, qt // 4) for i in range(2)]
                        kch = g
                    for ki, (src, ti, msk) in enumerate(kts):
                        b = sbc[0] % 4
                        sbc[0] += 1
                        if src == "c":
                            lhsT = kcT[prt, kch, ti * 128:(ti + 1) * 128]
                            kr = [("kcT", ti)]
                        else:
                            lhsT = kT[prt, kch, ti * 128:(ti + 1) * 128]
                            kr = [("kT", kch, ti // 4)]
                        mm(ps[:, b, 0:G * 128], lhsT, rhs, True, True, kr + qreads, PSK[b])
                        S.op("act", lambda e, pt=pt, ki=ki, b=b: e.activation(out=pt[:, ki, :], in_=ps[:, b, 0:G * 128], func=AF.Exp, scale=scale),
                             reads=[PSK[b]], writes=[(ptk, ki)])
                        if msk is not None:
                            S.op("pool", lambda e, pt=pt, ki=ki, msk=msk: e.tensor_tensor(
                                out=pt[:, ki, :], in0=pt[:, ki, :], in1=msk[:, 0:G, :].rearrange("p g q -> p (g q)"), op=ALU.mult),
                                reads=[(ptk, ki), "cst_misc"], writes=[(ptk, ki)])

                def emit_pv(qt, g, it):
                    kts = key_tiles(qt)
                    pt = PT[it % 2]
                    ptk = ("PT", it % 2)
                    ob = 4 + it % 2
                    ot = otok[qt % 2]
                    otk = ("otok", qt % 2)
                    opv = ps[:, ob, 0:G * (dh + 1)].rearrange("p (i d) -> p i d", i=G)
                    for i in range(G):
                        for ki, (src, ti, msk) in enumerate(kts):
                            if src == "c":
                                vr = vcx[:, ti, g, 0:dh + 1]
                                vk = [("vcx", ti)]
                            else:
                                vr = vx[:, ti, g, 0:dh + 1]
                                vk = [("vx", ti)]
                            mm(opv[:, i, :], pt[:, ki, i * 128:(i + 1) * 128], vr, ki == 0, ki == len(kts) - 1,
                               [(ptk, ki)] + vk, PSK[ob])
                    if esink is not None:
                        S.op("dve", lambda e, opv=opv, g=g: e.tensor_tensor(out=den[:, 0:G], in0=opv[:, :, dh], in1=esink[:, g * G:(g + 1) * G], op=ALU.add),
                             reads=[PSK[ob], "esink"], writes=["den"])
                        S.op("dve", lambda e: e.reciprocal(out=rden[:, 0:G], in_=den[:, 0:G]), reads=["den"], writes=["rden"])
                    else:
                        S.op("dve", lambda e, opv=opv: e.reciprocal(out=rden[:, 0:G], in_=opv[:, :, dh]), reads=[PSK[ob]], writes=["rden"])
                    for i in range(G):
                        h = g * G + i
                        S.op("dve", lambda e, opv=opv, i=i, h=h, ot=ot: e.tensor_scalar(out=ot[:, h * dh:(h + 1) * dh], in0=opv[:, i, 0:dh],
                                                                                         scalar1=rden[:, i:i + 1], scalar2=None, op0=ALU.mult),
                             reads=[PSK[ob], "rden"], writes=[(otk, h)])
                    if g == 3:
                        b = 6 + qt % 2
                        pb = ps[:, b, :].bitcast(BF16).rearrange("p (c t) -> p c t", c=8)
                        for c in range(8):
                            S.op("pe", lambda e, pb=pb, c=c, ot=ot: e.transpose(pb[:, c, :], ot[:, c * 128:(c + 1) * 128], ident_b),
                                 reads=[(otk, h) for h in range(H)] + ["ident_b"], writes=[PSK[b]], acc=True)
                        S.op("act", lambda e, pb=pb, qt=qt: e.activation(out=OT[:, :, qt * 128:(qt + 1) * 128], in_=pb, func=AF.Copy),
                             reads=[PSK[b]], writes=[("OT", qt)] + XK)

                items = [(qt, g) for qt in range(8) for g in range(4)]
                emit_scores(items[0][0], items[0][1], 0)
                for it, (qt, g) in enumerate(items):
                    if it + 1 < len(items):
                        emit_scores(items[it + 1][0], items[it + 1][1], it + 1)
                    emit_pv(qt, g, it)
                bi = 0
                for oc in range(8):
                    for hf in range(2):
                        b = bi % 4
                        bi += 1
                        for kc in range(8):
                            mm(ps[:, b, :], wo_b[:, kc, oc * 128:(oc + 1) * 128], OT[:, kc, hf * 512:(hf + 1) * 512], kc == 0, kc == 7,
                               ["wo_b"] + [("OT", hf * 4 + q) for q in range(4)], PSK[b])
                        resid(oc, hf, b, 16)
                AR.release(m)

            def s5_layer(l):
                CH = 128 // nseq
                m = AR.mark()
                norm_mod(A1, "A1", 0, perm8=True)
                U = AR.alloc([128, 64, 128], BF16)
                TB = AR.alloc([128, 2, 64, 128], BF16)
                SS = AR.alloc([128, 2, 64, 128], F32)
                ALL = AR.alloc([128, 2, 64], F32)
                ALS = AR.alloc([128, 2, 64], F32)
                xf = xnT.rearrange("q a n -> q (a n)").bitcast(F32)
                H0v = xf[:, 0:128 * nseq].rearrange("q (r g s) -> q r g s", r=2, g=64)
                L8 = AR.alloc([128, 2, 128], F32)
                FS = AR.alloc([128, 2, 64], F32)
                FSo = AR.alloc([128, 2, 128], F32)
                tt_ = [[xf[:, 128 * nseq * (1 + 2 * a_ + b_):128 * nseq * (2 + 2 * a_ + b_)].rearrange("q (r g s) -> q r g s", r=2, g=64)
                        for b_ in range(2)] for a_ in range(2)]
                S.dma("sp", lambda e: e.dma_start(out=T["D1"][p].rearrange("(fc q) n -> q fc n", q=128), in_=xnT), reads=XK, writes=["D1"])
                d1v = T["D1"][p].rearrange("(g c) (t k) -> t c g k", c=16, t=8)
                for t in range(8):
                    S.dma("sp", lambda e, t=t: e.dma_start(out=U[16 * t:16 * (t + 1)], in_=d1v[t]), reads=["D1"], writes=[("U", t)])
                UK = [("U", t) for t in range(8)]
                for r_ in range(2):
                    for g4 in range(4):
                        S.dma("sp", lambda e, r_=r_, g4=g4: e.dma_start(out=TB[:, r_, g4 * 16:(g4 + 1) * 16, :], in_=T["TS"][r_, :, g4 * 16:(g4 + 1) * 16, :]),
                              writes=[("TB", r_, g4)])
                S.dma("sp", lambda e: e.dma_start(out=L8[0:64], in_=T["TP8"].rearrange("r (d g) p -> g r d p", d=2)), writes=["L8"])
                for r_ in range(2):
                    S.op("pe", lambda e, r_=r_: e.transpose(ps[:, 7, 0:64], L8[0:64, r_, :], ident_f[0:64, 0:64]), reads=["L8", "ident_f"], writes=[PSK[7]])
                    if r_ == 0:
                        S.op("dve", lambda e: e.tensor_copy(out=ALL[:, 0, :], in_=ps[:, 7, 0:64]), reads=[PSK[7]], writes=["AL"])
                        S.op("dve", lambda e: e.tensor_copy(out=ALL[:, 1, :], in_=ps[:, 7, 0:64]), reads=[PSK[7]], writes=["AL"])
                    else:
                        S.op("dve", lambda e: e.tensor_scalar(out=ALS[:, 0, :], in0=ps[:, 7, 0:64], scalar1=-1.0, scalar2=None, op0=ALU.mult), reads=[PSK[7]], writes=["AL"])
                        S.op("dve", lambda e: e.tensor_copy(out=ALS[:, 1, :], in_=ps[:, 7, 0:64]), reads=[PSK[7]], writes=["AL"])
                if p == 1:
                    for r_ in range(2):
                        S.dma("sp", lambda e, r_=r_: e.dma_start(out=L8[0:64, r_, :].rearrange("g (d p) -> g d p", d=2),
                                                                  in_=T["st1"][:, r_].rearrange("d g p -> g d p")), reads=["L8"], writes=["L8"], key=("L8", r_))
                    for r_ in range(2):
                        S.op("pe", lambda e, r_=r_: e.transpose(ps[:, 7, 0:64], L8[0:64, r_, :], ident_f[0:64, 0:64]), reads=["L8", "ident_f"], writes=[PSK[7]])
                        S.op("dve", lambda e, r_=r_: e.tensor_copy(out=H0v[:, r_, :, 0], in_=ps[:, 7, 0:64]), reads=[PSK[7]], writes=["H0"] + XK)
                else:
                    S.op("dve", lambda e: e.memset(H0v, 0.0), writes=["H0"] + XK)
                for gq in range(16):
                    for r_ in range(2):
                        b = (gq * 2 + r_) % 4
                        for gi in range(4):
                            g = gq * 4 + gi
                            S.op("pe", lambda e, b=b, gi=gi, g=g, r_=r_: e.matmul(ps[:, b, gi * 128:(gi + 1) * 128], TB[:, r_, g, :], U[:, g, :], start=True, stop=True),
                                 reads=UK + [("TB", r_, g // 16)], writes=[PSK[b]], acc=True)
                        eng = "act" if r_ == 0 else "dve"
                        if r_ == 0:
                            S.op("act", lambda e, b=b, gq=gq: e.activation(out=SS[:, 0, gq * 4:(gq + 1) * 4, :].rearrange("q g k -> q (g k)"), in_=ps[:, b, :], func=AF.Copy),
                                 reads=[PSK[b]], writes=[("SS", 0)])
                        else:
                            S.op("dve", lambda e, b=b, gq=gq: e.tensor_copy(out=SS[:, 1, gq * 4:(gq + 1) * 4, :].rearrange("q g k -> q (g k)"), in_=ps[:, b, :]),
                                 reads=[PSK[b]], writes=[("SS", 1)])
                S.barrier()
                NS, CHV = nseq, CH
                SSv = SS.rearrange("q r g (s k) -> q r g s k", k=CHV)
                for d, eng in ((0, "dve"), (1, "pool")):
                    hs = slice(64 * d, 64 * d + 64)
                    order = list(range(CHV)) if d == 0 else list(range(CHV - 1, -1, -1))
                    t1, t2 = tt_[d]
                    k1, k2, kS = ("rt1", d), ("rt2", d), ("SSd", d)
                    all_ = ALL[hs].unsqueeze(3).broadcast_to([64, 2, 64, NS])
                    als_ = ALS[hs].unsqueeze(3).broadcast_to([64, 2, 64, NS])
                    for ii, kk in enumerate(order):
                        if ii == 0:
                            prev, prev_sw = H0v[hs], H0v[hs, ::-1]
                        else:
                            kp = order[ii - 1]
                            prev, prev_sw = SSv[hs, :, :, :, kp], SSv[hs, ::-1, :, :, kp]
                        cur = SSv[hs, :, :, :, kk]
                        S.op(eng, lambda e, hs=hs, t1=t1, prev=prev, all_=all_: e.tensor_tensor(out=t1[hs], in0=prev, in1=all_, op=ALU.mult),
                             reads=[kS, "AL", "H0"], writes=[k1])
                        S.op(eng, lambda e, hs=hs, t2=t2, prev_sw=prev_sw, als_=als_: e.tensor_tensor(out=t2[hs], in0=prev_sw, in1=als_, op=ALU.mult),
                             reads=[kS, "AL", "H0"], writes=[k2])
                        S.op(eng, lambda e, hs=hs, t1=t1, t2=t2: e.tensor_tensor(out=t1[hs], in0=t1[hs], in1=t2[hs], op=ALU.add), reads=[k1, k2], writes=[k1])
                        S.op(eng, lambda e, hs=hs, t1=t1, cur=cur: e.tensor_tensor(out=cur, in0=cur, in1=t1[hs], op=ALU.add), reads=[kS, k1], writes=[kS])
                S.barrier()
                if p == 0:
                    for s_ in range(nseq):
                        for r_ in range(2):
                            S.op("dve", lambda e, r_=r_, s_=s_: e.tensor_copy(out=FS[0:64, r_, :], in_=SS[0:64, r_, :, s_ * CH + CH - 1]), reads=[], writes=["FS"])
                            S.op("dve", lambda e, r_=r_, s_=s_: e.tensor_copy(out=FS[64:128, r_, :], in_=SS[64:128, r_, :, s_ * CH]), reads=[], writes=["FS"])
                            S.op("pe", lambda e, r_=r_: e.transpose(ps[0:64, 7, r_ * 128:(r_ + 1) * 128], FS[:, r_, :], ident_f), reads=["FS", "ident_f"], writes=[PSK[7]], acc=True)
                        S.op("dve", lambda e: e.tensor_copy(out=FSo[0:64].rearrange("g r n -> g (r n)"), in_=ps[0:64, 7, 0:256]), reads=[PSK[7]], writes=["FSo"])
                        for r_ in range(2):
                            S.dma("sp", lambda e, s_=s_, r_=r_: e.dma_start(out=T["nst"][s_][:, r_].rearrange("d g p -> g d p"),
                                                                           in_=FSo[0:64, r_, :].rearrange("g (d p) -> g d p", d=2)), reads=["FSo"], key=("nst", r_))
                for r_ in range(2):
                    for s_ in range(nseq):
                        k0 = s_ * CH
                        S.op("act", lambda e, r_=r_, k0=k0: e.activation(out=TB[0:64, r_, :, k0 + 1:k0 + CH], in_=SS[0:64, r_, :, k0:k0 + CH - 1], func=AF.Copy),
                             reads=[], writes=["Hin"])
                        S.op("act", lambda e, r_=r_, k0=k0, s_=s_: e.activation(out=TB[0:64, r_, :, k0], in_=H0v[0:64, r_, :, s_], func=AF.Copy), reads=["H0"], writes=["Hin"])
                        S.op("dve", lambda e, r_=r_, k0=k0: e.tensor_copy(out=TB[64:128, r_, :, k0:k0 + CH - 1], in_=SS[64:128, r_, :, k0 + 1:k0 + CH]),
                             reads=[], writes=["Hin"])
                        S.op("dve", lambda e, r_=r_, k0=k0, s_=s_: e.tensor_copy(out=TB[64:128, r_, :, k0 + CH - 1], in_=H0v[64:128, r_, :, s_]), reads=["H0"], writes=["Hin"])
                S.barrier()
                SSb = SS.rearrange("q r g k -> q (r g k)").bitcast(BF16)
                EL = SSb[:, 0:16384].rearrange("q (r g n) -> q r g n", r=2, g=64)
                ML = SSb[:, 16384:24576].rearrange("q (g n) -> q g n", g=64)
                YC = SSb[:, 24576:32768].rearrange("q (g n) -> q g n", g=64)
                for r_ in range(2):
                    for g4 in range(4):
                        S.dma("sp", lambda e, r_=r_, g4=g4: e.dma_start(out=EL[:, r_, g4 * 16:(g4 + 1) * 16, :], in_=T["TE"][r_, :, g4 * 16:(g4 + 1) * 16, :]),
                              writes=[("EL", r_, g4)])
                for g4 in range(4):
                    S.dma("sp", lambda e, g4=g4: e.dma_start(out=ML[:, g4 * 16:(g4 + 1) * 16, :], in_=T["TM"][:, g4 * 16:(g4 + 1) * 16, :]),
                          writes=[("ML", g4)])
                gu = [AR.alloc([128, 512], F32) for _ in range(2)]
                gs = [AR.alloc([128, 512], F32) for _ in range(2)]
                for gq in range(16):
                    b = gq % 4
                    for gi in range(4):
                        g = gq * 4 + gi
                        o_ = ps[:, b, gi * 128:(gi + 1) * 128]
                        S.op("pe", lambda e, o_=o_, g=g: e.matmul(o_, ML[:, g, :], U[:, g, :], start=True, stop=False), reads=[("ML", g // 16)] + UK, writes=[PSK[b]], acc=True)
                        S.op("pe", lambda e, o_=o_, g=g: e.matmul(o_, EL[:, 0, g, :], TB[:, 0, g, :], start=False, stop=False), reads=[("EL", 0, g // 16), "Hin"], writes=[PSK[b]], acc=True)
                        S.op("pe", lambda e, o_=o_, g=g: e.matmul(o_, EL[:, 1, g, :], TB[:, 1, g, :], start=False, stop=True), reads=[("EL", 1, g // 16), "Hin"], writes=[PSK[b]], acc=True)
                    u_, s2 = gu[gq % 2], gs[gq % 2]
                    uk, sk = ("gu", gq % 2), ("gs", gq % 2)
                    S.op("act", lambda e, u_=u_, b=b: e.activation(out=u_, in_=ps[:, b, :], func=AF.Square), reads=[PSK[b]], writes=[uk])
                    S.op("dve", lambda e, u_=u_: e.tensor_scalar(out=u_, in0=u_, scalar1=0.044715, scalar2=1.0, op0=ALU.mult, op1=ALU.add), reads=[uk], writes=[uk])
                    S.op("dve", lambda e, u_=u_, b=b: e.tensor_tensor(out=u_, in0=u_, in1=ps[:, b, :], op=ALU.mult), reads=[uk, PSK[b]], writes=[uk])
                    S.op("act", lambda e, u_=u_, s2=s2: e.activation(out=s2, in_=u_, func=AF.Sigmoid, scale=1.5957691216057308), reads=[uk], writes=[sk])
                    S.op("dve", lambda e, s2=s2, b=b, gq=gq: e.tensor_tensor(out=YC[:, gq * 4:(gq + 1) * 4, :].rearrange("q g k -> q (g k)"), in0=s2, in1=ps[:, b, :], op=ALU.mult),
                         reads=[sk, PSK[b]], writes=["YC"])
                d2v = T["D2"][p].rearrange("(g c) (t k) -> t c g k", c=16, t=8)
                for t in range(8):
                    S.dma("sp", lambda e, t=t: e.dma_start(out=d2v[t], in_=YC[16 * t:16 * (t + 1)]), reads=["YC"], writes=["D2"], key=("D2w", t))
                S.dma("sp", lambda e: e.dma_start(out=xnT, in_=T["D2"][p].rearrange("(fc q) n -> q fc n", q=128)), reads=["D2"], writes=XK, key="D2r")
                S.barrier()
                AR.release(m)
                m = AR.mark()
                wg = AR.alloc([128, 8, 2 * D], BF16)
                sg2 = [AR.alloc([128, 512], F32) for _ in range(2)]
                mx = [AR.alloc([128, 512], F32) for _ in range(2)]
                for h2 in range(2):
                    S.dma("pool", lambda e, h2=h2: e.dma_start(out=wg[:, :, h2 * D:(h2 + 1) * D],
                                                               in_=T["l1_w_glu"][:, h2 * D:(h2 + 1) * D].rearrange("(kc q) n -> q kc n", q=128)),
                          writes=[("wg", h2)])
                it = 0
                for oc in range(8):
                    for hf in range(2):
                        bA, bB = (it % 2) * 2, (it % 2) * 2 + 1
                        s_, x_ = sg2[it % 2], mx[it % 2]
                        sk, xk_ = ("sg2", it % 2), ("mx", it % 2)
                        it += 1
                        for kc in range(8):
                            mm(ps[:, bA, :], wg[:, kc, oc * 128:(oc + 1) * 128], xnT[:, kc, hf * 512:(hf + 1) * 512], kc == 0, kc == 7, [("wg", 0), XK[kc]], PSK[bA])
                        for kc in range(8):
                            mm(ps[:, bB, :], wg[:, kc, D + oc * 128:D + (oc + 1) * 128], xnT[:, kc, hf * 512:(hf + 1) * 512], kc == 0, kc == 7, [("wg", 1), XK[kc]], PSK[bB])
                        S.op("act", lambda e, s_=s_, bB=bB: e.activation(out=s_, in_=ps[:, bB, :], func=AF.Sigmoid), reads=[PSK[bB]], writes=[sk])
                        S.op("dve", lambda e, s_=s_, x_=x_, bA=bA: e.tensor_tensor(out=x_, in0=ps[:, bA, :], in1=s_, op=ALU.mult), reads=[PSK[bA], sk], writes=[xk_])
                        hv = hT[:, oc, :].rearrange("q (k t) -> q t k", t=8)[:, 4 * hf:4 * hf + 4, :]
                        S.op("dve", lambda e, x_=x_, hv=hv, oc=oc: e.scalar_tensor_tensor(out=hv, in0=x_.rearrange("q (t k) -> q t k", t=4), scalar=modT[:, 16 + oc:17 + oc],
                                                                                        in1=hv, op0=ALU.mult, op1=ALU.add),
                             reads=[xk_, "modT", HK[oc]], writes=[HK[oc]])
                AR.release(m)

            for l in range(NLAYERS):
                if STAGE & 1:
                    compute_mod(l)
                S.barrier()
                if STAGE & 2:
                    if LAYERS[l]["kind"] == "B":
                        if not SKIP_S5:
                            s5_layer(l)
                    else:
                        attention(l)
                S.barrier()
                if STAGE & 4:
                    ffn(l)
                S.barrier()

            m0 = AR.mark()
            yst = [AR.alloc([128, D], F32) for _ in range(2)]
            for tt in range(8):
                y_ = yst[tt % 2]
                yk = ("yst", tt % 2)
                for hf in range(2):
                    b = (tt * 2 + hf) % 8
                    for q in range(4):
                        kc = hf * 4 + q
                        S.op("pe", lambda e, b=b, q=q, kc=kc, tt=tt: e.transpose(ps[:, b, q * 128:(q + 1) * 128], hT[:, kc, tt * 128:(tt + 1) * 128], ident_f),
                             reads=[HK[kc], "ident_f"], writes=[PSK[b]], acc=True)
                    if hf:
                        S.op("act", lambda e, b=b, y_=y_: e.activation(out=y_[:, 512:1024], in_=ps[:, b, :], func=AF.Copy), reads=[PSK[b]], writes=[(yk, 1)])
                    else:
                        S.op("dve", lambda e, b=b, y_=y_: e.tensor_copy(out=y_[:, 0:512], in_=ps[:, b, :]), reads=[PSK[b]], writes=[(yk, 0)])
                S.dma("sp", lambda e, y_=y_, tt=tt: e.dma_start(out=T["y"][p, tt * 128:(tt + 1) * 128, :], in_=y_), reads=[(yk, 0), (yk, 1)], key=yk)
            S.barrier()
            AR.release(m0)

        pm_mark = AR.mark()
        prologue_mod_start()
        if not SKIP_S5 and NLAYERS > 1 and (STAGE & 2):
            s5_tables()
        prologue_mod_finish()
        AR.release(pm_mark)
        run_path(0)
        run_path(1)
        S.barrier()
        print('n_sems', len(S.sem_objs), 'insts', {e: len(S.prog[e]) for e in ENGS})
        S.emit()
    return nc


def _consts():
    cst = np.zeros((9, 128, 128), np.float32)
    for tt_ in range(8):
        for cc_ in range(16):
            cst[CST_R16, cc_, tt_ * 16 + cc_] = 1.0
    cst[CST_IDENT] = np.eye(128)
    bd = np.zeros((128, 128), np.float32)
    bd[:64, :64] = 1
    bd[64:, 64:] = 1
    cst[CST_BD64] = bd
    kk = np.arange(128)[:, None]
    qq = np.arange(128)[None, :]
    cst[CST_MLO] = (qq <= kk)
    cst[CST_MHI] = (kk <= qq)

    def perm(dh):
        qt = dh // 4
        half = dh // 2
        Pm = np.zeros((128, 128), np.float32)
        for m in range(128):
            d = m % dh
            if (d % half) < qt:
                Pm[m, m + qt] = -1.0
            else:
                Pm[m, m - qt] = 1.0
        return Pm.T.copy()

    cst[CST_PA] = perm(64)
    cst[CST_PC] = perm(128)
    t1 = np.arange(8)
    tau_p = np.repeat(t1, 16)[:, None]
    tau = np.repeat(t1, 16)[None, :]
    cst[CST_S5F] = (tau >= tau_p)
    cst[CST_S5B] = (tau <= tau_p)
    rope = np.zeros((4, 128, NT), np.float32)
    t = np.arange(NT)
    row = (t // 64).astype(np.float32)
    col = (t % 64).astype(np.float32)
    for idx, dh in ((0, 64), (2, 128)):
        half, qt = dh // 2, dh // 4
        freqs = (1.0 / (np.float32(10000.0) ** (np.arange(qt, dtype=np.float32) / np.float32(qt)))).astype(np.float32)
        for pp in range(128):
            d = pp % dh
            pos = row if d < half else col
            ang = (pos * freqs[d % qt]).astype(np.float32)
            rope[idx, pp] = np.cos(ang)
            rope[idx + 1, pp] = np.sin(ang)
    return cst, rope


_PROG = {}


def kernel(**inp):
    f32 = lambda a: np.ascontiguousarray(np.asarray(a, dtype=np.float32))
    inp = {k: f32(v) for k, v in inp.items()}
    if "nc" not in _PROG:
        _PROG["nc"] = build_program()
    nc = _PROG["nc"]
    cst, rope = _consts()
    shared = {}
    for name, shape in IN_SPECS:
        if name in inp:
            shared[name] = inp[name].reshape(shape)
    shared["cst"] = cst
    shared["rope"] = rope
    in_maps = []
    for c in range(8):
        m = dict(shared)
        m["xin"] = np.stack([inp["x_prompt"][4 * c:4 * c + 4].reshape(NT, D), inp["x_sample"][c]])
        m["cond"] = np.stack([inp["c_ctx"], inp["c"][c]])
        m["ck0"] = inp["cache_l0_k"][c].reshape(512, 256)
        m["cv0"] = inp["cache_l0_v"][c].reshape(512, 256)
        m["st1"] = inp["state_l1"][c]
        m["ck2"] = inp["cache_l2_k"][c].reshape(512, 512)
        m["cv2"] = inp["cache_l2_v"][c].reshape(512, 512)
        m["ck3"] = inp["cache_l3_k"][c].reshape(512, 256)
        m["cv3"] = inp["cache_l3_v"][c].reshape(512, 256)
        in_maps.append({k: np.ascontiguousarray(v) for k, v in m.items()})
    res = run_bass_kernel_spmd(nc, in_maps, core_ids=list(range(8)))
    R = res.results
    y_prompt = np.concatenate([R[c]["y"][0].reshape(4, 256, D) for c in range(8)], 0)
    y_sample = np.stack([R[c]["y"][1] for c in range(8)], 0)
    cat = lambda k, shp: np.concatenate([R[c][k].reshape(shp) for c in range(8)], 0)
    out = (y_prompt, y_sample,
           cat("nk0", (4, 256, 4, 64)), cat("nv0", (4, 256, 4, 64)),
           cat("nst", (4, 2, 2, 64, 64)),
           cat("nk2", (4, 256, 4, 128)), cat("nv2", (4, 256, 4, 128)),
           cat("nk3", (4, 256, 4, 64)), cat("nv3", (4, 256, 4, 64)))
    return tuple(np.ascontiguousarray(o, dtype=np.float32) for o in out)
```

```python
import math
import numpy as np
from contextlib import ExitStack
import concourse.bass as bass
import concourse.mybir as mybir
from concourse.bass_utils import run_bass_kernel_spmd

F32 = mybir.dt.float32
BF16 = mybir.dt.bfloat16
I32 = mybir.dt.int32
U8 = mybir.dt.uint8
AF = mybir.ActivationFunctionType
ALU = mybir.AluOpType
AX = mybir.AxisListType

D = 1024
DFF = 2816
NT = 1024
ENGS = ["pe", "act", "dve", "pool", "sp"]
EPS = 1e-6
NLAYERS = 4
SKIP_S5 = False
DEBUG_H = False
STAGE = 7
ATT_STOP = 9


class Sched:
    def __init__(self, nc, stack):
        self.nc = nc
        self.stack = stack
        self.prog = {e: [] for e in ENGS}
        self.cnt = {e: 0 for e in ENGS}
        self.seen = {e: {} for e in ENGS}
        self.lastw = {}
        self.readers = {}
        self.sem_objs = []
        self.esid = {}
        for e in ENGS:
            self.esid[e] = self._new_sem("s_" + e)
        self.dsem = {}
        self.dfree = []
        self.psem = {}

    def _new_sem(self, name):
        self.sem_objs.append(self.stack.enter_context(self.nc.semaphore(name)))
        return len(self.sem_objs) - 1

    def _dma_sem(self, key, q):
        if q == "pool":
            if key not in self.psem:
                self.psem[key] = [self._new_sem("p%d" % len(self.sem_objs)), 0]
            return self.psem[key]
        if key not in self.dsem:
            if self.dfree:
                self.dsem[key] = self.dfree.pop()
            else:
                self.dsem[key] = [self._new_sem("d%d" % len(self.sem_objs)), 0]
        return self.dsem[key]

    def _deps(self, eng, reads, writes, acc=False):
        need = {}

        def add(rec, same_ok=False):
            sid, val, e = rec
            if same_ok and e == eng:
                return
            if need.get(sid, 0) < val:
                need[sid] = val

        for k in reads:
            if k in self.lastw:
                add(self.lastw[k])
        for k in writes:
            if k in self.lastw:
                add(self.lastw[k], same_ok=(acc and eng == "pe"))
            for r in self.readers.get(k, ()):
                add(r)
        waits = []
        seen = self.seen[eng]
        for sid, val in need.items():
            if seen.get(sid, 0) >= val:
                continue
            seen[sid] = val
            waits.append((sid, val))
        return waits

    def op(self, eng, fn, reads=(), writes=(), acc=False):
        waits = self._deps(eng, reads, writes, acc)
        self.cnt[eng] += 1
        rec = (self.esid[eng], self.cnt[eng], eng)
        self.prog[eng].append((waits, fn, self.esid[eng], 1))
        for k in reads:
            self.readers.setdefault(k, []).append(rec)
        for k in writes:
            self.lastw[k] = rec
            self.readers[k] = []

    def dma(self, q, fn, reads=(), writes=(), key=None):
        if key is None:
            key = writes[0] if writes else reads[0]
        waits = self._deps(q, reads, writes)
        ds = self._dma_sem(key, q)
        ds[1] += 16
        rec = (ds[0], ds[1], "dma")
        self.prog[q].append((waits, fn, ds[0], 16))
        for k in reads:
            self.readers.setdefault(k, []).append(rec)
        for k in writes:
            self.lastw[k] = rec
            self.readers[k] = []

    def barrier(self):
        allw = [(self.esid[e], self.cnt[e], e) for e in ENGS if self.cnt[e] > 0]
        allw += [(sid, c, None) for k, (sid, c) in self.dsem.items() if c > 0]
        allw += [(sid, c, None) for k, (sid, c) in self.psem.items() if c > 0]
        for e in ENGS:
            waits = []
            for sid, val, owner in allw:
                if self.seen[e].get(sid, 0) >= val:
                    continue
                self.seen[e][sid] = val
                waits.append((sid, val))
            if waits:
                self.prog[e].append((waits, None, None, 0))
        self.dfree.extend(self.dsem.values())
        self.dsem = {}
        self.lastw = {}
        self.readers = {}

    def emit(self):
        nc, sems, prog = self.nc, self.sem_objs, self.prog

        def run(ename):
            def body(eng):
                for waits, fn, incsid, incv in prog[ename]:
                    for sid, val in waits:
                        eng.wait_ge(sems[sid], val)
                    if fn is not None:
                        fn(eng).then_inc(sems[incsid], incv)
            return body

        with nc.Block() as block:
            block.tensor(run("pe"))
            block.scalar(run("act"))
            block.vector(run("dve"))
            block.gpsimd(run("pool"))
            block.sync(run("sp"))


class Arena:
    def __init__(self, tensor, size):
        self.t = tensor
        self.size = size
        self.off = 0

    def alloc(self, shape, dt):
        nb = {F32: 4, BF16: 2, I32: 4}[dt]
        n = int(np.prod(shape[1:]))
        off = (self.off + 63) // 64 * 64
        assert off + n * nb <= self.size, ("SBUF arena overflow", off, n * nb, self.size)
        self.off = off + n * nb
        ap = self.t[:, off:off + n * nb].bitcast(dt)
        if len(shape) == 3:
            ap = ap.rearrange("p (a b) -> p a b", a=shape[1])
        elif len(shape) == 4:
            ap = ap.rearrange("p (a b c) -> p a b c", a=shape[1], b=shape[2])
        if shape[0] != 128:
            ap = ap[0:shape[0]]
        return ap

    def mark(self):
        return self.off

    def release(self, m):
        self.off = m


LAYERS = [
    dict(kind="A", dh=64, H=16, ckv=256, wqkv="l0_w_qkv", qn="l0_q_norm", kn="l0_k_norm", sink="l0_sink", wo="l0_w_o",
         ck="ck0", cv="cv0", nk="nk0", nv="nv0"),
    dict(kind="B"),
    dict(kind="C", dh=128, H=8, ckv=512, wqkv="l2_w_qkv", qn="l2_q_norm", kn="l2_k_norm", sink=None, wo="l2_w_o",
         ck="ck2", cv="cv2", nk="nk2", nv="nv2"),
    dict(kind="A", dh=64, H=16, ckv=256, wqkv="l3_w_qkv", qn="l3_q_norm", kn="l3_k_norm", sink="l3_sink", wo="l3_w_o",
         ck="ck3", cv="cv3", nk="nk3", nv="nv3"),
]

IN_SPECS = [
    ("xin", [2, NT, D]), ("cond", [2, D]),
    ("ck0", [512, 256]), ("cv0", [512, 256]), ("st1", [2, 2, 64, 64]),
    ("ck2", [512, 512]), ("cv2", [512, 512]), ("ck3", [512, 256]), ("cv3", [512, 256]),
    ("norm1", [4, D]), ("norm2", [4, D]), ("w_mod", [4, D, 6 * D]), ("b_mod", [4, 6 * D]),
    ("w_up", [4, D, 2 * DFF]), ("b_up", [4, 2 * DFF]), ("conv_k", [4, 3, 2 * DFF]), ("conv_b", [4, 2 * DFF]),
    ("w_down", [4, DFF, D]),
    ("l0_w_qkv", [D, 1536]), ("l0_q_norm", [64]), ("l0_k_norm", [64]), ("l0_sink", [16]), ("l0_w_o", [D, D]),
    ("l1_lam_re", [2, 64, 64]), ("l1_lam_im", [2, 64, 64]), ("l1_log_dt", [2, 64]),
    ("l1_b_re", [2, 64, 64, 16]), ("l1_b_im", [2, 64, 64, 16]), ("l1_c_re", [2, 64, 16, 64]), ("l1_c_im", [2, 64, 16, 64]),
    ("l1_d_skip", [D]), ("l1_w_glu", [D, 2 * D]),
    ("l2_w_qkv", [D, 2048]), ("l2_q_norm", [128]), ("l2_k_norm", [128]), ("l2_w_o", [D, D]),
    ("l3_w_qkv", [D, 1536]), ("l3_q_norm", [64]), ("l3_k_norm", [64]), ("l3_sink", [16]), ("l3_w_o", [D, D]),
    ("cst", [9, 128, 128]), ("rope", [4, 128, NT]),
]
OUT_SPECS = [
    ("y", [2, NT, D]), ("nk0", [NT, 256]), ("nv0", [NT, 256]), ("nst", [4, 2, 2, 64, 64]),
    ("nk2", [NT, 512]), ("nv2", [NT, 512]), ("nk3", [NT, 256]), ("nv3", [NT, 256]),
]
CST_IDENT, CST_BD64, CST_MLO, CST_MHI, CST_PA, CST_PC, CST_S5F, CST_S5B, CST_R16 = range(9)


def build_program():
    nc = bass.Bass("TRN2", target_bir_lowering=False)
    T = {}
    for name, shape in IN_SPECS:
        T[name] = nc.dram_tensor(name, shape, F32, kind="ExternalInput").ap()
    for name, shape in OUT_SPECS:
        T[name] = nc.dram_tensor(name, shape, F32, kind="ExternalOutput").ap()
    for name, shape, dt_ in [("TX", [128, 16384], BF16), ("TZ", [128, 16384], BF16), ("TM", [128, 64, 128], BF16),
                             ("TE", [2, 128, 64, 128], BF16), ("TS", [2, 128, 64, 128], BF16), ("TP8", [2, 128, 64], F32),
                             ("D1", [2, 1024, 1024], BF16), ("D2", [2, 1024, 1024], BF16)]:
        T[name] = nc.dram_tensor(name, shape, dt_, kind="Internal").ap()
    if DEBUG_H:
        T["dbg"] = nc.dram_tensor("dbg", [2, NT, D], F32, kind="ExternalOutput").ap()

    with ExitStack() as st:
        S = Sched(nc, st)
        ARENA_BYTES = 190 * 1024
        arena_t = st.enter_context(nc.sbuf_tensor("arena", [128, ARENA_BYTES], U8))
        AR = Arena(arena_t, ARENA_BYTES)
        ps = st.enter_context(nc.psum_tensor("ps", [128, 8, 512], F32))
        PSK = [("ps", b) for b in range(8)]
        uid = [0]

        def fresh(prefix):
            uid[0] += 1
            return (prefix, uid[0])

        ident_f = AR.alloc([128, 128], F32)
        ident_b = AR.alloc([128, 128], BF16)
        ones_b = AR.alloc([128, 128], BF16)
        ones_f = AR.alloc([128, 128], F32)
        bd64_b = AR.alloc([128, 128], BF16)
        mlo_b = AR.alloc([128, 4, 128], BF16)
        mhi_b = AR.alloc([128, 4, 128], BF16)
        pa_b = AR.alloc([128, 128], BF16)
        pc_b = AR.alloc([128, 128], BF16)
        cstage = AR.alloc([128, 128], F32)
        S.dma("sp", lambda e: e.dma_start(out=ident_f, in_=T["cst"][CST_IDENT]), writes=["ident_f"])
        S.op("dve", lambda e: e.tensor_copy(out=ident_b, in_=ident_f), reads=["ident_f"], writes=["ident_b"])
        S.op("dve", lambda e: e.memset(ones_b, 1.0), writes=["ones_b"])
        S.op("dve", lambda e: e.memset(ones_f, 1.0), writes=["ones_f"])
        for ci, dst, rep in [(CST_BD64, bd64_b, 0), (CST_MLO, mlo_b, 4), (CST_MHI, mhi_b, 4), (CST_PA, pa_b, 0), (CST_PC, pc_b, 0)]:
            S.dma("sp", lambda e, ci=ci: e.dma_start(out=cstage, in_=T["cst"][ci]), writes=["cstage"])
            if rep:
                for r in range(rep):
                    S.op("dve", lambda e, dst=dst, r=r: e.tensor_copy(out=dst[:, r, :], in_=cstage), reads=["cstage"], writes=["cst_misc"])
            else:
                S.op("dve", lambda e, dst=dst: e.tensor_copy(out=dst, in_=cstage), reads=["cstage"], writes=["cst_misc"])
        S.barrier()
        base_mark = AR.mark()

        def mm(out, lhsT, rhs, start, stop, reads, wkey):
            S.op("pe", lambda e: e.matmul(out, lhsT, rhs, start=start, stop=stop), reads=reads, writes=[wkey], acc=True)

        tstage = AR.alloc([128, 128], F32)
        modAll = AR.alloc([128, 4, 2, 48], F32)
        bmodAll = AR.alloc([128, 4, 48], F32)
        cond2 = AR.alloc([128, 2, 8], F32)
        sT2 = AR.alloc([128, 8, 2], BF16)
        tst2 = AR.alloc([128, 128], F32)
        S.op("dve", lambda e: e.memset(tst2, 0.0), writes=["tst2z"])
        S.barrier()
        base_mark2 = [None]

        def small_load_T(dst, src_ap, pattern, key, **kw):
            n = dst.shape[1]
            rows = src_ap.rearrange("(c q) -> c q", q=128)
            S.dma("sp", lambda e: e.dma_start(out=tstage[0:n, :], in_=rows), writes=["tstage"])
            S.op("pe", lambda e: e.transpose(ps[:, 7, 0:n], tstage[0:n, :], ident_f[0:n, 0:n]), reads=["tstage", "ident_f"], writes=[PSK[7]])
            S.op("dve", lambda e: e.tensor_copy(out=dst, in_=ps[:, 7, 0:n]), reads=[PSK[7]], writes=[key])

        def small_load_col(dst, src_ap, dh, key):
            rep = 128 // dh
            for r in range(rep):
                S.dma("sp", lambda e, r=r: e.dma_start(out=tst2[0:1, r * dh:(r + 1) * dh], in_=src_ap.rearrange("(o q) -> o q", o=1)),
                      writes=[("tst2", r)], key=("tst2", r))
            S.op("pe", lambda e: e.transpose(ps[:, 7, 0:2], tst2[0:2, :], ident_f[0:2, 0:2]),
                 reads=[("tst2", r) for r in range(rep)] + ["tst2z", "ident_f"], writes=[PSK[7], ("tst2", 0), ("tst2", 1)])
            S.op("dve", lambda e: e.tensor_copy(out=dst, in_=ps[:, 7, 0:1]), reads=[PSK[7]], writes=[key])

        base_mark = AR.mark()

        def s5_tables():
            m = AR.mark()
            TWO_PI = 2.0 * math.pi
            lr = AR.alloc([128, 64], F32)
            li = AR.alloc([128, 64], F32)
            dtc = AR.alloc([128, 1], F32)
            xx = AR.alloc([128, 64], F32)
            th = AR.alloc([128, 64], F32)
            PWr = AR.alloc([128, 16, 64], F32)
            PWi = AR.alloc([128, 16, 64], F32)
            Br = AR.alloc([128, 64, 16], F32)
            Bi = AR.alloc([128, 64, 16], F32)
            Bbr = AR.alloc([128, 64, 16], F32)
            Bbi = AR.alloc([128, 64, 16], F32)
            Cr = AR.alloc([128, 16, 64], F32)
            Ci = AR.alloc([128, 16, 64], F32)
            w = [AR.alloc([128, 1024], F32) for _ in range(4)]
            s64 = [AR.alloc([128, 64], F32) for _ in range(8)]
            ki = AR.alloc([128, 64], I32)
            PMr = AR.alloc([128, 64], F32)
            PMi = AR.alloc([128, 64], F32)
            ST = [AR.alloc([128, 16384], BF16) for _ in range(2)]

            def dv(fn, reads, writes):
                S.op("dve", fn, reads=reads, writes=writes)

            S.dma("sp", lambda e: e.dma_start(out=lr, in_=T["l1_lam_re"].rearrange("d g p -> (d g) p")), writes=["lr"])
            S.dma("sp", lambda e: e.dma_start(out=li, in_=T["l1_lam_im"].rearrange("d g p -> (d g) p")), writes=["li"])
            S.dma("sp", lambda e: e.dma_start(out=dtc, in_=T["l1_log_dt"].rearrange("d (g o) -> (d g) o", o=1)), writes=["dtc"])
            S.dma("sp", lambda e: e.dma_start(out=Br, in_=T["l1_b_re"].rearrange("d g p c -> (d g) p c")), writes=["Br"])
            S.dma("sp", lambda e: e.dma_start(out=Bi, in_=T["l1_b_im"].rearrange("d g p c -> (d g) p c")), writes=["Bi"])
            S.dma("sp", lambda e: e.dma_start(out=Cr, in_=T["l1_c_re"].rearrange("d g c p -> (d g) c p")), writes=["Cr"])
            S.dma("sp", lambda e: e.dma_start(out=Ci, in_=T["l1_c_im"].rearrange("d g c p -> (d g) c p")), writes=["Ci"])
            S.op("act", lambda e: e.activation(out=dtc, in_=dtc, func=AF.Exp), reads=["dtc"], writes=["dtc"])
            dv(lambda e: e.tensor_scalar(out=xx, in0=lr, scalar1=dtc[:, 0:1], scalar2=None, op0=ALU.mult), ["lr", "dtc"], ["xx"])
            dv(lambda e: e.tensor_scalar(out=th, in0=li, scalar1=dtc[:, 0:1], scalar2=None, op0=ALU.mult), ["li", "dtc"], ["th"])

            def sinred(out, okey, shift):
                v, kf, r, mk = s64[0], s64[1], s64[2], s64[3]
                dv(lambda e: e.tensor_scalar(out=v, in0=th, scalar1=float(shift), scalar2=None, op0=ALU.add), ["th"], [("s64", 0)])
                dv(lambda e: e.tensor_scalar(out=kf, in0=v, scalar1=1.0 / TWO_PI, scalar2=None, op0=ALU.mult), [("s64", 0)], [("s64", 1)])
                dv(lambda e: e.tensor_copy(out=ki, in_=kf), [("s64", 1)], ["sr_ki"])
                dv(lambda e: e.tensor_copy(out=kf, in_=ki), ["sr_ki"], [("s64", 1)])
                dv(lambda e: e.scalar_tensor_tensor(out=r, in0=kf, scalar=-TWO_PI, in1=v, op0=ALU.mult, op1=ALU.add), [("s64", 1), ("s64", 0)], [("s64", 2)])
                dv(lambda e: e.tensor_scalar(out=mk, in0=r, scalar1=math.pi, scalar2=None, op0=ALU.is_gt), [("s64", 2)], [("s64", 3)])
                dv(lambda e: e.scalar_tensor_tensor(out=r, in0=mk, scalar=-TWO_PI, in1=r, op0=ALU.mult, op1=ALU.add), [("s64", 3), ("s64", 2)], [("s64", 2)])
                dv(lambda e: e.tensor_scalar(out=mk, in0=r, scalar1=-math.pi, scalar2=None, op0=ALU.is_lt), [("s64", 2)], [("s64", 3)])
                dv(lambda e: e.scalar_tensor_tensor(out=r, in0=mk, scalar=TWO_PI, in1=r, op0=ALU.mult, op1=ALU.add), [("s64", 3), ("s64", 2)], [("s64", 2)])
                dv(lambda e: e.tensor_scalar(out=r, in0=r, scalar1=3.1415925, scalar2=-3.1415925, op0=ALU.min, op1=ALU.max), [("s64", 2)], [("s64", 2)])
                S.op("act", lambda e: e.activation(out=out, in_=r, func=AF.Sin), reads=[("s64", 2)], writes=[okey])

            sn, cs, ex, exm = s64[4], s64[5], s64[6], s64[7]
            sinred(sn, ("s64", 4), 0.0)
            sinred(cs, ("s64", 5), math.pi / 2.0)
            S.op("act", lambda e: e.activation(out=ex, in_=xx, func=AF.Exp), reads=["xx"], writes=[("s64", 6)])
            S.op("act", lambda e: e.activation(out=exm, in_=xx, func=AF.Exp, scale=-1.0), reads=["xx"], writes=[("s64", 7)])
            PK = lambda j: ("PW", j)
            dv(lambda e: e.memset(PWr[:, 7, :], 1.0), [], [("PWr", 0)])
            dv(lambda e: e.memset(PWi[:, 7, :], 0.0), [], [("PWi", 0)])
            dv(lambda e: e.tensor_tensor(out=PWr[:, 8, :], in0=ex, in1=cs, op=ALU.mult), [("s64", 6), ("s64", 5)], [("PWr", 1)])
            dv(lambda e: e.tensor_tensor(out=PWi[:, 8, :], in0=ex, in1=sn, op=ALU.mult), [("s64", 6), ("s64", 4)], [("PWi", 1)])
            dv(lambda e: e.tensor_tensor(out=PWr[:, 6, :], in0=exm, in1=cs, op=ALU.mult), [("s64", 7), ("s64", 5)], [("PWr", -1)])
            dv(lambda e: e.scalar_tensor_tensor(out=PWi[:, 6, :], in0=exm, scalar=-1.0, in1=sn, op0=ALU.mult, op1=ALU.mult), [("s64", 7), ("s64", 4)], [("PWi", -1)])

            def cmul64(j_out, j_a, j_b):
                orr, oi = PWr[:, j_out + 7, :], PWi[:, j_out + 7, :]
                ar, ai = PWr[:, j_a + 7, :], PWi[:, j_a + 7, :]
                br, bi = PWr[:, j_b + 7, :], PWi[:, j_b + 7, :]
                ka = [("PWr", j_a), ("PWi", j_a), ("PWr", j_b), ("PWi", j_b)]
                t1, t2 = s64[0], s64[1]
                dv(lambda e: e.tensor_tensor(out=t1, in0=ar, in1=br, op=ALU.mult), ka, [("s64", 0)])
                dv(lambda e: e.tensor_tensor(out=t2, in0=ai, in1=bi, op=ALU.mult), ka, [("s64", 1)])
                dv(lambda e: e.tensor_tensor(out=orr, in0=t1, in1=t2, op=ALU.subtract), [("s64", 0), ("s64", 1)], [("PWr", j_out)])
                dv(lambda e: e.tensor_tensor(out=t1, in0=ar, in1=bi, op=ALU.mult), ka, [("s64", 0)])
                dv(lambda e: e.tensor_tensor(out=t2, in0=ai, in1=br, op=ALU.mult), ka, [("s64", 1)])
                dv(lambda e: e.tensor_tensor(out=oi, in0=t1, in1=t2, op=ALU.add), [("s64", 0), ("s64", 1)], [("PWi", j_out)])

            for j in range(2, 9):
                cmul64(j, j - 1, 1)
            for j in range(2, 8):
                cmul64(-j, -(j - 1), -1)
            S.dma("sp", lambda e: e.dma_start(out=T["TP8"][0], in_=PWr[:, 15, :]), reads=[("PWr", 8)], key="tp8a")
            S.dma("sp", lambda e: e.dma_start(out=T["TP8"][1], in_=PWi[:, 15, :]), reads=[("PWi", 8)], key="tp8b")
            nr, den, fr, fi = s64[0], s64[1], s64[2], s64[3]
            t5, t6 = s64[4], s64[5]
            a1 = [("PWr", 1), ("PWi", 1)]
            dv(lambda e: e.tensor_scalar(out=nr, in0=PWr[:, 8, :], scalar1=-1.0, scalar2=None, op0=ALU.add), a1, [("s64", 0)])
            dv(lambda e: e.tensor_tensor(out=den, in0=lr, in1=lr, op=ALU.mult), ["lr"], [("s64", 1)])
            dv(lambda e: e.tensor_tensor(out=t5, in0=li, in1=li, op=ALU.mult), ["li"], [("s64", 4)])
            dv(lambda e: e.tensor_tensor(out=den, in0=den, in1=t5, op=ALU.add), [("s64", 1), ("s64", 4)], [("s64", 1)])
            dv(lambda e: e.reciprocal(out=den, in_=den), [("s64", 1)], [("s64", 1)])
            dv(lambda e: e.tensor_tensor(out=fr, in0=nr, in1=lr, op=ALU.mult), [("s64", 0), "lr"], [("s64", 2)])
            dv(lambda e: e.tensor_tensor(out=t5, in0=PWi[:, 8, :], in1=li, op=ALU.mult), a1 + ["li"], [("s64", 4)])
            dv(lambda e: e.tensor_tensor(out=fr, in0=fr, in1=t5, op=ALU.add), [("s64", 2), ("s64", 4)], [("s64", 2)])
            dv(lambda e: e.tensor_tensor(out=fr, in0=fr, in1=den, op=ALU.mult), [("s64", 2), ("s64", 1)], [("s64", 2)])
            dv(lambda e: e.tensor_tensor(out=fi, in0=PWi[:, 8, :], in1=lr, op=ALU.mult), a1 + ["lr"], [("s64", 3)])
            dv(lambda e: e.tensor_tensor(out=t6, in0=nr, in1=li, op=ALU.mult), [("s64", 0), "li"], [("s64", 5)])
            dv(lambda e: e.tensor_tensor(out=fi, in0=fi, in1=t6, op=ALU.subtract), [("s64", 3), ("s64", 5)], [("s64", 3)])
            dv(lambda e: e.tensor_tensor(out=fi, in0=fi, in1=den, op=ALU.mult), [("s64", 3), ("s64", 1)], [("s64", 3)])

            def bc_b(a):
                return a.unsqueeze(2).broadcast_to([128, 64, 16])

            def bc_c(a):
                return a.unsqueeze(1).broadcast_to([128, 16, 64])

            w3b = [x.rearrange("q (p c) -> q p c", c=16) for x in w]
            w3c = [x.rearrange("q (c p) -> q c p", p=64) for x in w]

            def cmul_big(eng, pr, pi, pk, xr, xi, xk, bc, wv, out_r, out_i, okey, xr_n=None, xi_n=None):
                t1, t2, t3, t4 = wv
                tg = "cb" + eng
                op = lambda fn, r, w_: S.op(eng, fn, reads=r, writes=w_)
                op(lambda e: e.tensor_tensor(out=t1, in0=xr, in1=bc(pr), op=ALU.mult), pk + xk, [tg + "1"])
                op(lambda e: e.tensor_tensor(out=t2, in0=xi, in1=bc(pi), op=ALU.mult), pk + xk, [tg + "2"])
                op(lambda e: e.tensor_tensor(out=out_r, in0=t1, in1=t2, op=ALU.subtract), [tg + "1", tg + "2"], [okey])
                a_i = xi if xi_n is None else xi_n
                a_r = xr if xr_n is None else xr_n
                op(lambda e: e.tensor_tensor(out=t3, in0=a_i, in1=bc(pr), op=ALU.mult), pk + xk, [tg + "3"])
                op(lambda e: e.tensor_tensor(out=t4, in0=a_r, in1=bc(pi), op=ALU.mult), pk + xk, [tg + "4"])
                op(lambda e: e.tensor_tensor(out=out_i, in0=t3, in1=t4, op=ALU.add), [tg + "3", tg + "4"], [okey])

            cmul_big("dve", fr, fi, [("s64", 2), ("s64", 3)], Br, Bi, ["Br", "Bi"], bc_b, w3b, Bbr, Bbi, "Bb")
            nCr = Br.rearrange("q p c -> q (p c)").rearrange("q (c p) -> q c p", p=64)
            nCi = Bi.rearrange("q p c -> q (p c)").rearrange("q (c p) -> q c p", p=64)
            dv(lambda e: e.tensor_scalar(out=nCr, in0=Cr, scalar1=-1.0, scalar2=None, op0=ALU.mult), ["Cr", "Bb"], ["Br"])
            dv(lambda e: e.tensor_scalar(out=nCi, in0=Ci, scalar1=-1.0, scalar2=None, op0=ALU.mult), ["Ci", "Bb"], ["Bi"])
            PM = {"dve": (PMr, PMi), "pool": (AR.alloc([128, 64], F32), AR.alloc([128, 64], F32))}
            wp = [AR.alloc([128, 1024], F32) for _ in range(4)]
            WV = {"dve": (w3b, w3c),
                  "pool": ([x.rearrange("q (p c) -> q p c", c=16) for x in wp], [x.rearrange("q (c p) -> q c p", p=64) for x in wp])}

            def mixed_power(eng, jf, jb):
                pmr, pmi = PM[eng]
                k = "PM" + eng
                S.op(eng, lambda e: e.tensor_copy(out=pmr[0:64], in_=PWr[0:64, jf + 7, :]), reads=[("PWr", jf)], writes=[k])
                S.op(eng, lambda e: e.tensor_copy(out=pmi[0:64], in_=PWi[0:64, jf + 7, :]), reads=[("PWi", jf)], writes=[k])
                S.op(eng, lambda e: e.tensor_copy(out=pmr[64:128], in_=PWr[64:128, jb + 7, :]), reads=[("PWr", jb)], writes=[k])
                S.op(eng, lambda e: e.tensor_copy(out=pmi[64:128], in_=PWi[64:128, jb + 7, :]), reads=[("PWi", jb)], writes=[k])

            specs = [("B", lambda t: -t, lambda t: t, "TX", "ript", "dve"),
                     ("C", lambda t: t, lambda t: -t, "TZ", "ript", "dve"),
                     ("B", lambda t: 7 - t, lambda t: t, "TS", "ritp", "dve"),
                     ("C", lambda t: t + 1, lambda t: 8 - t, "TE", "ript", "dve")]
            for ti, (side, pf, pb_, dst, layout, eng) in enumerate(specs):
                st_ = ST[ti % 2]
                stk = ("ST", ti % 2)
                pmr, pmi = PM[eng]
                pmk = ["PM" + eng]
                if layout == "ript":
                    st5 = st_.rearrange("q (r p t c) -> q r p t c", r=2, p=64, t=8)
                else:
                    st5 = st_.rearrange("q (r t c p) -> q r t c p", r=2, t=8, c=16)
                for t in range(8):
                    mixed_power(eng, pf(t), pb_(t))
                    if side == "B":
                        if layout == "ript":
                            o_r, o_i = st5[:, 0, :, t, :], st5[:, 1, :, t, :]
                        else:
                            o_r = st5[:, 0, t, :, :].rearrange("q c p -> q p c")
                            o_i = st5[:, 1, t, :, :].rearrange("q c p -> q p c")
                        cmul_big(eng, pmr, pmi, pmk, Bbr, Bbi, ["Bb"], bc_b, WV[eng][0], o_r, o_i, stk)
                    else:
                        o_r = st5[:, 0, :, t, :].rearrange("q p c -> q c p")
                        o_i = st5[:, 1, :, t, :].rearrange("q p c -> q c p")
                        cmul_big(eng, pmr, pmi, pmk, Cr, Ci, ["Cr", "Ci", "Br", "Bi"], bc_c, WV[eng][1], o_r, o_i, stk, xr_n=nCr, xi_n=nCi)
                if dst in ("TX", "TZ"):
                    S.dma("sp", lambda e, st_=st_, dst=dst: e.dma_start(out=T[dst], in_=st_), reads=[stk], writes=[dst], key=stk)
                elif dst == "TE":
                    st4 = st_.rearrange("q (r p n) -> q r p n", r=2, p=64)
                    for d in range(2):
                        for r_ in range(2):
                            for p4 in range(4):
                                S.dma("sp", lambda e, d=d, r_=r_, p4=p4, st4=st4: e.dma_start(
                                    out=T["TE"][r_, d * 64 + p4 * 16:d * 64 + (p4 + 1) * 16].rearrange("k g n -> g k n"),
                                    in_=st4[d * 64:(d + 1) * 64, r_, p4 * 16:(p4 + 1) * 16, :]), reads=[stk], writes=["TE"], key=stk)
                else:
                    st4 = st_.rearrange("q (r n p) -> q r n p", r=2, p=64)
                    for d in range(2):
                        for r_ in range(2):
                            for k8 in range(8):
                                S.dma("sp", lambda e, d=d, r_=r_, k8=k8, st4=st4: e.dma_start(
                                    out=T["TS"][r_, k8 * 16:(k8 + 1) * 16, :, d * 64:(d + 1) * 64].rearrange("k g p -> g k p"),
                                    in_=st4[d * 64:(d + 1) * 64, r_, k8 * 16:(k8 + 1) * 16, :]), reads=[stk], writes=["TS"], key=stk)
            S.barrier()
            AR.release(m)
            m = AR.mark()
            mkf = AR.alloc([128, 128], F32)
            mkb = AR.alloc([128, 128], F32)
            r16 = AR.alloc([128, 128], F32)
            dgc = AR.alloc([128, 16], F32)
            dmat = AR.alloc([128, 64], F32)
            dcol = AR.alloc([128, 64], F32)
            S.dma("sp", lambda e: e.dma_start(out=mkf, in_=T["cst"][CST_S5F]), writes=["mkf"])
            S.dma("sp", lambda e: e.dma_start(out=mkb, in_=T["cst"][CST_S5B]), writes=["mkb"])
            S.dma("sp", lambda e: e.dma_start(out=r16, in_=T["cst"][CST_R16]), writes=["r16"])
            S.dma("sp", lambda e: e.dma_start(out=dgc[0:64, :], in_=T["l1_d_skip"].rearrange("(g c) -> g c", c=16)), writes=["dgc"])
            S.op("pe", lambda e: e.transpose(ps[0:16, 7, 0:64], dgc[0:64, :], ident_f[0:64, 0:64]), reads=["dgc", "ident_f"], writes=[PSK[7]])
            S.op("dve", lambda e: e.tensor_copy(out=dmat[0:16, :], in_=ps[0:16, 7, 0:64]), reads=[PSK[7]], writes=["dmat"])
            S.op("pe", lambda e: e.matmul(ps[:, 7, 64:128], r16[0:16, :], dmat[0:16, :], start=True, stop=True), reads=["r16", "dmat"], writes=[PSK[7]])
            S.op("dve", lambda e: e.tensor_copy(out=dcol, in_=ps[:, 7, 64:128]), reads=[PSK[7]], writes=["dcol"])
            XL = [[AR.alloc([128, 16, 128], BF16) for _ in range(2)] for _ in range(2)]
            ZL = [[AR.alloc([128, 16, 128], BF16) for _ in range(2)] for _ in range(2)]
            Mst = [AR.alloc([128, 16, 128], BF16) for _ in range(2)]
            tA = [AR.alloc([128, 128], F32) for _ in range(2)]
            tB = [AR.alloc([128, 128], F32) for _ in range(2)]
            TX4 = T["TX"].rearrange("(d g) (k n) -> d g k n", d=2, k=128)
            TZ4 = T["TZ"].rearrange("(d g) (k n) -> d g k n", d=2, k=128)
            for gb in range(4):
                bf = gb % 2
                for d in range(2):
                    S.dma("sp", lambda e, d=d, gb=gb, bf=bf: e.dma_start(out=XL[bf][d], in_=TX4[d, gb * 16:(gb + 1) * 16].rearrange("g k n -> k g n")),
                          reads=["TX"], writes=[("XL", bf, d)])
                    S.dma("sp", lambda e, d=d, gb=gb, bf=bf: e.dma_start(out=ZL[bf][d], in_=TZ4[d, gb * 16:(gb + 1) * 16].rearrange("g k n -> k g n")),
                          reads=["TZ"], writes=[("ZL", bf, d)])
                for gi in range(16):
                    g = gb * 16 + gi
                    b = gi % 4
                    for d in range(2):
                        S.op("pe", lambda e, b=b, d=d, bf=bf, gi=gi: e.matmul(ps[:, b, d * 128:(d + 1) * 128], XL[bf][d][:, gi, :], ZL[bf][d][:, gi, :],
                                                                            start=True, stop=True),
                             reads=[("XL", bf, d), ("ZL", bf, d)], writes=[PSK[b]], acc=True)
                    a_, b_ = tA[gi % 2], tB[gi % 2]
                    S.op("dve", lambda e, a_=a_, b=b: e.tensor_tensor(out=a_, in0=ps[:, b, 0:128], in1=mkf, op=ALU.mult), reads=[PSK[b], "mkf"], writes=[("tA", gi % 2)])
                    S.op("dve", lambda e, b_=b_, b=b: e.tensor_tensor(out=b_, in0=ps[:, b, 128:256], in1=mkb, op=ALU.mult), reads=[PSK[b], "mkb"], writes=[("tB", gi % 2)])
                    S.op("dve", lambda e, a_=a_, b_=b_: e.tensor_tensor(out=a_, in0=a_, in1=b_, op=ALU.add), reads=[("tA", gi % 2), ("tB", gi % 2)], writes=[("tA", gi % 2)])
                    S.op("dve", lambda e, a_=a_, g=g, gi=gi, bf=bf: e.scalar_tensor_tensor(out=Mst[bf][:, gi, :], in0=ident_f, scalar=dcol[:, g:g + 1], in1=a_,
                                                                                         op0=ALU.mult, op1=ALU.add),
                         reads=[("tA", gi % 2), "dcol", "ident_f"], writes=[("Mst", bf)])
                S.dma("sp", lambda e, gb=gb, bf=bf: e.dma_start(out=T["TM"][:, gb * 16:(gb + 1) * 16, :], in_=Mst[bf]),
                      reads=[("Mst", bf)], writes=["TM"], key=("Mst", bf))
            S.barrier()
            AR.release(m)

        def prologue_mod_start():
            for c_ in range(2):
                small_load_T(cond2[:, c_, :], T["cond"][c_], "(kc q) -> q kc", ("cond2", c_), q=128)
            S.op("act", lambda e: e.activation(out=sT2.rearrange("q k c -> q c k"), in_=cond2, func=AF.Silu),
                 reads=[("cond2", 0), ("cond2", 1)], writes=["sT"])
            wm = [AR.alloc([128, 8, 512], BF16) for _ in range(2)]
            bi_ = 0
            for l in range(NLAYERS):
                for blk in range(12):
                    w = wm[bi_ % 2]
                    wk = ("wm", bi_ % 2)
                    bi_ += 1
                    S.dma("pool", lambda e, w=w, blk=blk, l=l: e.dma_start(
                        out=w, in_=T["w_mod"][l][:, blk * 512:(blk + 1) * 512].rearrange("(kc q) n -> q kc n", q=128)), writes=[wk])
                    for cc in range(4):
                        col = l * 96 + 2 * (blk * 4 + cc)
                        for kc in range(8):
                            mm(ps[:, 6, col:col + 2], w[:, kc, cc * 128:(cc + 1) * 128], sT2[:, kc, :], kc == 0, kc == 7, [wk, "sT"], PSK[6])

        def prologue_mod_finish():
            for l in range(NLAYERS):
                small_load_T(bmodAll[:, l, :], T["b_mod"][l], "(c q) -> q c", ("bmodAll", l), q=128)
                pv = ps[:, 6, l * 96:(l + 1) * 96].rearrange("q (n c) -> q c n", c=2)
                for c_ in range(2):
                    S.op("dve", lambda e, c_=c_, l=l, pv=pv: e.tensor_tensor(out=modAll[:, l, c_, :], in0=pv[:, c_, :], in1=bmodAll[:, l, :], op=ALU.add),
                         reads=[PSK[6], ("bmodAll", l)], writes=[("modAll", l)])
            S.barrier()

        def run_path(p):
            nseq, slen = (4, 256) if p == 0 else (1, 1024)
            AR.release(base_mark)
            hT = AR.alloc([128, 8, NT], F32)
            xnT = AR.alloc([128, 8, NT], BF16)
            modT = AR.alloc([128, 48], F32)
            bmodT = AR.alloc([128, 48], F32)
            n1T = AR.alloc([128, 8], F32)
            n2T = AR.alloc([128, 8], F32)
            A1 = AR.alloc([128, 8], F32)
            A2 = AR.alloc([128, 8], F32)
            condT = AR.alloc([128, 8], F32)
            sT = AR.alloc([128, 8], BF16)
            rstd = AR.alloc([128, NT], F32)
            lnv = AR.alloc([128, 512], F32)
            path_mark = AR.mark()
            HK = [("hT", kc) for kc in range(8)]
            XK = [("xnT", kc) for kc in range(8)]

            m0 = AR.mark()
            xst = [AR.alloc([128, D], F32) for _ in range(8)]
            for tt in range(8):
                xs = xst[tt]
                xk = ("xst", tt)
                S.dma("sp", lambda e, xs=xs, tt=tt: e.dma_start(out=xs, in_=T["xin"][p, tt * 128:(tt + 1) * 128, :]), writes=[xk])
                for hf in range(2):
                    b = (tt * 2 + hf) % 8
                    for q in range(4):
                        kc = hf * 4 + q
                        S.op("pe", lambda e, b=b, q=q, kc=kc, xs=xs: e.transpose(ps[:, b, q * 128:(q + 1) * 128], xs[:, kc * 128:(kc + 1) * 128], ident_f),
                             reads=[xk, "ident_f"], writes=[PSK[b]], acc=True)
                    S.op("act" if hf else "dve",
                         (lambda e, b=b, hf=hf, tt=tt: e.activation(out=hT[:, hf * 4:hf * 4 + 4, tt * 128:(tt + 1) * 128],
                                                                     in_=ps[:, b, :].rearrange("p (a b) -> p a b", a=4), func=AF.Copy)) if hf else
                         (lambda e, b=b, hf=hf, tt=tt: e.tensor_copy(out=hT[:, hf * 4:hf * 4 + 4, tt * 128:(tt + 1) * 128],
                                                                      in_=ps[:, b, :].rearrange("p (a b) -> p a b", a=4))),
                         reads=[PSK[b]], writes=HK[hf * 4:hf * 4 + 4])
            S.barrier()
            AR.release(m0)
            def compute_mod(l):
                m = AR.mark()
                small_load_T(n1T, T["norm1"][l], "(kc q) -> q kc", "n1T", q=128)
                small_load_T(n2T, T["norm2"][l], "(kc q) -> q kc", "n2T", q=128)
                S.op("dve", lambda e: e.tensor_copy(out=modT, in_=modAll[:, l, p, :]), reads=[("modAll", l)], writes=["modT"])
                S.op("dve", lambda e: e.scalar_tensor_tensor(out=A1, in0=modT[:, 8:16], scalar=1.0, in1=n1T, op0=ALU.add, op1=ALU.mult),
                     reads=["modT", "n1T"], writes=["A1"])
                S.op("dve", lambda e: e.scalar_tensor_tensor(out=A2, in0=modT[:, 32:40], scalar=1.0, in1=n2T, op0=ALU.add, op1=ALU.mult),
                     reads=["modT", "n2T"], writes=["A2"])
                AR.release(m)

            def norm_mod(Acol, Akey, Boff, perm8=False, final_barrier=True):
                m = AR.mark()
                sq = AR.alloc([128, 8, NT], BF16)
                tmp = [AR.alloc([128, NT], F32) for _ in range(2)]
                S.op("act", lambda e: e.activation(out=sq, in_=hT, func=AF.Square), reads=HK, writes=["sq"])
                for hf in range(2):
                    b = hf
                    for kc in range(8):
                        mm(ps[:, b, :], ones_b, sq[:, kc, hf * 512:(hf + 1) * 512], kc == 0, kc == 7, ["sq", "ones_b"], PSK[b])
                    S.op("act", lambda e, b=b: e.activation(out=lnv, in_=ps[:, b, :], func=AF.Ln, bias=EPS, scale=1.0 / D),
                         reads=[PSK[b]], writes=["lnv"])
                    S.op("act", lambda e, hf=hf: e.activation(out=rstd[:, hf * 512:(hf + 1) * 512], in_=lnv, func=AF.Exp, scale=-0.5),
                         reads=["lnv"], writes=[("rstd", hf)])
                for kc in range(8):
                    t = tmp[kc % 2]
                    tk = ("nm_tmp", kc % 2)
                    S.op("dve", lambda e, t=t, kc=kc: e.scalar_tensor_tensor(out=t, in0=hT[:, kc, :], scalar=Acol[:, kc:kc + 1], in1=rstd,
                                                                              op0=ALU.mult, op1=ALU.mult),
                         reads=[HK[kc], Akey, ("rstd", 0), ("rstd", 1)], writes=[tk])
                    if perm8:
                        o = xnT[:, kc, :].rearrange("p (t k) -> p k t", t=8)
                    else:
                        o = xnT[:, kc, :]
                    S.op("act", lambda e, t=t, kc=kc, o=o: e.activation(out=o, in_=t, func=AF.Identity, bias=modT[:, Boff + kc:Boff + kc + 1]),
                         reads=[tk, "modT"], writes=[XK[kc]])
                if final_barrier:
                    S.barrier()
                AR.release(m)

            def resid(oc, hf, b, goff):
                S.op("dve", lambda e: e.scalar_tensor_tensor(out=hT[:, oc, hf * 512:(hf + 1) * 512], in0=ps[:, b, :],
                                                              scalar=modT[:, goff + oc:goff + oc + 1],
                                                              in1=hT[:, oc, hf * 512:(hf + 1) * 512], op0=ALU.mult, op1=ALU.add),
                     reads=[PSK[b], "modT", HK[oc]], writes=[HK[oc]])

            def ffn(l):
                m = AR.mark()
                wd = AR.alloc([128, 22, D], BF16)
                wu = [AR.alloc([128, 2, 8, 256], BF16) for _ in range(2)]
                for j4 in range(0, 22, 6):
                    je = min(22, j4 + 6)
                    S.dma("pool", lambda e, j4=j4, je=je: e.dma_start(
                        out=wd[:, j4:je, :], in_=T["w_down"][l][j4 * 128:je * 128, :].rearrange("(j q) n -> q j n", q=128)),
                        writes=[("wd", j4)])

                def wu_dma(jb):
                    w = wu[jb % 2]
                    wk = ("wu", jb % 2)
                    for gv in range(2):
                        c0 = gv * DFF + jb * 256
                        S.dma("pool", lambda e, w=w, gv=gv, c0=c0: e.dma_start(
                            out=w[:, gv], in_=T["w_up"][l][:, c0:c0 + 256].rearrange("(kc q) n -> q kc n", q=128)),
                            writes=[(wk, gv)], key=(wk, gv))

                wu_dma(0)
                wu_dma(1)
                norm_mod(A2, "A2", 24, final_barrier=False)
                gT = AR.alloc([128, 22, NT], BF16)
                bupT = AR.alloc([128, 44], F32)
                cbT = AR.alloc([128, 44], F32)
                ckT = AR.alloc([128, 3, 44], F32)
                tc = [AR.alloc([128, nseq, slen], F32) for _ in range(2)]
                sg = AR.alloc([128, nseq, slen], F32)
                beff = AR.alloc([128, 44], F32)
                e0 = AR.alloc([128, 44], F32)
                e2 = AR.alloc([128, 44], F32)
                small_load_T(bupT, T["b_up"][l], "(c q) -> q c", "bupT", q=128)
                small_load_T(cbT, T["conv_b"][l], "(c q) -> q c", "cbT", q=128)
                for t3 in range(3):
                    small_load_T(ckT[:, t3, :], T["conv_k"][l, t3], "(c q) -> q c", ("ckT", t3), q=128)
                CK = [("ckT", t3) for t3 in range(3)]
                S.op("dve", lambda e: e.tensor_tensor(out=beff, in0=ckT[:, 0, :], in1=ckT[:, 1, :], op=ALU.add), reads=CK, writes=["beff"])
                S.op("dve", lambda e: e.tensor_tensor(out=beff, in0=beff, in1=ckT[:, 2, :], op=ALU.add), reads=CK + ["beff"], writes=["beff"])
                S.op("dve", lambda e: e.tensor_tensor(out=beff, in0=beff, in1=bupT, op=ALU.mult), reads=["beff", "bupT"], writes=["beff"])
                S.op("dve", lambda e: e.tensor_tensor(out=beff, in0=beff, in1=cbT, op=ALU.add), reads=["beff", "cbT"], writes=["beff"])
                S.op("dve", lambda e: e.scalar_tensor_tensor(out=e0, in0=bupT, scalar=-1.0, in1=ckT[:, 0, :], op0=ALU.mult, op1=ALU.mult), reads=["bupT"] + CK, writes=["e0"])
                S.op("dve", lambda e: e.scalar_tensor_tensor(out=e2, in0=bupT, scalar=-1.0, in1=ckT[:, 2, :], op0=ALU.mult, op1=ALU.mult), reads=["bupT"] + CK, writes=["e2"])
                WDK = [("wd", j4) for j4 in range(0, 22, 6)]
                for jb in range(11):
                    w = wu[jb % 2]
                    wk = ("wu", jb % 2)
                    if jb >= 2:
                        wu_dma(jb)
                    for sub in range(2):
                        j = jb * 2 + sub
                        bs = (j % 2) * 4
                        for gv in range(2):
                            for hf in range(2):
                                b = bs + gv * 2 + hf
                                for kc in range(8):
                                    mm(ps[:, b, :], w[:, gv, kc, sub * 128:(sub + 1) * 128], xnT[:, kc, hf * 512:(hf + 1) * 512],
                                       kc == 0, kc == 7, [(wk, gv), XK[kc]], PSK[b])
                        for gv in range(2):
                            b = bs + gv * 2
                            col = gv * 22 + j
                            t_ = tc[gv]
                            tk = ("tc", gv)
                            pin = ps[:, b:b + 2, :].rearrange("p b (s t) -> p (b s) t", t=slen) if nseq == 4 else \
                                ps[:, b:b + 2, :].rearrange("p (o b) t -> p o (b t)", o=1)
                            PB = [PSK[b], PSK[b + 1]]
                            S.op("act", lambda e, t_=t_, pin=pin, col=col: e.activation(out=t_, in_=pin, func=AF.Identity,
                                                                                      bias=beff[:, col:col + 1], scale=ckT[:, 1, col:col + 1]),
                                 reads=PB + ["beff", CK[1]], writes=[tk])
                            S.op("dve", lambda e, t_=t_, pin=pin, col=col: e.scalar_tensor_tensor(out=t_[:, :, 1:slen], in0=pin[:, :, 0:slen - 1],
                                                                                                scalar=ckT[:, 0, col:col + 1], in1=t_[:, :, 1:slen],
                                                                                                op0=ALU.mult, op1=ALU.add),
                                 reads=PB + [CK[0], tk], writes=[tk])
                            S.op("dve", lambda e, t_=t_, pin=pin, col=col: e.scalar_tensor_tensor(out=t_[:, :, 0:slen - 1], in0=pin[:, :, 1:slen],
                                                                                                scalar=ckT[:, 2, col:col + 1], in1=t_[:, :, 0:slen - 1],
                                                                                                op0=ALU.mult, op1=ALU.add),
                                 reads=PB + [CK[2], tk], writes=[tk])
                            S.op("dve", lambda e, t_=t_, col=col: e.tensor_scalar(out=t_[:, :, 0], in0=t_[:, :, 0], scalar1=e0[:, col:col + 1], scalar2=None, op0=ALU.add),
                                 reads=[tk, "e0"], writes=[tk])
                            S.op("dve", lambda e, t_=t_, col=col: e.tensor_scalar(out=t_[:, :, slen - 1], in0=t_[:, :, slen - 1], scalar1=e2[:, col:col + 1], scalar2=None,
                                                                                  op0=ALU.add),
                                 reads=[tk, "e2"], writes=[tk])
                        S.op("act", lambda e: e.activation(out=sg, in_=tc[0], func=AF.Silu), reads=[("tc", 0)], writes=["sg"])
                        go = gT[:, j, :].rearrange("p (s t) -> p s t", s=nseq)
                        S.op("dve", lambda e, go=go: e.tensor_tensor(out=go, in0=sg, in1=tc[1], op=ALU.mult),
                             reads=["sg", ("tc", 1)], writes=[("gT", j)])
                bi = 0
                for oc in range(8):
                    for hf in range(2):
                        b = bi % 8
                        bi += 1
                        for j in range(22):
                            mm(ps[:, b, :], wd[:, j, oc * 128:(oc + 1) * 128], gT[:, j, hf * 512:(hf + 1) * 512], j == 0, j == 21,
                               [WDK[j // 6], ("gT", j)], PSK[b])
                        resid(oc, hf, b, 40)
                AR.release(m)

            def attention(l):
                L = LAYERS[l]
                dh, H, ckv = L["dh"], L["H"], L["ckv"]
                G = H // 4
                nkc = ckv // 128
                ctot = D + 2 * ckv
                scale = dh ** -0.5
                rope = (p == 1)
                m = AR.mark()
                wbig = AR.alloc([128, 8, ctot], BF16)
                wo_b = AR.alloc([128, 8, D], BF16)
                S.dma("pool", lambda e: e.dma_start(out=wbig[:, :, 0:768], in_=T[L["wqkv"]][:, 0:768].rearrange("(kc q) n -> q kc n", q=128)),
                      writes=[("wbig", 0)])
                S.dma("pool", lambda e: e.dma_start(out=wbig[:, :, 768:ctot], in_=T[L["wqkv"]][:, 768:ctot].rearrange("(kc q) n -> q kc n", q=128)),
                      writes=[("wbig", 1)])
                S.dma("pool", lambda e: e.dma_start(out=wo_b, in_=T[L["wo"]].rearrange("(kc q) n -> q kc n", q=128)), writes=["wo_b"])
                norm_mod(A1, "A1", 0, final_barrier=False)
                qT = AR.alloc([128, 8, NT], BF16)
                kT = AR.alloc([128, nkc, NT], BF16)
                vx = AR.alloc([128, 8, 4, dh + 2], BF16)
                qn_c = AR.alloc([128, 1], F32)
                kn_c = AR.alloc([128, 1], F32)
                esink = AR.alloc([128, 16], F32) if L["sink"] else None
                if p == 1:
                    kcT = AR.alloc([128, nkc, 512], BF16)
                    vcx = AR.alloc([128, 4, 4, dh + 2], BF16)
                tmp_mark = AR.mark()
                sqb = [AR.alloc([128, 512], BF16) for _ in range(2)]
                rs = [AR.alloc([128, 512], F32) for _ in range(2)]
                qf = [AR.alloc([128, 512], F32) for _ in range(2)]
                WB = [("wbig", 0), ("wbig", 1)]
                small_load_col(qn_c, T[L["qn"]], dh, "qn_c")
                small_load_col(kn_c, T[L["kn"]], dh, "kn_c")
                QNK = ["qn_c"]
                KNK = ["kn_c"]
                if rope:
                    cosT = AR.alloc([128, NT], F32)
                    sinT = AR.alloc([128, NT], F32)
                    ri = 0 if dh == 64 else 2
                    S.dma("sp", lambda e: e.dma_start(out=cosT, in_=T["rope"][ri]), writes=["cosT"])
                    S.dma("sp", lambda e: e.dma_start(out=sinT, in_=T["rope"][ri + 1]), writes=["sinT"])
                    pm = pa_b if dh == 64 else pc_b
                    r1 = [AR.alloc([128, 512], F32) for _ in range(2)]
                    qb16 = [AR.alloc([128, 512], BF16) for _ in range(2)]
                if L["sink"]:
                    S.dma("sp", lambda e: e.dma_start(out=tst2[0:1, 0:16], in_=T[L["sink"]].rearrange("(o q) -> o q", o=1)),
                          writes=[("tst2", 0)], key=("tst2", 0))
                    S.op("pe", lambda e: e.matmul(ps[:, 7, 0:16], ones_f[0:2, :], tst2[0:2, 0:16], start=True, stop=True),
                         reads=[("tst2", 0), "tst2z", "ones_f"], writes=[PSK[7], ("tst2", 0)])
                    S.op("act", lambda e: e.activation(out=esink, in_=ps[:, 7, 0:16], func=AF.Exp), reads=[PSK[7]], writes=["esink"])
                if p == 0:
                    kst = [AR.alloc([128, ckv], F32) for _ in range(1)]
                    vst = [AR.alloc([128, ckv], F32) for _ in range(1)]
                    kf32 = AR.alloc([128, nkc, NT], F32)
                S.op("pool", lambda e: e.memset(vx, 1.0), reads=XK, writes=["vx_init"])

                if ATT_STOP <= 0:
                    AR.release(m)
                    return
                for tt in range(8):
                    b = 4 + tt % 2
                    for kc in range(8):
                        mm(ps[:, b, 0:ckv], xnT[:, kc, tt * 128:(tt + 1) * 128], wbig[:, kc, D + ckv:D + 2 * ckv], kc == 0, kc == 7,
                           [XK[kc], WB[1]], PSK[b])
                    if p == 0:
                        v_ = vst[0]
                        vk = ("vst", 0)
                        S.op("dve", lambda e, b=b, v_=v_: e.tensor_copy(out=v_, in_=ps[:, b, 0:ckv]), reads=[PSK[b]], writes=[vk])
                        S.op("act", lambda e, v_=v_, tt=tt: e.activation(out=vx[:, tt, :, 0:dh], in_=v_.rearrange("p (g d) -> p g d", g=4),
                                                                        func=AF.Copy),
                             reads=[vk, "vx_init"], writes=[("vx", tt)])
                        S.dma("sp", lambda e, v_=v_, tt=tt: e.dma_start(out=T[L["nv"]][tt * 128:(tt + 1) * 128, :], in_=v_), reads=[vk])
                    else:
                        S.op("act", lambda e, b=b, tt=tt: e.activation(out=vx[:, tt, :, 0:dh], in_=ps[:, b, 0:ckv].rearrange("p (g d) -> p g d", g=4),
                                                                       func=AF.Copy),
                             reads=[PSK[b], "vx_init"], writes=[("vx", tt)])
                if ATT_STOP <= 1:
                    AR.release(m)
                    return
                def lhs_cols(kind, c):
                    if kind == "q":
                        if dh == 128:
                            return [(c * 128, 128, 0)]
                        mq, i = c // 4, c % 4
                        return [((8 * mq + i) * 64, 64, 0), ((8 * mq + 4 + i) * 64, 64, 64)]
                    return [(D + c * 128, 128, 0)]

                blocks = [(kind, c, hf) for kind, nch in (("k", nkc), ("q", 8)) for c in range(nch) for hf in range(2)]
                nb_ = len(blocks)
                sq3 = sqb + [AR.alloc([128, 512], BF16)]

                def stage_a(i):
                    kind, c, hf = blocks[i]
                    b = i % 3
                    sl = slice(hf * 512, (hf + 1) * 512)
                    for (c0, w_, po) in lhs_cols(kind, c):
                        for kc in range(8):
                            mm(ps[po:po + w_, b, :], wbig[:, kc, c0:c0 + w_], xnT[:, kc, sl], kc == 0, kc == 7,
                               [XK[kc], WB[0], WB[1]], PSK[b])
                    sq_ = sq3[i % 3]
                    S.op("act", lambda e, sq_=sq_, b=b: e.activation(out=sq_, in_=ps[:, b, :], func=AF.Square), reads=[PSK[b]], writes=[("sqb", i % 3)])

                def stage_b(i):
                    kind, c, hf = blocks[i]
                    b = i % 3
                    b2 = 3 + i % 2
                    sl = slice(hf * 512, (hf + 1) * 512)
                    sq_ = sq3[i % 3]
                    mm(ps[:, b2, :], bd64_b if dh == 64 else ones_b, sq_, True, True, [("sqb", i % 3), "cst_misc", "ones_b"], PSK[b2])
                    rs_ = rs[i % 2]
                    rk = ("rs", i % 2)
                    S.op("act", lambda e, b2=b2: e.activation(out=lnv, in_=ps[:, b2, :], func=AF.Ln, bias=EPS, scale=1.0 / dh),
                         reads=[PSK[b2]], writes=["lnv"])
                    S.op("act", lambda e, rs_=rs_: e.activation(out=rs_, in_=lnv, func=AF.Exp, scale=-0.5), reads=["lnv"], writes=[rk])
                    gcol, gk = (qn_c, QNK) if kind == "q" else (kn_c, KNK)
                    dstT = qT if kind == "q" else kT
                    dk = (kind + "T", c, hf)
                    if not rope:
                        if kind == "k":
                            S.op("dve", lambda e, c=c, sl=sl, b=b, rs_=rs_: e.scalar_tensor_tensor(
                                out=kf32[:, c, sl], in0=ps[:, b, :], scalar=kn_c[:, 0:1], in1=rs_, op0=ALU.mult, op1=ALU.mult),
                                reads=[PSK[b], rk] + gk, writes=[("kf32", c, hf)])
                            S.op("pool", lambda e, c=c, sl=sl: e.tensor_copy(out=kT[:, c, sl], in_=kf32[:, c, sl]),
                                 reads=[("kf32", c, hf)], writes=[dk])
                        else:
                            S.op("dve", lambda e, c=c, sl=sl, b=b, rs_=rs_, gcol=gcol, dstT=dstT: e.scalar_tensor_tensor(
                                out=dstT[:, c, sl], in0=ps[:, b, :], scalar=gcol[:, 0:1], in1=rs_, op0=ALU.mult, op1=ALU.mult),
                                reads=[PSK[b], rk] + gk, writes=[dk])
                    else:
                        q_, qk_ = qf[i % 2], ("qf", i % 2)
                        qb_, qbk = qb16[i % 2], ("qb16", i % 2)
                        S.op("dve", lambda e, q_=q_, b=b, rs_=rs_, gcol=gcol: e.scalar_tensor_tensor(
                            out=q_, in0=ps[:, b, :], scalar=gcol[:, 0:1], in1=rs_, op0=ALU.mult, op1=ALU.mult),
                            reads=[PSK[b], rk] + gk, writes=[qk_])
                        S.op("pool", lambda e, q_=q_, qb_=qb_: e.tensor_copy(out=qb_, in_=q_), reads=[qk_], writes=[qbk])

                def stage_c(i):
                    kind, c, hf = blocks[i]
                    b3 = 5 + i % 3
                    sl = slice(hf * 512, (hf + 1) * 512)
                    dstT = qT if kind == "q" else kT
                    dk = (kind + "T", c, hf)
                    q_, qk_ = qf[i % 2], ("qf", i % 2)
                    qb_, qbk = qb16[i % 2], ("qb16", i % 2)
                    r_, r1k = r1[i % 2], ("r1", i % 2)
                    mm(ps[:, b3, :], pm, qb_, True, True, [qbk, "cst_misc"], PSK[b3])
                    S.op("dve", lambda e, r_=r_, b3=b3, sl=sl: e.tensor_tensor(out=r_, in0=ps[:, b3, :], in1=sinT[:, sl], op=ALU.mult),
                         reads=[PSK[b3], "sinT"], writes=[r1k])
                    S.op("pool", lambda e, q_=q_, sl=sl: e.tensor_tensor(out=q_, in0=q_, in1=cosT[:, sl], op=ALU.mult),
                         reads=[qk_, "cosT"], writes=[qk_])
                    S.op("dve", lambda e, q_=q_, r_=r_, c=c, sl=sl, dstT=dstT: e.tensor_tensor(out=dstT[:, c, sl], in0=q_, in1=r_, op=ALU.add),
                         reads=[qk_, r1k], writes=[dk])

                for step in range(nb_ + 2):
                    if step < nb_:
                        stage_a(step)
                    if 0 <= step - 1 < nb_:
                        stage_b(step - 1)
                    if rope and 0 <= step - 2 < nb_:
                        stage_c(step - 2)
                if ATT_STOP <= 2:
                    AR.release(m)
                    return
                if p == 0:
                    for tt in range(8):
                        b = 6 + tt % 2
                        for c in range(nkc):
                            S.op("pe", lambda e, b=b, c=c, tt=tt: e.transpose(ps[:, b, c * 128:(c + 1) * 128], kf32[:, c, tt * 128:(tt + 1) * 128], ident_f),
                                 reads=[("kf32", c, tt // 4), "ident_f"], writes=[PSK[b]], acc=True)
                        k_ = kst[0]
                        kk = ("kst", 0)
                        S.op("dve", lambda e, b=b, k_=k_: e.tensor_copy(out=k_, in_=ps[:, b, 0:ckv]), reads=[PSK[b]], writes=[kk])
                        S.dma("sp", lambda e, k_=k_, tt=tt: e.dma_start(out=T[L["nk"]][tt * 128:(tt + 1) * 128, :], in_=k_), reads=[kk])

                if p == 1:
                    cst_ = [AR.alloc([128, ckv], F32) for _ in range(1)]
                    cb_ = [AR.alloc([128, ckv], BF16) for _ in range(1)]
                    S.op("pool", lambda e: e.memset(vcx, 1.0), writes=["vcx_init"])
                    for t4 in range(4):
                        c_ = cst_[0]
                        ck_ = ("cst_", 0)
                        S.dma("sp", lambda e, c_=c_, t4=t4: e.dma_start(out=c_, in_=T[L["cv"]][t4 * 128:(t4 + 1) * 128, :]), writes=[ck_])
                        S.op("dve", lambda e, c_=c_, t4=t4: e.tensor_copy(out=vcx[:, t4, :, 0:dh], in_=c_.rearrange("p (g d) -> p g d", g=4)),
                             reads=[ck_, "vcx_init"], writes=[("vcx", t4)])
                    for t4 in range(4):
                        c_ = cst_[0]
                        ck_ = ("cst_", 0)
                        bb = cb_[0]
                        bk = ("cb_", 0)
                        S.dma("sp", lambda e, c_=c_, t4=t4: e.dma_start(out=c_, in_=T[L["ck"]][t4 * 128:(t4 + 1) * 128, :]), writes=[ck_])
                        S.op("dve", lambda e, c_=c_, bb=bb: e.tensor_copy(out=bb, in_=c_), reads=[ck_], writes=[bk])
                        b = 6 + t4 % 2
                        pb = ps[:, b, 0:64 * nkc].bitcast(BF16).rearrange("p (c t) -> p c t", c=nkc)
                        for c in range(nkc):
                            S.op("pe", lambda e, pb=pb, c=c, bb=bb: e.transpose(pb[:, c, :], bb[:, c * 128:(c + 1) * 128], ident_b),
                                 reads=[bk, "ident_b"], writes=[PSK[b]], acc=True)
                        S.op("dve", lambda e, pb=pb, t4=t4: e.tensor_copy(out=kcT[:, :, t4 * 128:(t4 + 1) * 128], in_=pb),
                             reads=[PSK[b]], writes=[("kcT", t4)])

                if ATT_STOP <= 3:
                    AR.release(m)
                    return
                nkt_max = 7 if L["kind"] == "A" else 12
                S.barrier()
                AR.release(tmp_mark)
                PT = [AR.alloc([128, nkt_max, G * 128], BF16) for _ in range(2)]
                otok = [AR.alloc([128, D], BF16) for _ in range(2)]
                OT = xnT
                den = AR.alloc([128, 4], F32)
                rden = AR.alloc([128, 4], F32)
                sbc = [0]

                def key_tiles(qt):
                    s_ = qt // (slen // 128)
                    kts = []
                    if p == 1:
                        kts += [("c", t4, None) for t4 in range(4)]
                        if L["kind"] == "A":
                            if qt > 0:
                                kts.append(("n", qt - 1, mlo_b))
                            kts.append(("n", qt, None))
                            if qt < 7:
                                kts.append(("n", qt + 1, mhi_b))
                        else:
                            kts += [("n", t8, None) for t8 in range(8)]
                    else:
                        kts += [("n", s_ * 2 + t2, None) for t2 in range(2)]
                    return kts

                def emit_scores(qt, g, it):
                    kts = key_tiles(qt)
                    pt = PT[it % 2]
                    ptk = ("PT", it % 2)
                    if dh == 64:
                        mq, e2_ = g // 2, g % 2
                        prt = slice(64 * e2_, 64 * e2_ + 64)
                        rhs = qT[prt, 4 * mq:4 * mq + 4, qt * 128:(qt + 1) * 128]
                        qreads = [("qT", 4 * mq + i, qt // 4) for i in range(4)]
                        kch = mq
                    else:
                        prt = slice(0, 128)
                        rhs = qT[:, 2 * g:2 * g + 2, qt * 128:(qt + 1) * 128]
                        qreads = [("qT", 2 * g + i, qt // 4) for i in range(2)]
                        kch = g
                    for ki, (src, ti, msk) in enumerate(kts):
                        b = sbc[0] % 4
                        sbc[0] += 1
                        if src == "c":
                            lhsT = kcT[prt, kch, ti * 128:(ti + 1) * 128]
                            kr = [("kcT", ti)]
                        else:
                            lhsT = kT[prt, kch, ti * 128:(ti + 1) * 128]
                            kr = [("kT", kch, ti // 4)]
                        mm(ps[:, b, 0:G * 128], lhsT, rhs, True, True, kr + qreads, PSK[b])
                        S.op("act", lambda e, pt=pt, ki=ki, b=b: e.activation(out=pt[:, ki, :], in_=ps[:, b, 0:G * 128], func=AF.Exp, scale=scale),
                             reads=[PSK[b]], writes=[(ptk, ki)])
                        if msk is not None:
                            S.op("pool", lambda e, pt=pt, ki=ki, msk=msk: e.tensor_tensor(
                                out=pt[:, ki, :], in0=pt[:, ki, :], in1=msk[:, 0:G, :].rearrange("p g q -> p (g q)"), op=ALU.mult),
                                reads=[(ptk, ki), "cst_misc"], writes=[(ptk, ki)])

                def emit_pv(qt, g, it):
                    kts = key_tiles(qt)
                    pt = PT[it % 2]
                    ptk = ("PT", it % 2)
                    ob = 4 + it % 2
                    ot = otok[qt % 2]
                    otk = ("otok", qt % 2)
                    opv = ps[:, ob, 0:G * (dh + 1)].rearrange("p (i d) -> p i d", i=G)
                    for i in range(G):
                        for ki, (src, ti, msk) in enumerate(kts):
                            if src == "c":
                                vr = vcx[:, ti, g, 0:dh + 1]
                                vk = [("vcx", ti)]
                            else:
                                vr = vx[:, ti, g, 0:dh + 1]
                                vk = [("vx", ti)]
                            mm(opv[:, i, :], pt[:, ki, i * 128:(i + 1) * 128], vr, ki == 0, ki == len(kts) - 1,
                               [(ptk, ki)] + vk, PSK[ob])
                    if esink is not None:
                        S.op("dve", lambda e, opv=opv, g=g: e.tensor_tensor(out=den[:, 0:G], in0=opv[:, :, dh], in1=esink[:, g * G:(g + 1) * G], op=ALU.add),
                             reads=[PSK[ob], "esink"], writes=["den"])
                        S.op("dve", lambda e: e.reciprocal(out=rden[:, 0:G], in_=den[:, 0:G]), reads=["den"], writes=["rden"])
                    else:
                        S.op("dve", lambda e, opv=opv: e.reciprocal(out=rden[:, 0:G], in_=opv[:, :, dh]), reads=[PSK[ob]], writes=["rden"])
                    for i in range(G):
                        h = g * G + i
                        S.op("dve", lambda e, opv=opv, i=i, h=h, ot=ot: e.tensor_scalar(out=ot[:, h * dh:(h + 1) * dh], in0=opv[:, i, 0:dh],
                                                                                         scalar1=rden[:, i:i + 1], scalar2=None, op0=ALU.mult),
                             reads=[PSK[ob], "rden"], writes=[(otk, h)])
                    if g == 3:
                        b = 6 + qt % 2
                        pb = ps[:, b, :].bitcast(BF16).rearrange("p (c t) -> p c t", c=8)
                        for c in range(8):
                            S.op("pe", lambda e, pb=pb, c=c, ot=ot: e.transpose(pb[:, c, :], ot[:, c * 128:(c + 1) * 128], ident_b),
                                 reads=[(otk, h) for h in range(H)] + ["ident_b"], writes=[PSK[b]], acc=True)
                        S.op("act", lambda e, pb=pb, qt=qt: e.activation(out=OT[:, :, qt * 128:(qt + 1) * 128], in_=pb, func=AF.Copy),
                             reads=[PSK[b]], writes=[("OT", qt)] + XK)

                items = [(qt, g) for qt in range(8) for g in range(4)]
                emit_scores(items[0][0], items[0][1], 0)
                for it, (qt, g) in enumerate(items):
                    if it + 1 < len(items):
                        emit_scores(items[it + 1][0], items[it + 1][1], it + 1)
                    emit_pv(qt, g, it)
                bi = 0
                for oc in range(8):
                    for hf in range(2):
                        b = bi % 4
                        bi += 1
                        for kc in range(8):
                            mm(ps[:, b, :], wo_b[:, kc, oc * 128:(oc + 1) * 128], OT[:, kc, hf * 512:(hf + 1) * 512], kc == 0, kc == 7,
                               ["wo_b"] + [("OT", hf * 4 + q) for q in range(4)], PSK[b])
                        resid(oc, hf, b, 16)
                AR.release(m)

            def s5_layer(l):
                CH = 128 // nseq
                m = AR.mark()
                norm_mod(A1, "A1", 0, perm8=True)
                U = AR.alloc([128, 64, 128], BF16)
                TB = AR.alloc([128, 2, 64, 128], BF16)
                SS = AR.alloc([128, 2, 64, 128], F32)
                ALL = AR.alloc([128, 2, 64], F32)
                ALS = AR.alloc([128, 2, 64], F32)
                xf = xnT.rearrange("q a n -> q (a n)").bitcast(F32)
                H0v = xf[:, 0:128 * nseq].rearrange("q (r g s) -> q r g s", r=2, g=64)
                L8 = AR.alloc([128, 2, 128], F32)
                FS = AR.alloc([128, 2, 64], F32)
                FSo = AR.alloc([128, 2, 128], F32)
                tt_ = [[xf[:, 128 * nseq * (1 + 2 * a_ + b_):128 * nseq * (2 + 2 * a_ + b_)].rearrange("q (r g s) -> q r g s", r=2, g=64)
                        for b_ in range(2)] for a_ in range(2)]
                S.dma("sp", lambda e: e.dma_start(out=T["D1"][p].rearrange("(fc q) n -> q fc n", q=128), in_=xnT), reads=XK, writes=["D1"])
                d1v = T["D1"][p].rearrange("(g c) (t k) -> t c g k", c=16, t=8)
                for t in range(8):
                    S.dma("sp", lambda e, t=t: e.dma_start(out=U[16 * t:16 * (t + 1)], in_=d1v[t]), reads=["D1"], writes=[("U", t)])
                UK = [("U", t) for t in range(8)]
                for r_ in range(2):
                    for g4 in range(4):
                        S.dma("sp", lambda e, r_=r_, g4=g4: e.dma_start(out=TB[:, r_, g4 * 16:(g4 + 1) * 16, :], in_=T["TS"][r_, :, g4 * 16:(g4 + 1) * 16, :]),
                              writes=[("TB", r_, g4)])
                S.dma("sp", lambda e: e.dma_start(out=L8[0:64], in_=T["TP8"].rearrange("r (d g) p -> g r d p", d=2)), writes=["L8"])
                for r_ in range(2):
                    S.op("pe", lambda e, r_=r_: e.transpose(ps[:, 7, 0:64], L8[0:64, r_, :], ident_f[0:64, 0:64]), reads=["L8", "ident_f"], writes=[PSK[7]])
                    if r_ == 0:
                        S.op("dve", lambda e: e.tensor_copy(out=ALL[:, 0, :], in_=ps[:, 7, 0:64]), reads=[PSK[7]], writes=["AL"])
                        S.op("dve", lambda e: e.tensor_copy(out=ALL[:, 1, :], in_=ps[:, 7, 0:64]), reads=[PSK[7]], writes=["AL"])
                    else:
                        S.op("dve", lambda e: e.tensor_scalar(out=ALS[:, 0, :], in0=ps[:, 7, 0:64], scalar1=-1.0, scalar2=None, op0=ALU.mult), reads=[PSK[7]], writes=["AL"])
                        S.op("dve", lambda e: e.tensor_copy(out=ALS[:, 1, :], in_=ps[:, 7, 0:64]), reads=[PSK[7]], writes=["AL"])
                if p == 1:
                    for r_ in range(2):
                        S.dma("sp", lambda e, r_=r_: e.dma_start(out=L8[0:64, r_, :].rearrange("g (d p) -> g d p", d=2),
                                                                  in_=T["st1"][:, r_].rearrange("d g p -> g d p")), reads=["L8"], writes=["L8"], key=("L8", r_))
                    for r_ in range(2):
                        S.op("pe", lambda e, r_=r_: e.transpose(ps[:, 7, 0:64], L8[0:64, r_, :], ident_f[0:64, 0:64]), reads=["L8", "ident_f"], writes=[PSK[7]])
                        S.op("dve", lambda e, r_=r_: e.tensor_copy(out=H0v[:, r_, :, 0], in_=ps[:, 7, 0:64]), reads=[PSK[7]], writes=["H0"] + XK)
                else:
                    S.op("dve", lambda e: e.memset(H0v, 0.0), writes=["H0"] + XK)
                for gq in range(16):
                    for r_ in range(2):
                        b = (gq * 2 + r_) % 4
                        for gi in range(4):
                            g = gq * 4 + gi
                            S.op("pe", lambda e, b=b, gi=gi, g=g, r_=r_: e.matmul(ps[:, b, gi * 128:(gi + 1) * 128], TB[:, r_, g, :], U[:, g, :], start=True, stop=True),
                                 reads=UK + [("TB", r_, g // 16)], writes=[PSK[b]], acc=True)
                        eng = "act" if r_ == 0 else "dve"
                        if r_ == 0:
                            S.op("act", lambda e, b=b, gq=gq: e.activation(out=SS[:, 0, gq * 4:(gq + 1) * 4, :].rearrange("q g k -> q (g k)"), in_=ps[:, b, :], func=AF.Copy),
                                 reads=[PSK[b]], writes=[("SS", 0)])
                        else:
                            S.op("dve", lambda e, b=b, gq=gq: e.tensor_copy(out=SS[:, 1, gq * 4:(gq + 1) * 4, :].rearrange("q g k -> q (g k)"), in_=ps[:, b, :]),
                                 reads=[PSK[b]], writes=[("SS", 1)])
                S.barrier()
                NS, CHV = nseq, CH
                SSv = SS.rearrange("q r g (s k) -> q r g s k", k=CHV)
                for d, eng in ((0, "dve"), (1, "pool")):
                    hs = slice(64 * d, 64 * d + 64)
                    order = list(range(CHV)) if d == 0 else list(range(CHV - 1, -1, -1))
                    t1, t2 = tt_[d]
                    k1, k2, kS = ("rt1", d), ("rt2", d), ("SSd", d)
                    all_ = ALL[hs].unsqueeze(3).broadcast_to([64, 2, 64, NS])
                    als_ = ALS[hs].unsqueeze(3).broadcast_to([64, 2, 64, NS])
                    for ii, kk in enumerate(order):
                        if ii == 0:
                            prev, prev_sw = H0v[hs], H0v[hs, ::-1]
                        else:
                            kp = order[ii - 1]
                            prev, prev_sw = SSv[hs, :, :, :, kp], SSv[hs, ::-1, :, :, kp]
                        cur = SSv[hs, :, :, :, kk]
                        S.op(eng, lambda e, hs=hs, t1=t1, prev=prev, all_=all_: e.tensor_tensor(out=t1[hs], in0=prev, in1=all_, op=ALU.mult),
                             reads=[kS, "AL", "H0"], writes=[k1])
                        S.op(eng, lambda e, hs=hs, t2=t2, prev_sw=prev_sw, als_=als_: e.tensor_tensor(out=t2[hs], in0=prev_sw, in1=als_, op=ALU.mult),
                             reads=[kS, "AL", "H0"], writes=[k2])
                        S.op(eng, lambda e, hs=hs, t1=t1, t2=t2: e.tensor_tensor(out=t1[hs], in0=t1[hs], in1=t2[hs], op=ALU.add), reads=[k1, k2], writes=[k1])
                        S.op(eng, lambda e, hs=hs, t1=t1, cur=cur: e.tensor_tensor(out=cur, in0=cur, in1=t1[hs], op=ALU.add), reads=[kS, k1], writes=[kS])
                S.barrier()
                if p == 0:
                    for s_ in range(nseq):
                        for r_ in range(2):
                            S.op("dve", lambda e, r_=r_, s_=s_: e.tensor_copy(out=FS[0:64, r_, :], in_=SS[0:64, r_, :, s_ * CH + CH - 1]), reads=[], writes=["FS"])
                            S.op("dve", lambda e, r_=r_, s_=s_: e.tensor_copy(out=FS[64:128, r_, :], in_=SS[64:128, r_, :, s_ * CH]), reads=[], writes=["FS"])
                            S.op("pe", lambda e, r_=r_: e.transpose(ps[0:64, 7, r_ * 128:(r_ + 1) * 128], FS[:, r_, :], ident_f), reads=["FS", "ident_f"], writes=[PSK[7]], acc=True)
                        S.op("dve", lambda e: e.tensor_copy(out=FSo[0:64].rearrange("g r n -> g (r n)"), in_=ps[0:64, 7, 0:256]), reads=[PSK[7]], writes=["FSo"])
                        for r_ in range(2):
                            S.dma("sp", lambda e, s_=s_, r_=r_: e.dma_start(out=T["nst"][s_][:, r_].rearrange("d g p -> g d p"),
                                                                           in_=FSo[0:64, r_, :].rearrange("g (d p) -> g d p", d=2)), reads=["FSo"], key=("nst", r_))
                for r_ in range(2):
                    for s_ in range(nseq):
                        k0 = s_ * CH
                        S.op("act", lambda e, r_=r_, k0=k0: e.activation(out=TB[0:64, r_, :, k0 + 1:k0 + CH], in_=SS[0:64, r_, :, k0:k0 + CH - 1], func=AF.Copy),
                             reads=[], writes=["Hin"])
                        S.op("act", lambda e, r_=r_, k0=k0, s_=s_: e.activation(out=TB[0:64, r_, :, k0], in_=H0v[0:64, r_, :, s_], func=AF.Copy), reads=["H0"], writes=["Hin"])
                        S.op("dve", lambda e, r_=r_, k0=k0: e.tensor_copy(out=TB[64:128, r_, :, k0:k0 + CH - 1], in_=SS[64:128, r_, :, k0 + 1:k0 + CH]),
                             reads=[], writes=["Hin"])
                        S.op("dve", lambda e, r_=r_, k0=k0, s_=s_: e.tensor_copy(out=TB[64:128, r_, :, k0 + CH - 1], in_=H0v[64:128, r_, :, s_]), reads=["H0"], writes=["Hin"])
                S.barrier()
                SSb = SS.rearrange("q r g k -> q (r g k)").bitcast(BF16)
                EL = SSb[:, 0:16384].rearrange("q (r g n) -> q r g n", r=2, g=64)
                ML = SSb[:, 16384:24576].rearrange("q (g n) -> q g n", g=64)
                YC = SSb[:, 24576:32768].rearrange("q (g n) -> q g n", g=64)
                for r_ in range(2):
                    for g4 in range(4):
                        S.dma("sp", lambda e, r_=r_, g4=g4: e.dma_start(out=EL[:, r_, g4 * 16:(g4 + 1) * 16, :], in_=T["TE"][r_, :, g4 * 16:(g4 + 1) * 16, :]),
                              writes=[("EL", r_, g4)])
                for g4 in range(4):
                    S.dma("sp", lambda e, g4=g4: e.dma_start(out=ML[:, g4 * 16:(g4 + 1) * 16, :], in_=T["TM"][:, g4 * 16:(g4 + 1) * 16, :]),
                          writes=[("ML", g4)])
                gu = [AR.alloc([128, 512], F32) for _ in range(2)]
                gs = [AR.alloc([128, 512], F32) for _ in range(2)]
                for gq in range(16):
                    b = gq % 4
                    for gi in range(4):
                        g = gq * 4 + gi
                        o_ = ps[:, b, gi * 128:(gi + 1) * 128]
                        S.op("pe", lambda e, o_=o_, g=g: e.matmul(o_, ML[:, g, :], U[:, g, :], start=True, stop=False), reads=[("ML", g // 16)] + UK, writes=[PSK[b]], acc=True)
                        S.op("pe", lambda e, o_=o_, g=g: e.matmul(o_, EL[:, 0, g, :], TB[:, 0, g, :], start=False, stop=False), reads=[("EL", 0, g // 16), "Hin"], writes=[PSK[b]], acc=True)
                        S.op("pe", lambda e, o_=o_, g=g: e.matmul(o_, EL[:, 1, g, :], TB[:, 1, g, :], start=False, stop=True), reads=[("EL", 1, g // 16), "Hin"], writes=[PSK[b]], acc=True)
                    u_, s2 = gu[gq % 2], gs[gq % 2]
                    uk, sk = ("gu", gq % 2), ("gs", gq % 2)
                    S.op("act", lambda e, u_=u_, b=b: e.activation(out=u_, in_=ps[:, b, :], func=AF.Square), reads=[PSK[b]], writes=[uk])
                    S.op("dve", lambda e, u_=u_: e.tensor_scalar(out=u_, in0=u_, scalar1=0.044715, scalar2=1.0, op0=ALU.mult, op1=ALU.add), reads=[uk], writes=[uk])
                    S.op("dve", lambda e, u_=u_, b=b: e.tensor_tensor(out=u_, in0=u_, in1=ps[:, b, :], op=ALU.mult), reads=[uk, PSK[b]], writes=[uk])
                    S.op("act", lambda e, u_=u_, s2=s2: e.activation(out=s2, in_=u_, func=AF.Sigmoid, scale=1.5957691216057308), reads=[uk], writes=[sk])
                    S.op("dve", lambda e, s2=s2, b=b, gq=gq: e.tensor_tensor(out=YC[:, gq * 4:(gq + 1) * 4, :].rearrange("q g k -> q (g k)"), in0=s2, in1=ps[:, b, :], op=ALU.mult),
                         reads=[sk, PSK[b]], writes=["YC"])
                d2v = T["D2"][p].rearrange("(g c) (t k) -> t c g k", c=16, t=8)
                for t in range(8):
                    S.dma("sp", lambda e, t=t: e.dma_start(out=d2v[t], in_=YC[16 * t:16 * (t + 1)]), reads=["YC"], writes=["D2"], key=("D2w", t))
                S.dma("sp", lambda e: e.dma_start(out=xnT, in_=T["D2"][p].rearrange("(fc q) n -> q fc n", q=128)), reads=["D2"], writes=XK, key="D2r")
                S.barrier()
                AR.release(m)
                m = AR.mark()
                wg = AR.alloc([128, 8, 2 * D], BF16)
                sg2 = [AR.alloc([128, 512], F32) for _ in range(2)]
                mx = [AR.alloc([128, 512], F32) for _ in range(2)]
                for h2 in range(2):
                    S.dma("pool", lambda e, h2=h2: e.dma_start(out=wg[:, :, h2 * D:(h2 + 1) * D],
                                                               in_=T["l1_w_glu"][:, h2 * D:(h2 + 1) * D].rearrange("(kc q) n -> q kc n", q=128)),
                          writes=[("wg", h2)])
                it = 0
                for oc in range(8):
                    for hf in range(2):
                        bA, bB = (it % 2) * 2, (it % 2) * 2 + 1
                        s_, x_ = sg2[it % 2], mx[it % 2]
                        sk, xk_ = ("sg2", it % 2), ("mx", it % 2)
                        it += 1
                        for kc in range(8):
                            mm(ps[:, bA, :], wg[:, kc, oc * 128:(oc + 1) * 128], xnT[:, kc, hf * 512:(hf + 1) * 512], kc == 0, kc == 7, [("wg", 0), XK[kc]], PSK[bA])
                        for kc in range(8):
                            mm(ps[:, bB, :], wg[:, kc, D + oc * 128:D + (oc + 1) * 128], xnT[:, kc, hf * 512:(hf + 1) * 512], kc == 0, kc == 7, [("wg", 1), XK[kc]], PSK[bB])
                        S.op("act", lambda e, s_=s_, bB=bB: e.activation(out=s_, in_=ps[:, bB, :], func=AF.Sigmoid), reads=[PSK[bB]], writes=[sk])
                        S.op("dve", lambda e, s_=s_, x_=x_, bA=bA: e.tensor_tensor(out=x_, in0=ps[:, bA, :], in1=s_, op=ALU.mult), reads=[PSK[bA], sk], writes=[xk_])
                        hv = hT[:, oc, :].rearrange("q (k t) -> q t k", t=8)[:, 4 * hf:4 * hf + 4, :]
                        S.op("dve", lambda e, x_=x_, hv=hv, oc=oc: e.scalar_tensor_tensor(out=hv, in0=x_.rearrange("q (t k) -> q t k", t=4), scalar=modT[:, 16 + oc:17 + oc],
                                                                                        in1=hv, op0=ALU.mult, op1=ALU.add),
                             reads=[xk_, "modT", HK[oc]], writes=[HK[oc]])
                AR.release(m)

            for l in range(NLAYERS):
                if STAGE & 1:
                    compute_mod(l)
                S.barrier()
                if STAGE & 2:
                    if LAYERS[l]["kind"] == "B":
                        if not SKIP_S5:
                            s5_layer(l)
                    else:
                        attention(l)
                S.barrier()
                if STAGE & 4:
                    ffn(l)
                S.barrier()

            m0 = AR.mark()
            yst = [AR.alloc([128, D], F32) for _ in range(8)]
            for tt in range(8):
                y_ = yst[tt]
                yk = ("yst", tt)
                for hf in range(2):
                    b = (tt * 2 + hf) % 8
                    for q in range(4):
                        kc = hf * 4 + q
                        S.op("pe", lambda e, b=b, q=q, kc=kc, tt=tt: e.transpose(ps[:, b, q * 128:(q + 1) * 128], hT[:, kc, tt * 128:(tt + 1) * 128], ident_f),
                             reads=[HK[kc], "ident_f"], writes=[PSK[b]], acc=True)
                    if hf:
                        S.op("act", lambda e, b=b, y_=y_: e.activation(out=y_[:, 512:1024], in_=ps[:, b, :], func=AF.Copy), reads=[PSK[b]], writes=[(yk, 1)])
                    else:
                        S.op("dve", lambda e, b=b, y_=y_: e.tensor_copy(out=y_[:, 0:512], in_=ps[:, b, :]), reads=[PSK[b]], writes=[(yk, 0)])
                S.dma("sp", lambda e, y_=y_, tt=tt: e.dma_start(out=T["y"][p, tt * 128:(tt + 1) * 128, :], in_=y_), reads=[(yk, 0), (yk, 1)], key=yk)
            S.barrier()
            AR.release(m0)

        pm_mark = AR.mark()
        prologue_mod_start()
        if not SKIP_S5 and NLAYERS > 1 and (STAGE & 2):
            s5_tables()
        prologue_mod_finish()
        AR.release(pm_mark)
        run_path(0)
        run_path(1)
        S.barrier()
        print('n_sems', len(S.sem_objs), 'insts', {e: len(S.prog[e]) for e in ENGS})
        S.emit()
    return nc


def _consts():
    cst = np.zeros((9, 128, 128), np.float32)
    for tt_ in range(8):
        for cc_ in range(16):
            cst[CST_R16, cc_, tt_ * 16 + cc_] = 1.0
    cst[CST_IDENT] = np.eye(128)
    bd = np.zeros((128, 128), np.float32)
    bd[:64, :64] = 1
    bd[64:, 64:] = 1
    cst[CST_BD64] = bd
    kk = np.arange(128)[:, None]
    qq = np.arange(128)[None, :]
    cst[CST_MLO] = (qq <= kk)
    cst[CST_MHI] = (kk <= qq)

    def perm(dh):
        qt = dh // 4
        half = dh // 2
        Pm = np.zeros((128, 128), np.float32)
        for m in range(128):
            d = m % dh
            if (d % half) < qt:
                Pm[m, m + qt] = -1.0
            else:
                Pm[m, m - qt] = 1.0
        return Pm.T.copy()

    cst[CST_PA] = perm(64)
    cst[CST_PC] = perm(128)
    t1 = np.arange(8)
    tau_p = np.repeat(t1, 16)[:, None]
    tau = np.repeat(t1, 16)[None, :]
    cst[CST_S5F] = (tau >= tau_p)
    cst[CST_S5B] = (tau <= tau_p)
    rope = np.zeros((4, 128, NT), np.float32)
    t = np.arange(NT)
    row = (t // 64).astype(np.float32)
    col = (t % 64).astype(np.float32)
    for idx, dh in ((0, 64), (2, 128)):
        half, qt = dh // 2, dh // 4
        freqs = (1.0 / (np.float32(10000.0) ** (np.arange(qt, dtype=np.float32) / np.float32(qt)))).astype(np.float32)
        for pp in range(128):
            d = pp % dh
            pos = row if d < half else col
            ang = (pos * freqs[d % qt]).astype(np.float32)
            rope[idx, pp] = np.cos(ang)
            rope[idx + 1, pp] = np.sin(ang)
    return cst, rope


_PROG = {}


def kernel(**inp):
    f32 = lambda a: np.ascontiguousarray(np.asarray(a, dtype=np.float32))
    inp = {k: f32(v) for k, v in inp.items()}
    if "nc" not in _PROG:
        _PROG["nc"] = build_program()
    nc = _PROG["nc"]
    cst, rope = _consts()
    shared = {}
    for name, shape in IN_SPECS:
        if name in inp:
            shared[name] = inp[name].reshape(shape)
    shared["cst"] = cst
    shared["rope"] = rope
    in_maps = []
    for c in range(8):
        m = dict(shared)
        m["xin"] = np.stack([inp["x_prompt"][4 * c:4 * c + 4].reshape(NT, D), inp["x_sample"][c]])
        m["cond"] = np.stack([inp["c_ctx"], inp["c"][c]])
        m["ck0"] = inp["cache_l0_k"][c].reshape(512, 256)
        m["cv0"] = inp["cache_l0_v"][c].reshape(512, 256)
        m["st1"] = inp["state_l1"][c]
        m["ck2"] = inp["cache_l2_k"][c].reshape(512, 512)
        m["cv2"] = inp["cache_l2_v"][c].reshape(512, 512)
        m["ck3"] = inp["cache_l3_k"][c].reshape(512, 256)
        m["cv3"] = inp["cache_l3_v"][c].reshape(512, 256)
        in_maps.append({k: np.ascontiguousarray(v) for k, v in m.items()})
    res = run_bass_kernel_spmd(nc, in_maps, core_ids=list(range(8)))
    R = res.results
    y_prompt = np.concatenate([R[c]["y"][0].reshape(4, 256, D) for c in range(8)], 0)
    y_sample = np.stack([R[c]["y"][1] for c in range(8)], 0)
    cat = lambda k, shp: np.concatenate([R[c][k].reshape(shp) for c in range(8)], 0)
    out = (y_prompt, y_sample,
           cat("nk0", (4, 256, 4, 64)), cat("nv0", (4, 256, 4, 64)),
           cat("nst", (4, 2, 2, 64, 64)),
           cat("nk2", (4, 256, 4, 128)), cat("nv2", (4, 256, 4, 128)),
           cat("nk3", (4, 256, 4, 64)), cat("nv3", (4, 256, 4, 64)))
    return tuple(np.ascontiguousarray(o, dtype=np.float32) for o in out)
```

```python
import math
import numpy as np
from contextlib import ExitStack
import concourse.bass as bass
import concourse.mybir as mybir
from concourse.bass_utils import run_bass_kernel_spmd

F32 = mybir.dt.float32
BF16 = mybir.dt.bfloat16
I32 = mybir.dt.int32
U8 = mybir.dt.uint8
AF = mybir.ActivationFunctionType
ALU = mybir.AluOpType
AX = mybir.AxisListType

D = 1024
DFF = 2816
NT = 1024
ENGS = ["pe", "act", "dve", "pool", "sp"]
EPS = 1e-6
NLAYERS = 4
SKIP_S5 = False
DEBUG_H = False
STAGE = 7
ATT_STOP = 9


class Sched:
    def __init__(self, nc, stack):
        self.nc = nc
        self.stack = stack
        self.prog = {e: [] for e in ENGS}
        self.cnt = {e: 0 for e in ENGS}
        self.seen = {e: {} for e in ENGS}
        self.lastw = {}
        self.readers = {}
        self.sem_objs = []
        self.esid = {}
        for e in ENGS:
            self.esid[e] = self._new_sem("s_" + e)
        self.dsem = {}
        self.dfree = []
        self.psem = {}

    def _new_sem(self, name):
        self.sem_objs.append(self.stack.enter_context(self.nc.semaphore(name)))
        return len(self.sem_objs) - 1

    def _dma_sem(self, key, q):
        if q == "pool":
            if key not in self.psem:
                self.psem[key] = [self._new_sem("p%d" % len(self.sem_objs)), 0]
            return self.psem[key]
        if key not in self.dsem:
            if self.dfree:
                self.dsem[key] = self.dfree.pop()
            else:
                self.dsem[key] = [self._new_sem("d%d" % len(self.sem_objs)), 0]
        return self.dsem[key]

    def _deps(self, eng, reads, writes, acc=False):
        need = {}

        def add(rec, same_ok=False):
            sid, val, e = rec
            if same_ok and e == eng:
                return
            if need.get(sid, 0) < val:
                need[sid] = val

        for k in reads:
            if k in self.lastw:
                add(self.lastw[k])
        for k in writes:
            if k in self.lastw:
                add(self.lastw[k], same_ok=(acc and eng == "pe"))
            for r in self.readers.get(k, ()):
                add(r)
        waits = []
        seen = self.seen[eng]
        for sid, val in need.items():
            if seen.get(sid, 0) >= val:
                continue
            seen[sid] = val
            waits.append((sid, val))
        return waits

    def op(self, eng, fn, reads=(), writes=(), acc=False):
        waits = self._deps(eng, reads, writes, acc)
        self.cnt[eng] += 1
        rec = (self.esid[eng], self.cnt[eng], eng)
        self.prog[eng].append((waits, fn, self.esid[eng], 1))
        for k in reads:
            self.readers.setdefault(k, []).append(rec)
        for k in writes:
            self.lastw[k] = rec
            self.readers[k] = []

    def dma(self, q, fn, reads=(), writes=(), key=None):
        if key is None:
            key = writes[0] if writes else reads[0]
        waits = self._deps(q, reads, writes)
        ds = self._dma_sem(key, q)
        ds[1] += 16
        rec = (ds[0], ds[1], "dma")
        self.prog[q].append((waits, fn, ds[0], 16))
        for k in reads:
            self.readers.setdefault(k, []).append(rec)
        for k in writes:
            self.lastw[k] = rec
            self.readers[k] = []

    def barrier(self):
        allw = [(self.esid[e], self.cnt[e], e) for e in ENGS if self.cnt[e] > 0]
        allw += [(sid, c, None) for k, (sid, c) in self.dsem.items() if c > 0]
        allw += [(sid, c, None) for k, (sid, c) in self.psem.items() if c > 0]
        for e in ENGS:
            waits = []
            for sid, val, owner in allw:
                if self.seen[e].get(sid, 0) >= val:
                    continue
                self.seen[e][sid] = val
                waits.append((sid, val))
            if waits:
                self.prog[e].append((waits, None, None, 0))
        self.dfree.extend(self.dsem.values())
        self.dsem = {}
        self.lastw = {}
        self.readers = {}

    def emit(self):
        nc, sems, prog = self.nc, self.sem_objs, self.prog

        def run(ename):
            def body(eng):
                for waits, fn, incsid, incv in prog[ename]:
                    for sid, val in waits:
                        eng.wait_ge(sems[sid], val)
                    if fn is not None:
                        fn(eng).then_inc(sems[incsid], incv)
            return body

        with nc.Block() as block:
            block.tensor(run("pe"))
            block.scalar(run("act"))
            block.vector(run("dve"))
            block.gpsimd(run("pool"))
            block.sync(run("sp"))


class Arena:
    def __init__(self, tensor, size):
        self.t = tensor
        self.size = size
        self.off = 0

    def alloc(self, shape, dt):
        nb = {F32: 4, BF16: 2, I32: 4}[dt]
        n = int(np.prod(shape[1:]))
        off = (self.off + 63) // 64 * 64
        assert off + n * nb <= self.size, ("SBUF arena overflow", off, n * nb, self.size)
        self.off = off + n * nb
        ap = self.t[:, off:off + n * nb].bitcast(dt)
        if len(shape) == 3:
            ap = ap.rearrange("p (a b) -> p a b", a=shape[1])
        elif len(shape) == 4:
            ap = ap.rearrange("p (a b c) -> p a b c", a=shape[1], b=shape[2])
        if shape[0] != 128:
            ap = ap[0:shape[0]]
        return ap

    def mark(self):
        return self.off

    def release(self, m):
        self.off = m


LAYERS = [
    dict(kind="A", dh=64, H=16, ckv=256, wqkv="l0_w_qkv", qn="l0_q_norm", kn="l0_k_norm", sink="l0_sink", wo="l0_w_o",
         ck="ck0", cv="cv0", nk="nk0", nv="nv0"),
    dict(kind="B"),
    dict(kind="C", dh=128, H=8, ckv=512, wqkv="l2_w_qkv", qn="l2_q_norm", kn="l2_k_norm", sink=None, wo="l2_w_o",
         ck="ck2", cv="cv2", nk="nk2", nv="nv2"),
    dict(kind="A", dh=64, H=16, ckv=256, wqkv="l3_w_qkv", qn="l3_q_norm", kn="l3_k_norm", sink="l3_sink", wo="l3_w_o",
         ck="ck3", cv="cv3", nk="nk3", nv="nv3"),
]

IN_SPECS = [
    ("xin", [2, NT, D]), ("cond", [2, D]),
    ("ck0", [512, 256]), ("cv0", [512, 256]), ("st1", [2, 2, 64, 64]),
    ("ck2", [512, 512]), ("cv2", [512, 512]), ("ck3", [512, 256]), ("cv3", [512, 256]),
    ("norm1", [4, D]), ("norm2", [4, D]), ("w_mod", [4, D, 6 * D]), ("b_mod", [4, 6 * D]),
    ("w_up", [4, D, 2 * DFF]), ("b_up", [4, 2 * DFF]), ("conv_k", [4, 3, 2 * DFF]), ("conv_b", [4, 2 * DFF]),
    ("w_down", [4, DFF, D]),
    ("l0_w_qkv", [D, 1536]), ("l0_q_norm", [64]), ("l0_k_norm", [64]), ("l0_sink", [16]), ("l0_w_o", [D, D]),
    ("l1_lam_re", [2, 64, 64]), ("l1_lam_im", [2, 64, 64]), ("l1_log_dt", [2, 64]),
    ("l1_b_re", [2, 64, 64, 16]), ("l1_b_im", [2, 64, 64, 16]), ("l1_c_re", [2, 64, 16, 64]), ("l1_c_im", [2, 64, 16, 64]),
    ("l1_d_skip", [D]), ("l1_w_glu", [D, 2 * D]),
    ("l2_w_qkv", [D, 2048]), ("l2_q_norm", [128]), ("l2_k_norm", [128]), ("l2_w_o", [D, D]),
    ("l3_w_qkv", [D, 1536]), ("l3_q_norm", [64]), ("l3_k_norm", [64]), ("l3_sink", [16]), ("l3_w_o", [D, D]),
    ("cst", [9, 128, 128]), ("rope", [4, 128, NT]),
]
OUT_SPECS = [
    ("y", [2, NT, D]), ("nk0", [NT, 256]), ("nv0", [NT, 256]), ("nst", [4, 2, 2, 64, 64]),
    ("nk2", [NT, 512]), ("nv2", [NT, 512]), ("nk3", [NT, 256]), ("nv3", [NT, 256]),
]
CST_IDENT, CST_BD64, CST_MLO, CST_MHI, CST_PA, CST_PC, CST_S5F, CST_S5B, CST_R16 = range(9)


def build_program():
    nc = bass.Bass("TRN2", target_bir_lowering=False)
    T = {}
    for name, shape in IN_SPECS:
        T[name] = nc.dram_tensor(name, shape, F32, kind="ExternalInput").ap()
    for name, shape in OUT_SPECS:
        T[name] = nc.dram_tensor(name, shape, F32, kind="ExternalOutput").ap()
    for name, shape, dt_ in [("TX", [128, 16384], BF16), ("TZ", [128, 16384], BF16), ("TM", [128, 64, 128], BF16),
                             ("TE", [2, 128, 64, 128], BF16), ("TS", [2, 128, 64, 128], BF16), ("TP8", [2, 128, 64], F32),
                             ("D1", [2, 1024, 1024], BF16), ("D2", [2, 1024, 1024], BF16)]:
        T[name] = nc.dram_tensor(name, shape, dt_, kind="Internal").ap()
    if DEBUG_H:
        T["dbg"] = nc.dram_tensor("dbg", [2, NT, D], F32, kind="ExternalOutput").ap()

    with ExitStack() as st:
        S = Sched(nc, st)
        ARENA_BYTES = 190 * 1024
        arena_t = st.enter_context(nc.sbuf_tensor("arena", [128, ARENA_BYTES], U8))
        AR = Arena(arena_t, ARENA_BYTES)
        ps = st.enter_context(nc.psum_tensor("ps", [128, 8, 512], F32))
        PSK = [("ps", b) for b in range(8)]
        uid = [0]

        def fresh(prefix):
            uid[0] += 1
            return (prefix, uid[0])

        ident_f = AR.alloc([128, 128], F32)
        ident_b = AR.alloc([128, 128], BF16)
        ones_b = AR.alloc([128, 128], BF16)
        ones_f = AR.alloc([128, 128], F32)
        bd64_b = AR.alloc([128, 128], BF16)
        mlo_b = AR.alloc([128, 4, 128], BF16)
        mhi_b = AR.alloc([128, 4, 128], BF16)
        pa_b = AR.alloc([128, 128], BF16)
        pc_b = AR.alloc([128, 128], BF16)
        cstage = AR.alloc([128, 128], F32)
        S.dma("sp", lambda e: e.dma_start(out=ident_f, in_=T["cst"][CST_IDENT]), writes=["ident_f"])
        S.op("dve", lambda e: e.tensor_copy(out=ident_b, in_=ident_f), reads=["ident_f"], writes=["ident_b"])
        S.op("dve", lambda e: e.memset(ones_b, 1.0), writes=["ones_b"])
        S.op("dve", lambda e: e.memset(ones_f, 1.0), writes=["ones_f"])
        for ci, dst, rep in [(CST_BD64, bd64_b, 0), (CST_MLO, mlo_b, 4), (CST_MHI, mhi_b, 4), (CST_PA, pa_b, 0), (CST_PC, pc_b, 0)]:
            S.dma("sp", lambda e, ci=ci: e.dma_start(out=cstage, in_=T["cst"][ci]), writes=["cstage"])
            if rep:
                for r in range(rep):
                    S.op("dve", lambda e, dst=dst, r=r: e.tensor_copy(out=dst[:, r, :], in_=cstage), reads=["cstage"], writes=["cst_misc"])
            else:
                S.op("dve", lambda e, dst=dst: e.tensor_copy(out=dst, in_=cstage), reads=["cstage"], writes=["cst_misc"])
        S.barrier()
        base_mark = AR.mark()

        def mm(out, lhsT, rhs, start, stop, reads, wkey):
            S.op("pe", lambda e: e.matmul(out, lhsT, rhs, start=start, stop=stop), reads=reads, writes=[wkey], acc=True)

        tstage = AR.alloc([128, 128], F32)
        modAll = AR.alloc([128, 4, 2, 48], F32)
        bmodAll = AR.alloc([128, 4, 48], F32)
        cond2 = AR.alloc([128, 2, 8], F32)
        sT2 = AR.alloc([128, 8, 2], BF16)
        tst2 = AR.alloc([128, 128], F32)
        S.op("dve", lambda e: e.memset(tst2, 0.0), writes=["tst2z"])
        S.barrier()
        base_mark2 = [None]

        def small_load_T(dst, src_ap, pattern, key, **kw):
            n = dst.shape[1]
            rows = src_ap.rearrange("(c q) -> c q", q=128)
            S.dma("sp", lambda e: e.dma_start(out=tstage[0:n, :], in_=rows), writes=["tstage"])
            S.op("pe", lambda e: e.transpose(ps[:, 7, 0:n], tstage[0:n, :], ident_f[0:n, 0:n]), reads=["tstage", "ident_f"], writes=[PSK[7]])
            S.op("dve", lambda e: e.tensor_copy(out=dst, in_=ps[:, 7, 0:n]), reads=[PSK[7]], writes=[key])

        def small_load_col(dst, src_ap, dh, key):
            rep = 128 // dh
            for r in range(rep):
                S.dma("sp", lambda e, r=r: e.dma_start(out=tst2[0:1, r * dh:(r + 1) * dh], in_=src_ap.rearrange("(o q) -> o q", o=1)),
                      writes=[("tst2", r)], key=("tst2", r))
            S.op("pe", lambda e: e.transpose(ps[:, 7, 0:2], tst2[0:2, :], ident_f[0:2, 0:2]),
                 reads=[("tst2", r) for r in range(rep)] + ["tst2z", "ident_f"], writes=[PSK[7], ("tst2", 0), ("tst2", 1)])
            S.op("dve", lambda e: e.tensor_copy(out=dst, in_=ps[:, 7, 0:1]), reads=[PSK[7]], writes=[key])

        base_mark = AR.mark()

        def s5_tables():
            m = AR.mark()
            TWO_PI = 2.0 * math.pi
            lr = AR.alloc([128, 64], F32)
            li = AR.alloc([128, 64], F32)
            dtc = AR.alloc([128, 1], F32)
            xx = AR.alloc([128, 64], F32)
            th = AR.alloc([128, 64], F32)
            PWr = AR.alloc([128, 16, 64], F32)
            PWi = AR.alloc([128, 16, 64], F32)
            Br = AR.alloc([128, 64, 16], F32)
            Bi = AR.alloc([128, 64, 16], F32)
            Bbr = AR.alloc([128, 64, 16], F32)
            Bbi = AR.alloc([128, 64, 16], F32)
            Cr = AR.alloc([128, 16, 64], F32)
            Ci = AR.alloc([128, 16, 64], F32)
            w = [AR.alloc([128, 1024], F32) for _ in range(4)]
            s64 = [AR.alloc([128, 64], F32) for _ in range(8)]
            ki = AR.alloc([128, 64], I32)
            PMr = AR.alloc([128, 64], F32)
            PMi = AR.alloc([128, 64], F32)
            ST = [AR.alloc([128, 16384], BF16) for _ in range(2)]

            def dv(fn, reads, writes):
                S.op("dve", fn, reads=reads, writes=writes)

            S.dma("sp", lambda e: e.dma_start(out=lr, in_=T["l1_lam_re"].rearrange("d g p -> (d g) p")), writes=["lr"])
            S.dma("sp", lambda e: e.dma_start(out=li, in_=T["l1_lam_im"].rearrange("d g p -> (d g) p")), writes=["li"])
            S.dma("sp", lambda e: e.dma_start(out=dtc, in_=T["l1_log_dt"].rearrange("d (g o) -> (d g) o", o=1)), writes=["dtc"])
            S.dma("sp", lambda e: e.dma_start(out=Br, in_=T["l1_b_re"].rearrange("d g p c -> (d g) p c")), writes=["Br"])
            S.dma("sp", lambda e: e.dma_start(out=Bi, in_=T["l1_b_im"].rearrange("d g p c -> (d g) p c")), writes=["Bi"])
            S.dma("sp", lambda e: e.dma_start(out=Cr, in_=T["l1_c_re"].rearrange("d g c p -> (d g) c p")), writes=["Cr"])
            S.dma("sp", lambda e: e.dma_start(out=Ci, in_=T["l1_c_im"].rearrange("d g c p -> (d g) c p")), writes=["Ci"])
            S.op("act", lambda e: e.activation(out=dtc, in_=dtc, func=AF.Exp), reads=["dtc"], writes=["dtc"])
            dv(lambda e: e.tensor_scalar(out=xx, in0=lr, scalar1=dtc[:, 0:1], scalar2=None, op0=ALU.mult), ["lr", "dtc"], ["xx"])
            dv(lambda e: e.tensor_scalar(out=th, in0=li, scalar1=dtc[:, 0:1], scalar2=None, op0=ALU.mult), ["li", "dtc"], ["th"])

            def sinred(out, okey, shift):
                v, kf, r, mk = s64[0], s64[1], s64[2], s64[3]
                dv(lambda e: e.tensor_scalar(out=v, in0=th, scalar1=float(shift), scalar2=None, op0=ALU.add), ["th"], [("s64", 0)])
                dv(lambda e: e.tensor_scalar(out=kf, in0=v, scalar1=1.0 / TWO_PI, scalar2=None, op0=ALU.mult), [("s64", 0)], [("s64", 1)])
                dv(lambda e: e.tensor_copy(out=ki, in_=kf), [("s64", 1)], ["sr_ki"])
                dv(lambda e: e.tensor_copy(out=kf, in_=ki), ["sr_ki"], [("s64", 1)])
                dv(lambda e: e.scalar_tensor_tensor(out=r, in0=kf, scalar=-TWO_PI, in1=v, op0=ALU.mult, op1=ALU.add), [("s64", 1), ("s64", 0)], [("s64", 2)])
                dv(lambda e: e.tensor_scalar(out=mk, in0=r, scalar1=math.pi, scalar2=None, op0=ALU.is_gt), [("s64", 2)], [("s64", 3)])
                dv(lambda e: e.scalar_tensor_tensor(out=r, in0=mk, scalar=-TWO_PI, in1=r, op0=ALU.mult, op1=ALU.add), [("s64", 3), ("s64", 2)], [("s64", 2)])
                dv(lambda e: e.tensor_scalar(out=mk, in0=r, scalar1=-math.pi, scalar2=None, op0=ALU.is_lt), [("s64", 2)], [("s64", 3)])
                dv(lambda e: e.scalar_tensor_tensor(out=r, in0=mk, scalar=TWO_PI, in1=r, op0=ALU.mult, op1=ALU.add), [("s64", 3), ("s64", 2)], [("s64", 2)])
                dv(lambda e: e.tensor_scalar(out=r, in0=r, scalar1=3.1415925, scalar2=-3.1415925, op0=ALU.min, op1=ALU.max), [("s64", 2)], [("s64", 2)])
                S.op("act", lambda e: e.activation(out=out, in_=r, func=AF.Sin), reads=[("s64", 2)], writes=[okey])

            sn, cs, ex, exm = s64[4], s64[5], s64[6], s64[7]
            sinred(sn, ("s64", 4), 0.0)
            sinred(cs, ("s64", 5), math.pi / 2.0)
            S.op("act", lambda e: e.activation(out=ex, in_=xx, func=AF.Exp), reads=["xx"], writes=[("s64", 6)])
            S.op("act", lambda e: e.activation(out=exm, in_=xx, func=AF.Exp, scale=-1.0), reads=["xx"], writes=[("s64", 7)])
            PK = lambda j: ("PW", j)
            dv(lambda e: e.memset(PWr[:, 7, :], 1.0), [], [("PWr", 0)])
            dv(lambda e: e.memset(PWi[:, 7, :], 0.0), [], [("PWi", 0)])
            dv(lambda e: e.tensor_tensor(out=PWr[:, 8, :], in0=ex, in1=cs, op=ALU.mult), [("s64", 6), ("s64", 5)], [("PWr", 1)])
            dv(lambda e: e.tensor_tensor(out=PWi[:, 8, :], in0=ex, in1=sn, op=ALU.mult), [("s64", 6), ("s64", 4)], [("PWi", 1)])
            dv(lambda e: e.tensor_tensor(out=PWr[:, 6, :], in0=exm, in1=cs, op=ALU.mult), [("s64", 7), ("s64", 5)], [("PWr", -1)])
            dv(lambda e: e.scalar_tensor_tensor(out=PWi[:, 6, :], in0=exm, scalar=-1.0, in1=sn, op0=ALU.mult, op1=ALU.mult), [("s64", 7), ("s64", 4)], [("PWi", -1)])

            def cmul64(j_out, j_a, j_b):
                orr, oi = PWr[:, j_out + 7, :], PWi[:, j_out + 7, :]
                ar, ai = PWr[:, j_a + 7, :], PWi[:, j_a + 7, :]
                br, bi = PWr[:, j_b + 7, :], PWi[:, j_b + 7, :]
                ka = [("PWr", j_a), ("PWi", j_a), ("PWr", j_b), ("PWi", j_b)]
                t1, t2 = s64[0], s64[1]
                dv(lambda e: e.tensor_tensor(out=t1, in0=ar, in1=br, op=ALU.mult), ka, [("s64", 0)])
                dv(lambda e: e.tensor_tensor(out=t2, in0=ai, in1=bi, op=ALU.mult), ka, [("s64", 1)])
                dv(lambda e: e.tensor_tensor(out=orr, in0=t1, in1=t2, op=ALU.subtract), [("s64", 0), ("s64", 1)], [("PWr", j_out)])
                dv(lambda e: e.tensor_tensor(out=t1, in0=ar, in1=bi, op=ALU.mult), ka, [("s64", 0)])
                dv(lambda e: e.tensor_tensor(out=t2, in0=ai, in1=br, op=ALU.mult), ka, [("s64", 1)])
                dv(lambda e: e.tensor_tensor(out=oi, in0=t1, in1=t2, op=ALU.add), [("s64", 0), ("s64", 1)], [("PWi", j_out)])

            for j in range(2, 9):
                cmul64(j, j - 1, 1)
            for j in range(2, 8):
                cmul64(-j, -(j - 1), -1)
            S.dma("sp", lambda e: e.dma_start(out=T["TP8"][0], in_=PWr[:, 15, :]), reads=[("PWr", 8)], key="tp8a")
            S.dma("sp", lambda e: e.dma_start(out=T["TP8"][1], in_=PWi[:, 15, :]), reads=[("PWi", 8)], key="tp8b")
            nr, den, fr, fi = s64[0], s64[1], s64[2], s64[3]
            t5, t6 = s64[4], s64[5]
            a1 = [("PWr", 1), ("PWi", 1)]
            dv(lambda e: e.tensor_scalar(out=nr, in0=PWr[:, 8, :], scalar1=-1.0, scalar2=None, op0=ALU.add), a1, [("s64", 0)])
            dv(lambda e: e.tensor_tensor(out=den, in0=lr, in1=lr, op=ALU.mult), ["lr"], [("s64", 1)])
            dv(lambda e: e.tensor_tensor(out=t5, in0=li, in1=li, op=ALU.mult), ["li"], [("s64", 4)])
            dv(lambda e: e.tensor_tensor(out=den, in0=den, in1=t5, op=ALU.add), [("s64", 1), ("s64", 4)], [("s64", 1)])
            dv(lambda e: e.reciprocal(out=den, in_=den), [("s64", 1)], [("s64", 1)])
            dv(lambda e: e.tensor_tensor(out=fr, in0=nr, in1=lr, op=ALU.mult), [("s64", 0), "lr"], [("s64", 2)])
            dv(lambda e: e.tensor_tensor(out=t5, in0=PWi[:, 8, :], in1=li, op=ALU.mult), a1 + ["li"], [("s64", 4)])
            dv(lambda e: e.tensor_tensor(out=fr, in0=fr, in1=t5, op=ALU.add), [("s64", 2), ("s64", 4)], [("s64", 2)])
            dv(lambda e: e.tensor_tensor(out=fr, in0=fr, in1=den, op=ALU.mult), [("s64", 2), ("s64", 1)], [("s64", 2)])
            dv(lambda e: e.tensor_tensor(out=fi, in0=PWi[:, 8, :], in1=lr, op=ALU.mult), a1 + ["lr"], [("s64", 3)])
            dv(lambda e: e.tensor_tensor(out=t6, in0=nr, in1=li, op=ALU.mult), [("s64", 0), "li"], [("s64", 5)])
            dv(lambda e: e.tensor_tensor(out=fi, in0=fi, in1=t6, op=ALU.subtract), [("s64", 3), ("s64", 5)], [("s64", 3)])
            dv(lambda e: e.tensor_tensor(out=fi, in0=fi, in1=den, op=ALU.mult), [("s64", 3), ("s64", 1)], [("s64", 3)])

            def bc_b(a):
                return a.unsqueeze(2).broadcast_to([128, 64, 16])

            def bc_c(a):
                return a.unsqueeze(1).broadcast_to([128, 16, 64])

            w3b = [x.rearrange("q (p c) -> q p c", c=16) for x in w]
            w3c = [x.rearrange("q (c p) -> q c p", p=64) for x in w]

            def cmul_big(eng, pr, pi, pk, xr, xi, xk, bc, wv, out_r, out_i, okey, xr_n=None, xi_n=None):
                t1, t2, t3, t4 = wv
                tg = "cb" + eng
                op = lambda fn, r, w_: S.op(eng, fn, reads=r, writes=w_)
                op(lambda e: e.tensor_tensor(out=t1, in0=xr, in1=bc(pr), op=ALU.mult), pk + xk, [tg + "1"])
                op(lambda e: e.tensor_tensor(out=t2, in0=xi, in1=bc(pi), op=ALU.mult), pk + xk, [tg + "2"])
                op(lambda e: e.tensor_tensor(out=out_r, in0=t1, in1=t2, op=ALU.subtract), [tg + "1", tg + "2"], [okey])
                a_i = xi if xi_n is None else xi_n
                a_r = xr if xr_n is None else xr_n
                op(lambda e: e.tensor_tensor(out=t3, in0=a_i, in1=bc(pr), op=ALU.mult), pk + xk, [tg + "3"])
                op(lambda e: e.tensor_tensor(out=t4, in0=a_r, in1=bc(pi), op=ALU.mult), pk + xk, [tg + "4"])
                op(lambda e: e.tensor_tensor(out=out_i, in0=t3, in1=t4, op=ALU.add), [tg + "3", tg + "4"], [okey])

            cmul_big("dve", fr, fi, [("s64", 2), ("s64", 3)], Br, Bi, ["Br", "Bi"], bc_b, w3b, Bbr, Bbi, "Bb")
            nCr = Br.rearrange("q p c -> q (p c)").rearrange("q (c p) -> q c p", p=64)
            nCi = Bi.rearrange("q p c -> q (p c)").rearrange("q (c p) -> q c p", p=64)
            dv(lambda e: e.tensor_scalar(out=nCr, in0=Cr, scalar1=-1.0, scalar2=None, op0=ALU.mult), ["Cr", "Bb"], ["Br"])
            dv(lambda e: e.tensor_scalar(out=nCi, in0=Ci, scalar1=-1.0, scalar2=None, op0=ALU.mult), ["Ci", "Bb"], ["Bi"])
            PM = {"dve": (PMr, PMi), "pool": (AR.alloc([128, 64], F32), AR.alloc([128, 64], F32))}
            wp = [AR.alloc([128, 1024], F32) for _ in range(4)]
            WV = {"dve": (w3b, w3c),
                  "pool": ([x.rearrange("q (p c) -> q p c", c=16) for x in wp], [x.rearrange("q (c p) -> q c p", p=64) for x in wp])}

            def mixed_power(eng, jf, jb):
                pmr, pmi = PM[eng]
                k = "PM" + eng
                S.op(eng, lambda e: e.tensor_copy(out=pmr[0:64], in_=PWr[0:64, jf + 7, :]), reads=[("PWr", jf)], writes=[k])
                S.op(eng, lambda e: e.tensor_copy(out=pmi[0:64], in_=PWi[0:64, jf + 7, :]), reads=[("PWi", jf)], writes=[k])
                S.op(eng, lambda e: e.tensor_copy(out=pmr[64:128], in_=PWr[64:128, jb + 7, :]), reads=[("PWr", jb)], writes=[k])
                S.op(eng, lambda e: e.tensor_copy(out=pmi[64:128], in_=PWi[64:128, jb + 7, :]), reads=[("PWi", jb)], writes=[k])

            specs = [("B", lambda t: -t, lambda t: t, "TX", "ript", "dve"),
                     ("C", lambda t: t, lambda t: -t, "TZ", "ript", "dve"),
                     ("B", lambda t: 7 - t, lambda t: t, "TS", "ritp", "dve"),
                     ("C", lambda t: t + 1, lambda t: 8 - t, "TE", "ript", "dve")]
            for ti, (side, pf, pb_, dst, layout, eng) in enumerate(specs):
                st_ = ST[ti % 2]
                stk = ("ST", ti % 2)
                pmr, pmi = PM[eng]
                pmk = ["PM" + eng]
                if layout == "ript":
                    st5 = st_.rearrange("q (r p t c) -> q r p t c", r=2, p=64, t=8)
                else:
                    st5 = st_.rearrange("q (r t c p) -> q r t c p", r=2, t=8, c=16)
                for t in range(8):
                    mixed_power(eng, pf(t), pb_(t))
                    if side == "B":
                        if layout == "ript":
                            o_r, o_i = st5[:, 0, :, t, :], st5[:, 1, :, t, :]
                        else:
                            o_r = st5[:, 0, t, :, :].rearrange("q c p -> q p c")
                            o_i = st5[:, 1, t, :, :].rearrange("q c p -> q p c")
                        cmul_big(eng, pmr, pmi, pmk, Bbr, Bbi, ["Bb"], bc_b, WV[eng][0], o_r, o_i, stk)
                    else:
                        o_r = st5[:, 0, :, t, :].rearrange("q p c -> q c p")
                        o_i = st5[:, 1, :, t, :].rearrange("q p c -> q c p")
                        cmul_big(eng, pmr, pmi, pmk, Cr, Ci, ["Cr", "Ci", "Br", "Bi"], bc_c, WV[eng][1], o_r, o_i, stk, xr_n=nCr, xi_n=nCi)
                if dst in ("TX", "TZ"):
                    S.dma("sp", lambda e, st_=st_, dst=dst: e.dma_start(out=T[dst], in_=st_), reads=[stk], writes=[dst], key=stk)
                elif dst == "TE":
                    st4 = st_.rearrange("q (r p n) -> q r p n", r=2, p=64)
                    for d in range(2):
                        for r_ in range(2):
                            for p4 in range(4):
                                S.dma("sp", lambda e, d=d, r_=r_, p4=p4, st4=st4: e.dma_start(
                                    out=T["TE"][r_, d * 64 + p4 * 16:d * 64 + (p4 + 1) * 16].rearrange("k g n -> g k n"),
                                    in_=st4[d * 64:(d + 1) * 64, r_, p4 * 16:(p4 + 1) * 16, :]), reads=[stk], writes=["TE"], key=stk)
                else:
                    st4 = st_.rearrange("q (r n p) -> q r n p", r=2, p=64)
                    for d in range(2):
                        for r_ in range(2):
                            for k8 in range(8):
                                S.dma("sp", lambda e, d=d, r_=r_, k8=k8, st4=st4: e.dma_start(
                                    out=T["TS"][r_, k8 * 16:(k8 + 1) * 16, :, d * 64:(d + 1) * 64].rearrange("k g p -> g k p"),
                                    in_=st4[d * 64:(d + 1) * 64, r_, k8 * 16:(k8 + 1) * 16, :]), reads=[stk], writes=["TS"], key=stk)
            S.barrier()
            AR.release(m)
            m = AR.mark()
            mkf = AR.alloc([128, 128], F32)
            mkb = AR.alloc([128, 128], F32)
            r16 = AR.alloc([128, 128], F32)
            dgc = AR.alloc([128, 16], F32)
            dmat = AR.alloc([128, 64], F32)
            dcol = AR.alloc([128, 64], F32)
            S.dma("sp", lambda e: e.dma_start(out=mkf, in_=T["cst"][CST_S5F]), writes=["mkf"])
            S.dma("sp", lambda e: e.dma_start(out=mkb, in_=T["cst"][CST_S5B]), writes=["mkb"])
            S.dma("sp", lambda e: e.dma_start(out=r16, in_=T["cst"][CST_R16]), writes=["r16"])
            S.dma("sp", lambda e: e.dma_start(out=dgc[0:64, :], in_=T["l1_d_skip"].rearrange("(g c) -> g c", c=16)), writes=["dgc"])
            S.op("pe", lambda e: e.transpose(ps[0:16, 7, 0:64], dgc[0:64, :], ident_f[0:64, 0:64]), reads=["dgc", "ident_f"], writes=[PSK[7]])
            S.op("dve", lambda e: e.tensor_copy(out=dmat[0:16, :], in_=ps[0:16, 7, 0:64]), reads=[PSK[7]], writes=["dmat"])
            S.op("pe", lambda e: e.matmul(ps[:, 7, 64:128], r16[0:16, :], dmat[0:16, :], start=True, stop=True), reads=["r16", "dmat"], writes=[PSK[7]])
            S.op("dve", lambda e: e.tensor_copy(out=dcol, in_=ps[:, 7, 64:128]), reads=[PSK[7]], writes=["dcol"])
            XL = [[AR.alloc([128, 16, 128], BF16) for _ in range(2)] for _ in range(2)]
            ZL = [[AR.alloc([128, 16, 128], BF16) for _ in range(2)] for _ in range(2)]
            Mst = [AR.alloc([128, 16, 128], BF16) for _ in range(2)]
            tA = [AR.alloc([128, 128], F32) for _ in range(2)]
            tB = [AR.alloc([128, 128], F32) for _ in range(2)]
            TX4 = T["TX"].rearrange("(d g) (k n) -> d g k n", d=2, k=128)
            TZ4 = T["TZ"].rearrange("(d g) (k n) -> d g k n", d=2, k=128)
            for gb in range(4):
                bf = gb % 2
                for d in range(2):
                    S.dma("sp", lambda e, d=d, gb=gb, bf=bf: e.dma_start(out=XL[bf][d], in_=TX4[d, gb * 16:(gb + 1) * 16].rearrange("g k n -> k g n")),
                          reads=["TX"], writes=[("XL", bf, d)])
                    S.dma("sp", lambda e, d=d, gb=gb, bf=bf: e.dma_start(out=ZL[bf][d], in_=TZ4[d, gb * 16:(gb + 1) * 16].rearrange("g k n -> k g n")),
                          reads=["TZ"], writes=[("ZL", bf, d)])
                for gi in range(16):
                    g = gb * 16 + gi
                    b = gi % 4
                    for d in range(2):
                        S.op("pe", lambda e, b=b, d=d, bf=bf, gi=gi: e.matmul(ps[:, b, d * 128:(d + 1) * 128], XL[bf][d][:, gi, :], ZL[bf][d][:, gi, :],
                                                                            start=True, stop=True),
                             reads=[("XL", bf, d), ("ZL", bf, d)], writes=[PSK[b]], acc=True)
                    a_, b_ = tA[gi % 2], tB[gi % 2]
                    S.op("dve", lambda e, a_=a_, b=b: e.tensor_tensor(out=a_, in0=ps[:, b, 0:128], in1=mkf, op=ALU.mult), reads=[PSK[b], "mkf"], writes=[("tA", gi % 2)])
                    S.op("dve", lambda e, b_=b_, b=b: e.tensor_tensor(out=b_, in0=ps[:, b, 128:256], in1=mkb, op=ALU.mult), reads=[PSK[b], "mkb"], writes=[("tB", gi % 2)])
                    S.op("dve", lambda e, a_=a_, b_=b_: e.tensor_tensor(out=a_, in0=a_, in1=b_, op=ALU.add), reads=[("tA", gi % 2), ("tB", gi % 2)], writes=[("tA", gi % 2)])
                    S.op("dve", lambda e, a_=a_, g=g, gi=gi, bf=bf: e.scalar_tensor_tensor(out=Mst[bf][:, gi, :], in0=ident_f, scalar=dcol[:, g:g + 1], in1=a_,
                                                                                         op0=ALU.mult, op1=ALU.add),
                         reads=[("tA", gi % 2), "dcol", "ident_f"], writes=[("Mst", bf)])
                S.dma("sp", lambda e, gb=gb, bf=bf: e.dma_start(out=T["TM"][:, gb * 16:(gb + 1) * 16, :], in_=Mst[bf]),
                      reads=[("Mst", bf)], writes=["TM"], key=("Mst", bf))
            S.barrier()
            AR.release(m)

        def prologue_mod_start():
            for c_ in range(2):
                small_load_T(cond2[:, c_, :], T["cond"][c_], "(kc q) -> q kc", ("cond2", c_), q=128)
            S.op("act", lambda e: e.activation(out=sT2.rearrange("q k c -> q c k"), in_=cond2, func=AF.Silu),
                 reads=[("cond2", 0), ("cond2", 1)], writes=["sT"])
            wm = [AR.alloc([128, 8, 512], BF16) for _ in range(2)]
            bi_ = 0
            for l in range(NLAYERS):
                for blk in range(12):
                    w = wm[bi_ % 2]
                    wk = ("wm", bi_ % 2)
                    bi_ += 1
                    S.dma("pool", lambda e, w=w, blk=blk, l=l: e.dma_start(
                        out=w, in_=T["w_mod"][l][:, blk * 512:(blk + 1) * 512].rearrange("(kc q) n -> q kc n", q=128)), writes=[wk])
                    for cc in range(4):
                        col = l * 96 + 2 * (blk * 4 + cc)
                        for kc in range(8):
                            mm(ps[:, 6, col:col + 2], w[:, kc, cc * 128:(cc + 1) * 128], sT2[:, kc, :], kc == 0, kc == 7, [wk, "sT"], PSK[6])

        def prologue_mod_finish():
            for l in range(NLAYERS):
                small_load_T(bmodAll[:, l, :], T["b_mod"][l], "(c q) -> q c", ("bmodAll", l), q=128)
                pv = ps[:, 6, l * 96:(l + 1) * 96].rearrange("q (n c) -> q c n", c=2)
                for c_ in range(2):
                    S.op("dve", lambda e, c_=c_, l=l, pv=pv: e.tensor_tensor(out=modAll[:, l, c_, :], in0=pv[:, c_, :], in1=bmodAll[:, l, :], op=ALU.add),
                         reads=[PSK[6], ("bmodAll", l)], writes=[("modAll", l)])
            S.barrier()

        def run_path(p):
            nseq, slen = (4, 256) if p == 0 else (1, 1024)
            AR.release(base_mark)
            hT = AR.alloc([128, 8, NT], F32)
            xnT = AR.alloc([128, 8, NT], BF16)
            modT = AR.alloc([128, 48], F32)
            bmodT = AR.alloc([128, 48], F32)
            n1T = AR.alloc([128, 8], F32)
            n2T = AR.alloc([128, 8], F32)
            A1 = AR.alloc([128, 8], F32)
            A2 = AR.alloc([128, 8], F32)
            condT = AR.alloc([128, 8], F32)
            sT = AR.alloc([128, 8], BF16)
            rstd = AR.alloc([128, NT], F32)
            lnv = AR.alloc([128, 512], F32)
            path_mark = AR.mark()
            HK = [("hT", kc) for kc in range(8)]
            XK = [("xnT", kc) for kc in range(8)]

            m0 = AR.mark()
            xst = [AR.alloc([128, D], F32) for _ in range(8)]
            for tt in range(8):
                xs = xst[tt]
                xk = ("xst", tt)
                S.dma("sp", lambda e, xs=xs, tt=tt: e.dma_start(out=xs, in_=T["xin"][p, tt * 128:(tt + 1) * 128, :]), writes=[xk])
                for hf in range(2):
                    b = (tt * 2 + hf) % 8
                    for q in range(4):
                        kc = hf * 4 + q
                        S.op("pe", lambda e, b=b, q=q, kc=kc, xs=xs: e.transpose(ps[:, b, q * 128:(q + 1) * 128], xs[:, kc * 128:(kc + 1) * 128], ident_f),
                             reads=[xk, "ident_f"], writes=[PSK[b]], acc=True)
                    S.op("act" if hf else "dve",
                         (lambda e, b=b, hf=hf, tt=tt: e.activation(out=hT[:, hf * 4:hf * 4 + 4, tt * 128:(tt + 1) * 128],
                                                                     in_=ps[:, b, :].rearrange("p (a b) -> p a b", a=4), func=AF.Copy)) if hf else
                         (lambda e, b=b, hf=hf, tt=tt: e.tensor_copy(out=hT[:, hf * 4:hf * 4 + 4, tt * 128:(tt + 1) * 128],
                                                                      in_=ps[:, b, :].rearrange("p (a b) -> p a b", a=4))),
                         reads=[PSK[b]], writes=HK[hf * 4:hf * 4 + 4])
            S.barrier()
            AR.release(m0)
            def compute_mod(l):
                m = AR.mark()
                small_load_T(n1T, T["norm1"][l], "(kc q) -> q kc", "n1T", q=128)
                small_load_T(n2T, T["norm2"][l], "(kc q) -> q kc", "n2T", q=128)
                S.op("dve", lambda e: e.tensor_copy(out=modT, in_=modAll[:, l, p, :]), reads=[("modAll", l)], writes=["modT"])
                S.op("dve", lambda e: e.scalar_tensor_tensor(out=A1, in0=modT[:, 8:16], scalar=1.0, in1=n1T, op0=ALU.add, op1=ALU.mult),
                     reads=["modT", "n1T"], writes=["A1"])
                S.op("dve", lambda e: e.scalar_tensor_tensor(out=A2, in0=modT[:, 32:40], scalar=1.0, in1=n2T, op0=ALU.add, op1=ALU.mult),
                     reads=["modT", "n2T"], writes=["A2"])
                AR.release(m)

            def norm_mod(Acol, Akey, Boff, perm8=False, final_barrier=True):
                m = AR.mark()
                sq = AR.alloc([128, 8, NT], BF16)
                tmp = [AR.alloc([128, NT], F32) for _ in range(2)]
                S.op("act", lambda e: e.activation(out=sq, in_=hT, func=AF.Square), reads=HK, writes=["sq"])
                for hf in range(2):
                    b = hf
                    for kc in range(8):
                        mm(ps[:, b, :], ones_b, sq[:, kc, hf * 512:(hf + 1) * 512], kc == 0, kc == 7, ["sq", "ones_b"], PSK[b])
                    S.op("act", lambda e, b=b: e.activation(out=lnv, in_=ps[:, b, :], func=AF.Ln, bias=EPS, scale=1.0 / D),
                         reads=[PSK[b]], writes=["lnv"])
                    S.op("act", lambda e, hf=hf: e.activation(out=rstd[:, hf * 512:(hf + 1) * 512], in_=lnv, func=AF.Exp, scale=-0.5),
                         reads=["lnv"], writes=[("rstd", hf)])
                for kc in range(8):
                    t = tmp[kc % 2]
                    tk = ("nm_tmp", kc % 2)
                    S.op("dve", lambda e, t=t, kc=kc: e.scalar_tensor_tensor(out=t, in0=hT[:, kc, :], scalar=Acol[:, kc:kc + 1], in1=rstd,
                                                                              op0=ALU.mult, op1=ALU.mult),
                         reads=[HK[kc], Akey, ("rstd", 0), ("rstd", 1)], writes=[tk])
                    if perm8:
                        o = xnT[:, kc, :].rearrange("p (t k) -> p k t", t=8)
                    else:
                        o = xnT[:, kc, :]
                    S.op("act", lambda e, t=t, kc=kc, o=o: e.activation(out=o, in_=t, func=AF.Identity, bias=modT[:, Boff + kc:Boff + kc + 1]),
                         reads=[tk, "modT"], writes=[XK[kc]])
                if final_barrier:
                    S.barrier()
                AR.release(m)

            def resid(oc, hf, b, goff):
                S.op("dve", lambda e: e.scalar_tensor_tensor(out=hT[:, oc, hf * 512:(hf + 1) * 512], in0=ps[:, b, :],
                                                              scalar=modT[:, goff + oc:goff + oc + 1],
                                                              in1=hT[:, oc, hf * 512:(hf + 1) * 512], op0=ALU.mult, op1=ALU.add),
                     reads=[PSK[b], "modT", HK[oc]], writes=[HK[oc]])

            def ffn(l):
                m = AR.mark()
                wd = AR.alloc([128, 22, D], BF16)
                wu = [AR.alloc([128, 2, 8, 256], BF16) for _ in range(2)]
                for j4 in range(0, 22, 6):
                    je = min(22, j4 + 6)
                    S.dma("pool", lambda e, j4=j4, je=je: e.dma_start(
                        out=wd[:, j4:je, :], in_=T["w_down"][l][j4 * 128:je * 128, :].rearrange("(j q) n -> q j n", q=128)),
                        writes=[("wd", j4)])

                def wu_dma(jb):
                    w = wu[jb % 2]
                    wk = ("wu", jb % 2)
                    for gv in range(2):
                        c0 = gv * DFF + jb * 256
                        S.dma("pool", lambda e, w=w, gv=gv, c0=c0: e.dma_start(
                            out=w[:, gv], in_=T["w_up"][l][:, c0:c0 + 256].rearrange("(kc q) n -> q kc n", q=128)),
                            writes=[(wk, gv)], key=(wk, gv))

                wu_dma(0)
                wu_dma(1)
                norm_mod(A2, "A2", 24, final_barrier=False)
                gT = AR.alloc([128, 22, NT], BF16)
                bupT = AR.alloc([128, 44], F32)
                cbT = AR.alloc([128, 44], F32)
                ckT = AR.alloc([128, 3, 44], F32)
                tc = [AR.alloc([128, nseq, slen], F32) for _ in range(2)]
                sg = AR.alloc([128, nseq, slen], F32)
                beff = AR.alloc([128, 44], F32)
                e0 = AR.alloc([128, 44], F32)
                e2 = AR.alloc([128, 44], F32)
                small_load_T(bupT, T["b_up"][l], "(c q) -> q c", "bupT", q=128)
                small_load_T(cbT, T["conv_b"][l], "(c q) -> q c", "cbT", q=128)
                for t3 in range(3):
                    small_load_T(ckT[:, t3, :], T["conv_k"][l, t3], "(c q) -> q c", ("ckT", t3), q=128)
                CK = [("ckT", t3) for t3 in range(3)]
                S.op("dve", lambda e: e.tensor_tensor(out=beff, in0=ckT[:, 0, :], in1=ckT[:, 1, :], op=ALU.add), reads=CK, writes=["beff"])
                S.op("dve", lambda e: e.tensor_tensor(out=beff, in0=beff, in1=ckT[:, 2, :], op=ALU.add), reads=CK + ["beff"], writes=["beff"])
                S.op("dve", lambda e: e.tensor_tensor(out=beff, in0=beff, in1=bupT, op=ALU.mult), reads=["beff", "bupT"], writes=["beff"])
                S.op("dve", lambda e: e.tensor_tensor(out=beff, in0=beff, in1=cbT, op=ALU.add), reads=["beff", "cbT"], writes=["beff"])
                S.op("dve", lambda e: e.scalar_tensor_tensor(out=e0, in0=bupT, scalar=-1.0, in1=ckT[:, 0, :], op0=ALU.mult, op1=ALU.mult), reads=["bupT"] + CK, writes=["e0"])
                S.op("dve", lambda e: e.scalar_tensor_tensor(out=e2, in0=bupT, scalar=-1.0, in1=ckT[:, 2, :], op0=ALU.mult, op1=ALU.mult), reads=["bupT"] + CK, writes=["e2"])
                WDK = [("wd", j4) for j4 in range(0, 22, 6)]
                for jb in range(11):
                    w = wu[jb % 2]
                    wk = ("wu", jb % 2)
                    if jb >= 2:
                        wu_dma(jb)
                    for sub in range(2):
                        j = jb * 2 + sub
                        bs = (j % 2) * 4
                        for gv in range(2):
                            for hf in range(2):
                                b = bs + gv * 2 + hf
                                for kc in range(8):
                                    mm(ps[:, b, :], w[:, gv, kc, sub * 128:(sub + 1) * 128], xnT[:, kc, hf * 512:(hf + 1) * 512],
                                       kc == 0, kc == 7, [(wk, gv), XK[kc]], PSK[b])
                        for gv in range(2):
                            b = bs + gv * 2
                            col = gv * 22 + j
                            t_ = tc[gv]
                            tk = ("tc", gv)
                            pin = ps[:, b:b + 2, :].rearrange("p b (s t) -> p (b s) t", t=slen) if nseq == 4 else \
                                ps[:, b:b + 2, :].rearrange("p (o b) t -> p o (b t)", o=1)
                            PB = [PSK[b], PSK[b + 1]]
                            S.op("act", lambda e, t_=t_, pin=pin, col=col: e.activation(out=t_, in_=pin, func=AF.Identity,
                                                                                      bias=beff[:, col:col + 1], scale=ckT[:, 1, col:col + 1]),
                                 reads=PB + ["beff", CK[1]], writes=[tk])
                            S.op("dve", lambda e, t_=t_, pin=pin, col=col: e.scalar_tensor_tensor(out=t_[:, :, 1:slen], in0=pin[:, :, 0:slen - 1],
                                                                                                scalar=ckT[:, 0, col:col + 1], in1=t_[:, :, 1:slen],
                                                                                                op0=ALU.mult, op1=ALU.add),
                                 reads=PB + [CK[0], tk], writes=[tk])
                            S.op("dve", lambda e, t_=t_, pin=pin, col=col: e.scalar_tensor_tensor(out=t_[:, :, 0:slen - 1], in0=pin[:, :, 1:slen],
                                                                                                scalar=ckT[:, 2, col:col + 1], in1=t_[:, :, 0:slen - 1],
                                                                                                op0=ALU.mult, op1=ALU.add),
                                 reads=PB + [CK[2], tk], writes=[tk])
                            S.op("dve", lambda e, t_=t_, col=col: e.tensor_scalar(out=t_[:, :, 0], in0=t_[:, :, 0], scalar1=e0[:, col:col + 1], scalar2=None, op0=ALU.add),
                                 reads=[tk, "e0"], writes=[tk])
                            S.op("dve", lambda e, t_=t_, col=col: e.tensor_scalar(out=t_[:, :, slen - 1], in0=t_[:, :, slen - 1], scalar1=e2[:, col:col + 1], scalar2=None,
                                                                                  op0=ALU.add),
                                 reads=[tk, "e2"], writes=[tk])
                        S.op("act", lambda e: e.activation(out=sg, in_=tc[0], func=AF.Silu), reads=[("tc", 0)], writes=["sg"])
                        go = gT[:, j, :].rearrange("p (s t) -> p s t", s=nseq)
                        S.op("dve", lambda e, go=go: e.tensor_tensor(out=go, in0=sg, in1=tc[1], op=ALU.mult),
                             reads=["sg", ("tc", 1)], writes=[("gT", j)])
                bi = 0
                for oc in range(8):
                    for hf in range(2):
                        b = bi % 8
                        bi += 1
                        for j in range(22):
                            mm(ps[:, b, :], wd[:, j, oc * 128:(oc + 1) * 128], gT[:, j, hf * 512:(hf + 1) * 512], j == 0, j == 21,
                               [WDK[j // 6], ("gT", j)], PSK[b])
                        resid(oc, hf, b, 40)
                AR.release(m)

            def attention(l):
                L = LAYERS[l]
                dh, H, ckv = L["dh"], L["H"], L["ckv"]
                G = H // 4
                nkc = ckv // 128
                ctot = D + 2 * ckv
                scale = dh ** -0.5
                rope = (p == 1)
                m = AR.mark()
                wbig = AR.alloc([128, 8, ctot], BF16)
                wo_b = AR.alloc([128, 8, D], BF16)
                S.dma("pool", lambda e: e.dma_start(out=wbig[:, :, 0:768], in_=T[L["wqkv"]][:, 0:768].rearrange("(kc q) n -> q kc n", q=128)),
                      writes=[("wbig", 0)])
                S.dma("pool", lambda e: e.dma_start(out=wbig[:, :, 768:ctot], in_=T[L["wqkv"]][:, 768:ctot].rearrange("(kc q) n -> q kc n", q=128)),
                      writes=[("wbig", 1)])
                S.dma("pool", lambda e: e.dma_start(out=wo_b, in_=T[L["wo"]].rearrange("(kc q) n -> q kc n", q=128)), writes=["wo_b"])
                norm_mod(A1, "A1", 0, final_barrier=False)
                qT = AR.alloc([128, 8, NT], BF16)
                kT = AR.alloc([128, nkc, NT], BF16)
                vx = AR.alloc([128, 8, 4, dh + 2], BF16)
                qn_c = AR.alloc([128, 1], F32)
                kn_c = AR.alloc([128, 1], F32)
                esink = AR.alloc([128, 16], F32) if L["sink"] else None
                if p == 1:
                    kcT = AR.alloc([128, nkc, 512], BF16)
                    vcx = AR.alloc([128, 4, 4, dh + 2], BF16)
                tmp_mark = AR.mark()
                sqb = [AR.alloc([128, 512], BF16) for _ in range(2)]
                rs = [AR.alloc([128, 512], F32) for _ in range(2)]
                qf = [AR.alloc([128, 512], F32) for _ in range(2)]
                WB = [("wbig", 0), ("wbig", 1)]
                small_load_col(qn_c, T[L["qn"]], dh, "qn_c")
                small_load_col(kn_c, T[L["kn"]], dh, "kn_c")
                QNK = ["qn_c"]
                KNK = ["kn_c"]
                if rope:
                    cosT = AR.alloc([128, NT], F32)
                    sinT = AR.alloc([128, NT], F32)
                    ri = 0 if dh == 64 else 2
                    S.dma("sp", lambda e: e.dma_start(out=cosT, in_=T["rope"][ri]), writes=["cosT"])
                    S.dma("sp", lambda e: e.dma_start(out=sinT, in_=T["rope"][ri + 1]), writes=["sinT"])
                    pm = pa_b if dh == 64 else pc_b
                    r1 = [AR.alloc([128, 512], F32) for _ in range(2)]
                    qb16 = [AR.alloc([128, 512], BF16) for _ in range(2)]
                if L["sink"]:
                    S.dma("sp", lambda e: e.dma_start(out=tst2[0:1, 0:16], in_=T[L["sink"]].rearrange("(o q) -> o q", o=1)),
                          writes=[("tst2", 0)], key=("tst2", 0))
                    S.op("pe", lambda e: e.matmul(ps[:, 7, 0:16], ones_f[0:2, :], tst2[0:2, 0:16], start=True, stop=True),
                         reads=[("tst2", 0), "tst2z", "ones_f"], writes=[PSK[7], ("tst2", 0)])
                    S.op("act", lambda e: e.activation(out=esink, in_=ps[:, 7, 0:16], func=AF.Exp), reads=[PSK[7]], writes=["esink"])
                if p == 0:
                    kst = [AR.alloc([128, ckv], F32) for _ in range(1)]
                    vst = [AR.alloc([128, ckv], F32) for _ in range(1)]
                    kf32 = AR.alloc([128, nkc, NT], F32)
                S.op("pool", lambda e: e.memset(vx, 1.0), reads=XK, writes=["vx_init"])

                if ATT_STOP <= 0:
                    AR.release(m)
                    return
                for tt in range(8):
                    b = 4 + tt % 2
                    for kc in range(8):
                        mm(ps[:, b, 0:ckv], xnT[:, kc, tt * 128:(tt + 1) * 128], wbig[:, kc, D + ckv:D + 2 * ckv], kc == 0, kc == 7,
                           [XK[kc], WB[1]], PSK[b])
                    if p == 0:
                        v_ = vst[0]
                        vk = ("vst", 0)
                        S.op("dve", lambda e, b=b, v_=v_: e.tensor_copy(out=v_, in_=ps[:, b, 0:ckv]), reads=[PSK[b]], writes=[vk])
                        S.op("act", lambda e, v_=v_, tt=tt: e.activation(out=vx[:, tt, :, 0:dh], in_=v_.rearrange("p (g d) -> p g d", g=4),
                                                                        func=AF.Copy),
                             reads=[vk, "vx_init"], writes=[("vx", tt)])
                        S.dma("sp", lambda e, v_=v_, tt=tt: e.dma_start(out=T[L["nv"]][tt * 128:(tt + 1) * 128, :], in_=v_), reads=[vk])
                    else:
                        S.op("act", lambda e, b=b, tt=tt: e.activation(out=vx[:, tt, :, 0:dh], in_=ps[:, b, 0:ckv].rearrange("p (g d) -> p g d", g=4),
                                                                       func=AF.Copy),
                             reads=[PSK[b], "vx_init"], writes=[("vx", tt)])
                if ATT_STOP <= 1:
                    AR.release(m)
                    return
                def lhs_cols(kind, c):
                    if kind == "q":
                        if dh == 128:
                            return [(c * 128, 128, 0)]
                        mq, i = c // 4, c % 4
                        return [((8 * mq + i) * 64, 64, 0), ((8 * mq + 4 + i) * 64, 64, 64)]
                    return [(D + c * 128, 128, 0)]

                blocks = [(kind, c, hf) for kind, nch in (("k", nkc), ("q", 8)) for c in range(nch) for hf in range(2)]
                nb_ = len(blocks)
                sq3 = sqb + [AR.alloc([128, 512], BF16)]

                def stage_a(i):
                    kind, c, hf = blocks[i]
                    b = i % 3
                    sl = slice(hf * 512, (hf + 1) * 512)
                    for (c0, w_, po) in lhs_cols(kind, c):
                        for kc in range(8):
                            mm(ps[po:po + w_, b, :], wbig[:, kc, c0:c0 + w_], xnT[:, kc, sl], kc == 0, kc == 7,
                               [XK[kc], WB[0], WB[1]], PSK[b])
                    sq_ = sq3[i % 3]
                    S.op("act", lambda e, sq_=sq_, b=b: e.activation(out=sq_, in_=ps[:, b, :], func=AF.Square), reads=[PSK[b]], writes=[("sqb", i % 3)])

                def stage_b(i):
                    kind, c, hf = blocks[i]
                    b = i % 3
                    b2 = 3 + i % 2
                    sl = slice(hf * 512, (hf + 1) * 512)
                    sq_ = sq3[i % 3]
                    mm(ps[:, b2, :], bd64_b if dh == 64 else ones_b, sq_, True, True, [("sqb", i % 3), "cst_misc", "ones_b"], PSK[b2])
                    rs_ = rs[i % 2]
                    rk = ("rs", i % 2)
                    S.op("act", lambda e, b2=b2: e.activation(out=lnv, in_=ps[:, b2, :], func=AF.Ln, bias=EPS, scale=1.0 / dh),
                         reads=[PSK[b2]], writes=["lnv"])
                    S.op("act", lambda e, rs_=rs_: e.activation(out=rs_, in_=lnv, func=AF.Exp, scale=-0.5), reads=["lnv"], writes=[rk])
                    gcol, gk = (qn_c, QNK) if kind == "q" else (kn_c, KNK)
                    dstT = qT if kind == "q" else kT
                    dk = (kind + "T", c, hf)
                    if not rope:
                        if kind == "k":
                            S.op("dve", lambda e, c=c, sl=sl, b=b, rs_=rs_: e.scalar_tensor_tensor(
                                out=kf32[:, c, sl], in0=ps[:, b, :], scalar=kn_c[:, 0:1], in1=rs_, op0=ALU.mult, op1=ALU.mult),
                                reads=[PSK[b], rk] + gk, writes=[("kf32", c, hf)])
                            S.op("pool", lambda e, c=c, sl=sl: e.tensor_copy(out=kT[:, c, sl], in_=kf32[:, c, sl]),
                                 reads=[("kf32", c, hf)], writes=[dk])
                        else:
                            S.op("dve", lambda e, c=c, sl=sl, b=b, rs_=rs_, gcol=gcol, dstT=dstT: e.scalar_tensor_tensor(
                                out=dstT[:, c, sl], in0=ps[:, b, :], scalar=gcol[:, 0:1], in1=rs_, op0=ALU.mult, op1=ALU.mult),
                                reads=[PSK[b], rk] + gk, writes=[dk])
                    else:
                        q_, qk_ = qf[i % 2], ("qf", i % 2)
                        qb_, qbk = qb16[i % 2], ("qb16", i % 2)
                        S.op("dve", lambda e, q_=q_, b=b, rs_=rs_, gcol=gcol: e.scalar_tensor_tensor(
                            out=q_, in0=ps[:, b, :], scalar=gcol[:, 0:1], in1=rs_, op0=ALU.mult, op1=ALU.mult),
                            reads=[PSK[b], rk] + gk, writes=[qk_])
                        S.op("pool", lambda e, q_=q_, qb_=qb_: e.tensor_copy(out=qb_, in_=q_), reads=[qk_], writes=[qbk])

                def stage_c(i):
                    kind, c, hf = blocks[i]
                    b3 = 5 + i % 3
                    sl = slice(hf * 512, (hf + 1) * 512)
                    dstT = qT if kind == "q" else kT
                    dk = (kind + "T", c, hf)
                    q_, qk_ = qf[i % 2], ("qf", i % 2)
                    qb_, qbk = qb16[i % 2], ("qb16", i % 2)
                    r_, r1k = r1[i % 2], ("r1", i % 2)
                    mm(ps[:, b3, :], pm, qb_, True, True, [qbk, "cst_misc"], PSK[b3])
                    S.op("dve", lambda e, r_=r_, b3=b3, sl=sl: e.tensor_tensor(out=r_, in0=ps[:, b3, :], in1=sinT[:, sl], op=ALU.mult),
                         reads=[PSK[b3], "sinT"], writes=[r1k])
                    S.op("pool", lambda e, q_=q_, sl=sl: e.tensor_tensor(out=q_, in0=q_, in1=cosT[:, sl], op=ALU.mult),
                         reads=[qk_, "cosT"], writes=[qk_])
                    S.op("dve", lambda e, q_=q_, r_=r_, c=c, sl=sl, dstT=dstT: e.tensor_tensor(out=dstT[:, c, sl], in0=q_, in1=r_, op=ALU.add),
                         reads=[qk_, r1k], writes=[dk])

                for step in range(nb_ + 2):
                    if step < nb_:
                        stage_a(step)
                    if 0 <= step - 1 < nb_:
                        stage_b(step - 1)
                    if rope and 0 <= step - 2 < nb_:
                        stage_c(step - 2)
                if ATT_STOP <= 2:
                    AR.release(m)
                    return
                if p == 0:
                    for tt in range(8):
                        b = 6 + tt % 2
                        for c in range(nkc):
                            S.op("pe", lambda e, b=b, c=c, tt=tt: e.transpose(ps[:, b, c * 128:(c + 1) * 128], kf32[:, c, tt * 128:(tt + 1) * 128], ident_f),
                                 reads=[("kf32", c, tt // 4), "ident_f"], writes=[PSK[b]], acc=True)
                        k_ = kst[0]
                        kk = ("kst", 0)
                        S.op("dve", lambda e, b=b, k_=k_: e.tensor_copy(out=k_, in_=ps[:, b, 0:ckv]), reads=[PSK[b]], writes=[kk])
                        S.dma("sp", lambda e, k_=k_, tt=tt: e.dma_start(out=T[L["nk"]][tt * 128:(tt + 1) * 128, :], in_=k_), reads=[kk])

                if p == 1:
                    cst_ = [AR.alloc([128, ckv], F32) for _ in range(1)]
                    cb_ = [AR.alloc([128, ckv], BF16) for _ in range(1)]
                    S.op("pool", lambda e: e.memset(vcx, 1.0), writes=["vcx_init"])
                    for t4 in range(4):
                        c_ = cst_[0]
                        ck_ = ("cst_", 0)
                        S.dma("sp", lambda e, c_=c_, t4=t4: e.dma_start(out=c_, in_=T[L["cv"]][t4 * 128:(t4 + 1) * 128, :]), writes=[ck_])
                        S.op("dve", lambda e, c_=c_, t4=t4: e.tensor_copy(out=vcx[:, t4, :, 0:dh], in_=c_.rearrange("p (g d) -> p g d", g=4)),
                             reads=[ck_, "vcx_init"], writes=[("vcx", t4)])
                    for t4 in range(4):
                        c_ = cst_[0]
                        ck_ = ("cst_", 0)
                        bb = cb_[0]
                        bk = ("cb_", 0)
                        S.dma("sp", lambda e, c_=c_, t4=t4: e.dma_start(out=c_, in_=T[L["ck"]][t4 * 128:(t4 + 1) * 128, :]), writes=[ck_])
                        S.op("dve", lambda e, c_=c_, bb=bb: e.tensor_copy(out=bb, in_=c_), reads=[ck_], writes=[bk])
                        b = 6 + t4 % 2
                        pb = ps[:, b, 0:64 * nkc].bitcast(BF16).rearrange("p (c t) -> p c t", c=nkc)
                        for c in range(nkc):
                            S.op("pe", lambda e, pb=pb, c=c, bb=bb: e.transpose(pb[:, c, :], bb[:, c * 128:(c + 1) * 128], ident_b),
                                 reads=[bk, "ident_b"], writes=[PSK[b]], acc=True)
                        S.op("dve", lambda e, pb=pb, t4=t4: e.tensor_copy(out=kcT[:, :, t4 * 128:(t4 + 1) * 128], in_=pb),
                             reads=[PSK[b]], writes=[("kcT", t4)])

                if ATT_STOP <= 3:
                    AR.release(m)
                    return
                nkt_max = 7 if L["kind"] == "A" else 12
                S.barrier()
                AR.release(tmp_mark)
                PT = [AR.alloc([128, nkt_max, G * 128], BF16) for _ in range(2)]
                otok = [AR.alloc([128, D], BF16) for _ in range(2)]
                OT = xnT
                den = AR.alloc([128, 4], F32)
                rden = AR.alloc([128, 4], F32)
                sbc = [0]

                def key_tiles(qt):
                    s_ = qt // (slen // 128)
                    kts = []
                    if p == 1:
                        kts += [("c", t4, None) for t4 in range(4)]
                        if L["kind"] == "A":
                            if qt > 0:
                                kts.append(("n", qt - 1, mlo_b))
                            kts.append(("n", qt, None))
                            if qt < 7:
                                kts.append(("n", qt + 1, mhi_b))
                        else:
                            kts += [("n", t8, None) for t8 in range(8)]
                    else:
                        kts += [("n", s_ * 2 + t2, None) for t2 in range(2)]
                    return kts

                def emit_scores(qt, g, it):
                    kts = key_tiles(qt)
                    pt = PT[it % 2]
                    ptk = ("PT", it % 2)
                    if dh == 64:
                        mq, e2_ = g // 2, g % 2
                        prt = slice(64 * e2_, 64 * e2_ + 64)
                        rhs = qT[prt, 4 * mq:4 * mq + 4, qt * 128:(qt + 1) * 128]
                        qreads = [("qT", 4 * mq + i, qt // 4) for i in range(4)]
                        kch = mq
                    else:
                        prt = slice(0, 128)
                        rhs = qT[:, 2 * g:2 * g + 2, qt * 128:(qt + 1) * 128]
                        qreads = [("qT", 2 * g + i, qt // 4) for i in range(2)]
                        kch = g
                    for ki, (src, ti, msk) in enumerate(kts):
                        b = sbc[0] % 4
                        sbc[0] += 1
                        if src == "c":
                            lhsT = kcT[prt, kch, ti * 128:(ti + 1) * 128]
                            kr = [("kcT", ti)]
                        else:
                            lhsT = kT[prt, kch, ti * 128:(ti + 1) * 128]
                            kr = [("kT", kch, ti // 4)]
                        mm(ps[:, b, 0:G * 128], lhsT, rhs, True, True, kr + qreads, PSK[b])
                        S.op("act", lambda e, pt=pt, ki=ki, b=b: e.activation(out=pt[:, ki, :], in_=ps[:, b, 0:G * 128], func=AF.Exp, scale=scale),
                             reads=[PSK[b]], writes=[(ptk, ki)])
                        if msk is not None:
                            S.op("pool", lambda e, pt=pt, ki=ki, msk=msk: e.tensor_tensor(
                                out=pt[:, ki, :], in0=pt[:, ki, :], in1=msk[:, 0:G, :].rearrange("p g q -> p (g q)"), op=ALU.mult),
                                reads=[(ptk, ki), "cst_misc"], writes=[(ptk, ki)])

                def emit_pv(qt, g, it):
                    kts = key_tiles(qt)
                    pt = PT[it % 2]
                    ptk = ("PT", it % 2)
                    ob = 4 + it % 2
                    ot = otok[qt % 2]
                    otk = ("otok", qt % 2)
                    opv = ps[:, ob, 0:G * (dh + 1)].rearrange("p (i d) -> p i d", i=G)
                    for i in range(G):
                        for ki, (src, ti, msk) in enumerate(kts):
                            if src == "c":
                                vr = vcx[:, ti, g, 0:dh + 1]
                                vk = [("vcx", ti)]
                            else:
                                vr = vx[:, ti, g, 0:dh + 1]
                                vk = [("vx", ti)]
                            mm(opv[:, i, :], pt[:, ki, i * 128:(i + 1) * 128], vr, ki == 0, ki == len(kts) - 1,
                               [(ptk, ki)] + vk, PSK[ob])
                    if esink is not None:
                        S.op("dve", lambda e, opv=opv, g=g: e.tensor_tensor(out=den[:, 0:G], in0=opv[:, :, dh], in1=esink[:, g * G:(g + 1) * G], op=ALU.add),
                             reads=[PSK[ob], "esink"], writes=["den"])
                        S.op("dve", lambda e: e.reciprocal(out=rden[:, 0:G], in_=den[:, 0:G]), reads=["den"], writes=["rden"])
                    else:
                        S.op("dve", lambda e, opv=opv: e.reciprocal(out=rden[:, 0:G], in_=opv[:, :, dh]), reads=[PSK[ob]], writes=["rden"])
                    for i in range(G):
                        h = g * G + i
                        S.op("dve", lambda e, opv=opv, i=i, h=h, ot=ot: e.tensor_scalar(out=ot[:, h * dh:(h + 1) * dh], in0=opv[:, i, 0:dh],
                                                                                         scalar1=rden[:, i:i + 1], scalar2=None, op0=ALU.mult),
                             reads=[PSK[ob], "rden"], writes=[(otk, h)])
                    if g == 3:
                        b = 6 + qt % 2
                        pb = ps[:, b, :].bitcast(BF16).rearrange("p (c t) -> p c t", c=8)
                        for c in range(8):
                            S.op("pe", lambda e, pb=pb, c=c, ot=ot: e.transpose(pb[:, c, :], ot[:, c * 128:(c + 1) * 128], ident_b),
                                 reads=[(otk, h) for h in range(H)] + ["ident_b"], writes=[PSK[b]], acc=True)
                        S.op("act", lambda e, pb=pb, qt=qt: e.activation(out=OT[:, :, qt * 128:(qt + 1) * 128], in_=pb, func=AF.Copy),
                             reads=[PSK[b]], writes=[("OT", qt)] + XK)

                items = [(qt, g) for qt in range(8) for g in range(4)]
                emit_scores(items[0][0], items[0][1], 0)
                for it, (qt, g) in enumerate(items):
                    if it + 1 < len(items):
                        emit_scores(items[it + 1][0], items[it + 1][1], it + 1)
                    emit_pv(qt, g, it)
                bi = 0
                for oc in range(8):
                    for hf in range(2):
                        b = bi % 4
                        bi += 1
                        for kc in range(8):
                            mm(ps[:, b, :], wo_b[:, kc, oc * 128:(oc + 1) * 128], OT[:, kc, hf * 512:(hf + 1) * 512], kc == 0, kc == 7,
                               ["wo_b"] + [("OT", hf * 4 + q) for q in range(4)], PSK[b])
                        resid(oc, hf, b, 16)
                AR.release(m)

            def s5_layer(l):
                CH = 128 // nseq
                m = AR.mark()
                norm_mod(A1, "A1", 0, perm8=True)
                U = AR.alloc([128, 64, 128], BF16)
                TB = AR.alloc([128, 2, 64, 128], BF16)
                SS = AR.alloc([128, 2, 64, 128], F32)
                ALL = AR.alloc([128, 2, 64], F32)
                ALS = AR.alloc([128, 2, 64], F32)
                xf = xnT.rearrange("q a n -> q (a n)").bitcast(F32)
                H0v = xf[:, 0:128 * nseq].rearrange("q (r g s) -> q r g s", r=2, g=64)
                L8 = AR.alloc([128, 2, 128], F32)
                FS = AR.alloc([128, 2, 64], F32)
                FSo = AR.alloc([128, 2, 128], F32)
                tt_ = [[xf[:, 128 * nseq * (1 + 2 * a_ + b_):128 * nseq * (2 + 2 * a_ + b_)].rearrange("q (r g s) -> q r g s", r=2, g=64)
                        for b_ in range(2)] for a_ in range(2)]
                S.dma("sp", lambda e: e.dma_start(out=T["D1"][p].rearrange("(fc q) n -> q fc n", q=128), in_=xnT), reads=XK, writes=["D1"])
                d1v = T["D1"][p].rearrange("(g c) (t k) -> t c g k", c=16, t=8)
                for t in range(8):
                    S.dma("sp", lambda e, t=t: e.dma_start(out=U[16 * t:16 * (t + 1)], in_=d1v[t]), reads=["D1"], writes=[("U", t)])
                UK = [("U", t) for t in range(8)]
                for r_ in range(2):
                    for g4 in range(4):
                        S.dma("sp", lambda e, r_=r_, g4=g4: e.dma_start(out=TB[:, r_, g4 * 16:(g4 + 1) * 16, :], in_=T["TS"][r_, :, g4 * 16:(g4 + 1) * 16, :]),
                              writes=[("TB", r_, g4)])
                S.dma("sp", lambda e: e.dma_start(out=L8[0:64], in_=T["TP8"].rearrange("r (d g) p -> g r d p", d=2)), writes=["L8"])
                for r_ in range(2):
                    S.op("pe", lambda e, r_=r_: e.transpose(ps[:, 7, 0:64], L8[0:64, r_, :], ident_f[0:64, 0:64]), reads=["L8", "ident_f"], writes=[PSK[7]])
                    if r_ == 0:
                        S.op("dve", lambda e: e.tensor_copy(out=ALL[:, 0, :], in_=ps[:, 7, 0:64]), reads=[PSK[7]], writes=["AL"])
                        S.op("dve", lambda e: e.tensor_copy(out=ALL[:, 1, :], in_=ps[:, 7, 0:64]), reads=[PSK[7]], writes=["AL"])
                    else:
                        S.op("dve", lambda e: e.tensor_scalar(out=ALS[:, 0, :], in0=ps[:, 7, 0:64], scalar1=-1.0, scalar2=None, op0=ALU.mult), reads=[PSK[7]], writes=["AL"])
                        S.op("dve", lambda e: e.tensor_copy(out=ALS[:, 1, :], in_=ps[:, 7, 0:64]), reads=[PSK[7]], writes=["AL"])
                if p == 1:
                    for r_ in range(2):
                        S.dma("sp", lambda e, r_=r_: e.dma_start(out=L8[0:64, r_, :].rearrange("g (d p) -> g d p", d=2),
                                                                  in_=T["st1"][:, r_].rearrange("d g p -> g d p")), reads=["L8"], writes=["L8"], key=("L8", r_))
                    for r_ in range(2):
                        S.op("pe", lambda e, r_=r_: e.transpose(ps[:, 7, 0:64], L8[0:64, r_, :], ident_f[0:64, 0:64]), reads=["L8", "ident_f"], writes=[PSK[7]])
                        S.op("dve", lambda e, r_=r_: e.tensor_copy(out=H0v[:, r_, :, 0], in_=ps[:, 7, 0:64]), reads=[PSK[7]], writes=["H0"] + XK)
                else:
                    S.op("dve", lambda e: e.memset(H0v, 0.0), writes=["H0"] + XK)
                for gq in range(16):
                    for r_ in range(2):
                        b = (gq * 2 + r_) % 4
                        for gi in range(4):
                            g = gq * 4 + gi
                            S.op("pe", lambda e, b=b, gi=gi, g=g, r_=r_: e.matmul(ps[:, b, gi * 128:(gi + 1) * 128], TB[:, r_, g, :], U[:, g, :], start=True, stop=True),
                                 reads=UK + [("TB", r_, g // 16)], writes=[PSK[b]], acc=True)
                        eng = "act" if r_ == 0 else "dve"
                        if r_ == 0:
                            S.op("act", lambda e, b=b, gq=gq: e.activation(out=SS[:, 0, gq * 4:(gq + 1) * 4, :].rearrange("q g k -> q (g k)"), in_=ps[:, b, :], func=AF.Copy),
                                 reads=[PSK[b]], writes=[("SS", 0)])
                        else:
                            S.op("dve", lambda e, b=b, gq=gq: e.tensor_copy(out=SS[:, 1, gq * 4:(gq + 1) * 4, :].rearrange("q g k -> q (g k)"), in_=ps[:, b, :]),
                                 reads=[PSK[b]], writes=[("SS", 1)])
                S.barrier()
                NS, CHV = nseq, CH
                SSv = SS.rearrange("q r g (s k) -> q r g s k", k=CHV)
                for d, eng in ((0, "dve"), (1, "pool")):
                    hs = slice(64 * d, 64 * d + 64)
                    order = list(range(CHV)) if d == 0 else list(range(CHV - 1, -1, -1))
                    t1, t2 = tt_[d]
                    k1, k2, kS = ("rt1", d), ("rt2", d), ("SSd", d)
                    all_ = ALL[hs].unsqueeze(3).broadcast_to([64, 2, 64, NS])
                    als_ = ALS[hs].unsqueeze(3).broadcast_to([64, 2, 64, NS])
                    for ii, kk in enumerate(order):
                        stages = [[], [], [], []]
                        for gh in range(2):
                            gs = slice(32 * gh, 32 * gh + 32)
                            if ii == 0:
                                prev, prev_sw = H0v[hs, :, gs], H0v[hs, ::-1, gs]
                            else:
                                kp = order[ii - 1]
                                prev, prev_sw = SSv[hs, :, gs, :, kp], SSv[hs, ::-1, gs, :, kp]
                            cur = SSv[hs, :, gs, :, kk]
                            a_, b_ = all_[:, :, gs], als_[:, :, gs]
                            x1, x2 = t1[hs, :, gs], t2[hs, :, gs]
                            c1, c2, cS = (k1, gh), (k2, gh), (kS, gh)
                            stages[0].append((lambda e, x1=x1, prev=prev, a_=a_: e.tensor_tensor(out=x1, in0=prev, in1=a_, op=ALU.mult), [cS, "AL", "H0"], [c1]))
                            stages[1].append((lambda e, x2=x2, prev_sw=prev_sw, b_=b_: e.tensor_tensor(out=x2, in0=prev_sw, in1=b_, op=ALU.mult), [cS, "AL", "H0"], [c2]))
                            stages[2].append((lambda e, x1=x1, x2=x2: e.tensor_tensor(out=x1, in0=x1, in1=x2, op=ALU.add), [c1, c2], [c1]))
                            stages[3].append((lambda e, x1=x1, cur=cur: e.tensor_tensor(out=cur, in0=cur, in1=x1, op=ALU.add), [cS, c1], [cS]))
                        for st_ in stages:
                            for fn_, r__, w__ in st_:
                                S.op(eng, fn_, reads=r__, writes=w__)
                S.barrier()
                if p == 0:
                    for s_ in range(nseq):
                        for r_ in range(2):
                            S.op("dve", lambda e, r_=r_, s_=s_: e.tensor_copy(out=FS[0:64, r_, :], in_=SS[0:64, r_, :, s_ * CH + CH - 1]), reads=[], writes=["FS"])
                            S.op("dve", lambda e, r_=r_, s_=s_: e.tensor_copy(out=FS[64:128, r_, :], in_=SS[64:128, r_, :, s_ * CH]), reads=[], writes=["FS"])
                            S.op("pe", lambda e, r_=r_: e.transpose(ps[0:64, 7, r_ * 128:(r_ + 1) * 128], FS[:, r_, :], ident_f), reads=["FS", "ident_f"], writes=[PSK[7]], acc=True)
                        S.op("dve", lambda e: e.tensor_copy(out=FSo[0:64].rearrange("g r n -> g (r n)"), in_=ps[0:64, 7, 0:256]), reads=[PSK[7]], writes=["FSo"])
                        for r_ in range(2):
                            S.dma("sp", lambda e, s_=s_, r_=r_: e.dma_start(out=T["nst"][s_][:, r_].rearrange("d g p -> g d p"),
                                                                           in_=FSo[0:64, r_, :].rearrange("g (d p) -> g d p", d=2)), reads=["FSo"], key=("nst", r_))
                for r_ in range(2):
                    for s_ in range(nseq):
                        k0 = s_ * CH
                        S.op("act", lambda e, r_=r_, k0=k0: e.activation(out=TB[0:64, r_, :, k0 + 1:k0 + CH], in_=SS[0:64, r_, :, k0:k0 + CH - 1], func=AF.Copy),
                             reads=[], writes=["Hin"])
                        S.op("act", lambda e, r_=r_, k0=k0, s_=s_: e.activation(out=TB[0:64, r_, :, k0], in_=H0v[0:64, r_, :, s_], func=AF.Copy), reads=["H0"], writes=["Hin"])
                        S.op("dve", lambda e, r_=r_, k0=k0: e.tensor_copy(out=TB[64:128, r_, :, k0:k0 + CH - 1], in_=SS[64:128, r_, :, k0 + 1:k0 + CH]),
                             reads=[], writes=["Hin"])
                        S.op("dve", lambda e, r_=r_, k0=k0, s_=s_: e.tensor_copy(out=TB[64:128, r_, :, k0 + CH - 1], in_=H0v[64:128, r_, :, s_]), reads=["H0"], writes=["Hin"])
                S.barrier()
                SSb = SS.rearrange("q r g k -> q (r g k)").bitcast(BF16)
                EL = SSb[:, 0:16384].rearrange("q (r g n) -> q r g n", r=2, g=64)
                ML = SSb[:, 16384:24576].rearrange("q (g n) -> q g n", g=64)
                YC = SSb[:, 24576:32768].rearrange("q (g n) -> q g n", g=64)
                for r_ in range(2):
                    for g4 in range(4):
                        S.dma("sp", lambda e, r_=r_, g4=g4: e.dma_start(out=EL[:, r_, g4 * 16:(g4 + 1) * 16, :], in_=T["TE"][r_, :, g4 * 16:(g4 + 1) * 16, :]),
                              writes=[("EL", r_, g4)])
                for g4 in range(4):
                    S.dma("sp", lambda e, g4=g4: e.dma_start(out=ML[:, g4 * 16:(g4 + 1) * 16, :], in_=T["TM"][:, g4 * 16:(g4 + 1) * 16, :]),
                          writes=[("ML", g4)])
                gu = [AR.alloc([128, 512], F32) for _ in range(2)]
                gs = [AR.alloc([128, 512], F32) for _ in range(2)]
                for gq in range(16):
                    b = gq % 4
                    for gi in range(4):
                        g = gq * 4 + gi
                        o_ = ps[:, b, gi * 128:(gi + 1) * 128]
                        S.op("pe", lambda e, o_=o_, g=g: e.matmul(o_, ML[:, g, :], U[:, g, :], start=True, stop=False), reads=[("ML", g // 16)] + UK, writes=[PSK[b]], acc=True)
                        S.op("pe", lambda e, o_=o_, g=g: e.matmul(o_, EL[:, 0, g, :], TB[:, 0, g, :], start=False, stop=False), reads=[("EL", 0, g // 16), "Hin"], writes=[PSK[b]], acc=True)
                        S.op("pe", lambda e, o_=o_, g=g: e.matmul(o_, EL[:, 1, g, :], TB[:, 1, g, :], start=False, stop=True), reads=[("EL", 1, g // 16), "Hin"], writes=[PSK[b]], acc=True)
                    u_, s2 = gu[gq % 2], gs[gq % 2]
                    uk, sk = ("gu", gq % 2), ("gs", gq % 2)
                    S.op("act", lambda e, u_=u_, b=b: e.activation(out=u_, in_=ps[:, b, :], func=AF.Square), reads=[PSK[b]], writes=[uk])
                    S.op("dve", lambda e, u_=u_: e.tensor_scalar(out=u_, in0=u_, scalar1=0.044715, scalar2=1.0, op0=ALU.mult, op1=ALU.add), reads=[uk], writes=[uk])
                    S.op("dve", lambda e, u_=u_, b=b: e.tensor_tensor(out=u_, in0=u_, in1=ps[:, b, :], op=ALU.mult), reads=[uk, PSK[b]], writes=[uk])
                    S.op("act", lambda e, u_=u_, s2=s2: e.activation(out=s2, in_=u_, func=AF.Sigmoid, scale=1.5957691216057308), reads=[uk], writes=[sk])
                    S.op("dve", lambda e, s2=s2, b=b, gq=gq: e.tensor_tensor(out=YC[:, gq * 4:(gq + 1) * 4, :].rearrange("q g k -> q (g k)"), in0=s2, in1=ps[:, b, :], op=ALU.mult),
                         reads=[sk, PSK[b]], writes=["YC"])
                d2v = T["D2"][p].rearrange("(g c) (t k) -> t c g k", c=16, t=8)
                for t in range(8):
                    S.dma("sp", lambda e, t=t: e.dma_start(out=d2v[t], in_=YC[16 * t:16 * (t + 1)]), reads=["YC"], writes=["D2"], key=("D2w", t))
                S.dma("sp", lambda e: e.dma_start(out=xnT, in_=T["D2"][p].rearrange("(fc q) n -> q fc n", q=128)), reads=["D2"], writes=XK, key="D2r")
                S.barrier()
                AR.release(m)
                m = AR.mark()
                wg = AR.alloc([128, 8, 2 * D], BF16)
                sg2 = [AR.alloc([128, 512], F32) for _ in range(2)]
                mx = [AR.alloc([128, 512], F32) for _ in range(2)]
                for h2 in range(2):
                    S.dma("pool", lambda e, h2=h2: e.dma_start(out=wg[:, :, h2 * D:(h2 + 1) * D],
                                                               in_=T["l1_w_glu"][:, h2 * D:(h2 + 1) * D].rearrange("(kc q) n -> q kc n", q=128)),
                          writes=[("wg", h2)])
                it = 0
                for oc in range(8):
                    for hf in range(2):
                        bA, bB = (it % 2) * 2, (it % 2) * 2 + 1
                        s_, x_ = sg2[it % 2], mx[it % 2]
                        sk, xk_ = ("sg2", it % 2), ("mx", it % 2)
                        it += 1
                        for kc in range(8):
                            mm(ps[:, bA, :], wg[:, kc, oc * 128:(oc + 1) * 128], xnT[:, kc, hf * 512:(hf + 1) * 512], kc == 0, kc == 7, [("wg", 0), XK[kc]], PSK[bA])
                        for kc in range(8):
                            mm(ps[:, bB, :], wg[:, kc, D + oc * 128:D + (oc + 1) * 128], xnT[:, kc, hf * 512:(hf + 1) * 512], kc == 0, kc == 7, [("wg", 1), XK[kc]], PSK[bB])
                        S.op("act", lambda e, s_=s_, bB=bB: e.activation(out=s_, in_=ps[:, bB, :], func=AF.Sigmoid), reads=[PSK[bB]], writes=[sk])
                        S.op("dve", lambda e, s_=s_, x_=x_, bA=bA: e.tensor_tensor(out=x_, in0=ps[:, bA, :], in1=s_, op=ALU.mult), reads=[PSK[bA], sk], writes=[xk_])
                        hv = hT[:, oc, :].rearrange("q (k t) -> q t k", t=8)[:, 4 * hf:4 * hf + 4, :]
                        S.op("dve", lambda e, x_=x_, hv=hv, oc=oc: e.scalar_tensor_tensor(out=hv, in0=x_.rearrange("q (t k) -> q t k", t=4), scalar=modT[:, 16 + oc:17 + oc],
                                                                                        in1=hv, op0=ALU.mult, op1=ALU.add),
                             reads=[xk_, "modT", HK[oc]], writes=[HK[oc]])
                AR.release(m)

            for l in range(NLAYERS):
                if STAGE & 1:
                    compute_mod(l)
                S.barrier()
                if STAGE & 2:
                    if LAYERS[l]["kind"] == "B":
                        if not SKIP_S5:
                            s5_layer(l)
                    else:
                        attention(l)
                S.barrier()
                if STAGE & 4:
                    ffn(l)
                S.barrier()

            m0 = AR.mark()
            yst = [AR.alloc([128, D], F32) for _ in range(8)]
            for tt in range(8):
                y_ = yst[tt]
                yk = ("yst", tt)
                for hf in range(2):
                    b = (tt * 2 + hf) % 8
                    for q in range(4):
                        kc = hf * 4 + q
                        S.op("pe", lambda e, b=b, q=q, kc=kc, tt=tt: e.transpose(ps[:, b, q * 128:(q + 1) * 128], hT[:, kc, tt * 128:(tt + 1) * 128], ident_f),
                             reads=[HK[kc], "ident_f"], writes=[PSK[b]], acc=True)
                    if hf:
                        S.op("act", lambda e, b=b, y_=y_: e.activation(out=y_[:, 512:1024], in_=ps[:, b, :], func=AF.Copy), reads=[PSK[b]], writes=[(yk, 1)])
                    else:
                        S.op("dve", lambda e, b=b, y_=y_: e.tensor_copy(out=y_[:, 0:512], in_=ps[:, b, :]), reads=[PSK[b]], writes=[(yk, 0)])
                S.dma("sp", lambda e, y_=y_, tt=tt: e.dma_start(out=T["y"][p, tt * 128:(tt + 1) * 128, :], in_=y_), reads=[(yk, 0), (yk, 1)], key=yk)
            S.barrier()
            AR.release(m0)

        pm_mark = AR.mark()
        prologue_mod_start()
        if not SKIP_S5 and NLAYERS > 1 and (STAGE & 2):
            s5_tables()
        prologue_mod_finish()
        AR.release(pm_mark)
        run_path(0)
        run_path(1)
        S.barrier()
        print('n_sems', len(S.sem_objs), 'insts', {e: len(S.prog[e]) for e in ENGS})
        S.emit()
    return nc


def _consts():
    cst = np.zeros((9, 128, 128), np.float32)
    for tt_ in range(8):
        for cc_ in range(16):
            cst[CST_R16, cc_, tt_ * 16 + cc_] = 1.0
    cst[CST_IDENT] = np.eye(128)
    bd = np.zeros((128, 128), np.float32)
    bd[:64, :64] = 1
    bd[64:, 64:] = 1
    cst[CST_BD64] = bd
    kk = np.arange(128)[:, None]
    qq = np.arange(128)[None, :]
    cst[CST_MLO] = (qq <= kk)
    cst[CST_MHI] = (kk <= qq)

    def perm(dh):
        qt = dh // 4
        half = dh // 2
        Pm = np.zeros((128, 128), np.float32)
        for m in range(128):
            d = m % dh
            if (d % half) < qt:
                Pm[m, m + qt] = -1.0
            else:
                Pm[m, m - qt] = 1.0
        return Pm.T.copy()

    cst[CST_PA] = perm(64)
    cst[CST_PC] = perm(128)
    t1 = np.arange(8)
    tau_p = np.repeat(t1, 16)[:, None]
    tau = np.repeat(t1, 16)[None, :]
    cst[CST_S5F] = (tau >= tau_p)
    cst[CST_S5B] = (tau <= tau_p)
    rope = np.zeros((4, 128, NT), np.float32)
    t = np.arange(NT)
    row = (t // 64).astype(np.float32)
    col = (t % 64).astype(np.float32)
    for idx, dh in ((0, 64), (2, 128)):
        half, qt = dh // 2, dh // 4
        freqs = (1.0 / (np.float32(10000.0) ** (np.arange(qt, dtype=np.float32) / np.float32(qt)))).astype(np.float32)
        for pp in range(128):
            d = pp % dh
            pos = row if d < half else col
            ang = (pos * freqs[d % qt]).astype(np.float32)
            rope[idx, pp] = np.cos(ang)
            rope[idx + 1, pp] = np.sin(ang)
    return cst, rope


_PROG = {}


def kernel(**inp):
    f32 = lambda a: np.ascontiguousarray(np.asarray(a, dtype=np.float32))
    inp = {k: f32(v) for k, v in inp.items()}
    if "nc" not in _PROG:
        _PROG["nc"] = build_program()
    nc = _PROG["nc"]
    cst, rope = _consts()
    shared = {}
    for name, shape in IN_SPECS:
        if name in inp:
            shared[name] = inp[name].reshape(shape)
    shared["cst"] = cst
    shared["rope"] = rope
    in_maps = []
    for c in range(8):
        m = dict(shared)
        m["xin"] = np.stack([inp["x_prompt"][4 * c:4 * c + 4].reshape(NT, D), inp["x_sample"][c]])
        m["cond"] = np.stack([inp["c_ctx"], inp["c"][c]])
        m["ck0"] = inp["cache_l0_k"][c].reshape(512, 256)
        m["cv0"] = inp["cache_l0_v"][c].reshape(512, 256)
        m["st1"] = inp["state_l1"][c]
        m["ck2"] = inp["cache_l2_k"][c].reshape(512, 512)
        m["cv2"] = inp["cache_l2_v"][c].reshape(512, 512)
        m["ck3"] = inp["cache_l3_k"][c].reshape(512, 256)
        m["cv3"] = inp["cache_l3_v"][c].reshape(512, 256)
        in_maps.append({k: np.ascontiguousarray(v) for k, v in m.items()})
    res = run_bass_kernel_spmd(nc, in_maps, core_ids=list(range(8)))
    R = res.results
    y_prompt = np.concatenate([R[c]["y"][0].reshape(4, 256, D) for c in range(8)], 0)
    y_sample = np.stack([R[c]["y"][1] for c in range(8)], 0)
    cat = lambda k, shp: np.concatenate([R[c][k].reshape(shp) for c in range(8)], 0)
    out = (y_prompt, y_sample,
           cat("nk0", (4, 256, 4, 64)), cat("nv0", (4, 256, 4, 64)),
           cat("nst", (4, 2, 2, 64, 64)),
           cat("nk2", (4, 256, 4, 128)), cat("nv2", (4, 256, 4, 128)),
           cat("nk3", (4, 256, 4, 64)), cat("nv3", (4, 256, 4, 64)))
    return tuple(np.ascontiguousarray(o, dtype=np.float32) for o in out)
```

```python
import math
import numpy as np
from contextlib import ExitStack
import concourse.bass as bass
import concourse.mybir as mybir
from concourse.bass_utils import run_bass_kernel_spmd

F32 = mybir.dt.float32
BF16 = mybir.dt.bfloat16
I32 = mybir.dt.int32
U8 = mybir.dt.uint8
AF = mybir.ActivationFunctionType
ALU = mybir.AluOpType
AX = mybir.AxisListType

D = 1024
DFF = 2816
NT = 1024
ENGS = ["pe", "act", "dve", "pool", "sp"]
EPS = 1e-6
NLAYERS = 4
SKIP_S5 = False
DEBUG_H = False
STAGE = 7
ATT_STOP = 9


class Sched:
    def __init__(self, nc, stack):
        self.nc = nc
        self.stack = stack
        self.prog = {e: [] for e in ENGS}
        self.cnt = {e: 0 for e in ENGS}
        self.seen = {e: {} for e in ENGS}
        self.lastw = {}
        self.readers = {}
        self.sem_objs = []
        self.esid = {}
        for e in ENGS:
            self.esid[e] = self._new_sem("s_" + e)
        self.dsem = {}
        self.dfree = []
        self.psem = {}

    def _new_sem(self, name):
        self.sem_objs.append(self.stack.enter_context(self.nc.semaphore(name)))
        return len(self.sem_objs) - 1

    def _dma_sem(self, key, q):
        if q == "pool":
            if key not in self.psem:
                self.psem[key] = [self._new_sem("p%d" % len(self.sem_objs)), 0]
            return self.psem[key]
        if key not in self.dsem:
            if self.dfree:
                self.dsem[key] = self.dfree.pop()
            else:
                self.dsem[key] = [self._new_sem("d%d" % len(self.sem_objs)), 0]
        return self.dsem[key]

    def _deps(self, eng, reads, writes, acc=False):
        need = {}

        def add(rec, same_ok=False):
            sid, val, e = rec
            if same_ok and e == eng:
                return
            if need.get(sid, 0) < val:
                need[sid] = val

        for k in reads:
            if k in self.lastw:
                add(self.lastw[k])
        for k in writes:
            if k in self.lastw:
                add(self.lastw[k], same_ok=(acc and eng == "pe"))
            for r in self.readers.get(k, ()):
                add(r)
        waits = []
        seen = self.seen[eng]
        for sid, val in need.items():
            if seen.get(sid, 0) >= val:
                continue
            seen[sid] = val
            waits.append((sid, val))
        return waits

    def op(self, eng, fn, reads=(), writes=(), acc=False):
        waits = self._deps(eng, reads, writes, acc)
        self.cnt[eng] += 1
        rec = (self.esid[eng], self.cnt[eng], eng)
        self.prog[eng].append((waits, fn, self.esid[eng], 1))
        for k in reads:
            self.readers.setdefault(k, []).append(rec)
        for k in writes:
            self.lastw[k] = rec
            self.readers[k] = []

    def dma(self, q, fn, reads=(), writes=(), key=None):
        if key is None:
            key = writes[0] if writes else reads[0]
        waits = self._deps(q, reads, writes)
        ds = self._dma_sem(key, q)
        ds[1] += 16
        rec = (ds[0], ds[1], "dma")
        self.prog[q].append((waits, fn, ds[0], 16))
        for k in reads:
            self.readers.setdefault(k, []).append(rec)
        for k in writes:
            self.lastw[k] = rec
            self.readers[k] = []

    def barrier(self):
        allw = [(self.esid[e], self.cnt[e], e) for e in ENGS if self.cnt[e] > 0]
        allw += [(sid, c, None) for k, (sid, c) in self.dsem.items() if c > 0]
        allw += [(sid, c, None) for k, (sid, c) in self.psem.items() if c > 0]
        for e in ENGS:
            waits = []
            for sid, val, owner in allw:
                if self.seen[e].get(sid, 0) >= val:
                    continue
                self.seen[e][sid] = val
                waits.append((sid, val))
            if waits:
                self.prog[e].append((waits, None, None, 0))
        self.dfree.extend(self.dsem.values())
        self.dsem = {}
        self.lastw = {}
        self.readers = {}

    def emit(self):
        nc, sems, prog = self.nc, self.sem_objs, self.prog

        def run(ename):
            def body(eng):
                for waits, fn, incsid, incv in prog[ename]:
                    for sid, val in waits:
                        eng.wait_ge(sems[sid], val)
                    if fn is not None:
                        fn(eng).then_inc(sems[incsid], incv)
            return body

        with nc.Block() as block:
            block.tensor(run("pe"))
            block.scalar(run("act"))
            block.vector(run("dve"))
            block.gpsimd(run("pool"))
            block.sync(run("sp"))


class Arena:
    def __init__(self, tensor, size):
        self.t = tensor
        self.size = size
        self.off = 0

    def alloc(self, shape, dt):
        nb = {F32: 4, BF16: 2, I32: 4}[dt]
        n = int(np.prod(shape[1:]))
        off = (self.off + 63) // 64 * 64
        assert off + n * nb <= self.size, ("SBUF arena overflow", off, n * nb, self.size)
        self.off = off + n * nb
        ap = self.t[:, off:off + n * nb].bitcast(dt)
        if len(shape) == 3:
            ap = ap.rearrange("p (a b) -> p a b", a=shape[1])
        elif len(shape) == 4:
            ap = ap.rearrange("p (a b c) -> p a b c", a=shape[1], b=shape[2])
        if shape[0] != 128:
            ap = ap[0:shape[0]]
        return ap

    def mark(self):
        return self.off

    def release(self, m):
        self.off = m


LAYERS = [
    dict(kind="A", dh=64, H=16, ckv=256, wqkv="l0_w_qkv", qn="l0_q_norm", kn="l0_k_norm", sink="l0_sink", wo="l0_w_o",
         ck="ck0", cv="cv0", nk="nk0", nv="nv0"),
    dict(kind="B"),
    dict(kind="C", dh=128, H=8, ckv=512, wqkv="l2_w_qkv", qn="l2_q_norm", kn="l2_k_norm", sink=None, wo="l2_w_o",
         ck="ck2", cv="cv2", nk="nk2", nv="nv2"),
    dict(kind="A", dh=64, H=16, ckv=256, wqkv="l3_w_qkv", qn="l3_q_norm", kn="l3_k_norm", sink="l3_sink", wo="l3_w_o",
         ck="ck3", cv="cv3", nk="nk3", nv="nv3"),
]

IN_SPECS = [
    ("xin", [2, NT, D]), ("cond", [2, D]),
    ("ck0", [512, 256]), ("cv0", [512, 256]), ("st1", [2, 2, 64, 64]),
    ("ck2", [512, 512]), ("cv2", [512, 512]), ("ck3", [512, 256]), ("cv3", [512, 256]),
    ("norm1", [4, D]), ("norm2", [4, D]), ("w_mod", [4, D, 6 * D]), ("b_mod", [4, 6 * D]),
    ("w_up", [4, D, 2 * DFF]), ("b_up", [4, 2 * DFF]), ("conv_k", [4, 3, 2 * DFF]), ("conv_b", [4, 2 * DFF]),
    ("w_down", [4, DFF, D]),
    ("l0_w_qkv", [D, 1536]), ("l0_q_norm", [64]), ("l0_k_norm", [64]), ("l0_sink", [16]), ("l0_w_o", [D, D]),
    ("l1_lam_re", [2, 64, 64]), ("l1_lam_im", [2, 64, 64]), ("l1_log_dt", [2, 64]),
    ("l1_b_re", [2, 64, 64, 16]), ("l1_b_im", [2, 64, 64, 16]), ("l1_c_re", [2, 64, 16, 64]), ("l1_c_im", [2, 64, 16, 64]),
    ("l1_d_skip", [D]), ("l1_w_glu", [D, 2 * D]),
    ("l2_w_qkv", [D, 2048]), ("l2_q_norm", [128]), ("l2_k_norm", [128]), ("l2_w_o", [D, D]),
    ("l3_w_qkv", [D, 1536]), ("l3_q_norm", [64]), ("l3_k_norm", [64]), ("l3_sink", [16]), ("l3_w_o", [D, D]),
    ("cst", [9, 128, 128]), ("rope", [4, 128, NT]),
]
OUT_SPECS = [
    ("y", [2, NT, D]), ("nk0", [NT, 256]), ("nv0", [NT, 256]), ("nst", [4, 2, 2, 64, 64]),
    ("nk2", [NT, 512]), ("nv2", [NT, 512]), ("nk3", [NT, 256]), ("nv3", [NT, 256]),
]
CST_IDENT, CST_BD64, CST_MLO, CST_MHI, CST_PA, CST_PC, CST_S5F, CST_S5B, CST_R16 = range(9)


def build_program():
    nc = bass.Bass("TRN2", target_bir_lowering=False)
    T = {}
    for name, shape in IN_SPECS:
        T[name] = nc.dram_tensor(name, shape, F32, kind="ExternalInput").ap()
    for name, shape in OUT_SPECS:
        T[name] = nc.dram_tensor(name, shape, F32, kind="ExternalOutput").ap()
    for name, shape, dt_ in [("TX", [128, 16384], BF16), ("TZ", [128, 16384], BF16), ("TM", [128, 64, 128], BF16),
                             ("TE", [2, 128, 64, 128], BF16), ("TS", [2, 128, 64, 128], BF16), ("TP8", [2, 128, 64], F32),
                             ("D1", [2, 1024, 1024], BF16), ("D2", [2, 1024, 1024], BF16)]:
        T[name] = nc.dram_tensor(name, shape, dt_, kind="Internal").ap()
    if DEBUG_H:
        T["dbg"] = nc.dram_tensor("dbg", [2, NT, D], F32, kind="ExternalOutput").ap()

    with ExitStack() as st:
        S = Sched(nc, st)
        ARENA_BYTES = 190 * 1024
        arena_t = st.enter_context(nc.sbuf_tensor("arena", [128, ARENA_BYTES], U8))
        AR = Arena(arena_t, ARENA_BYTES)
        ps = st.enter_context(nc.psum_tensor("ps", [128, 8, 512], F32))
        PSK = [("ps", b) for b in range(8)]
        uid = [0]

        def fresh(prefix):
            uid[0] += 1
            return (prefix, uid[0])

        ident_f = AR.alloc([128, 128], F32)
        ident_b = AR.alloc([128, 128], BF16)
        ones_b = AR.alloc([128, 128], BF16)
        ones_f = AR.alloc([128, 128], F32)
        bd64_b = AR.alloc([128, 128], BF16)
        mlo_b = AR.alloc([128, 4, 128], BF16)
        mhi_b = AR.alloc([128, 4, 128], BF16)
        pa_b = AR.alloc([128, 128], BF16)
        pc_b = AR.alloc([128, 128], BF16)
        cstage = AR.alloc([128, 128], F32)
        S.dma("sp", lambda e: e.dma_start(out=ident_f, in_=T["cst"][CST_IDENT]), writes=["ident_f"])
        S.op("dve", lambda e: e.tensor_copy(out=ident_b, in_=ident_f), reads=["ident_f"], writes=["ident_b"])
        S.op("dve", lambda e: e.memset(ones_b, 1.0), writes=["ones_b"])
        S.op("dve", lambda e: e.memset(ones_f, 1.0), writes=["ones_f"])
        for ci, dst, rep in [(CST_BD64, bd64_b, 0), (CST_MLO, mlo_b, 4), (CST_MHI, mhi_b, 4), (CST_PA, pa_b, 0), (CST_PC, pc_b, 0)]:
            S.dma("sp", lambda e, ci=ci: e.dma_start(out=cstage, in_=T["cst"][ci]), writes=["cstage"])
            if rep:
                for r in range(rep):
                    S.op("dve", lambda e, dst=dst, r=r: e.tensor_copy(out=dst[:, r, :], in_=cstage), reads=["cstage"], writes=["cst_misc"])
            else:
                S.op("dve", lambda e, dst=dst: e.tensor_copy(out=dst, in_=cstage), reads=["cstage"], writes=["cst_misc"])
        S.barrier()
        base_mark = AR.mark()

        def mm(out, lhsT, rhs, start, stop, reads, wkey):
            S.op("pe", lambda e: e.matmul(out, lhsT, rhs, start=start, stop=stop), reads=reads, writes=[wkey], acc=True)

        tstage = AR.alloc([128, 128], F32)
        modAll = AR.alloc([128, 4, 2, 48], F32)
        bmodAll = AR.alloc([128, 4, 48], F32)
        cond2 = AR.alloc([128, 2, 8], F32)
        sT2 = AR.alloc([128, 8, 2], BF16)
        tst2 = AR.alloc([128, 128], F32)
        S.op("dve", lambda e: e.memset(tst2, 0.0), writes=["tst2z"])
        S.barrier()
        base_mark2 = [None]

        def small_load_T(dst, src_ap, pattern, key, **kw):
            n = dst.shape[1]
            rows = src_ap.rearrange("(c q) -> c q", q=128)
            S.dma("sp", lambda e: e.dma_start(out=tstage[0:n, :], in_=rows), writes=["tstage"])
            S.op("pe", lambda e: e.transpose(ps[:, 7, 0:n], tstage[0:n, :], ident_f[0:n, 0:n]), reads=["tstage", "ident_f"], writes=[PSK[7]])
            S.op("dve", lambda e: e.tensor_copy(out=dst, in_=ps[:, 7, 0:n]), reads=[PSK[7]], writes=[key])

        def small_load_col(dst, src_ap, dh, key):
            rep = 128 // dh
            for r in range(rep):
                S.dma("sp", lambda e, r=r: e.dma_start(out=tst2[0:1, r * dh:(r + 1) * dh], in_=src_ap.rearrange("(o q) -> o q", o=1)),
                      writes=[("tst2", r)], key=("tst2", r))
            S.op("pe", lambda e: e.transpose(ps[:, 7, 0:2], tst2[0:2, :], ident_f[0:2, 0:2]),
                 reads=[("tst2", r) for r in range(rep)] + ["tst2z", "ident_f"], writes=[PSK[7], ("tst2", 0), ("tst2", 1)])
            S.op("dve", lambda e: e.tensor_copy(out=dst, in_=ps[:, 7, 0:1]), reads=[PSK[7]], writes=[key])

        base_mark = AR.mark()

        def s5_tables():
            m = AR.mark()
            TWO_PI = 2.0 * math.pi
            lr = AR.alloc([128, 64], F32)
            li = AR.alloc([128, 64], F32)
            dtc = AR.alloc([128, 1], F32)
            xx = AR.alloc([128, 64], F32)
            th = AR.alloc([128, 64], F32)
            PWr = AR.alloc([128, 16, 64], F32)
            PWi = AR.alloc([128, 16, 64], F32)
            Br = AR.alloc([128, 64, 16], F32)
            Bi = AR.alloc([128, 64, 16], F32)
            Bbr = AR.alloc([128, 64, 16], F32)
            Bbi = AR.alloc([128, 64, 16], F32)
            Cr = AR.alloc([128, 16, 64], F32)
            Ci = AR.alloc([128, 16, 64], F32)
            w = [AR.alloc([128, 1024], F32) for _ in range(4)]
            s64 = [AR.alloc([128, 64], F32) for _ in range(8)]
            ki = AR.alloc([128, 64], I32)
            PMr = AR.alloc([128, 64], F32)
            PMi = AR.alloc([128, 64], F32)
            ST = [AR.alloc([128, 16384], BF16) for _ in range(2)]

            def dv(fn, reads, writes):
                S.op("dve", fn, reads=reads, writes=writes)

            S.dma("sp", lambda e: e.dma_start(out=lr, in_=T["l1_lam_re"].rearrange("d g p -> (d g) p")), writes=["lr"])
            S.dma("sp", lambda e: e.dma_start(out=li, in_=T["l1_lam_im"].rearrange("d g p -> (d g) p")), writes=["li"])
            S.dma("sp", lambda e: e.dma_start(out=dtc, in_=T["l1_log_dt"].rearrange("d (g o) -> (d g) o", o=1)), writes=["dtc"])
            S.dma("sp", lambda e: e.dma_start(out=Br, in_=T["l1_b_re"].rearrange("d g p c -> (d g) p c")), writes=["Br"])
            S.dma("sp", lambda e: e.dma_start(out=Bi, in_=T["l1_b_im"].rearrange("d g p c -> (d g) p c")), writes=["Bi"])
            S.dma("sp", lambda e: e.dma_start(out=Cr, in_=T["l1_c_re"].rearrange("d g c p -> (d g) c p")), writes=["Cr"])
            S.dma("sp", lambda e: e.dma_start(out=Ci, in_=T["l1_c_im"].rearrange("d g c p -> (d g) c p")), writes=["Ci"])
            S.op("act", lambda e: e.activation(out=dtc, in_=dtc, func=AF.Exp), reads=["dtc"], writes=["dtc"])
            dv(lambda e: e.tensor_scalar(out=xx, in0=lr, scalar1=dtc[:, 0:1], scalar2=None, op0=ALU.mult), ["lr", "dtc"], ["xx"])
            dv(lambda e: e.tensor_scalar(out=th, in0=li, scalar1=dtc[:, 0:1], scalar2=None, op0=ALU.mult), ["li", "dtc"], ["th"])

            def sinred(out, okey, shift):
                v, kf, r, mk = s64[0], s64[1], s64[2], s64[3]
                dv(lambda e: e.tensor_scalar(out=v, in0=th, scalar1=float(shift), scalar2=None, op0=ALU.add), ["th"], [("s64", 0)])
                dv(lambda e: e.tensor_scalar(out=kf, in0=v, scalar1=1.0 / TWO_PI, scalar2=None, op0=ALU.mult), [("s64", 0)], [("s64", 1)])
                dv(lambda e: e.tensor_copy(out=ki, in_=kf), [("s64", 1)], ["sr_ki"])
                dv(lambda e: e.tensor_copy(out=kf, in_=ki), ["sr_ki"], [("s64", 1)])
                dv(lambda e: e.scalar_tensor_tensor(out=r, in0=kf, scalar=-TWO_PI, in1=v, op0=ALU.mult, op1=ALU.add), [("s64", 1), ("s64", 0)], [("s64", 2)])
                dv(lambda e: e.tensor_scalar(out=mk, in0=r, scalar1=math.pi, scalar2=None, op0=ALU.is_gt), [("s64", 2)], [("s64", 3)])
                dv(lambda e: e.scalar_tensor_tensor(out=r, in0=mk, scalar=-TWO_PI, in1=r, op0=ALU.mult, op1=ALU.add), [("s64", 3), ("s64", 2)], [("s64", 2)])
                dv(lambda e: e.tensor_scalar(out=mk, in0=r, scalar1=-math.pi, scalar2=None, op0=ALU.is_lt), [("s64", 2)], [("s64", 3)])
                dv(lambda e: e.scalar_tensor_tensor(out=r, in0=mk, scalar=TWO_PI, in1=r, op0=ALU.mult, op1=ALU.add), [("s64", 3), ("s64", 2)], [("s64", 2)])
                dv(lambda e: e.tensor_scalar(out=r, in0=r, scalar1=3.1415925, scalar2=-3.1415925, op0=ALU.min, op1=ALU.max), [("s64", 2)], [("s64", 2)])
                S.op("act", lambda e: e.activation(out=out, in_=r, func=AF.Sin), reads=[("s64", 2)], writes=[okey])

            sn, cs, ex, exm = s64[4], s64[5], s64[6], s64[7]
            sinred(sn, ("s64", 4), 0.0)
            sinred(cs, ("s64", 5), math.pi / 2.0)
            S.op("act", lambda e: e.activation(out=ex, in_=xx, func=AF.Exp), reads=["xx"], writes=[("s64", 6)])
            S.op("act", lambda e: e.activation(out=exm, in_=xx, func=AF.Exp, scale=-1.0), reads=["xx"], writes=[("s64", 7)])
            PK = lambda j: ("PW", j)
            dv(lambda e: e.memset(PWr[:, 7, :], 1.0), [], [("PWr", 0)])
            dv(lambda e: e.memset(PWi[:, 7, :], 0.0), [], [("PWi", 0)])
            dv(lambda e: e.tensor_tensor(out=PWr[:, 8, :], in0=ex, in1=cs, op=ALU.mult), [("s64", 6), ("s64", 5)], [("PWr", 1)])
            dv(lambda e: e.tensor_tensor(out=PWi[:, 8, :], in0=ex, in1=sn, op=ALU.mult), [("s64", 6), ("s64", 4)], [("PWi", 1)])
            dv(lambda e: e.tensor_tensor(out=PWr[:, 6, :], in0=exm, in1=cs, op=ALU.mult), [("s64", 7), ("s64", 5)], [("PWr", -1)])
            dv(lambda e: e.scalar_tensor_tensor(out=PWi[:, 6, :], in0=exm, scalar=-1.0, in1=sn, op0=ALU.mult, op1=ALU.mult), [("s64", 7), ("s64", 4)], [("PWi", -1)])

            def cmul64(j_out, j_a, j_b):
                orr, oi = PWr[:, j_out + 7, :], PWi[:, j_out + 7, :]
                ar, ai = PWr[:, j_a + 7, :], PWi[:, j_a + 7, :]
                br, bi = PWr[:, j_b + 7, :], PWi[:, j_b + 7, :]
                ka = [("PWr", j_a), ("PWi", j_a), ("PWr", j_b), ("PWi", j_b)]
                t1, t2 = s64[0], s64[1]
                dv(lambda e: e.tensor_tensor(out=t1, in0=ar, in1=br, op=ALU.mult), ka, [("s64", 0)])
                dv(lambda e: e.tensor_tensor(out=t2, in0=ai, in1=bi, op=ALU.mult), ka, [("s64", 1)])
                dv(lambda e: e.tensor_tensor(out=orr, in0=t1, in1=t2, op=ALU.subtract), [("s64", 0), ("s64", 1)], [("PWr", j_out)])
                dv(lambda e: e.tensor_tensor(out=t1, in0=ar, in1=bi, op=ALU.mult), ka, [("s64", 0)])
                dv(lambda e: e.tensor_tensor(out=t2, in0=ai, in1=br, op=ALU.mult), ka, [("s64", 1)])
                dv(lambda e: e.tensor_tensor(out=oi, in0=t1, in1=t2, op=ALU.add), [("s64", 0), ("s64", 1)], [("PWi", j_out)])

            for j in range(2, 9):
                cmul64(j, j - 1, 1)
            for j in range(2, 8):
                cmul64(-j, -(j - 1), -1)
            S.dma("sp", lambda e: e.dma_start(out=T["TP8"][0], in_=PWr[:, 15, :]), reads=[("PWr", 8)], key="tp8a")
            S.dma("sp", lambda e: e.dma_start(out=T["TP8"][1], in_=PWi[:, 15, :]), reads=[("PWi", 8)], key="tp8b")
            nr, den, fr, fi = s64[0], s64[1], s64[2], s64[3]
            t5, t6 = s64[4], s64[5]
            a1 = [("PWr", 1), ("PWi", 1)]
            dv(lambda e: e.tensor_scalar(out=nr, in0=PWr[:, 8, :], scalar1=-1.0, scalar2=None, op0=ALU.add), a1, [("s64", 0)])
            dv(lambda e: e.tensor_tensor(out=den, in0=lr, in1=lr, op=ALU.mult), ["lr"], [("s64", 1)])
            dv(lambda e: e.tensor_tensor(out=t5, in0=li, in1=li, op=ALU.mult), ["li"], [("s64", 4)])
            dv(lambda e: e.tensor_tensor(out=den, in0=den, in1=t5, op=ALU.add), [("s64", 1), ("s64", 4)], [("s64", 1)])
            dv(lambda e: e.reciprocal(out=den, in_=den), [("s64", 1)], [("s64", 1)])
            dv(lambda e: e.tensor_tensor(out=fr, in0=nr, in1=lr, op=ALU.mult), [("s64", 0), "lr"], [("s64", 2)])
            dv(lambda e: e.tensor_tensor(out=t5, in0=PWi[:, 8, :], in1=li, op=ALU.mult), a1 + ["li"], [("s64", 4)])
            dv(lambda e: e.tensor_tensor(out=fr, in0=fr, in1=t5, op=ALU.add), [("s64", 2), ("s64", 4)], [("s64", 2)])
            dv(lambda e: e.tensor_tensor(out=fr, in0=fr, in1=den, op=ALU.mult), [("s64", 2), ("s64", 1)], [("s64", 2)])
            dv(lambda e: e.tensor_tensor(out=fi, in0=PWi[:, 8, :], in1=lr, op=ALU.mult), a1 + ["lr"], [("s64", 3)])
            dv(lambda e: e.tensor_tensor(out=t6, in0=nr, in1=li, op=ALU.mult), [("s64", 0), "li"], [("s64", 5)])
            dv(lambda e: e.tensor_tensor(out=fi, in0=fi, in1=t6, op=ALU.subtract), [("s64", 3), ("s64", 5)], [("s64", 3)])
            dv(lambda e: e.tensor_tensor(out=fi, in0=fi, in1=den, op=ALU.mult), [("s64", 3), ("s64", 1)], [("s64", 3)])

            def bc_b(a):
                return a.unsqueeze(2).broadcast_to([128, 64, 16])

            def bc_c(a):
                return a.unsqueeze(1).broadcast_to([128, 16, 64])

            w3b = [x.rearrange("q (p c) -> q p c", c=16) for x in w]
            w3c = [x.rearrange("q (c p) -> q c p", p=64) for x in w]

            def cmul_big(eng, pr, pi, pk, xr, xi, xk, bc, wv, out_r, out_i, okey, xr_n=None, xi_n=None):
                t1, t2, t3, t4 = wv
                tg = "cb" + eng
                op = lambda fn, r, w_: S.op(eng, fn, reads=r, writes=w_)
                op(lambda e: e.tensor_tensor(out=t1, in0=xr, in1=bc(pr), op=ALU.mult), pk + xk, [tg + "1"])
                op(lambda e: e.tensor_tensor(out=t2, in0=xi, in1=bc(pi), op=ALU.mult), pk + xk, [tg + "2"])
                op(lambda e: e.tensor_tensor(out=out_r, in0=t1, in1=t2, op=ALU.subtract), [tg + "1", tg + "2"], [okey])
                a_i = xi if xi_n is None else xi_n
                a_r = xr if xr_n is None else xr_n
                op(lambda e: e.tensor_tensor(out=t3, in0=a_i, in1=bc(pr), op=ALU.mult), pk + xk, [tg + "3"])
                op(lambda e: e.tensor_tensor(out=t4, in0=a_r, in1=bc(pi), op=ALU.mult), pk + xk, [tg + "4"])
                op(lambda e: e.tensor_tensor(out=out_i, in0=t3, in1=t4, op=ALU.add), [tg + "3", tg + "4"], [okey])

            cmul_big("dve", fr, fi, [("s64", 2), ("s64", 3)], Br, Bi, ["Br", "Bi"], bc_b, w3b, Bbr, Bbi, "Bb")
            nCr = Br.rearrange("q p c -> q (p c)").rearrange("q (c p) -> q c p", p=64)
            nCi = Bi.rearrange("q p c -> q (p c)").rearrange("q (c p) -> q c p", p=64)
            dv(lambda e: e.tensor_scalar(out=nCr, in0=Cr, scalar1=-1.0, scalar2=None, op0=ALU.mult), ["Cr", "Bb"], ["Br"])
            dv(lambda e: e.tensor_scalar(out=nCi, in0=Ci, scalar1=-1.0, scalar2=None, op0=ALU.mult), ["Ci", "Bb"], ["Bi"])
            PM = {"dve": (PMr, PMi), "pool": (AR.alloc([128, 64], F32), AR.alloc([128, 64], F32))}
            wp = [AR.alloc([128, 1024], F32) for _ in range(4)]
            WV = {"dve": (w3b, w3c),
                  "pool": ([x.rearrange("q (p c) -> q p c", c=16) for x in wp], [x.rearrange("q (c p) -> q c p", p=64) for x in wp])}

            def mixed_power(eng, jf, jb):
                pmr, pmi = PM[eng]
                k = "PM" + eng
                S.op("act", lambda e: e.activation(out=pmr[0:64], in_=PWr[0:64, jf + 7, :], func=AF.Copy), reads=[("PWr", jf)], writes=[k])
                S.op("act", lambda e: e.activation(out=pmi[0:64], in_=PWi[0:64, jf + 7, :], func=AF.Copy), reads=[("PWi", jf)], writes=[k])
                S.op("act", lambda e: e.activation(out=pmr[64:128], in_=PWr[64:128, jb + 7, :], func=AF.Copy), reads=[("PWr", jb)], writes=[k])
                S.op("act", lambda e: e.activation(out=pmi[64:128], in_=PWi[64:128, jb + 7, :], func=AF.Copy), reads=[("PWi", jb)], writes=[k])

            specs = [("B", lambda t: -t, lambda t: t, "TX", "ript", "dve"),
                     ("C", lambda t: t, lambda t: -t, "TZ", "ript", "dve"),
                     ("B", lambda t: 7 - t, lambda t: t, "TS", "ritp", "dve"),
                     ("C", lambda t: t + 1, lambda t: 8 - t, "TE", "ript", "dve")]
            for ti, (side, pf, pb_, dst, layout, eng) in enumerate(specs):
                st_ = ST[ti % 2]
                stk = ("ST", ti % 2)
                pmr, pmi = PM[eng]
                pmk = ["PM" + eng]
                if layout == "ript":
                    st5 = st_.rearrange("q (r p t c) -> q r p t c", r=2, p=64, t=8)
                else:
                    st5 = st_.rearrange("q (r t c p) -> q r t c p", r=2, t=8, c=16)
                for t in range(8):
                    mixed_power(eng, pf(t), pb_(t))
                    if side == "B":
                        if layout == "ript":
                            o_r, o_i = st5[:, 0, :, t, :], st5[:, 1, :, t, :]
                        else:
                            o_r = st5[:, 0, t, :, :].rearrange("q c p -> q p c")
                            o_i = st5[:, 1, t, :, :].rearrange("q c p -> q p c")
                        cmul_big(eng, pmr, pmi, pmk, Bbr, Bbi, ["Bb"], bc_b, WV[eng][0], o_r, o_i, stk)
                    else:
                        o_r = st5[:, 0, :, t, :].rearrange("q p c -> q c p")
                        o_i = st5[:, 1, :, t, :].rearrange("q p c -> q c p")
                        cmul_big(eng, pmr, pmi, pmk, Cr, Ci, ["Cr", "Ci", "Br", "Bi"], bc_c, WV[eng][1], o_r, o_i, stk, xr_n=nCr, xi_n=nCi)
                if dst in ("TX", "TZ"):
                    S.dma("sp", lambda e, st_=st_, dst=dst: e.dma_start(out=T[dst], in_=st_), reads=[stk], writes=[dst], key=stk)
                elif dst == "TE":
                    st4 = st_.rearrange("q (r p n) -> q r p n", r=2, p=64)
                    for d in range(2):
                        for r_ in range(2):
                            for p4 in range(4):
                                S.dma("sp", lambda e, d=d, r_=r_, p4=p4, st4=st4: e.dma_start(
                                    out=T["TE"][r_, d * 64 + p4 * 16:d * 64 + (p4 + 1) * 16].rearrange("k g n -> g k n"),
                                    in_=st4[d * 64:(d + 1) * 64, r_, p4 * 16:(p4 + 1) * 16, :]), reads=[stk], writes=["TE"], key=stk)
                else:
                    st4 = st_.rearrange("q (r n p) -> q r n p", r=2, p=64)
                    for d in range(2):
                        for r_ in range(2):
                            for k8 in range(8):
                                S.dma("sp", lambda e, d=d, r_=r_, k8=k8, st4=st4: e.dma_start(
                                    out=T["TS"][r_, k8 * 16:(k8 + 1) * 16, :, d * 64:(d + 1) * 64].rearrange("k g p -> g k p"),
                                    in_=st4[d * 64:(d + 1) * 64, r_, k8 * 16:(k8 + 1) * 16, :]), reads=[stk], writes=["TS"], key=stk)
            S.barrier()
            AR.release(m)
            m = AR.mark()
            mkf = AR.alloc([128, 128], F32)
            mkb = AR.alloc([128, 128], F32)
            r16 = AR.alloc([128, 128], F32)
            dgc = AR.alloc([128, 16], F32)
            dmat = AR.alloc([128, 64], F32)
            dcol = AR.alloc([128, 64], F32)
            S.dma("sp", lambda e: e.dma_start(out=mkf, in_=T["cst"][CST_S5F]), writes=["mkf"])
            S.dma("sp", lambda e: e.dma_start(out=mkb, in_=T["cst"][CST_S5B]), writes=["mkb"])
            S.dma("sp", lambda e: e.dma_start(out=r16, in_=T["cst"][CST_R16]), writes=["r16"])
            S.dma("sp", lambda e: e.dma_start(out=dgc[0:64, :], in_=T["l1_d_skip"].rearrange("(g c) -> g c", c=16)), writes=["dgc"])
            S.op("pe", lambda e: e.transpose(ps[0:16, 7, 0:64], dgc[0:64, :], ident_f[0:64, 0:64]), reads=["dgc", "ident_f"], writes=[PSK[7]])
            S.op("dve", lambda e: e.tensor_copy(out=dmat[0:16, :], in_=ps[0:16, 7, 0:64]), reads=[PSK[7]], writes=["dmat"])
            S.op("pe", lambda e: e.matmul(ps[:, 7, 64:128], r16[0:16, :], dmat[0:16, :], start=True, stop=True), reads=["r16", "dmat"], writes=[PSK[7]])
            S.op("dve", lambda e: e.tensor_copy(out=dcol, in_=ps[:, 7, 64:128]), reads=[PSK[7]], writes=["dcol"])
            XL = [[AR.alloc([128, 16, 128], BF16) for _ in range(2)] for _ in range(2)]
            ZL = [[AR.alloc([128, 16, 128], BF16) for _ in range(2)] for _ in range(2)]
            Mst = [AR.alloc([128, 16, 128], BF16) for _ in range(2)]
            tA = [AR.alloc([128, 128], F32) for _ in range(2)]
            tB = [AR.alloc([128, 128], F32) for _ in range(2)]
            TX4 = T["TX"].rearrange("(d g) (k n) -> d g k n", d=2, k=128)
            TZ4 = T["TZ"].rearrange("(d g) (k n) -> d g k n", d=2, k=128)
            for gb in range(4):
                bf = gb % 2
                for d in range(2):
                    S.dma("sp", lambda e, d=d, gb=gb, bf=bf: e.dma_start(out=XL[bf][d], in_=TX4[d, gb * 16:(gb + 1) * 16].rearrange("g k n -> k g n")),
                          reads=["TX"], writes=[("XL", bf, d)])
                    S.dma("sp", lambda e, d=d, gb=gb, bf=bf: e.dma_start(out=ZL[bf][d], in_=TZ4[d, gb * 16:(gb + 1) * 16].rearrange("g k n -> k g n")),
                          reads=["TZ"], writes=[("ZL", bf, d)])
                for gi in range(16):
                    g = gb * 16 + gi
                    b = gi % 4
                    for d in range(2):
                        S.op("pe", lambda e, b=b, d=d, bf=bf, gi=gi: e.matmul(ps[:, b, d * 128:(d + 1) * 128], XL[bf][d][:, gi, :], ZL[bf][d][:, gi, :],
                                                                            start=True, stop=True),
                             reads=[("XL", bf, d), ("ZL", bf, d)], writes=[PSK[b]], acc=True)
                    a_, b_ = tA[gi % 2], tB[gi % 2]
                    S.op("dve", lambda e, a_=a_, b=b: e.tensor_tensor(out=a_, in0=ps[:, b, 0:128], in1=mkf, op=ALU.mult), reads=[PSK[b], "mkf"], writes=[("tA", gi % 2)])
                    S.op("dve", lambda e, b_=b_, b=b: e.tensor_tensor(out=b_, in0=ps[:, b, 128:256], in1=mkb, op=ALU.mult), reads=[PSK[b], "mkb"], writes=[("tB", gi % 2)])
                    S.op("dve", lambda e, a_=a_, b_=b_: e.tensor_tensor(out=a_, in0=a_, in1=b_, op=ALU.add), reads=[("tA", gi % 2), ("tB", gi % 2)], writes=[("tA", gi % 2)])
                    S.op("dve", lambda e, a_=a_, g=g, gi=gi, bf=bf: e.scalar_tensor_tensor(out=Mst[bf][:, gi, :], in0=ident_f, scalar=dcol[:, g:g + 1], in1=a_,
                                                                                         op0=ALU.mult, op1=ALU.add),
                         reads=[("tA", gi % 2), "dcol", "ident_f"], writes=[("Mst", bf)])
                S.dma("sp", lambda e, gb=gb, bf=bf: e.dma_start(out=T["TM"][:, gb * 16:(gb + 1) * 16, :], in_=Mst[bf]),
                      reads=[("Mst", bf)], writes=["TM"], key=("Mst", bf))
            S.barrier()
            AR.release(m)

        def prologue_mod_start():
            for c_ in range(2):
                small_load_T(cond2[:, c_, :], T["cond"][c_], "(kc q) -> q kc", ("cond2", c_), q=128)
            S.op("act", lambda e: e.activation(out=sT2.rearrange("q k c -> q c k"), in_=cond2, func=AF.Silu),
                 reads=[("cond2", 0), ("cond2", 1)], writes=["sT"])
            wm = [AR.alloc([128, 8, 512], BF16) for _ in range(2)]
            bi_ = 0
            for l in range(NLAYERS):
                for blk in range(12):
                    w = wm[bi_ % 2]
                    wk = ("wm", bi_ % 2)
                    bi_ += 1
                    S.dma("pool", lambda e, w=w, blk=blk, l=l: e.dma_start(
                        out=w, in_=T["w_mod"][l][:, blk * 512:(blk + 1) * 512].rearrange("(kc q) n -> q kc n", q=128)), writes=[wk])
                    for cc in range(4):
                        col = l * 96 + 2 * (blk * 4 + cc)
                        for kc in range(8):
                            mm(ps[:, 6, col:col + 2], w[:, kc, cc * 128:(cc + 1) * 128], sT2[:, kc, :], kc == 0, kc == 7, [wk, "sT"], PSK[6])

        def prologue_mod_finish():
            for l in range(NLAYERS):
                small_load_T(bmodAll[:, l, :], T["b_mod"][l], "(c q) -> q c", ("bmodAll", l), q=128)
                pv = ps[:, 6, l * 96:(l + 1) * 96].rearrange("q (n c) -> q c n", c=2)
                for c_ in range(2):
                    S.op("dve", lambda e, c_=c_, l=l, pv=pv: e.tensor_tensor(out=modAll[:, l, c_, :], in0=pv[:, c_, :], in1=bmodAll[:, l, :], op=ALU.add),
                         reads=[PSK[6], ("bmodAll", l)], writes=[("modAll", l)])
            S.barrier()

        def run_path(p):
            nseq, slen = (4, 256) if p == 0 else (1, 1024)
            AR.release(base_mark)
            hT = AR.alloc([128, 8, NT], F32)
            xnT = AR.alloc([128, 8, NT], BF16)
            modT = AR.alloc([128, 48], F32)
            bmodT = AR.alloc([128, 48], F32)
            n1T = AR.alloc([128, 8], F32)
            n2T = AR.alloc([128, 8], F32)
            A1 = AR.alloc([128, 8], F32)
            A2 = AR.alloc([128, 8], F32)
            condT = AR.alloc([128, 8], F32)
            sT = AR.alloc([128, 8], BF16)
            rstd = AR.alloc([128, NT], F32)
            lnv = AR.alloc([128, 512], F32)
            path_mark = AR.mark()
            HK = [("hT", kc) for kc in range(8)]
            XK = [("xnT", kc) for kc in range(8)]

            m0 = AR.mark()
            xst = [AR.alloc([128, D], F32) for _ in range(8)]
            for tt in range(8):
                xs = xst[tt]
                xk = ("xst", tt)
                S.dma("sp", lambda e, xs=xs, tt=tt: e.dma_start(out=xs, in_=T["xin"][p, tt * 128:(tt + 1) * 128, :]), writes=[xk])
                for hf in range(2):
                    b = (tt * 2 + hf) % 8
                    for q in range(4):
                        kc = hf * 4 + q
                        S.op("pe", lambda e, b=b, q=q, kc=kc, xs=xs: e.transpose(ps[:, b, q * 128:(q + 1) * 128], xs[:, kc * 128:(kc + 1) * 128], ident_f),
                             reads=[xk, "ident_f"], writes=[PSK[b]], acc=True)
                    S.op("act" if hf else "dve",
                         (lambda e, b=b, hf=hf, tt=tt: e.activation(out=hT[:, hf * 4:hf * 4 + 4, tt * 128:(tt + 1) * 128],
                                                                     in_=ps[:, b, :].rearrange("p (a b) -> p a b", a=4), func=AF.Copy)) if hf else
                         (lambda e, b=b, hf=hf, tt=tt: e.tensor_copy(out=hT[:, hf * 4:hf * 4 + 4, tt * 128:(tt + 1) * 128],
                                                                      in_=ps[:, b, :].rearrange("p (a b) -> p a b", a=4))),
                         reads=[PSK[b]], writes=HK[hf * 4:hf * 4 + 4])
            S.barrier()
            AR.release(m0)
            def compute_mod(l):
                m = AR.mark()
                small_load_T(n1T, T["norm1"][l], "(kc q) -> q kc", "n1T", q=128)
                small_load_T(n2T, T["norm2"][l], "(kc q) -> q kc", "n2T", q=128)
                S.op("dve", lambda e: e.tensor_copy(out=modT, in_=modAll[:, l, p, :]), reads=[("modAll", l)], writes=["modT"])
                S.op("dve", lambda e: e.scalar_tensor_tensor(out=A1, in0=modT[:, 8:16], scalar=1.0, in1=n1T, op0=ALU.add, op1=ALU.mult),
                     reads=["modT", "n1T"], writes=["A1"])
                S.op("dve", lambda e: e.scalar_tensor_tensor(out=A2, in0=modT[:, 32:40], scalar=1.0, in1=n2T, op0=ALU.add, op1=ALU.mult),
                     reads=["modT", "n2T"], writes=["A2"])
                AR.release(m)

            def norm_mod(Acol, Akey, Boff, perm8=False, final_barrier=True):
                m = AR.mark()
                sq = AR.alloc([128, 8, NT], BF16)
                tmp = [AR.alloc([128, NT], F32) for _ in range(2)]
                S.op("act", lambda e: e.activation(out=sq, in_=hT, func=AF.Square), reads=HK, writes=["sq"])
                for hf in range(2):
                    b = hf
                    for kc in range(8):
                        mm(ps[:, b, :], ones_b, sq[:, kc, hf * 512:(hf + 1) * 512], kc == 0, kc == 7, ["sq", "ones_b"], PSK[b])
                    S.op("act", lambda e, b=b: e.activation(out=lnv, in_=ps[:, b, :], func=AF.Ln, bias=EPS, scale=1.0 / D),
                         reads=[PSK[b]], writes=["lnv"])
                    S.op("act", lambda e, hf=hf: e.activation(out=rstd[:, hf * 512:(hf + 1) * 512], in_=lnv, func=AF.Exp, scale=-0.5),
                         reads=["lnv"], writes=[("rstd", hf)])
                for kc in range(8):
                    t = tmp[kc % 2]
                    tk = ("nm_tmp", kc % 2)
                    S.op("dve", lambda e, t=t, kc=kc: e.scalar_tensor_tensor(out=t, in0=hT[:, kc, :], scalar=Acol[:, kc:kc + 1], in1=rstd,
                                                                              op0=ALU.mult, op1=ALU.mult),
                         reads=[HK[kc], Akey, ("rstd", 0), ("rstd", 1)], writes=[tk])
                    if perm8:
                        o = xnT[:, kc, :].rearrange("p (t k) -> p k t", t=8)
                    else:
                        o = xnT[:, kc, :]
                    S.op("act", lambda e, t=t, kc=kc, o=o: e.activation(out=o, in_=t, func=AF.Identity, bias=modT[:, Boff + kc:Boff + kc + 1]),
                         reads=[tk, "modT"], writes=[XK[kc]])
                if final_barrier:
                    S.barrier()
                AR.release(m)

            def resid(oc, hf, b, goff):
                S.op("dve", lambda e: e.scalar_tensor_tensor(out=hT[:, oc, hf * 512:(hf + 1) * 512], in0=ps[:, b, :],
                                                              scalar=modT[:, goff + oc:goff + oc + 1],
                                                              in1=hT[:, oc, hf * 512:(hf + 1) * 512], op0=ALU.mult, op1=ALU.add),
                     reads=[PSK[b], "modT", HK[oc]], writes=[HK[oc]])

            def ffn(l):
                m = AR.mark()
                wd = AR.alloc([128, 22, D], BF16)
                wu = [AR.alloc([128, 2, 8, 256], BF16) for _ in range(2)]
                for j4 in range(0, 22, 6):
                    je = min(22, j4 + 6)
                    S.dma("pool", lambda e, j4=j4, je=je: e.dma_start(
                        out=wd[:, j4:je, :], in_=T["w_down"][l][j4 * 128:je * 128, :].rearrange("(j q) n -> q j n", q=128)),
                        writes=[("wd", j4)])

                def wu_dma(jb):
                    w = wu[jb % 2]
                    wk = ("wu", jb % 2)
                    for gv in range(2):
                        c0 = gv * DFF + jb * 256
                        S.dma("pool", lambda e, w=w, gv=gv, c0=c0: e.dma_start(
                            out=w[:, gv], in_=T["w_up"][l][:, c0:c0 + 256].rearrange("(kc q) n -> q kc n", q=128)),
                            writes=[(wk, gv)], key=(wk, gv))

                wu_dma(0)
                wu_dma(1)
                norm_mod(A2, "A2", 24, final_barrier=False)
                gT = AR.alloc([128, 22, NT], BF16)
                bupT = AR.alloc([128, 44], F32)
                cbT = AR.alloc([128, 44], F32)
                ckT = AR.alloc([128, 3, 44], F32)
                tc = [AR.alloc([128, nseq, slen], F32) for _ in range(2)]
                sg = AR.alloc([128, nseq, slen], F32)
                beff = AR.alloc([128, 44], F32)
                e0 = AR.alloc([128, 44], F32)
                e2 = AR.alloc([128, 44], F32)
                small_load_T(bupT, T["b_up"][l], "(c q) -> q c", "bupT", q=128)
                small_load_T(cbT, T["conv_b"][l], "(c q) -> q c", "cbT", q=128)
                for t3 in range(3):
                    small_load_T(ckT[:, t3, :], T["conv_k"][l, t3], "(c q) -> q c", ("ckT", t3), q=128)
                CK = [("ckT", t3) for t3 in range(3)]
                S.op("dve", lambda e: e.tensor_tensor(out=beff, in0=ckT[:, 0, :], in1=ckT[:, 1, :], op=ALU.add), reads=CK, writes=["beff"])
                S.op("dve", lambda e: e.tensor_tensor(out=beff, in0=beff, in1=ckT[:, 2, :], op=ALU.add), reads=CK + ["beff"], writes=["beff"])
                S.op("dve", lambda e: e.tensor_tensor(out=beff, in0=beff, in1=bupT, op=ALU.mult), reads=["beff", "bupT"], writes=["beff"])
                S.op("dve", lambda e: e.tensor_tensor(out=beff, in0=beff, in1=cbT, op=ALU.add), reads=["beff", "cbT"], writes=["beff"])
                S.op("dve", lambda e: e.scalar_tensor_tensor(out=e0, in0=bupT, scalar=-1.0, in1=ckT[:, 0, :], op0=ALU.mult, op1=ALU.mult), reads=["bupT"] + CK, writes=["e0"])
                S.op("dve", lambda e: e.scalar_tensor_tensor(out=e2, in0=bupT, scalar=-1.0, in1=ckT[:, 2, :], op0=ALU.mult, op1=ALU.mult), reads=["bupT"] + CK, writes=["e2"])
                WDK = [("wd", j4) for j4 in range(0, 22, 6)]
                for jb in range(11):
                    w = wu[jb % 2]
                    wk = ("wu", jb % 2)
                    if jb >= 2:
                        wu_dma(jb)
                    for sub in range(2):
                        j = jb * 2 + sub
                        bs = (j % 2) * 4
                        for gv in range(2):
                            for hf in range(2):
                                b = bs + gv * 2 + hf
                                for kc in range(8):
                                    mm(ps[:, b, :], w[:, gv, kc, sub * 128:(sub + 1) * 128], xnT[:, kc, hf * 512:(hf + 1) * 512],
                                       kc == 0, kc == 7, [(wk, gv), XK[kc]], PSK[b])
                        for gv in range(2):
                            b = bs + gv * 2
                            col = gv * 22 + j
                            t_ = tc[gv]
                            tk = ("tc", gv)
                            pin = ps[:, b:b + 2, :].rearrange("p b (s t) -> p (b s) t", t=slen) if nseq == 4 else \
                                ps[:, b:b + 2, :].rearrange("p (o b) t -> p o (b t)", o=1)
                            PB = [PSK[b], PSK[b + 1]]
                            S.op("act", lambda e, t_=t_, pin=pin, col=col: e.activation(out=t_, in_=pin, func=AF.Identity,
                                                                                      bias=beff[:, col:col + 1], scale=ckT[:, 1, col:col + 1]),
                                 reads=PB + ["beff", CK[1]], writes=[tk])
                            S.op("dve", lambda e, t_=t_, pin=pin, col=col: e.scalar_tensor_tensor(out=t_[:, :, 1:slen], in0=pin[:, :, 0:slen - 1],
                                                                                                scalar=ckT[:, 0, col:col + 1], in1=t_[:, :, 1:slen],
                                                                                                op0=ALU.mult, op1=ALU.add),
                                 reads=PB + [CK[0], tk], writes=[tk])
                            S.op("dve", lambda e, t_=t_, pin=pin, col=col: e.scalar_tensor_tensor(out=t_[:, :, 0:slen - 1], in0=pin[:, :, 1:slen],
                                                                                                scalar=ckT[:, 2, col:col + 1], in1=t_[:, :, 0:slen - 1],
                                                                                                op0=ALU.mult, op1=ALU.add),
                                 reads=PB + [CK[2], tk], writes=[tk])
                            S.op("dve", lambda e, t_=t_, col=col: e.tensor_scalar(out=t_[:, :, 0], in0=t_[:, :, 0], scalar1=e0[:, col:col + 1], scalar2=None, op0=ALU.add),
                                 reads=[tk, "e0"], writes=[tk])
                            S.op("dve", lambda e, t_=t_, col=col: e.tensor_scalar(out=t_[:, :, slen - 1], in0=t_[:, :, slen - 1], scalar1=e2[:, col:col + 1], scalar2=None,
                                                                                  op0=ALU.add),
                                 reads=[tk, "e2"], writes=[tk])
                        S.op("act", lambda e: e.activation(out=sg, in_=tc[0], func=AF.Silu), reads=[("tc", 0)], writes=["sg"])
                        go = gT[:, j, :].rearrange("p (s t) -> p s t", s=nseq)
                        S.op("dve", lambda e, go=go: e.tensor_tensor(out=go, in0=sg, in1=tc[1], op=ALU.mult),
                             reads=["sg", ("tc", 1)], writes=[("gT", j)])
                bi = 0
                for oc in range(8):
                    for hf in range(2):
                        b = bi % 8
                        bi += 1
                        for j in range(22):
                            mm(ps[:, b, :], wd[:, j, oc * 128:(oc + 1) * 128], gT[:, j, hf * 512:(hf + 1) * 512], j == 0, j == 21,
                               [WDK[j // 6], ("gT", j)], PSK[b])
                        resid(oc, hf, b, 40)
                AR.release(m)

            def attention(l):
                L = LAYERS[l]
                dh, H, ckv = L["dh"], L["H"], L["ckv"]
                G = H // 4
                nkc = ckv // 128
                ctot = D + 2 * ckv
                scale = dh ** -0.5
                rope = (p == 1)
                m = AR.mark()
                wbig = AR.alloc([128, 8, ctot], BF16)
                wo_b = AR.alloc([128, 8, D], BF16)
                S.dma("pool", lambda e: e.dma_start(out=wbig[:, :, 0:768], in_=T[L["wqkv"]][:, 0:768].rearrange("(kc q) n -> q kc n", q=128)),
                      writes=[("wbig", 0)])
                S.dma("pool", lambda e: e.dma_start(out=wbig[:, :, 768:ctot], in_=T[L["wqkv"]][:, 768:ctot].rearrange("(kc q) n -> q kc n", q=128)),
                      writes=[("wbig", 1)])
                S.dma("pool", lambda e: e.dma_start(out=wo_b, in_=T[L["wo"]].rearrange("(kc q) n -> q kc n", q=128)), writes=["wo_b"])
                norm_mod(A1, "A1", 0, final_barrier=False)
                qT = AR.alloc([128, 8, NT], BF16)
                kT = AR.alloc([128, nkc, NT], BF16)
                vx = AR.alloc([128, 8, 4, dh + 2], BF16)
                qn_c = AR.alloc([128, 1], F32)
                kn_c = AR.alloc([128, 1], F32)
                esink = AR.alloc([128, 16], F32) if L["sink"] else None
                if p == 1:
                    kcT = AR.alloc([128, nkc, 512], BF16)
                    vcx = AR.alloc([128, 4, 4, dh + 2], BF16)
                tmp_mark = AR.mark()
                sqb = [AR.alloc([128, 512], BF16) for _ in range(2)]
                rs = [AR.alloc([128, 512], F32) for _ in range(2)]
                qf = [AR.alloc([128, 512], F32) for _ in range(2)]
                WB = [("wbig", 0), ("wbig", 1)]
                small_load_col(qn_c, T[L["qn"]], dh, "qn_c")
                small_load_col(kn_c, T[L["kn"]], dh, "kn_c")
                QNK = ["qn_c"]
                KNK = ["kn_c"]
                if rope:
                    cosT = AR.alloc([128, NT], F32)
                    sinT = AR.alloc([128, NT], F32)
                    ri = 0 if dh == 64 else 2
                    S.dma("sp", lambda e: e.dma_start(out=cosT, in_=T["rope"][ri]), writes=["cosT"])
                    S.dma("sp", lambda e: e.dma_start(out=sinT, in_=T["rope"][ri + 1]), writes=["sinT"])
                    pm = pa_b if dh == 64 else pc_b
                    r1 = [AR.alloc([128, 512], F32) for _ in range(2)]
                    qb16 = [AR.alloc([128, 512], BF16) for _ in range(2)]
                if L["sink"]:
                    S.dma("sp", lambda e: e.dma_start(out=tst2[0:1, 0:16], in_=T[L["sink"]].rearrange("(o q) -> o q", o=1)),
                          writes=[("tst2", 0)], key=("tst2", 0))
                    S.op("pe", lambda e: e.matmul(ps[:, 7, 0:16], ones_f[0:2, :], tst2[0:2, 0:16], start=True, stop=True),
                         reads=[("tst2", 0), "tst2z", "ones_f"], writes=[PSK[7], ("tst2", 0)])
                    S.op("act", lambda e: e.activation(out=esink, in_=ps[:, 7, 0:16], func=AF.Exp), reads=[PSK[7]], writes=["esink"])
                if p == 0:
                    kst = [AR.alloc([128, ckv], F32) for _ in range(1)]
                    vst = [AR.alloc([128, ckv], F32) for _ in range(1)]
                    kf32 = AR.alloc([128, nkc, NT], F32)
                S.op("pool", lambda e: e.memset(vx, 1.0), reads=XK, writes=["vx_init"])

                if ATT_STOP <= 0:
                    AR.release(m)
                    return
                for tt in range(8):
                    b = 4 + tt % 2
                    for kc in range(8):
                        mm(ps[:, b, 0:ckv], xnT[:, kc, tt * 128:(tt + 1) * 128], wbig[:, kc, D + ckv:D + 2 * ckv], kc == 0, kc == 7,
                           [XK[kc], WB[1]], PSK[b])
                    if p == 0:
                        v_ = vst[0]
                        vk = ("vst", 0)
                        S.op("dve", lambda e, b=b, v_=v_: e.tensor_copy(out=v_, in_=ps[:, b, 0:ckv]), reads=[PSK[b]], writes=[vk])
                        S.op("act", lambda e, v_=v_, tt=tt: e.activation(out=vx[:, tt, :, 0:dh], in_=v_.rearrange("p (g d) -> p g d", g=4),
                                                                        func=AF.Copy),
                             reads=[vk, "vx_init"], writes=[("vx", tt)])
                        S.dma("sp", lambda e, v_=v_, tt=tt: e.dma_start(out=T[L["nv"]][tt * 128:(tt + 1) * 128, :], in_=v_), reads=[vk])
                    else:
                        S.op("act", lambda e, b=b, tt=tt: e.activation(out=vx[:, tt, :, 0:dh], in_=ps[:, b, 0:ckv].rearrange("p (g d) -> p g d", g=4),
                                                                       func=AF.Copy),
                             reads=[PSK[b], "vx_init"], writes=[("vx", tt)])
                if ATT_STOP <= 1:
                    AR.release(m)
                    return
                def lhs_cols(kind, c):
                    if kind == "q":
                        if dh == 128:
                            return [(c * 128, 128, 0)]
                        mq, i = c // 4, c % 4
                        return [((8 * mq + i) * 64, 64, 0), ((8 * mq + 4 + i) * 64, 64, 64)]
                    return [(D + c * 128, 128, 0)]

                blocks = [(kind, c, hf) for kind, nch in (("k", nkc), ("q", 8)) for c in range(nch) for hf in range(2)]
                nb_ = len(blocks)
                sq3 = sqb + [AR.alloc([128, 512], BF16)]

                def stage_a(i):
                    kind, c, hf = blocks[i]
                    b = i % 3
                    sl = slice(hf * 512, (hf + 1) * 512)
                    for (c0, w_, po) in lhs_cols(kind, c):
                        for kc in range(8):
                            mm(ps[po:po + w_, b, :], wbig[:, kc, c0:c0 + w_], xnT[:, kc, sl], kc == 0, kc == 7,
                               [XK[kc], WB[0], WB[1]], PSK[b])
                    sq_ = sq3[i % 3]
                    S.op("act", lambda e, sq_=sq_, b=b: e.activation(out=sq_, in_=ps[:, b, :], func=AF.Square), reads=[PSK[b]], writes=[("sqb", i % 3)])

                def stage_b(i):
                    kind, c, hf = blocks[i]
                    b = i % 3
                    b2 = 3 + i % 2
                    sl = slice(hf * 512, (hf + 1) * 512)
                    sq_ = sq3[i % 3]
                    mm(ps[:, b2, :], bd64_b if dh == 64 else ones_b, sq_, True, True, [("sqb", i % 3), "cst_misc", "ones_b"], PSK[b2])
                    rs_ = rs[i % 2]
                    rk = ("rs", i % 2)
                    S.op("act", lambda e, b2=b2: e.activation(out=lnv, in_=ps[:, b2, :], func=AF.Ln, bias=EPS, scale=1.0 / dh),
                         reads=[PSK[b2]], writes=["lnv"])
                    S.op("act", lambda e, rs_=rs_: e.activation(out=rs_, in_=lnv, func=AF.Exp, scale=-0.5), reads=["lnv"], writes=[rk])
                    gcol, gk = (qn_c, QNK) if kind == "q" else (kn_c, KNK)
                    dstT = qT if kind == "q" else kT
                    dk = (kind + "T", c, hf)
                    if not rope:
                        if kind == "k":
                            S.op("dve", lambda e, c=c, sl=sl, b=b, rs_=rs_: e.scalar_tensor_tensor(
                                out=kf32[:, c, sl], in0=ps[:, b, :], scalar=kn_c[:, 0:1], in1=rs_, op0=ALU.mult, op1=ALU.mult),
                                reads=[PSK[b], rk] + gk, writes=[("kf32", c, hf)])
                            S.op("pool", lambda e, c=c, sl=sl: e.tensor_copy(out=kT[:, c, sl], in_=kf32[:, c, sl]),
                                 reads=[("kf32", c, hf)], writes=[dk])
                        else:
                            S.op("dve", lambda e, c=c, sl=sl, b=b, rs_=rs_, gcol=gcol, dstT=dstT: e.scalar_tensor_tensor(
                                out=dstT[:, c, sl], in0=ps[:, b, :], scalar=gcol[:, 0:1], in1=rs_, op0=ALU.mult, op1=ALU.mult),
                                reads=[PSK[b], rk] + gk, writes=[dk])
                    else:
                        q_, qk_ = qf[i % 2], ("qf", i % 2)
                        qb_, qbk = qb16[i % 2], ("qb16", i % 2)
                        S.op("dve", lambda e, q_=q_, b=b, rs_=rs_, gcol=gcol: e.scalar_tensor_tensor(
                            out=q_, in0=ps[:, b, :], scalar=gcol[:, 0:1], in1=rs_, op0=ALU.mult, op1=ALU.mult),
                            reads=[PSK[b], rk] + gk, writes=[qk_])
                        S.op("act", lambda e, q_=q_, qb_=qb_: e.activation(out=qb_, in_=q_, func=AF.Copy), reads=[qk_], writes=[qbk])

                def stage_c(i):
                    kind, c, hf = blocks[i]
                    b3 = 5 + i % 3
                    sl = slice(hf * 512, (hf + 1) * 512)
                    dstT = qT if kind == "q" else kT
                    dk = (kind + "T", c, hf)
                    q_, qk_ = qf[i % 2], ("qf", i % 2)
                    qb_, qbk = qb16[i % 2], ("qb16", i % 2)
                    r_, r1k = r1[i % 2], ("r1", i % 2)
                    mm(ps[:, b3, :], pm, qb_, True, True, [qbk, "cst_misc"], PSK[b3])
                    S.op("dve", lambda e, r_=r_, b3=b3, sl=sl: e.tensor_tensor(out=r_, in0=ps[:, b3, :], in1=sinT[:, sl], op=ALU.mult),
                         reads=[PSK[b3], "sinT"], writes=[r1k])
                    S.op("pool", lambda e, q_=q_, sl=sl: e.tensor_tensor(out=q_, in0=q_, in1=cosT[:, sl], op=ALU.mult),
                         reads=[qk_, "cosT"], writes=[qk_])
                    S.op("dve", lambda e, q_=q_, r_=r_, c=c, sl=sl, dstT=dstT: e.tensor_tensor(out=dstT[:, c, sl], in0=q_, in1=r_, op=ALU.add),
                         reads=[qk_, r1k], writes=[dk])

                for step in range(nb_ + 2):
                    if step < nb_:
                        stage_a(step)
                    if 0 <= step - 1 < nb_:
                        stage_b(step - 1)
                    if rope and 0 <= step - 2 < nb_:
                        stage_c(step - 2)
                if ATT_STOP <= 2:
                    AR.release(m)
                    return
                if p == 0:
                    for tt in range(8):
                        b = 6 + tt % 2
                        for c in range(nkc):
                            S.op("pe", lambda e, b=b, c=c, tt=tt: e.transpose(ps[:, b, c * 128:(c + 1) * 128], kf32[:, c, tt * 128:(tt + 1) * 128], ident_f),
                                 reads=[("kf32", c, tt // 4), "ident_f"], writes=[PSK[b]], acc=True)
                        k_ = kst[0]
                        kk = ("kst", 0)
                        S.op("dve", lambda e, b=b, k_=k_: e.tensor_copy(out=k_, in_=ps[:, b, 0:ckv]), reads=[PSK[b]], writes=[kk])
                        S.dma("sp", lambda e, k_=k_, tt=tt: e.dma_start(out=T[L["nk"]][tt * 128:(tt + 1) * 128, :], in_=k_), reads=[kk])

                if p == 1:
                    cst_ = [AR.alloc([128, ckv], F32) for _ in range(1)]
                    cb_ = [AR.alloc([128, ckv], BF16) for _ in range(1)]
                    S.op("pool", lambda e: e.memset(vcx, 1.0), writes=["vcx_init"])
                    for t4 in range(4):
                        c_ = cst_[0]
                        ck_ = ("cst_", 0)
                        S.dma("sp", lambda e, c_=c_, t4=t4: e.dma_start(out=c_, in_=T[L["cv"]][t4 * 128:(t4 + 1) * 128, :]), writes=[ck_])
                        S.op("dve", lambda e, c_=c_, t4=t4: e.tensor_copy(out=vcx[:, t4, :, 0:dh], in_=c_.rearrange("p (g d) -> p g d", g=4)),
                             reads=[ck_, "vcx_init"], writes=[("vcx", t4)])
                    for t4 in range(4):
                        c_ = cst_[0]
                        ck_ = ("cst_", 0)
                        bb = cb_[0]
                        bk = ("cb_", 0)
                        S.dma("sp", lambda e, c_=c_, t4=t4: e.dma_start(out=c_, in_=T[L["ck"]][t4 * 128:(t4 + 1) * 128, :]), writes=[ck_])
                        S.op("dve", lambda e, c_=c_, bb=bb: e.tensor_copy(out=bb, in_=c_), reads=[ck_], writes=[bk])
                        b = 6 + t4 % 2
                        pb = ps[:, b, 0:64 * nkc].bitcast(BF16).rearrange("p (c t) -> p c t", c=nkc)
                        for c in range(nkc):
                            S.op("pe", lambda e, pb=pb, c=c, bb=bb: e.transpose(pb[:, c, :], bb[:, c * 128:(c + 1) * 128], ident_b),
                                 reads=[bk, "ident_b"], writes=[PSK[b]], acc=True)
                        S.op("dve", lambda e, pb=pb, t4=t4: e.tensor_copy(out=kcT[:, :, t4 * 128:(t4 + 1) * 128], in_=pb),
                             reads=[PSK[b]], writes=[("kcT", t4)])

                if ATT_STOP <= 3:
                    AR.release(m)
                    return
                nkt_max = 7 if L["kind"] == "A" else 12
                S.barrier()
                AR.release(tmp_mark)
                PT = [AR.alloc([128, nkt_max, G * 128], BF16) for _ in range(2)]
                otok = [AR.alloc([128, D], BF16) for _ in range(2)]
                OT = xnT
                den = AR.alloc([128, 4], F32)
                rden = AR.alloc([128, 4], F32)
                sbc = [0]

                def key_tiles(qt):
                    s_ = qt // (slen // 128)
                    kts = []
                    if p == 1:
                        kts += [("c", t4, None) for t4 in range(4)]
                        if L["kind"] == "A":
                            if qt > 0:
                                kts.append(("n", qt - 1, mlo_b))
                            kts.append(("n", qt, None))
                            if qt < 7:
                                kts.append(("n", qt + 1, mhi_b))
                        else:
                            kts += [("n", t8, None) for t8 in range(8)]
                    else:
                        kts += [("n", s_ * 2 + t2, None) for t2 in range(2)]
                    return kts

                def emit_scores(qt, g, it):
                    kts = key_tiles(qt)
                    pt = PT[it % 2]
                    ptk = ("PT", it % 2)
                    if dh == 64:
                        mq, e2_ = g // 2, g % 2
                        prt = slice(64 * e2_, 64 * e2_ + 64)
                        rhs = qT[prt, 4 * mq:4 * mq + 4, qt * 128:(qt + 1) * 128]
                        qreads = [("qT", 4 * mq + i, qt // 4) for i in range(4)]
                        kch = mq
                    else:
                        prt = slice(0, 128)
                        rhs = qT[:, 2 * g:2 * g + 2, qt * 128:(qt + 1) * 128]
                        qreads = [("qT", 2 * g + i, qt // 4) for i in range(2)]
                        kch = g
                    for ki, (src, ti, msk) in enumerate(kts):
                        b = sbc[0] % 4
                        sbc[0] += 1
                        if src == "c":
                            lhsT = kcT[prt, kch, ti * 128:(ti + 1) * 128]
                            kr = [("kcT", ti)]
                        else:
                            lhsT = kT[prt, kch, ti * 128:(ti + 1) * 128]
                            kr = [("kT", kch, ti // 4)]
                        mm(ps[:, b, 0:G * 128], lhsT, rhs, True, True, kr + qreads, PSK[b])
                        S.op("act", lambda e, pt=pt, ki=ki, b=b: e.activation(out=pt[:, ki, :], in_=ps[:, b, 0:G * 128], func=AF.Exp, scale=scale),
                             reads=[PSK[b]], writes=[(ptk, ki)])
                        if msk is not None:
                            S.op("pool", lambda e, pt=pt, ki=ki, msk=msk: e.tensor_tensor(
                                out=pt[:, ki, :], in0=pt[:, ki, :], in1=msk[:, 0:G, :].rearrange("p g q -> p (g q)"), op=ALU.mult),
                                reads=[(ptk, ki), "cst_misc"], writes=[(ptk, ki)])

                def emit_pv(qt, g, it):
                    kts = key_tiles(qt)
                    pt = PT[it % 2]
                    ptk = ("PT", it % 2)
                    ob = 4 + it % 2
                    ot = otok[qt % 2]
                    otk = ("otok", qt % 2)
                    opv = ps[:, ob, 0:G * (dh + 1)].rearrange("p (i d) -> p i d", i=G)
                    for i in range(G):
                        for ki, (src, ti, msk) in enumerate(kts):
                            if src == "c":
                                vr = vcx[:, ti, g, 0:dh + 1]
                                vk = [("vcx", ti)]
                            else:
                                vr = vx[:, ti, g, 0:dh + 1]
                                vk = [("vx", ti)]
                            mm(opv[:, i, :], pt[:, ki, i * 128:(i + 1) * 128], vr, ki == 0, ki == len(kts) - 1,
                               [(ptk, ki)] + vk, PSK[ob])
                    if esink is not None:
                        S.op("dve", lambda e, opv=opv, g=g: e.tensor_tensor(out=den[:, 0:G], in0=opv[:, :, dh], in1=esink[:, g * G:(g + 1) * G], op=ALU.add),
                             reads=[PSK[ob], "esink"], writes=["den"])
                        S.op("dve", lambda e: e.reciprocal(out=rden[:, 0:G], in_=den[:, 0:G]), reads=["den"], writes=["rden"])
                    else:
                        S.op("dve", lambda e, opv=opv: e.reciprocal(out=rden[:, 0:G], in_=opv[:, :, dh]), reads=[PSK[ob]], writes=["rden"])
                    for i in range(G):
                        h = g * G + i
                        S.op("dve", lambda e, opv=opv, i=i, h=h, ot=ot: e.tensor_scalar(out=ot[:, h * dh:(h + 1) * dh], in0=opv[:, i, 0:dh],
                                                                                         scalar1=rden[:, i:i + 1], scalar2=None, op0=ALU.mult),
                             reads=[PSK[ob], "rden"], writes=[(otk, h)])
                    if g == 3:
                        b = 6 + qt % 2
                        pb = ps[:, b, :].bitcast(BF16).rearrange("p (c t) -> p c t", c=8)
                        for c in range(8):
                            S.op("pe", lambda e, pb=pb, c=c, ot=ot: e.transpose(pb[:, c, :], ot[:, c * 128:(c + 1) * 128], ident_b),
                                 reads=[(otk, h) for h in range(H)] + ["ident_b"], writes=[PSK[b]], acc=True)
                        S.op("act", lambda e, pb=pb, qt=qt: e.activation(out=OT[:, :, qt * 128:(qt + 1) * 128], in_=pb, func=AF.Copy),
                             reads=[PSK[b]], writes=[("OT", qt)] + XK)

                items = [(qt, g) for qt in range(8) for g in range(4)]
                emit_scores(items[0][0], items[0][1], 0)
                for it, (qt, g) in enumerate(items):
                    if it + 1 < len(items):
                        emit_scores(items[it + 1][0], items[it + 1][1], it + 1)
                    emit_pv(qt, g, it)
                bi = 0
                for oc in range(8):
                    for hf in range(2):
                        b = bi % 4
                        bi += 1
                        for kc in range(8):
                            mm(ps[:, b, :], wo_b[:, kc, oc * 128:(oc + 1) * 128], OT[:, kc, hf * 512:(hf + 1) * 512], kc == 0, kc == 7,
                               ["wo_b"] + [("OT", hf * 4 + q) for q in range(4)], PSK[b])
                        resid(oc, hf, b, 16)
                AR.release(m)

            def s5_layer(l):
                CH = 128 // nseq
                m = AR.mark()
                TB = AR.alloc([128, 2, 64, 128], BF16)
                for r_ in range(2):
                    for g4 in range(4):
                        S.dma("sp", lambda e, r_=r_, g4=g4: e.dma_start(out=TB[:, r_, g4 * 16:(g4 + 1) * 16, :], in_=T["TS"][r_, :, g4 * 16:(g4 + 1) * 16, :]),
                              writes=[("TB", r_, g4)])
                norm_mod(A1, "A1", 0, perm8=True, final_barrier=False)
                U = AR.alloc([128, 64, 128], BF16)
                SS = AR.alloc([128, 2, 64, 128], F32)
                ALL = AR.alloc([128, 2, 64], F32)
                ALS = AR.alloc([128, 2, 64], F32)
                xf = xnT.rearrange("q a n -> q (a n)").bitcast(F32)
                H0v = xf[:, 0:128 * nseq].rearrange("q (r g s) -> q r g s", r=2, g=64)
                L8 = AR.alloc([128, 2, 128], F32)
                FS = AR.alloc([128, 2, 64], F32)
                FSo = AR.alloc([128, 2, 128], F32)
                tt_ = [[xf[:, 128 * nseq * (1 + 2 * a_ + b_):128 * nseq * (2 + 2 * a_ + b_)].rearrange("q (r g s) -> q r g s", r=2, g=64)
                        for b_ in range(2)] for a_ in range(2)]
                S.dma("sp", lambda e: e.dma_start(out=T["D1"][p].rearrange("(fc q) n -> q fc n", q=128), in_=xnT), reads=XK, writes=["D1"])
                d1v = T["D1"][p].rearrange("(g c) (t k) -> t c g k", c=16, t=8)
                for t in range(8):
                    S.dma("sp", lambda e, t=t: e.dma_start(out=U[16 * t:16 * (t + 1)], in_=d1v[t]), reads=["D1"], writes=[("U", t)])
                UK = [("U", t) for t in range(8)]
                S.dma("sp", lambda e: e.dma_start(out=L8[0:64], in_=T["TP8"].rearrange("r (d g) p -> g r d p", d=2)), writes=["L8"])
                for r_ in range(2):
                    S.op("pe", lambda e, r_=r_: e.transpose(ps[:, 7, 0:64], L8[0:64, r_, :], ident_f[0:64, 0:64]), reads=["L8", "ident_f"], writes=[PSK[7]])
                    if r_ == 0:
                        S.op("dve", lambda e: e.tensor_copy(out=ALL[:, 0, :], in_=ps[:, 7, 0:64]), reads=[PSK[7]], writes=["AL"])
                        S.op("dve", lambda e: e.tensor_copy(out=ALL[:, 1, :], in_=ps[:, 7, 0:64]), reads=[PSK[7]], writes=["AL"])
                    else:
                        S.op("dve", lambda e: e.tensor_scalar(out=ALS[:, 0, :], in0=ps[:, 7, 0:64], scalar1=-1.0, scalar2=None, op0=ALU.mult), reads=[PSK[7]], writes=["AL"])
                        S.op("dve", lambda e: e.tensor_copy(out=ALS[:, 1, :], in_=ps[:, 7, 0:64]), reads=[PSK[7]], writes=["AL"])
                if p == 1:
                    for r_ in range(2):
                        S.dma("sp", lambda e, r_=r_: e.dma_start(out=L8[0:64, r_, :].rearrange("g (d p) -> g d p", d=2),
                                                                  in_=T["st1"][:, r_].rearrange("d g p -> g d p")), reads=["L8"], writes=["L8"], key=("L8", r_))
                    for r_ in range(2):
                        S.op("pe", lambda e, r_=r_: e.transpose(ps[:, 7, 0:64], L8[0:64, r_, :], ident_f[0:64, 0:64]), reads=["L8", "ident_f"], writes=[PSK[7]])
                        S.op("dve", lambda e, r_=r_: e.tensor_copy(out=H0v[:, r_, :, 0], in_=ps[:, 7, 0:64]), reads=[PSK[7]], writes=["H0"] + XK)
                else:
                    S.op("dve", lambda e: e.memset(H0v, 0.0), writes=["H0"] + XK)
                for gq in range(16):
                    for r_ in range(2):
                        b = (gq * 2 + r_) % 4
                        for gi in range(4):
                            g = gq * 4 + gi
                            S.op("pe", lambda e, b=b, gi=gi, g=g, r_=r_: e.matmul(ps[:, b, gi * 128:(gi + 1) * 128], TB[:, r_, g, :], U[:, g, :], start=True, stop=True),
                                 reads=UK + [("TB", r_, g // 16)], writes=[PSK[b]], acc=True)
                        eng = "act" if r_ == 0 else "dve"
                        if r_ == 0:
                            S.op("act", lambda e, b=b, gq=gq: e.activation(out=SS[:, 0, gq * 4:(gq + 1) * 4, :].rearrange("q g k -> q (g k)"), in_=ps[:, b, :], func=AF.Copy),
                                 reads=[PSK[b]], writes=[("SS", 0)])
                        else:
                            S.op("dve", lambda e, b=b, gq=gq: e.tensor_copy(out=SS[:, 1, gq * 4:(gq + 1) * 4, :].rearrange("q g k -> q (g k)"), in_=ps[:, b, :]),
                                 reads=[PSK[b]], writes=[("SS", 1)])
                S.barrier()
                NS, CHV = nseq, CH
                SSv = SS.rearrange("q r g (s k) -> q r g s k", k=CHV)
                for d, eng in ((0, "dve"), (1, "pool")):
                    hs = slice(64 * d, 64 * d + 64)
                    order = list(range(CHV)) if d == 0 else list(range(CHV - 1, -1, -1))
                    t1, t2 = tt_[d]
                    k1, k2, kS = ("rt1", d), ("rt2", d), ("SSd", d)
                    all_ = ALL[hs].unsqueeze(3).broadcast_to([64, 2, 64, NS])
                    als_ = ALS[hs].unsqueeze(3).broadcast_to([64, 2, 64, NS])
                    for ii, kk in enumerate(order):
                        if ii == 0:
                            prev, prev_sw = H0v[hs], H0v[hs, ::-1]
                        else:
                            kp = order[ii - 1]
                            prev, prev_sw = SSv[hs, :, :, :, kp], SSv[hs, ::-1, :, :, kp]
                        cur = SSv[hs, :, :, :, kk]
                        S.op(eng, lambda e, hs=hs, t1=t1, prev=prev, all_=all_: e.tensor_tensor(out=t1[hs], in0=prev, in1=all_, op=ALU.mult),
                             reads=[kS, "AL", "H0"], writes=[k1])
                        S.op(eng, lambda e, hs=hs, t2=t2, prev_sw=prev_sw, als_=als_: e.tensor_tensor(out=t2[hs], in0=prev_sw, in1=als_, op=ALU.mult),
                             reads=[kS, "AL", "H0"], writes=[k2])
                        S.op(eng, lambda e, hs=hs, t1=t1, t2=t2: e.tensor_tensor(out=t1[hs], in0=t1[hs], in1=t2[hs], op=ALU.add), reads=[k1, k2], writes=[k1])
                        S.op(eng, lambda e, hs=hs, t1=t1, cur=cur: e.tensor_tensor(out=cur, in0=cur, in1=t1[hs], op=ALU.add), reads=[kS, k1], writes=[kS])
                S.barrier()
                if p == 0:
                    for s_ in range(nseq):
                        for r_ in range(2):
                            S.op("dve", lambda e, r_=r_, s_=s_: e.tensor_copy(out=FS[0:64, r_, :], in_=SS[0:64, r_, :, s_ * CH + CH - 1]), reads=[], writes=["FS"])
                            S.op("dve", lambda e, r_=r_, s_=s_: e.tensor_copy(out=FS[64:128, r_, :], in_=SS[64:128, r_, :, s_ * CH]), reads=[], writes=["FS"])
                            S.op("pe", lambda e, r_=r_: e.transpose(ps[0:64, 7, r_ * 128:(r_ + 1) * 128], FS[:, r_, :], ident_f), reads=["FS", "ident_f"], writes=[PSK[7]], acc=True)
                        S.op("dve", lambda e: e.tensor_copy(out=FSo[0:64].rearrange("g r n -> g (r n)"), in_=ps[0:64, 7, 0:256]), reads=[PSK[7]], writes=["FSo"])
                        for r_ in range(2):
                            S.dma("sp", lambda e, s_=s_, r_=r_: e.dma_start(out=T["nst"][s_][:, r_].rearrange("d g p -> g d p"),
                                                                           in_=FSo[0:64, r_, :].rearrange("g (d p) -> g d p", d=2)), reads=["FSo"], key=("nst", r_))
                for r_ in range(2):
                    for s_ in range(nseq):
                        k0 = s_ * CH
                        S.op("act", lambda e, r_=r_, k0=k0: e.activation(out=TB[0:64, r_, :, k0 + 1:k0 + CH], in_=SS[0:64, r_, :, k0:k0 + CH - 1], func=AF.Copy),
                             reads=[], writes=["Hin"])
                        S.op("act", lambda e, r_=r_, k0=k0, s_=s_: e.activation(out=TB[0:64, r_, :, k0], in_=H0v[0:64, r_, :, s_], func=AF.Copy), reads=["H0"], writes=["Hin"])
                        S.op("dve", lambda e, r_=r_, k0=k0: e.tensor_copy(out=TB[64:128, r_, :, k0:k0 + CH - 1], in_=SS[64:128, r_, :, k0 + 1:k0 + CH]),
                             reads=[], writes=["Hin"])
                        S.op("dve", lambda e, r_=r_, k0=k0, s_=s_: e.tensor_copy(out=TB[64:128, r_, :, k0 + CH - 1], in_=H0v[64:128, r_, :, s_]), reads=["H0"], writes=["Hin"])
                S.barrier()
                SSb = SS.rearrange("q r g k -> q (r g k)").bitcast(BF16)
                EL = SSb[:, 0:16384].rearrange("q (r g n) -> q r g n", r=2, g=64)
                ML = SSb[:, 16384:24576].rearrange("q (g n) -> q g n", g=64)
                YC = SSb[:, 24576:32768].rearrange("q (g n) -> q g n", g=64)
                for r_ in range(2):
                    for g4 in range(4):
                        S.dma("sp", lambda e, r_=r_, g4=g4: e.dma_start(out=EL[:, r_, g4 * 16:(g4 + 1) * 16, :], in_=T["TE"][r_, :, g4 * 16:(g4 + 1) * 16, :]),
                              writes=[("EL", r_, g4)])
                for g4 in range(4):
                    S.dma("sp", lambda e, g4=g4: e.dma_start(out=ML[:, g4 * 16:(g4 + 1) * 16, :], in_=T["TM"][:, g4 * 16:(g4 + 1) * 16, :]),
                          writes=[("ML", g4)])
                gu = [AR.alloc([128, 512], F32) for _ in range(2)]
                gs = [AR.alloc([128, 512], F32) for _ in range(2)]
                for gq in range(16):
                    b = gq % 4
                    for gi in range(4):
                        g = gq * 4 + gi
                        o_ = ps[:, b, gi * 128:(gi + 1) * 128]
                        S.op("pe", lambda e, o_=o_, g=g: e.matmul(o_, ML[:, g, :], U[:, g, :], start=True, stop=False), reads=[("ML", g // 16)] + UK, writes=[PSK[b]], acc=True)
                        S.op("pe", lambda e, o_=o_, g=g: e.matmul(o_, EL[:, 0, g, :], TB[:, 0, g, :], start=False, stop=False), reads=[("EL", 0, g // 16), "Hin"], writes=[PSK[b]], acc=True)
                        S.op("pe", lambda e, o_=o_, g=g: e.matmul(o_, EL[:, 1, g, :], TB[:, 1, g, :], start=False, stop=True), reads=[("EL", 1, g // 16), "Hin"], writes=[PSK[b]], acc=True)
                    u_, s2 = gu[gq % 2], gs[gq % 2]
                    uk, sk = ("gu", gq % 2), ("gs", gq % 2)
                    S.op("act", lambda e, u_=u_, b=b: e.activation(out=u_, in_=ps[:, b, :], func=AF.Square), reads=[PSK[b]], writes=[uk])
                    S.op("dve", lambda e, u_=u_: e.tensor_scalar(out=u_, in0=u_, scalar1=0.044715, scalar2=1.0, op0=ALU.mult, op1=ALU.add), reads=[uk], writes=[uk])
                    S.op("dve", lambda e, u_=u_, b=b: e.tensor_tensor(out=u_, in0=u_, in1=ps[:, b, :], op=ALU.mult), reads=[uk, PSK[b]], writes=[uk])
                    S.op("act", lambda e, u_=u_, s2=s2: e.activation(out=s2, in_=u_, func=AF.Sigmoid, scale=1.5957691216057308), reads=[uk], writes=[sk])
                    S.op("dve", lambda e, s2=s2, b=b, gq=gq: e.tensor_tensor(out=YC[:, gq * 4:(gq + 1) * 4, :].rearrange("q g k -> q (g k)"), in0=s2, in1=ps[:, b, :], op=ALU.mult),
                         reads=[sk, PSK[b]], writes=["YC"])
                d2v = T["D2"][p].rearrange("(g c) (t k) -> t c g k", c=16, t=8)
                for t in range(8):
                    S.dma("sp", lambda e, t=t: e.dma_start(out=d2v[t], in_=YC[16 * t:16 * (t + 1)]), reads=["YC"], writes=["D2"], key=("D2w", t))
                S.dma("sp", lambda e: e.dma_start(out=xnT, in_=T["D2"][p].rearrange("(fc q) n -> q fc n", q=128)), reads=["D2"], writes=XK, key="D2r")
                S.barrier()
                AR.release(m)
                m = AR.mark()
                wg = AR.alloc([128, 8, 2 * D], BF16)
                sg2 = [AR.alloc([128, 512], F32) for _ in range(2)]
                mx = [AR.alloc([128, 512], F32) for _ in range(2)]
                for h2 in range(2):
                    S.dma("pool", lambda e, h2=h2: e.dma_start(out=wg[:, :, h2 * D:(h2 + 1) * D],
                                                               in_=T["l1_w_glu"][:, h2 * D:(h2 + 1) * D].rearrange("(kc q) n -> q kc n", q=128)),
                          writes=[("wg", h2)])
                it = 0
                for oc in range(8):
                    for hf in range(2):
                        bA, bB = (it % 2) * 2, (it % 2) * 2 + 1
                        s_, x_ = sg2[it % 2], mx[it % 2]
                        sk, xk_ = ("sg2", it % 2), ("mx", it % 2)
                        it += 1
                        for kc in range(8):
                            mm(ps[:, bA, :], wg[:, kc, oc * 128:(oc + 1) * 128], xnT[:, kc, hf * 512:(hf + 1) * 512], kc == 0, kc == 7, [("wg", 0), XK[kc]], PSK[bA])
                        for kc in range(8):
                            mm(ps[:, bB, :], wg[:, kc, D + oc * 128:D + (oc + 1) * 128], xnT[:, kc, hf * 512:(hf + 1) * 512], kc == 0, kc == 7, [("wg", 1), XK[kc]], PSK[bB])
                        S.op("act", lambda e, s_=s_, bB=bB: e.activation(out=s_, in_=ps[:, bB, :], func=AF.Sigmoid), reads=[PSK[bB]], writes=[sk])
                        S.op("dve", lambda e, s_=s_, x_=x_, bA=bA: e.tensor_tensor(out=x_, in0=ps[:, bA, :], in1=s_, op=ALU.mult), reads=[PSK[bA], sk], writes=[xk_])
                        hv = hT[:, oc, :].rearrange("q (k t) -> q t k", t=8)[:, 4 * hf:4 * hf + 4, :]
                        S.op("dve", lambda e, x_=x_, hv=hv, oc=oc: e.scalar_tensor_tensor(out=hv, in0=x_.rearrange("q (t k) -> q t k", t=4), scalar=modT[:, 16 + oc:17 + oc],
                                                                                        in1=hv, op0=ALU.mult, op1=ALU.add),
                             reads=[xk_, "modT", HK[oc]], writes=[HK[oc]])
                AR.release(m)

            for l in range(NLAYERS):
                if STAGE & 1:
                    compute_mod(l)
                S.barrier()
                if STAGE & 2:
                    if LAYERS[l]["kind"] == "B":
                        if not SKIP_S5:
                            s5_layer(l)
                    else:
                        attention(l)
                S.barrier()
                if STAGE & 4:
                    ffn(l)
                S.barrier()

            m0 = AR.mark()
            yst = [AR.alloc([128, D], F32) for _ in range(8)]
            for tt in range(8):
                y_ = yst[tt]
                yk = ("yst", tt)
                for hf in range(2):
                    b = (tt * 2 + hf) % 8
                    for q in range(4):
                        kc = hf * 4 + q
                        S.op("pe", lambda e, b=b, q=q, kc=kc, tt=tt: e.transpose(ps[:, b, q * 128:(q + 1) * 128], hT[:, kc, tt * 128:(tt + 1) * 128], ident_f),
                             reads=[HK[kc], "ident_f"], writes=[PSK[b]], acc=True)
                    if hf:
                        S.op("act", lambda e, b=b, y_=y_: e.activation(out=y_[:, 512:1024], in_=ps[:, b, :], func=AF.Copy), reads=[PSK[b]], writes=[(yk, 1)])
                    else:
                        S.op("dve", lambda e, b=b, y_=y_: e.tensor_copy(out=y_[:, 0:512], in_=ps[:, b, :]), reads=[PSK[b]], writes=[(yk, 0)])
                S.dma("sp", lambda e, y_=y_, tt=tt: e.dma_start(out=T["y"][p, tt * 128:(tt + 1) * 128, :], in_=y_), reads=[(yk, 0), (yk, 1)], key=yk)
            S.barrier()
            AR.release(m0)

        pm_mark = AR.mark()
        prologue_mod_start()
        if not SKIP_S5 and NLAYERS > 1 and (STAGE & 2):
            s5_tables()
        prologue_mod_finish()
        AR.release(pm_mark)
        run_path(0)
        run_path(1)
        S.barrier()
        print('n_sems', len(S.sem_objs), 'insts', {e: len(S.prog[e]) for e in ENGS})
        S.emit()
    return nc


def _consts():
    cst = np.zeros((9, 128, 128), np.float32)
    for tt_ in range(8):
        for cc_ in range(16):
            cst[CST_R16, cc_, tt_ * 16 + cc_] = 1.0
    cst[CST_IDENT] = np.eye(128)
    bd = np.zeros((128, 128), np.float32)
    bd[:64, :64] = 1
    bd[64:, 64:] = 1
    cst[CST_BD64] = bd
    kk = np.arange(128)[:, None]
    qq = np.arange(128)[None, :]
    cst[CST_MLO] = (qq <= kk)
    cst[CST_MHI] = (kk <= qq)

    def perm(dh):
        qt = dh // 4
        half = dh // 2
        Pm = np.zeros((128, 128), np.float32)
        for m in range(128):
            d = m % dh
            if (d % half) < qt:
                Pm[m, m + qt] = -1.0
            else:
                Pm[m, m - qt] = 1.0
        return Pm.T.copy()

    cst[CST_PA] = perm(64)
    cst[CST_PC] = perm(128)
    t1 = np.arange(8)
    tau_p = np.repeat(t1, 16)[:, None]
    tau = np.repeat(t1, 16)[None, :]
    cst[CST_S5F] = (tau >= tau_p)
    cst[CST_S5B] = (tau <= tau_p)
    rope = np.zeros((4, 128, NT), np.float32)
    t = np.arange(NT)
    row = (t // 64).astype(np.float32)
    col = (t % 64).astype(np.float32)
    for idx, dh in ((0, 64), (2, 128)):
        half, qt = dh // 2, dh // 4
        freqs = (1.0 / (np.float32(10000.0) ** (np.arange(qt, dtype=np.float32) / np.float32(qt)))).astype(np.float32)
        for pp in range(128):
            d = pp % dh
            pos = row if d < half else col
            ang = (pos * freqs[d % qt]).astype(np.float32)
            rope[idx, pp] = np.cos(ang)
            rope[idx + 1, pp] = np.sin(ang)
    return cst, rope


_PROG = {}


def kernel(**inp):
    f32 = lambda a: np.ascontiguousarray(np.asarray(a, dtype=np.float32))
    inp = {k: f32(v) for k, v in inp.items()}
    if "nc" not in _PROG:
        _PROG["nc"] = build_program()
    nc = _PROG["nc"]
    cst, rope = _consts()
    shared = {}
    for name, shape in IN_SPECS:
        if name in inp:
            shared[name] = inp[name].reshape(shape)
    shared["cst"] = cst
    shared["rope"] = rope
    in_maps = []
    for c in range(8):
        m = dict(shared)
        m["xin"] = np.stack([inp["x_prompt"][4 * c:4 * c + 4].reshape(NT, D), inp["x_sample"][c]])
        m["cond"] = np.stack([inp["c_ctx"], inp["c"][c]])
        m["ck0"] = inp["cache_l0_k"][c].reshape(512, 256)
        m["cv0"] = inp["cache_l0_v"][c].reshape(512, 256)
        m["st1"] = inp["state_l1"][c]
        m["ck2"] = inp["cache_l2_k"][c].reshape(512, 512)
        m["cv2"] = inp["cache_l2_v"][c].reshape(512, 512)
        m["ck3"] = inp["cache_l3_k"][c].reshape(512, 256)
        m["cv3"] = inp["cache_l3_v"][c].reshape(512, 256)
        in_maps.append({k: np.ascontiguousarray(v) for k, v in m.items()})
    res = run_bass_kernel_spmd(nc, in_maps, core_ids=list(range(8)))
    R = res.results
    y_prompt = np.concatenate([R[c]["y"][0].reshape(4, 256, D) for c in range(8)], 0)
    y_sample = np.stack([R[c]["y"][1] for c in range(8)], 0)
    cat = lambda k, shp: np.concatenate([R[c][k].reshape(shp) for c in range(8)], 0)
    out = (y_prompt, y_sample,
           cat("nk0", (4, 256, 4, 64)), cat("nv0", (4, 256, 4, 64)),
           cat("nst", (4, 2, 2, 64, 64)),
           cat("nk2", (4, 256, 4, 128)), cat("nv2", (4, 256, 4, 128)),
           cat("nk3", (4, 256, 4, 64)), cat("nv3", (4, 256, 4, 64)))
    return tuple(np.ascontiguousarray(o, dtype=np.float32) for o in out)
```
